# Optimizing a Trainium2 kernel written in Bass

```python
import math
import jax, jax.numpy as jnp
from jax import lax
import numpy as np

D_MODEL = 2048
BATCH = 32
SEQ = 256
DEPTH = 2
DEC_BATCH = 4
DEC_SEQ = 1024
PAST_LEN = 256

GRID_W = 64
EPS = 1e-6
ROPE_BASE = 10000.0
RET_HEADS = 8
RET_DK = 64
RET_DV = 128
RET_QK = RET_HEADS * RET_DK
RET_W = RET_HEADS * RET_DV
RET_CHUNK = 128
HY_W = 1024
HY_ORDER = 2
HY_EMB = 33
HY_HID = 64
HY_SHORT = 3
HY_TARGET = 1e-2
HY_FAST = 0.3
HY_SLOW = 1.5
DN_HEADS = 8
DN_DK = 128
DN_DV = 128
DN_QK = DN_HEADS * DN_DK
DN_W = DN_HEADS * DN_DV
DN_CONV = 3
DN_CHUNK = 64
N_BRANCH = 3
FFN_DIM = 5504
FFN_CONV = 3
N_MOD = 6
SPLIT_SIZES = (RET_QK, RET_QK, RET_W, RET_W, (HY_ORDER + 1) * HY_W, 2 * DN_QK + DN_W, DN_W,
               2 * DN_HEADS, 2 * DN_HEADS, N_BRANCH * D_MODEL)
IN_COLS = 16416

kernel_name = 'hybrid_retention_hyena_deltanet_diffusion_step'


def _split_points(sizes):
    return [int(v) for v in np.cumsum(np.array(sizes))[:-1]]


def rms_normalize(x):
    xf = x.astype(jnp.float32)
    return xf * lax.rsqrt(jnp.mean(xf * xf, axis=-1, keepdims=True) + EPS)


def rmsnorm(x, w):
    return (rms_normalize(x) * w.astype(jnp.float32)).astype(x.dtype)


def l2norm(x):
    return x * lax.rsqrt(jnp.sum(x * x, axis=-1, keepdims=True) + EPS)


def dwconv(x, w):
    k, ch = w.shape
    return lax.conv_general_dilated(x, w[:, None, :].astype(x.dtype), window_strides=(1,),
                                    padding=[(k // 2, k // 2)],
                                    dimension_numbers=('NWC', 'WIO', 'NWC'),
                                    feature_group_count=ch)


def rope_2d(x):
    _, L, _, dk = x.shape
    n_rows = L // GRID_W
    pos_r = jnp.repeat(jnp.arange(n_rows, dtype=jnp.float32), GRID_W)
    pos_c = jnp.tile(jnp.arange(GRID_W, dtype=jnp.float32), n_rows)
    nf = dk // 4
    inv = ROPE_BASE ** (-jnp.arange(nf, dtype=jnp.float32) / nf)

    def rot(xh, pos):
        ang = pos[:, None] * inv[None, :]
        cos = jnp.cos(ang)[None, :, None, :]
        sin = jnp.sin(ang)[None, :, None, :]
        x1, x2 = xh[..., :nf], xh[..., nf:]
        return jnp.concatenate([x1 * cos - x2 * sin, x1 * sin + x2 * cos], axis=-1)

    half = dk // 2
    return jnp.concatenate([rot(x[..., :half], pos_r), rot(x[..., half:], pos_c)], axis=-1)


def retention_scan(q, k, v, log_g, s0):
    B, L, H, _ = q.shape
    dv = v.shape[-1]
    C = RET_CHUNK
    n = L // C

    def blk(t):
        return jnp.moveaxis(t.reshape(B, n, C, H, t.shape[-1]), (1, 3), (0, 2))

    idx = jnp.arange(C, dtype=jnp.float32)
    rel = idx[:, None] - idx[None, :]
    causal = rel >= 0
    dmat = jnp.where(causal, jnp.exp(jnp.where(causal, rel, 0.0)[None] * log_g[:, None, None]), 0.0)
    q_dec = jnp.exp((idx + 1.0)[None, :] * log_g[:, None])[:, :, None]
    k_dec = jnp.exp((C - 1.0 - idx)[None, :] * log_g[:, None])[:, :, None]
    c_dec = jnp.exp(C * log_g)[:, None, None]

    def step(s, inp):
        qc, kc, vc = inp
        scores = jnp.einsum('bhid,bhjd->bhij', qc, kc) * dmat
        o = (jnp.einsum('bhij,bhjv->bhiv', scores, vc)
             + jnp.einsum('bhid,bhdv->bhiv', qc * q_dec, s))
        s = s * c_dec + jnp.einsum('bhjd,bhjv->bhdv', kc * k_dec, vc)
        return s, o

    s_fin, o = lax.scan(step, s0, (blk(q), blk(k), blk(v)))
    o = jnp.moveaxis(o, (0, 2), (1, 3)).reshape(B, L, H, dv)
    return o, s_fin


def gated_delta_scan(q, k, v, g, beta, s0):
    B, L, H, _ = q.shape
    dv = v.shape[-1]
    C = DN_CHUNK
    n = L // C

    def blk(t):
        return jnp.moveaxis(t.reshape((B, n, C, H) + t.shape[3:]), 3, 1)

    q, k, v, g, beta = blk(q), blk(k), blk(v), blk(g), blk(beta)
    gc = jnp.cumsum(g, axis=-1)
    idx = jnp.arange(C)
    incl = idx[:, None] >= idx[None, :]
    strict = idx[:, None] > idx[None, :]
    decay = jnp.exp(jnp.where(incl, gc[..., :, None] - gc[..., None, :], -jnp.inf))
    kb = k * beta[..., None]
    a_mat = (jnp.where(strict, jnp.einsum('bhncd,bhnsd->bhncs', kb, k) * decay, 0.0)
             + jnp.eye(C, dtype=jnp.float32))
    rhs = jnp.concatenate([v * beta[..., None], kb * jnp.exp(gc)[..., None]], axis=-1)
    sol = lax.linalg.triangular_solve(a_mat, rhs, left_side=True, lower=True, unit_diagonal=True)
    u, w = sol[..., :dv], sol[..., dv:]
    attn = jnp.einsum('bhncd,bhnsd->bhncs', q, k) * decay
    g_last = gc[..., -1]
    qd = q * jnp.exp(gc)[..., None]
    kd = k * jnp.exp(g_last[..., None] - gc)[..., None]

    def step(s, inp):
        qd_c, kd_c, u_c, w_c, a_c, gl_c = inp
        v_new = u_c - jnp.einsum('bhcd,bhdv->bhcv', w_c, s)
        o = jnp.einsum('bhcd,bhdv->bhcv', qd_c, s) + jnp.einsum('bhcs,bhsv->bhcv', a_c, v_new)
        s = s * jnp.exp(gl_c)[..., None, None] + jnp.einsum('bhcd,bhcv->bhdv', kd_c, v_new)
        return s, o

    xs = (jnp.moveaxis(qd, 2, 0), jnp.moveaxis(kd, 2, 0), jnp.moveaxis(u, 2, 0),
          jnp.moveaxis(w, 2, 0), jnp.moveaxis(attn, 2, 0), jnp.moveaxis(g_last, 2, 0))
    s_fin, o = lax.scan(step, s0, xs)
    o = jnp.moveaxis(o, 0, 2).reshape(B, H, L, dv)
    return jnp.moveaxis(o, 1, 2), s_fin


def hyena_filters(L, w1, b1, fr1, w2, b2, fr2, w3):
    f32 = jnp.float32
    t = jnp.linspace(0.0, 1.0, L, dtype=f32)[:, None]
    bands = (HY_EMB - 1) // 2
    wpos = 2.0 * math.pi * jnp.arange(L, dtype=f32)[:, None] / L
    fr = jnp.linspace(1e-4, bands - 1, bands, dtype=f32)[None, :]
    feats = jnp.concatenate([t, jnp.cos(fr * wpos), -jnp.sin(fr * wpos)], axis=-1)
    hid = jnp.sin(fr1.astype(f32) * (feats @ w1.astype(f32) + b1.astype(f32)))
    hid = jnp.sin(fr2.astype(f32) * (hid @ w2.astype(f32) + b2.astype(f32)))
    h = (hid @ w3.astype(f32)).reshape(L, 2, HY_ORDER, HY_W)
    deltas = jnp.abs(jnp.linspace(math.log(HY_TARGET) / HY_FAST, math.log(HY_TARGET) / HY_SLOW,
                                  HY_W, dtype=f32))
    h = h * jnp.exp(-t * deltas[None, :])[:, None, None, :]
    return h / (jnp.sum(jnp.abs(h), axis=0, keepdims=True) + EPS)


def fft_long_conv(z, h_f, h_b, bias):
    L = z.shape[1]
    n = 2 * L
    zf = z.astype(jnp.float32)
    zs = jnp.fft.rfft(zf, n=n, axis=1)
    hs = jnp.fft.rfft(h_f, n=n, axis=0) + jnp.conj(jnp.fft.rfft(h_b, n=n, axis=0))
    y = jnp.fft.irfft(zs * hs[None], n=n, axis=1)[:, :L]
    return (y + zf * bias.astype(jnp.float32)).astype(z.dtype)


def hyena(u, filt, bias, w_short):
    u = dwconv(u, w_short)
    v, x1, x2 = jnp.split(u, 3, axis=-1)
    z = x1 * fft_long_conv(v, filt[:, 0, 0], filt[:, 1, 0], bias[0])
    z = x2 * fft_long_conv(z, filt[:, 0, 1], filt[:, 1, 1], bias[1])
    return z


def trunk_layer(x, mod, s_ret0, s_dn0, latent, p):
    f32 = jnp.float32
    B, L, _ = x.shape
    fl = lambda t: jnp.flip(t, axis=1)
    sh1, sc1, g1, sh2, sc2, g2 = jnp.split(mod, N_MOD, axis=-1)
    h = rmsnorm(x, p['norm1']) * (1 + sc1) + sh1
    proj = h @ p['w_in']
    rq, rk, rv, rg, hy, dqkv, dz, da, db, mg = jnp.split(proj, _split_points(SPLIT_SIZES), axis=-1)

    q = rq.reshape(B, L, RET_HEADS, RET_DK).astype(f32)
    k = rk.reshape(B, L, RET_HEADS, RET_DK).astype(f32) * (RET_DK ** -0.5)
    if latent:
        q, k = rope_2d(q), rope_2d(k)
    v = rv.reshape(B, L, RET_HEADS, RET_DV).astype(f32)
    lg = jax.nn.log_sigmoid(p['ret_decay'].astype(f32))
    o_f, sr_f = retention_scan(q, k, v, lg[0], s_ret0[:, 0])
    o_b, sr_b = retention_scan(fl(q), fl(k), fl(v), lg[1], s_ret0[:, 1])
    o_r = rms_normalize(o_f + fl(o_b)).reshape(B, L, RET_W)
    ret_out = (o_r * jax.nn.silu(rg.astype(f32))).astype(x.dtype)

    filt = hyena_filters(L, p['hy_w1'], p['hy_b1'], p['hy_freq1'], p['hy_w2'], p['hy_b2'],
                         p['hy_freq2'], p['hy_w3'])
    hy_out = hyena(hy, filt, p['hy_bias'], p['hy_short'])

    qkv = jax.nn.silu(dwconv(dqkv, p['dn_conv'])).astype(f32)
    q2, k2, v2 = jnp.split(qkv, [DN_QK, 2 * DN_QK], axis=-1)
    q2 = l2norm(q2.reshape(B, L, DN_HEADS, DN_DK)) * (DN_DK ** -0.5)
    k2 = l2norm(k2.reshape(B, L, DN_HEADS, DN_DK))
    v2 = v2.reshape(B, L, DN_HEADS, DN_DV)
    beta = jax.nn.sigmoid(db.astype(f32)).reshape(B, L, 2, DN_HEADS)
    gdec = (-jnp.exp(p['dn_a_log'].astype(f32))
            * jax.nn.softplus(da.astype(f32).reshape(B, L, 2, DN_HEADS) + p['dn_dt_bias'].astype(f32)))
    od_f, sd_f = gated_delta_scan(q2, k2, v2, gdec[:, :, 0], beta[:, :, 0], s_dn0[:, 0])
    od_b, sd_b = gated_delta_scan(fl(q2), fl(k2), fl(v2), fl(gdec[:, :, 1]), fl(beta[:, :, 1]),
                                  s_dn0[:, 1])
    o_d = (rms_normalize(od_f + fl(od_b)) * p['dn_norm'].astype(f32)).reshape(B, L, DN_W)
    dn_out = (o_d * jax.nn.silu(dz.astype(f32))).astype(x.dtype)

    gr, gh, gd = jnp.split(jax.nn.sigmoid(mg), N_BRANCH, axis=-1)
    mix = gr * (ret_out @ p['p_ret']) + gh * (hy_out @ p['p_hy']) + gd * (dn_out @ p['p_dn'])
    x = x + g1 * (mix @ p['w_o'])

    h2 = rmsnorm(x, p['norm2']) * (1 + sc2) + sh2
    up = dwconv(h2 @ p['w_up'], p['ffn_conv'])
    ga, gb = jnp.split(up, 2, axis=-1)
    x = x + g2 * ((jax.nn.silu(ga) * gb) @ p['w_down'])
    return x, jnp.stack([sr_f, sr_b], axis=1), jnp.stack([sd_f, sd_b], axis=1)


def setup_inputs(seed: int = 0) -> dict:
    key = jax.random.key(seed)
    ks = jax.random.split(key, 40)
    f32 = jnp.float32
    nrm = lambda k, shape, s: jax.random.normal(k, shape, f32) * s
    D = D_MODEL
    gam = 1.0 - 2.0 ** (-5.0 - jnp.arange(RET_HEADS, dtype=f32))
    ret_logit = jnp.log(gam) - jnp.log1p(-gam)
    dt = jnp.exp(jax.random.uniform(ks[20], (DEPTH, 2, DN_HEADS), f32,
                                    minval=math.log(1e-3), maxval=math.log(1e-1)))
    return {
        'x_prompt': nrm(ks[0], (BATCH, SEQ, D), 1.0),
        'x_sample': nrm(ks[1], (DEC_BATCH, DEC_SEQ, D), 1.0),
        'state_ret': nrm(ks[2], (DEC_BATCH, DEPTH, 2, RET_HEADS, RET_DK, RET_DV), 0.5),
        'state_dn': nrm(ks[3], (DEC_BATCH, DEPTH, 2, DN_HEADS, DN_DK, DN_DV), 0.5),
        'c': nrm(ks[4], (DEC_BATCH, D), 1.0),
        'c_ctx': nrm(ks[5], (D,), 1.0),
        'w_ada': nrm(ks[6], (DEPTH, D, N_MOD * D), 0.5 * D ** -0.5),
        'b_ada': nrm(ks[7], (DEPTH, N_MOD * D), 0.02),
        'norm1': 1.0 + nrm(ks[8], (DEPTH, D), 0.02),
        'w_in': nrm(ks[9], (DEPTH, D, IN_COLS), D ** -0.5),
        'ret_decay': ret_logit[None, None, :] + nrm(ks[10], (DEPTH, 2, RET_HEADS), 0.1),
        'hy_short': nrm(ks[11], (DEPTH, HY_SHORT, (HY_ORDER + 1) * HY_W), HY_SHORT ** -0.5),
        'hy_w1': nrm(ks[12], (DEPTH, HY_EMB, HY_HID), HY_EMB ** -0.5),
        'hy_b1': nrm(ks[13], (DEPTH, HY_HID), 0.02),
        'hy_freq1': 1.0 + nrm(ks[14], (DEPTH, HY_HID), 0.02),
        'hy_w2': nrm(ks[15], (DEPTH, HY_HID, HY_HID), HY_HID ** -0.5),
        'hy_b2': nrm(ks[16], (DEPTH, HY_HID), 0.02),
        'hy_freq2': 1.0 + nrm(ks[17], (DEPTH, HY_HID), 0.02),
        'hy_w3': nrm(ks[18], (DEPTH, HY_HID, 2 * HY_ORDER * HY_W), HY_HID ** -0.5),
        'hy_bias': nrm(ks[19], (DEPTH, HY_ORDER, HY_W), 0.1),
        'dn_conv': nrm(ks[21], (DEPTH, DN_CONV, 2 * DN_QK + DN_W), DN_CONV ** -0.5),
        'dn_a_log': jnp.log(jax.random.uniform(ks[22], (DEPTH, 2, DN_HEADS), f32, minval=1.0, maxval=16.0)),
        'dn_dt_bias': dt + jnp.log(-jnp.expm1(-dt)),
        'dn_norm': 1.0 + nrm(ks[23], (DEPTH, DN_DV), 0.02),
        'p_ret': nrm(ks[24], (DEPTH, RET_W, D), RET_W ** -0.5),
        'p_hy': nrm(ks[25], (DEPTH, HY_W, D), HY_W ** -0.5),
        'p_dn': nrm(ks[26], (DEPTH, DN_W, D), DN_W ** -0.5),
        'w_o': nrm(ks[27], (DEPTH, D, D), D ** -0.5),
        'norm2': 1.0 + nrm(ks[28], (DEPTH, D), 0.02),
        'w_up': nrm(ks[29], (DEPTH, D, 2 * FFN_DIM), D ** -0.5),
        'ffn_conv': nrm(ks[30], (DEPTH, FFN_CONV, 2 * FFN_DIM), FFN_CONV ** -0.5),
        'w_down': nrm(ks[31], (DEPTH, FFN_DIM, D), FFN_DIM ** -0.5),
        'norm_f': 1.0 + nrm(ks[32], (D,), 0.02),
    }


def reference(x_prompt, x_sample, state_ret, state_dn, c, c_ctx, w_ada, b_ada, norm1, w_in,
              ret_decay, hy_short, hy_w1, hy_b1, hy_freq1, hy_w2, hy_b2, hy_freq2, hy_w3, hy_bias,
              dn_conv, dn_a_log, dn_dt_bias, dn_norm, p_ret, p_hy, p_dn, w_o, norm2, w_up,
              ffn_conv, w_down, norm_f):
    f32 = jnp.float32
    xp, xs = x_prompt, x_sample
    bp = xp.shape[0]
    zero_ret = jnp.zeros((bp, 2, RET_HEADS, RET_DK, RET_DV), f32)
    zero_dn = jnp.zeros((bp, 2, DN_HEADS, DN_DK, DN_DV), f32)
    new_ret, new_dn = [], []
    for l in range(DEPTH):
        p = dict(norm1=norm1[l], w_in=w_in[l], ret_decay=ret_decay[l], hy_short=hy_short[l],
                 hy_w1=hy_w1[l], hy_b1=hy_b1[l], hy_freq1=hy_freq1[l], hy_w2=hy_w2[l],
                 hy_b2=hy_b2[l], hy_freq2=hy_freq2[l], hy_w3=hy_w3[l], hy_bias=hy_bias[l],
                 dn_conv=dn_conv[l], dn_a_log=dn_a_log[l], dn_dt_bias=dn_dt_bias[l],
                 dn_norm=dn_norm[l], p_ret=p_ret[l], p_hy=p_hy[l], p_dn=p_dn[l], w_o=w_o[l],
                 norm2=norm2[l], w_up=w_up[l], ffn_conv=ffn_conv[l], w_down=w_down[l])
        mod_ctx = (jax.nn.silu(c_ctx) @ w_ada[l] + b_ada[l])[None, None, :]
        mod_lat = (jax.nn.silu(c) @ w_ada[l] + b_ada[l])[:, None, :]
        xp, s_r, s_d = trunk_layer(xp, mod_ctx, zero_ret, zero_dn, False, p)
        new_ret.append(s_r)
        new_dn.append(s_d)
        xs, _, _ = trunk_layer(xs, mod_lat, state_ret[:, l].astype(f32), state_dn[:, l].astype(f32),
                               True, p)
    y_prompt = rmsnorm(xp, norm_f)
    y_sample = rmsnorm(xs, norm_f)
    new_state_ret = jnp.stack(new_ret, axis=1).astype(x_prompt.dtype)
    new_state_dn = jnp.stack(new_dn, axis=1).astype(x_prompt.dtype)
    return (y_prompt, y_sample, new_state_ret, new_state_dn)
```

```python
import numpy as np
from contextlib import ExitStack
import concourse.bass as bass
import concourse.mybir as mybir
from concourse.bass_utils import run_bass_kernel_spmd

F32 = mybir.dt.float32
BF16 = mybir.dt.bfloat16
I32 = mybir.dt.int32
AF = mybir.ActivationFunctionType
ALU = mybir.AluOpType
AX = mybir.AxisListType


class Buf:
    __slots__ = ("t", "name", "w", "r", "psum")

    def __init__(self, t, name, psum=False):
        self.t = t
        self.name = name
        self.psum = psum
        self.w = None
        self.r = {}

    def __getitem__(self, idx):
        return self.t[idx]


class Ctx:
    SEM_LIMIT = 30000
    NDMA = 24

    def __init__(self, nc):
        self.nc = nc
        self.es = ExitStack()
        self.eng = {"pe": nc.tensor, "act": nc.scalar, "dve": nc.vector, "pool": nc.gpsimd, "sp": nc.sync}
        self.cur = {}
        self.waited = {k: {} for k in self.eng}
        self.nsem = 0
        for k in self.eng:
            self._new_sem(k)
        self.dpool = {}
        for q in ("sp", "pool", "act"):
            self.dpool[q] = [[self._alloc_sem("d%s%d" % (q, i)), 0] for i in range(self.NDMA)]
        self.dnext = {q: 0 for q in self.dpool}
        self.stage_es = None
        self.uid = 0

    def _alloc_sem(self, name):
        self.nsem += 1
        s = self.es.enter_context(self.nc.semaphore("%s_%d" % (name, self.nsem)))
        if not hasattr(self, "allsems"):
            self.allsems = []
        self.allsems.append(s)
        return s

    def _new_sem(self, k):
        self.cur[k] = [self._alloc_sem("e" + k), 0]

    def begin_stage(self):
        if not hasattr(self, "stack"):
            self.stack = []
        self.stack.append(ExitStack())
        self.stage_es = self.stack[-1]

    def end_stage(self):
        self.barrier()
        self.stack.pop().close()
        self.stage_es = self.stack[-1] if self.stack else None

    def sb(self, shape, dt, name="t", persist=False):
        self.uid += 1
        nm = "%s_%d" % (name, self.uid)
        es = self.es if persist else self.stage_es
        t = es.enter_context(self.nc.sbuf_tensor(nm, list(shape), dt))
        return Buf(t, nm)

    def ps(self, shape, dt, name="p"):
        self.uid += 1
        nm = "%s_%d" % (name, self.uid)
        t = self.es.enter_context(self.nc.psum_tensor(nm, list(shape), dt))
        return Buf(t, nm, psum=True)

    def _wait(self, k, tok):
        if tok is None:
            return
        sem, val = tok
        if k == "pe" and sem is self.cur["pe"][0]:
            return
        w = self.waited[k]
        key = id(sem)
        if w.get(key, (None, 0))[1] >= val:
            return
        w[key] = (sem, val)
        self.eng[k].wait_ge(sem, val)

    def _deps(self, k, reads, writes):
        for b in reads:
            self._wait(k, b.w)
            if b.psum:
                for tok in list(b.r.values()):
                    self._wait(k, tok)
        for b in writes:
            self._wait(k, b.w)
            for tok in list(b.r.values()):
                self._wait(k, tok)

    def _commit(self, tok, reads, writes):
        for b in reads:
            b.r[id(tok[0])] = tok
        for b in writes:
            b.w = tok
            b.r = {}

    def op(self, k, fn, reads=(), writes=()):
        self._deps(k, reads, writes)
        c = self.cur[k]
        if c[1] >= self.SEM_LIMIT:
            self._new_sem(k)
            c = self.cur[k]
        c[1] += 1
        fn(self.eng[k]).then_inc(c[0], 1)
        tok = (c[0], c[1])
        self._commit(tok, reads, writes)
        return tok

    def dma(self, q, out, in_, reads=(), writes=(), **kw):
        pool = self.dpool[q]
        i = self.dnext[q]
        self.dnext[q] = (i + 1) % len(pool)
        slot = pool[i]
        if slot[1] > 0:
            self._wait(q, (slot[0], slot[1]))
        if slot[1] >= self.SEM_LIMIT:
            slot[0] = self._alloc_sem("d" + q)
            slot[1] = 0
        self._deps(q, reads, writes)
        slot[1] += 16
        self.eng[q].dma_start(out=out, in_=in_, **kw).then_inc(slot[0], 16)
        tok = (slot[0], slot[1])
        self._commit(tok, reads, writes)
        return tok

    def barrier(self):
        toks = [(c[0], c[1]) for c in self.cur.values() if c[1] > 0]
        for q in self.dpool:
            for slot in self.dpool[q]:
                if slot[1] > 0:
                    toks.append((slot[0], slot[1]))
        for k in self.eng:
            for tok in toks:
                self._wait(k, tok)

    def finish(self):
        self.barrier()
        self.es.close()

import math
import numpy as np


def make_consts():
    f = np.float32
    C = {}
    i = np.arange(128)
    C["ident"] = np.eye(128, dtype=f)
    C["ones"] = np.ones((128, 128), f)
    C["UF"] = (i[:, None] <= i[None, :]).astype(f)
    C["UB"] = (i[:, None] >= i[None, :]).astype(f)
    C["SU"] = (i[:, None] < i[None, :]).astype(f)
    C["SL"] = (i[:, None] > i[None, :]).astype(f)
    L = 1024
    pos = np.arange(L, dtype=np.float64)
    pos_r = np.floor(pos / 64)
    pos_c = pos % 64
    inv = 10000.0 ** (-np.arange(16, dtype=np.float64) / 16)
    cosT = np.zeros((128, L))
    sinT = np.zeros((128, L))
    for p in range(128):
        q = p % 64
        half = q // 32
        x2 = (q % 32) // 16
        fi = q % 16
        ang = (pos_r if half == 0 else pos_c) * inv[fi]
        cosT[p] = np.cos(ang)
        sinT[p] = np.sin(ang) * (1.0 if x2 else -1.0)
    C["ropeC"] = cosT.astype(f)
    C["ropeS"] = sinT.astype(f)
    for nm, LL in (("S", 1024), ("P", 256)):
        u = np.arange(2 * LL - 128)[None, :]
        p = np.arange(128)[:, None]
        d = u - p - (LL - 128)
        C["dpos" + nm] = np.maximum(d, 0).astype(f)
        C["dneg" + nm] = np.maximum(-d, 0).astype(f)
        C["dz" + nm] = (d == 0).astype(f)
    j = np.arange(128)[:, None] + 128 * np.arange(2)[None, :]
    C["expfb"] = np.concatenate([255 - j, j], axis=1).astype(f)
    ii = np.arange(1024)[None, :].repeat(128, 0)
    C["idx1"] = (ii + 1).astype(f)
    C["idx2"] = (1024 - ii).astype(f)
    for LL in (256, 1024):
        t = np.linspace(0.0, 1.0, LL, dtype=np.float32)[:, None].astype(np.float64)
        wpos = 2.0 * math.pi * np.arange(LL, dtype=np.float64)[:, None] / LL
        fr = np.linspace(1e-4, 15, 16, dtype=np.float32)[None, :].astype(np.float64)
        feats = np.concatenate([t, np.cos(fr * wpos), -np.sin(fr * wpos)], axis=-1)
        C["featsT%d" % LL] = np.ascontiguousarray(feats.T).astype(f)
        deltas = np.abs(np.linspace(math.log(1e-2) / 0.3, math.log(1e-2) / 1.5, 1024, dtype=np.float32)).astype(np.float64)
        C["win%d" % LL] = np.exp(-t * deltas[None, :]).astype(f)
        N = 2 * LL
        tt = np.arange(LL, dtype=np.float64)[:, None]
        kk = np.arange(LL, dtype=np.float64)[None, :]
        ang = 2.0 * math.pi * tt * kk / N
        CA = np.cos(ang)
        MB = -np.sin(ang)
        MB[:, 0] = (-1.0) ** np.arange(LL)
        IA = (2.0 / N) * np.cos(ang.T)
        IA[0, :] = 1.0 / N
        IB = -(2.0 / N) * np.sin(ang.T)
        IB[0, :] = ((-1.0) ** np.arange(LL)) / N
        C["CA%d" % LL] = CA.astype(f)
        C["MB%d" % LL] = MB.astype(f)
        C["IA%d" % LL] = IA.astype(f)
        C["IB%d" % LL] = IB.astype(f)
    ii, jj = np.meshgrid(np.arange(128), np.arange(128), indexing="ij")
    ML = np.stack([(((ii >> (s + 1)) == (jj >> (s + 1))) & ((ii >> s) != (jj >> s)) & (ii > jj)).astype(f) for s in range(7)])
    MU = np.ascontiguousarray(ML.transpose(0, 2, 1))
    C["MLU"] = np.ascontiguousarray(np.concatenate([ML, MU], axis=2).transpose(1, 0, 2))
    C["MUL"] = np.ascontiguousarray(np.concatenate([MU, ML], axis=2).transpose(1, 0, 2))
    m0 = np.ones((128, 2), f)
    m0[0, 0] = 0.0
    m0[:, 1] = 1.0 - m0[:, 0]
    C["m0"] = m0
    C["eps"] = np.full((128, 1), 1e-6, f)
    return C


import math

NT = 2048
TILES = [(0, 512), (512, 512), (1024, 512), (1536, 512)]
EPS = 1e-6

WSPEC = dict(
    w_ada=[2, 2048, 12288], b_ada=[2, 12288], norm1=[2, 2048], w_in=[2, 2048, 16416], ret_decay=[2, 2, 8],
    hy_short=[2, 3, 3072], hy_w1=[2, 33, 64], hy_b1=[2, 64], hy_freq1=[2, 64], hy_w2=[2, 64, 64], hy_b2=[2, 64],
    hy_freq2=[2, 64], hy_w3=[2, 64, 4096], hy_bias=[2, 2, 1024], dn_conv=[2, 3, 3072], dn_a_log=[2, 2, 8],
    dn_dt_bias=[2, 2, 8], dn_norm=[2, 128], p_ret=[2, 1024, 2048], p_hy=[2, 1024, 2048], p_dn=[2, 1024, 2048],
    w_o=[2, 2048, 2048], norm2=[2, 2048], w_up=[2, 2048, 11008], ffn_conv=[2, 3, 11008], w_down=[2, 5504, 2048],
    norm_f=[2048])


class KB:
    def __init__(self, stop_after=None, dbg=(), wspec=None, ext_in=()):
        self.stop_after = stop_after
        self.dbg = set(dbg)
        nc = self.nc = bass.Bass("TRN2", target_bir_lowering=False)
        self.c = Ctx(nc)
        self.I = {}
        self.consts = make_consts()
        for k, v in self.consts.items():
            self.I[k] = nc.dram_tensor(k, list(v.shape), F32, kind="ExternalInput").ap()
        self.ext_in = set(ext_in)
        for k, s in (wspec or WSPEC).items():
            self.I[k] = nc.dram_tensor(k, s, F32, kind="ExternalInput").ap()
        self.I["xin"] = nc.dram_tensor("xin", [NT, 2048], F32, kind="ExternalInput").ap()
        self.I["sret"] = nc.dram_tensor("sret", [2, 2, 8, 64, 128], F32, kind="ExternalInput").ap()
        self.I["sdn"] = nc.dram_tensor("sdn", [2, 2, 8, 128, 128], F32, kind="ExternalInput").ap()
        self.I["cvec"] = nc.dram_tensor("cvec", [2, 2048], F32, kind="ExternalInput").ap()
        self.O = {}
        self.O["y"] = nc.dram_tensor("y", [NT, 2048], F32, kind="ExternalOutput").ap()
        self.O["osret"] = nc.dram_tensor("osret", [4, 2, 2, 8, 64, 128], F32, kind="ExternalOutput").ap()
        self.O["osdn"] = nc.dram_tensor("osdn", [4, 2, 2, 8, 128, 128], F32, kind="ExternalOutput").ap()
        self.S = {}
        c = self.c
        self.PB = [c.ps([128, 512], F32, "pb") for _ in range(8)]
        self.pbi = 0
        self.identF = c.sb([128, 128], F32, "identF", persist=True)
        self.identB = c.sb([128, 128], BF16, "identB", persist=True)
        self.onesB = c.sb([128, 128], BF16, "onesB", persist=True)
        self.onesF = c.sb([128, 128], F32, "onesF", persist=True)
        self.epsT = c.sb([128, 1], F32, "epsT", persist=True)
        self.rows = c.sb([128, 128], F32, "rows", persist=True)
        c.dma("sp", self.identF[:], self.I["ident"], writes=[self.identF])
        c.dma("pool", self.identB[:], self.I["ident"], writes=[self.identB])
        c.dma("pool", self.onesB[:], self.I["ones"], writes=[self.onesB])
        c.dma("sp", self.onesF[:], self.I["ones"], writes=[self.onesF])
        c.dma("sp", self.epsT[:], self.I["eps"], writes=[self.epsT])
        self.HT = None
        self.modT = c.sb([128, 2, 96, 2], F32, "modT", persist=True)
        self.sca = c.sb([128, 2, 2, 16, 2], F32, "sca", persist=True)
        self.gbraw = c.sb([128, 16, 32], F32, "gbraw", persist=True)

    def scr(self, name, shape, dt):
        if name not in self.S:
            kind = "ExternalOutput" if name in self.dbg else ("ExternalInput" if name in self.ext_in else "Internal")
            self.S[name] = self.nc.dram_tensor(name, list(shape), dt, kind=kind).ap()
        return self.S[name]

    def pb(self):
        self.pbi = (self.pbi + 1) % 6
        return self.PB[self.pbi]

    def pacc(self):
        self.pai = (getattr(self, "pai", 0) + 1) % 2
        return self.PB[6 + self.pai]

    def load_cols(self, dstbuf, dst_ap, src_rows, n):
        c = self.c
        rows = self.rows
        c.dma("sp", rows[:n, :], src_rows, writes=[rows])
        ps = self.pb()
        idf = self.identF
        c.op("pe", lambda e: e.transpose(out=ps[:, :n], in_=rows[:n, :], identity=idf[:n, :n]), reads=[rows, idf], writes=[ps])
        c.op("dve", lambda e: e.tensor_copy(out=dst_ap, in_=ps[:, :n]), reads=[ps], writes=[dstbuf])

    def stage_input(self):
        c = self.c
        XT = self.scr("XT0", [16, 128, NT], F32)
        c.begin_stage()
        xs = [c.sb([128, 2048], F32, "xs") for _ in range(2)]
        xo = [c.sb([128, 16, 128], F32, "xo") for _ in range(2)]
        idf = self.identF
        for tt in range(16):
            a = xs[tt % 2]
            o = xo[tt % 2]
            c.dma("sp", a[:], self.I["xin"][tt * 128:(tt + 1) * 128, :], writes=[a])
            for g in range(4):
                ps = self.pb()
                for j in range(4):
                    ch = g * 4 + j
                    c.op("pe", lambda e: e.transpose(out=ps[:, j * 128:(j + 1) * 128], in_=a[:, ch * 128:(ch + 1) * 128], identity=idf[:]),
                         reads=[a, idf], writes=[ps])
                eng = "dve" if g % 2 == 0 else "act"
                if eng == "dve":
                    c.op("dve", lambda e: e.tensor_copy(out=o[:, g * 4:(g + 1) * 4, :], in_=ps[:].rearrange("p (j t) -> p j t", j=4)), reads=[ps], writes=[o])
                else:
                    c.op("act", lambda e: e.copy(out=o[:, g * 4:(g + 1) * 4, :], in_=ps[:].rearrange("p (j t) -> p j t", j=4)), reads=[ps], writes=[o])
            c.dma("sp", XT[:, :, tt * 128:(tt + 1) * 128].rearrange("c p t -> p c t"), o[:], reads=[o])
        c.end_stage()

    def stage_mod(self):
        c = self.c
        I = self.I
        c.begin_stage()
        scT = c.sb([128, 2, 16], F32, "scT")
        self.load_cols(scT, scT[:].rearrange("p b k -> p (b k)"), I["cvec"].rearrange("b (k p) -> (b k) p", p=128), 32)
        c.op("act", lambda e: e.activation(out=scT[:], in_=scT[:], func=AF.Silu), reads=[scT], writes=[scT])
        bada = c.sb([128, 96], F32, "bada")
        nw = c.sb([128, 16], F32, "nw")
        wa = [c.sb([128, 16, 256], F32, "wa") for _ in range(3)]
        for l in range(2):
            self.load_cols(bada, bada[:], I["b_ada"][l].rearrange("(n p) -> n p", p=128), 96)
            ps = self.pb()
            for blk in range(48):
                w = wa[blk % 3]
                c.dma("sp" if blk % 2 == 0 else "act", w[:], I["w_ada"][l][:, blk * 256:(blk + 1) * 256].rearrange("(k p) n -> p k n", p=128), writes=[w])
                for half in range(2):
                    ch = blk * 2 + half
                    for k in range(16):
                        c.op("pe", lambda e: e.matmul(ps[:, ch * 2:ch * 2 + 2], lhsT=w[:, k, half * 128:(half + 1) * 128], rhs=scT[:, :, k],
                                                      start=(k == 0), stop=(k == 15)), reads=[w, scT], writes=[ps])
            mt = self.modT
            for b in range(2):
                c.op("dve", lambda e: e.tensor_tensor(out=mt[:, l, :, b], in0=ps[:, 0:192].rearrange("p (c b) -> p c b", b=2)[:, :, b], in1=bada[:], op=ALU.add),
                     reads=[ps, bada], writes=[mt])
            for which, (nm, scb) in enumerate((("norm1", 16), ("norm2", 64))):
                self.load_cols(nw, nw[:], I[nm][l].rearrange("(n p) -> n p", p=128), 16)
                sc = self.sca
                for b in range(2):
                    c.op("dve", lambda e: e.scalar_tensor_tensor(out=sc[:, l, which, :, b], in0=mt[:, l, scb:scb + 16, b], scalar=1.0, in1=nw[:], op0=ALU.add, op1=ALU.mult),
                         reads=[mt, nw], writes=[sc])
        c.end_stage()

    def stage_norm(self, XT, l, which):
        c = self.c
        c.begin_stage()
        self.HT = c.sb([128, 16, NT], BF16, "HT")
        c.begin_stage()
        shb = 0 if which == 0 else 48
        xs = [c.sb([128, 16, 256], F32, "nx") for _ in range(2)]
        sq = c.sb([128, 16, 256], BF16, "nsq")
        tmp = c.sb([128, 16, 256], F32, "ntmp")
        r = c.sb([128, 256], F32, "nr")
        HT, sc, mt, ones, eps = self.HT, self.sca, self.modT, self.onesB, self.epsT
        for ti in range(8):
            t0 = ti * 256
            b = 0 if t0 < 1024 else 1
            x = xs[ti % 2]
            c.dma("sp", x[:], XT[:, :, t0:t0 + 256].rearrange("c p t -> p c t"), writes=[x])
            c.op("act", lambda e: e.activation(out=sq[:], in_=x[:], func=AF.Square), reads=[x], writes=[sq])
            ps = self.pb()
            for ch in range(16):
                c.op("pe", lambda e: e.matmul(ps[:, :256], lhsT=ones[:], rhs=sq[:, ch, :], start=(ch == 0), stop=(ch == 15)), reads=[ones, sq], writes=[ps])
            c.op("act", lambda e: e.activation(out=r[:], in_=ps[:, :256], func=AF.Sqrt, scale=1.0 / 2048, bias=eps[:, 0:1]), reads=[ps, eps], writes=[r])
            c.op("dve", lambda e: e.reciprocal(out=r[:], in_=r[:]), reads=[r], writes=[r])
            for ch in range(16):
                c.op("dve", lambda e: e.tensor_tensor(out=tmp[:, ch, :], in0=x[:, ch, :], in1=r[:], op=ALU.mult), reads=[x, r], writes=[tmp])
                c.op("act", lambda e: e.activation(out=HT[:, ch, t0:t0 + 256], in_=tmp[:, ch, :], func=AF.Identity,
                                                   scale=sc[:, l, which, ch, b:b + 1], bias=mt[:, l, shb + ch, b:b + 1]), reads=[tmp, sc, mt], writes=[HT])
        c.end_stage()


class KB2(KB):
    def wbufs(self, KC, n=3):
        return [self.c.sb([128, KC, 256], BF16, "wb") for _ in range(n)]

    def lin_fm(self, W, blocks, IN, KC, epi, tiles=TILES, wb=None, prep=None):
        c = self.c
        if wb is None:
            wb = self.wbufs(KC)
        for bi, (c0, ncol) in enumerate(blocks):
            w = wb[bi % len(wb)]
            c.dma("pool", w[:, :KC, :ncol], W[:, c0:c0 + ncol].rearrange("(k p) n -> p k n", p=128), writes=[w])
            aux = prep(w, c0, ncol) if prep else None
            for off in range(0, ncol, 128):
                n = min(128, ncol - off)
                for ti, (t0, nt) in enumerate(tiles):
                    ps = self.pb()
                    for k in range(KC):
                        c.op("pe", lambda e: e.matmul(ps[:n, :nt], lhsT=w[:, k, off:off + n], rhs=IN[:, k, t0:t0 + nt], start=(k == 0), stop=(k == KC - 1)),
                             reads=[w, IN], writes=[ps])
                    epi(c0 + off, n, ti, t0, nt, ps, (aux, off))

    def lin_tm(self, W, blocks, IN, KC, epi, ttiles, wb=None):
        c = self.c
        if wb is None:
            wb = self.wbufs(KC)
        for bi, (c0, ncol) in enumerate(blocks):
            w = wb[bi % len(wb)]
            c.dma("pool", w[:, :KC, :ncol], W[:, c0:c0 + ncol].rearrange("(k p) n -> p k n", p=128), writes=[w])
            for tt in ttiles:
                ps = self.pb()
                for k in range(KC):
                    c.op("pe", lambda e: e.matmul(ps[:, :ncol], lhsT=IN[:, k, tt * 128:(tt + 1) * 128], rhs=w[:, k, :ncol], start=(k == 0), stop=(k == KC - 1)),
                         reads=[w, IN], writes=[ps])
                epi(c0, ncol, tt, ps)

    @staticmethod
    def blocks(a, b, step=256):
        return [(x, min(step, b - x)) for x in range(a, b, step)]

    def conv_row(self, src, dst, wt, ci, ntap_chunks):
        c = self.c
        w0 = wt[:, 0 * ntap_chunks + ci:0 * ntap_chunks + ci + 1]
        w1 = wt[:, 1 * ntap_chunks + ci:1 * ntap_chunks + ci + 1]
        w2 = wt[:, 2 * ntap_chunks + ci:2 * ntap_chunks + ci + 1]
        c.op("act", lambda e: e.activation(out=dst[:], in_=src[:], func=AF.Copy, scale=w1), reads=[src, wt], writes=[dst])
        sp = src[:, 0:1024].rearrange("p (s t) -> p s t", t=256)
        dp = dst[:, 0:1024].rearrange("p (s t) -> p s t", t=256)
        c.op("dve", lambda e: e.scalar_tensor_tensor(out=dp[:, :, 1:256], in0=sp[:, :, 0:255], scalar=w0, in1=dp[:, :, 1:256], op0=ALU.mult, op1=ALU.add), reads=[src, wt, dst], writes=[dst])
        c.op("dve", lambda e: e.scalar_tensor_tensor(out=dp[:, :, 0:255], in0=sp[:, :, 1:256], scalar=w2, in1=dp[:, :, 0:255], op0=ALU.mult, op1=ALU.add), reads=[src, wt, dst], writes=[dst])
        c.op("dve", lambda e: e.scalar_tensor_tensor(out=dst[:, 1025:2048], in0=src[:, 1024:2047], scalar=w0, in1=dst[:, 1025:2048], op0=ALU.mult, op1=ALU.add), reads=[src, wt, dst], writes=[dst])
        c.op("dve", lambda e: e.scalar_tensor_tensor(out=dst[:, 1024:2047], in0=src[:, 1025:2048], scalar=w2, in1=dst[:, 1024:2047], op0=ALU.mult, op1=ALU.add), reads=[src, wt, dst], writes=[dst])

    def stage_proj(self, l):
        c = self.c
        I = self.I
        W = I["w_in"][l]
        HT = self.HT
        c.begin_stage()
        wb = self.wbufs(16, 3)
        QK = self.scr("QK", [8, 128, NT], BF16)
        VTM = self.scr("VTM", [16, 128, 1024], BF16)
        KTM = self.scr("KTM", [8, 128, 512], BF16)
        SRG = self.scr("SRG", [8, 128, NT], BF16)
        HV = self.scr("HV", [8, 128, NT], BF16)
        HX = self.scr("HX", [16, 128, NT], F32)
        DQK = self.scr("DQK", [16, 128, NT], BF16)
        DVT = self.scr("DVT", [8, 128, NT], BF16)
        SDZ = self.scr("SDZ", [8, 128, NT], BF16)
        GATE = self.scr("GATE", [48, 128, NT], BF16)
        rowb = [c.sb([128, NT], BF16, "rowb") for _ in range(2)]
        rowf = [c.sb([128, NT], F32, "rowf") for _ in range(2)]
        rowg = [c.sb([128, NT], F32, "rowg") for _ in range(2)]
        cnt = [0]

        ropeC = c.sb([128, 1024], F32, "ropeC")
        ropeS = c.sb([128, 1024], F32, "ropeS")
        c.dma("sp", ropeC[:], I["ropeC"], writes=[ropeC])
        c.dma("sp", ropeS[:], I["ropeS"], writes=[ropeS])
        wperm = [c.sb([128, 16, 256], BF16, "wperm") for _ in range(2)]
        t1 = c.sb([128, 512], F32, "t1")
        t2 = c.sb([128, 512], F32, "t2")
        pc = [0]

        def prep_qk(w, c0, ncol):
            wp = wperm[pc[0] % 2]
            pc[0] += 1
            src = w[:].rearrange("p k (a two s) -> p (k a) two s", two=2, s=16)
            dst = wp[:].rearrange("p k (a two s) -> p (k a) two s", two=2, s=16)
            c.op("dve", lambda e: e.tensor_copy(out=dst[:, :, 0, :], in_=src[:, :, 1, :]), reads=[w], writes=[wp])
            c.op("pool", lambda e: e.tensor_copy(out=dst[:, :, 1, :], in_=src[:, :, 0, :]), reads=[w], writes=[wp])
            return wp

        def epi_qk(col0, n, ti, t0, nt, ps, auxoff):
            wp, off = auxoff
            ci = col0 // 128
            row = rowb[ci % 2]
            scale = 1.0 if ci < 4 else 0.125
            if ti < 2:
                c.op("act", lambda e: e.activation(out=row[:, t0:t0 + nt], in_=ps[:, :nt], func=AF.Copy, scale=scale), reads=[ps], writes=[row])
            else:
                ps2 = self.pb()
                for k in range(16):
                    c.op("pe", lambda e: e.matmul(ps2[:, :nt], lhsT=wp[:, k, off:off + 128], rhs=HT[:, k, t0:t0 + nt], start=(k == 0), stop=(k == 15)), reads=[wp, HT], writes=[ps2])
                s0 = t0 - 1024
                c.op("dve", lambda e: e.tensor_tensor(out=t1[:, :nt], in0=ps[:, :nt], in1=ropeC[:, s0:s0 + nt], op=ALU.mult), reads=[ps, ropeC], writes=[t1])
                c.op("dve", lambda e: e.tensor_tensor(out=t2[:, :nt], in0=ps2[:, :nt], in1=ropeS[:, s0:s0 + nt], op=ALU.mult), reads=[ps2, ropeS], writes=[t2])
                c.op("dve", lambda e: e.tensor_tensor(out=t1[:, :nt], in0=t1[:, :nt], in1=t2[:, :nt], op=ALU.add), reads=[t1, t2], writes=[t1])
                c.op("act", lambda e: e.activation(out=row[:, t0:t0 + nt], in_=t1[:, :nt], func=AF.Copy, scale=scale), reads=[t1], writes=[row])
            if ti == 3:
                c.dma("sp", QK[ci], row[:], reads=[row])

        self.lin_fm(W, self.blocks(0, 1024), HT, 16, epi_qk, wb=wb, prep=prep_qk)

        stv = [c.sb([128, 256], BF16, "stv") for _ in range(4)]

        def epi_v(col0, ncol, tt, ps):
            s = stv[cnt[0] % 4]
            cnt[0] += 1
            if cnt[0] % 2:
                c.op("dve", lambda e: e.tensor_copy(out=s[:, :ncol], in_=ps[:, :ncol]), reads=[ps], writes=[s])
            else:
                c.op("act", lambda e: e.copy(out=s[:, :ncol], in_=ps[:, :ncol]), reads=[ps], writes=[s])
            c.dma("sp", VTM[tt][:, col0 - 1024:col0 - 1024 + ncol], s[:, :ncol], reads=[s])

        def epi_ktm(col0, ncol, tt, ps):
            s = stv[cnt[0] % 4]
            cnt[0] += 1
            c.op("act", lambda e: e.activation(out=s[:, :ncol], in_=ps[:, :ncol], func=AF.Copy, scale=0.125), reads=[ps], writes=[s])
            c.dma("sp", KTM[tt][:, col0 - 512:col0 - 512 + ncol], s[:, :ncol], reads=[s])

        self.lin_tm(W, self.blocks(1024, 2048), HT, 16, epi_v, range(16), wb=wb)
        self.lin_tm(W, self.blocks(512, 1024), HT, 16, epi_ktm, range(8), wb=wb)

        def mk_epi_act(base, dst, func):
            def epi(col0, n, ti, t0, nt, ps, aux):
                ci = (col0 - base) // 128
                row = rowb[ci % 2]
                c.op("act", lambda e: e.activation(out=row[:, t0:t0 + nt], in_=ps[:, :nt], func=func), reads=[ps], writes=[row])
                if ti == 3:
                    c.dma("sp", dst[ci], row[:], reads=[row])
            return epi

        self.lin_fm(W, self.blocks(2048, 3072), HT, 16, mk_epi_act(2048, SRG, AF.Silu), wb=wb)
        self.lin_fm(W, self.blocks(9216, 10240), HT, 16, mk_epi_act(9216, SDZ, AF.Silu), wb=wb)
        self.lin_fm(W, self.blocks(10272, 16416), HT, 16, mk_epi_act(10272, GATE, AF.Sigmoid), wb=wb)

        hyw = c.sb([128, 72], F32, "hyw")
        self.load_cols(hyw, hyw[:], I["hy_short"][l].rearrange("k (n p) -> (k n) p", p=128), 72)
        dnw = c.sb([128, 72], F32, "dnw")
        self.load_cols(dnw, dnw[:], I["dn_conv"][l].rearrange("k (n p) -> (k n) p", p=128), 72)

        def epi_hy(col0, n, ti, t0, nt, ps, aux):
            ci = (col0 - 3072) // 128
            raw = rowf[ci % 2]
            c.op("act", lambda e: e.copy(out=raw[:, t0:t0 + nt], in_=ps[:, :nt]), reads=[ps], writes=[raw])
            if ti == 3:
                cv = rowg[ci % 2]
                self.conv_row(raw, cv, hyw, ci, 24)
                if ci < 8:
                    row = rowb[ci % 2]
                    c.op("pool", lambda e: e.tensor_copy(out=row[:], in_=cv[:]), reads=[cv], writes=[row])
                    c.dma("sp", HV[ci], row[:], reads=[row])
                else:
                    c.dma("sp", HX[ci - 8], cv[:], reads=[cv])

        self.lin_fm(W, self.blocks(3072, 6144), HT, 16, epi_hy, wb=wb)

        sqb = c.sb([128, NT], BF16, "sqb")
        rn = c.sb([128, 512], F32, "rn")
        ones, eps = self.onesB, self.epsT

        def epi_dn(col0, n, ti, t0, nt, ps, aux):
            ci = (col0 - 6144) // 128
            raw = rowf[ci % 2]
            c.op("act", lambda e: e.copy(out=raw[:, t0:t0 + nt], in_=ps[:, :nt]), reads=[ps], writes=[raw])
            if ti == 3:
                cv = rowg[ci % 2]
                self.conv_row(raw, cv, dnw, ci, 24)
                c.op("act", lambda e: e.activation(out=cv[:], in_=cv[:], func=AF.Silu), reads=[cv], writes=[cv])
                row = rowb[ci % 2]
                if ci < 16:
                    c.op("act", lambda e: e.activation(out=sqb[:], in_=cv[:], func=AF.Square), reads=[cv], writes=[sqb])
                    for tj, (u0, nu) in enumerate(TILES):
                        p2 = self.pb()
                        c.op("pe", lambda e: e.matmul(p2[:, :nu], lhsT=ones[:], rhs=sqb[:, u0:u0 + nu], start=True, stop=True), reads=[ones, sqb], writes=[p2])
                        c.op("act", lambda e: e.activation(out=rn[:, :nu], in_=p2[:, :nu], func=AF.Sqrt, bias=eps[:, 0:1]), reads=[p2, eps], writes=[rn])
                        c.op("dve", lambda e: e.reciprocal(out=rn[:, :nu], in_=rn[:, :nu]), reads=[rn], writes=[rn])
                        sc_ = (128 ** -0.5) if ci < 8 else 1.0
                        c.op("dve", lambda e: e.scalar_tensor_tensor(out=row[:, u0:u0 + nu], in0=cv[:, u0:u0 + nu], scalar=sc_, in1=rn[:, :nu], op0=ALU.mult, op1=ALU.mult),
                             reads=[cv, rn], writes=[row])
                    c.dma("sp", DQK[ci], row[:], reads=[row])
                else:
                    c.op("pool", lambda e: e.tensor_copy(out=row[:], in_=cv[:]), reads=[cv], writes=[row])
                    c.dma("sp", DVT[ci - 16], row[:], reads=[row])

        self.lin_fm(W, self.blocks(6144, 9216), HT, 16, epi_dn, wb=wb)

        gbraw = self.gbraw

        def epi_gb(col0, ncol, tt, ps):
            c.op("dve", lambda e: e.tensor_copy(out=gbraw[:, tt, :], in_=ps[:, :32]), reads=[ps], writes=[gbraw])

        self.lin_tm(W, [(10240, 32)], HT, 16, epi_gb, range(16), wb=wb)
        c.end_stage()

    def stage_merge(self, l, XTin, XTout):
        c = self.c
        I = self.I
        c.begin_stage()
        MO = self.scr("MO", [3, 8, 128, NT], BF16)
        GATE = self.S["GATE"]
        MIX = self.scr("MIX", [16, 128, NT], BF16)
        mo = [c.sb([128, 8, NT], BF16, "mo") for _ in range(3)]
        for b in range(3):
            c.dma("sp", mo[b][:], MO[b].rearrange("c p t -> p c t"), writes=[mo[b]])
        wp = [c.sb([128, 8, 128], BF16, "wp") for _ in range(4)]
        gr = [c.sb([128, NT], BF16, "gr") for _ in range(4)]
        acc = [c.sb([128, NT], F32, "acc") for _ in range(2)]
        mixb = [c.sb([128, NT], BF16, "mixb") for _ in range(2)]
        tmp = c.sb([128, 512], F32, "mtmp")
        PW = [I["p_ret"][l], I["p_hy"][l], I["p_dn"][l]]
        n = 0
        for j in range(16):
            a = acc[j % 2]
            for b in range(3):
                w = wp[n % 4]
                g = gr[n % 4]
                n += 1
                c.dma("pool", w[:], PW[b][:, j * 128:(j + 1) * 128].rearrange("(k p) n -> p k n", p=128), writes=[w])
                c.dma("sp", g[:], GATE[b * 16 + j], writes=[g])
                for ti, (t0, nt) in enumerate(TILES):
                    ps = self.pb()
                    for k in range(8):
                        c.op("pe", lambda e: e.matmul(ps[:, :nt], lhsT=w[:, k, :], rhs=mo[b][:, k, t0:t0 + nt], start=(k == 0), stop=(k == 7)), reads=[w, mo[b]], writes=[ps])
                    if b == 0:
                        c.op("dve", lambda e: e.tensor_tensor(out=a[:, t0:t0 + nt], in0=ps[:, :nt], in1=g[:, t0:t0 + nt], op=ALU.mult), reads=[ps, g], writes=[a])
                    else:
                        c.op("dve", lambda e: e.tensor_tensor(out=tmp[:, :nt], in0=ps[:, :nt], in1=g[:, t0:t0 + nt], op=ALU.mult), reads=[ps, g], writes=[tmp])
                        c.op("pool", lambda e: e.tensor_tensor(out=a[:, t0:t0 + nt], in0=a[:, t0:t0 + nt], in1=tmp[:, :nt], op=ALU.add), reads=[a, tmp], writes=[a])
            m = mixb[j % 2]
            c.op("act", lambda e: e.copy(out=m[:], in_=a[:]), reads=[a], writes=[m])
            c.dma("sp", MIX[j], m[:], reads=[m])
        c.end_stage()
        c.begin_stage()
        mix = c.sb([128, 16, NT], BF16, "mix")
        c.dma("sp", mix[:], MIX.rearrange("c p t -> p c t"), writes=[mix])
        self.resid_epilogue(I["w_o"][l], 2048, mix, 16, l, 32, XTin, XTout)
        c.end_stage()

    def resid_epilogue(self, W, ncols, IN, KC, l, gbase, XTin, XTout, tiles=TILES, wb=None):
        c = self.c
        mt = self.modT
        xr = [c.sb([128, NT], F32, "xr") for _ in range(2)]
        lo = tiles[0][0]
        hi = tiles[-1][0] + tiles[-1][1]

        def epi(col0, n, ti, t0, nt, ps, aux):
            j = col0 // 128
            x = xr[j % 2]
            if ti == 0:
                c.dma("sp", x[:, lo:hi], XTin[j][:, lo:hi], writes=[x])
            b = 0 if t0 < 1024 else 1
            c.op("dve", lambda e: e.scalar_tensor_tensor(out=x[:, t0:t0 + nt], in0=ps[:, :nt], scalar=mt[:, l, gbase + j, b:b + 1], in1=x[:, t0:t0 + nt], op0=ALU.mult, op1=ALU.add),
                 reads=[ps, mt, x], writes=[x])
            if ti == len(tiles) - 1:
                c.dma("sp", XTout[j][:, lo:hi], x[:, lo:hi], reads=[x])

        self.lin_fm(W, self.blocks(0, ncols), IN, KC, epi, tiles=tiles, wb=wb)

    def stage_ffn_up(self, l):
        c = self.c
        I = self.I
        ACTT = self.scr("ACTT", [43, 128, NT], BF16)
        c.begin_stage()
        fw = c.sb([128, 3 * 86], F32, "fw")
        for k in range(3):
            self.load_cols(fw, fw[:, k * 86:(k + 1) * 86], I["ffn_conv"][l][k].rearrange("(n p) -> n p", p=128), 86)
        rowf = [c.sb([128, NT], F32, "frow") for _ in range(2)]
        ga = [c.sb([128, NT], F32, "ga") for _ in range(2)]
        gb = c.sb([128, NT], F32, "gb")
        ab = [c.sb([128, NT], BF16, "ab") for _ in range(2)]
        wb = self.wbufs(16, 3)
        HT = self.HT
        W = I["w_up"][l]
        cnt = [0]

        def epi(col0, n, ti, t0, nt, ps, aux):
            ci = col0 // 128
            raw = rowf[cnt[0] % 2]
            c.op("act", lambda e: e.copy(out=raw[:, t0:t0 + nt], in_=ps[:, :nt]), reads=[ps], writes=[raw])
            if ti == 3:
                cnt[0] += 1
                if ci < 43:
                    g = ga[ci % 2]
                    self.conv_row(raw, g, fw, ci, 86)
                    c.op("act", lambda e: e.activation(out=g[:], in_=g[:], func=AF.Silu), reads=[g], writes=[g])
                else:
                    g = ga[(ci - 43) % 2]
                    self.conv_row(raw, gb, fw, ci, 86)
                    a = ab[ci % 2]
                    c.op("pool", lambda e: e.tensor_tensor(out=a[:], in0=g[:], in1=gb[:], op=ALU.mult), reads=[g, gb], writes=[a])
                    c.dma("sp", ACTT[ci - 43], a[:], reads=[a])

        blocks = []
        for ci in range(43):
            blocks.append((ci * 128, 128))
            blocks.append((5504 + ci * 128, 128))
        self.lin_fm(W, blocks, HT, 16, epi, wb=wb)
        c.end_stage()

    def stage_ffn_down(self, l, XTin, XTout):
        c = self.c
        I = self.I
        ACTT = self.S["ACTT"]
        for half in range(2):
            c.begin_stage()
            a = c.sb([128, 43, 1024], BF16, "actin")
            c.dma("sp", a[:], ACTT[:, :, half * 1024:(half + 1) * 1024].rearrange("c p t -> p c t"), writes=[a])

            class Shift:
                def __init__(s, buf, sh):
                    s.buf, s.sh = buf, sh
            wb = self.wbufs(43, 2)
            tiles = [(half * 1024, 512), (half * 1024 + 512, 512)]
            self._resid_shift(I["w_down"][l], a, 43, l, 80, XTin, XTout, tiles, half * 1024, wb)
            c.end_stage()

    def _resid_shift(self, W, IN, KC, l, gbase, XTin, XTout, tiles, tshift, wb):
        c = self.c
        mt = self.modT
        xr = [c.sb([128, 1024], F32, "xr2") for _ in range(2)]
        blocks = self.blocks(0, 2048)
        for bi, (c0, ncol) in enumerate(blocks):
            w = wb[bi % len(wb)]
            c.dma("pool", w[:, :KC, :ncol], W[:, c0:c0 + ncol].rearrange("(k p) n -> p k n", p=128), writes=[w])
            for off in range(0, ncol, 128):
                j = (c0 + off) // 128
                x = xr[j % 2]
                c.dma("sp", x[:], XTin[j][:, tshift:tshift + 1024], writes=[x])
                for ti, (t0, nt) in enumerate(tiles):
                    ps = self.pb()
                    for k in range(KC):
                        c.op("pe", lambda e: e.matmul(ps[:, :nt], lhsT=w[:, k, off:off + 128], rhs=IN[:, k, t0 - tshift:t0 - tshift + nt], start=(k == 0), stop=(k == KC - 1)),
                             reads=[w, IN], writes=[ps])
                    b = 0 if t0 < 1024 else 1
                    c.op("dve", lambda e: e.scalar_tensor_tensor(out=x[:, t0 - tshift:t0 - tshift + nt], in0=ps[:, :nt], scalar=mt[:, l, gbase + j, b:b + 1],
                                                                 in1=x[:, t0 - tshift:t0 - tshift + nt], op0=ALU.mult, op1=ALU.add), reads=[ps, mt, x], writes=[x])
                c.dma("sp", XTout[j][:, tshift:tshift + 1024], x[:], reads=[x])

    def stage_final(self, XT):
        c = self.c
        I = self.I
        c.begin_stage()
        nf = c.sb([128, 16], F32, "nf")
        self.load_cols(nf, nf[:], I["norm_f"].rearrange("(n p) -> n p", p=128), 16)
        xs = [c.sb([128, 16, 128], F32, "fx") for _ in range(2)]
        sq = c.sb([128, 16, 128], BF16, "fsq")
        r = c.sb([128, 128], F32, "fr")
        yo = [c.sb([128, 2048], F32, "yo") for _ in range(2)]
        ones, eps, idf = self.onesB, self.epsT, self.identF
        for tt in range(16):
            t0 = tt * 128
            x = xs[tt % 2]
            y = yo[tt % 2]
            c.dma("sp", x[:], XT[:, :, t0:t0 + 128].rearrange("c p t -> p c t"), writes=[x])
            c.op("act", lambda e: e.activation(out=sq[:], in_=x[:], func=AF.Square), reads=[x], writes=[sq])
            ps = self.pb()
            for ch in range(16):
                c.op("pe", lambda e: e.matmul(ps[:, :128], lhsT=ones[:], rhs=sq[:, ch, :], start=(ch == 0), stop=(ch == 15)), reads=[ones, sq], writes=[ps])
            c.op("act", lambda e: e.activation(out=r[:], in_=ps[:, :128], func=AF.Sqrt, scale=1.0 / 2048, bias=eps[:, 0:1]), reads=[ps, eps], writes=[r])
            c.op("dve", lambda e: e.reciprocal(out=r[:], in_=r[:]), reads=[r], writes=[r])
            for ch in range(16):
                c.op("dve", lambda e: e.scalar_tensor_tensor(out=x[:, ch, :], in0=x[:, ch, :], scalar=nf[:, ch:ch + 1], in1=r[:], op0=ALU.mult, op1=ALU.mult), reads=[x, nf, r], writes=[x])
            for g in range(4):
                p2 = self.pb()
                for j in range(4):
                    ch = g * 4 + j
                    c.op("pe", lambda e: e.transpose(out=p2[:, j * 128:(j + 1) * 128], in_=x[:, ch, :], identity=idf[:]), reads=[x, idf], writes=[p2])
                if g % 2 == 0:
                    c.op("dve", lambda e: e.tensor_copy(out=y[:, g * 512:(g + 1) * 512], in_=p2[:]), reads=[p2], writes=[y])
                else:
                    c.op("act", lambda e: e.copy(out=y[:, g * 512:(g + 1) * 512], in_=p2[:]), reads=[p2], writes=[y])
            c.dma("sp", self.O["y"][t0:t0 + 128, :], y[:], reads=[y])
        c.end_stage()


class KB3(KB2):
    def stage_ret(self, l):
        c = self.c
        I = self.I
        c.begin_stage()
        QK, VTM, KTM, SRG = self.S["QK"], self.S["VTM"], self.S["KTM"], self.S["SRG"]
        MO = self.scr("MO", [3, 8, 128, NT], BF16)
        OSR = self.O["osret"]
        ones, eps = self.onesB, self.epsT
        V = c.sb([128, 16, 1024], BF16, "V")
        c.dma("sp", V[:], VTM.rearrange("t p n -> p t n"), writes=[V])
        Kt = c.sb([128, 8, 512], BF16, "Kt")
        c.dma("sp", Kt[:], KTM.rearrange("t p n -> p t n"), writes=[Kt])
        lg = c.sb([128, 16], F32, "lg")
        c.dma("sp", lg[:], I["ret_decay"][l].rearrange("d h -> (d h)").partition_broadcast(128), writes=[lg])
        c.op("act", lambda e: e.activation(out=lg[:], in_=lg[:], func=AF.Exp, scale=-1.0), reads=[lg], writes=[lg])
        c.op("act", lambda e: e.activation(out=lg[:], in_=lg[:], func=AF.Ln, bias=self.onesF[:, 0:1]), reads=[lg], writes=[lg])
        c.op("dve", lambda e: e.tensor_scalar(out=lg[:], in0=lg[:], scalar1=-1.0, scalar2=None, op0=ALU.mult), reads=[lg], writes=[lg])
        tabs = {}
        for nm, w in (("S", 1920), ("P", 384)):
            for k in ("dpos", "dneg", "dz"):
                t = c.sb([128, w], F32, k + nm)
                c.dma("sp", t[:], I[k + nm], writes=[t])
                tabs[k + nm] = t
        expfb = c.sb([128, 4], F32, "expfb")
        c.dma("sp", expfb[:], I["expfb"], writes=[expfb])
        idx1 = c.sb([128, 1024], F32, "idx1")
        idx2 = c.sb([128, 1024], F32, "idx2")
        c.dma("sp", idx1[:], I["idx1"], writes=[idx1])
        c.dma("sp", idx2[:], I["idx2"], writes=[idx2])
        KD = c.sb([128, 2, 8, 2], F32, "KD")
        for d in range(2):
            for h in range(8):
                c.op("act", lambda e: e.activation(out=KD[:, d, h, :], in_=expfb[:, d * 2:d * 2 + 2], func=AF.Exp, scale=lg[:, d * 8 + h:d * 8 + h + 1]), reads=[expfb, lg], writes=[KD])
        TabS = c.sb([128, 1920], F32, "TabS")
        TabP = c.sb([128, 384], F32, "TabP")
        ttmp = c.sb([128, 1920], F32, "ttmp")
        qrow = c.sb([128, NT], BF16, "qrow")
        krow = c.sb([128, NT], BF16, "krow")
        lgsel = c.sb([128, 2], F32, "lgsel")
        dec = c.sb([128, 1024], F32, "dec")
        qdf = c.sb([128, 1024], BF16, "qdf")
        qdb = c.sb([128, 1024], BF16, "qdb")
        S0 = c.sb([128, 2, 128], BF16, "S0")
        srg = c.sb([128, NT], BF16, "srg")
        sm = [c.sb([128, 512], BF16, "sm") for _ in range(3)]
        osb = c.sb([128, 512], F32, "osb")
        osq = c.sb([128, 512], BF16, "osq")
        rr = c.sb([128, 512], F32, "rr")
        orow = c.sb([128, NT], BF16, "orow")
        kf = [c.sb([128, 64], BF16, "kf") for _ in range(4)]
        sst = [c.sb([64, 128], F32, "sst") for _ in range(4)]
        n_sm = [0]
        n_kf = [0]

        def build_tab(Tab, nm, w, h):
            dp, dn, dz = tabs["dpos" + nm], tabs["dneg" + nm], tabs["dz" + nm]
            c.op("dve", lambda e: e.tensor_scalar(out=ttmp[:, :w], in0=dp[:], scalar1=lg[:, h:h + 1], scalar2=None, op0=ALU.mult), reads=[dp, lg], writes=[ttmp])
            c.op("dve", lambda e: e.scalar_tensor_tensor(out=ttmp[:, :w], in0=dn[:], scalar=lg[:, 8 + h:9 + h], in1=ttmp[:, :w], op0=ALU.mult, op1=ALU.add), reads=[dn, lg, ttmp], writes=[ttmp])
            c.op("act", lambda e: e.activation(out=ttmp[:, :w], in_=ttmp[:, :w], func=AF.Exp), reads=[ttmp], writes=[ttmp])
            c.op("dve", lambda e: e.tensor_tensor(out=Tab[:, :w], in0=ttmp[:, :w], in1=dz[:], op=ALU.add), reads=[ttmp, dz], writes=[Tab])

        def finish_o(pso, n, h, tok0):
            c.op("act", lambda e: e.copy(out=osb[:, :n], in_=pso[:, :n]), reads=[pso], writes=[osb])
            c.op("act", lambda e: e.activation(out=osq[:, :n], in_=osb[:, :n], func=AF.Square), reads=[osb], writes=[osq])
            p2 = self.pb()
            c.op("pe", lambda e: e.matmul(p2[:, :n], lhsT=ones[:], rhs=osq[:, :n], start=True, stop=True), reads=[ones, osq], writes=[p2])
            c.op("act", lambda e: e.activation(out=rr[:, :n], in_=p2[:, :n], func=AF.Sqrt, scale=1.0 / 128, bias=eps[:, 0:1]), reads=[p2, eps], writes=[rr])
            c.op("dve", lambda e: e.reciprocal(out=rr[:, :n], in_=rr[:, :n]), reads=[rr], writes=[rr])
            c.op("dve", lambda e: e.tensor_tensor(out=osb[:, :n], in0=osb[:, :n], in1=rr[:, :n], op=ALU.mult), reads=[osb, rr], writes=[osb])
            c.op("dve", lambda e: e.tensor_tensor(out=orow[:, tok0:tok0 + n], in0=osb[:, :n], in1=srg[:, tok0:tok0 + n], op=ALU.mult), reads=[osb, srg], writes=[orow])

        for h in range(8):
            hp, po = h // 2, (h % 2) * 64
            if h % 2 == 0:
                c.dma("sp", qrow[:], QK[hp], writes=[qrow])
                c.dma("sp", krow[:], QK[4 + hp], writes=[krow])
                for d in range(2):
                    c.op("dve", lambda e: e.tensor_copy(out=lgsel[0:64, d:d + 1], in_=lg[0:64, d * 8 + h:d * 8 + h + 1]), reads=[lg], writes=[lgsel])
                    c.op("dve", lambda e: e.tensor_copy(out=lgsel[64:128, d:d + 1], in_=lg[64:128, d * 8 + h + 1:d * 8 + h + 2]), reads=[lg], writes=[lgsel])
                c.op("act", lambda e: e.activation(out=dec[:], in_=idx1[:], func=AF.Exp, scale=lgsel[:, 0:1]), reads=[idx1, lgsel], writes=[dec])
                c.op("dve", lambda e: e.tensor_tensor(out=qdf[:], in0=qrow[:, 1024:2048], in1=dec[:], op=ALU.mult), reads=[qrow, dec], writes=[qdf])
                c.op("act", lambda e: e.activation(out=dec[:], in_=idx2[:], func=AF.Exp, scale=lgsel[:, 1:2]), reads=[idx2, lgsel], writes=[dec])
                c.op("dve", lambda e: e.tensor_tensor(out=qdb[:], in0=qrow[:, 1024:2048], in1=dec[:], op=ALU.mult), reads=[qrow, dec], writes=[qdb])
            c.dma("sp", srg[:], SRG[h], writes=[srg])
            for d in range(2):
                c.dma("pool", S0[po:po + 64, d, :], I["sret"][l, d, h], writes=[S0])
            build_tab(TabS, "S", 1920, h)
            build_tab(TabP, "P", 384, h)
            for i0 in (0, 512):
                pso = self.pacc()
                for jc in range(8):
                    pss = self.pb()
                    c.op("pe", lambda e: e.matmul(pss[:, :512], lhsT=krow[po:po + 64, 1024 + jc * 128:1024 + (jc + 1) * 128], rhs=qrow[po:po + 64, 1024 + i0:1024 + i0 + 512], start=True, stop=True),
                         reads=[krow, qrow], writes=[pss])
                    s = sm[n_sm[0] % 3]
                    n_sm[0] += 1
                    u0 = i0 - 128 * jc + 896
                    c.op("dve", lambda e: e.tensor_tensor(out=s[:], in0=pss[:, :512], in1=TabS[:, u0:u0 + 512], op=ALU.mult), reads=[pss, TabS], writes=[s])
                    c.op("pe", lambda e: e.matmul(pso[:, :512], lhsT=V[:, 8 + jc, h * 128:(h + 1) * 128], rhs=s[:], start=(jc == 0), stop=False), reads=[V, s], writes=[pso])
                c.op("pe", lambda e: e.matmul(pso[:, :512], lhsT=S0[po:po + 64, 0, :], rhs=qdf[po:po + 64, i0:i0 + 512], start=False, stop=False), reads=[S0, qdf], writes=[pso])
                c.op("pe", lambda e: e.matmul(pso[:, :512], lhsT=S0[po:po + 64, 1, :], rhs=qdb[po:po + 64, i0:i0 + 512], start=False, stop=True), reads=[S0, qdb], writes=[pso])
                finish_o(pso, 512, h, 1024 + i0)
            for s_ in range(4):
                b0 = s_ * 256
                pso = self.pacc()
                for jc in range(2):
                    pss = self.pb()
                    c.op("pe", lambda e: e.matmul(pss[:, :256], lhsT=krow[po:po + 64, b0 + jc * 128:b0 + (jc + 1) * 128], rhs=qrow[po:po + 64, b0:b0 + 256], start=True, stop=True),
                         reads=[krow, qrow], writes=[pss])
                    s = sm[n_sm[0] % 3]
                    n_sm[0] += 1
                    u0 = 128 - 128 * jc
                    c.op("dve", lambda e: e.tensor_tensor(out=s[:, :256], in0=pss[:, :256], in1=TabP[:, u0:u0 + 256], op=ALU.mult), reads=[pss, TabP], writes=[s])
                    c.op("pe", lambda e: e.matmul(pso[:, :256], lhsT=V[:, s_ * 2 + jc, h * 128:(h + 1) * 128], rhs=s[:, :256], start=(jc == 0), stop=(jc == 1)), reads=[V, s], writes=[pso])
                finish_o(pso, 256, h, b0)
                for d in range(2):
                    pst = self.pacc()
                    for tt in range(2):
                        k_ = kf[n_kf[0] % 4]
                        n_kf[0] += 1
                        c.op("dve", lambda e: e.tensor_scalar(out=k_[:], in0=Kt[:, s_ * 2 + tt, h * 64:(h + 1) * 64], scalar1=KD[:, d, h, tt:tt + 1], scalar2=None, op0=ALU.mult), reads=[Kt, KD], writes=[k_])
                        c.op("pe", lambda e: e.matmul(pst[:64, :128], lhsT=k_[:], rhs=V[:, s_ * 2 + tt, h * 128:(h + 1) * 128], start=(tt == 0), stop=(tt == 1)), reads=[k_, V], writes=[pst])
                    st = sst[(s_ * 2 + d) % 4]
                    c.op("act", lambda e: e.copy(out=st[:], in_=pst[:64, :128]), reads=[pst], writes=[st])
                    c.dma("sp", OSR[s_, l, d, h], st[:], reads=[st])
            c.dma("sp", MO[0][h], orow[:], reads=[orow])
        c.end_stage()


import os


class KB4(KB3):
    def stage_dn(self, l):
        c = self.c
        I = self.I
        c.begin_stage()
        DQK, DVT, SDZ = self.S["DQK"], self.S["DVT"], self.S["SDZ"]
        MO = self.scr("MO", [3, 8, 128, NT], BF16)
        OSD = self.O["osdn"]
        onesF, onesB, eps, idF, idB = self.onesF, self.onesB, self.epsT, self.identF, self.identB
        gbraw = self.gbraw
        cm = {}
        for nm in ("UF", "UB", "SU", "SL"):
            t = c.sb([128, 128], F32, nm)
            c.dma("sp", t[:], I[nm], writes=[t])
            cm[nm] = t
        alog = c.sb([128, 16], F32, "alog")
        dtb = c.sb([128, 16], F32, "dtb")
        c.dma("sp", alog[:], I["dn_a_log"][l].rearrange("d h -> (d h)").partition_broadcast(128), writes=[alog])
        c.dma("sp", dtb[:], I["dn_dt_bias"][l].rearrange("d h -> (d h)").partition_broadcast(128), writes=[dtb])
        dnn = c.sb([128, 1], F32, "dnn")
        c.dma("sp", dnn[:], I["dn_norm"][l].rearrange("(p o) -> p o", o=1), writes=[dnn])
        c.op("act", lambda e: e.activation(out=alog[:], in_=alog[:], func=AF.Exp), reads=[alog], writes=[alog])
        c.op("dve", lambda e: e.tensor_scalar(out=alog[:], in0=alog[:], scalar1=-1.0, scalar2=None, op0=ALU.mult), reads=[alog], writes=[alog])
        G = c.sb([128, 16, 16], F32, "G")
        BT = c.sb([128, 16, 16], F32, "BT")
        NBT = c.sb([128, 16, 16], F32, "NBT")
        for tt in range(16):
            c.op("dve", lambda e: e.tensor_tensor(out=G[:, tt, :], in0=gbraw[:, tt, 0:16], in1=dtb[:], op=ALU.add), reads=[gbraw, dtb], writes=[G])
        c.op("act", lambda e: e.activation(out=G[:], in_=G[:], func=AF.Exp), reads=[G], writes=[G])
        c.op("act", lambda e: e.activation(out=G[:], in_=G[:], func=AF.Ln, bias=onesF[:, 0:1]), reads=[G, onesF], writes=[G])
        for tt in range(16):
            c.op("dve", lambda e: e.tensor_tensor(out=G[:, tt, :], in0=G[:, tt, :], in1=alog[:], op=ALU.mult), reads=[G, alog], writes=[G])
        c.op("act", lambda e: e.activation(out=BT[:], in_=gbraw[:, :, 16:32], func=AF.Sigmoid), reads=[gbraw], writes=[BT])
        c.op("dve", lambda e: e.tensor_scalar(out=NBT[:], in0=BT[:], scalar1=-1.0, scalar2=None, op0=ALU.mult), reads=[BT], writes=[NBT])
        GC = c.sb([128, 16, 16], F32, "GC")
        GL = c.sb([128, 16, 16], F32, "GL")
        for tt in range(16):
            for d in range(2):
                U = cm["UF"] if d == 0 else cm["UB"]
                ps = self.pb()
                c.op("pe", lambda e: e.matmul(ps[:, 0:8], lhsT=U[:], rhs=G[:, tt, d * 8:d * 8 + 8], start=True, stop=True), reads=[U, G], writes=[ps])
                c.op("pe", lambda e: e.matmul(ps[:, 8:16], lhsT=onesF[:], rhs=G[:, tt, d * 8:d * 8 + 8], start=True, stop=True), reads=[onesF, G], writes=[ps])
                c.op("dve", lambda e: e.tensor_copy(out=GC[:, tt, d * 8:d * 8 + 8], in_=ps[:, 0:8]), reads=[ps], writes=[GC])
                c.op("dve", lambda e: e.tensor_copy(out=GL[:, tt, d * 8:d * 8 + 8], in_=ps[:, 8:16]), reads=[ps], writes=[GL])
        BK = c.sb([128, 16, 16], F32, "BK")
        KDS = c.sb([128, 16, 16], F32, "KDS")
        EGL = c.sb([128, 16, 16], F32, "EGL")
        c.op("act", lambda e: e.activation(out=BK[:], in_=GC[:], func=AF.Exp), reads=[GC], writes=[BK])
        c.op("dve", lambda e: e.tensor_tensor(out=BK[:], in0=BK[:], in1=BT[:], op=ALU.mult), reads=[BK, BT], writes=[BK])
        c.op("dve", lambda e: e.tensor_tensor(out=KDS[:], in0=GL[:], in1=GC[:], op=ALU.subtract), reads=[GL, GC], writes=[KDS])
        c.op("act", lambda e: e.activation(out=KDS[:], in_=KDS[:], func=AF.Exp), reads=[KDS], writes=[KDS])
        c.op("act", lambda e: e.activation(out=EGL[:], in_=GL[:], func=AF.Exp), reads=[GL], writes=[EGL])

        if "GDBG" in self.dbg:
            gd = self.scr("GDBG", [4, 128, 16, 16], F32)
            for i_, t_ in enumerate((G, BT, GC, GL)):
                c.dma("sp", gd[i_], t_[:], reads=[t_])
        OACC = [c.sb([128, NT], F32, "oacc") for _ in range(8)]
        c.begin_stage()
        Sf = [c.sb([128, 128], F32, "Sf") for _ in range(8)]
        Sb = [c.sb([128, 128], BF16, "Sb") for _ in range(8)]
        NI = 8

        def ring(shape, dt, nm, n=NI):
            bufs = [c.sb(shape, dt, nm) for _ in range(n)]
            st = [0]

            def nxt():
                st[0] += 1
                return bufs[st[0] % n]
            return nxt
        r_q = ring([128, 128], BF16, "rq")
        r_k = ring([128, 128], BF16, "rk")
        r_v = ring([128, 128], BF16, "rv")
        r_gbc = ring([128, 128], F32, "rgbc")
        r_t = ring([128, 128], F32, "rt", 2 * NI)
        r_p = ring([128, 128], F32, "rp")
        r_pt = ring([128, 128], F32, "rpt")
        r_x = ring([128, 128], F32, "rx")
        r_vb = ring([128, 128], F32, "rvb")
        r_kbe = ring([128, 128], F32, "rkbe")
        r_kd = ring([128, 128], F32, "rkd")
        r_nw = ring([128, 128], F32, "rnw")
        r_at = ring([128, 128], BF16, "rat")
        r_qd = ring([128, 128], BF16, "rqd")
        r_vn = ring([128, 128], F32, "rvn")
        r_vnb = ring([128, 128], BF16, "rvnb")
        r_so = ring([128, 128], F32, "rso", 3)
        r_tw = ring([128, 256], F32, "rtw", 2 * NI)
        r_tq = ring([128, 256], F32, "rtq", 2 * NI)
        mlu = c.sb([128, 7, 256], F32, "mlu")
        mul = c.sb([128, 7, 256], F32, "mul")
        c.dma("sp", mlu[:], I["MLU"], writes=[mlu])
        c.dma("sp", mul[:], I["MUL"], writes=[mul])
        id2 = c.sb([128, 256], F32, "id2")
        c.dma("sp", id2[:, 0:128], I["ident"], writes=[id2])
        c.dma("sp", id2[:, 128:256], I["ident"], writes=[id2])
        PB = self.PB
        pbn = [0]

        def pbank():
            pbn[0] += 1
            return PB[pbn[0] % 8]

        def process(tt, h, d, last_seq):
            t0 = tt * 128
            col = d * 8 + h
            U = cm["UF"] if d == 0 else cm["UB"]
            MS = cm["SL"] if d == 0 else cm["SU"]
            MI = cm["UF"] if d == 0 else cm["UB"]
            gc_ap = GC[:, tt, col:col + 1]
            qT, kT, vT = r_q(), r_k(), r_v()
            c.dma("sp", qT[:], DQK[h][:, t0:t0 + 128], writes=[qT])
            c.dma("sp", kT[:], DQK[8 + h][:, t0:t0 + 128], writes=[kT])
            c.dma("sp", vT[:], DVT[h][:, t0:t0 + 128], writes=[vT])
            gbc = r_gbc()
            c.op("act", lambda e: e.activation(out=gbc[:], in_=onesF[:], func=AF.Copy, scale=G[:, tt, col:col + 1]), reads=[onesF, G], writes=[gbc])
            yield
            bank = PB[h]
            ptr = pR = pG = pT = ps1 = ps2 = pw = psv = pso = pss = bank
            ptb = bank[:].bitcast(BF16)
            c.op("pe", lambda e: e.transpose(out=ptb[:, 0:128], in_=kT[:], identity=idB[:]), reads=[kT, idB], writes=[ptr])
            c.op("pe", lambda e: e.transpose(out=ptb[:, 128:256], in_=vT[:], identity=idB[:]), reads=[vT, idB], writes=[ptr])
            c.op("pe", lambda e: e.matmul(pR[:, 128:256], lhsT=gbc[:], rhs=U[:], start=True, stop=True), reads=[gbc, U], writes=[pR])
            c.op("pe", lambda e: e.matmul(pG[:, 256:384], lhsT=kT[:], rhs=kT[:], start=True, stop=True), reads=[kT], writes=[pG])
            c.op("pe", lambda e: e.matmul(pG[:, 384:512], lhsT=kT[:], rhs=qT[:], start=True, stop=True), reads=[kT, qT], writes=[pG])
            yield
            vb, kbe, kd = r_vb(), r_kbe(), r_kd()
            c.op("dve", lambda e: e.tensor_scalar(out=kbe[:], in0=ptb[:, 0:128], scalar1=BK[:, tt, col:col + 1], scalar2=None, op0=ALU.mult), reads=[ptr, BK], writes=[kbe])
            c.op("dve", lambda e: e.tensor_scalar(out=kd[:], in0=ptb[:, 0:128], scalar1=KDS[:, tt, col:col + 1], scalar2=None, op0=ALU.mult), reads=[ptr, KDS], writes=[kd])
            c.op("dve", lambda e: e.tensor_scalar(out=vb[:], in0=ptb[:, 128:256], scalar1=BT[:, tt, col:col + 1], scalar2=None, op0=ALU.mult), reads=[ptr, BT], writes=[vb])
            d1, d2 = r_t(), r_t()
            c.op("dve", lambda e: e.tensor_scalar(out=d1[:], in0=pR[:, 128:256], scalar1=gc_ap, scalar2=0.0, op0=ALU.subtract, op1=ALU.max), reads=[pR, GC], writes=[d1])
            c.op("dve", lambda e: e.tensor_scalar(out=d2[:], in0=pR[:, 128:256], scalar1=gc_ap, scalar2=0.0, op0=ALU.subtract, op1=ALU.min), reads=[pR, GC], writes=[d2])
            er = gbc
            c.op("act", lambda e: e.activation(out=er[:], in_=pR[:, 128:256], func=AF.Exp), reads=[pR], writes=[er])
            yield
            c.op("act", lambda e: e.activation(out=d1[:], in_=d1[:], func=AF.Exp, scale=-1.0), reads=[d1], writes=[d1])
            c.op("act", lambda e: e.activation(out=d2[:], in_=d2[:], func=AF.Exp), reads=[d2], writes=[d2])
            qd = r_qd()
            c.op("pool", lambda e: e.tensor_tensor(out=qd[:], in0=qT[:], in1=er[:], op=ALU.mult), reads=[qT, er], writes=[qd])
            yield
            p0t = r_pt()
            c.op("dve", lambda e: e.scalar_tensor_tensor(out=d1[:], in0=pG[:, 256:384], scalar=NBT[:, tt, col:col + 1], in1=d1[:], op0=ALU.mult, op1=ALU.mult), reads=[pG, NBT, d1], writes=[d1])
            c.op("dve", lambda e: e.tensor_tensor(out=d2[:], in0=pG[:, 384:512], in1=d2[:], op=ALU.mult), reads=[pG, d2], writes=[d2])
            yield
            c.op("pool", lambda e: e.tensor_tensor(out=p0t[:], in0=d1[:], in1=MS[:], op=ALU.mult), reads=[d1, MS], writes=[p0t])
            at = r_at()
            c.op("pool", lambda e: e.tensor_tensor(out=at[:], in0=d2[:], in1=MI[:], op=ALU.mult), reads=[d2, MI], writes=[at])
            yield
            c.op("pe", lambda e: e.transpose(out=pT[:, :128], in_=p0t[:], identity=idF[:]), reads=[p0t, idF], writes=[pT])
            yield
            p0 = r_p()
            c.op("act", lambda e: e.copy(out=p0[:], in_=pT[:, :128]), reads=[pT], writes=[p0])
            MM = mlu if d == 0 else mul
            tw = r_tw()
            tq = r_tq()
            c.op("pool", lambda e: e.tensor_tensor(out=tq[:, 0:128], in0=p0t[:], in1=MM[:, 0, 0:128], op=ALU.mult), reads=[p0t, MM], writes=[tq])
            yield
            c.op("pool", lambda e: e.tensor_tensor(out=tq[:, 128:256], in0=p0[:], in1=MM[:, 0, 128:256], op=ALU.mult), reads=[p0, MM], writes=[tq])
            yield
            c.op("dve", lambda e: e.tensor_tensor(out=tw[:], in0=tq[:], in1=id2[:], op=ALU.add), reads=[tq, id2], writes=[tw])
            yield
            for s in range(1, 7):
                c.op("pe", lambda e: e.matmul(ps1[:, 0:128], lhsT=p0[:], rhs=tw[:, 0:128], start=True, stop=True), reads=[p0, tw], writes=[ps1])
                c.op("pe", lambda e: e.matmul(ps1[:, 128:256], lhsT=p0t[:], rhs=tw[:, 128:256], start=True, stop=True), reads=[p0t, tw], writes=[ps1])
                yield
                p1 = r_tq()
                c.op("act", lambda e: e.copy(out=p1[:], in_=ps1[:, 0:256]), reads=[ps1], writes=[p1])
                yield
                c.op("pe", lambda e: e.matmul(ps2[:, 256:384], lhsT=tw[:, 128:256], rhs=p1[:, 0:128], start=True, stop=True), reads=[tw, p1], writes=[ps2])
                c.op("pe", lambda e: e.matmul(ps2[:, 384:512], lhsT=tw[:, 0:128], rhs=p1[:, 128:256], start=True, stop=True), reads=[tw, p1], writes=[ps2])
                yield
                t2 = r_tq()
                c.op("dve", lambda e: e.tensor_tensor(out=t2[:], in0=ps2[:, 256:512], in1=MM[:, s, :], op=ALU.mult), reads=[ps2, MM], writes=[t2])
                yield
                ntw = r_tw()
                c.op("pool", lambda e: e.tensor_tensor(out=ntw[:], in0=t2[:], in1=tw[:], op=ALU.add), reads=[t2, tw], writes=[ntw])
                tw = ntw
                yield
            c.op("pe", lambda e: e.matmul(pw[:, :128], lhsT=kbe[:], rhs=tw[:, 128:256], start=True, stop=True), reads=[kbe, tw], writes=[pw])
            yield
            nw = r_nw()
            c.op("act", lambda e: e.activation(out=nw[:], in_=pw[:, :128], func=AF.Copy, scale=-1.0), reads=[pw], writes=[nw])
            yield
            S_f, S_b = Sf[h], Sb[h]
            c.op("pe", lambda e: e.matmul(psv[:, 128:256], lhsT=tw[:, 128:256], rhs=vb[:], start=True, stop=False), reads=[tw, vb], writes=[psv])
            c.op("pe", lambda e: e.matmul(psv[:, 128:256], lhsT=nw[:], rhs=S_f[:], start=False, stop=True), reads=[nw, S_f], writes=[psv])
            yield
            vn = r_vn()
            vnb = r_vnb()
            c.op("act", lambda e: e.copy(out=vn[:], in_=psv[:, 128:256]), reads=[psv], writes=[vn])
            c.op("act", lambda e: e.copy(out=vnb[:], in_=vn[:]), reads=[vn], writes=[vnb])
            yield
            c.op("pe", lambda e: e.matmul(pso[:, 256:384], lhsT=S_b[:], rhs=qd[:], start=True, stop=False), reads=[S_b, qd], writes=[pso])
            c.op("pe", lambda e: e.matmul(pso[:, 256:384], lhsT=vnb[:], rhs=at[:], start=False, stop=True), reads=[vnb, at], writes=[pso])
            c.op("pe", lambda e: e.matmul(pss[:, 384:512], lhsT=kd[:], rhs=vn[:], start=True, stop=True), reads=[kd, vn], writes=[pss])
            yield
            oa = OACC[h]
            if d == 0:
                c.op("act", lambda e: e.copy(out=oa[:, t0:t0 + 128], in_=pso[:, 256:384]), reads=[pso], writes=[oa])
            else:
                c.op("dve", lambda e: e.tensor_tensor(out=oa[:, t0:t0 + 128], in0=pso[:, 256:384], in1=oa[:, t0:t0 + 128], op=ALU.add), reads=[pso, oa], writes=[oa])
            c.op("dve", lambda e: e.scalar_tensor_tensor(out=S_f[:], in0=S_f[:], scalar=EGL[:, tt, col:col + 1], in1=pss[:, 384:512], op0=ALU.mult, op1=ALU.add), reads=[S_f, EGL, pss], writes=[S_f])
            yield
            c.op("act", lambda e: e.copy(out=S_b[:], in_=S_f[:]), reads=[S_f], writes=[S_b])
            if last_seq is not None:
                so = r_so()
                c.op("pool", lambda e: e.tensor_copy(out=so[:], in_=S_f[:]), reads=[S_f], writes=[so])
                c.dma("sp", OSD[last_seq, l, d, h], so[:], reads=[so])

        def run_round(gens):
            gens = list(gens)
            while gens:
                alive = []
                for g in gens:
                    try:
                        next(g)
                        alive.append(g)
                    except StopIteration:
                        pass
                gens = alive

        for d in range(2):
            seqs = [(s, [2 * s, 2 * s + 1]) for s in range(4)] + [(None, list(range(8, 16)))]
            for s, tiles in seqs:
                order = tiles if d == 0 else tiles[::-1]
                for h in range(8):
                    if s is None:
                        c.dma("sp", Sf[h][:], I["sdn"][l, d, h], writes=[Sf[h]])
                        c.op("act", lambda e: e.copy(out=Sb[h][:], in_=Sf[h][:]), reads=[Sf[h]], writes=[Sb[h]])
                    else:
                        c.op("pool", lambda e: e.memset(Sf[h][:], 0.0), writes=[Sf[h]])
                        c.op("pool", lambda e: e.memset(Sb[h][:], 0.0), writes=[Sb[h]])
                for i, tt in enumerate(order):
                    run_round([process(tt, h, d, s if (s is not None and i == len(order) - 1) else None) for h in range(8)])
        c.end_stage()
        osq = c.sb([128, 512], BF16, "dosq")
        rr = c.sb([128, 512], F32, "drr")
        sdz = [c.sb([128, NT], BF16, "sdz") for _ in range(2)]
        orow = [c.sb([128, NT], BF16, "dorow") for _ in range(2)]
        for h in range(8):
            oa = OACC[h]
            z_ = sdz[h % 2]
            orw = orow[h % 2]
            c.dma("sp", z_[:], SDZ[h], writes=[z_])
            for (u0, nu) in TILES:
                c.op("act", lambda e: e.activation(out=osq[:, :nu], in_=oa[:, u0:u0 + nu], func=AF.Square), reads=[oa], writes=[osq])
                p2 = self.pb()
                c.op("pe", lambda e: e.matmul(p2[:, :nu], lhsT=onesB[:], rhs=osq[:, :nu], start=True, stop=True), reads=[onesB, osq], writes=[p2])
                c.op("act", lambda e: e.activation(out=rr[:, :nu], in_=p2[:, :nu], func=AF.Sqrt, scale=1.0 / 128, bias=eps[:, 0:1]), reads=[p2, eps], writes=[rr])
                c.op("dve", lambda e: e.reciprocal(out=rr[:, :nu], in_=rr[:, :nu]), reads=[rr], writes=[rr])
                c.op("dve", lambda e: e.scalar_tensor_tensor(out=rr[:, :nu], in0=oa[:, u0:u0 + nu], scalar=dnn[:, 0:1], in1=rr[:, :nu], op0=ALU.mult, op1=ALU.mult), reads=[oa, dnn, rr], writes=[rr])
                c.op("dve", lambda e: e.tensor_tensor(out=orw[:, u0:u0 + nu], in0=rr[:, :nu], in1=z_[:, u0:u0 + nu], op=ALU.mult), reads=[rr, z_], writes=[orw])
            c.dma("sp", MO[2][h], orw[:], reads=[orw])
        c.end_stage()


PI = math.pi


class KB5(KB4):
    def hy_tables(self, l, L):
        c = self.c
        I = self.I
        nch = L // 128
        TAB = self.scr("HTAB%d" % L, [2, 2 * nch + 1, 128, 1024], BF16)
        c.begin_stage()
        onesF, m0 = self.onesF, None
        m0 = c.sb([128, 2], F32, "m0")
        c.dma("sp", m0[:], I["m0"], writes=[m0])
        fT = c.sb([33, L], F32, "fT")
        c.dma("sp", fT[:], I["featsT%d" % L], writes=[fT])
        w1 = c.sb([33, 64], F32, "w1")
        w2 = c.sb([64, 64], F32, "w2")
        w3 = c.sb([64, 4096], F32, "w3")
        c.dma("sp", w1[:], I["hy_w1"][l], writes=[w1])
        c.dma("sp", w2[:], I["hy_w2"][l], writes=[w2])
        c.dma("sp", w3[:], I["hy_w3"][l], writes=[w3])
        vec = c.sb([64, 4], F32, "hvec")
        for i, nm in enumerate(("hy_b1", "hy_freq1", "hy_b2", "hy_freq2")):
            c.dma("sp", vec[:, i:i + 1], I[nm][l].rearrange("(p o) -> p o", o=1), writes=[vec])
        hid1 = c.sb([64, L], F32, "hid1")
        hid2 = c.sb([64, L], F32, "hid2")
        msk = c.sb([64, 512], F32, "msk")

        def sin_layer(dst, wT, K, src, bi, fi):
            for t0 in range(0, L, 512):
                n = min(512, L - t0)
                ps = self.pb()
                c.op("pe", lambda e: e.matmul(ps[:64, :n], lhsT=wT[:K, :], rhs=src[:K, t0:t0 + n], start=True, stop=True), reads=[wT, src], writes=[ps])
                d = dst
                c.op("dve", lambda e: e.tensor_scalar(out=d[:, t0:t0 + n], in0=ps[:64, :n], scalar1=vec[:, bi:bi + 1], scalar2=vec[:, fi:fi + 1], op0=ALU.add, op1=ALU.mult), reads=[ps, vec], writes=[d])
                for _ in range(2):
                    c.op("dve", lambda e: e.tensor_scalar(out=msk[:, :n], in0=d[:, t0:t0 + n], scalar1=PI, scalar2=-2 * PI, op0=ALU.is_gt, op1=ALU.mult), reads=[d], writes=[msk])
                    c.op("dve", lambda e: e.tensor_tensor(out=d[:, t0:t0 + n], in0=d[:, t0:t0 + n], in1=msk[:, :n], op=ALU.add), reads=[d, msk], writes=[d])
                    c.op("dve", lambda e: e.tensor_scalar(out=msk[:, :n], in0=d[:, t0:t0 + n], scalar1=-PI, scalar2=2 * PI, op0=ALU.is_lt, op1=ALU.mult), reads=[d], writes=[msk])
                    c.op("dve", lambda e: e.tensor_tensor(out=d[:, t0:t0 + n], in0=d[:, t0:t0 + n], in1=msk[:, :n], op=ALU.add), reads=[d, msk], writes=[d])
                c.op("act", lambda e: e.activation(out=d[:, t0:t0 + n], in_=d[:, t0:t0 + n], func=AF.Sin), reads=[d], writes=[d])

        sin_layer(hid1, w1, 33, fT, 0, 1)
        sin_layer(hid2, w2, 64, hid1, 2, 3)
        win = c.sb([128, nch, 1024], F32, "win")
        c.dma("sp", win[:], I["win%d" % L].rearrange("(t p) c -> p t c", p=128), writes=[win])
        CA = c.sb([128, nch, L], BF16, "CA")
        MB = c.sb([128, nch, L], BF16, "MB")
        c.dma("pool", CA[:], I["CA%d" % L].rearrange("(t p) k -> p t k", p=128), writes=[CA])
        c.dma("pool", MB[:], I["MB%d" % L].rearrange("(t p) k -> p t k", p=128), writes=[MB])
        hw = [c.sb([128, nch, 512], F32, "hw") for _ in range(2)]
        ab = c.sb([128, 512], F32, "hab")
        rinv = c.sb([128, 512], F32, "hrinv")
        hs = c.sb([128, nch, 512], BF16, "hs")
        hd = c.sb([128, nch, 512], BF16, "hd")
        to = [c.sb([128, 512], BF16, "hto") for _ in range(3)]
        tf = [c.sb([128, 512], F32, "htf") for _ in range(2)]
        nto = [0]
        for order in range(2):
            for half in range(2):
                for d in range(2):
                    col0 = d * 2048 + order * 1024 + half * 512
                    H = hw[d]
                    pn = self.pacc()
                    for tt in range(nch):
                        ps = self.pb()
                        c.op("pe", lambda e: e.matmul(ps[:, :512], lhsT=hid2[:, tt * 128:(tt + 1) * 128], rhs=w3[:, col0:col0 + 512], start=True, stop=True), reads=[hid2, w3], writes=[ps])
                        c.op("dve", lambda e: e.tensor_tensor(out=H[:, tt, :], in0=ps[:, :512], in1=win[:, tt, half * 512:(half + 1) * 512], op=ALU.mult), reads=[ps, win], writes=[H])
                        c.op("act", lambda e: e.activation(out=ab[:], in_=H[:, tt, :], func=AF.Abs), reads=[H], writes=[ab])
                        c.op("pe", lambda e: e.matmul(pn[:, :512], lhsT=onesF[:], rhs=ab[:], start=(tt == 0), stop=(tt == nch - 1)), reads=[onesF, ab], writes=[pn])
                    c.op("dve", lambda e: e.tensor_scalar(out=rinv[:], in0=pn[:, :512], scalar1=EPS, scalar2=None, op0=ALU.add), reads=[pn], writes=[rinv])
                    c.op("dve", lambda e: e.reciprocal(out=rinv[:], in_=rinv[:]), reads=[rinv], writes=[rinv])
                    for tt in range(nch):
                        c.op("dve", lambda e: e.tensor_tensor(out=H[:, tt, :], in0=H[:, tt, :], in1=rinv[:], op=ALU.mult), reads=[H, rinv], writes=[H])
                c.op("dve", lambda e: e.tensor_tensor(out=hs[:], in0=hw[0][:], in1=hw[1][:], op=ALU.add), reads=[hw[0], hw[1]], writes=[hs])
                c.op("pool", lambda e: e.tensor_tensor(out=hd[:], in0=hw[0][:], in1=hw[1][:], op=ALU.subtract), reads=[hw[0], hw[1]], writes=[hd])
                cs = slice(half * 512, (half + 1) * 512)
                for kc in range(nch):
                    pa = self.pb()
                    for tt in range(nch):
                        c.op("pe", lambda e: e.matmul(pa[:, :512], lhsT=CA[:, tt, kc * 128:(kc + 1) * 128], rhs=hs[:, tt, :], start=(tt == 0), stop=(tt == nch - 1)), reads=[CA, hs], writes=[pa])
                    pbd = self.pb()
                    for tt in range(nch):
                        c.op("pe", lambda e: e.matmul(pbd[:, :512], lhsT=MB[:, tt, kc * 128:(kc + 1) * 128], rhs=hd[:, tt, :], start=(tt == 0), stop=(tt == nch - 1)), reads=[MB, hd], writes=[pbd])
                    oa = to[nto[0] % 3]
                    nto[0] += 1
                    c.op("act", lambda e: e.copy(out=oa[:], in_=pa[:, :512]), reads=[pa], writes=[oa])
                    c.dma("sp", TAB[order, kc][:, cs], oa[:], reads=[oa])
                    ob = to[nto[0] % 3]
                    nto[0] += 1
                    if kc > 0:
                        c.op("act", lambda e: e.copy(out=ob[:], in_=pbd[:, :512]), reads=[pbd], writes=[ob])
                        c.dma("sp", TAB[order, nch + kc][:, cs], ob[:], reads=[ob])
                    else:
                        pbs = self.pb()
                        for tt in range(nch):
                            c.op("pe", lambda e: e.matmul(pbs[:, :512], lhsT=MB[:, tt, 0:128], rhs=hs[:, tt, :], start=(tt == 0), stop=(tt == nch - 1)), reads=[MB, hs], writes=[pbs])
                        c.op("dve", lambda e: e.tensor_scalar(out=ob[:], in0=pbd[:, :512], scalar1=m0[:, 0:1], scalar2=None, op0=ALU.mult), reads=[pbd, m0], writes=[ob])
                        c.dma("sp", TAB[order, nch][:, cs], ob[:], reads=[ob])
                        t1, t2 = tf[0], tf[1]
                        c.op("dve", lambda e: e.tensor_scalar(out=t1[:], in0=pa[:, :512], scalar1=m0[:, 0:1], scalar2=None, op0=ALU.mult), reads=[pa, m0], writes=[t1])
                        c.op("dve", lambda e: e.tensor_scalar(out=t2[:], in0=pbs[:, :512], scalar1=m0[:, 1:2], scalar2=None, op0=ALU.mult), reads=[pbs, m0], writes=[t2])
                        oc = to[nto[0] % 3]
                        nto[0] += 1
                        c.op("pool", lambda e: e.tensor_tensor(out=oc[:], in0=t1[:], in1=t2[:], op=ALU.add), reads=[t1, t2], writes=[oc])
                        c.dma("sp", TAB[order, 2 * nch][:, cs], oc[:], reads=[oc])
        c.end_stage()

    def hy_data(self, l, L, tokbase, B):
        c = self.c
        I = self.I
        nch = L // 128
        ncol = B * 1024
        TAB = self.S["HTAB%d" % L]
        HV, HX = self.S["HV"], self.S["HX"]
        MO = self.scr("MO", [3, 8, 128, NT], BF16)
        Z1 = self.scr("HZ1", [8, 128, NT], F32)
        idB = self.identB
        c.begin_stage()
        CA = c.sb([128, nch, L], BF16, "CA")
        MB = c.sb([128, nch, L], BF16, "MB")
        IA = c.sb([128, nch, L], BF16, "IA")
        IB = c.sb([128, nch, L], BF16, "IB")
        for t, nm in ((CA, "CA"), (MB, "MB"), (IA, "IA"), (IB, "IB")):
            c.dma("pool", t[:], I["%s%d" % (nm, L)].rearrange("(t p) k -> p t k", p=128), writes=[t])
        AH = c.sb([128, nch, 1024], BF16, "AH")
        HB1 = c.sb([128, nch, 1024], BF16, "HB1")
        HA2 = c.sb([128, 1024], BF16, "HA2")
        hb = c.sb([128, 16], F32, "hbias")
        self.load_cols(hb, hb[:], I["hy_bias"][l].rearrange("o (n p) -> (o n) p", p=128), 16)
        z = c.sb([128, nch, ncol], BF16, "z")
        Y = c.sb([128, 2 * nch, ncol], BF16, "Y")
        ntok = B * L
        rowv = [c.sb([128, ntok], BF16, "hrv") for _ in range(2)]
        rowx = [c.sb([128, ntok], F32, "hrx") for _ in range(2)]
        rowz = [c.sb([128, ntok], F32, "hrz") for _ in range(2)]
        rowo = [c.sb([128, ntok], BF16, "hro") for _ in range(2)]
        tm = [c.sb([128, 512], F32, "htm") for _ in range(4)]

        def to_tokmajor(src_rows_fn):
            for ch in range(8):
                row = src_rows_fn(ch)
                for b in range(B):
                    for tq in range(0, nch, 4):
                        ps = self.pb()
                        pb16 = ps[:].bitcast(BF16)
                        nq = min(4, nch - tq)
                        for j in range(nq):
                            tt = tq + j
                            t0 = b * L + tt * 128
                            c.op("pe", lambda e: e.transpose(out=pb16[:, j * 128:(j + 1) * 128], in_=row[:, t0:t0 + 128], identity=idB[:]), reads=[row, idB], writes=[ps])
                        for j in range(nq):
                            tt = tq + j
                            eng = "dve" if j % 2 == 0 else "act"
                            dst = z[:, tt, b * 1024 + ch * 128:b * 1024 + (ch + 1) * 128]
                            if eng == "dve":
                                c.op("dve", lambda e: e.tensor_copy(out=dst, in_=pb16[:, j * 128:(j + 1) * 128]), reads=[ps], writes=[z])
                            else:
                                c.op("act", lambda e: e.copy(out=dst, in_=pb16[:, j * 128:(j + 1) * 128]), reads=[ps], writes=[z])

        for order in range(2):
            c.dma("sp", AH[:], TAB[order, 0:nch].rearrange("k p c -> p k c"), writes=[AH])
            c.dma("sp", HB1[:], TAB[order, nch:2 * nch].rearrange("k p c -> p k c"), writes=[HB1])
            c.dma("sp", HA2[:], TAB[order, 2 * nch], writes=[HA2])
            if order == 0:
                def rows_v(ch):
                    r = rowv[ch % 2]
                    c.dma("sp", r[:], HV[ch][:, tokbase:tokbase + ntok], writes=[r])
                    return r
                to_tokmajor(rows_v)
            else:
                def rows_z(ch):
                    rz = rowz[ch % 2]
                    r = rowv[ch % 2]
                    c.dma("sp", rz[:], Z1[ch][:, tokbase:tokbase + ntok], writes=[rz])
                    c.op("act", lambda e: e.copy(out=r[:], in_=rz[:]), reads=[rz], writes=[r])
                    return r
                to_tokmajor(rows_z)
            for ct in range(ncol // 512):
                c0 = (ct * 512) % 1024
                for kc in range(nch):
                    pa = self.pb()
                    for tt in range(nch):
                        c.op("pe", lambda e: e.matmul(pa[:, :512], lhsT=CA[:, tt, kc * 128:(kc + 1) * 128], rhs=z[:, tt, ct * 512:(ct + 1) * 512], start=(tt == 0), stop=(tt == nch - 1)), reads=[CA, z], writes=[pa])
                    pq = self.pb()
                    for tt in range(nch):
                        c.op("pe", lambda e: e.matmul(pq[:, :512], lhsT=MB[:, tt, kc * 128:(kc + 1) * 128], rhs=z[:, tt, ct * 512:(ct + 1) * 512], start=(tt == 0), stop=(tt == nch - 1)), reads=[MB, z], writes=[pq])
                    t1, t2, t3, t4 = tm
                    ah = AH[:, kc, c0:c0 + 512]
                    h1 = HB1[:, kc, c0:c0 + 512]
                    a2 = HA2[:, c0:c0 + 512] if kc == 0 else ah
                    c.op("dve", lambda e: e.tensor_tensor(out=t1[:], in0=pa[:, :512], in1=ah, op=ALU.mult), reads=[pa, AH], writes=[t1])
                    c.op("dve", lambda e: e.tensor_tensor(out=t2[:], in0=pq[:, :512], in1=h1, op=ALU.mult), reads=[pq, HB1], writes=[t2])
                    c.op("pool", lambda e: e.tensor_tensor(out=Y[:, kc, ct * 512:(ct + 1) * 512], in0=t1[:], in1=t2[:], op=ALU.subtract), reads=[t1, t2], writes=[Y])
                    c.op("dve", lambda e: e.tensor_tensor(out=t3[:], in0=pa[:, :512], in1=h1, op=ALU.mult), reads=[pa, HB1], writes=[t3])
                    c.op("dve", lambda e: e.tensor_tensor(out=t4[:], in0=pq[:, :512], in1=a2, op=ALU.mult), reads=[pq, HA2, AH], writes=[t4])
                    c.op("pool", lambda e: e.tensor_tensor(out=Y[:, nch + kc, ct * 512:(ct + 1) * 512], in0=t3[:], in1=t4[:], op=ALU.add), reads=[t3, t4], writes=[Y])
            for ch in range(8):
                rx = rowx[ch % 2]
                c.dma("sp", rx[:], HX[order * 8 + ch][:, tokbase:tokbase + ntok], writes=[rx])
                if order == 0:
                    rb = rowv[ch % 2]
                    c.dma("sp", rb[:], HV[ch][:, tokbase:tokbase + ntok], writes=[rb])
                    ro = rowz[ch % 2]
                else:
                    rb = rowz[ch % 2]
                    c.dma("sp", rb[:], Z1[ch][:, tokbase:tokbase + ntok], writes=[rb])
                    ro = rowo[ch % 2]
                for b in range(B):
                    for r0 in range(0, L, 512):
                        n = min(512, L - r0)
                        ps = self.pacc()
                        for kk in range(2 * nch):
                            M = IA if kk < nch else IB
                            c.op("pe", lambda e: e.matmul(ps[:, :n], lhsT=Y[:, kk, b * 1024 + ch * 128:b * 1024 + (ch + 1) * 128], rhs=M[:, kk % nch, r0:r0 + n], start=(kk == 0), stop=(kk == 2 * nch - 1)),
                                 reads=[Y, M], writes=[ps])
                        g0 = b * L + r0
                        t1 = tm[0]
                        c.op("dve", lambda e: e.scalar_tensor_tensor(out=t1[:, :n], in0=rb[:, g0:g0 + n], scalar=hb[:, order * 8 + ch:order * 8 + ch + 1], in1=ps[:, :n], op0=ALU.mult, op1=ALU.add), reads=[rb, hb, ps], writes=[t1])
                        c.op("dve", lambda e: e.tensor_tensor(out=ro[:, g0:g0 + n], in0=t1[:, :n], in1=rx[:, g0:g0 + n], op=ALU.mult), reads=[t1, rx], writes=[ro])
                if order == 0:
                    c.dma("sp", Z1[ch][:, tokbase:tokbase + ntok], ro[:], reads=[ro])
                else:
                    c.dma("sp", MO[1][ch][:, tokbase:tokbase + ntok], ro[:], reads=[ro])
            c.barrier()
        c.end_stage()

    def stage_hy(self, l):
        self.hy_tables(l, 256)
        self.hy_data(l, 256, 0, 4)
        self.hy_tables(l, 1024)
        self.hy_data(l, 1024, 1024, 1)

_CACHE = {}


def build_full():
    kb = KB5()
    c = kb.c
    kb.stage_input()
    kb.stage_mod()
    XA = kb.S["XT0"]
    XB = kb.scr("XT1", [16, 128, NT], F32)
    XC = kb.scr("XT2", [16, 128, NT], F32)
    cur = XA
    for l in range(2):
        kb.stage_norm(cur, l, 0)
        kb.stage_proj(l)
        c.end_stage()
        kb.stage_ret(l)
        kb.stage_hy(l)
        kb.stage_dn(l)
        kb.stage_merge(l, cur, XB)
        kb.stage_norm(XB, l, 1)
        kb.stage_ffn_up(l)
        c.end_stage()
        kb.stage_ffn_down(l, XB, XC)
        cur, XB, XC = XC, cur, XB
    kb.stage_final(cur)
    c.finish()
    return kb


def kernel(**inputs):
    z = {k: np.ascontiguousarray(np.asarray(v)) for k, v in inputs.items()}
    if "kb" not in _CACHE:
        _CACHE["kb"] = build_full()
    kb = _CACHE["kb"]
    in_maps = []
    for core in range(8):
        m = dict(kb.consts)
        for k in WSPEC:
            m[k] = z[k]
        xp = z["x_prompt"][4 * core:4 * core + 4].reshape(1024, 2048)
        xs = z["x_sample"][core // 2]
        m["xin"] = np.ascontiguousarray(np.concatenate([xp, xs], 0))
        m["sret"] = np.ascontiguousarray(z["state_ret"][core // 2])
        m["sdn"] = np.ascontiguousarray(z["state_dn"][core // 2])
        m["cvec"] = np.ascontiguousarray(np.stack([z["c_ctx"], z["c"][core // 2]]))
        in_maps.append(m)
    res = run_bass_kernel_spmd(kb.nc, in_maps, core_ids=list(range(8))).results
    y_prompt = np.concatenate([np.asarray(res[c]["y"])[:1024].reshape(4, 256, 2048) for c in range(8)], 0).astype(np.float32)
    y_sample = np.stack([np.asarray(res[2 * b]["y"])[1024:] for b in range(4)], 0).astype(np.float32)
    new_ret = np.concatenate([np.asarray(res[c]["osret"]) for c in range(8)], 0).astype(np.float32)
    new_dn = np.concatenate([np.asarray(res[c]["osdn"]) for c in range(8)], 0).astype(np.float32)
    return (y_prompt, y_sample, new_ret, new_dn)
```

```python
import numpy as np
from contextlib import ExitStack
import concourse.bass as bass
import concourse.mybir as mybir
from concourse.bass_utils import run_bass_kernel_spmd

F32 = mybir.dt.float32
BF16 = mybir.dt.bfloat16
I32 = mybir.dt.int32
AF = mybir.ActivationFunctionType
ALU = mybir.AluOpType
AX = mybir.AxisListType


class Buf:
    __slots__ = ("t", "name", "w", "r", "psum")

    def __init__(self, t, name, psum=False):
        self.t = t
        self.name = name
        self.psum = psum
        self.w = None
        self.r = {}

    def __getitem__(self, idx):
        return self.t[idx]


class Ctx:
    SEM_LIMIT = 30000
    NDMA = 24

    def __init__(self, nc):
        self.nc = nc
        self.es = ExitStack()
        self.eng = {"pe": nc.tensor, "act": nc.scalar, "dve": nc.vector, "pool": nc.gpsimd, "sp": nc.sync}
        self.cur = {}
        self.waited = {k: {} for k in self.eng}
        self.nsem = 0
        for k in self.eng:
            self._new_sem(k)
        self.dpool = {}
        for q in ("sp", "pool", "act"):
            self.dpool[q] = [[self._alloc_sem("d%s%d" % (q, i)), 0] for i in range(self.NDMA)]
        self.dnext = {q: 0 for q in self.dpool}
        self.stage_es = None
        self.uid = 0
        self.tot = {}
        self.marks = []

    def mark(self, name):
        self.marks.append((name, dict(self.tot)))

    def _alloc_sem(self, name):
        self.nsem += 1
        s = self.es.enter_context(self.nc.semaphore("%s_%d" % (name, self.nsem)))
        if not hasattr(self, "allsems"):
            self.allsems = []
        self.allsems.append(s)
        return s

    def _new_sem(self, k):
        self.cur[k] = [self._alloc_sem("e" + k), 0]

    def begin_stage(self):
        if not hasattr(self, "stack"):
            self.stack = []
        self.stack.append(ExitStack())
        self.stage_es = self.stack[-1]

    def end_stage(self):
        self.barrier()
        self.stack.pop().close()
        self.stage_es = self.stack[-1] if self.stack else None

    def sb(self, shape, dt, name="t", persist=False):
        self.uid += 1
        nm = "%s_%d" % (name, self.uid)
        es = self.es if persist else self.stage_es
        t = es.enter_context(self.nc.sbuf_tensor(nm, list(shape), dt))
        return Buf(t, nm)

    def ps(self, shape, dt, name="p"):
        self.uid += 1
        nm = "%s_%d" % (name, self.uid)
        t = self.es.enter_context(self.nc.psum_tensor(nm, list(shape), dt))
        return Buf(t, nm, psum=True)

    def _wait(self, k, tok):
        if tok is None:
            return
        sem, val = tok
        if k == "pe" and sem is self.cur["pe"][0]:
            return
        w = self.waited[k]
        key = id(sem)
        if w.get(key, (None, 0))[1] >= val:
            return
        w[key] = (sem, val)
        self.eng[k].wait_ge(sem, val)

    def _deps(self, k, reads, writes):
        for b in reads:
            self._wait(k, b.w)
            if b.psum:
                for tok in list(b.r.values()):
                    self._wait(k, tok)
        for b in writes:
            self._wait(k, b.w)
            for tok in list(b.r.values()):
                self._wait(k, tok)

    def _commit(self, tok, reads, writes):
        for b in reads:
            b.r[id(tok[0])] = tok
        for b in writes:
            b.w = tok
            b.r = {}

    def op(self, k, fn, reads=(), writes=()):
        self._deps(k, reads, writes)
        c = self.cur[k]
        if c[1] >= self.SEM_LIMIT:
            self._new_sem(k)
            c = self.cur[k]
        c[1] += 1
        self.tot[k] = self.tot.get(k, 0) + 1
        fn(self.eng[k]).then_inc(c[0], 1)
        tok = (c[0], c[1])
        self._commit(tok, reads, writes)
        return tok

    def dma(self, q, out, in_, reads=(), writes=(), **kw):
        pool = self.dpool[q]
        i = self.dnext[q]
        self.dnext[q] = (i + 1) % len(pool)
        slot = pool[i]
        if slot[1] > 0:
            self._wait(q, (slot[0], slot[1]))
        if slot[1] >= self.SEM_LIMIT:
            slot[0] = self._alloc_sem("d" + q)
            slot[1] = 0
        self._deps(q, reads, writes)
        slot[1] += 16
        self.eng[q].dma_start(out=out, in_=in_, **kw).then_inc(slot[0], 16)
        tok = (slot[0], slot[1])
        self._commit(tok, reads, writes)
        return tok

    def barrier(self):
        toks = [(c[0], c[1]) for c in self.cur.values() if c[1] > 0]
        for q in self.dpool:
            for slot in self.dpool[q]:
                if slot[1] > 0:
                    toks.append((slot[0], slot[1]))
        for k in self.eng:
            for tok in toks:
                self._wait(k, tok)

    def finish(self):
        self.barrier()
        self.es.close()

import math
import numpy as np


def make_consts():
    f = np.float32
    C = {}
    i = np.arange(128)
    C["ident"] = np.eye(128, dtype=f)
    C["ones"] = np.ones((128, 128), f)
    C["UF"] = (i[:, None] <= i[None, :]).astype(f)
    C["UB"] = (i[:, None] >= i[None, :]).astype(f)
    C["SU"] = (i[:, None] < i[None, :]).astype(f)
    C["SL"] = (i[:, None] > i[None, :]).astype(f)
    L = 1024
    pos = np.arange(L, dtype=np.float64)
    pos_r = np.floor(pos / 64)
    pos_c = pos % 64
    inv = 10000.0 ** (-np.arange(16, dtype=np.float64) / 16)
    cosT = np.zeros((128, L))
    sinT = np.zeros((128, L))
    for p in range(128):
        q = p % 64
        half = q // 32
        x2 = (q % 32) // 16
        fi = q % 16
        ang = (pos_r if half == 0 else pos_c) * inv[fi]
        cosT[p] = np.cos(ang)
        sinT[p] = np.sin(ang) * (1.0 if x2 else -1.0)
    C["ropeC"] = cosT.astype(f)
    C["ropeS"] = sinT.astype(f)
    for nm, LL in (("S", 1024), ("P", 256)):
        u = np.arange(2 * LL - 128)[None, :]
        p = np.arange(128)[:, None]
        d = u - p - (LL - 128)
        C["dpos" + nm] = np.maximum(d, 0).astype(f)
        C["dneg" + nm] = np.maximum(-d, 0).astype(f)
        C["dz" + nm] = (d == 0).astype(f)
    j = np.arange(128)[:, None] + 128 * np.arange(2)[None, :]
    C["expfb"] = np.concatenate([255 - j, j], axis=1).astype(f)
    ii = np.arange(1024)[None, :].repeat(128, 0)
    C["idx1"] = (ii + 1).astype(f)
    C["idx2"] = (1024 - ii).astype(f)
    for LL in (256, 1024):
        t = np.linspace(0.0, 1.0, LL, dtype=np.float32)[:, None].astype(np.float64)
        wpos = 2.0 * math.pi * np.arange(LL, dtype=np.float64)[:, None] / LL
        fr = np.linspace(1e-4, 15, 16, dtype=np.float32)[None, :].astype(np.float64)
        feats = np.concatenate([t, np.cos(fr * wpos), -np.sin(fr * wpos)], axis=-1)
        C["featsT%d" % LL] = np.ascontiguousarray(feats.T).astype(f)
        deltas = np.abs(np.linspace(math.log(1e-2) / 0.3, math.log(1e-2) / 1.5, 1024, dtype=np.float32)).astype(np.float64)
        C["win%d" % LL] = np.exp(-t * deltas[None, :]).astype(f)
        N = 2 * LL
        tt = np.arange(LL, dtype=np.float64)[:, None]
        kk = np.arange(LL, dtype=np.float64)[None, :]
        ang = 2.0 * math.pi * tt * kk / N
        CA = np.cos(ang)
        MB = -np.sin(ang)
        MB[:, 0] = (-1.0) ** np.arange(LL)
        IA = (2.0 / N) * np.cos(ang.T)
        IA[0, :] = 1.0 / N
        IB = -(2.0 / N) * np.sin(ang.T)
        IB[0, :] = ((-1.0) ** np.arange(LL)) / N
        C["CA%d" % LL] = CA.astype(f)
        C["MB%d" % LL] = MB.astype(f)
        C["IA%d" % LL] = IA.astype(f)
        C["IB%d" % LL] = IB.astype(f)
    ii, jj = np.meshgrid(np.arange(128), np.arange(128), indexing="ij")
    ML = np.stack([(((ii >> (s + 1)) == (jj >> (s + 1))) & ((ii >> s) != (jj >> s)) & (ii > jj)).astype(f) for s in range(7)])
    MU = np.ascontiguousarray(ML.transpose(0, 2, 1))
    C["MLU"] = np.ascontiguousarray(np.concatenate([ML, MU], axis=2).transpose(1, 0, 2))
    C["MUL"] = np.ascontiguousarray(np.concatenate([MU, ML], axis=2).transpose(1, 0, 2))
    m0 = np.ones((128, 2), f)
    m0[0, 0] = 0.0
    m0[:, 1] = 1.0 - m0[:, 0]
    C["m0"] = m0
    C["eps"] = np.full((128, 1), 1e-6, f)
    return C


import math

NT = 2048
TILES = [(0, 512), (512, 512), (1024, 512), (1536, 512)]
EPS = 1e-6

WSPEC = dict(
    w_ada=[2, 2048, 12288], b_ada=[2, 12288], norm1=[2, 2048], w_in=[2, 2048, 16416], ret_decay=[2, 2, 8],
    hy_short=[2, 3, 3072], hy_w1=[2, 33, 64], hy_b1=[2, 64], hy_freq1=[2, 64], hy_w2=[2, 64, 64], hy_b2=[2, 64],
    hy_freq2=[2, 64], hy_w3=[2, 64, 4096], hy_bias=[2, 2, 1024], dn_conv=[2, 3, 3072], dn_a_log=[2, 2, 8],
    dn_dt_bias=[2, 2, 8], dn_norm=[2, 128], p_ret=[2, 1024, 2048], p_hy=[2, 1024, 2048], p_dn=[2, 1024, 2048],
    w_o=[2, 2048, 2048], norm2=[2, 2048], w_up=[2, 2048, 11008], ffn_conv=[2, 3, 11008], w_down=[2, 5504, 2048],
    norm_f=[2048])


class KB:
    def __init__(self, stop_after=None, dbg=(), wspec=None, ext_in=()):
        self.stop_after = stop_after
        self.dbg = set(dbg)
        nc = self.nc = bass.Bass("TRN2", target_bir_lowering=False)
        self.c = Ctx(nc)
        self.I = {}
        self.consts = make_consts()
        for k, v in self.consts.items():
            self.I[k] = nc.dram_tensor(k, list(v.shape), F32, kind="ExternalInput").ap()
        self.ext_in = set(ext_in)
        for k, s in (wspec or WSPEC).items():
            self.I[k] = nc.dram_tensor(k, s, F32, kind="ExternalInput").ap()
        self.I["xin"] = nc.dram_tensor("xin", [NT, 2048], F32, kind="ExternalInput").ap()
        self.I["sret"] = nc.dram_tensor("sret", [2, 2, 8, 64, 128], F32, kind="ExternalInput").ap()
        self.I["sdn"] = nc.dram_tensor("sdn", [2, 2, 8, 128, 128], F32, kind="ExternalInput").ap()
        self.I["cvec"] = nc.dram_tensor("cvec", [2, 2048], F32, kind="ExternalInput").ap()
        self.O = {}
        self.O["y"] = nc.dram_tensor("y", [NT, 2048], F32, kind="ExternalOutput").ap()
        self.O["osret"] = nc.dram_tensor("osret", [4, 2, 2, 8, 64, 128], F32, kind="ExternalOutput").ap()
        self.O["osdn"] = nc.dram_tensor("osdn", [4, 2, 2, 8, 128, 128], F32, kind="ExternalOutput").ap()
        self.S = {}
        c = self.c
        self.PB = [c.ps([128, 512], F32, "pb") for _ in range(8)]
        self.pbi = 0
        self.identF = c.sb([128, 128], F32, "identF", persist=True)
        self.identB = c.sb([128, 128], BF16, "identB", persist=True)
        self.onesB = c.sb([128, 128], BF16, "onesB", persist=True)
        self.onesF = c.sb([128, 128], F32, "onesF", persist=True)
        self.epsT = c.sb([128, 1], F32, "epsT", persist=True)
        self.rows = c.sb([128, 128], F32, "rows", persist=True)
        c.dma("sp", self.identF[:], self.I["ident"], writes=[self.identF])
        c.dma("pool", self.identB[:], self.I["ident"], writes=[self.identB])
        c.dma("pool", self.onesB[:], self.I["ones"], writes=[self.onesB])
        c.dma("sp", self.onesF[:], self.I["ones"], writes=[self.onesF])
        c.dma("sp", self.epsT[:], self.I["eps"], writes=[self.epsT])
        self.HT = None
        self.modT = c.sb([128, 2, 96, 2], F32, "modT", persist=True)
        self.sca = c.sb([128, 2, 2, 16, 2], F32, "sca", persist=True)
        self.gbraw = c.sb([128, 16, 32], F32, "gbraw", persist=True)

    def scr(self, name, shape, dt):
        if name not in self.S:
            kind = "ExternalOutput" if name in self.dbg else ("ExternalInput" if name in self.ext_in else "Internal")
            self.S[name] = self.nc.dram_tensor(name, list(shape), dt, kind=kind).ap()
        return self.S[name]

    def pb(self):
        self.pbi = (self.pbi + 1) % 6
        return self.PB[self.pbi]

    def pacc(self):
        self.pai = (getattr(self, "pai", 0) + 1) % 2
        return self.PB[6 + self.pai]

    def load_cols(self, dstbuf, dst_ap, src_rows, n):
        c = self.c
        rows = self.rows
        c.dma("sp", rows[:n, :], src_rows, writes=[rows])
        ps = self.pb()
        idf = self.identF
        c.op("pe", lambda e: e.transpose(out=ps[:, :n], in_=rows[:n, :], identity=idf[:n, :n]), reads=[rows, idf], writes=[ps])
        c.op("dve", lambda e: e.tensor_copy(out=dst_ap, in_=ps[:, :n]), reads=[ps], writes=[dstbuf])

    def stage_input(self):
        c = self.c
        XT = self.scr("XT0", [16, 128, NT], F32)
        c.begin_stage()
        xs = [c.sb([128, 2048], F32, "xs") for _ in range(2)]
        xo = [c.sb([128, 16, 128], F32, "xo") for _ in range(2)]
        idf = self.identF
        for tt in range(16):
            a = xs[tt % 2]
            o = xo[tt % 2]
            c.dma("sp", a[:], self.I["xin"][tt * 128:(tt + 1) * 128, :], writes=[a])
            for g in range(4):
                ps = self.pb()
                for j in range(4):
                    ch = g * 4 + j
                    c.op("pe", lambda e: e.transpose(out=ps[:, j * 128:(j + 1) * 128], in_=a[:, ch * 128:(ch + 1) * 128], identity=idf[:]),
                         reads=[a, idf], writes=[ps])
                eng = "dve" if g % 2 == 0 else "act"
                if eng == "dve":
                    c.op("dve", lambda e: e.tensor_copy(out=o[:, g * 4:(g + 1) * 4, :], in_=ps[:].rearrange("p (j t) -> p j t", j=4)), reads=[ps], writes=[o])
                else:
                    c.op("act", lambda e: e.copy(out=o[:, g * 4:(g + 1) * 4, :], in_=ps[:].rearrange("p (j t) -> p j t", j=4)), reads=[ps], writes=[o])
            c.dma("sp", XT[:, :, tt * 128:(tt + 1) * 128].rearrange("c p t -> p c t"), o[:], reads=[o])
        c.end_stage()

    def stage_mod(self):
        c = self.c
        I = self.I
        c.begin_stage()
        scT = c.sb([128, 2, 16], F32, "scT")
        self.load_cols(scT, scT[:].rearrange("p b k -> p (b k)"), I["cvec"].rearrange("b (k p) -> (b k) p", p=128), 32)
        c.op("act", lambda e: e.activation(out=scT[:], in_=scT[:], func=AF.Silu), reads=[scT], writes=[scT])
        bada = c.sb([128, 96], F32, "bada")
        nw = c.sb([128, 16], F32, "nw")
        wa = [c.sb([128, 16, 512], F32, "wa") for _ in range(3)]
        mrow = [c.sb([2, 512], F32, "mrow") for _ in range(2)]
        idf = self.identF
        for l in range(2):
            self.load_cols(bada, bada[:], I["b_ada"][l].rearrange("(n p) -> n p", p=128), 96)
            ps = self.pacc()
            for blk in range(24):
                w = wa[blk % 3]
                c.dma("sp" if blk % 2 == 0 else "act", w[:], I["w_ada"][l][:, blk * 512:(blk + 1) * 512].rearrange("(k p) n -> p k n", p=128), writes=[w])
                pr = self.pb()
                for k in range(16):
                    c.op("pe", lambda e: e.matmul(pr[:2, :512], lhsT=scT[:, :, k], rhs=w[:, k, :], start=(k == 0), stop=(k == 15)), reads=[w, scT], writes=[pr])
                mr = mrow[blk % 2]
                c.op("act", lambda e: e.copy(out=mr[:], in_=pr[:2, :512]), reads=[pr], writes=[mr])
                for j in range(4):
                    ch = blk * 4 + j
                    c.op("pe", lambda e: e.transpose(out=ps[:, ch * 2:ch * 2 + 2], in_=mr[:2, j * 128:(j + 1) * 128], identity=idf[:2, :2]), reads=[mr, idf], writes=[ps])
            mt = self.modT
            for b in range(2):
                c.op("dve", lambda e: e.tensor_tensor(out=mt[:, l, :, b], in0=ps[:, 0:192].rearrange("p (c b) -> p c b", b=2)[:, :, b], in1=bada[:], op=ALU.add),
                     reads=[ps, bada], writes=[mt])
            for which, (nm, scb) in enumerate((("norm1", 16), ("norm2", 64))):
                self.load_cols(nw, nw[:], I[nm][l].rearrange("(n p) -> n p", p=128), 16)
                sc = self.sca
                for b in range(2):
                    c.op("dve", lambda e: e.scalar_tensor_tensor(out=sc[:, l, which, :, b], in0=mt[:, l, scb:scb + 16, b], scalar=1.0, in1=nw[:], op0=ALU.add, op1=ALU.mult),
                         reads=[mt, nw], writes=[sc])
        c.end_stage()

    def stage_norm(self, XT, l, which):
        c = self.c
        c.begin_stage()
        self.HT = c.sb([128, 16, NT], BF16, "HT")
        c.begin_stage()
        shb = 0 if which == 0 else 48
        xs = [c.sb([128, 16, 256], F32, "nx") for _ in range(2)]
        sq = c.sb([128, 16, 256], BF16, "nsq")
        tmp = c.sb([128, 16, 256], F32, "ntmp")
        r = c.sb([128, 256], F32, "nr")
        HT, sc, mt, ones, eps = self.HT, self.sca, self.modT, self.onesB, self.epsT
        for ti in range(8):
            t0 = ti * 256
            b = 0 if t0 < 1024 else 1
            x = xs[ti % 2]
            c.dma("sp", x[:], XT[:, :, t0:t0 + 256].rearrange("c p t -> p c t"), writes=[x])
            c.op("act", lambda e: e.activation(out=sq[:], in_=x[:], func=AF.Square), reads=[x], writes=[sq])
            ps = self.pb()
            for ch in range(16):
                c.op("pe", lambda e: e.matmul(ps[:, :256], lhsT=ones[:], rhs=sq[:, ch, :], start=(ch == 0), stop=(ch == 15)), reads=[ones, sq], writes=[ps])
            c.op("act", lambda e: e.activation(out=r[:], in_=ps[:, :256], func=AF.Sqrt, scale=1.0 / 2048, bias=eps[:, 0:1]), reads=[ps, eps], writes=[r])
            c.op("dve", lambda e: e.reciprocal(out=r[:], in_=r[:]), reads=[r], writes=[r])
            for ch in range(16):
                c.op("dve", lambda e: e.tensor_tensor(out=tmp[:, ch, :], in0=x[:, ch, :], in1=r[:], op=ALU.mult), reads=[x, r], writes=[tmp])
                c.op("act", lambda e: e.activation(out=HT[:, ch, t0:t0 + 256], in_=tmp[:, ch, :], func=AF.Identity,
                                                   scale=sc[:, l, which, ch, b:b + 1], bias=mt[:, l, shb + ch, b:b + 1]), reads=[tmp, sc, mt], writes=[HT])
        c.end_stage()


class KB2(KB):
    def wbufs(self, KC, n=3, width=256):
        return [self.c.sb([128, KC, width], BF16, "wb") for _ in range(n)]

    def lin_fm(self, W, blocks, IN, KC, epi, tiles=TILES, wb=None, prep=None):
        c = self.c
        if wb is None:
            wb = self.wbufs(KC)
        for bi, (c0, ncol) in enumerate(blocks):
            w = wb[bi % len(wb)]
            c.dma("pool", w[:, :KC, :ncol], W[:, c0:c0 + ncol].rearrange("(k p) n -> p k n", p=128), writes=[w])
            aux = prep(w, c0, ncol) if prep else None
            for off in range(0, ncol, 128):
                n = min(128, ncol - off)
                for ti, (t0, nt) in enumerate(tiles):
                    ps = self.pb()
                    for k in range(KC):
                        c.op("pe", lambda e: e.matmul(ps[:n, :nt], lhsT=w[:, k, off:off + n], rhs=IN[:, k, t0:t0 + nt], start=(k == 0), stop=(k == KC - 1)),
                             reads=[w, IN], writes=[ps])
                    epi(c0 + off, n, ti, t0, nt, ps, (aux, off))

    def lin_tm(self, W, blocks, IN, KC, epi, ttiles, wb=None):
        c = self.c
        if wb is None:
            wb = self.wbufs(KC)
        for bi, (c0, ncol) in enumerate(blocks):
            w = wb[bi % len(wb)]
            c.dma("pool", w[:, :KC, :ncol], W[:, c0:c0 + ncol].rearrange("(k p) n -> p k n", p=128), writes=[w])
            for tt in ttiles:
                ps = self.pb()
                for k in range(KC):
                    c.op("pe", lambda e: e.matmul(ps[:, :ncol], lhsT=IN[:, k, tt * 128:(tt + 1) * 128], rhs=w[:, k, :ncol], start=(k == 0), stop=(k == KC - 1)),
                         reads=[w, IN], writes=[ps])
                epi(c0, ncol, tt, ps)

    @staticmethod
    def blocks(a, b, step=256):
        return [(x, min(step, b - x)) for x in range(a, b, step)]

    def conv_row(self, src, dst, wt, ci, ntap_chunks):
        c = self.c
        w0 = wt[:, 0 * ntap_chunks + ci:0 * ntap_chunks + ci + 1]
        w1 = wt[:, 1 * ntap_chunks + ci:1 * ntap_chunks + ci + 1]
        w2 = wt[:, 2 * ntap_chunks + ci:2 * ntap_chunks + ci + 1]
        c.op("act", lambda e: e.activation(out=dst[:], in_=src[:], func=AF.Copy, scale=w1), reads=[src, wt], writes=[dst])
        sp = src[:, 0:1024].rearrange("p (s t) -> p s t", t=256)
        dp = dst[:, 0:1024].rearrange("p (s t) -> p s t", t=256)
        c.op("dve", lambda e: e.scalar_tensor_tensor(out=dp[:, :, 1:256], in0=sp[:, :, 0:255], scalar=w0, in1=dp[:, :, 1:256], op0=ALU.mult, op1=ALU.add), reads=[src, wt, dst], writes=[dst])
        c.op("dve", lambda e: e.scalar_tensor_tensor(out=dp[:, :, 0:255], in0=sp[:, :, 1:256], scalar=w2, in1=dp[:, :, 0:255], op0=ALU.mult, op1=ALU.add), reads=[src, wt, dst], writes=[dst])
        c.op("dve", lambda e: e.scalar_tensor_tensor(out=dst[:, 1025:2048], in0=src[:, 1024:2047], scalar=w0, in1=dst[:, 1025:2048], op0=ALU.mult, op1=ALU.add), reads=[src, wt, dst], writes=[dst])
        c.op("dve", lambda e: e.scalar_tensor_tensor(out=dst[:, 1024:2047], in0=src[:, 1025:2048], scalar=w2, in1=dst[:, 1024:2047], op0=ALU.mult, op1=ALU.add), reads=[src, wt, dst], writes=[dst])

    def stage_proj(self, l):
        c = self.c
        I = self.I
        W = I["w_in"][l]
        HT = self.HT
        c.begin_stage()
        wb = self.wbufs(16, 3, 512)
        QK = self.scr("QK", [8, 128, NT], BF16)
        VTM = self.scr("VTM", [16, 128, 1024], BF16)
        KTM = self.scr("KTM", [8, 128, 512], BF16)
        SRG = self.scr("SRG", [8, 128, NT], BF16)
        HV = self.scr("HV", [8, 128, NT], BF16)
        HX = self.scr("HX", [16, 128, NT], F32)
        DQK = self.scr("DQK", [16, 128, NT], BF16)
        DVT = self.scr("DVT", [8, 128, NT], BF16)
        SDZ = self.scr("SDZ", [8, 128, NT], BF16)
        GATE = self.scr("GATE", [48, 128, NT], BF16)
        rowb = [c.sb([128, NT], BF16, "rowb") for _ in range(2)]
        rowf = [c.sb([128, NT], F32, "rowf") for _ in range(2)]
        rowg = [c.sb([128, NT], F32, "rowg") for _ in range(2)]
        cnt = [0]

        ropeC = c.sb([128, 1024], F32, "ropeC")
        ropeS = c.sb([128, 1024], F32, "ropeS")
        c.dma("sp", ropeC[:], I["ropeC"], writes=[ropeC])
        c.dma("sp", ropeS[:], I["ropeS"], writes=[ropeS])
        wperm = [c.sb([128, 16, 256], BF16, "wperm") for _ in range(2)]
        t1 = c.sb([128, 512], F32, "t1")
        t2 = c.sb([128, 512], F32, "t2")
        pc = [0]

        def prep_qk(w, c0, ncol):
            wp = wperm[pc[0] % 2]
            pc[0] += 1
            src = w[:, :, 0:256].rearrange("p k (a two s) -> p k a two s", two=2, s=16)
            dst = wp[:].rearrange("p k (a two s) -> p k a two s", two=2, s=16)
            c.op("dve", lambda e: e.tensor_copy(out=dst[:, :, :, 0, :], in_=src[:, :, :, 1, :]), reads=[w], writes=[wp])
            c.op("pool", lambda e: e.tensor_copy(out=dst[:, :, :, 1, :], in_=src[:, :, :, 0, :]), reads=[w], writes=[wp])
            return wp

        def epi_qk(col0, n, ti, t0, nt, ps, auxoff):
            wp, off = auxoff
            ci = col0 // 128
            row = rowb[ci % 2]
            scale = 1.0 if ci < 4 else 0.125
            if ti < 2:
                c.op("act", lambda e: e.activation(out=row[:, t0:t0 + nt], in_=ps[:, :nt], func=AF.Copy, scale=scale), reads=[ps], writes=[row])
            else:
                ps2 = self.pb()
                for k in range(16):
                    c.op("pe", lambda e: e.matmul(ps2[:, :nt], lhsT=wp[:, k, off:off + 128], rhs=HT[:, k, t0:t0 + nt], start=(k == 0), stop=(k == 15)), reads=[wp, HT], writes=[ps2])
                s0 = t0 - 1024
                c.op("dve", lambda e: e.tensor_tensor(out=t1[:, :nt], in0=ps[:, :nt], in1=ropeC[:, s0:s0 + nt], op=ALU.mult), reads=[ps, ropeC], writes=[t1])
                c.op("dve", lambda e: e.tensor_tensor(out=t2[:, :nt], in0=ps2[:, :nt], in1=ropeS[:, s0:s0 + nt], op=ALU.mult), reads=[ps2, ropeS], writes=[t2])
                c.op("dve", lambda e: e.tensor_tensor(out=t1[:, :nt], in0=t1[:, :nt], in1=t2[:, :nt], op=ALU.add), reads=[t1, t2], writes=[t1])
                c.op("act", lambda e: e.activation(out=row[:, t0:t0 + nt], in_=t1[:, :nt], func=AF.Copy, scale=scale), reads=[t1], writes=[row])
            if ti == 3:
                c.dma("sp", QK[ci], row[:], reads=[row])

        self.lin_fm(W, self.blocks(0, 1024), HT, 16, epi_qk, wb=wb, prep=prep_qk)

        stv = [c.sb([128, 512], BF16, "stv") for _ in range(4)]

        def epi_v(col0, ncol, tt, ps):
            s = stv[cnt[0] % 4]
            cnt[0] += 1
            if cnt[0] % 2:
                c.op("dve", lambda e: e.tensor_copy(out=s[:, :ncol], in_=ps[:, :ncol]), reads=[ps], writes=[s])
            else:
                c.op("act", lambda e: e.copy(out=s[:, :ncol], in_=ps[:, :ncol]), reads=[ps], writes=[s])
            c.dma("sp", VTM[tt][:, col0 - 1024:col0 - 1024 + ncol], s[:, :ncol], reads=[s])

        def epi_ktm(col0, ncol, tt, ps):
            s = stv[cnt[0] % 4]
            cnt[0] += 1
            c.op("act", lambda e: e.activation(out=s[:, :ncol], in_=ps[:, :ncol], func=AF.Copy, scale=0.125), reads=[ps], writes=[s])
            c.dma("sp", KTM[tt][:, col0 - 512:col0 - 512 + ncol], s[:, :ncol], reads=[s])

        self.lin_tm(W, self.blocks(1024, 2048, 512), HT, 16, epi_v, range(16), wb=wb)
        self.lin_tm(W, self.blocks(512, 1024, 512), HT, 16, epi_ktm, range(8), wb=wb)

        def mk_epi_act(base, dst, func):
            def epi(col0, n, ti, t0, nt, ps, aux):
                ci = (col0 - base) // 128
                row = rowb[ci % 2]
                c.op("act", lambda e: e.activation(out=row[:, t0:t0 + nt], in_=ps[:, :nt], func=func), reads=[ps], writes=[row])
                if ti == 3:
                    c.dma("sp", dst[ci], row[:], reads=[row])
            return epi

        self.lin_fm(W, self.blocks(2048, 3072, 512), HT, 16, mk_epi_act(2048, SRG, AF.Silu), wb=wb)
        self.lin_fm(W, self.blocks(9216, 10240, 512), HT, 16, mk_epi_act(9216, SDZ, AF.Silu), wb=wb)
        self.lin_fm(W, self.blocks(10272, 16416, 512), HT, 16, mk_epi_act(10272, GATE, AF.Sigmoid), wb=wb)

        hyw = c.sb([128, 72], F32, "hyw")
        self.load_cols(hyw, hyw[:], I["hy_short"][l].rearrange("k (n p) -> (k n) p", p=128), 72)
        dnw = c.sb([128, 72], F32, "dnw")
        self.load_cols(dnw, dnw[:], I["dn_conv"][l].rearrange("k (n p) -> (k n) p", p=128), 72)

        def epi_hy(col0, n, ti, t0, nt, ps, aux):
            ci = (col0 - 3072) // 128
            raw = rowf[ci % 2]
            c.op("act", lambda e: e.copy(out=raw[:, t0:t0 + nt], in_=ps[:, :nt]), reads=[ps], writes=[raw])
            if ti == 3:
                cv = rowg[ci % 2]
                self.conv_row(raw, cv, hyw, ci, 24)
                if ci < 8:
                    row = rowb[ci % 2]
                    c.op("pool", lambda e: e.tensor_copy(out=row[:], in_=cv[:]), reads=[cv], writes=[row])
                    c.dma("sp", HV[ci], row[:], reads=[row])
                else:
                    c.dma("sp", HX[ci - 8], cv[:], reads=[cv])

        self.lin_fm(W, self.blocks(3072, 6144, 512), HT, 16, epi_hy, wb=wb)

        sqb = c.sb([128, NT], BF16, "sqb")
        rn = c.sb([128, 512], F32, "rn")
        ones, eps = self.onesB, self.epsT

        def epi_dn(col0, n, ti, t0, nt, ps, aux):
            ci = (col0 - 6144) // 128
            raw = rowf[ci % 2]
            c.op("act", lambda e: e.copy(out=raw[:, t0:t0 + nt], in_=ps[:, :nt]), reads=[ps], writes=[raw])
            if ti == 3:
                cv = rowg[ci % 2]
                self.conv_row(raw, cv, dnw, ci, 24)
                c.op("act", lambda e: e.activation(out=cv[:], in_=cv[:], func=AF.Silu), reads=[cv], writes=[cv])
                row = rowb[ci % 2]
                if ci < 16:
                    c.op("act", lambda e: e.activation(out=sqb[:], in_=cv[:], func=AF.Square), reads=[cv], writes=[sqb])
                    for tj, (u0, nu) in enumerate(TILES):
                        p2 = self.pb()
                        c.op("pe", lambda e: e.matmul(p2[:, :nu], lhsT=ones[:], rhs=sqb[:, u0:u0 + nu], start=True, stop=True), reads=[ones, sqb], writes=[p2])
                        c.op("act", lambda e: e.activation(out=rn[:, :nu], in_=p2[:, :nu], func=AF.Sqrt, bias=eps[:, 0:1]), reads=[p2, eps], writes=[rn])
                        c.op("dve", lambda e: e.reciprocal(out=rn[:, :nu], in_=rn[:, :nu]), reads=[rn], writes=[rn])
                        sc_ = (128 ** -0.5) if ci < 8 else 1.0
                        c.op("dve", lambda e: e.scalar_tensor_tensor(out=row[:, u0:u0 + nu], in0=cv[:, u0:u0 + nu], scalar=sc_, in1=rn[:, :nu], op0=ALU.mult, op1=ALU.mult),
                             reads=[cv, rn], writes=[row])
                    c.dma("sp", DQK[ci], row[:], reads=[row])
                else:
                    c.op("pool", lambda e: e.tensor_copy(out=row[:], in_=cv[:]), reads=[cv], writes=[row])
                    c.dma("sp", DVT[ci - 16], row[:], reads=[row])

        self.lin_fm(W, self.blocks(6144, 9216, 512), HT, 16, epi_dn, wb=wb)

        gbraw = self.gbraw

        def epi_gb(col0, ncol, tt, ps):
            c.op("dve", lambda e: e.tensor_copy(out=gbraw[:, tt, :], in_=ps[:, :32]), reads=[ps], writes=[gbraw])

        self.lin_tm(W, [(10240, 32)], HT, 16, epi_gb, range(16), wb=wb)
        c.end_stage()

    def stage_merge(self, l, XTin, XTout):
        c = self.c
        I = self.I
        c.begin_stage()
        MO = self.scr("MO", [3, 8, 128, NT], BF16)
        GATE = self.S["GATE"]
        MIX = self.scr("MIX", [16, 128, NT], BF16)
        mo = [c.sb([128, 8, NT], BF16, "mo") for _ in range(3)]
        for b in range(3):
            c.dma("sp", mo[b][:], MO[b].rearrange("c p t -> p c t"), writes=[mo[b]])
        wp = [c.sb([128, 8, 512], BF16, "wp") for _ in range(3)]
        gr = [c.sb([128, NT], BF16, "gr") for _ in range(4)]
        acc = [c.sb([128, NT], F32, "acc") for _ in range(4)]
        mixb = [c.sb([128, NT], BF16, "mixb") for _ in range(2)]
        tmp = c.sb([128, 512], F32, "mtmp")
        PW = [I["p_ret"][l], I["p_hy"][l], I["p_dn"][l]]
        n = 0
        nw_ = 0
        for jg in range(4):
            for b in range(3):
                w = wp[nw_ % 3]
                nw_ += 1
                c.dma("pool", w[:], PW[b][:, jg * 512:(jg + 1) * 512].rearrange("(k p) n -> p k n", p=128), writes=[w])
                for jj in range(4):
                    j = jg * 4 + jj
                    a = acc[jj]
                    g = gr[n % 4]
                    n += 1
                    c.dma("sp", g[:], GATE[b * 16 + j], writes=[g])
                    for ti, (t0, nt) in enumerate(TILES):
                        ps = self.pb()
                        for k_ in range(8):
                            c.op("pe", lambda e: e.matmul(ps[:, :nt], lhsT=w[:, k_, jj * 128:(jj + 1) * 128], rhs=mo[b][:, k_, t0:t0 + nt], start=(k_ == 0), stop=(k_ == 7)), reads=[w, mo[b]], writes=[ps])
                        if b == 0:
                            c.op("dve", lambda e: e.tensor_tensor(out=a[:, t0:t0 + nt], in0=ps[:, :nt], in1=g[:, t0:t0 + nt], op=ALU.mult), reads=[ps, g], writes=[a])
                        else:
                            c.op("dve", lambda e: e.tensor_tensor(out=tmp[:, :nt], in0=ps[:, :nt], in1=g[:, t0:t0 + nt], op=ALU.mult), reads=[ps, g], writes=[tmp])
                            c.op("pool", lambda e: e.tensor_tensor(out=a[:, t0:t0 + nt], in0=a[:, t0:t0 + nt], in1=tmp[:, :nt], op=ALU.add), reads=[a, tmp], writes=[a])
            for jj in range(4):
                j = jg * 4 + jj
                m = mixb[j % 2]
                a = acc[jj]
                c.op("act", lambda e: e.copy(out=m[:], in_=a[:]), reads=[a], writes=[m])
                c.dma("sp", MIX[j], m[:], reads=[m])
        c.end_stage()
        c.begin_stage()
        mix = c.sb([128, 16, NT], BF16, "mix")
        c.dma("sp", mix[:], MIX.rearrange("c p t -> p c t"), writes=[mix])
        self.resid_epilogue(I["w_o"][l], 2048, mix, 16, l, 32, XTin, XTout)
        c.end_stage()

    def resid_epilogue(self, W, ncols, IN, KC, l, gbase, XTin, XTout, tiles=TILES, wb=None):
        c = self.c
        mt = self.modT
        xr = [c.sb([128, NT], F32, "xr") for _ in range(2)]
        lo = tiles[0][0]
        hi = tiles[-1][0] + tiles[-1][1]

        def epi(col0, n, ti, t0, nt, ps, aux):
            j = col0 // 128
            x = xr[j % 2]
            if ti == 0:
                c.dma("sp", x[:, lo:hi], XTin[j][:, lo:hi], writes=[x])
            b = 0 if t0 < 1024 else 1
            c.op("dve", lambda e: e.scalar_tensor_tensor(out=x[:, t0:t0 + nt], in0=ps[:, :nt], scalar=mt[:, l, gbase + j, b:b + 1], in1=x[:, t0:t0 + nt], op0=ALU.mult, op1=ALU.add),
                 reads=[ps, mt, x], writes=[x])
            if ti == len(tiles) - 1:
                c.dma("sp", XTout[j][:, lo:hi], x[:, lo:hi], reads=[x])

        if wb is None:
            wb = self.wbufs(KC, 3, 512)
        self.lin_fm(W, self.blocks(0, ncols, 512), IN, KC, epi, tiles=tiles, wb=wb)

    def stage_ffn_up(self, l):
        c = self.c
        I = self.I
        ACTT = self.scr("ACTT", [43, 128, NT], BF16)
        c.begin_stage()
        fw = c.sb([128, 3 * 86], F32, "fw")
        for k in range(3):
            self.load_cols(fw, fw[:, k * 86:(k + 1) * 86], I["ffn_conv"][l][k].rearrange("(n p) -> n p", p=128), 86)
        rowf = [c.sb([128, NT], F32, "frow") for _ in range(3)]
        ga = [c.sb([128, NT], F32, "ga") for _ in range(2)]
        gb = [c.sb([128, NT], F32, "gb") for _ in range(2)]
        ab = [c.sb([128, NT], BF16, "ab") for _ in range(2)]
        wb = self.wbufs(16, 4, 512)
        HT = self.HT
        W = I["w_up"][l]
        cnt = [0]
        nb = 0
        for blk in range(11):
            c0 = blk * 512
            ncol = min(512, 5504 - c0)
            wa_, wg_ = wb[nb % 4], wb[(nb + 1) % 4]
            nb += 2
            c.dma("pool", wa_[:, :, :ncol], W[:, c0:c0 + ncol].rearrange("(k p) n -> p k n", p=128), writes=[wa_])
            c.dma("pool", wg_[:, :, :ncol], W[:, 5504 + c0:5504 + c0 + ncol].rearrange("(k p) n -> p k n", p=128), writes=[wg_])
            for off in range(0, ncol, 128):
                ci = (c0 + off) // 128
                for which, w in ((0, wa_), (1, wg_)):
                    raw = rowf[cnt[0] % 3]
                    cnt[0] += 1
                    for ti, (t0, nt) in enumerate(TILES):
                        ps = self.pb()
                        for kk in range(16):
                            c.op("pe", lambda e: e.matmul(ps[:, :nt], lhsT=w[:, kk, off:off + 128], rhs=HT[:, kk, t0:t0 + nt], start=(kk == 0), stop=(kk == 15)), reads=[w, HT], writes=[ps])
                        if ti % 2 == 0:
                            c.op("act", lambda e: e.copy(out=raw[:, t0:t0 + nt], in_=ps[:, :nt]), reads=[ps], writes=[raw])
                        else:
                            c.op("dve", lambda e: e.tensor_copy(out=raw[:, t0:t0 + nt], in_=ps[:, :nt]), reads=[ps], writes=[raw])
                    if which == 0:
                        g = ga[ci % 2]
                        self.conv_row(raw, g, fw, ci, 86)
                        c.op("act", lambda e: e.activation(out=g[:], in_=g[:], func=AF.Silu), reads=[g], writes=[g])
                    else:
                        g = ga[ci % 2]
                        g2 = gb[ci % 2]
                        self.conv_row(raw, g2, fw, 43 + ci, 86)
                        a_ = ab[ci % 2]
                        c.op("pool", lambda e: e.tensor_tensor(out=a_[:], in0=g[:], in1=g2[:], op=ALU.mult), reads=[g, g2], writes=[a_])
                        c.dma("sp", ACTT[ci], a_[:], reads=[a_])
        c.end_stage()

    def stage_ffn_down(self, l, XTin, XTout):
        c = self.c
        I = self.I
        ACTT = self.S["ACTT"]
        for half in range(2):
            c.begin_stage()
            a = c.sb([128, 43, 1024], BF16, "actin")
            c.dma("sp", a[:], ACTT[:, :, half * 1024:(half + 1) * 1024].rearrange("c p t -> p c t"), writes=[a])

            class Shift:
                def __init__(s, buf, sh):
                    s.buf, s.sh = buf, sh
            wb = self.wbufs(43, 2, 512)
            tiles = [(half * 1024, 512), (half * 1024 + 512, 512)]
            self._resid_shift(I["w_down"][l], a, 43, l, 80, XTin, XTout, tiles, half * 1024, wb)
            c.end_stage()

    def _resid_shift(self, W, IN, KC, l, gbase, XTin, XTout, tiles, tshift, wb):
        c = self.c
        mt = self.modT
        xr = [c.sb([128, 1024], F32, "xr2") for _ in range(2)]
        blocks = self.blocks(0, 2048, 512)
        for bi, (c0, ncol) in enumerate(blocks):
            w = wb[bi % len(wb)]
            c.dma("pool", w[:, :KC, :ncol], W[:, c0:c0 + ncol].rearrange("(k p) n -> p k n", p=128), writes=[w])
            for off in range(0, ncol, 128):
                j = (c0 + off) // 128
                x = xr[j % 2]
                c.dma("sp", x[:], XTin[j][:, tshift:tshift + 1024], writes=[x])
                for ti, (t0, nt) in enumerate(tiles):
                    ps = self.pb()
                    for k in range(KC):
                        c.op("pe", lambda e: e.matmul(ps[:, :nt], lhsT=w[:, k, off:off + 128], rhs=IN[:, k, t0 - tshift:t0 - tshift + nt], start=(k == 0), stop=(k == KC - 1)),
                             reads=[w, IN], writes=[ps])
                    b = 0 if t0 < 1024 else 1
                    c.op("dve", lambda e: e.scalar_tensor_tensor(out=x[:, t0 - tshift:t0 - tshift + nt], in0=ps[:, :nt], scalar=mt[:, l, gbase + j, b:b + 1],
                                                                 in1=x[:, t0 - tshift:t0 - tshift + nt], op0=ALU.mult, op1=ALU.add), reads=[ps, mt, x], writes=[x])
                c.dma("sp", XTout[j][:, tshift:tshift + 1024], x[:], reads=[x])

    def stage_final(self, XT):
        c = self.c
        I = self.I
        c.begin_stage()
        nf = c.sb([128, 16], F32, "nf")
        self.load_cols(nf, nf[:], I["norm_f"].rearrange("(n p) -> n p", p=128), 16)
        xs = [c.sb([128, 16, 128], F32, "fx") for _ in range(2)]
        sq = c.sb([128, 16, 128], BF16, "fsq")
        r = c.sb([128, 128], F32, "fr")
        yo = [c.sb([128, 2048], F32, "yo") for _ in range(2)]
        ones, eps, idf = self.onesB, self.epsT, self.identF
        for tt in range(16):
            t0 = tt * 128
            x = xs[tt % 2]
            y = yo[tt % 2]
            c.dma("sp", x[:], XT[:, :, t0:t0 + 128].rearrange("c p t -> p c t"), writes=[x])
            c.op("act", lambda e: e.activation(out=sq[:], in_=x[:], func=AF.Square), reads=[x], writes=[sq])
            ps = self.pb()
            for ch in range(16):
                c.op("pe", lambda e: e.matmul(ps[:, :128], lhsT=ones[:], rhs=sq[:, ch, :], start=(ch == 0), stop=(ch == 15)), reads=[ones, sq], writes=[ps])
            c.op("act", lambda e: e.activation(out=r[:], in_=ps[:, :128], func=AF.Sqrt, scale=1.0 / 2048, bias=eps[:, 0:1]), reads=[ps, eps], writes=[r])
            c.op("dve", lambda e: e.reciprocal(out=r[:], in_=r[:]), reads=[r], writes=[r])
            for ch in range(16):
                c.op("dve", lambda e: e.scalar_tensor_tensor(out=x[:, ch, :], in0=x[:, ch, :], scalar=nf[:, ch:ch + 1], in1=r[:], op0=ALU.mult, op1=ALU.mult), reads=[x, nf, r], writes=[x])
            for g in range(4):
                p2 = self.pb()
                for j in range(4):
                    ch = g * 4 + j
                    c.op("pe", lambda e: e.transpose(out=p2[:, j * 128:(j + 1) * 128], in_=x[:, ch, :], identity=idf[:]), reads=[x, idf], writes=[p2])
                if g % 2 == 0:
                    c.op("dve", lambda e: e.tensor_copy(out=y[:, g * 512:(g + 1) * 512], in_=p2[:]), reads=[p2], writes=[y])
                else:
                    c.op("act", lambda e: e.copy(out=y[:, g * 512:(g + 1) * 512], in_=p2[:]), reads=[p2], writes=[y])
            c.dma("sp", self.O["y"][t0:t0 + 128, :], y[:], reads=[y])
        c.end_stage()


class KB3(KB2):
    def stage_ret(self, l):
        c = self.c
        I = self.I
        c.begin_stage()
        QK, VTM, KTM, SRG = self.S["QK"], self.S["VTM"], self.S["KTM"], self.S["SRG"]
        MO = self.scr("MO", [3, 8, 128, NT], BF16)
        OSR = self.O["osret"]
        ones, eps = self.onesB, self.epsT
        V = c.sb([128, 16, 1024], BF16, "V")
        c.dma("sp", V[:], VTM.rearrange("t p n -> p t n"), writes=[V])
        Kt = c.sb([128, 8, 512], BF16, "Kt")
        c.dma("sp", Kt[:], KTM.rearrange("t p n -> p t n"), writes=[Kt])
        lg = c.sb([128, 16], F32, "lg")
        c.dma("sp", lg[:], I["ret_decay"][l].rearrange("d h -> (d h)").partition_broadcast(128), writes=[lg])
        c.op("act", lambda e: e.activation(out=lg[:], in_=lg[:], func=AF.Exp, scale=-1.0), reads=[lg], writes=[lg])
        c.op("act", lambda e: e.activation(out=lg[:], in_=lg[:], func=AF.Ln, bias=self.onesF[:, 0:1]), reads=[lg], writes=[lg])
        c.op("dve", lambda e: e.tensor_scalar(out=lg[:], in0=lg[:], scalar1=-1.0, scalar2=None, op0=ALU.mult), reads=[lg], writes=[lg])
        tabs = {}
        for nm, w in (("S", 1920), ("P", 384)):
            for k in ("dpos", "dneg", "dz"):
                t = c.sb([128, w], F32, k + nm)
                c.dma("sp", t[:], I[k + nm], writes=[t])
                tabs[k + nm] = t
        expfb = c.sb([128, 4], F32, "expfb")
        c.dma("sp", expfb[:], I["expfb"], writes=[expfb])
        idx1 = c.sb([128, 1024], F32, "idx1")
        idx2 = c.sb([128, 1024], F32, "idx2")
        c.dma("sp", idx1[:], I["idx1"], writes=[idx1])
        c.dma("sp", idx2[:], I["idx2"], writes=[idx2])
        KD = c.sb([128, 2, 8, 2], F32, "KD")
        for d in range(2):
            for h in range(8):
                c.op("act", lambda e: e.activation(out=KD[:, d, h, :], in_=expfb[:, d * 2:d * 2 + 2], func=AF.Exp, scale=lg[:, d * 8 + h:d * 8 + h + 1]), reads=[expfb, lg], writes=[KD])
        TabS = c.sb([128, 1920], F32, "TabS")
        TabP = c.sb([128, 384], F32, "TabP")
        ttmp = c.sb([128, 1920], F32, "ttmp")
        qrow = c.sb([128, NT], BF16, "qrow")
        krow = c.sb([128, NT], BF16, "krow")
        lgsel = c.sb([128, 2], F32, "lgsel")
        dec = c.sb([128, 1024], F32, "dec")
        qdf = c.sb([128, 1024], BF16, "qdf")
        qdb = c.sb([128, 1024], BF16, "qdb")
        S0 = c.sb([128, 2, 128], BF16, "S0")
        srg = c.sb([128, NT], BF16, "srg")
        sm = [c.sb([128, 512], BF16, "sm") for _ in range(3)]
        osb = c.sb([128, 512], F32, "osb")
        osq = c.sb([128, 512], BF16, "osq")
        rr = c.sb([128, 512], F32, "rr")
        orow = c.sb([128, NT], BF16, "orow")
        kf = [c.sb([128, 64], BF16, "kf") for _ in range(4)]
        sst = [c.sb([64, 128], F32, "sst") for _ in range(4)]
        n_sm = [0]
        n_kf = [0]

        def build_tab(Tab, nm, w, h):
            dp, dn, dz = tabs["dpos" + nm], tabs["dneg" + nm], tabs["dz" + nm]
            c.op("dve", lambda e: e.tensor_scalar(out=ttmp[:, :w], in0=dp[:], scalar1=lg[:, h:h + 1], scalar2=None, op0=ALU.mult), reads=[dp, lg], writes=[ttmp])
            c.op("dve", lambda e: e.scalar_tensor_tensor(out=ttmp[:, :w], in0=dn[:], scalar=lg[:, 8 + h:9 + h], in1=ttmp[:, :w], op0=ALU.mult, op1=ALU.add), reads=[dn, lg, ttmp], writes=[ttmp])
            c.op("act", lambda e: e.activation(out=ttmp[:, :w], in_=ttmp[:, :w], func=AF.Exp), reads=[ttmp], writes=[ttmp])
            c.op("dve", lambda e: e.tensor_tensor(out=Tab[:, :w], in0=ttmp[:, :w], in1=dz[:], op=ALU.add), reads=[ttmp, dz], writes=[Tab])

        def finish_o(pso, n, h, tok0):
            c.op("act", lambda e: e.copy(out=osb[:, :n], in_=pso[:, :n]), reads=[pso], writes=[osb])
            c.op("act", lambda e: e.activation(out=osq[:, :n], in_=osb[:, :n], func=AF.Square), reads=[osb], writes=[osq])
            p2 = self.pb()
            c.op("pe", lambda e: e.matmul(p2[:, :n], lhsT=ones[:], rhs=osq[:, :n], start=True, stop=True), reads=[ones, osq], writes=[p2])
            c.op("act", lambda e: e.activation(out=rr[:, :n], in_=p2[:, :n], func=AF.Sqrt, scale=1.0 / 128, bias=eps[:, 0:1]), reads=[p2, eps], writes=[rr])
            c.op("dve", lambda e: e.reciprocal(out=rr[:, :n], in_=rr[:, :n]), reads=[rr], writes=[rr])
            c.op("dve", lambda e: e.tensor_tensor(out=osb[:, :n], in0=osb[:, :n], in1=rr[:, :n], op=ALU.mult), reads=[osb, rr], writes=[osb])
            c.op("dve", lambda e: e.tensor_tensor(out=orow[:, tok0:tok0 + n], in0=osb[:, :n], in1=srg[:, tok0:tok0 + n], op=ALU.mult), reads=[osb, srg], writes=[orow])

        for h in range(8):
            hp, po = h // 2, (h % 2) * 64
            if h % 2 == 0:
                c.dma("sp", qrow[:], QK[hp], writes=[qrow])
                c.dma("sp", krow[:], QK[4 + hp], writes=[krow])
                for d in range(2):
                    c.op("dve", lambda e: e.tensor_copy(out=lgsel[0:64, d:d + 1], in_=lg[0:64, d * 8 + h:d * 8 + h + 1]), reads=[lg], writes=[lgsel])
                    c.op("dve", lambda e: e.tensor_copy(out=lgsel[64:128, d:d + 1], in_=lg[64:128, d * 8 + h + 1:d * 8 + h + 2]), reads=[lg], writes=[lgsel])
                c.op("act", lambda e: e.activation(out=dec[:], in_=idx1[:], func=AF.Exp, scale=lgsel[:, 0:1]), reads=[idx1, lgsel], writes=[dec])
                c.op("dve", lambda e: e.tensor_tensor(out=qdf[:], in0=qrow[:, 1024:2048], in1=dec[:], op=ALU.mult), reads=[qrow, dec], writes=[qdf])
                c.op("act", lambda e: e.activation(out=dec[:], in_=idx2[:], func=AF.Exp, scale=lgsel[:, 1:2]), reads=[idx2, lgsel], writes=[dec])
                c.op("dve", lambda e: e.tensor_tensor(out=qdb[:], in0=qrow[:, 1024:2048], in1=dec[:], op=ALU.mult), reads=[qrow, dec], writes=[qdb])
            c.dma("sp", srg[:], SRG[h], writes=[srg])
            for d in range(2):
                c.dma("pool", S0[po:po + 64, d, :], I["sret"][l, d, h], writes=[S0])
            build_tab(TabS, "S", 1920, h)
            build_tab(TabP, "P", 384, h)
            for i0 in (0, 512):
                pso = self.pacc()
                for jc in range(8):
                    pss = self.pb()
                    c.op("pe", lambda e: e.matmul(pss[:, :512], lhsT=krow[po:po + 64, 1024 + jc * 128:1024 + (jc + 1) * 128], rhs=qrow[po:po + 64, 1024 + i0:1024 + i0 + 512], start=True, stop=True),
                         reads=[krow, qrow], writes=[pss])
                    s = sm[n_sm[0] % 3]
                    n_sm[0] += 1
                    u0 = i0 - 128 * jc + 896
                    c.op("dve", lambda e: e.tensor_tensor(out=s[:], in0=pss[:, :512], in1=TabS[:, u0:u0 + 512], op=ALU.mult), reads=[pss, TabS], writes=[s])
                    c.op("pe", lambda e: e.matmul(pso[:, :512], lhsT=V[:, 8 + jc, h * 128:(h + 1) * 128], rhs=s[:], start=(jc == 0), stop=False), reads=[V, s], writes=[pso])
                c.op("pe", lambda e: e.matmul(pso[:, :512], lhsT=S0[po:po + 64, 0, :], rhs=qdf[po:po + 64, i0:i0 + 512], start=False, stop=False), reads=[S0, qdf], writes=[pso])
                c.op("pe", lambda e: e.matmul(pso[:, :512], lhsT=S0[po:po + 64, 1, :], rhs=qdb[po:po + 64, i0:i0 + 512], start=False, stop=True), reads=[S0, qdb], writes=[pso])
                finish_o(pso, 512, h, 1024 + i0)
            for s_ in range(4):
                b0 = s_ * 256
                pso = self.pacc()
                for jc in range(2):
                    pss = self.pb()
                    c.op("pe", lambda e: e.matmul(pss[:, :256], lhsT=krow[po:po + 64, b0 + jc * 128:b0 + (jc + 1) * 128], rhs=qrow[po:po + 64, b0:b0 + 256], start=True, stop=True),
                         reads=[krow, qrow], writes=[pss])
                    s = sm[n_sm[0] % 3]
                    n_sm[0] += 1
                    u0 = 128 - 128 * jc
                    c.op("dve", lambda e: e.tensor_tensor(out=s[:, :256], in0=pss[:, :256], in1=TabP[:, u0:u0 + 256], op=ALU.mult), reads=[pss, TabP], writes=[s])
                    c.op("pe", lambda e: e.matmul(pso[:, :256], lhsT=V[:, s_ * 2 + jc, h * 128:(h + 1) * 128], rhs=s[:, :256], start=(jc == 0), stop=(jc == 1)), reads=[V, s], writes=[pso])
                finish_o(pso, 256, h, b0)
                for d in range(2):
                    pst = self.pacc()
                    for tt in range(2):
                        k_ = kf[n_kf[0] % 4]
                        n_kf[0] += 1
                        c.op("dve", lambda e: e.tensor_scalar(out=k_[:], in0=Kt[:, s_ * 2 + tt, h * 64:(h + 1) * 64], scalar1=KD[:, d, h, tt:tt + 1], scalar2=None, op0=ALU.mult), reads=[Kt, KD], writes=[k_])
                        c.op("pe", lambda e: e.matmul(pst[:64, :128], lhsT=k_[:], rhs=V[:, s_ * 2 + tt, h * 128:(h + 1) * 128], start=(tt == 0), stop=(tt == 1)), reads=[k_, V], writes=[pst])
                    st = sst[(s_ * 2 + d) % 4]
                    c.op("act", lambda e: e.copy(out=st[:], in_=pst[:64, :128]), reads=[pst], writes=[st])
                    c.dma("sp", OSR[s_, l, d, h], st[:], reads=[st])
            c.dma("sp", MO[0][h], orow[:], reads=[orow])
        c.end_stage()


import os


class KB4(KB3):
    def stage_dn(self, l):
        c = self.c
        I = self.I
        c.begin_stage()
        DQK, DVT, SDZ = self.S["DQK"], self.S["DVT"], self.S["SDZ"]
        MO = self.scr("MO", [3, 8, 128, NT], BF16)
        OSD = self.O["osdn"]
        onesF, onesB, eps, idF, idB = self.onesF, self.onesB, self.epsT, self.identF, self.identB
        gbraw = self.gbraw
        cm = {}
        for nm in ("UF", "UB", "SU", "SL"):
            t = c.sb([128, 128], F32, nm)
            c.dma("sp", t[:], I[nm], writes=[t])
            cm[nm] = t
        alog = c.sb([128, 16], F32, "alog")
        dtb = c.sb([128, 16], F32, "dtb")
        c.dma("sp", alog[:], I["dn_a_log"][l].rearrange("d h -> (d h)").partition_broadcast(128), writes=[alog])
        c.dma("sp", dtb[:], I["dn_dt_bias"][l].rearrange("d h -> (d h)").partition_broadcast(128), writes=[dtb])
        dnn = c.sb([128, 1], F32, "dnn")
        c.dma("sp", dnn[:], I["dn_norm"][l].rearrange("(p o) -> p o", o=1), writes=[dnn])
        c.op("act", lambda e: e.activation(out=alog[:], in_=alog[:], func=AF.Exp), reads=[alog], writes=[alog])
        c.op("dve", lambda e: e.tensor_scalar(out=alog[:], in0=alog[:], scalar1=-1.0, scalar2=None, op0=ALU.mult), reads=[alog], writes=[alog])
        G = c.sb([128, 16, 16], F32, "G")
        BT = c.sb([128, 16, 16], F32, "BT")
        NBT = c.sb([128, 16, 16], F32, "NBT")
        for tt in range(16):
            c.op("dve", lambda e: e.tensor_tensor(out=G[:, tt, :], in0=gbraw[:, tt, 0:16], in1=dtb[:], op=ALU.add), reads=[gbraw, dtb], writes=[G])
        c.op("act", lambda e: e.activation(out=G[:], in_=G[:], func=AF.Exp), reads=[G], writes=[G])
        c.op("act", lambda e: e.activation(out=G[:], in_=G[:], func=AF.Ln, bias=onesF[:, 0:1]), reads=[G, onesF], writes=[G])
        for tt in range(16):
            c.op("dve", lambda e: e.tensor_tensor(out=G[:, tt, :], in0=G[:, tt, :], in1=alog[:], op=ALU.mult), reads=[G, alog], writes=[G])
        c.op("act", lambda e: e.activation(out=BT[:], in_=gbraw[:, :, 16:32], func=AF.Sigmoid), reads=[gbraw], writes=[BT])
        c.op("dve", lambda e: e.tensor_scalar(out=NBT[:], in0=BT[:], scalar1=-1.0, scalar2=None, op0=ALU.mult), reads=[BT], writes=[NBT])
        GC = c.sb([128, 16, 16], F32, "GC")
        GL = c.sb([128, 16, 16], F32, "GL")
        for tt in range(16):
            for d in range(2):
                U = cm["UF"] if d == 0 else cm["UB"]
                ps = self.pb()
                c.op("pe", lambda e: e.matmul(ps[:, 0:8], lhsT=U[:], rhs=G[:, tt, d * 8:d * 8 + 8], start=True, stop=True), reads=[U, G], writes=[ps])
                c.op("pe", lambda e: e.matmul(ps[:, 8:16], lhsT=onesF[:], rhs=G[:, tt, d * 8:d * 8 + 8], start=True, stop=True), reads=[onesF, G], writes=[ps])
                c.op("dve", lambda e: e.tensor_copy(out=GC[:, tt, d * 8:d * 8 + 8], in_=ps[:, 0:8]), reads=[ps], writes=[GC])
                c.op("dve", lambda e: e.tensor_copy(out=GL[:, tt, d * 8:d * 8 + 8], in_=ps[:, 8:16]), reads=[ps], writes=[GL])
        BK = c.sb([128, 16, 16], F32, "BK")
        KDS = c.sb([128, 16, 16], F32, "KDS")
        EGL = c.sb([128, 16, 16], F32, "EGL")
        c.op("act", lambda e: e.activation(out=BK[:], in_=GC[:], func=AF.Exp), reads=[GC], writes=[BK])
        c.op("dve", lambda e: e.tensor_tensor(out=BK[:], in0=BK[:], in1=BT[:], op=ALU.mult), reads=[BK, BT], writes=[BK])
        c.op("dve", lambda e: e.tensor_tensor(out=KDS[:], in0=GL[:], in1=GC[:], op=ALU.subtract), reads=[GL, GC], writes=[KDS])
        c.op("act", lambda e: e.activation(out=KDS[:], in_=KDS[:], func=AF.Exp), reads=[KDS], writes=[KDS])
        c.op("act", lambda e: e.activation(out=EGL[:], in_=GL[:], func=AF.Exp), reads=[GL], writes=[EGL])

        if "GDBG" in self.dbg:
            gd = self.scr("GDBG", [4, 128, 16, 16], F32)
            for i_, t_ in enumerate((G, BT, GC, GL)):
                c.dma("sp", gd[i_], t_[:], reads=[t_])
        OACC = [c.sb([128, NT], F32, "oacc") for _ in range(8)]
        c.begin_stage()
        Sf = [c.sb([128, 128], F32, "Sf") for _ in range(8)]
        Sb = [c.sb([128, 128], BF16, "Sb") for _ in range(8)]
        NI = 8

        def ring(shape, dt, nm, n=NI):
            bufs = [c.sb(shape, dt, nm) for _ in range(n)]
            st = [0]

            def nxt():
                st[0] += 1
                return bufs[st[0] % n]
            return nxt
        r_q = ring([128, 128], BF16, "rq")
        r_k = ring([128, 128], BF16, "rk")
        r_v = ring([128, 128], BF16, "rv")
        r_gbc = ring([128, 128], F32, "rgbc")
        r_t = ring([128, 128], F32, "rt", 2 * NI)
        r_p = ring([128, 128], F32, "rp")
        r_pt = ring([128, 128], F32, "rpt")
        r_x = ring([128, 128], F32, "rx")
        r_vb = ring([128, 128], F32, "rvb")
        r_kbe = ring([128, 128], F32, "rkbe")
        r_kd = ring([128, 128], F32, "rkd")
        r_nw = ring([128, 128], F32, "rnw")
        r_at = ring([128, 128], BF16, "rat")
        r_qd = ring([128, 128], BF16, "rqd")
        r_vn = ring([128, 128], F32, "rvn")
        r_vnb = ring([128, 128], BF16, "rvnb")
        r_so = ring([128, 128], F32, "rso", 3)
        r_tw = ring([128, 256], F32, "rtw", 2 * NI)
        r_tq = ring([128, 256], F32, "rtq", 2 * NI)
        mlu = c.sb([128, 7, 256], F32, "mlu")
        mul = c.sb([128, 7, 256], F32, "mul")
        c.dma("sp", mlu[:], I["MLU"], writes=[mlu])
        c.dma("sp", mul[:], I["MUL"], writes=[mul])
        id2 = c.sb([128, 256], F32, "id2")
        c.dma("sp", id2[:, 0:128], I["ident"], writes=[id2])
        c.dma("sp", id2[:, 128:256], I["ident"], writes=[id2])
        PB = self.PB
        pbn = [0]

        def pbank():
            pbn[0] += 1
            return PB[pbn[0] % 8]

        def process(tt, h, d, last_seq):
            t0 = tt * 128
            col = d * 8 + h
            U = cm["UF"] if d == 0 else cm["UB"]
            MS = cm["SL"] if d == 0 else cm["SU"]
            MI = cm["UF"] if d == 0 else cm["UB"]
            gc_ap = GC[:, tt, col:col + 1]
            qT, kT, vT = r_q(), r_k(), r_v()
            c.dma("sp", qT[:], DQK[h][:, t0:t0 + 128], writes=[qT])
            c.dma("sp", kT[:], DQK[8 + h][:, t0:t0 + 128], writes=[kT])
            c.dma("sp", vT[:], DVT[h][:, t0:t0 + 128], writes=[vT])
            gbc = r_gbc()
            c.op("act", lambda e: e.activation(out=gbc[:], in_=onesF[:], func=AF.Copy, scale=G[:, tt, col:col + 1]), reads=[onesF, G], writes=[gbc])
            yield
            bank = PB[h]
            ptr = pR = pG = pT = ps1 = ps2 = pw = psv = pso = pss = bank
            ptb = bank[:].bitcast(BF16)
            c.op("pe", lambda e: e.transpose(out=ptb[:, 0:128], in_=kT[:], identity=idB[:]), reads=[kT, idB], writes=[ptr])
            c.op("pe", lambda e: e.transpose(out=ptb[:, 128:256], in_=vT[:], identity=idB[:]), reads=[vT, idB], writes=[ptr])
            c.op("pe", lambda e: e.matmul(pR[:, 128:256], lhsT=gbc[:], rhs=U[:], start=True, stop=True), reads=[gbc, U], writes=[pR])
            c.op("pe", lambda e: e.matmul(pG[:, 256:384], lhsT=kT[:], rhs=kT[:], start=True, stop=True), reads=[kT], writes=[pG])
            c.op("pe", lambda e: e.matmul(pG[:, 384:512], lhsT=kT[:], rhs=qT[:], start=True, stop=True), reads=[kT, qT], writes=[pG])
            yield
            vb, kbe, kd = r_vb(), r_kbe(), r_kd()
            c.op("dve", lambda e: e.tensor_scalar(out=kbe[:], in0=ptb[:, 0:128], scalar1=BK[:, tt, col:col + 1], scalar2=None, op0=ALU.mult), reads=[ptr, BK], writes=[kbe])
            c.op("dve", lambda e: e.tensor_scalar(out=kd[:], in0=ptb[:, 0:128], scalar1=KDS[:, tt, col:col + 1], scalar2=None, op0=ALU.mult), reads=[ptr, KDS], writes=[kd])
            c.op("dve", lambda e: e.tensor_scalar(out=vb[:], in0=ptb[:, 128:256], scalar1=BT[:, tt, col:col + 1], scalar2=None, op0=ALU.mult), reads=[ptr, BT], writes=[vb])
            d1, d2 = r_t(), r_t()
            c.op("dve", lambda e: e.tensor_scalar(out=d1[:], in0=pR[:, 128:256], scalar1=gc_ap, scalar2=0.0, op0=ALU.subtract, op1=ALU.max), reads=[pR, GC], writes=[d1])
            c.op("dve", lambda e: e.tensor_scalar(out=d2[:], in0=pR[:, 128:256], scalar1=gc_ap, scalar2=0.0, op0=ALU.subtract, op1=ALU.min), reads=[pR, GC], writes=[d2])
            er = gbc
            c.op("act", lambda e: e.activation(out=er[:], in_=pR[:, 128:256], func=AF.Exp), reads=[pR], writes=[er])
            yield
            c.op("act", lambda e: e.activation(out=d1[:], in_=d1[:], func=AF.Exp, scale=-1.0), reads=[d1], writes=[d1])
            c.op("act", lambda e: e.activation(out=d2[:], in_=d2[:], func=AF.Exp), reads=[d2], writes=[d2])
            qd = r_qd()
            c.op("pool", lambda e: e.tensor_tensor(out=qd[:], in0=qT[:], in1=er[:], op=ALU.mult), reads=[qT, er], writes=[qd])
            yield
            p0t = r_pt()
            c.op("dve", lambda e: e.scalar_tensor_tensor(out=d1[:], in0=pG[:, 256:384], scalar=NBT[:, tt, col:col + 1], in1=d1[:], op0=ALU.mult, op1=ALU.mult), reads=[pG, NBT, d1], writes=[d1])
            c.op("dve", lambda e: e.tensor_tensor(out=d2[:], in0=pG[:, 384:512], in1=d2[:], op=ALU.mult), reads=[pG, d2], writes=[d2])
            yield
            c.op("pool", lambda e: e.tensor_tensor(out=p0t[:], in0=d1[:], in1=MS[:], op=ALU.mult), reads=[d1, MS], writes=[p0t])
            at = r_at()
            c.op("pool", lambda e: e.tensor_tensor(out=at[:], in0=d2[:], in1=MI[:], op=ALU.mult), reads=[d2, MI], writes=[at])
            yield
            c.op("pe", lambda e: e.transpose(out=pT[:, :128], in_=p0t[:], identity=idF[:]), reads=[p0t, idF], writes=[pT])
            yield
            p0 = r_p()
            c.op("act", lambda e: e.copy(out=p0[:], in_=pT[:, :128]), reads=[pT], writes=[p0])
            MM = mlu if d == 0 else mul
            tw = r_tw()
            tq = r_tq()
            c.op("pool", lambda e: e.tensor_tensor(out=tq[:, 0:128], in0=p0t[:], in1=MM[:, 0, 0:128], op=ALU.mult), reads=[p0t, MM], writes=[tq])
            yield
            c.op("pool", lambda e: e.tensor_tensor(out=tq[:, 128:256], in0=p0[:], in1=MM[:, 0, 128:256], op=ALU.mult), reads=[p0, MM], writes=[tq])
            yield
            c.op("dve", lambda e: e.tensor_tensor(out=tw[:], in0=tq[:], in1=id2[:], op=ALU.add), reads=[tq, id2], writes=[tw])
            yield
            for s in range(1, 7):
                c.op("pe", lambda e: e.matmul(ps1[:, 0:128], lhsT=p0[:], rhs=tw[:, 0:128], start=True, stop=True), reads=[p0, tw], writes=[ps1])
                c.op("pe", lambda e: e.matmul(ps1[:, 128:256], lhsT=p0t[:], rhs=tw[:, 128:256], start=True, stop=True), reads=[p0t, tw], writes=[ps1])
                yield
                p1 = r_tq()
                c.op("act", lambda e: e.copy(out=p1[:], in_=ps1[:, 0:256]), reads=[ps1], writes=[p1])
                yield
                c.op("pe", lambda e: e.matmul(ps2[:, 256:384], lhsT=tw[:, 128:256], rhs=p1[:, 0:128], start=True, stop=True), reads=[tw, p1], writes=[ps2])
                c.op("pe", lambda e: e.matmul(ps2[:, 384:512], lhsT=tw[:, 0:128], rhs=p1[:, 128:256], start=True, stop=True), reads=[tw, p1], writes=[ps2])
                yield
                t2 = r_tq()
                c.op("dve", lambda e: e.tensor_tensor(out=t2[:], in0=ps2[:, 256:512], in1=MM[:, s, :], op=ALU.mult), reads=[ps2, MM], writes=[t2])
                yield
                ntw = r_tw()
                c.op("pool", lambda e: e.tensor_tensor(out=ntw[:], in0=t2[:], in1=tw[:], op=ALU.add), reads=[t2, tw], writes=[ntw])
                tw = ntw
                yield
            c.op("pe", lambda e: e.matmul(pw[:, :128], lhsT=kbe[:], rhs=tw[:, 128:256], start=True, stop=True), reads=[kbe, tw], writes=[pw])
            yield
            nw = r_nw()
            c.op("act", lambda e: e.activation(out=nw[:], in_=pw[:, :128], func=AF.Copy, scale=-1.0), reads=[pw], writes=[nw])
            yield
            S_f, S_b = Sf[h], Sb[h]
            c.op("pe", lambda e: e.matmul(psv[:, 128:256], lhsT=tw[:, 128:256], rhs=vb[:], start=True, stop=False), reads=[tw, vb], writes=[psv])
            c.op("pe", lambda e: e.matmul(psv[:, 128:256], lhsT=nw[:], rhs=S_f[:], start=False, stop=True), reads=[nw, S_f], writes=[psv])
            yield
            vn = r_vn()
            vnb = r_vnb()
            c.op("act", lambda e: e.copy(out=vn[:], in_=psv[:, 128:256]), reads=[psv], writes=[vn])
            c.op("act", lambda e: e.copy(out=vnb[:], in_=vn[:]), reads=[vn], writes=[vnb])
            yield
            c.op("pe", lambda e: e.matmul(pso[:, 256:384], lhsT=S_b[:], rhs=qd[:], start=True, stop=False), reads=[S_b, qd], writes=[pso])
            c.op("pe", lambda e: e.matmul(pso[:, 256:384], lhsT=vnb[:], rhs=at[:], start=False, stop=True), reads=[vnb, at], writes=[pso])
            c.op("pe", lambda e: e.matmul(pss[:, 384:512], lhsT=kd[:], rhs=vn[:], start=True, stop=True), reads=[kd, vn], writes=[pss])
            yield
            oa = OACC[h]
            if d == 0:
                c.op("act", lambda e: e.copy(out=oa[:, t0:t0 + 128], in_=pso[:, 256:384]), reads=[pso], writes=[oa])
            else:
                c.op("dve", lambda e: e.tensor_tensor(out=oa[:, t0:t0 + 128], in0=pso[:, 256:384], in1=oa[:, t0:t0 + 128], op=ALU.add), reads=[pso, oa], writes=[oa])
            c.op("dve", lambda e: e.scalar_tensor_tensor(out=S_f[:], in0=S_f[:], scalar=EGL[:, tt, col:col + 1], in1=pss[:, 384:512], op0=ALU.mult, op1=ALU.add), reads=[S_f, EGL, pss], writes=[S_f])
            yield
            c.op("act", lambda e: e.copy(out=S_b[:], in_=S_f[:]), reads=[S_f], writes=[S_b])
            if last_seq is not None:
                so = r_so()
                c.op("pool", lambda e: e.tensor_copy(out=so[:], in_=S_f[:]), reads=[S_f], writes=[so])
                c.dma("sp", OSD[last_seq, l, d, h], so[:], reads=[so])

        def run_round(gens):
            gens = list(gens)
            while gens:
                alive = []
                for g in gens:
                    try:
                        next(g)
                        alive.append(g)
                    except StopIteration:
                        pass
                gens = alive

        for d in range(2):
            seqs = [(s, [2 * s, 2 * s + 1]) for s in range(4)] + [(None, list(range(8, 16)))]
            for s, tiles in seqs:
                order = tiles if d == 0 else tiles[::-1]
                for h in range(8):
                    if s is None:
                        c.dma("sp", Sf[h][:], I["sdn"][l, d, h], writes=[Sf[h]])
                        c.op("act", lambda e: e.copy(out=Sb[h][:], in_=Sf[h][:]), reads=[Sf[h]], writes=[Sb[h]])
                    else:
                        c.op("pool", lambda e: e.memset(Sf[h][:], 0.0), writes=[Sf[h]])
                        c.op("pool", lambda e: e.memset(Sb[h][:], 0.0), writes=[Sb[h]])
                for i, tt in enumerate(order):
                    run_round([process(tt, h, d, s if (s is not None and i == len(order) - 1) else None) for h in range(8)])
        c.end_stage()
        osq = c.sb([128, 512], BF16, "dosq")
        rr = c.sb([128, 512], F32, "drr")
        sdz = [c.sb([128, NT], BF16, "sdz") for _ in range(2)]
        orow = [c.sb([128, NT], BF16, "dorow") for _ in range(2)]
        for h in range(8):
            oa = OACC[h]
            z_ = sdz[h % 2]
            orw = orow[h % 2]
            c.dma("sp", z_[:], SDZ[h], writes=[z_])
            for (u0, nu) in TILES:
                c.op("act", lambda e: e.activation(out=osq[:, :nu], in_=oa[:, u0:u0 + nu], func=AF.Square), reads=[oa], writes=[osq])
                p2 = self.pb()
                c.op("pe", lambda e: e.matmul(p2[:, :nu], lhsT=onesB[:], rhs=osq[:, :nu], start=True, stop=True), reads=[onesB, osq], writes=[p2])
                c.op("act", lambda e: e.activation(out=rr[:, :nu], in_=p2[:, :nu], func=AF.Sqrt, scale=1.0 / 128, bias=eps[:, 0:1]), reads=[p2, eps], writes=[rr])
                c.op("dve", lambda e: e.reciprocal(out=rr[:, :nu], in_=rr[:, :nu]), reads=[rr], writes=[rr])
                c.op("dve", lambda e: e.scalar_tensor_tensor(out=rr[:, :nu], in0=oa[:, u0:u0 + nu], scalar=dnn[:, 0:1], in1=rr[:, :nu], op0=ALU.mult, op1=ALU.mult), reads=[oa, dnn, rr], writes=[rr])
                c.op("dve", lambda e: e.tensor_tensor(out=orw[:, u0:u0 + nu], in0=rr[:, :nu], in1=z_[:, u0:u0 + nu], op=ALU.mult), reads=[rr, z_], writes=[orw])
            c.dma("sp", MO[2][h], orw[:], reads=[orw])
        c.end_stage()


PI = math.pi


class KB5(KB4):
    def hy_tables(self, l, L):
        c = self.c
        I = self.I
        nch = L // 128
        TAB = self.scr("HTAB%d" % L, [2, 2 * nch + 1, 128, 1024], BF16)
        c.begin_stage()
        onesF, m0 = self.onesF, None
        m0 = c.sb([128, 2], F32, "m0")
        c.dma("sp", m0[:], I["m0"], writes=[m0])
        fT = c.sb([33, L], F32, "fT")
        c.dma("sp", fT[:], I["featsT%d" % L], writes=[fT])
        w1 = c.sb([33, 64], F32, "w1")
        w2 = c.sb([64, 64], F32, "w2")
        w3 = c.sb([64, 4096], F32, "w3")
        c.dma("sp", w1[:], I["hy_w1"][l], writes=[w1])
        c.dma("sp", w2[:], I["hy_w2"][l], writes=[w2])
        c.dma("sp", w3[:], I["hy_w3"][l], writes=[w3])
        vec = c.sb([64, 4], F32, "hvec")
        for i, nm in enumerate(("hy_b1", "hy_freq1", "hy_b2", "hy_freq2")):
            c.dma("sp", vec[:, i:i + 1], I[nm][l].rearrange("(p o) -> p o", o=1), writes=[vec])
        hid1 = c.sb([64, L], F32, "hid1")
        hid2 = c.sb([64, L], F32, "hid2")
        msk = c.sb([64, 512], F32, "msk")

        def sin_layer(dst, wT, K, src, bi, fi):
            for t0 in range(0, L, 512):
                n = min(512, L - t0)
                ps = self.pb()
                c.op("pe", lambda e: e.matmul(ps[:64, :n], lhsT=wT[:K, :], rhs=src[:K, t0:t0 + n], start=True, stop=True), reads=[wT, src], writes=[ps])
                d = dst
                c.op("dve", lambda e: e.tensor_scalar(out=d[:, t0:t0 + n], in0=ps[:64, :n], scalar1=vec[:, bi:bi + 1], scalar2=vec[:, fi:fi + 1], op0=ALU.add, op1=ALU.mult), reads=[ps, vec], writes=[d])
                for _ in range(2):
                    c.op("dve", lambda e: e.tensor_scalar(out=msk[:, :n], in0=d[:, t0:t0 + n], scalar1=PI, scalar2=-2 * PI, op0=ALU.is_gt, op1=ALU.mult), reads=[d], writes=[msk])
                    c.op("dve", lambda e: e.tensor_tensor(out=d[:, t0:t0 + n], in0=d[:, t0:t0 + n], in1=msk[:, :n], op=ALU.add), reads=[d, msk], writes=[d])
                    c.op("dve", lambda e: e.tensor_scalar(out=msk[:, :n], in0=d[:, t0:t0 + n], scalar1=-PI, scalar2=2 * PI, op0=ALU.is_lt, op1=ALU.mult), reads=[d], writes=[msk])
                    c.op("dve", lambda e: e.tensor_tensor(out=d[:, t0:t0 + n], in0=d[:, t0:t0 + n], in1=msk[:, :n], op=ALU.add), reads=[d, msk], writes=[d])
                c.op("act", lambda e: e.activation(out=d[:, t0:t0 + n], in_=d[:, t0:t0 + n], func=AF.Sin), reads=[d], writes=[d])

        sin_layer(hid1, w1, 33, fT, 0, 1)
        sin_layer(hid2, w2, 64, hid1, 2, 3)
        win = c.sb([128, nch, 1024], F32, "win")
        c.dma("sp", win[:], I["win%d" % L].rearrange("(t p) c -> p t c", p=128), writes=[win])
        CA = c.sb([128, nch, L], BF16, "CA")
        MB = c.sb([128, nch, L], BF16, "MB")
        c.dma("pool", CA[:], I["CA%d" % L].rearrange("(t p) k -> p t k", p=128), writes=[CA])
        c.dma("pool", MB[:], I["MB%d" % L].rearrange("(t p) k -> p t k", p=128), writes=[MB])
        hw = [c.sb([128, nch, 512], F32, "hw") for _ in range(2)]
        ab = c.sb([128, 512], F32, "hab")
        rinv = c.sb([128, 512], F32, "hrinv")
        hs = c.sb([128, nch, 512], BF16, "hs")
        hd = c.sb([128, nch, 512], BF16, "hd")
        to = [c.sb([128, 512], BF16, "hto") for _ in range(3)]
        tf = [c.sb([128, 512], F32, "htf") for _ in range(2)]
        nto = [0]
        for order in range(2):
            for half in range(2):
                for d in range(2):
                    col0 = d * 2048 + order * 1024 + half * 512
                    H = hw[d]
                    pn = self.pacc()
                    for tt in range(nch):
                        ps = self.pb()
                        c.op("pe", lambda e: e.matmul(ps[:, :512], lhsT=hid2[:, tt * 128:(tt + 1) * 128], rhs=w3[:, col0:col0 + 512], start=True, stop=True), reads=[hid2, w3], writes=[ps])
                        c.op("dve", lambda e: e.tensor_tensor(out=H[:, tt, :], in0=ps[:, :512], in1=win[:, tt, half * 512:(half + 1) * 512], op=ALU.mult), reads=[ps, win], writes=[H])
                        c.op("act", lambda e: e.activation(out=ab[:], in_=H[:, tt, :], func=AF.Abs), reads=[H], writes=[ab])
                        c.op("pe", lambda e: e.matmul(pn[:, :512], lhsT=onesF[:], rhs=ab[:], start=(tt == 0), stop=(tt == nch - 1)), reads=[onesF, ab], writes=[pn])
                    c.op("dve", lambda e: e.tensor_scalar(out=rinv[:], in0=pn[:, :512], scalar1=EPS, scalar2=None, op0=ALU.add), reads=[pn], writes=[rinv])
                    c.op("dve", lambda e: e.reciprocal(out=rinv[:], in_=rinv[:]), reads=[rinv], writes=[rinv])
                    for tt in range(nch):
                        c.op("dve", lambda e: e.tensor_tensor(out=H[:, tt, :], in0=H[:, tt, :], in1=rinv[:], op=ALU.mult), reads=[H, rinv], writes=[H])
                c.op("dve", lambda e: e.tensor_tensor(out=hs[:], in0=hw[0][:], in1=hw[1][:], op=ALU.add), reads=[hw[0], hw[1]], writes=[hs])
                c.op("pool", lambda e: e.tensor_tensor(out=hd[:], in0=hw[0][:], in1=hw[1][:], op=ALU.subtract), reads=[hw[0], hw[1]], writes=[hd])
                cs = slice(half * 512, (half + 1) * 512)
                for kc in range(nch):
                    pa = self.pb()
                    for tt in range(nch):
                        c.op("pe", lambda e: e.matmul(pa[:, :512], lhsT=CA[:, tt, kc * 128:(kc + 1) * 128], rhs=hs[:, tt, :], start=(tt == 0), stop=(tt == nch - 1)), reads=[CA, hs], writes=[pa])
                    pbd = self.pb()
                    for tt in range(nch):
                        c.op("pe", lambda e: e.matmul(pbd[:, :512], lhsT=MB[:, tt, kc * 128:(kc + 1) * 128], rhs=hd[:, tt, :], start=(tt == 0), stop=(tt == nch - 1)), reads=[MB, hd], writes=[pbd])
                    oa = to[nto[0] % 3]
                    nto[0] += 1
                    c.op("act", lambda e: e.copy(out=oa[:], in_=pa[:, :512]), reads=[pa], writes=[oa])
                    c.dma("sp", TAB[order, kc][:, cs], oa[:], reads=[oa])
                    ob = to[nto[0] % 3]
                    nto[0] += 1
                    if kc > 0:
                        c.op("act", lambda e: e.copy(out=ob[:], in_=pbd[:, :512]), reads=[pbd], writes=[ob])
                        c.dma("sp", TAB[order, nch + kc][:, cs], ob[:], reads=[ob])
                    else:
                        pbs = self.pb()
                        for tt in range(nch):
                            c.op("pe", lambda e: e.matmul(pbs[:, :512], lhsT=MB[:, tt, 0:128], rhs=hs[:, tt, :], start=(tt == 0), stop=(tt == nch - 1)), reads=[MB, hs], writes=[pbs])
                        c.op("dve", lambda e: e.tensor_scalar(out=ob[:], in0=pbd[:, :512], scalar1=m0[:, 0:1], scalar2=None, op0=ALU.mult), reads=[pbd, m0], writes=[ob])
                        c.dma("sp", TAB[order, nch][:, cs], ob[:], reads=[ob])
                        t1, t2 = tf[0], tf[1]
                        c.op("dve", lambda e: e.tensor_scalar(out=t1[:], in0=pa[:, :512], scalar1=m0[:, 0:1], scalar2=None, op0=ALU.mult), reads=[pa, m0], writes=[t1])
                        c.op("dve", lambda e: e.tensor_scalar(out=t2[:], in0=pbs[:, :512], scalar1=m0[:, 1:2], scalar2=None, op0=ALU.mult), reads=[pbs, m0], writes=[t2])
                        oc = to[nto[0] % 3]
                        nto[0] += 1
                        c.op("pool", lambda e: e.tensor_tensor(out=oc[:], in0=t1[:], in1=t2[:], op=ALU.add), reads=[t1, t2], writes=[oc])
                        c.dma("sp", TAB[order, 2 * nch][:, cs], oc[:], reads=[oc])
        c.end_stage()

    def hy_data(self, l, L, tokbase, B):
        c = self.c
        I = self.I
        nch = L // 128
        ncol = B * 1024
        TAB = self.S["HTAB%d" % L]
        HV, HX = self.S["HV"], self.S["HX"]
        MO = self.scr("MO", [3, 8, 128, NT], BF16)
        Z1 = self.scr("HZ1", [8, 128, NT], F32)
        idB = self.identB
        c.begin_stage()
        CA = c.sb([128, nch, L], BF16, "CA")
        MB = c.sb([128, nch, L], BF16, "MB")
        IA = c.sb([128, nch, L], BF16, "IA")
        IB = c.sb([128, nch, L], BF16, "IB")
        for t, nm in ((CA, "CA"), (MB, "MB"), (IA, "IA"), (IB, "IB")):
            c.dma("pool", t[:], I["%s%d" % (nm, L)].rearrange("(t p) k -> p t k", p=128), writes=[t])
        AH = c.sb([128, nch, 1024], BF16, "AH")
        HB1 = c.sb([128, nch, 1024], BF16, "HB1")
        HA2 = c.sb([128, 1024], BF16, "HA2")
        hb = c.sb([128, 16], F32, "hbias")
        self.load_cols(hb, hb[:], I["hy_bias"][l].rearrange("o (n p) -> (o n) p", p=128), 16)
        z = c.sb([128, nch, ncol], BF16, "z")
        Y = c.sb([128, 2 * nch, ncol], BF16, "Y")
        ntok = B * L
        rowv = [c.sb([128, ntok], BF16, "hrv") for _ in range(2)]
        rowx = [c.sb([128, ntok], F32, "hrx") for _ in range(2)]
        rowz = [c.sb([128, ntok], F32, "hrz") for _ in range(2)]
        rowo = [c.sb([128, ntok], BF16, "hro") for _ in range(2)]
        tm = [c.sb([128, 512], F32, "htm") for _ in range(4)]

        def to_tokmajor(src_rows_fn):
            for ch in range(8):
                row = src_rows_fn(ch)
                for b in range(B):
                    for tq in range(0, nch, 4):
                        ps = self.pb()
                        pb16 = ps[:].bitcast(BF16)
                        nq = min(4, nch - tq)
                        for j in range(nq):
                            tt = tq + j
                            t0 = b * L + tt * 128
                            c.op("pe", lambda e: e.transpose(out=pb16[:, j * 128:(j + 1) * 128], in_=row[:, t0:t0 + 128], identity=idB[:]), reads=[row, idB], writes=[ps])
                        for j in range(nq):
                            tt = tq + j
                            eng = "dve" if j % 2 == 0 else "act"
                            dst = z[:, tt, b * 1024 + ch * 128:b * 1024 + (ch + 1) * 128]
                            if eng == "dve":
                                c.op("dve", lambda e: e.tensor_copy(out=dst, in_=pb16[:, j * 128:(j + 1) * 128]), reads=[ps], writes=[z])
                            else:
                                c.op("act", lambda e: e.copy(out=dst, in_=pb16[:, j * 128:(j + 1) * 128]), reads=[ps], writes=[z])

        for order in range(2):
            c.dma("sp", AH[:], TAB[order, 0:nch].rearrange("k p c -> p k c"), writes=[AH])
            c.dma("sp", HB1[:], TAB[order, nch:2 * nch].rearrange("k p c -> p k c"), writes=[HB1])
            c.dma("sp", HA2[:], TAB[order, 2 * nch], writes=[HA2])
            if order == 0:
                def rows_v(ch):
                    r = rowv[ch % 2]
                    c.dma("sp", r[:], HV[ch][:, tokbase:tokbase + ntok], writes=[r])
                    return r
                to_tokmajor(rows_v)
            else:
                def rows_z(ch):
                    rz = rowz[ch % 2]
                    r = rowv[ch % 2]
                    c.dma("sp", rz[:], Z1[ch][:, tokbase:tokbase + ntok], writes=[rz])
                    c.op("act", lambda e: e.copy(out=r[:], in_=rz[:]), reads=[rz], writes=[r])
                    return r
                to_tokmajor(rows_z)
            for ct in range(ncol // 512):
                c0 = (ct * 512) % 1024
                for kc in range(nch):
                    pa = self.pb()
                    for tt in range(nch):
                        c.op("pe", lambda e: e.matmul(pa[:, :512], lhsT=CA[:, tt, kc * 128:(kc + 1) * 128], rhs=z[:, tt, ct * 512:(ct + 1) * 512], start=(tt == 0), stop=(tt == nch - 1)), reads=[CA, z], writes=[pa])
                    pq = self.pb()
                    for tt in range(nch):
                        c.op("pe", lambda e: e.matmul(pq[:, :512], lhsT=MB[:, tt, kc * 128:(kc + 1) * 128], rhs=z[:, tt, ct * 512:(ct + 1) * 512], start=(tt == 0), stop=(tt == nch - 1)), reads=[MB, z], writes=[pq])
                    t1, t2, t3, t4 = tm
                    ah = AH[:, kc, c0:c0 + 512]
                    h1 = HB1[:, kc, c0:c0 + 512]
                    a2 = HA2[:, c0:c0 + 512] if kc == 0 else ah
                    c.op("dve", lambda e: e.tensor_tensor(out=t1[:], in0=pa[:, :512], in1=ah, op=ALU.mult), reads=[pa, AH], writes=[t1])
                    c.op("dve", lambda e: e.tensor_tensor(out=t2[:], in0=pq[:, :512], in1=h1, op=ALU.mult), reads=[pq, HB1], writes=[t2])
                    c.op("pool", lambda e: e.tensor_tensor(out=Y[:, kc, ct * 512:(ct + 1) * 512], in0=t1[:], in1=t2[:], op=ALU.subtract), reads=[t1, t2], writes=[Y])
                    c.op("dve", lambda e: e.tensor_tensor(out=t3[:], in0=pa[:, :512], in1=h1, op=ALU.mult), reads=[pa, HB1], writes=[t3])
                    c.op("dve", lambda e: e.tensor_tensor(out=t4[:], in0=pq[:, :512], in1=a2, op=ALU.mult), reads=[pq, HA2, AH], writes=[t4])
                    c.op("pool", lambda e: e.tensor_tensor(out=Y[:, nch + kc, ct * 512:(ct + 1) * 512], in0=t3[:], in1=t4[:], op=ALU.add), reads=[t3, t4], writes=[Y])
            for ch in range(8):
                rx = rowx[ch % 2]
                c.dma("sp", rx[:], HX[order * 8 + ch][:, tokbase:tokbase + ntok], writes=[rx])
                if order == 0:
                    rb = rowv[ch % 2]
                    c.dma("sp", rb[:], HV[ch][:, tokbase:tokbase + ntok], writes=[rb])
                    ro = rowz[ch % 2]
                else:
                    rb = rowz[ch % 2]
                    c.dma("sp", rb[:], Z1[ch][:, tokbase:tokbase + ntok], writes=[rb])
                    ro = rowo[ch % 2]
                for b in range(B):
                    for r0 in range(0, L, 512):
                        n = min(512, L - r0)
                        ps = self.pacc()
                        for kk in range(2 * nch):
                            M = IA if kk < nch else IB
                            c.op("pe", lambda e: e.matmul(ps[:, :n], lhsT=Y[:, kk, b * 1024 + ch * 128:b * 1024 + (ch + 1) * 128], rhs=M[:, kk % nch, r0:r0 + n], start=(kk == 0), stop=(kk == 2 * nch - 1)),
                                 reads=[Y, M], writes=[ps])
                        g0 = b * L + r0
                        t1 = tm[0]
                        c.op("dve", lambda e: e.scalar_tensor_tensor(out=t1[:, :n], in0=rb[:, g0:g0 + n], scalar=hb[:, order * 8 + ch:order * 8 + ch + 1], in1=ps[:, :n], op0=ALU.mult, op1=ALU.add), reads=[rb, hb, ps], writes=[t1])
                        c.op("dve", lambda e: e.tensor_tensor(out=ro[:, g0:g0 + n], in0=t1[:, :n], in1=rx[:, g0:g0 + n], op=ALU.mult), reads=[t1, rx], writes=[ro])
                if order == 0:
                    c.dma("sp", Z1[ch][:, tokbase:tokbase + ntok], ro[:], reads=[ro])
                else:
                    c.dma("sp", MO[1][ch][:, tokbase:tokbase + ntok], ro[:], reads=[ro])
            c.barrier()
        c.end_stage()

    def stage_hy(self, l):
        self.hy_tables(l, 256)
        self.hy_data(l, 256, 0, 4)
        self.hy_tables(l, 1024)
        self.hy_data(l, 1024, 1024, 1)

_CACHE = {}


def build_full():
    kb = KB5()
    c = kb.c
    kb.stage_input()
    c.mark('input')
    kb.stage_mod()
    c.mark('mod')
    XA = kb.S["XT0"]
    XB = kb.scr("XT1", [16, 128, NT], F32)
    XC = kb.scr("XT2", [16, 128, NT], F32)
    cur = XA
    for l in range(2):
        kb.stage_norm(cur, l, 0)
        c.mark('norm1')
        kb.stage_proj(l)
        c.end_stage()
        c.mark('proj')
        kb.stage_ret(l)
        c.mark('ret')
        kb.stage_hy(l)
        c.mark('hy')
        kb.stage_dn(l)
        c.mark('dn')
        kb.stage_merge(l, cur, XB)
        c.mark('merge+wo')
        kb.stage_norm(XB, l, 1)
        c.mark('norm2')
        kb.stage_ffn_up(l)
        c.end_stage()
        c.mark('ffn_up')
        kb.stage_ffn_down(l, XB, XC)
        c.mark('ffn_down')
        cur, XB, XC = XC, cur, XB
    kb.stage_final(cur)
    c.mark('final')
    c.finish()
    return kb


def kernel(**inputs):
    z = {k: np.ascontiguousarray(np.asarray(v)) for k, v in inputs.items()}
    if "kb" not in _CACHE:
        _CACHE["kb"] = build_full()
    kb = _CACHE["kb"]
    in_maps = []
    for core in range(8):
        m = dict(kb.consts)
        for k in WSPEC:
            m[k] = z[k]
        xp = z["x_prompt"][4 * core:4 * core + 4].reshape(1024, 2048)
        xs = z["x_sample"][core // 2]
        m["xin"] = np.ascontiguousarray(np.concatenate([xp, xs], 0))
        m["sret"] = np.ascontiguousarray(z["state_ret"][core // 2])
        m["sdn"] = np.ascontiguousarray(z["state_dn"][core // 2])
        m["cvec"] = np.ascontiguousarray(np.stack([z["c_ctx"], z["c"][core // 2]]))
        in_maps.append(m)
    res = run_bass_kernel_spmd(kb.nc, in_maps, core_ids=list(range(8))).results
    y_prompt = np.concatenate([np.asarray(res[c]["y"])[:1024].reshape(4, 256, 2048) for c in range(8)], 0).astype(np.float32)
    y_sample = np.stack([np.asarray(res[2 * b]["y"])[1024:] for b in range(4)], 0).astype(np.float32)
    new_ret = np.concatenate([np.asarray(res[c]["osret"]) for c in range(8)], 0).astype(np.float32)
    new_dn = np.concatenate([np.asarray(res[c]["osdn"]) for c in range(8)], 0).astype(np.float32)
    return (y_prompt, y_sample, new_ret, new_dn)
```

```python
import numpy as np
from contextlib import ExitStack
import concourse.bass as bass
import concourse.mybir as mybir
from concourse.bass_utils import run_bass_kernel_spmd

F32 = mybir.dt.float32
BF16 = mybir.dt.bfloat16
I32 = mybir.dt.int32
AF = mybir.ActivationFunctionType
ALU = mybir.AluOpType
AX = mybir.AxisListType


class Buf:
    __slots__ = ("t", "name", "w", "r", "psum")

    def __init__(self, t, name, psum=False):
        self.t = t
        self.name = name
        self.psum = psum
        self.w = None
        self.r = {}

    def __getitem__(self, idx):
        return self.t[idx]


class Ctx:
    SEM_LIMIT = 30000
    NDMA = 24

    def __init__(self, nc):
        self.nc = nc
        self.es = ExitStack()
        self.eng = {"pe": nc.tensor, "act": nc.scalar, "dve": nc.vector, "pool": nc.gpsimd, "sp": nc.sync}
        self.cur = {}
        self.waited = {k: {} for k in self.eng}
        self.nsem = 0
        for k in self.eng:
            self._new_sem(k)
        self.dpool = {}
        for q in ("sp", "pool", "act"):
            self.dpool[q] = [[self._alloc_sem("d%s%d" % (q, i)), 0] for i in range(self.NDMA)]
        self.dnext = {q: 0 for q in self.dpool}
        self.stage_es = None
        self.uid = 0
        self.tot = {}
        self.marks = []

    def mark(self, name):
        self.marks.append((name, dict(self.tot)))

    def _alloc_sem(self, name):
        self.nsem += 1
        s = self.es.enter_context(self.nc.semaphore("%s_%d" % (name, self.nsem)))
        if not hasattr(self, "allsems"):
            self.allsems = []
        self.allsems.append(s)
        return s

    def _new_sem(self, k):
        self.cur[k] = [self._alloc_sem("e" + k), 0]

    def begin_stage(self):
        if not hasattr(self, "stack"):
            self.stack = []
        self.stack.append(ExitStack())
        self.stage_es = self.stack[-1]

    def end_stage(self):
        self.barrier()
        self.stack.pop().close()
        self.stage_es = self.stack[-1] if self.stack else None

    def sb(self, shape, dt, name="t", persist=False):
        self.uid += 1
        nm = "%s_%d" % (name, self.uid)
        es = self.es if persist else self.stage_es
        t = es.enter_context(self.nc.sbuf_tensor(nm, list(shape), dt))
        return Buf(t, nm)

    def ps(self, shape, dt, name="p"):
        self.uid += 1
        nm = "%s_%d" % (name, self.uid)
        t = self.es.enter_context(self.nc.psum_tensor(nm, list(shape), dt))
        return Buf(t, nm, psum=True)

    def _wait(self, k, tok):
        if tok is None:
            return
        sem, val = tok
        if k == "pe" and sem is self.cur["pe"][0]:
            return
        w = self.waited[k]
        key = id(sem)
        if w.get(key, (None, 0))[1] >= val:
            return
        w[key] = (sem, val)
        self.eng[k].wait_ge(sem, val)

    def _deps(self, k, reads, writes):
        for b in reads:
            self._wait(k, b.w)
            if b.psum:
                for tok in list(b.r.values()):
                    self._wait(k, tok)
        for b in writes:
            self._wait(k, b.w)
            for tok in list(b.r.values()):
                self._wait(k, tok)

    def _commit(self, tok, reads, writes):
        for b in reads:
            b.r[id(tok[0])] = tok
        for b in writes:
            b.w = tok
            b.r = {}

    def op(self, k, fn, reads=(), writes=()):
        self._deps(k, reads, writes)
        c = self.cur[k]
        if c[1] >= self.SEM_LIMIT:
            self._new_sem(k)
            c = self.cur[k]
        c[1] += 1
        self.tot[k] = self.tot.get(k, 0) + 1
        fn(self.eng[k]).then_inc(c[0], 1)
        tok = (c[0], c[1])
        self._commit(tok, reads, writes)
        return tok

    def dma(self, q, out, in_, reads=(), writes=(), **kw):
        pool = self.dpool[q]
        i = self.dnext[q]
        self.dnext[q] = (i + 1) % len(pool)
        slot = pool[i]
        if slot[1] > 0:
            self._wait(q, (slot[0], slot[1]))
        if slot[1] >= self.SEM_LIMIT:
            slot[0] = self._alloc_sem("d" + q)
            slot[1] = 0
        self._deps(q, reads, writes)
        slot[1] += 16
        self.eng[q].dma_start(out=out, in_=in_, **kw).then_inc(slot[0], 16)
        tok = (slot[0], slot[1])
        self._commit(tok, reads, writes)
        return tok

    def barrier(self):
        toks = [(c[0], c[1]) for c in self.cur.values() if c[1] > 0]
        for q in self.dpool:
            for slot in self.dpool[q]:
                if slot[1] > 0:
                    toks.append((slot[0], slot[1]))
        for k in self.eng:
            for tok in toks:
                self._wait(k, tok)

    def finish(self):
        self.barrier()
        self.es.close()

import math
import numpy as np


def make_consts():
    f = np.float32
    C = {}
    i = np.arange(128)
    C["ident"] = np.eye(128, dtype=f)
    C["ones"] = np.ones((128, 128), f)
    C["UF"] = (i[:, None] <= i[None, :]).astype(f)
    C["UB"] = (i[:, None] >= i[None, :]).astype(f)
    C["SU"] = (i[:, None] < i[None, :]).astype(f)
    C["SL"] = (i[:, None] > i[None, :]).astype(f)
    L = 1024
    pos = np.arange(L, dtype=np.float64)
    pos_r = np.floor(pos / 64)
    pos_c = pos % 64
    inv = 10000.0 ** (-np.arange(16, dtype=np.float64) / 16)
    cosT = np.zeros((128, L))
    sinT = np.zeros((128, L))
    for p in range(128):
        q = p % 64
        half = q // 32
        x2 = (q % 32) // 16
        fi = q % 16
        ang = (pos_r if half == 0 else pos_c) * inv[fi]
        cosT[p] = np.cos(ang)
        sinT[p] = np.sin(ang) * (1.0 if x2 else -1.0)
    C["ropeC"] = cosT.astype(f)
    C["ropeS"] = sinT.astype(f)
    for nm, LL in (("S", 1024), ("P", 256)):
        u = np.arange(2 * LL - 128)[None, :]
        p = np.arange(128)[:, None]
        d = u - p - (LL - 128)
        C["dpos" + nm] = np.maximum(d, 0).astype(f)
        C["dneg" + nm] = np.maximum(-d, 0).astype(f)
        C["dz" + nm] = (d == 0).astype(f)
    j = np.arange(128)[:, None] + 128 * np.arange(2)[None, :]
    C["expfb"] = np.concatenate([255 - j, j], axis=1).astype(f)
    ii = np.arange(1024)[None, :].repeat(128, 0)
    C["idx1"] = (ii + 1).astype(f)
    C["idx2"] = (1024 - ii).astype(f)
    for LL in (256, 1024):
        t = np.linspace(0.0, 1.0, LL, dtype=np.float32)[:, None].astype(np.float64)
        wpos = 2.0 * math.pi * np.arange(LL, dtype=np.float64)[:, None] / LL
        fr = np.linspace(1e-4, 15, 16, dtype=np.float32)[None, :].astype(np.float64)
        feats = np.concatenate([t, np.cos(fr * wpos), -np.sin(fr * wpos)], axis=-1)
        C["featsT%d" % LL] = np.ascontiguousarray(feats.T).astype(f)
        deltas = np.abs(np.linspace(math.log(1e-2) / 0.3, math.log(1e-2) / 1.5, 1024, dtype=np.float32)).astype(np.float64)
        C["win%d" % LL] = np.exp(-t * deltas[None, :]).astype(f)
        N = 2 * LL
        tt = np.arange(LL, dtype=np.float64)[:, None]
        kk = np.arange(LL, dtype=np.float64)[None, :]
        ang = 2.0 * math.pi * tt * kk / N
        CA = np.cos(ang)
        MB = -np.sin(ang)
        MB[:, 0] = (-1.0) ** np.arange(LL)
        IA = (2.0 / N) * np.cos(ang.T)
        IA[0, :] = 1.0 / N
        IB = -(2.0 / N) * np.sin(ang.T)
        IB[0, :] = ((-1.0) ** np.arange(LL)) / N
        C["CA%d" % LL] = CA.astype(f)
        C["MB%d" % LL] = MB.astype(f)
        C["IA%d" % LL] = IA.astype(f)
        C["IB%d" % LL] = IB.astype(f)
    ii, jj = np.meshgrid(np.arange(128), np.arange(128), indexing="ij")
    ML = np.stack([(((ii >> (s + 1)) == (jj >> (s + 1))) & ((ii >> s) != (jj >> s)) & (ii > jj)).astype(f) for s in range(7)])
    MU = np.ascontiguousarray(ML.transpose(0, 2, 1))
    C["MLU"] = np.ascontiguousarray(np.concatenate([ML, MU], axis=2).transpose(1, 0, 2))
    C["MUL"] = np.ascontiguousarray(np.concatenate([MU, ML], axis=2).transpose(1, 0, 2))
    m0 = np.ones((128, 2), f)
    m0[0, 0] = 0.0
    m0[:, 1] = 1.0 - m0[:, 0]
    C["m0"] = m0
    C["eps"] = np.full((128, 1), 1e-6, f)
    return C


import math

NT = 2048
TILES = [(0, 512), (512, 512), (1024, 512), (1536, 512)]
EPS = 1e-6

WSPEC = dict(
    w_ada=[2, 2048, 12288], b_ada=[2, 12288], norm1=[2, 2048], w_in=[2, 2048, 16416], ret_decay=[2, 2, 8],
    hy_short=[2, 3, 3072], hy_w1=[2, 33, 64], hy_b1=[2, 64], hy_freq1=[2, 64], hy_w2=[2, 64, 64], hy_b2=[2, 64],
    hy_freq2=[2, 64], hy_w3=[2, 64, 4096], hy_bias=[2, 2, 1024], dn_conv=[2, 3, 3072], dn_a_log=[2, 2, 8],
    dn_dt_bias=[2, 2, 8], dn_norm=[2, 128], p_ret=[2, 1024, 2048], p_hy=[2, 1024, 2048], p_dn=[2, 1024, 2048],
    w_o=[2, 2048, 2048], norm2=[2, 2048], w_up=[2, 2048, 11008], ffn_conv=[2, 3, 11008], w_down=[2, 5504, 2048],
    norm_f=[2048])


class KB:
    def __init__(self, stop_after=None, dbg=(), wspec=None, ext_in=()):
        self.stop_after = stop_after
        self.dbg = set(dbg)
        nc = self.nc = bass.Bass("TRN2", target_bir_lowering=False)
        self.c = Ctx(nc)
        self.I = {}
        self.consts = make_consts()
        for k, v in self.consts.items():
            self.I[k] = nc.dram_tensor(k, list(v.shape), F32, kind="ExternalInput").ap()
        self.ext_in = set(ext_in)
        for k, s in (wspec or WSPEC).items():
            self.I[k] = nc.dram_tensor(k, s, F32, kind="ExternalInput").ap()
        self.I["xin"] = nc.dram_tensor("xin", [NT, 2048], F32, kind="ExternalInput").ap()
        self.I["sret"] = nc.dram_tensor("sret", [2, 2, 8, 64, 128], F32, kind="ExternalInput").ap()
        self.I["sdn"] = nc.dram_tensor("sdn", [2, 2, 8, 128, 128], F32, kind="ExternalInput").ap()
        self.I["cvec"] = nc.dram_tensor("cvec", [2, 2048], F32, kind="ExternalInput").ap()
        self.O = {}
        self.O["y"] = nc.dram_tensor("y", [NT, 2048], F32, kind="ExternalOutput").ap()
        self.O["osret"] = nc.dram_tensor("osret", [4, 2, 2, 8, 64, 128], F32, kind="ExternalOutput").ap()
        self.O["osdn"] = nc.dram_tensor("osdn", [4, 2, 2, 8, 128, 128], F32, kind="ExternalOutput").ap()
        self.S = {}
        c = self.c
        self.PB = [c.ps([128, 512], F32, "pb") for _ in range(8)]
        self.pbi = 0
        self.identF = c.sb([128, 128], F32, "identF", persist=True)
        self.identB = c.sb([128, 128], BF16, "identB", persist=True)
        self.onesB = c.sb([128, 128], BF16, "onesB", persist=True)
        self.onesF = c.sb([128, 128], F32, "onesF", persist=True)
        self.epsT = c.sb([128, 1], F32, "epsT", persist=True)
        self.rows = c.sb([128, 128], F32, "rows", persist=True)
        c.dma("sp", self.identF[:], self.I["ident"], writes=[self.identF])
        c.dma("pool", self.identB[:], self.I["ident"], writes=[self.identB])
        c.dma("pool", self.onesB[:], self.I["ones"], writes=[self.onesB])
        c.dma("sp", self.onesF[:], self.I["ones"], writes=[self.onesF])
        c.dma("sp", self.epsT[:], self.I["eps"], writes=[self.epsT])
        self.HT = None
        self.modT = c.sb([128, 2, 96, 2], F32, "modT", persist=True)
        self.sca = c.sb([128, 2, 2, 16, 2], F32, "sca", persist=True)
        self.gbraw = c.sb([128, 16, 32], F32, "gbraw", persist=True)

    def scr(self, name, shape, dt):
        if name not in self.S:
            kind = "ExternalOutput" if name in self.dbg else ("ExternalInput" if name in self.ext_in else "Internal")
            self.S[name] = self.nc.dram_tensor(name, list(shape), dt, kind=kind).ap()
        return self.S[name]

    def pb(self):
        self.pbi = (self.pbi + 1) % 6
        return self.PB[self.pbi]

    def pacc(self):
        self.pai = (getattr(self, "pai", 0) + 1) % 2
        return self.PB[6 + self.pai]

    def load_cols(self, dstbuf, dst_ap, src_rows, n):
        c = self.c
        rows = self.rows
        c.dma("sp", rows[:n, :], src_rows, writes=[rows])
        ps = self.pb()
        idf = self.identF
        c.op("pe", lambda e: e.transpose(out=ps[:, :n], in_=rows[:n, :], identity=idf[:n, :n]), reads=[rows, idf], writes=[ps])
        c.op("dve", lambda e: e.tensor_copy(out=dst_ap, in_=ps[:, :n]), reads=[ps], writes=[dstbuf])

    def stage_input(self):
        c = self.c
        XT = self.scr("XT0", [16, 128, NT], F32)
        c.begin_stage()
        xs = [c.sb([128, 2048], F32, "xs") for _ in range(2)]
        xo = [c.sb([128, 16, 128], F32, "xo") for _ in range(2)]
        idf = self.identF
        for tt in range(16):
            a = xs[tt % 2]
            o = xo[tt % 2]
            c.dma("sp", a[:], self.I["xin"][tt * 128:(tt + 1) * 128, :], writes=[a])
            for g in range(4):
                ps = self.pb()
                for j in range(4):
                    ch = g * 4 + j
                    c.op("pe", lambda e: e.transpose(out=ps[:, j * 128:(j + 1) * 128], in_=a[:, ch * 128:(ch + 1) * 128], identity=idf[:]),
                         reads=[a, idf], writes=[ps])
                eng = "dve" if g % 2 == 0 else "act"
                if eng == "dve":
                    c.op("dve", lambda e: e.tensor_copy(out=o[:, g * 4:(g + 1) * 4, :], in_=ps[:].rearrange("p (j t) -> p j t", j=4)), reads=[ps], writes=[o])
                else:
                    c.op("act", lambda e: e.copy(out=o[:, g * 4:(g + 1) * 4, :], in_=ps[:].rearrange("p (j t) -> p j t", j=4)), reads=[ps], writes=[o])
            c.dma("sp", XT[:, :, tt * 128:(tt + 1) * 128].rearrange("c p t -> p c t"), o[:], reads=[o])
        c.end_stage()

    def stage_mod(self):
        c = self.c
        I = self.I
        c.begin_stage()
        scT = c.sb([128, 2, 16], F32, "scT")
        self.load_cols(scT, scT[:].rearrange("p b k -> p (b k)"), I["cvec"].rearrange("b (k p) -> (b k) p", p=128), 32)
        c.op("act", lambda e: e.activation(out=scT[:], in_=scT[:], func=AF.Silu), reads=[scT], writes=[scT])
        bada = c.sb([128, 96], F32, "bada")
        nw = c.sb([128, 16], F32, "nw")
        wa = [c.sb([128, 16, 512], F32, "wa") for _ in range(3)]
        mrow = [c.sb([2, 512], F32, "mrow") for _ in range(2)]
        idf = self.identF
        for l in range(2):
            self.load_cols(bada, bada[:], I["b_ada"][l].rearrange("(n p) -> n p", p=128), 96)
            ps = self.pacc()
            for blk in range(24):
                w = wa[blk % 3]
                c.dma("sp" if blk % 2 == 0 else "act", w[:], I["w_ada"][l][:, blk * 512:(blk + 1) * 512].rearrange("(k p) n -> p k n", p=128), writes=[w])
                pr = self.pb()
                for k in range(16):
                    c.op("pe", lambda e: e.matmul(pr[:2, :512], lhsT=scT[:, :, k], rhs=w[:, k, :], start=(k == 0), stop=(k == 15)), reads=[w, scT], writes=[pr])
                mr = mrow[blk % 2]
                c.op("act", lambda e: e.copy(out=mr[:], in_=pr[:2, :512]), reads=[pr], writes=[mr])
                for j in range(4):
                    ch = blk * 4 + j
                    c.op("pe", lambda e: e.transpose(out=ps[:, ch * 2:ch * 2 + 2], in_=mr[:2, j * 128:(j + 1) * 128], identity=idf[:2, :2]), reads=[mr, idf], writes=[ps])
            mt = self.modT
            for b in range(2):
                c.op("dve", lambda e: e.tensor_tensor(out=mt[:, l, :, b], in0=ps[:, 0:192].rearrange("p (c b) -> p c b", b=2)[:, :, b], in1=bada[:], op=ALU.add),
                     reads=[ps, bada], writes=[mt])
            for which, (nm, scb) in enumerate((("norm1", 16), ("norm2", 64))):
                self.load_cols(nw, nw[:], I[nm][l].rearrange("(n p) -> n p", p=128), 16)
                sc = self.sca
                for b in range(2):
                    c.op("dve", lambda e: e.scalar_tensor_tensor(out=sc[:, l, which, :, b], in0=mt[:, l, scb:scb + 16, b], scalar=1.0, in1=nw[:], op0=ALU.add, op1=ALU.mult),
                         reads=[mt, nw], writes=[sc])
        c.end_stage()

    def stage_norm(self, XT, l, which):
        c = self.c
        c.begin_stage()
        self.HT = c.sb([128, 16, NT], BF16, "HT")
        c.begin_stage()
        shb = 0 if which == 0 else 48
        xs = [c.sb([128, 16, 256], F32, "nx") for _ in range(2)]
        sq = c.sb([128, 16, 256], BF16, "nsq")
        tmp = c.sb([128, 16, 256], F32, "ntmp")
        r = c.sb([128, 256], F32, "nr")
        HT, sc, mt, ones, eps = self.HT, self.sca, self.modT, self.onesB, self.epsT
        for ti in range(8):
            t0 = ti * 256
            b = 0 if t0 < 1024 else 1
            x = xs[ti % 2]
            c.dma("sp", x[:], XT[:, :, t0:t0 + 256].rearrange("c p t -> p c t"), writes=[x])
            c.op("act", lambda e: e.activation(out=sq[:], in_=x[:], func=AF.Square), reads=[x], writes=[sq])
            ps = self.pb()
            for ch in range(16):
                c.op("pe", lambda e: e.matmul(ps[:, :256], lhsT=ones[:], rhs=sq[:, ch, :], start=(ch == 0), stop=(ch == 15)), reads=[ones, sq], writes=[ps])
            c.op("act", lambda e: e.activation(out=r[:], in_=ps[:, :256], func=AF.Sqrt, scale=1.0 / 2048, bias=eps[:, 0:1]), reads=[ps, eps], writes=[r])
            c.op("dve", lambda e: e.reciprocal(out=r[:], in_=r[:]), reads=[r], writes=[r])
            for ch in range(16):
                c.op("dve", lambda e: e.tensor_tensor(out=tmp[:, ch, :], in0=x[:, ch, :], in1=r[:], op=ALU.mult), reads=[x, r], writes=[tmp])
                c.op("act", lambda e: e.activation(out=HT[:, ch, t0:t0 + 256], in_=tmp[:, ch, :], func=AF.Identity,
                                                   scale=sc[:, l, which, ch, b:b + 1], bias=mt[:, l, shb + ch, b:b + 1]), reads=[tmp, sc, mt], writes=[HT])
        c.end_stage()


class KB2(KB):
    def wbufs(self, KC, n=3, width=256):
        return [self.c.sb([128, KC, width], BF16, "wb") for _ in range(n)]

    def lin_fm(self, W, blocks, IN, KC, epi, tiles=TILES, wb=None, prep=None):
        c = self.c
        if wb is None:
            wb = self.wbufs(KC)
        for bi, (c0, ncol) in enumerate(blocks):
            w = wb[bi % len(wb)]
            c.dma("pool", w[:, :KC, :ncol], W[:, c0:c0 + ncol].rearrange("(k p) n -> p k n", p=128), writes=[w])
            aux = prep(w, c0, ncol) if prep else None
            for off in range(0, ncol, 128):
                n = min(128, ncol - off)
                for ti, (t0, nt) in enumerate(tiles):
                    ps = self.pb()
                    for k in range(KC):
                        c.op("pe", lambda e: e.matmul(ps[:n, :nt], lhsT=w[:, k, off:off + n], rhs=IN[:, k, t0:t0 + nt], start=(k == 0), stop=(k == KC - 1)),
                             reads=[w, IN], writes=[ps])
                    epi(c0 + off, n, ti, t0, nt, ps, (aux, off))

    def lin_tm(self, W, blocks, IN, KC, epi, ttiles, wb=None):
        c = self.c
        if wb is None:
            wb = self.wbufs(KC)
        for bi, (c0, ncol) in enumerate(blocks):
            w = wb[bi % len(wb)]
            c.dma("pool", w[:, :KC, :ncol], W[:, c0:c0 + ncol].rearrange("(k p) n -> p k n", p=128), writes=[w])
            for tt in ttiles:
                ps = self.pb()
                for k in range(KC):
                    c.op("pe", lambda e: e.matmul(ps[:, :ncol], lhsT=IN[:, k, tt * 128:(tt + 1) * 128], rhs=w[:, k, :ncol], start=(k == 0), stop=(k == KC - 1)),
                         reads=[w, IN], writes=[ps])
                epi(c0, ncol, tt, ps)

    @staticmethod
    def blocks(a, b, step=256):
        return [(x, min(step, b - x)) for x in range(a, b, step)]

    def conv_row(self, src, dst, wt, ci, ntap_chunks):
        c = self.c
        w0 = wt[:, 0 * ntap_chunks + ci:0 * ntap_chunks + ci + 1]
        w1 = wt[:, 1 * ntap_chunks + ci:1 * ntap_chunks + ci + 1]
        w2 = wt[:, 2 * ntap_chunks + ci:2 * ntap_chunks + ci + 1]
        c.op("act", lambda e: e.activation(out=dst[:], in_=src[:], func=AF.Copy, scale=w1), reads=[src, wt], writes=[dst])
        sp = src[:, 0:1024].rearrange("p (s t) -> p s t", t=256)
        dp = dst[:, 0:1024].rearrange("p (s t) -> p s t", t=256)
        c.op("dve", lambda e: e.scalar_tensor_tensor(out=dp[:, :, 1:256], in0=sp[:, :, 0:255], scalar=w0, in1=dp[:, :, 1:256], op0=ALU.mult, op1=ALU.add), reads=[src, wt, dst], writes=[dst])
        c.op("dve", lambda e: e.scalar_tensor_tensor(out=dp[:, :, 0:255], in0=sp[:, :, 1:256], scalar=w2, in1=dp[:, :, 0:255], op0=ALU.mult, op1=ALU.add), reads=[src, wt, dst], writes=[dst])
        c.op("dve", lambda e: e.scalar_tensor_tensor(out=dst[:, 1025:2048], in0=src[:, 1024:2047], scalar=w0, in1=dst[:, 1025:2048], op0=ALU.mult, op1=ALU.add), reads=[src, wt, dst], writes=[dst])
        c.op("dve", lambda e: e.scalar_tensor_tensor(out=dst[:, 1024:2047], in0=src[:, 1025:2048], scalar=w2, in1=dst[:, 1024:2047], op0=ALU.mult, op1=ALU.add), reads=[src, wt, dst], writes=[dst])

    def stage_proj(self, l):
        c = self.c
        I = self.I
        W = I["w_in"][l]
        HT = self.HT
        c.begin_stage()
        wb = self.wbufs(16, 3, 512)
        QK = self.scr("QK", [8, 128, NT], BF16)
        VTM = self.scr("VTM", [16, 128, 1024], BF16)
        KTM = self.scr("KTM", [8, 128, 512], BF16)
        SRG = self.scr("SRG", [8, 128, NT], BF16)
        HV = self.scr("HV", [8, 128, NT], BF16)
        HX = self.scr("HX", [16, 128, NT], F32)
        DQK = self.scr("DQK", [16, 128, NT], BF16)
        DVT = self.scr("DVT", [8, 128, NT], BF16)
        SDZ = self.scr("SDZ", [8, 128, NT], BF16)
        GATE = self.scr("GATE", [48, 128, NT], BF16)
        rowb = [c.sb([128, NT], BF16, "rowb") for _ in range(2)]
        rowf = [c.sb([128, NT], F32, "rowf") for _ in range(2)]
        rowg = [c.sb([128, NT], F32, "rowg") for _ in range(2)]
        cnt = [0]

        ropeC = c.sb([128, 1024], F32, "ropeC")
        ropeS = c.sb([128, 1024], F32, "ropeS")
        c.dma("sp", ropeC[:], I["ropeC"], writes=[ropeC])
        c.dma("sp", ropeS[:], I["ropeS"], writes=[ropeS])
        wperm = [c.sb([128, 16, 256], BF16, "wperm") for _ in range(2)]
        t1 = c.sb([128, 512], F32, "t1")
        t2 = c.sb([128, 512], F32, "t2")
        pc = [0]

        def prep_qk(w, c0, ncol):
            wp = wperm[pc[0] % 2]
            pc[0] += 1
            src = w[:, :, 0:256].rearrange("p k (a two s) -> p k a two s", two=2, s=16)
            dst = wp[:].rearrange("p k (a two s) -> p k a two s", two=2, s=16)
            c.op("dve", lambda e: e.tensor_copy(out=dst[:, :, :, 0, :], in_=src[:, :, :, 1, :]), reads=[w], writes=[wp])
            c.op("pool", lambda e: e.tensor_copy(out=dst[:, :, :, 1, :], in_=src[:, :, :, 0, :]), reads=[w], writes=[wp])
            return wp

        def epi_qk(col0, n, ti, t0, nt, ps, auxoff):
            wp, off = auxoff
            ci = col0 // 128
            row = rowb[ci % 2]
            scale = 1.0 if ci < 4 else 0.125
            if ti < 2:
                c.op("act", lambda e: e.activation(out=row[:, t0:t0 + nt], in_=ps[:, :nt], func=AF.Copy, scale=scale), reads=[ps], writes=[row])
            else:
                ps2 = self.pb()
                for k in range(16):
                    c.op("pe", lambda e: e.matmul(ps2[:, :nt], lhsT=wp[:, k, off:off + 128], rhs=HT[:, k, t0:t0 + nt], start=(k == 0), stop=(k == 15)), reads=[wp, HT], writes=[ps2])
                s0 = t0 - 1024
                c.op("dve", lambda e: e.tensor_tensor(out=t1[:, :nt], in0=ps[:, :nt], in1=ropeC[:, s0:s0 + nt], op=ALU.mult), reads=[ps, ropeC], writes=[t1])
                c.op("dve", lambda e: e.tensor_tensor(out=t2[:, :nt], in0=ps2[:, :nt], in1=ropeS[:, s0:s0 + nt], op=ALU.mult), reads=[ps2, ropeS], writes=[t2])
                c.op("dve", lambda e: e.tensor_tensor(out=t1[:, :nt], in0=t1[:, :nt], in1=t2[:, :nt], op=ALU.add), reads=[t1, t2], writes=[t1])
                c.op("act", lambda e: e.activation(out=row[:, t0:t0 + nt], in_=t1[:, :nt], func=AF.Copy, scale=scale), reads=[t1], writes=[row])
            if ti == 3:
                c.dma("sp", QK[ci], row[:], reads=[row])

        self.lin_fm(W, self.blocks(0, 1024), HT, 16, epi_qk, wb=wb, prep=prep_qk)

        stv = [c.sb([128, 512], BF16, "stv") for _ in range(4)]

        def epi_v(col0, ncol, tt, ps):
            s = stv[cnt[0] % 4]
            cnt[0] += 1
            if cnt[0] % 2:
                c.op("dve", lambda e: e.tensor_copy(out=s[:, :ncol], in_=ps[:, :ncol]), reads=[ps], writes=[s])
            else:
                c.op("act", lambda e: e.copy(out=s[:, :ncol], in_=ps[:, :ncol]), reads=[ps], writes=[s])
            c.dma("sp", VTM[tt][:, col0 - 1024:col0 - 1024 + ncol], s[:, :ncol], reads=[s])

        def epi_ktm(col0, ncol, tt, ps):
            s = stv[cnt[0] % 4]
            cnt[0] += 1
            c.op("act", lambda e: e.activation(out=s[:, :ncol], in_=ps[:, :ncol], func=AF.Copy, scale=0.125), reads=[ps], writes=[s])
            c.dma("sp", KTM[tt][:, col0 - 512:col0 - 512 + ncol], s[:, :ncol], reads=[s])

        self.lin_tm(W, self.blocks(1024, 2048, 512), HT, 16, epi_v, range(16), wb=wb)
        self.lin_tm(W, self.blocks(512, 1024, 512), HT, 16, epi_ktm, range(8), wb=wb)

        def mk_epi_act(base, dst, func):
            def epi(col0, n, ti, t0, nt, ps, aux):
                ci = (col0 - base) // 128
                row = rowb[ci % 2]
                c.op("act", lambda e: e.activation(out=row[:, t0:t0 + nt], in_=ps[:, :nt], func=func), reads=[ps], writes=[row])
                if ti == 3:
                    c.dma("sp", dst[ci], row[:], reads=[row])
            return epi

        self.lin_fm(W, self.blocks(2048, 3072, 512), HT, 16, mk_epi_act(2048, SRG, AF.Silu), wb=wb)
        self.lin_fm(W, self.blocks(9216, 10240, 512), HT, 16, mk_epi_act(9216, SDZ, AF.Silu), wb=wb)
        self.lin_fm(W, self.blocks(10272, 16416, 512), HT, 16, mk_epi_act(10272, GATE, AF.Sigmoid), wb=wb)

        hyw = c.sb([128, 72], F32, "hyw")
        self.load_cols(hyw, hyw[:], I["hy_short"][l].rearrange("k (n p) -> (k n) p", p=128), 72)
        dnw = c.sb([128, 72], F32, "dnw")
        self.load_cols(dnw, dnw[:], I["dn_conv"][l].rearrange("k (n p) -> (k n) p", p=128), 72)

        def epi_hy(col0, n, ti, t0, nt, ps, aux):
            ci = (col0 - 3072) // 128
            raw = rowf[ci % 2]
            c.op("act", lambda e: e.copy(out=raw[:, t0:t0 + nt], in_=ps[:, :nt]), reads=[ps], writes=[raw])
            if ti == 3:
                cv = rowg[ci % 2]
                self.conv_row(raw, cv, hyw, ci, 24)
                if ci < 8:
                    row = rowb[ci % 2]
                    c.op("pool", lambda e: e.tensor_copy(out=row[:], in_=cv[:]), reads=[cv], writes=[row])
                    c.dma("sp", HV[ci], row[:], reads=[row])
                else:
                    c.dma("sp", HX[ci - 8], cv[:], reads=[cv])

        self.lin_fm(W, self.blocks(3072, 6144, 512), HT, 16, epi_hy, wb=wb)

        sqb = c.sb([128, NT], BF16, "sqb")
        rn = c.sb([128, 512], F32, "rn")
        ones, eps = self.onesB, self.epsT

        def epi_dn(col0, n, ti, t0, nt, ps, aux):
            ci = (col0 - 6144) // 128
            raw = rowf[ci % 2]
            c.op("act", lambda e: e.copy(out=raw[:, t0:t0 + nt], in_=ps[:, :nt]), reads=[ps], writes=[raw])
            if ti == 3:
                cv = rowg[ci % 2]
                self.conv_row(raw, cv, dnw, ci, 24)
                c.op("act", lambda e: e.activation(out=cv[:], in_=cv[:], func=AF.Silu), reads=[cv], writes=[cv])
                row = rowb[ci % 2]
                if ci < 16:
                    c.op("act", lambda e: e.activation(out=sqb[:], in_=cv[:], func=AF.Square), reads=[cv], writes=[sqb])
                    for tj, (u0, nu) in enumerate(TILES):
                        p2 = self.pb()
                        c.op("pe", lambda e: e.matmul(p2[:, :nu], lhsT=ones[:], rhs=sqb[:, u0:u0 + nu], start=True, stop=True), reads=[ones, sqb], writes=[p2])
                        c.op("act", lambda e: e.activation(out=rn[:, :nu], in_=p2[:, :nu], func=AF.Sqrt, bias=eps[:, 0:1]), reads=[p2, eps], writes=[rn])
                        c.op("dve", lambda e: e.reciprocal(out=rn[:, :nu], in_=rn[:, :nu]), reads=[rn], writes=[rn])
                        sc_ = (128 ** -0.5) if ci < 8 else 1.0
                        c.op("dve", lambda e: e.scalar_tensor_tensor(out=row[:, u0:u0 + nu], in0=cv[:, u0:u0 + nu], scalar=sc_, in1=rn[:, :nu], op0=ALU.mult, op1=ALU.mult),
                             reads=[cv, rn], writes=[row])
                    c.dma("sp", DQK[ci], row[:], reads=[row])
                else:
                    c.op("pool", lambda e: e.tensor_copy(out=row[:], in_=cv[:]), reads=[cv], writes=[row])
                    c.dma("sp", DVT[ci - 16], row[:], reads=[row])

        self.lin_fm(W, self.blocks(6144, 9216, 512), HT, 16, epi_dn, wb=wb)

        gbraw = self.gbraw

        def epi_gb(col0, ncol, tt, ps):
            c.op("dve", lambda e: e.tensor_copy(out=gbraw[:, tt, :], in_=ps[:, :32]), reads=[ps], writes=[gbraw])

        self.lin_tm(W, [(10240, 32)], HT, 16, epi_gb, range(16), wb=wb)
        c.end_stage()

    def stage_merge(self, l, XTin, XTout):
        c = self.c
        I = self.I
        c.begin_stage()
        MO = self.scr("MO", [3, 8, 128, NT], BF16)
        GATE = self.S["GATE"]
        MIX = self.scr("MIX", [16, 128, NT], BF16)
        mo = [c.sb([128, 8, NT], BF16, "mo") for _ in range(3)]
        for b in range(3):
            c.dma("sp", mo[b][:], MO[b].rearrange("c p t -> p c t"), writes=[mo[b]])
        wp = [c.sb([128, 8, 512], BF16, "wp") for _ in range(3)]
        gr = [c.sb([128, NT], BF16, "gr") for _ in range(4)]
        acc = [c.sb([128, NT], F32, "acc") for _ in range(4)]
        mixb = [c.sb([128, NT], BF16, "mixb") for _ in range(2)]
        tmp = c.sb([128, 512], F32, "mtmp")
        PW = [I["p_ret"][l], I["p_hy"][l], I["p_dn"][l]]
        n = 0
        nw_ = 0
        for jg in range(4):
            for b in range(3):
                w = wp[nw_ % 3]
                nw_ += 1
                c.dma("pool", w[:], PW[b][:, jg * 512:(jg + 1) * 512].rearrange("(k p) n -> p k n", p=128), writes=[w])
                for jj in range(4):
                    j = jg * 4 + jj
                    a = acc[jj]
                    g = gr[n % 4]
                    n += 1
                    c.dma("sp", g[:], GATE[b * 16 + j], writes=[g])
                    for ti, (t0, nt) in enumerate(TILES):
                        ps = self.pb()
                        for k_ in range(8):
                            c.op("pe", lambda e: e.matmul(ps[:, :nt], lhsT=w[:, k_, jj * 128:(jj + 1) * 128], rhs=mo[b][:, k_, t0:t0 + nt], start=(k_ == 0), stop=(k_ == 7)), reads=[w, mo[b]], writes=[ps])
                        if b == 0:
                            c.op("dve", lambda e: e.tensor_tensor(out=a[:, t0:t0 + nt], in0=ps[:, :nt], in1=g[:, t0:t0 + nt], op=ALU.mult), reads=[ps, g], writes=[a])
                        else:
                            c.op("dve", lambda e: e.tensor_tensor(out=tmp[:, :nt], in0=ps[:, :nt], in1=g[:, t0:t0 + nt], op=ALU.mult), reads=[ps, g], writes=[tmp])
                            c.op("pool", lambda e: e.tensor_tensor(out=a[:, t0:t0 + nt], in0=a[:, t0:t0 + nt], in1=tmp[:, :nt], op=ALU.add), reads=[a, tmp], writes=[a])
            for jj in range(4):
                j = jg * 4 + jj
                m = mixb[j % 2]
                a = acc[jj]
                c.op("act", lambda e: e.copy(out=m[:], in_=a[:]), reads=[a], writes=[m])
                c.dma("sp", MIX[j], m[:], reads=[m])
        c.end_stage()
        c.begin_stage()
        mix = c.sb([128, 16, NT], BF16, "mix")
        c.dma("sp", mix[:], MIX.rearrange("c p t -> p c t"), writes=[mix])
        self.resid_epilogue(I["w_o"][l], 2048, mix, 16, l, 32, XTin, XTout)
        c.end_stage()

    def resid_epilogue(self, W, ncols, IN, KC, l, gbase, XTin, XTout, tiles=TILES, wb=None):
        c = self.c
        mt = self.modT
        xr = [c.sb([128, NT], F32, "xr") for _ in range(2)]
        lo = tiles[0][0]
        hi = tiles[-1][0] + tiles[-1][1]

        def epi(col0, n, ti, t0, nt, ps, aux):
            j = col0 // 128
            x = xr[j % 2]
            if ti == 0:
                c.dma("sp", x[:, lo:hi], XTin[j][:, lo:hi], writes=[x])
            b = 0 if t0 < 1024 else 1
            c.op("dve", lambda e: e.scalar_tensor_tensor(out=x[:, t0:t0 + nt], in0=ps[:, :nt], scalar=mt[:, l, gbase + j, b:b + 1], in1=x[:, t0:t0 + nt], op0=ALU.mult, op1=ALU.add),
                 reads=[ps, mt, x], writes=[x])
            if ti == len(tiles) - 1:
                c.dma("sp", XTout[j][:, lo:hi], x[:, lo:hi], reads=[x])

        if wb is None:
            wb = self.wbufs(KC, 3, 512)
        self.lin_fm(W, self.blocks(0, ncols, 512), IN, KC, epi, tiles=tiles, wb=wb)

    def stage_ffn_up(self, l):
        c = self.c
        I = self.I
        ACTT = self.scr("ACTT", [43, 128, NT], BF16)
        c.begin_stage()
        fw = c.sb([128, 3 * 86], F32, "fw")
        for k in range(3):
            self.load_cols(fw, fw[:, k * 86:(k + 1) * 86], I["ffn_conv"][l][k].rearrange("(n p) -> n p", p=128), 86)
        rowf = [c.sb([128, NT], F32, "frow") for _ in range(3)]
        ga = [c.sb([128, NT], F32, "ga") for _ in range(2)]
        gb = [c.sb([128, NT], F32, "gb") for _ in range(2)]
        ab = [c.sb([128, NT], BF16, "ab") for _ in range(2)]
        wb = self.wbufs(16, 4, 512)
        HT = self.HT
        W = I["w_up"][l]
        cnt = [0]
        nb = 0
        for blk in range(11):
            c0 = blk * 512
            ncol = min(512, 5504 - c0)
            wa_, wg_ = wb[nb % 4], wb[(nb + 1) % 4]
            nb += 2
            c.dma("pool", wa_[:, :, :ncol], W[:, c0:c0 + ncol].rearrange("(k p) n -> p k n", p=128), writes=[wa_])
            c.dma("pool", wg_[:, :, :ncol], W[:, 5504 + c0:5504 + c0 + ncol].rearrange("(k p) n -> p k n", p=128), writes=[wg_])
            for off in range(0, ncol, 128):
                ci = (c0 + off) // 128
                for which, w in ((0, wa_), (1, wg_)):
                    raw = rowf[cnt[0] % 3]
                    cnt[0] += 1
                    for ti, (t0, nt) in enumerate(TILES):
                        ps = self.pb()
                        for kk in range(16):
                            c.op("pe", lambda e: e.matmul(ps[:, :nt], lhsT=w[:, kk, off:off + 128], rhs=HT[:, kk, t0:t0 + nt], start=(kk == 0), stop=(kk == 15)), reads=[w, HT], writes=[ps])
                        if ti % 2 == 0:
                            c.op("act", lambda e: e.copy(out=raw[:, t0:t0 + nt], in_=ps[:, :nt]), reads=[ps], writes=[raw])
                        else:
                            c.op("dve", lambda e: e.tensor_copy(out=raw[:, t0:t0 + nt], in_=ps[:, :nt]), reads=[ps], writes=[raw])
                    if which == 0:
                        g = ga[ci % 2]
                        self.conv_row(raw, g, fw, ci, 86)
                        c.op("act", lambda e: e.activation(out=g[:], in_=g[:], func=AF.Silu), reads=[g], writes=[g])
                    else:
                        g = ga[ci % 2]
                        g2 = gb[ci % 2]
                        self.conv_row(raw, g2, fw, 43 + ci, 86)
                        a_ = ab[ci % 2]
                        c.op("pool", lambda e: e.tensor_tensor(out=a_[:], in0=g[:], in1=g2[:], op=ALU.mult), reads=[g, g2], writes=[a_])
                        c.dma("sp", ACTT[ci], a_[:], reads=[a_])
        c.end_stage()

    def stage_ffn_down(self, l, XTin, XTout):
        c = self.c
        I = self.I
        ACTT = self.S["ACTT"]
        for half in range(2):
            c.begin_stage()
            a = c.sb([128, 43, 1024], BF16, "actin")
            c.dma("sp", a[:], ACTT[:, :, half * 1024:(half + 1) * 1024].rearrange("c p t -> p c t"), writes=[a])

            class Shift:
                def __init__(s, buf, sh):
                    s.buf, s.sh = buf, sh
            wb = self.wbufs(43, 2, 512)
            tiles = [(half * 1024, 512), (half * 1024 + 512, 512)]
            self._resid_shift(I["w_down"][l], a, 43, l, 80, XTin, XTout, tiles, half * 1024, wb)
            c.end_stage()

    def _resid_shift(self, W, IN, KC, l, gbase, XTin, XTout, tiles, tshift, wb):
        c = self.c
        mt = self.modT
        xr = [c.sb([128, 1024], F32, "xr2") for _ in range(2)]
        blocks = self.blocks(0, 2048, 512)
        for bi, (c0, ncol) in enumerate(blocks):
            w = wb[bi % len(wb)]
            c.dma("pool", w[:, :KC, :ncol], W[:, c0:c0 + ncol].rearrange("(k p) n -> p k n", p=128), writes=[w])
            for off in range(0, ncol, 128):
                j = (c0 + off) // 128
                x = xr[j % 2]
                c.dma("sp", x[:], XTin[j][:, tshift:tshift + 1024], writes=[x])
                for ti, (t0, nt) in enumerate(tiles):
                    ps = self.pb()
                    for k in range(KC):
                        c.op("pe", lambda e: e.matmul(ps[:, :nt], lhsT=w[:, k, off:off + 128], rhs=IN[:, k, t0 - tshift:t0 - tshift + nt], start=(k == 0), stop=(k == KC - 1)),
                             reads=[w, IN], writes=[ps])
                    b = 0 if t0 < 1024 else 1
                    c.op("dve", lambda e: e.scalar_tensor_tensor(out=x[:, t0 - tshift:t0 - tshift + nt], in0=ps[:, :nt], scalar=mt[:, l, gbase + j, b:b + 1],
                                                                 in1=x[:, t0 - tshift:t0 - tshift + nt], op0=ALU.mult, op1=ALU.add), reads=[ps, mt, x], writes=[x])
                c.dma("sp", XTout[j][:, tshift:tshift + 1024], x[:], reads=[x])

    def stage_final(self, XT):
        c = self.c
        I = self.I
        c.begin_stage()
        nf = c.sb([128, 16], F32, "nf")
        self.load_cols(nf, nf[:], I["norm_f"].rearrange("(n p) -> n p", p=128), 16)
        xs = [c.sb([128, 16, 128], F32, "fx") for _ in range(2)]
        sq = c.sb([128, 16, 128], BF16, "fsq")
        r = c.sb([128, 128], F32, "fr")
        yo = [c.sb([128, 2048], F32, "yo") for _ in range(2)]
        ones, eps, idf = self.onesB, self.epsT, self.identF
        for tt in range(16):
            t0 = tt * 128
            x = xs[tt % 2]
            y = yo[tt % 2]
            c.dma("sp", x[:], XT[:, :, t0:t0 + 128].rearrange("c p t -> p c t"), writes=[x])
            c.op("act", lambda e: e.activation(out=sq[:], in_=x[:], func=AF.Square), reads=[x], writes=[sq])
            ps = self.pb()
            for ch in range(16):
                c.op("pe", lambda e: e.matmul(ps[:, :128], lhsT=ones[:], rhs=sq[:, ch, :], start=(ch == 0), stop=(ch == 15)), reads=[ones, sq], writes=[ps])
            c.op("act", lambda e: e.activation(out=r[:], in_=ps[:, :128], func=AF.Sqrt, scale=1.0 / 2048, bias=eps[:, 0:1]), reads=[ps, eps], writes=[r])
            c.op("dve", lambda e: e.reciprocal(out=r[:], in_=r[:]), reads=[r], writes=[r])
            for ch in range(16):
                c.op("dve", lambda e: e.scalar_tensor_tensor(out=x[:, ch, :], in0=x[:, ch, :], scalar=nf[:, ch:ch + 1], in1=r[:], op0=ALU.mult, op1=ALU.mult), reads=[x, nf, r], writes=[x])
            for g in range(4):
                p2 = self.pb()
                for j in range(4):
                    ch = g * 4 + j
                    c.op("pe", lambda e: e.transpose(out=p2[:, j * 128:(j + 1) * 128], in_=x[:, ch, :], identity=idf[:]), reads=[x, idf], writes=[p2])
                if g % 2 == 0:
                    c.op("dve", lambda e: e.tensor_copy(out=y[:, g * 512:(g + 1) * 512], in_=p2[:]), reads=[p2], writes=[y])
                else:
                    c.op("act", lambda e: e.copy(out=y[:, g * 512:(g + 1) * 512], in_=p2[:]), reads=[p2], writes=[y])
            c.dma("sp", self.O["y"][t0:t0 + 128, :], y[:], reads=[y])
        c.end_stage()


class KB3(KB2):
    def stage_ret(self, l):
        c = self.c
        I = self.I
        c.begin_stage()
        QK, VTM, KTM, SRG = self.S["QK"], self.S["VTM"], self.S["KTM"], self.S["SRG"]
        MO = self.scr("MO", [3, 8, 128, NT], BF16)
        OSR = self.O["osret"]
        ones, eps = self.onesB, self.epsT
        V = c.sb([128, 16, 1024], BF16, "V")
        c.dma("sp", V[:], VTM.rearrange("t p n -> p t n"), writes=[V])
        Kt = c.sb([128, 8, 512], BF16, "Kt")
        c.dma("sp", Kt[:], KTM.rearrange("t p n -> p t n"), writes=[Kt])
        lg = c.sb([128, 16], F32, "lg")
        c.dma("sp", lg[:], I["ret_decay"][l].rearrange("d h -> (d h)").partition_broadcast(128), writes=[lg])
        c.op("act", lambda e: e.activation(out=lg[:], in_=lg[:], func=AF.Exp, scale=-1.0), reads=[lg], writes=[lg])
        c.op("act", lambda e: e.activation(out=lg[:], in_=lg[:], func=AF.Ln, bias=self.onesF[:, 0:1]), reads=[lg], writes=[lg])
        c.op("dve", lambda e: e.tensor_scalar(out=lg[:], in0=lg[:], scalar1=-1.0, scalar2=None, op0=ALU.mult), reads=[lg], writes=[lg])
        tabs = {}
        for nm, w in (("S", 1920), ("P", 384)):
            for k in ("dpos", "dneg", "dz"):
                t = c.sb([128, w], F32, k + nm)
                c.dma("sp", t[:], I[k + nm], writes=[t])
                tabs[k + nm] = t
        expfb = c.sb([128, 4], F32, "expfb")
        c.dma("sp", expfb[:], I["expfb"], writes=[expfb])
        idx1 = c.sb([128, 1024], F32, "idx1")
        idx2 = c.sb([128, 1024], F32, "idx2")
        c.dma("sp", idx1[:], I["idx1"], writes=[idx1])
        c.dma("sp", idx2[:], I["idx2"], writes=[idx2])
        KD = c.sb([128, 2, 8, 2], F32, "KD")
        for d in range(2):
            for h in range(8):
                c.op("act", lambda e: e.activation(out=KD[:, d, h, :], in_=expfb[:, d * 2:d * 2 + 2], func=AF.Exp, scale=lg[:, d * 8 + h:d * 8 + h + 1]), reads=[expfb, lg], writes=[KD])
        TabS = c.sb([128, 1920], F32, "TabS")
        TabP = c.sb([128, 384], F32, "TabP")
        ttmp = c.sb([128, 1920], F32, "ttmp")
        qrow = c.sb([128, NT], BF16, "qrow")
        krow = c.sb([128, NT], BF16, "krow")
        lgsel = c.sb([128, 2], F32, "lgsel")
        dec = c.sb([128, 1024], F32, "dec")
        qdf = c.sb([128, 1024], BF16, "qdf")
        qdb = c.sb([128, 1024], BF16, "qdb")
        S0 = c.sb([128, 2, 128], BF16, "S0")
        srg = c.sb([128, NT], BF16, "srg")
        sm = [c.sb([128, 512], BF16, "sm") for _ in range(3)]
        osb = c.sb([128, 512], F32, "osb")
        osq = c.sb([128, 512], BF16, "osq")
        rr = c.sb([128, 512], F32, "rr")
        orow = c.sb([128, NT], BF16, "orow")
        kf = [c.sb([128, 64], BF16, "kf") for _ in range(4)]
        sst = [c.sb([64, 128], F32, "sst") for _ in range(4)]
        n_sm = [0]
        n_kf = [0]

        def build_tab(Tab, nm, w, h):
            dp, dn, dz = tabs["dpos" + nm], tabs["dneg" + nm], tabs["dz" + nm]
            c.op("dve", lambda e: e.tensor_scalar(out=ttmp[:, :w], in0=dp[:], scalar1=lg[:, h:h + 1], scalar2=None, op0=ALU.mult), reads=[dp, lg], writes=[ttmp])
            c.op("dve", lambda e: e.scalar_tensor_tensor(out=ttmp[:, :w], in0=dn[:], scalar=lg[:, 8 + h:9 + h], in1=ttmp[:, :w], op0=ALU.mult, op1=ALU.add), reads=[dn, lg, ttmp], writes=[ttmp])
            c.op("act", lambda e: e.activation(out=ttmp[:, :w], in_=ttmp[:, :w], func=AF.Exp), reads=[ttmp], writes=[ttmp])
            c.op("dve", lambda e: e.tensor_tensor(out=Tab[:, :w], in0=ttmp[:, :w], in1=dz[:], op=ALU.add), reads=[ttmp, dz], writes=[Tab])

        def finish_o(pso, n, h, tok0):
            c.op("act", lambda e: e.copy(out=osb[:, :n], in_=pso[:, :n]), reads=[pso], writes=[osb])
            c.op("act", lambda e: e.activation(out=osq[:, :n], in_=osb[:, :n], func=AF.Square), reads=[osb], writes=[osq])
            p2 = self.pb()
            c.op("pe", lambda e: e.matmul(p2[:, :n], lhsT=ones[:], rhs=osq[:, :n], start=True, stop=True), reads=[ones, osq], writes=[p2])
            c.op("act", lambda e: e.activation(out=rr[:, :n], in_=p2[:, :n], func=AF.Sqrt, scale=1.0 / 128, bias=eps[:, 0:1]), reads=[p2, eps], writes=[rr])
            c.op("dve", lambda e: e.reciprocal(out=rr[:, :n], in_=rr[:, :n]), reads=[rr], writes=[rr])
            c.op("dve", lambda e: e.tensor_tensor(out=osb[:, :n], in0=osb[:, :n], in1=rr[:, :n], op=ALU.mult), reads=[osb, rr], writes=[osb])
            c.op("dve", lambda e: e.tensor_tensor(out=orow[:, tok0:tok0 + n], in0=osb[:, :n], in1=srg[:, tok0:tok0 + n], op=ALU.mult), reads=[osb, srg], writes=[orow])

        srg2 = [srg, c.sb([128, NT], BF16, "srg2")]
        orow2 = [orow, c.sb([128, NT], BF16, "orow2")]
        sm.append(c.sb([128, 512], BF16, "sm"))
        NS = len(sm)
        pending = []

        def flush(keep):
            while len(pending) > keep:
                pending.pop(0)()

        def scores_loop(n_j, mk_score, mk_acc):
            nxt = mk_score(0)
            for jc in range(n_j):
                cur = nxt
                nxt = mk_score(jc + 1) if jc + 1 < n_j else None
                mk_acc(jc, cur)

        for h in range(8):
            hp, po = h // 2, (h % 2) * 64
            srg_h, orow_h = srg2[h % 2], orow2[h % 2]
            if h % 2 == 0:
                c.dma("sp", qrow[:], QK[hp], writes=[qrow])
                c.dma("sp", krow[:], QK[4 + hp], writes=[krow])
                for d in range(2):
                    c.op("dve", lambda e: e.tensor_copy(out=lgsel[0:64, d:d + 1], in_=lg[0:64, d * 8 + h:d * 8 + h + 1]), reads=[lg], writes=[lgsel])
                    c.op("dve", lambda e: e.tensor_copy(out=lgsel[64:128, d:d + 1], in_=lg[64:128, d * 8 + h + 1:d * 8 + h + 2]), reads=[lg], writes=[lgsel])
                c.op("act", lambda e: e.activation(out=dec[:], in_=idx1[:], func=AF.Exp, scale=lgsel[:, 0:1]), reads=[idx1, lgsel], writes=[dec])
                c.op("dve", lambda e: e.tensor_tensor(out=qdf[:], in0=qrow[:, 1024:2048], in1=dec[:], op=ALU.mult), reads=[qrow, dec], writes=[qdf])
                c.op("act", lambda e: e.activation(out=dec[:], in_=idx2[:], func=AF.Exp, scale=lgsel[:, 1:2]), reads=[idx2, lgsel], writes=[dec])
                c.op("dve", lambda e: e.tensor_tensor(out=qdb[:], in0=qrow[:, 1024:2048], in1=dec[:], op=ALU.mult), reads=[qrow, dec], writes=[qdb])
            c.dma("sp", srg_h[:], SRG[h], writes=[srg_h])
            for d in range(2):
                c.dma("pool", S0[po:po + 64, d, :], I["sret"][l, d, h], writes=[S0])
            build_tab(TabS, "S", 1920, h)
            build_tab(TabP, "P", 384, h)

            def fin(pso, n, tok0, h=h, srg_h=srg_h, orow_h=orow_h, store=False):
                def f():
                    c.op("act", lambda e: e.copy(out=osb[:, :n], in_=pso[:, :n]), reads=[pso], writes=[osb])
                    c.op("act", lambda e: e.activation(out=osq[:, :n], in_=osb[:, :n], func=AF.Square), reads=[osb], writes=[osq])
                    p2 = self.pb()
                    c.op("pe", lambda e: e.matmul(p2[:, :n], lhsT=ones[:], rhs=osq[:, :n], start=True, stop=True), reads=[ones, osq], writes=[p2])
                    c.op("act", lambda e: e.activation(out=rr[:, :n], in_=p2[:, :n], func=AF.Sqrt, scale=1.0 / 128, bias=eps[:, 0:1]), reads=[p2, eps], writes=[rr])
                    c.op("dve", lambda e: e.reciprocal(out=rr[:, :n], in_=rr[:, :n]), reads=[rr], writes=[rr])
                    c.op("dve", lambda e: e.tensor_tensor(out=osb[:, :n], in0=osb[:, :n], in1=rr[:, :n], op=ALU.mult), reads=[osb, rr], writes=[osb])
                    c.op("dve", lambda e: e.tensor_tensor(out=orow_h[:, tok0:tok0 + n], in0=osb[:, :n], in1=srg_h[:, tok0:tok0 + n], op=ALU.mult), reads=[osb, srg_h], writes=[orow_h])
                    if store:
                        c.dma("sp", MO[0][h], orow_h[:], reads=[orow_h])
                return f

            for i0 in (0, 512):
                pso = self.pacc()

                def mk_score(jc, i0=i0):
                    pss = self.pb()
                    c.op("pe", lambda e: e.matmul(pss[:, :512], lhsT=krow[po:po + 64, 1024 + jc * 128:1024 + (jc + 1) * 128], rhs=qrow[po:po + 64, 1024 + i0:1024 + i0 + 512], start=True, stop=True),
                         reads=[krow, qrow], writes=[pss])
                    return pss

                def mk_acc(jc, pss, i0=i0, pso=pso):
                    s = sm[n_sm[0] % NS]
                    n_sm[0] += 1
                    u0 = i0 - 128 * jc + 896
                    c.op("dve", lambda e: e.tensor_tensor(out=s[:], in0=pss[:, :512], in1=TabS[:, u0:u0 + 512], op=ALU.mult), reads=[pss, TabS], writes=[s])
                    c.op("pe", lambda e: e.matmul(pso[:, :512], lhsT=V[:, 8 + jc, h * 128:(h + 1) * 128], rhs=s[:], start=(jc == 0), stop=False), reads=[V, s], writes=[pso])

                scores_loop(8, mk_score, mk_acc)
                c.op("pe", lambda e: e.matmul(pso[:, :512], lhsT=S0[po:po + 64, 0, :], rhs=qdf[po:po + 64, i0:i0 + 512], start=False, stop=False), reads=[S0, qdf], writes=[pso])
                c.op("pe", lambda e: e.matmul(pso[:, :512], lhsT=S0[po:po + 64, 1, :], rhs=qdb[po:po + 64, i0:i0 + 512], start=False, stop=True), reads=[S0, qdb], writes=[pso])
                pending.append(fin(pso, 512, 1024 + i0))
                flush(1)
            for s_ in range(4):
                b0 = s_ * 256
                pso = self.pacc()

                def mk_score(jc, b0=b0):
                    pss = self.pb()
                    c.op("pe", lambda e: e.matmul(pss[:, :256], lhsT=krow[po:po + 64, b0 + jc * 128:b0 + (jc + 1) * 128], rhs=qrow[po:po + 64, b0:b0 + 256], start=True, stop=True),
                         reads=[krow, qrow], writes=[pss])
                    return pss

                def mk_acc(jc, pss, pso=pso, s_=s_):
                    s = sm[n_sm[0] % NS]
                    n_sm[0] += 1
                    u0 = 128 - 128 * jc
                    c.op("dve", lambda e: e.tensor_tensor(out=s[:, :256], in0=pss[:, :256], in1=TabP[:, u0:u0 + 256], op=ALU.mult), reads=[pss, TabP], writes=[s])
                    c.op("pe", lambda e: e.matmul(pso[:, :256], lhsT=V[:, s_ * 2 + jc, h * 128:(h + 1) * 128], rhs=s[:, :256], start=(jc == 0), stop=(jc == 1)), reads=[V, s], writes=[pso])

                scores_loop(2, mk_score, mk_acc)
                pending.append(fin(pso, 256, b0, store=(s_ == 3)))
                flush(1)
                for d in range(2):
                    pst = self.pb()
                    for tt in range(2):
                        k_ = kf[n_kf[0] % 4]
                        n_kf[0] += 1
                        c.op("dve", lambda e: e.tensor_scalar(out=k_[:], in0=Kt[:, s_ * 2 + tt, h * 64:(h + 1) * 64], scalar1=KD[:, d, h, tt:tt + 1], scalar2=None, op0=ALU.mult), reads=[Kt, KD], writes=[k_])
                        c.op("pe", lambda e: e.matmul(pst[:64, :128], lhsT=k_[:], rhs=V[:, s_ * 2 + tt, h * 128:(h + 1) * 128], start=(tt == 0), stop=(tt == 1)), reads=[k_, V], writes=[pst])
                    st = sst[(s_ * 2 + d) % 4]
                    c.op("act", lambda e: e.copy(out=st[:], in_=pst[:64, :128]), reads=[pst], writes=[st])
                    c.dma("sp", OSR[s_, l, d, h], st[:], reads=[st])
        flush(0)
        c.end_stage()


import os


class KB4(KB3):
    def stage_dn(self, l):
        c = self.c
        I = self.I
        c.begin_stage()
        DQK, DVT, SDZ = self.S["DQK"], self.S["DVT"], self.S["SDZ"]
        MO = self.scr("MO", [3, 8, 128, NT], BF16)
        OSD = self.O["osdn"]
        onesF, onesB, eps, idF, idB = self.onesF, self.onesB, self.epsT, self.identF, self.identB
        gbraw = self.gbraw
        cm = {}
        for nm in ("UF", "UB", "SU", "SL"):
            t = c.sb([128, 128], F32, nm)
            c.dma("sp", t[:], I[nm], writes=[t])
            cm[nm] = t
        alog = c.sb([128, 16], F32, "alog")
        dtb = c.sb([128, 16], F32, "dtb")
        c.dma("sp", alog[:], I["dn_a_log"][l].rearrange("d h -> (d h)").partition_broadcast(128), writes=[alog])
        c.dma("sp", dtb[:], I["dn_dt_bias"][l].rearrange("d h -> (d h)").partition_broadcast(128), writes=[dtb])
        dnn = c.sb([128, 1], F32, "dnn")
        c.dma("sp", dnn[:], I["dn_norm"][l].rearrange("(p o) -> p o", o=1), writes=[dnn])
        c.op("act", lambda e: e.activation(out=alog[:], in_=alog[:], func=AF.Exp), reads=[alog], writes=[alog])
        c.op("dve", lambda e: e.tensor_scalar(out=alog[:], in0=alog[:], scalar1=-1.0, scalar2=None, op0=ALU.mult), reads=[alog], writes=[alog])
        G = c.sb([128, 16, 16], F32, "G")
        BT = c.sb([128, 16, 16], F32, "BT")
        NBT = c.sb([128, 16, 16], F32, "NBT")
        for tt in range(16):
            c.op("dve", lambda e: e.tensor_tensor(out=G[:, tt, :], in0=gbraw[:, tt, 0:16], in1=dtb[:], op=ALU.add), reads=[gbraw, dtb], writes=[G])
        c.op("act", lambda e: e.activation(out=G[:], in_=G[:], func=AF.Exp), reads=[G], writes=[G])
        c.op("act", lambda e: e.activation(out=G[:], in_=G[:], func=AF.Ln, bias=onesF[:, 0:1]), reads=[G, onesF], writes=[G])
        for tt in range(16):
            c.op("dve", lambda e: e.tensor_tensor(out=G[:, tt, :], in0=G[:, tt, :], in1=alog[:], op=ALU.mult), reads=[G, alog], writes=[G])
        c.op("act", lambda e: e.activation(out=BT[:], in_=gbraw[:, :, 16:32], func=AF.Sigmoid), reads=[gbraw], writes=[BT])
        c.op("dve", lambda e: e.tensor_scalar(out=NBT[:], in0=BT[:], scalar1=-1.0, scalar2=None, op0=ALU.mult), reads=[BT], writes=[NBT])
        GC = c.sb([128, 16, 16], F32, "GC")
        GL = c.sb([128, 16, 16], F32, "GL")
        for tt in range(16):
            for d in range(2):
                U = cm["UF"] if d == 0 else cm["UB"]
                ps = self.pb()
                c.op("pe", lambda e: e.matmul(ps[:, 0:8], lhsT=U[:], rhs=G[:, tt, d * 8:d * 8 + 8], start=True, stop=True), reads=[U, G], writes=[ps])
                c.op("pe", lambda e: e.matmul(ps[:, 8:16], lhsT=onesF[:], rhs=G[:, tt, d * 8:d * 8 + 8], start=True, stop=True), reads=[onesF, G], writes=[ps])
                c.op("dve", lambda e: e.tensor_copy(out=GC[:, tt, d * 8:d * 8 + 8], in_=ps[:, 0:8]), reads=[ps], writes=[GC])
                c.op("dve", lambda e: e.tensor_copy(out=GL[:, tt, d * 8:d * 8 + 8], in_=ps[:, 8:16]), reads=[ps], writes=[GL])
        BK = c.sb([128, 16, 16], F32, "BK")
        KDS = c.sb([128, 16, 16], F32, "KDS")
        EGL = c.sb([128, 16, 16], F32, "EGL")
        c.op("act", lambda e: e.activation(out=BK[:], in_=GC[:], func=AF.Exp), reads=[GC], writes=[BK])
        c.op("dve", lambda e: e.tensor_tensor(out=BK[:], in0=BK[:], in1=BT[:], op=ALU.mult), reads=[BK, BT], writes=[BK])
        c.op("dve", lambda e: e.tensor_tensor(out=KDS[:], in0=GL[:], in1=GC[:], op=ALU.subtract), reads=[GL, GC], writes=[KDS])
        c.op("act", lambda e: e.activation(out=KDS[:], in_=KDS[:], func=AF.Exp), reads=[KDS], writes=[KDS])
        c.op("act", lambda e: e.activation(out=EGL[:], in_=GL[:], func=AF.Exp), reads=[GL], writes=[EGL])

        if "GDBG" in self.dbg:
            gd = self.scr("GDBG", [4, 128, 16, 16], F32)
            for i_, t_ in enumerate((G, BT, GC, GL)):
                c.dma("sp", gd[i_], t_[:], reads=[t_])
        OACC = [c.sb([128, NT], F32, "oacc") for _ in range(8)]
        c.begin_stage()
        Sf = [c.sb([128, 128], F32, "Sf") for _ in range(8)]
        Sb = [c.sb([128, 128], BF16, "Sb") for _ in range(8)]
        NI = 8

        def ring(shape, dt, nm, n=NI):
            bufs = [c.sb(shape, dt, nm) for _ in range(n)]
            st = [0]

            def nxt():
                st[0] += 1
                return bufs[st[0] % n]
            return nxt
        r_q = ring([128, 128], BF16, "rq")
        r_k = ring([128, 128], BF16, "rk")
        r_v = ring([128, 128], BF16, "rv")
        r_gbc = ring([128, 128], F32, "rgbc")
        r_t = ring([128, 128], F32, "rt", 2 * NI)
        r_vb = ring([128, 128], F32, "rvb")
        r_kbe = ring([128, 128], F32, "rkbe")
        r_kd = ring([128, 128], F32, "rkd")
        r_nw = ring([128, 128], F32, "rnw")
        r_at = ring([128, 128], BF16, "rat")
        r_qd = ring([128, 128], BF16, "rqd")
        r_vn = ring([128, 128], F32, "rvn")
        r_vnb = ring([128, 128], BF16, "rvnb")
        r_so = ring([128, 128], F32, "rso", 3)
        r_tw = ring([128, 256], F32, "rtw", 2 * NI)
        r_tq = ring([128, 256], F32, "rtq", NI)
        r_am = ring([128, 256], F32, "ram", 2 * NI)
        r_pp = ring([128, 256], F32, "rpp", NI)
        mlu = c.sb([128, 7, 256], F32, "mlu")
        mul = c.sb([128, 7, 256], F32, "mul")
        c.dma("sp", mlu[:], I["MLU"], writes=[mlu])
        c.dma("sp", mul[:], I["MUL"], writes=[mul])
        id2 = c.sb([128, 256], F32, "id2")
        c.dma("sp", id2[:, 0:128], I["ident"], writes=[id2])
        c.dma("sp", id2[:, 128:256], I["ident"], writes=[id2])
        PB = self.PB
        pbn = [0]

        def pbank():
            pbn[0] += 1
            return PB[pbn[0] % 8]

        def process(tt, h, d, last_seq):
            t0 = tt * 128
            col = d * 8 + h
            U = cm["UF"] if d == 0 else cm["UB"]
            MS = cm["SL"] if d == 0 else cm["SU"]
            MI = cm["UF"] if d == 0 else cm["UB"]
            gc_ap = GC[:, tt, col:col + 1]
            qT, kT, vT = r_q(), r_k(), r_v()
            c.dma("sp", qT[:], DQK[h][:, t0:t0 + 128], writes=[qT])
            c.dma("sp", kT[:], DQK[8 + h][:, t0:t0 + 128], writes=[kT])
            c.dma("sp", vT[:], DVT[h][:, t0:t0 + 128], writes=[vT])
            gbc = r_gbc()
            c.op("act", lambda e: e.activation(out=gbc[:], in_=onesF[:], func=AF.Copy, scale=G[:, tt, col:col + 1]), reads=[onesF, G], writes=[gbc])
            yield
            bank = PB[h]
            ptr = pR = pG = pT = ps1 = ps2 = pw = psv = pso = pss = bank
            ptb = bank[:].bitcast(BF16)
            c.op("pe", lambda e: e.transpose(out=ptb[:, 0:128], in_=kT[:], identity=idB[:]), reads=[kT, idB], writes=[ptr])
            c.op("pe", lambda e: e.transpose(out=ptb[:, 128:256], in_=vT[:], identity=idB[:]), reads=[vT, idB], writes=[ptr])
            c.op("pe", lambda e: e.matmul(pR[:, 128:256], lhsT=gbc[:], rhs=U[:], start=True, stop=True), reads=[gbc, U], writes=[pR])
            c.op("pe", lambda e: e.matmul(pG[:, 256:384], lhsT=kT[:], rhs=kT[:], start=True, stop=True), reads=[kT], writes=[pG])
            c.op("pe", lambda e: e.matmul(pG[:, 384:512], lhsT=kT[:], rhs=qT[:], start=True, stop=True), reads=[kT, qT], writes=[pG])
            yield
            vb, kbe, kd = r_vb(), r_kbe(), r_kd()
            c.op("dve", lambda e: e.tensor_scalar(out=kbe[:], in0=ptb[:, 0:128], scalar1=BK[:, tt, col:col + 1], scalar2=None, op0=ALU.mult), reads=[ptr, BK], writes=[kbe])
            c.op("dve", lambda e: e.tensor_scalar(out=kd[:], in0=ptb[:, 0:128], scalar1=KDS[:, tt, col:col + 1], scalar2=None, op0=ALU.mult), reads=[ptr, KDS], writes=[kd])
            c.op("dve", lambda e: e.tensor_scalar(out=vb[:], in0=ptb[:, 128:256], scalar1=BT[:, tt, col:col + 1], scalar2=None, op0=ALU.mult), reads=[ptr, BT], writes=[vb])
            d1, d2 = r_t(), r_t()
            c.op("dve", lambda e: e.tensor_scalar(out=d1[:], in0=pR[:, 128:256], scalar1=gc_ap, scalar2=0.0, op0=ALU.subtract, op1=ALU.max), reads=[pR, GC], writes=[d1])
            c.op("dve", lambda e: e.tensor_scalar(out=d2[:], in0=pR[:, 128:256], scalar1=gc_ap, scalar2=0.0, op0=ALU.subtract, op1=ALU.min), reads=[pR, GC], writes=[d2])
            er = gbc
            c.op("act", lambda e: e.activation(out=er[:], in_=pR[:, 128:256], func=AF.Exp), reads=[pR], writes=[er])
            yield
            c.op("act", lambda e: e.activation(out=d1[:], in_=d1[:], func=AF.Exp, scale=-1.0), reads=[d1], writes=[d1])
            c.op("act", lambda e: e.activation(out=d2[:], in_=d2[:], func=AF.Exp), reads=[d2], writes=[d2])
            qd = r_qd()
            c.op("pool", lambda e: e.tensor_tensor(out=qd[:], in0=qT[:], in1=er[:], op=ALU.mult), reads=[qT, er], writes=[qd])
            yield
            pp = r_pp()
            c.op("dve", lambda e: e.scalar_tensor_tensor(out=d1[:], in0=pG[:, 256:384], scalar=NBT[:, tt, col:col + 1], in1=d1[:], op0=ALU.mult, op1=ALU.mult), reads=[pG, NBT, d1], writes=[d1])
            c.op("dve", lambda e: e.tensor_tensor(out=d2[:], in0=pG[:, 384:512], in1=d2[:], op=ALU.mult), reads=[pG, d2], writes=[d2])
            yield
            c.op("pool", lambda e: e.tensor_tensor(out=pp[:, 0:128], in0=d1[:], in1=MS[:], op=ALU.mult), reads=[d1, MS], writes=[pp])
            at = r_at()
            c.op("pool", lambda e: e.tensor_tensor(out=at[:], in0=d2[:], in1=MI[:], op=ALU.mult), reads=[d2, MI], writes=[at])
            yield
            c.op("pe", lambda e: e.transpose(out=pT[:, :128], in_=pp[:, 0:128], identity=idF[:]), reads=[pp, idF], writes=[pT])
            yield
            c.op("act", lambda e: e.copy(out=pp[:, 128:256], in_=pT[:, :128]), reads=[pT], writes=[pp])
            yield
            MM = mlu if d == 0 else mul
            am = r_am()
            c.op("pool", lambda e: e.tensor_tensor(out=am[:], in0=pp[:], in1=MM[:, 0, :], op=ALU.mult), reads=[pp, MM], writes=[am])
            yield
            tw = r_tw()
            c.op("dve", lambda e: e.tensor_tensor(out=tw[:], in0=am[:], in1=id2[:], op=ALU.add), reads=[am, id2], writes=[tw])
            am = r_am()
            c.op("pool", lambda e: e.tensor_tensor(out=am[:], in0=pp[:], in1=MM[:, 1, :], op=ALU.mult), reads=[pp, MM], writes=[am])
            yield
            for s in range(1, 7):
                c.op("pe", lambda e: e.matmul(ps1[:, 0:128], lhsT=am[:, 128:256], rhs=tw[:, 0:128], start=True, stop=True), reads=[am, tw], writes=[ps1])
                c.op("pe", lambda e: e.matmul(ps1[:, 128:256], lhsT=am[:, 0:128], rhs=tw[:, 128:256], start=True, stop=True), reads=[am, tw], writes=[ps1])
                if s < 6:
                    am = r_am()
                    c.op("pool", lambda e: e.tensor_tensor(out=am[:], in0=pp[:], in1=MM[:, s + 1, :], op=ALU.mult), reads=[pp, MM], writes=[am])
                yield
                p1 = r_tq()
                c.op("act", lambda e: e.copy(out=p1[:], in_=ps1[:, 0:256]), reads=[ps1], writes=[p1])
                yield
                c.op("pe", lambda e: e.matmul(ps2[:, 256:384], lhsT=tw[:, 128:256], rhs=p1[:, 0:128], start=True, stop=True), reads=[tw, p1], writes=[ps2])
                c.op("pe", lambda e: e.matmul(ps2[:, 384:512], lhsT=tw[:, 0:128], rhs=p1[:, 128:256], start=True, stop=True), reads=[tw, p1], writes=[ps2])
                yield
                ntw = r_tw()
                c.op("dve", lambda e: e.tensor_tensor(out=ntw[:], in0=ps2[:, 256:512], in1=tw[:], op=ALU.add), reads=[ps2, tw], writes=[ntw])
                tw = ntw
                yield
            c.op("pe", lambda e: e.matmul(pw[:, :128], lhsT=kbe[:], rhs=tw[:, 128:256], start=True, stop=True), reads=[kbe, tw], writes=[pw])
            yield
            nw = r_nw()
            c.op("act", lambda e: e.activation(out=nw[:], in_=pw[:, :128], func=AF.Copy, scale=-1.0), reads=[pw], writes=[nw])
            yield
            S_f, S_b = Sf[h], Sb[h]
            c.op("pe", lambda e: e.matmul(psv[:, 128:256], lhsT=tw[:, 128:256], rhs=vb[:], start=True, stop=False), reads=[tw, vb], writes=[psv])
            c.op("pe", lambda e: e.matmul(psv[:, 128:256], lhsT=nw[:], rhs=S_f[:], start=False, stop=True), reads=[nw, S_f], writes=[psv])
            yield
            vn = r_vn()
            vnb = r_vnb()
            c.op("act", lambda e: e.copy(out=vn[:], in_=psv[:, 128:256]), reads=[psv], writes=[vn])
            c.op("act", lambda e: e.copy(out=vnb[:], in_=vn[:]), reads=[vn], writes=[vnb])
            yield
            c.op("pe", lambda e: e.matmul(pso[:, 256:384], lhsT=S_b[:], rhs=qd[:], start=True, stop=False), reads=[S_b, qd], writes=[pso])
            c.op("pe", lambda e: e.matmul(pso[:, 256:384], lhsT=vnb[:], rhs=at[:], start=False, stop=True), reads=[vnb, at], writes=[pso])
            c.op("pe", lambda e: e.matmul(pss[:, 384:512], lhsT=kd[:], rhs=vn[:], start=True, stop=True), reads=[kd, vn], writes=[pss])
            yield
            oa = OACC[h]
            if d == 0:
                c.op("act", lambda e: e.copy(out=oa[:, t0:t0 + 128], in_=pso[:, 256:384]), reads=[pso], writes=[oa])
            else:
                c.op("dve", lambda e: e.tensor_tensor(out=oa[:, t0:t0 + 128], in0=pso[:, 256:384], in1=oa[:, t0:t0 + 128], op=ALU.add), reads=[pso, oa], writes=[oa])
            c.op("dve", lambda e: e.scalar_tensor_tensor(out=S_f[:], in0=S_f[:], scalar=EGL[:, tt, col:col + 1], in1=pss[:, 384:512], op0=ALU.mult, op1=ALU.add), reads=[S_f, EGL, pss], writes=[S_f])
            yield
            c.op("act", lambda e: e.copy(out=S_b[:], in_=S_f[:]), reads=[S_f], writes=[S_b])
            if last_seq is not None:
                so = r_so()
                c.op("pool", lambda e: e.tensor_copy(out=so[:], in_=S_f[:]), reads=[S_f], writes=[so])
                c.dma("sp", OSD[last_seq, l, d, h], so[:], reads=[so])

        def init_state(s, h, d):
            if s is None:
                c.dma("sp", Sf[h][:], I["sdn"][l, d, h], writes=[Sf[h]])
                c.op("act", lambda e: e.copy(out=Sb[h][:], in_=Sf[h][:]), reads=[Sf[h]], writes=[Sb[h]])
            else:
                c.op("pool", lambda e: e.memset(Sf[h][:], 0.0), writes=[Sf[h]])
                c.op("pool", lambda e: e.memset(Sb[h][:], 0.0), writes=[Sb[h]])

        def inst(tt, h, d, s, first, last):
            if first:
                init_state(s, h, d)
            yield from process(tt, h, d, s if (s is not None and last) else None)

        queue = []
        for d in range(2):
            seqs = [(s, [2 * s, 2 * s + 1]) for s in range(4)] + [(None, list(range(8, 16)))]
            for s, tiles in seqs:
                order = tiles if d == 0 else tiles[::-1]
                for i, tt in enumerate(order):
                    for h in range(8):
                        queue.append(inst(tt, h, d, s, i == 0, i == len(order) - 1))
        active = []
        qi = 0
        cyc = 0
        STAG = 5
        while qi < len(queue) or active:
            if qi < len(queue) and len(active) < NI and (qi >= NI or cyc >= qi * STAG):
                active.append(queue[qi])
                qi += 1
            alive = []
            for g in active:
                try:
                    next(g)
                    alive.append(g)
                except StopIteration:
                    pass
            active = alive
            cyc += 1
        c.end_stage()
        osq = c.sb([128, 512], BF16, "dosq")
        rr = c.sb([128, 512], F32, "drr")
        sdz = [c.sb([128, NT], BF16, "sdz") for _ in range(2)]
        orow = [c.sb([128, NT], BF16, "dorow") for _ in range(2)]
        for h in range(8):
            oa = OACC[h]
            z_ = sdz[h % 2]
            orw = orow[h % 2]
            c.dma("sp", z_[:], SDZ[h], writes=[z_])
            for (u0, nu) in TILES:
                c.op("act", lambda e: e.activation(out=osq[:, :nu], in_=oa[:, u0:u0 + nu], func=AF.Square), reads=[oa], writes=[osq])
                p2 = self.pb()
                c.op("pe", lambda e: e.matmul(p2[:, :nu], lhsT=onesB[:], rhs=osq[:, :nu], start=True, stop=True), reads=[onesB, osq], writes=[p2])
                c.op("act", lambda e: e.activation(out=rr[:, :nu], in_=p2[:, :nu], func=AF.Sqrt, scale=1.0 / 128, bias=eps[:, 0:1]), reads=[p2, eps], writes=[rr])
                c.op("dve", lambda e: e.reciprocal(out=rr[:, :nu], in_=rr[:, :nu]), reads=[rr], writes=[rr])
                c.op("dve", lambda e: e.scalar_tensor_tensor(out=rr[:, :nu], in0=oa[:, u0:u0 + nu], scalar=dnn[:, 0:1], in1=rr[:, :nu], op0=ALU.mult, op1=ALU.mult), reads=[oa, dnn, rr], writes=[rr])
                c.op("dve", lambda e: e.tensor_tensor(out=orw[:, u0:u0 + nu], in0=rr[:, :nu], in1=z_[:, u0:u0 + nu], op=ALU.mult), reads=[rr, z_], writes=[orw])
            c.dma("sp", MO[2][h], orw[:], reads=[orw])
        c.end_stage()


PI = math.pi


class KB5(KB4):
    def hy_tables(self, l, L):
        c = self.c
        I = self.I
        nch = L // 128
        TAB = self.scr("HTAB%d" % L, [2, 2 * nch + 1, 128, 1024], BF16)
        c.begin_stage()
        onesF, m0 = self.onesF, None
        m0 = c.sb([128, 2], F32, "m0")
        c.dma("sp", m0[:], I["m0"], writes=[m0])
        fT = c.sb([33, L], F32, "fT")
        c.dma("sp", fT[:], I["featsT%d" % L], writes=[fT])
        w1 = c.sb([33, 64], F32, "w1")
        w2 = c.sb([64, 64], F32, "w2")
        w3 = c.sb([64, 4096], F32, "w3")
        c.dma("sp", w1[:], I["hy_w1"][l], writes=[w1])
        c.dma("sp", w2[:], I["hy_w2"][l], writes=[w2])
        c.dma("sp", w3[:], I["hy_w3"][l], writes=[w3])
        vec = c.sb([64, 4], F32, "hvec")
        for i, nm in enumerate(("hy_b1", "hy_freq1", "hy_b2", "hy_freq2")):
            c.dma("sp", vec[:, i:i + 1], I[nm][l].rearrange("(p o) -> p o", o=1), writes=[vec])
        hid1 = c.sb([64, L], F32, "hid1")
        hid2 = c.sb([64, L], F32, "hid2")
        msk = c.sb([64, 512], F32, "msk")

        def sin_layer(dst, wT, K, src, bi, fi):
            for t0 in range(0, L, 512):
                n = min(512, L - t0)
                ps = self.pb()
                c.op("pe", lambda e: e.matmul(ps[:64, :n], lhsT=wT[:K, :], rhs=src[:K, t0:t0 + n], start=True, stop=True), reads=[wT, src], writes=[ps])
                d = dst
                c.op("dve", lambda e: e.tensor_scalar(out=d[:, t0:t0 + n], in0=ps[:64, :n], scalar1=vec[:, bi:bi + 1], scalar2=vec[:, fi:fi + 1], op0=ALU.add, op1=ALU.mult), reads=[ps, vec], writes=[d])
                for _ in range(2):
                    c.op("dve", lambda e: e.tensor_scalar(out=msk[:, :n], in0=d[:, t0:t0 + n], scalar1=PI, scalar2=-2 * PI, op0=ALU.is_gt, op1=ALU.mult), reads=[d], writes=[msk])
                    c.op("dve", lambda e: e.tensor_tensor(out=d[:, t0:t0 + n], in0=d[:, t0:t0 + n], in1=msk[:, :n], op=ALU.add), reads=[d, msk], writes=[d])
                    c.op("dve", lambda e: e.tensor_scalar(out=msk[:, :n], in0=d[:, t0:t0 + n], scalar1=-PI, scalar2=2 * PI, op0=ALU.is_lt, op1=ALU.mult), reads=[d], writes=[msk])
                    c.op("dve", lambda e: e.tensor_tensor(out=d[:, t0:t0 + n], in0=d[:, t0:t0 + n], in1=msk[:, :n], op=ALU.add), reads=[d, msk], writes=[d])
                c.op("act", lambda e: e.activation(out=d[:, t0:t0 + n], in_=d[:, t0:t0 + n], func=AF.Sin), reads=[d], writes=[d])

        sin_layer(hid1, w1, 33, fT, 0, 1)
        sin_layer(hid2, w2, 64, hid1, 2, 3)
        win = c.sb([128, nch, 1024], F32, "win")
        c.dma("sp", win[:], I["win%d" % L].rearrange("(t p) c -> p t c", p=128), writes=[win])
        CA = c.sb([128, nch, L], BF16, "CA")
        MB = c.sb([128, nch, L], BF16, "MB")
        c.dma("pool", CA[:], I["CA%d" % L].rearrange("(t p) k -> p t k", p=128), writes=[CA])
        c.dma("pool", MB[:], I["MB%d" % L].rearrange("(t p) k -> p t k", p=128), writes=[MB])
        hw = [c.sb([128, nch, 512], F32, "hw") for _ in range(2)]
        ab = c.sb([128, 512], F32, "hab")
        rinv = c.sb([128, 512], F32, "hrinv")
        hs = c.sb([128, nch, 512], BF16, "hs")
        hd = c.sb([128, nch, 512], BF16, "hd")
        to = [c.sb([128, 512], BF16, "hto") for _ in range(3)]
        tf = [c.sb([128, 512], F32, "htf") for _ in range(2)]
        nto = [0]
        for order in range(2):
            for half in range(2):
                for d in range(2):
                    col0 = d * 2048 + order * 1024 + half * 512
                    H = hw[d]
                    pn = self.pacc()
                    for tt in range(nch):
                        ps = self.pb()
                        c.op("pe", lambda e: e.matmul(ps[:, :512], lhsT=hid2[:, tt * 128:(tt + 1) * 128], rhs=w3[:, col0:col0 + 512], start=True, stop=True), reads=[hid2, w3], writes=[ps])
                        c.op("dve", lambda e: e.tensor_tensor(out=H[:, tt, :], in0=ps[:, :512], in1=win[:, tt, half * 512:(half + 1) * 512], op=ALU.mult), reads=[ps, win], writes=[H])
                        c.op("act", lambda e: e.activation(out=ab[:], in_=H[:, tt, :], func=AF.Abs), reads=[H], writes=[ab])
                        c.op("pe", lambda e: e.matmul(pn[:, :512], lhsT=onesF[:], rhs=ab[:], start=(tt == 0), stop=(tt == nch - 1)), reads=[onesF, ab], writes=[pn])
                    c.op("dve", lambda e: e.tensor_scalar(out=rinv[:], in0=pn[:, :512], scalar1=EPS, scalar2=None, op0=ALU.add), reads=[pn], writes=[rinv])
                    c.op("dve", lambda e: e.reciprocal(out=rinv[:], in_=rinv[:]), reads=[rinv], writes=[rinv])
                    for tt in range(nch):
                        c.op("dve", lambda e: e.tensor_tensor(out=H[:, tt, :], in0=H[:, tt, :], in1=rinv[:], op=ALU.mult), reads=[H, rinv], writes=[H])
                c.op("dve", lambda e: e.tensor_tensor(out=hs[:], in0=hw[0][:], in1=hw[1][:], op=ALU.add), reads=[hw[0], hw[1]], writes=[hs])
                c.op("pool", lambda e: e.tensor_tensor(out=hd[:], in0=hw[0][:], in1=hw[1][:], op=ALU.subtract), reads=[hw[0], hw[1]], writes=[hd])
                cs = slice(half * 512, (half + 1) * 512)
                for kc in range(nch):
                    pa = self.pb()
                    for tt in range(nch):
                        c.op("pe", lambda e: e.matmul(pa[:, :512], lhsT=CA[:, tt, kc * 128:(kc + 1) * 128], rhs=hs[:, tt, :], start=(tt == 0), stop=(tt == nch - 1)), reads=[CA, hs], writes=[pa])
                    pbd = self.pb()
                    for tt in range(nch):
                        c.op("pe", lambda e: e.matmul(pbd[:, :512], lhsT=MB[:, tt, kc * 128:(kc + 1) * 128], rhs=hd[:, tt, :], start=(tt == 0), stop=(tt == nch - 1)), reads=[MB, hd], writes=[pbd])
                    oa = to[nto[0] % 3]
                    nto[0] += 1
                    c.op("act", lambda e: e.copy(out=oa[:], in_=pa[:, :512]), reads=[pa], writes=[oa])
                    c.dma("sp", TAB[order, kc][:, cs], oa[:], reads=[oa])
                    ob = to[nto[0] % 3]
                    nto[0] += 1
                    if kc > 0:
                        c.op("act", lambda e: e.copy(out=ob[:], in_=pbd[:, :512]), reads=[pbd], writes=[ob])
                        c.dma("sp", TAB[order, nch + kc][:, cs], ob[:], reads=[ob])
                    else:
                        pbs = self.pb()
                        for tt in range(nch):
                            c.op("pe", lambda e: e.matmul(pbs[:, :512], lhsT=MB[:, tt, 0:128], rhs=hs[:, tt, :], start=(tt == 0), stop=(tt == nch - 1)), reads=[MB, hs], writes=[pbs])
                        c.op("dve", lambda e: e.tensor_scalar(out=ob[:], in0=pbd[:, :512], scalar1=m0[:, 0:1], scalar2=None, op0=ALU.mult), reads=[pbd, m0], writes=[ob])
                        c.dma("sp", TAB[order, nch][:, cs], ob[:], reads=[ob])
                        t1, t2 = tf[0], tf[1]
                        c.op("dve", lambda e: e.tensor_scalar(out=t1[:], in0=pa[:, :512], scalar1=m0[:, 0:1], scalar2=None, op0=ALU.mult), reads=[pa, m0], writes=[t1])
                        c.op("dve", lambda e: e.tensor_scalar(out=t2[:], in0=pbs[:, :512], scalar1=m0[:, 1:2], scalar2=None, op0=ALU.mult), reads=[pbs, m0], writes=[t2])
                        oc = to[nto[0] % 3]
                        nto[0] += 1
                        c.op("pool", lambda e: e.tensor_tensor(out=oc[:], in0=t1[:], in1=t2[:], op=ALU.add), reads=[t1, t2], writes=[oc])
                        c.dma("sp", TAB[order, 2 * nch][:, cs], oc[:], reads=[oc])
        c.end_stage()

    def hy_data(self, l, L, tokbase, B):
        c = self.c
        I = self.I
        nch = L // 128
        ncol = B * 1024
        TAB = self.S["HTAB%d" % L]
        HV, HX = self.S["HV"], self.S["HX"]
        MO = self.scr("MO", [3, 8, 128, NT], BF16)
        Z1 = self.scr("HZ1", [8, 128, NT], F32)
        idB = self.identB
        c.begin_stage()
        CA = c.sb([128, nch, L], BF16, "CA")
        MB = c.sb([128, nch, L], BF16, "MB")
        IA = c.sb([128, nch, L], BF16, "IA")
        IB = c.sb([128, nch, L], BF16, "IB")
        for t, nm in ((CA, "CA"), (MB, "MB"), (IA, "IA"), (IB, "IB")):
            c.dma("pool", t[:], I["%s%d" % (nm, L)].rearrange("(t p) k -> p t k", p=128), writes=[t])
        AH = c.sb([128, nch, 1024], BF16, "AH")
        HB1 = c.sb([128, nch, 1024], BF16, "HB1")
        HA2 = c.sb([128, 1024], BF16, "HA2")
        hb = c.sb([128, 16], F32, "hbias")
        self.load_cols(hb, hb[:], I["hy_bias"][l].rearrange("o (n p) -> (o n) p", p=128), 16)
        z = c.sb([128, nch, ncol], BF16, "z")
        Y = c.sb([128, 2 * nch, ncol], BF16, "Y")
        ntok = B * L
        rowv = [c.sb([128, ntok], BF16, "hrv") for _ in range(2)]
        rowx = [c.sb([128, ntok], F32, "hrx") for _ in range(2)]
        rowz = [c.sb([128, ntok], F32, "hrz") for _ in range(2)]
        rowo = [c.sb([128, ntok], BF16, "hro") for _ in range(2)]
        tm = [c.sb([128, 512], F32, "htm") for _ in range(4)]

        def to_tokmajor(src_rows_fn):
            for ch in range(8):
                row = src_rows_fn(ch)
                for b in range(B):
                    for tq in range(0, nch, 4):
                        ps = self.pb()
                        pb16 = ps[:].bitcast(BF16)
                        nq = min(4, nch - tq)
                        for j in range(nq):
                            tt = tq + j
                            t0 = b * L + tt * 128
                            c.op("pe", lambda e: e.transpose(out=pb16[:, j * 128:(j + 1) * 128], in_=row[:, t0:t0 + 128], identity=idB[:]), reads=[row, idB], writes=[ps])
                        for j in range(nq):
                            tt = tq + j
                            eng = "dve" if j % 2 == 0 else "act"
                            dst = z[:, tt, b * 1024 + ch * 128:b * 1024 + (ch + 1) * 128]
                            if eng == "dve":
                                c.op("dve", lambda e: e.tensor_copy(out=dst, in_=pb16[:, j * 128:(j + 1) * 128]), reads=[ps], writes=[z])
                            else:
                                c.op("act", lambda e: e.copy(out=dst, in_=pb16[:, j * 128:(j + 1) * 128]), reads=[ps], writes=[z])

        for order in range(2):
            c.dma("sp", AH[:], TAB[order, 0:nch].rearrange("k p c -> p k c"), writes=[AH])
            c.dma("sp", HB1[:], TAB[order, nch:2 * nch].rearrange("k p c -> p k c"), writes=[HB1])
            c.dma("sp", HA2[:], TAB[order, 2 * nch], writes=[HA2])
            if order == 0:
                def rows_v(ch):
                    r = rowv[ch % 2]
                    c.dma("sp", r[:], HV[ch][:, tokbase:tokbase + ntok], writes=[r])
                    return r
                to_tokmajor(rows_v)
            else:
                def rows_z(ch):
                    rz = rowz[ch % 2]
                    r = rowv[ch % 2]
                    c.dma("sp", rz[:], Z1[ch][:, tokbase:tokbase + ntok], writes=[rz])
                    c.op("act", lambda e: e.copy(out=r[:], in_=rz[:]), reads=[rz], writes=[r])
                    return r
                to_tokmajor(rows_z)
            for ct in range(ncol // 512):
                c0 = (ct * 512) % 1024
                for kc in range(nch):
                    pa = self.pb()
                    for tt in range(nch):
                        c.op("pe", lambda e: e.matmul(pa[:, :512], lhsT=CA[:, tt, kc * 128:(kc + 1) * 128], rhs=z[:, tt, ct * 512:(ct + 1) * 512], start=(tt == 0), stop=(tt == nch - 1)), reads=[CA, z], writes=[pa])
                    pq = self.pb()
                    for tt in range(nch):
                        c.op("pe", lambda e: e.matmul(pq[:, :512], lhsT=MB[:, tt, kc * 128:(kc + 1) * 128], rhs=z[:, tt, ct * 512:(ct + 1) * 512], start=(tt == 0), stop=(tt == nch - 1)), reads=[MB, z], writes=[pq])
                    t1, t2, t3, t4 = tm
                    ah = AH[:, kc, c0:c0 + 512]
                    h1 = HB1[:, kc, c0:c0 + 512]
                    a2 = HA2[:, c0:c0 + 512] if kc == 0 else ah
                    c.op("dve", lambda e: e.tensor_tensor(out=t1[:], in0=pa[:, :512], in1=ah, op=ALU.mult), reads=[pa, AH], writes=[t1])
                    c.op("dve", lambda e: e.tensor_tensor(out=t2[:], in0=pq[:, :512], in1=h1, op=ALU.mult), reads=[pq, HB1], writes=[t2])
                    c.op("pool", lambda e: e.tensor_tensor(out=Y[:, kc, ct * 512:(ct + 1) * 512], in0=t1[:], in1=t2[:], op=ALU.subtract), reads=[t1, t2], writes=[Y])
                    c.op("dve", lambda e: e.tensor_tensor(out=t3[:], in0=pa[:, :512], in1=h1, op=ALU.mult), reads=[pa, HB1], writes=[t3])
                    c.op("dve", lambda e: e.tensor_tensor(out=t4[:], in0=pq[:, :512], in1=a2, op=ALU.mult), reads=[pq, HA2, AH], writes=[t4])
                    c.op("pool", lambda e: e.tensor_tensor(out=Y[:, nch + kc, ct * 512:(ct + 1) * 512], in0=t3[:], in1=t4[:], op=ALU.add), reads=[t3, t4], writes=[Y])
            for ch in range(8):
                rx = rowx[ch % 2]
                c.dma("sp", rx[:], HX[order * 8 + ch][:, tokbase:tokbase + ntok], writes=[rx])
                if order == 0:
                    rb = rowv[ch % 2]
                    c.dma("sp", rb[:], HV[ch][:, tokbase:tokbase + ntok], writes=[rb])
                    ro = rowz[ch % 2]
                else:
                    rb = rowz[ch % 2]
                    c.dma("sp", rb[:], Z1[ch][:, tokbase:tokbase + ntok], writes=[rb])
                    ro = rowo[ch % 2]
                for b in range(B):
                    for r0 in range(0, L, 512):
                        n = min(512, L - r0)
                        ps = self.pacc()
                        for kk in range(2 * nch):
                            M = IA if kk < nch else IB
                            c.op("pe", lambda e: e.matmul(ps[:, :n], lhsT=Y[:, kk, b * 1024 + ch * 128:b * 1024 + (ch + 1) * 128], rhs=M[:, kk % nch, r0:r0 + n], start=(kk == 0), stop=(kk == 2 * nch - 1)),
                                 reads=[Y, M], writes=[ps])
                        g0 = b * L + r0
                        t1 = tm[0]
                        c.op("dve", lambda e: e.scalar_tensor_tensor(out=t1[:, :n], in0=rb[:, g0:g0 + n], scalar=hb[:, order * 8 + ch:order * 8 + ch + 1], in1=ps[:, :n], op0=ALU.mult, op1=ALU.add), reads=[rb, hb, ps], writes=[t1])
                        c.op("dve", lambda e: e.tensor_tensor(out=ro[:, g0:g0 + n], in0=t1[:, :n], in1=rx[:, g0:g0 + n], op=ALU.mult), reads=[t1, rx], writes=[ro])
                if order == 0:
                    c.dma("sp", Z1[ch][:, tokbase:tokbase + ntok], ro[:], reads=[ro])
                else:
                    c.dma("sp", MO[1][ch][:, tokbase:tokbase + ntok], ro[:], reads=[ro])
            c.barrier()
        c.end_stage()

    def stage_hy(self, l):
        self.hy_tables(l, 256)
        self.hy_data(l, 256, 0, 4)
        self.hy_tables(l, 1024)
        self.hy_data(l, 1024, 1024, 1)

_CACHE = {}


def build_full():
    kb = KB5()
    c = kb.c
    kb.stage_input()
    c.mark('input')
    kb.stage_mod()
    c.mark('mod')
    XA = kb.S["XT0"]
    XB = kb.scr("XT1", [16, 128, NT], F32)
    XC = kb.scr("XT2", [16, 128, NT], F32)
    cur = XA
    for l in range(2):
        kb.stage_norm(cur, l, 0)
        c.mark('norm1')
        kb.stage_proj(l)
        c.end_stage()
        c.mark('proj')
        kb.stage_ret(l)
        c.mark('ret')
        kb.stage_hy(l)
        c.mark('hy')
        kb.stage_dn(l)
        c.mark('dn')
        kb.stage_merge(l, cur, XB)
        c.mark('merge+wo')
        kb.stage_norm(XB, l, 1)
        c.mark('norm2')
        kb.stage_ffn_up(l)
        c.end_stage()
        c.mark('ffn_up')
        kb.stage_ffn_down(l, XB, XC)
        c.mark('ffn_down')
        cur, XB, XC = XC, cur, XB
    kb.stage_final(cur)
    c.mark('final')
    c.finish()
    return kb


def kernel(**inputs):
    z = {k: np.ascontiguousarray(np.asarray(v)) for k, v in inputs.items()}
    if "kb" not in _CACHE:
        _CACHE["kb"] = build_full()
    kb = _CACHE["kb"]
    in_maps = []
    for core in range(8):
        m = dict(kb.consts)
        for k in WSPEC:
            m[k] = z[k]
        xp = z["x_prompt"][4 * core:4 * core + 4].reshape(1024, 2048)
        xs = z["x_sample"][core // 2]
        m["xin"] = np.ascontiguousarray(np.concatenate([xp, xs], 0))
        m["sret"] = np.ascontiguousarray(z["state_ret"][core // 2])
        m["sdn"] = np.ascontiguousarray(z["state_dn"][core // 2])
        m["cvec"] = np.ascontiguousarray(np.stack([z["c_ctx"], z["c"][core // 2]]))
        in_maps.append(m)
    res = run_bass_kernel_spmd(kb.nc, in_maps, core_ids=list(range(8))).results
    y_prompt = np.concatenate([np.asarray(res[c]["y"])[:1024].reshape(4, 256, 2048) for c in range(8)], 0).astype(np.float32)
    y_sample = np.stack([np.asarray(res[2 * b]["y"])[1024:] for b in range(4)], 0).astype(np.float32)
    new_ret = np.concatenate([np.asarray(res[c]["osret"]) for c in range(8)], 0).astype(np.float32)
    new_dn = np.concatenate([np.asarray(res[c]["osdn"]) for c in range(8)], 0).astype(np.float32)
    return (y_prompt, y_sample, new_ret, new_dn)
```

```python
import numpy as np
from contextlib import ExitStack
import concourse.bass as bass
import concourse.mybir as mybir
from concourse.bass_utils import run_bass_kernel_spmd

F32 = mybir.dt.float32
BF16 = mybir.dt.bfloat16
I32 = mybir.dt.int32
AF = mybir.ActivationFunctionType
ALU = mybir.AluOpType
AX = mybir.AxisListType


class Buf:
    __slots__ = ("t", "name", "w", "r", "psum")

    def __init__(self, t, name, psum=False):
        self.t = t
        self.name = name
        self.psum = psum
        self.w = None
        self.r = {}

    def __getitem__(self, idx):
        return self.t[idx]


class Ctx:
    SEM_LIMIT = 30000
    NDMA = 24

    def __init__(self, nc):
        self.nc = nc
        self.es = ExitStack()
        self.eng = {"pe": nc.tensor, "act": nc.scalar, "dve": nc.vector, "pool": nc.gpsimd, "sp": nc.sync}
        self.cur = {}
        self.waited = {k: {} for k in self.eng}
        self.nsem = 0
        for k in self.eng:
            self._new_sem(k)
        self.dpool = {}
        for q in ("sp", "pool", "act"):
            self.dpool[q] = [[self._alloc_sem("d%s%d" % (q, i)), 0] for i in range(self.NDMA)]
        self.dnext = {q: 0 for q in self.dpool}
        self.stage_es = None
        self.uid = 0
        self.tot = {}
        self.marks = []

    def mark(self, name):
        self.marks.append((name, dict(self.tot)))

    def _alloc_sem(self, name):
        self.nsem += 1
        s = self.es.enter_context(self.nc.semaphore("%s_%d" % (name, self.nsem)))
        if not hasattr(self, "allsems"):
            self.allsems = []
        self.allsems.append(s)
        return s

    def _new_sem(self, k):
        self.cur[k] = [self._alloc_sem("e" + k), 0]

    def begin_stage(self):
        if not hasattr(self, "stack"):
            self.stack = []
        self.stack.append(ExitStack())
        self.stage_es = self.stack[-1]

    def end_stage(self):
        self.barrier()
        self.stack.pop().close()
        self.stage_es = self.stack[-1] if self.stack else None

    def sb(self, shape, dt, name="t", persist=False):
        self.uid += 1
        nm = "%s_%d" % (name, self.uid)
        es = self.es if persist else self.stage_es
        t = es.enter_context(self.nc.sbuf_tensor(nm, list(shape), dt))
        return Buf(t, nm)

    def ps(self, shape, dt, name="p"):
        self.uid += 1
        nm = "%s_%d" % (name, self.uid)
        t = self.es.enter_context(self.nc.psum_tensor(nm, list(shape), dt))
        return Buf(t, nm, psum=True)

    def _wait(self, k, tok):
        if tok is None:
            return
        sem, val = tok
        if k == "pe" and sem is self.cur["pe"][0]:
            return
        w = self.waited[k]
        key = id(sem)
        if w.get(key, (None, 0))[1] >= val:
            return
        w[key] = (sem, val)
        self.eng[k].wait_ge(sem, val)

    def _deps(self, k, reads, writes):
        for b in reads:
            self._wait(k, b.w)
            if b.psum:
                for tok in list(b.r.values()):
                    self._wait(k, tok)
        for b in writes:
            self._wait(k, b.w)
            for tok in list(b.r.values()):
                self._wait(k, tok)

    def _commit(self, tok, reads, writes):
        for b in reads:
            b.r[id(tok[0])] = tok
        for b in writes:
            b.w = tok
            b.r = {}

    def op(self, k, fn, reads=(), writes=()):
        self._deps(k, reads, writes)
        c = self.cur[k]
        if c[1] >= self.SEM_LIMIT:
            self._new_sem(k)
            c = self.cur[k]
        c[1] += 1
        self.tot[k] = self.tot.get(k, 0) + 1
        fn(self.eng[k]).then_inc(c[0], 1)
        tok = (c[0], c[1])
        self._commit(tok, reads, writes)
        return tok

    def dma(self, q, out, in_, reads=(), writes=(), **kw):
        pool = self.dpool[q]
        i = self.dnext[q]
        self.dnext[q] = (i + 1) % len(pool)
        slot = pool[i]
        if slot[1] > 0:
            self._wait(q, (slot[0], slot[1]))
        if slot[1] >= self.SEM_LIMIT:
            slot[0] = self._alloc_sem("d" + q)
            slot[1] = 0
        self._deps(q, reads, writes)
        slot[1] += 16
        self.eng[q].dma_start(out=out, in_=in_, **kw).then_inc(slot[0], 16)
        tok = (slot[0], slot[1])
        self._commit(tok, reads, writes)
        return tok

    def barrier(self):
        toks = [(c[0], c[1]) for c in self.cur.values() if c[1] > 0]
        for q in self.dpool:
            for slot in self.dpool[q]:
                if slot[1] > 0:
                    toks.append((slot[0], slot[1]))
        for k in self.eng:
            for tok in toks:
                self._wait(k, tok)

    def finish(self):
        self.barrier()
        self.es.close()

import math
import numpy as np


def make_consts():
    f = np.float32
    C = {}
    i = np.arange(128)
    C["ident"] = np.eye(128, dtype=f)
    C["ones"] = np.ones((128, 128), f)
    C["UF"] = (i[:, None] <= i[None, :]).astype(f)
    C["UB"] = (i[:, None] >= i[None, :]).astype(f)
    C["SU"] = (i[:, None] < i[None, :]).astype(f)
    C["SL"] = (i[:, None] > i[None, :]).astype(f)
    L = 1024
    pos = np.arange(L, dtype=np.float64)
    pos_r = np.floor(pos / 64)
    pos_c = pos % 64
    inv = 10000.0 ** (-np.arange(16, dtype=np.float64) / 16)
    cosT = np.zeros((128, L))
    sinT = np.zeros((128, L))
    for p in range(128):
        q = p % 64
        half = q // 32
        x2 = (q % 32) // 16
        fi = q % 16
        ang = (pos_r if half == 0 else pos_c) * inv[fi]
        cosT[p] = np.cos(ang)
        sinT[p] = np.sin(ang) * (1.0 if x2 else -1.0)
    C["ropeC"] = cosT.astype(f)
    C["ropeS"] = sinT.astype(f)
    for nm, LL in (("S", 1024), ("P", 256)):
        u = np.arange(2 * LL - 128)[None, :]
        p = np.arange(128)[:, None]
        d = u - p - (LL - 128)
        C["dpos" + nm] = np.maximum(d, 0).astype(f)
        C["dneg" + nm] = np.maximum(-d, 0).astype(f)
        C["dz" + nm] = (d == 0).astype(f)
    j = np.arange(128)[:, None] + 128 * np.arange(2)[None, :]
    C["expfb"] = np.concatenate([255 - j, j], axis=1).astype(f)
    ii = np.arange(1024)[None, :].repeat(128, 0)
    C["idx1"] = (ii + 1).astype(f)
    C["idx2"] = (1024 - ii).astype(f)
    for LL in (256, 1024):
        t = np.linspace(0.0, 1.0, LL, dtype=np.float32)[:, None].astype(np.float64)
        wpos = 2.0 * math.pi * np.arange(LL, dtype=np.float64)[:, None] / LL
        fr = np.linspace(1e-4, 15, 16, dtype=np.float32)[None, :].astype(np.float64)
        feats = np.concatenate([t, np.cos(fr * wpos), -np.sin(fr * wpos)], axis=-1)
        C["featsT%d" % LL] = np.ascontiguousarray(feats.T).astype(f)
        deltas = np.abs(np.linspace(math.log(1e-2) / 0.3, math.log(1e-2) / 1.5, 1024, dtype=np.float32)).astype(np.float64)
        C["win%d" % LL] = np.exp(-t * deltas[None, :]).astype(f)
        N = 2 * LL
        tt = np.arange(LL, dtype=np.float64)[:, None]
        kk = np.arange(LL, dtype=np.float64)[None, :]
        ang = 2.0 * math.pi * tt * kk / N
        CA = np.cos(ang)
        MB = -np.sin(ang)
        MB[:, 0] = (-1.0) ** np.arange(LL)
        IA = (2.0 / N) * np.cos(ang.T)
        IA[0, :] = 1.0 / N
        IB = -(2.0 / N) * np.sin(ang.T)
        IB[0, :] = ((-1.0) ** np.arange(LL)) / N
        C["CA%d" % LL] = CA.astype(f)
        C["MB%d" % LL] = MB.astype(f)
        C["IA%d" % LL] = IA.astype(f)
        C["IB%d" % LL] = IB.astype(f)
    ii, jj = np.meshgrid(np.arange(128), np.arange(128), indexing="ij")
    ML = np.stack([(((ii >> (s + 1)) == (jj >> (s + 1))) & ((ii >> s) != (jj >> s)) & (ii > jj)).astype(f) for s in range(7)])
    MU = np.ascontiguousarray(ML.transpose(0, 2, 1))
    C["MLU"] = np.ascontiguousarray(np.concatenate([ML, MU], axis=2).transpose(1, 0, 2))
    C["MUL"] = np.ascontiguousarray(np.concatenate([MU, ML], axis=2).transpose(1, 0, 2))
    m0 = np.ones((128, 2), f)
    m0[0, 0] = 0.0
    m0[:, 1] = 1.0 - m0[:, 0]
    C["m0"] = m0
    C["eps"] = np.full((128, 1), 1e-6, f)
    return C


import math

NT = 2048
TILES = [(0, 512), (512, 512), (1024, 512), (1536, 512)]
EPS = 1e-6

WSPEC = dict(
    w_ada=[2, 2048, 12288], b_ada=[2, 12288], norm1=[2, 2048], w_in=[2, 2048, 16416], ret_decay=[2, 2, 8],
    hy_short=[2, 3, 3072], hy_w1=[2, 33, 64], hy_b1=[2, 64], hy_freq1=[2, 64], hy_w2=[2, 64, 64], hy_b2=[2, 64],
    hy_freq2=[2, 64], hy_w3=[2, 64, 4096], hy_bias=[2, 2, 1024], dn_conv=[2, 3, 3072], dn_a_log=[2, 2, 8],
    dn_dt_bias=[2, 2, 8], dn_norm=[2, 128], p_ret=[2, 1024, 2048], p_hy=[2, 1024, 2048], p_dn=[2, 1024, 2048],
    w_o=[2, 2048, 2048], norm2=[2, 2048], w_up=[2, 2048, 11008], ffn_conv=[2, 3, 11008], w_down=[2, 5504, 2048],
    norm_f=[2048])


class KB:
    def __init__(self, stop_after=None, dbg=(), wspec=None, ext_in=()):
        self.stop_after = stop_after
        self.dbg = set(dbg)
        nc = self.nc = bass.Bass("TRN2", target_bir_lowering=False)
        self.c = Ctx(nc)
        self.I = {}
        self.consts = make_consts()
        for k, v in self.consts.items():
            self.I[k] = nc.dram_tensor(k, list(v.shape), F32, kind="ExternalInput").ap()
        self.ext_in = set(ext_in)
        for k, s in (wspec or WSPEC).items():
            self.I[k] = nc.dram_tensor(k, s, F32, kind="ExternalInput").ap()
        self.I["xin"] = nc.dram_tensor("xin", [NT, 2048], F32, kind="ExternalInput").ap()
        self.I["sret"] = nc.dram_tensor("sret", [2, 2, 8, 64, 128], F32, kind="ExternalInput").ap()
        self.I["sdn"] = nc.dram_tensor("sdn", [2, 2, 8, 128, 128], F32, kind="ExternalInput").ap()
        self.I["cvec"] = nc.dram_tensor("cvec", [2, 2048], F32, kind="ExternalInput").ap()
        self.O = {}
        self.O["y"] = nc.dram_tensor("y", [NT, 2048], F32, kind="ExternalOutput").ap()
        self.O["osret"] = nc.dram_tensor("osret", [4, 2, 2, 8, 64, 128], F32, kind="ExternalOutput").ap()
        self.O["osdn"] = nc.dram_tensor("osdn", [4, 2, 2, 8, 128, 128], F32, kind="ExternalOutput").ap()
        self.S = {}
        c = self.c
        self.PB = [c.ps([128, 512], F32, "pb") for _ in range(8)]
        self.pbi = 0
        self.identF = c.sb([128, 128], F32, "identF", persist=True)
        self.identB = c.sb([128, 128], BF16, "identB", persist=True)
        self.onesB = c.sb([128, 128], BF16, "onesB", persist=True)
        self.onesF = c.sb([128, 128], F32, "onesF", persist=True)
        self.epsT = c.sb([128, 1], F32, "epsT", persist=True)
        self.rows = c.sb([128, 128], F32, "rows", persist=True)
        c.dma("sp", self.identF[:], self.I["ident"], writes=[self.identF])
        c.dma("pool", self.identB[:], self.I["ident"], writes=[self.identB])
        c.dma("pool", self.onesB[:], self.I["ones"], writes=[self.onesB])
        c.dma("sp", self.onesF[:], self.I["ones"], writes=[self.onesF])
        c.dma("sp", self.epsT[:], self.I["eps"], writes=[self.epsT])
        self.HT = None
        self.modT = c.sb([128, 2, 96, 2], F32, "modT", persist=True)
        self.sca = c.sb([128, 2, 2, 16, 2], F32, "sca", persist=True)
        self.gbraw = c.sb([128, 16, 32], F32, "gbraw", persist=True)

    def scr(self, name, shape, dt):
        if name not in self.S:
            kind = "ExternalOutput" if name in self.dbg else ("ExternalInput" if name in self.ext_in else "Internal")
            self.S[name] = self.nc.dram_tensor(name, list(shape), dt, kind=kind).ap()
        return self.S[name]

    def pb(self):
        self.pbi = (self.pbi + 1) % 6
        return self.PB[self.pbi]

    def pacc(self):
        self.pai = (getattr(self, "pai", 0) + 1) % 2
        return self.PB[6 + self.pai]

    def load_cols(self, dstbuf, dst_ap, src_rows, n):
        c = self.c
        rows = self.rows
        c.dma("sp", rows[:n, :], src_rows, writes=[rows])
        ps = self.pb()
        idf = self.identF
        c.op("pe", lambda e: e.transpose(out=ps[:, :n], in_=rows[:n, :], identity=idf[:n, :n]), reads=[rows, idf], writes=[ps])
        c.op("dve", lambda e: e.tensor_copy(out=dst_ap, in_=ps[:, :n]), reads=[ps], writes=[dstbuf])

    def stage_input(self):
        c = self.c
        XT = self.scr("XT0", [16, 128, NT], F32)
        c.begin_stage()
        xs = [c.sb([128, 2048], F32, "xs") for _ in range(2)]
        xo = [c.sb([128, 16, 128], F32, "xo") for _ in range(2)]
        idf = self.identF
        for tt in range(16):
            a = xs[tt % 2]
            o = xo[tt % 2]
            c.dma("sp", a[:], self.I["xin"][tt * 128:(tt + 1) * 128, :], writes=[a])
            for g in range(4):
                ps = self.pb()
                for j in range(4):
                    ch = g * 4 + j
                    c.op("pe", lambda e: e.transpose(out=ps[:, j * 128:(j + 1) * 128], in_=a[:, ch * 128:(ch + 1) * 128], identity=idf[:]),
                         reads=[a, idf], writes=[ps])
                eng = "dve" if g % 2 == 0 else "act"
                if eng == "dve":
                    c.op("dve", lambda e: e.tensor_copy(out=o[:, g * 4:(g + 1) * 4, :], in_=ps[:].rearrange("p (j t) -> p j t", j=4)), reads=[ps], writes=[o])
                else:
                    c.op("act", lambda e: e.copy(out=o[:, g * 4:(g + 1) * 4, :], in_=ps[:].rearrange("p (j t) -> p j t", j=4)), reads=[ps], writes=[o])
            c.dma("sp", XT[:, :, tt * 128:(tt + 1) * 128].rearrange("c p t -> p c t"), o[:], reads=[o])
        c.end_stage()

    def stage_mod(self):
        c = self.c
        I = self.I
        c.begin_stage()
        scT = c.sb([128, 2, 16], F32, "scT")
        self.load_cols(scT, scT[:].rearrange("p b k -> p (b k)"), I["cvec"].rearrange("b (k p) -> (b k) p", p=128), 32)
        c.op("act", lambda e: e.activation(out=scT[:], in_=scT[:], func=AF.Silu), reads=[scT], writes=[scT])
        bada = c.sb([128, 96], F32, "bada")
        nw = c.sb([128, 16], F32, "nw")
        wa = [c.sb([128, 16, 512], F32, "wa") for _ in range(3)]
        mrow = [c.sb([2, 512], F32, "mrow") for _ in range(2)]
        idf = self.identF
        for l in range(2):
            self.load_cols(bada, bada[:], I["b_ada"][l].rearrange("(n p) -> n p", p=128), 96)
            ps = self.pacc()
            for blk in range(24):
                w = wa[blk % 3]
                c.dma("sp" if blk % 2 == 0 else "act", w[:], I["w_ada"][l][:, blk * 512:(blk + 1) * 512].rearrange("(k p) n -> p k n", p=128), writes=[w])
                pr = self.pb()
                for k in range(16):
                    c.op("pe", lambda e: e.matmul(pr[:2, :512], lhsT=scT[:, :, k], rhs=w[:, k, :], start=(k == 0), stop=(k == 15)), reads=[w, scT], writes=[pr])
                mr = mrow[blk % 2]
                c.op("act", lambda e: e.copy(out=mr[:], in_=pr[:2, :512]), reads=[pr], writes=[mr])
                for j in range(4):
                    ch = blk * 4 + j
                    c.op("pe", lambda e: e.transpose(out=ps[:, ch * 2:ch * 2 + 2], in_=mr[:2, j * 128:(j + 1) * 128], identity=idf[:2, :2]), reads=[mr, idf], writes=[ps])
            mt = self.modT
            for b in range(2):
                c.op("dve", lambda e: e.tensor_tensor(out=mt[:, l, :, b], in0=ps[:, 0:192].rearrange("p (c b) -> p c b", b=2)[:, :, b], in1=bada[:], op=ALU.add),
                     reads=[ps, bada], writes=[mt])
            for which, (nm, scb) in enumerate((("norm1", 16), ("norm2", 64))):
                self.load_cols(nw, nw[:], I[nm][l].rearrange("(n p) -> n p", p=128), 16)
                sc = self.sca
                for b in range(2):
                    c.op("dve", lambda e: e.scalar_tensor_tensor(out=sc[:, l, which, :, b], in0=mt[:, l, scb:scb + 16, b], scalar=1.0, in1=nw[:], op0=ALU.add, op1=ALU.mult),
                         reads=[mt, nw], writes=[sc])
        c.end_stage()

    def stage_norm(self, XT, l, which):
        c = self.c
        c.begin_stage()
        self.HT = c.sb([128, 16, NT], BF16, "HT")
        c.begin_stage()
        shb = 0 if which == 0 else 48
        xs = [c.sb([128, 16, 256], F32, "nx") for _ in range(2)]
        sq = c.sb([128, 16, 256], BF16, "nsq")
        tmps = [c.sb([128, 256], F32, "ntmp") for _ in range(6)]
        r = c.sb([128, 256], F32, "nr")
        HT, sc, mt, ones, eps = self.HT, self.sca, self.modT, self.onesB, self.epsT
        for ti in range(8):
            t0 = ti * 256
            b = 0 if t0 < 1024 else 1
            x = xs[ti % 2]
            c.dma("sp", x[:], XT[:, :, t0:t0 + 256].rearrange("c p t -> p c t"), writes=[x])
            c.op("act", lambda e: e.activation(out=sq[:], in_=x[:], func=AF.Square), reads=[x], writes=[sq])
            ps = self.pb()
            for ch in range(16):
                c.op("pe", lambda e: e.matmul(ps[:, :256], lhsT=ones[:], rhs=sq[:, ch, :], start=(ch == 0), stop=(ch == 15)), reads=[ones, sq], writes=[ps])
            c.op("act", lambda e: e.activation(out=r[:], in_=ps[:, :256], func=AF.Sqrt, scale=1.0 / 2048, bias=eps[:, 0:1]), reads=[ps, eps], writes=[r])
            c.op("dve", lambda e: e.reciprocal(out=r[:], in_=r[:]), reads=[r], writes=[r])
            for ch in range(16):
                tmp = tmps[ch % 6]
                c.op("dve", lambda e: e.tensor_tensor(out=tmp[:], in0=x[:, ch, :], in1=r[:], op=ALU.mult), reads=[x, r], writes=[tmp])
                c.op("act", lambda e: e.activation(out=HT[:, ch, t0:t0 + 256], in_=tmp[:], func=AF.Identity,
                                                   scale=sc[:, l, which, ch, b:b + 1], bias=mt[:, l, shb + ch, b:b + 1]), reads=[tmp, sc, mt], writes=[HT])
        c.end_stage()


class KB2(KB):
    def wbufs(self, KC, n=3, width=256):
        return [self.c.sb([128, KC, width], BF16, "wb") for _ in range(n)]

    def lin_fm(self, W, blocks, IN, KC, epi, tiles=TILES, wb=None, prep=None):
        c = self.c
        if wb is None:
            wb = self.wbufs(KC)
        for bi, (c0, ncol) in enumerate(blocks):
            w = wb[bi % len(wb)]
            c.dma("pool", w[:, :KC, :ncol], W[:, c0:c0 + ncol].rearrange("(k p) n -> p k n", p=128), writes=[w])
            aux = prep(w, c0, ncol) if prep else None
            for off in range(0, ncol, 128):
                n = min(128, ncol - off)
                for ti, (t0, nt) in enumerate(tiles):
                    ps = self.pb()
                    for k in range(KC):
                        c.op("pe", lambda e: e.matmul(ps[:n, :nt], lhsT=w[:, k, off:off + n], rhs=IN[:, k, t0:t0 + nt], start=(k == 0), stop=(k == KC - 1)),
                             reads=[w, IN], writes=[ps])
                    epi(c0 + off, n, ti, t0, nt, ps, (aux, off))

    def lin_tm(self, W, blocks, IN, KC, epi, ttiles, wb=None):
        c = self.c
        if wb is None:
            wb = self.wbufs(KC)
        for bi, (c0, ncol) in enumerate(blocks):
            w = wb[bi % len(wb)]
            c.dma("pool", w[:, :KC, :ncol], W[:, c0:c0 + ncol].rearrange("(k p) n -> p k n", p=128), writes=[w])
            for tt in ttiles:
                ps = self.pb()
                for k in range(KC):
                    c.op("pe", lambda e: e.matmul(ps[:, :ncol], lhsT=IN[:, k, tt * 128:(tt + 1) * 128], rhs=w[:, k, :ncol], start=(k == 0), stop=(k == KC - 1)),
                         reads=[w, IN], writes=[ps])
                epi(c0, ncol, tt, ps)

    @staticmethod
    def blocks(a, b, step=256):
        return [(x, min(step, b - x)) for x in range(a, b, step)]

    def conv_row(self, src, dst, wt, ci, ntap_chunks):
        c = self.c
        w0 = wt[:, 0 * ntap_chunks + ci:0 * ntap_chunks + ci + 1]
        w1 = wt[:, 1 * ntap_chunks + ci:1 * ntap_chunks + ci + 1]
        w2 = wt[:, 2 * ntap_chunks + ci:2 * ntap_chunks + ci + 1]
        c.op("act", lambda e: e.activation(out=dst[:], in_=src[:], func=AF.Copy, scale=w1), reads=[src, wt], writes=[dst])
        sp = src[:, 0:1024].rearrange("p (s t) -> p s t", t=256)
        dp = dst[:, 0:1024].rearrange("p (s t) -> p s t", t=256)
        c.op("dve", lambda e: e.scalar_tensor_tensor(out=dp[:, :, 1:256], in0=sp[:, :, 0:255], scalar=w0, in1=dp[:, :, 1:256], op0=ALU.mult, op1=ALU.add), reads=[src, wt, dst], writes=[dst])
        c.op("dve", lambda e: e.scalar_tensor_tensor(out=dp[:, :, 0:255], in0=sp[:, :, 1:256], scalar=w2, in1=dp[:, :, 0:255], op0=ALU.mult, op1=ALU.add), reads=[src, wt, dst], writes=[dst])
        c.op("dve", lambda e: e.scalar_tensor_tensor(out=dst[:, 1025:2048], in0=src[:, 1024:2047], scalar=w0, in1=dst[:, 1025:2048], op0=ALU.mult, op1=ALU.add), reads=[src, wt, dst], writes=[dst])
        c.op("dve", lambda e: e.scalar_tensor_tensor(out=dst[:, 1024:2047], in0=src[:, 1025:2048], scalar=w2, in1=dst[:, 1024:2047], op0=ALU.mult, op1=ALU.add), reads=[src, wt, dst], writes=[dst])

    def stage_proj(self, l):
        c = self.c
        I = self.I
        W = I["w_in"][l]
        HT = self.HT
        c.begin_stage()
        wb = self.wbufs(16, 3, 512)
        QK = self.scr("QK", [8, 128, NT], BF16)
        VTM = self.scr("VTM", [16, 128, 1024], BF16)
        KTM = self.scr("KTM", [8, 128, 512], BF16)
        SRG = self.scr("SRG", [8, 128, NT], BF16)
        HV = self.scr("HV", [8, 128, NT], BF16)
        HX = self.scr("HX", [16, 128, NT], F32)
        DQK = self.scr("DQK", [16, 128, NT], BF16)
        DVT = self.scr("DVT", [8, 128, NT], BF16)
        SDZ = self.scr("SDZ", [8, 128, NT], BF16)
        GATE = self.scr("GATE", [48, 128, NT], BF16)
        rowb = [c.sb([128, NT], BF16, "rowb") for _ in range(2)]
        rowf = [c.sb([128, NT], F32, "rowf") for _ in range(2)]
        rowg = [c.sb([128, NT], F32, "rowg") for _ in range(2)]
        cnt = [0]

        ropeC = c.sb([128, 1024], F32, "ropeC")
        ropeS = c.sb([128, 1024], F32, "ropeS")
        c.dma("sp", ropeC[:], I["ropeC"], writes=[ropeC])
        c.dma("sp", ropeS[:], I["ropeS"], writes=[ropeS])
        wperm = [c.sb([128, 16, 256], BF16, "wperm") for _ in range(2)]
        t1 = c.sb([128, 512], F32, "t1")
        t2 = c.sb([128, 512], F32, "t2")
        pc = [0]

        def prep_qk(w, c0, ncol):
            wp = wperm[pc[0] % 2]
            pc[0] += 1
            src = w[:, :, 0:256].rearrange("p k (a two s) -> p k a two s", two=2, s=16)
            dst = wp[:].rearrange("p k (a two s) -> p k a two s", two=2, s=16)
            c.op("dve", lambda e: e.tensor_copy(out=dst[:, :, :, 0, :], in_=src[:, :, :, 1, :]), reads=[w], writes=[wp])
            c.op("act", lambda e: e.copy(out=dst[:, :, :, 1, :], in_=src[:, :, :, 0, :]), reads=[w], writes=[wp])
            return wp

        def epi_qk(col0, n, ti, t0, nt, ps, auxoff):
            wp, off = auxoff
            ci = col0 // 128
            row = rowb[ci % 2]
            scale = 1.0 if ci < 4 else 0.125
            if ti < 2:
                c.op("act", lambda e: e.activation(out=row[:, t0:t0 + nt], in_=ps[:, :nt], func=AF.Copy, scale=scale), reads=[ps], writes=[row])
            else:
                ps2 = self.pb()
                for k in range(16):
                    c.op("pe", lambda e: e.matmul(ps2[:, :nt], lhsT=wp[:, k, off:off + 128], rhs=HT[:, k, t0:t0 + nt], start=(k == 0), stop=(k == 15)), reads=[wp, HT], writes=[ps2])
                s0 = t0 - 1024
                c.op("dve", lambda e: e.tensor_tensor(out=t1[:, :nt], in0=ps[:, :nt], in1=ropeC[:, s0:s0 + nt], op=ALU.mult), reads=[ps, ropeC], writes=[t1])
                c.op("dve", lambda e: e.tensor_tensor(out=t2[:, :nt], in0=ps2[:, :nt], in1=ropeS[:, s0:s0 + nt], op=ALU.mult), reads=[ps2, ropeS], writes=[t2])
                c.op("dve", lambda e: e.tensor_tensor(out=t1[:, :nt], in0=t1[:, :nt], in1=t2[:, :nt], op=ALU.add), reads=[t1, t2], writes=[t1])
                c.op("act", lambda e: e.activation(out=row[:, t0:t0 + nt], in_=t1[:, :nt], func=AF.Copy, scale=scale), reads=[t1], writes=[row])
            if ti == 3:
                c.dma("sp", QK[ci], row[:], reads=[row])

        self.lin_fm(W, self.blocks(0, 1024), HT, 16, epi_qk, wb=wb, prep=prep_qk)

        stv = [c.sb([128, 512], BF16, "stv") for _ in range(4)]

        def epi_v(col0, ncol, tt, ps):
            s = stv[cnt[0] % 4]
            cnt[0] += 1
            if cnt[0] % 2:
                c.op("dve", lambda e: e.tensor_copy(out=s[:, :ncol], in_=ps[:, :ncol]), reads=[ps], writes=[s])
            else:
                c.op("act", lambda e: e.copy(out=s[:, :ncol], in_=ps[:, :ncol]), reads=[ps], writes=[s])
            c.dma("sp", VTM[tt][:, col0 - 1024:col0 - 1024 + ncol], s[:, :ncol], reads=[s])

        def epi_ktm(col0, ncol, tt, ps):
            s = stv[cnt[0] % 4]
            cnt[0] += 1
            c.op("act", lambda e: e.activation(out=s[:, :ncol], in_=ps[:, :ncol], func=AF.Copy, scale=0.125), reads=[ps], writes=[s])
            c.dma("sp", KTM[tt][:, col0 - 512:col0 - 512 + ncol], s[:, :ncol], reads=[s])

        self.lin_tm(W, self.blocks(1024, 2048, 512), HT, 16, epi_v, range(16), wb=wb)
        self.lin_tm(W, self.blocks(512, 1024, 512), HT, 16, epi_ktm, range(8), wb=wb)

        def mk_epi_act(base, dst, func):
            def epi(col0, n, ti, t0, nt, ps, aux):
                ci = (col0 - base) // 128
                row = rowb[ci % 2]
                c.op("act", lambda e: e.activation(out=row[:, t0:t0 + nt], in_=ps[:, :nt], func=func), reads=[ps], writes=[row])
                if ti == 3:
                    c.dma("sp", dst[ci], row[:], reads=[row])
            return epi

        self.lin_fm(W, self.blocks(2048, 3072, 512), HT, 16, mk_epi_act(2048, SRG, AF.Silu), wb=wb)
        self.lin_fm(W, self.blocks(9216, 10240, 512), HT, 16, mk_epi_act(9216, SDZ, AF.Silu), wb=wb)
        self.lin_fm(W, self.blocks(10272, 16416, 512), HT, 16, mk_epi_act(10272, GATE, AF.Sigmoid), wb=wb)

        hyw = c.sb([128, 72], F32, "hyw")
        self.load_cols(hyw, hyw[:], I["hy_short"][l].rearrange("k (n p) -> (k n) p", p=128), 72)
        dnw = c.sb([128, 72], F32, "dnw")
        self.load_cols(dnw, dnw[:], I["dn_conv"][l].rearrange("k (n p) -> (k n) p", p=128), 72)

        def epi_hy(col0, n, ti, t0, nt, ps, aux):
            ci = (col0 - 3072) // 128
            raw = rowf[ci % 2]
            c.op("act", lambda e: e.copy(out=raw[:, t0:t0 + nt], in_=ps[:, :nt]), reads=[ps], writes=[raw])
            if ti == 3:
                cv = rowg[ci % 2]
                self.conv_row(raw, cv, hyw, ci, 24)
                if ci < 8:
                    row = rowb[ci % 2]
                    c.op("act", lambda e: e.copy(out=row[:], in_=cv[:]), reads=[cv], writes=[row])
                    c.dma("sp", HV[ci], row[:], reads=[row])
                else:
                    c.dma("sp", HX[ci - 8], cv[:], reads=[cv])

        self.lin_fm(W, self.blocks(3072, 6144, 512), HT, 16, epi_hy, wb=wb)

        sqb = c.sb([128, NT], BF16, "sqb")
        rn = c.sb([128, 512], F32, "rn")
        ones, eps = self.onesB, self.epsT

        dn_pending = []

        def epi_dn(col0, n, ti, t0, nt, ps, aux):
            ci = (col0 - 6144) // 128
            raw = rowf[ci % 2]
            c.op("act", lambda e: e.copy(out=raw[:, t0:t0 + nt], in_=ps[:, :nt]), reads=[ps], writes=[raw])
            if ti == 3:
                while dn_pending:
                    dn_pending.pop(0)()
                dn_pending.append(lambda ci=ci, raw=raw: dn_post(ci, raw))

        def dn_post(ci, raw):
            cv = rowg[ci % 2]
            self.conv_row(raw, cv, dnw, ci, 24)
            c.op("act", lambda e: e.activation(out=cv[:], in_=cv[:], func=AF.Silu), reads=[cv], writes=[cv])
            row = rowb[ci % 2]
            if ci < 16:
                c.op("act", lambda e: e.activation(out=sqb[:], in_=cv[:], func=AF.Square), reads=[cv], writes=[sqb])
                for tj, (u0, nu) in enumerate(TILES):
                    p2 = self.pb()
                    c.op("pe", lambda e: e.matmul(p2[:, :nu], lhsT=ones[:], rhs=sqb[:, u0:u0 + nu], start=True, stop=True), reads=[ones, sqb], writes=[p2])
                    c.op("act", lambda e: e.activation(out=rn[:, :nu], in_=p2[:, :nu], func=AF.Sqrt, bias=eps[:, 0:1]), reads=[p2, eps], writes=[rn])
                    c.op("dve", lambda e: e.reciprocal(out=rn[:, :nu], in_=rn[:, :nu]), reads=[rn], writes=[rn])
                    sc_ = (128 ** -0.5) if ci < 8 else 1.0
                    c.op("dve", lambda e: e.scalar_tensor_tensor(out=row[:, u0:u0 + nu], in0=cv[:, u0:u0 + nu], scalar=sc_, in1=rn[:, :nu], op0=ALU.mult, op1=ALU.mult),
                         reads=[cv, rn], writes=[row])
                c.dma("sp", DQK[ci], row[:], reads=[row])
            else:
                c.op("act", lambda e: e.copy(out=row[:], in_=cv[:]), reads=[cv], writes=[row])
                c.dma("sp", DVT[ci - 16], row[:], reads=[row])

        self.lin_fm(W, self.blocks(6144, 9216, 512), HT, 16, epi_dn, wb=wb)
        while dn_pending:
            dn_pending.pop(0)()

        gbraw = self.gbraw

        def epi_gb(col0, ncol, tt, ps):
            c.op("dve", lambda e: e.tensor_copy(out=gbraw[:, tt, :], in_=ps[:, :32]), reads=[ps], writes=[gbraw])

        self.lin_tm(W, [(10240, 32)], HT, 16, epi_gb, range(16), wb=wb)
        c.end_stage()

    def stage_merge(self, l, XTin, XTout):
        c = self.c
        I = self.I
        c.begin_stage()
        MO = self.scr("MO", [3, 8, 128, NT], BF16)
        GATE = self.S["GATE"]
        MIX = self.scr("MIX", [16, 128, NT], BF16)
        mo = [c.sb([128, 8, NT], BF16, "mo") for _ in range(3)]
        for b in range(3):
            c.dma("sp", mo[b][:], MO[b].rearrange("c p t -> p c t"), writes=[mo[b]])
        wp = [c.sb([128, 8, 512], BF16, "wp") for _ in range(3)]
        gr = [c.sb([128, NT], BF16, "gr") for _ in range(4)]
        acc = [c.sb([128, NT], F32, "acc") for _ in range(4)]
        mixb = [c.sb([128, NT], BF16, "mixb") for _ in range(2)]
        tmp = c.sb([128, 512], F32, "mtmp")
        PW = [I["p_ret"][l], I["p_hy"][l], I["p_dn"][l]]
        n = 0
        nw_ = 0
        for jg in range(4):
            for b in range(3):
                w = wp[nw_ % 3]
                nw_ += 1
                c.dma("pool", w[:], PW[b][:, jg * 512:(jg + 1) * 512].rearrange("(k p) n -> p k n", p=128), writes=[w])
                for jj in range(4):
                    j = jg * 4 + jj
                    a = acc[jj]
                    g = gr[n % 4]
                    n += 1
                    c.dma("sp", g[:], GATE[b * 16 + j], writes=[g])
                    for ti, (t0, nt) in enumerate(TILES):
                        ps = self.pb()
                        for k_ in range(8):
                            c.op("pe", lambda e: e.matmul(ps[:, :nt], lhsT=w[:, k_, jj * 128:(jj + 1) * 128], rhs=mo[b][:, k_, t0:t0 + nt], start=(k_ == 0), stop=(k_ == 7)), reads=[w, mo[b]], writes=[ps])
                        if b == 0:
                            c.op("dve", lambda e: e.tensor_tensor(out=a[:, t0:t0 + nt], in0=ps[:, :nt], in1=g[:, t0:t0 + nt], op=ALU.mult), reads=[ps, g], writes=[a])
                        else:
                            c.op("dve", lambda e: e.tensor_tensor(out=tmp[:, :nt], in0=ps[:, :nt], in1=g[:, t0:t0 + nt], op=ALU.mult), reads=[ps, g], writes=[tmp])
                            c.op("dve", lambda e: e.tensor_tensor(out=a[:, t0:t0 + nt], in0=a[:, t0:t0 + nt], in1=tmp[:, :nt], op=ALU.add), reads=[a, tmp], writes=[a])
            for jj in range(4):
                j = jg * 4 + jj
                m = mixb[j % 2]
                a = acc[jj]
                c.op("act", lambda e: e.copy(out=m[:], in_=a[:]), reads=[a], writes=[m])
                c.dma("sp", MIX[j], m[:], reads=[m])
        c.end_stage()
        c.begin_stage()
        mix = c.sb([128, 16, NT], BF16, "mix")
        c.dma("sp", mix[:], MIX.rearrange("c p t -> p c t"), writes=[mix])
        self.resid_epilogue(I["w_o"][l], 2048, mix, 16, l, 32, XTin, XTout)
        c.end_stage()

    def resid_epilogue(self, W, ncols, IN, KC, l, gbase, XTin, XTout, tiles=TILES, wb=None):
        c = self.c
        mt = self.modT
        xr = [c.sb([128, NT], F32, "xr") for _ in range(2)]
        lo = tiles[0][0]
        hi = tiles[-1][0] + tiles[-1][1]

        def epi(col0, n, ti, t0, nt, ps, aux):
            j = col0 // 128
            x = xr[j % 2]
            if ti == 0:
                c.dma("sp", x[:, lo:hi], XTin[j][:, lo:hi], writes=[x])
            b = 0 if t0 < 1024 else 1
            c.op("dve", lambda e: e.scalar_tensor_tensor(out=x[:, t0:t0 + nt], in0=ps[:, :nt], scalar=mt[:, l, gbase + j, b:b + 1], in1=x[:, t0:t0 + nt], op0=ALU.mult, op1=ALU.add),
                 reads=[ps, mt, x], writes=[x])
            if ti == len(tiles) - 1:
                c.dma("sp", XTout[j][:, lo:hi], x[:, lo:hi], reads=[x])

        if wb is None:
            wb = self.wbufs(KC, 3, 512)
        self.lin_fm(W, self.blocks(0, ncols, 512), IN, KC, epi, tiles=tiles, wb=wb)

    def stage_ffn_up(self, l):
        c = self.c
        I = self.I
        ACTT = self.scr("ACTT", [43, 128, NT], BF16)
        c.begin_stage()
        fw = c.sb([128, 3 * 86], F32, "fw")
        for k in range(3):
            self.load_cols(fw, fw[:, k * 86:(k + 1) * 86], I["ffn_conv"][l][k].rearrange("(n p) -> n p", p=128), 86)
        rowf = [c.sb([128, NT], F32, "frow") for _ in range(3)]
        ga = [c.sb([128, NT], F32, "ga") for _ in range(2)]
        gb = [c.sb([128, NT], F32, "gb") for _ in range(2)]
        ab = [c.sb([128, NT], BF16, "ab") for _ in range(2)]
        wb = self.wbufs(16, 4, 512)
        HT = self.HT
        W = I["w_up"][l]
        cnt = [0]
        nb = 0
        for blk in range(11):
            c0 = blk * 512
            ncol = min(512, 5504 - c0)
            wa_, wg_ = wb[nb % 4], wb[(nb + 1) % 4]
            nb += 2
            c.dma("pool", wa_[:, :, :ncol], W[:, c0:c0 + ncol].rearrange("(k p) n -> p k n", p=128), writes=[wa_])
            c.dma("pool", wg_[:, :, :ncol], W[:, 5504 + c0:5504 + c0 + ncol].rearrange("(k p) n -> p k n", p=128), writes=[wg_])
            for off in range(0, ncol, 128):
                ci = (c0 + off) // 128
                for which, w in ((0, wa_), (1, wg_)):
                    raw = rowf[cnt[0] % 3]
                    cnt[0] += 1
                    for ti, (t0, nt) in enumerate(TILES):
                        ps = self.pb()
                        for kk in range(16):
                            c.op("pe", lambda e: e.matmul(ps[:, :nt], lhsT=w[:, kk, off:off + 128], rhs=HT[:, kk, t0:t0 + nt], start=(kk == 0), stop=(kk == 15)), reads=[w, HT], writes=[ps])
                        if ti % 2 == 0:
                            c.op("act", lambda e: e.copy(out=raw[:, t0:t0 + nt], in_=ps[:, :nt]), reads=[ps], writes=[raw])
                        else:
                            c.op("dve", lambda e: e.tensor_copy(out=raw[:, t0:t0 + nt], in_=ps[:, :nt]), reads=[ps], writes=[raw])
                    if which == 0:
                        g = ga[ci % 2]
                        self.conv_row(raw, g, fw, ci, 86)
                        c.op("act", lambda e: e.activation(out=g[:], in_=g[:], func=AF.Silu), reads=[g], writes=[g])
                    else:
                        g = ga[ci % 2]
                        g2 = gb[ci % 2]
                        self.conv_row(raw, g2, fw, 43 + ci, 86)
                        a_ = ab[ci % 2]
                        c.op("dve", lambda e: e.tensor_tensor(out=a_[:], in0=g[:], in1=g2[:], op=ALU.mult), reads=[g, g2], writes=[a_])
                        c.dma("sp", ACTT[ci], a_[:], reads=[a_])
        c.end_stage()

    def stage_ffn_down(self, l, XTin, XTout):
        c = self.c
        I = self.I
        ACTT = self.S["ACTT"]
        for half in range(2):
            c.begin_stage()
            a = c.sb([128, 43, 1024], BF16, "actin")
            c.dma("sp", a[:], ACTT[:, :, half * 1024:(half + 1) * 1024].rearrange("c p t -> p c t"), writes=[a])

            class Shift:
                def __init__(s, buf, sh):
                    s.buf, s.sh = buf, sh
            wb = self.wbufs(43, 2, 512)
            tiles = [(half * 1024, 512), (half * 1024 + 512, 512)]
            self._resid_shift(I["w_down"][l], a, 43, l, 80, XTin, XTout, tiles, half * 1024, wb)
            c.end_stage()

    def _resid_shift(self, W, IN, KC, l, gbase, XTin, XTout, tiles, tshift, wb):
        c = self.c
        mt = self.modT
        xr = [c.sb([128, 1024], F32, "xr2") for _ in range(2)]
        blocks = self.blocks(0, 2048, 512)
        for bi, (c0, ncol) in enumerate(blocks):
            w = wb[bi % len(wb)]
            c.dma("pool", w[:, :KC, :ncol], W[:, c0:c0 + ncol].rearrange("(k p) n -> p k n", p=128), writes=[w])
            for off in range(0, ncol, 128):
                j = (c0 + off) // 128
                x = xr[j % 2]
                c.dma("sp", x[:], XTin[j][:, tshift:tshift + 1024], writes=[x])
                for ti, (t0, nt) in enumerate(tiles):
                    ps = self.pb()
                    for k in range(KC):
                        c.op("pe", lambda e: e.matmul(ps[:, :nt], lhsT=w[:, k, off:off + 128], rhs=IN[:, k, t0 - tshift:t0 - tshift + nt], start=(k == 0), stop=(k == KC - 1)),
                             reads=[w, IN], writes=[ps])
                    b = 0 if t0 < 1024 else 1
                    c.op("dve", lambda e: e.scalar_tensor_tensor(out=x[:, t0 - tshift:t0 - tshift + nt], in0=ps[:, :nt], scalar=mt[:, l, gbase + j, b:b + 1],
                                                                 in1=x[:, t0 - tshift:t0 - tshift + nt], op0=ALU.mult, op1=ALU.add), reads=[ps, mt, x], writes=[x])
                c.dma("sp", XTout[j][:, tshift:tshift + 1024], x[:], reads=[x])

    def stage_final(self, XT):
        c = self.c
        I = self.I
        c.begin_stage()
        nf = c.sb([128, 16], F32, "nf")
        self.load_cols(nf, nf[:], I["norm_f"].rearrange("(n p) -> n p", p=128), 16)
        xs = [c.sb([128, 16, 128], F32, "fx") for _ in range(2)]
        sq = c.sb([128, 16, 128], BF16, "fsq")
        r = c.sb([128, 128], F32, "fr")
        yo = [c.sb([128, 2048], F32, "yo") for _ in range(2)]
        ones, eps, idf = self.onesB, self.epsT, self.identF
        for tt in range(16):
            t0 = tt * 128
            x = xs[tt % 2]
            y = yo[tt % 2]
            c.dma("sp", x[:], XT[:, :, t0:t0 + 128].rearrange("c p t -> p c t"), writes=[x])
            c.op("act", lambda e: e.activation(out=sq[:], in_=x[:], func=AF.Square), reads=[x], writes=[sq])
            ps = self.pb()
            for ch in range(16):
                c.op("pe", lambda e: e.matmul(ps[:, :128], lhsT=ones[:], rhs=sq[:, ch, :], start=(ch == 0), stop=(ch == 15)), reads=[ones, sq], writes=[ps])
            c.op("act", lambda e: e.activation(out=r[:], in_=ps[:, :128], func=AF.Sqrt, scale=1.0 / 2048, bias=eps[:, 0:1]), reads=[ps, eps], writes=[r])
            c.op("dve", lambda e: e.reciprocal(out=r[:], in_=r[:]), reads=[r], writes=[r])
            for ch in range(16):
                c.op("dve", lambda e: e.scalar_tensor_tensor(out=x[:, ch, :], in0=x[:, ch, :], scalar=nf[:, ch:ch + 1], in1=r[:], op0=ALU.mult, op1=ALU.mult), reads=[x, nf, r], writes=[x])
            for g in range(4):
                p2 = self.pb()
                for j in range(4):
                    ch = g * 4 + j
                    c.op("pe", lambda e: e.transpose(out=p2[:, j * 128:(j + 1) * 128], in_=x[:, ch, :], identity=idf[:]), reads=[x, idf], writes=[p2])
                if g % 2 == 0:
                    c.op("dve", lambda e: e.tensor_copy(out=y[:, g * 512:(g + 1) * 512], in_=p2[:]), reads=[p2], writes=[y])
                else:
                    c.op("act", lambda e: e.copy(out=y[:, g * 512:(g + 1) * 512], in_=p2[:]), reads=[p2], writes=[y])
            c.dma("sp", self.O["y"][t0:t0 + 128, :], y[:], reads=[y])
        c.end_stage()


class KB3(KB2):
    def stage_ret(self, l):
        c = self.c
        I = self.I
        c.begin_stage()
        QK, VTM, KTM, SRG = self.S["QK"], self.S["VTM"], self.S["KTM"], self.S["SRG"]
        MO = self.scr("MO", [3, 8, 128, NT], BF16)
        OSR = self.O["osret"]
        ones, eps = self.onesB, self.epsT
        V = c.sb([128, 16, 1024], BF16, "V")
        c.dma("sp", V[:], VTM.rearrange("t p n -> p t n"), writes=[V])
        Kt = c.sb([128, 8, 512], BF16, "Kt")
        c.dma("sp", Kt[:], KTM.rearrange("t p n -> p t n"), writes=[Kt])
        lg = c.sb([128, 16], F32, "lg")
        c.dma("sp", lg[:], I["ret_decay"][l].rearrange("d h -> (d h)").partition_broadcast(128), writes=[lg])
        c.op("act", lambda e: e.activation(out=lg[:], in_=lg[:], func=AF.Exp, scale=-1.0), reads=[lg], writes=[lg])
        c.op("act", lambda e: e.activation(out=lg[:], in_=lg[:], func=AF.Ln, bias=self.onesF[:, 0:1]), reads=[lg], writes=[lg])
        c.op("dve", lambda e: e.tensor_scalar(out=lg[:], in0=lg[:], scalar1=-1.0, scalar2=None, op0=ALU.mult), reads=[lg], writes=[lg])
        tabs = {}
        for nm, w in (("S", 1920), ("P", 384)):
            for k in ("dpos", "dneg", "dz"):
                t = c.sb([128, w], F32, k + nm)
                c.dma("sp", t[:], I[k + nm], writes=[t])
                tabs[k + nm] = t
        expfb = c.sb([128, 4], F32, "expfb")
        c.dma("sp", expfb[:], I["expfb"], writes=[expfb])
        idx1 = c.sb([128, 1024], F32, "idx1")
        idx2 = c.sb([128, 1024], F32, "idx2")
        c.dma("sp", idx1[:], I["idx1"], writes=[idx1])
        c.dma("sp", idx2[:], I["idx2"], writes=[idx2])
        KD = c.sb([128, 2, 8, 2], F32, "KD")
        for d in range(2):
            for h in range(8):
                c.op("act", lambda e: e.activation(out=KD[:, d, h, :], in_=expfb[:, d * 2:d * 2 + 2], func=AF.Exp, scale=lg[:, d * 8 + h:d * 8 + h + 1]), reads=[expfb, lg], writes=[KD])
        TabS = c.sb([128, 1920], F32, "TabS")
        TabP = c.sb([128, 384], F32, "TabP")
        ttmp = c.sb([128, 1920], F32, "ttmp")
        qrow = c.sb([128, NT], BF16, "qrow")
        krow = c.sb([128, NT], BF16, "krow")
        lgsel = c.sb([128, 2], F32, "lgsel")
        dec = c.sb([128, 1024], F32, "dec")
        qdf = c.sb([128, 1024], BF16, "qdf")
        qdb = c.sb([128, 1024], BF16, "qdb")
        S0 = c.sb([128, 2, 128], BF16, "S0")
        srg = c.sb([128, NT], BF16, "srg")
        sm = [c.sb([128, 512], BF16, "sm") for _ in range(3)]
        osb = c.sb([128, 512], F32, "osb")
        osq = c.sb([128, 512], BF16, "osq")
        rr = c.sb([128, 512], F32, "rr")
        orow = c.sb([128, NT], BF16, "orow")
        kf = [c.sb([128, 64], BF16, "kf") for _ in range(4)]
        sst = [c.sb([64, 128], F32, "sst") for _ in range(4)]
        n_sm = [0]
        n_kf = [0]

        def build_tab(Tab, nm, w, h):
            dp, dn, dz = tabs["dpos" + nm], tabs["dneg" + nm], tabs["dz" + nm]
            c.op("dve", lambda e: e.tensor_scalar(out=ttmp[:, :w], in0=dp[:], scalar1=lg[:, h:h + 1], scalar2=None, op0=ALU.mult), reads=[dp, lg], writes=[ttmp])
            c.op("dve", lambda e: e.scalar_tensor_tensor(out=ttmp[:, :w], in0=dn[:], scalar=lg[:, 8 + h:9 + h], in1=ttmp[:, :w], op0=ALU.mult, op1=ALU.add), reads=[dn, lg, ttmp], writes=[ttmp])
            c.op("act", lambda e: e.activation(out=ttmp[:, :w], in_=ttmp[:, :w], func=AF.Exp), reads=[ttmp], writes=[ttmp])
            c.op("dve", lambda e: e.tensor_tensor(out=Tab[:, :w], in0=ttmp[:, :w], in1=dz[:], op=ALU.add), reads=[ttmp, dz], writes=[Tab])

        def finish_o(pso, n, h, tok0):
            c.op("act", lambda e: e.copy(out=osb[:, :n], in_=pso[:, :n]), reads=[pso], writes=[osb])
            c.op("act", lambda e: e.activation(out=osq[:, :n], in_=osb[:, :n], func=AF.Square), reads=[osb], writes=[osq])
            p2 = self.pb()
            c.op("pe", lambda e: e.matmul(p2[:, :n], lhsT=ones[:], rhs=osq[:, :n], start=True, stop=True), reads=[ones, osq], writes=[p2])
            c.op("act", lambda e: e.activation(out=rr[:, :n], in_=p2[:, :n], func=AF.Sqrt, scale=1.0 / 128, bias=eps[:, 0:1]), reads=[p2, eps], writes=[rr])
            c.op("dve", lambda e: e.reciprocal(out=rr[:, :n], in_=rr[:, :n]), reads=[rr], writes=[rr])
            c.op("dve", lambda e: e.tensor_tensor(out=osb[:, :n], in0=osb[:, :n], in1=rr[:, :n], op=ALU.mult), reads=[osb, rr], writes=[osb])
            c.op("dve", lambda e: e.tensor_tensor(out=orow[:, tok0:tok0 + n], in0=osb[:, :n], in1=srg[:, tok0:tok0 + n], op=ALU.mult), reads=[osb, srg], writes=[orow])

        srg2 = [srg, c.sb([128, NT], BF16, "srg2")]
        orow2 = [orow, c.sb([128, NT], BF16, "orow2")]
        sm.append(c.sb([128, 512], BF16, "sm"))
        NS = len(sm)
        pending = []

        def flush(keep):
            while len(pending) > keep:
                pending.pop(0)()

        def scores_loop(n_j, mk_score, mk_acc):
            nxt = mk_score(0)
            for jc in range(n_j):
                cur = nxt
                nxt = mk_score(jc + 1) if jc + 1 < n_j else None
                mk_acc(jc, cur)

        for h in range(8):
            hp, po = h // 2, (h % 2) * 64
            srg_h, orow_h = srg2[h % 2], orow2[h % 2]
            if h % 2 == 0:
                c.dma("sp", qrow[:], QK[hp], writes=[qrow])
                c.dma("sp", krow[:], QK[4 + hp], writes=[krow])
                for d in range(2):
                    c.op("dve", lambda e: e.tensor_copy(out=lgsel[0:64, d:d + 1], in_=lg[0:64, d * 8 + h:d * 8 + h + 1]), reads=[lg], writes=[lgsel])
                    c.op("dve", lambda e: e.tensor_copy(out=lgsel[64:128, d:d + 1], in_=lg[64:128, d * 8 + h + 1:d * 8 + h + 2]), reads=[lg], writes=[lgsel])
                c.op("act", lambda e: e.activation(out=dec[:], in_=idx1[:], func=AF.Exp, scale=lgsel[:, 0:1]), reads=[idx1, lgsel], writes=[dec])
                c.op("dve", lambda e: e.tensor_tensor(out=qdf[:], in0=qrow[:, 1024:2048], in1=dec[:], op=ALU.mult), reads=[qrow, dec], writes=[qdf])
                c.op("act", lambda e: e.activation(out=dec[:], in_=idx2[:], func=AF.Exp, scale=lgsel[:, 1:2]), reads=[idx2, lgsel], writes=[dec])
                c.op("dve", lambda e: e.tensor_tensor(out=qdb[:], in0=qrow[:, 1024:2048], in1=dec[:], op=ALU.mult), reads=[qrow, dec], writes=[qdb])
            c.dma("sp", srg_h[:], SRG[h], writes=[srg_h])
            for d in range(2):
                c.dma("pool", S0[po:po + 64, d, :], I["sret"][l, d, h], writes=[S0])
            build_tab(TabS, "S", 1920, h)
            build_tab(TabP, "P", 384, h)

            def fin(pso, n, tok0, h=h, srg_h=srg_h, orow_h=orow_h, store=False):
                def f():
                    c.op("act", lambda e: e.copy(out=osb[:, :n], in_=pso[:, :n]), reads=[pso], writes=[osb])
                    c.op("act", lambda e: e.activation(out=osq[:, :n], in_=osb[:, :n], func=AF.Square), reads=[osb], writes=[osq])
                    p2 = self.pb()
                    c.op("pe", lambda e: e.matmul(p2[:, :n], lhsT=ones[:], rhs=osq[:, :n], start=True, stop=True), reads=[ones, osq], writes=[p2])
                    c.op("act", lambda e: e.activation(out=rr[:, :n], in_=p2[:, :n], func=AF.Sqrt, scale=1.0 / 128, bias=eps[:, 0:1]), reads=[p2, eps], writes=[rr])
                    c.op("dve", lambda e: e.reciprocal(out=rr[:, :n], in_=rr[:, :n]), reads=[rr], writes=[rr])
                    c.op("dve", lambda e: e.tensor_tensor(out=osb[:, :n], in0=osb[:, :n], in1=rr[:, :n], op=ALU.mult), reads=[osb, rr], writes=[osb])
                    c.op("dve", lambda e: e.tensor_tensor(out=orow_h[:, tok0:tok0 + n], in0=osb[:, :n], in1=srg_h[:, tok0:tok0 + n], op=ALU.mult), reads=[osb, srg_h], writes=[orow_h])
                    if store:
                        c.dma("sp", MO[0][h], orow_h[:], reads=[orow_h])
                return f

            for i0 in (0, 512):
                pso = self.pacc()

                def mk_score(jc, i0=i0):
                    pss = self.pb()
                    c.op("pe", lambda e: e.matmul(pss[:, :512], lhsT=krow[po:po + 64, 1024 + jc * 128:1024 + (jc + 1) * 128], rhs=qrow[po:po + 64, 1024 + i0:1024 + i0 + 512], start=True, stop=True),
                         reads=[krow, qrow], writes=[pss])
                    return pss

                def mk_acc(jc, pss, i0=i0, pso=pso):
                    s = sm[n_sm[0] % NS]
                    n_sm[0] += 1
                    u0 = i0 - 128 * jc + 896
                    c.op("dve", lambda e: e.tensor_tensor(out=s[:], in0=pss[:, :512], in1=TabS[:, u0:u0 + 512], op=ALU.mult), reads=[pss, TabS], writes=[s])
                    c.op("pe", lambda e: e.matmul(pso[:, :512], lhsT=V[:, 8 + jc, h * 128:(h + 1) * 128], rhs=s[:], start=(jc == 0), stop=False), reads=[V, s], writes=[pso])

                scores_loop(8, mk_score, mk_acc)
                c.op("pe", lambda e: e.matmul(pso[:, :512], lhsT=S0[po:po + 64, 0, :], rhs=qdf[po:po + 64, i0:i0 + 512], start=False, stop=False), reads=[S0, qdf], writes=[pso])
                c.op("pe", lambda e: e.matmul(pso[:, :512], lhsT=S0[po:po + 64, 1, :], rhs=qdb[po:po + 64, i0:i0 + 512], start=False, stop=True), reads=[S0, qdb], writes=[pso])
                pending.append(fin(pso, 512, 1024 + i0))
                flush(1)
            for s_ in range(4):
                b0 = s_ * 256
                pso = self.pacc()

                def mk_score(jc, b0=b0):
                    pss = self.pb()
                    c.op("pe", lambda e: e.matmul(pss[:, :256], lhsT=krow[po:po + 64, b0 + jc * 128:b0 + (jc + 1) * 128], rhs=qrow[po:po + 64, b0:b0 + 256], start=True, stop=True),
                         reads=[krow, qrow], writes=[pss])
                    return pss

                def mk_acc(jc, pss, pso=pso, s_=s_):
                    s = sm[n_sm[0] % NS]
                    n_sm[0] += 1
                    u0 = 128 - 128 * jc
                    c.op("dve", lambda e: e.tensor_tensor(out=s[:, :256], in0=pss[:, :256], in1=TabP[:, u0:u0 + 256], op=ALU.mult), reads=[pss, TabP], writes=[s])
                    c.op("pe", lambda e: e.matmul(pso[:, :256], lhsT=V[:, s_ * 2 + jc, h * 128:(h + 1) * 128], rhs=s[:, :256], start=(jc == 0), stop=(jc == 1)), reads=[V, s], writes=[pso])

                scores_loop(2, mk_score, mk_acc)
                pending.append(fin(pso, 256, b0, store=(s_ == 3)))
                flush(1)
                for d in range(2):
                    pst = self.pb()
                    for tt in range(2):
                        k_ = kf[n_kf[0] % 4]
                        n_kf[0] += 1
                        c.op("dve", lambda e: e.tensor_scalar(out=k_[:], in0=Kt[:, s_ * 2 + tt, h * 64:(h + 1) * 64], scalar1=KD[:, d, h, tt:tt + 1], scalar2=None, op0=ALU.mult), reads=[Kt, KD], writes=[k_])
                        c.op("pe", lambda e: e.matmul(pst[:64, :128], lhsT=k_[:], rhs=V[:, s_ * 2 + tt, h * 128:(h + 1) * 128], start=(tt == 0), stop=(tt == 1)), reads=[k_, V], writes=[pst])
                    st = sst[(s_ * 2 + d) % 4]
                    c.op("act", lambda e: e.copy(out=st[:], in_=pst[:64, :128]), reads=[pst], writes=[st])
                    c.dma("sp", OSR[s_, l, d, h], st[:], reads=[st])
        flush(0)
        c.end_stage()


import os


class KB4(KB3):
    def stage_dn(self, l):
        c = self.c
        I = self.I
        c.begin_stage()
        DQK, DVT, SDZ = self.S["DQK"], self.S["DVT"], self.S["SDZ"]
        MO = self.scr("MO", [3, 8, 128, NT], BF16)
        OSD = self.O["osdn"]
        onesF, onesB, eps, idF, idB = self.onesF, self.onesB, self.epsT, self.identF, self.identB
        gbraw = self.gbraw
        cm = {}
        for nm in ("UF", "UB", "SU", "SL"):
            t = c.sb([128, 128], F32, nm)
            c.dma("sp", t[:], I[nm], writes=[t])
            cm[nm] = t
        alog = c.sb([128, 16], F32, "alog")
        dtb = c.sb([128, 16], F32, "dtb")
        c.dma("sp", alog[:], I["dn_a_log"][l].rearrange("d h -> (d h)").partition_broadcast(128), writes=[alog])
        c.dma("sp", dtb[:], I["dn_dt_bias"][l].rearrange("d h -> (d h)").partition_broadcast(128), writes=[dtb])
        dnn = c.sb([128, 1], F32, "dnn")
        c.dma("sp", dnn[:], I["dn_norm"][l].rearrange("(p o) -> p o", o=1), writes=[dnn])
        c.op("act", lambda e: e.activation(out=alog[:], in_=alog[:], func=AF.Exp), reads=[alog], writes=[alog])
        c.op("dve", lambda e: e.tensor_scalar(out=alog[:], in0=alog[:], scalar1=-1.0, scalar2=None, op0=ALU.mult), reads=[alog], writes=[alog])
        G = c.sb([128, 16, 16], F32, "G")
        BT = c.sb([128, 16, 16], F32, "BT")
        NBT = c.sb([128, 16, 16], F32, "NBT")
        for tt in range(16):
            c.op("dve", lambda e: e.tensor_tensor(out=G[:, tt, :], in0=gbraw[:, tt, 0:16], in1=dtb[:], op=ALU.add), reads=[gbraw, dtb], writes=[G])
        c.op("act", lambda e: e.activation(out=G[:], in_=G[:], func=AF.Exp), reads=[G], writes=[G])
        c.op("act", lambda e: e.activation(out=G[:], in_=G[:], func=AF.Ln, bias=onesF[:, 0:1]), reads=[G, onesF], writes=[G])
        for tt in range(16):
            c.op("dve", lambda e: e.tensor_tensor(out=G[:, tt, :], in0=G[:, tt, :], in1=alog[:], op=ALU.mult), reads=[G, alog], writes=[G])
        c.op("act", lambda e: e.activation(out=BT[:], in_=gbraw[:, :, 16:32], func=AF.Sigmoid), reads=[gbraw], writes=[BT])
        c.op("dve", lambda e: e.tensor_scalar(out=NBT[:], in0=BT[:], scalar1=-1.0, scalar2=None, op0=ALU.mult), reads=[BT], writes=[NBT])
        GC = c.sb([128, 16, 16], F32, "GC")
        GL = c.sb([128, 16, 16], F32, "GL")
        for tt in range(16):
            for d in range(2):
                U = cm["UF"] if d == 0 else cm["UB"]
                ps = self.pb()
                c.op("pe", lambda e: e.matmul(ps[:, 0:8], lhsT=U[:], rhs=G[:, tt, d * 8:d * 8 + 8], start=True, stop=True), reads=[U, G], writes=[ps])
                c.op("pe", lambda e: e.matmul(ps[:, 8:16], lhsT=onesF[:], rhs=G[:, tt, d * 8:d * 8 + 8], start=True, stop=True), reads=[onesF, G], writes=[ps])
                c.op("dve", lambda e: e.tensor_copy(out=GC[:, tt, d * 8:d * 8 + 8], in_=ps[:, 0:8]), reads=[ps], writes=[GC])
                c.op("dve", lambda e: e.tensor_copy(out=GL[:, tt, d * 8:d * 8 + 8], in_=ps[:, 8:16]), reads=[ps], writes=[GL])
        BK = c.sb([128, 16, 16], F32, "BK")
        KDS = c.sb([128, 16, 16], F32, "KDS")
        EGL = c.sb([128, 16, 16], F32, "EGL")
        c.op("act", lambda e: e.activation(out=BK[:], in_=GC[:], func=AF.Exp), reads=[GC], writes=[BK])
        c.op("dve", lambda e: e.tensor_tensor(out=BK[:], in0=BK[:], in1=BT[:], op=ALU.mult), reads=[BK, BT], writes=[BK])
        c.op("dve", lambda e: e.tensor_tensor(out=KDS[:], in0=GL[:], in1=GC[:], op=ALU.subtract), reads=[GL, GC], writes=[KDS])
        c.op("act", lambda e: e.activation(out=KDS[:], in_=KDS[:], func=AF.Exp), reads=[KDS], writes=[KDS])
        c.op("act", lambda e: e.activation(out=EGL[:], in_=GL[:], func=AF.Exp), reads=[GL], writes=[EGL])

        if "GDBG" in self.dbg:
            gd = self.scr("GDBG", [4, 128, 16, 16], F32)
            for i_, t_ in enumerate((G, BT, GC, GL)):
                c.dma("sp", gd[i_], t_[:], reads=[t_])
        OACC = [c.sb([128, NT], F32, "oacc") for _ in range(8)]
        c.begin_stage()
        Sf = [c.sb([128, 128], F32, "Sf") for _ in range(8)]
        Sb = [c.sb([128, 128], BF16, "Sb") for _ in range(8)]
        NI = 8

        def ring(shape, dt, nm, n=NI):
            bufs = [c.sb(shape, dt, nm) for _ in range(n)]
            st = [0]

            def nxt():
                st[0] += 1
                return bufs[st[0] % n]
            return nxt
        r_q = ring([128, 128], BF16, "rq")
        r_k = ring([128, 128], BF16, "rk")
        r_v = ring([128, 128], BF16, "rv")
        r_gbc = ring([128, 128], F32, "rgbc")
        r_t = ring([128, 128], F32, "rt", 2 * NI)
        r_vb = ring([128, 128], F32, "rvb")
        r_kbe = ring([128, 128], F32, "rkbe")
        r_kd = ring([128, 128], F32, "rkd")
        r_nw = ring([128, 128], F32, "rnw")
        r_at = ring([128, 128], BF16, "rat")
        r_qd = ring([128, 128], BF16, "rqd")
        r_vn = ring([128, 128], F32, "rvn")
        r_vnb = ring([128, 128], BF16, "rvnb")
        r_so = ring([128, 128], F32, "rso", 3)
        r_tw = ring([128, 256], F32, "rtw", 2 * NI)
        r_tq = ring([128, 256], F32, "rtq", NI)
        r_am = ring([128, 256], F32, "ram", 2 * NI)
        r_pp = ring([128, 256], F32, "rpp", NI)
        mlu = c.sb([128, 7, 256], F32, "mlu")
        mul = c.sb([128, 7, 256], F32, "mul")
        c.dma("sp", mlu[:], I["MLU"], writes=[mlu])
        c.dma("sp", mul[:], I["MUL"], writes=[mul])
        id2 = c.sb([128, 256], F32, "id2")
        c.dma("sp", id2[:, 0:128], I["ident"], writes=[id2])
        c.dma("sp", id2[:, 128:256], I["ident"], writes=[id2])
        PB = self.PB
        pbn = [0]

        def pbank():
            pbn[0] += 1
            return PB[pbn[0] % 8]

        def process(tt, h, d, last_seq):
            t0 = tt * 128
            col = d * 8 + h
            U = cm["UF"] if d == 0 else cm["UB"]
            MS = cm["SL"] if d == 0 else cm["SU"]
            MI = cm["UF"] if d == 0 else cm["UB"]
            gc_ap = GC[:, tt, col:col + 1]
            qT, kT, vT = r_q(), r_k(), r_v()
            c.dma("sp", qT[:], DQK[h][:, t0:t0 + 128], writes=[qT])
            c.dma("sp", kT[:], DQK[8 + h][:, t0:t0 + 128], writes=[kT])
            c.dma("sp", vT[:], DVT[h][:, t0:t0 + 128], writes=[vT])
            gbc = r_gbc()
            c.op("act", lambda e: e.activation(out=gbc[:], in_=onesF[:], func=AF.Copy, scale=G[:, tt, col:col + 1]), reads=[onesF, G], writes=[gbc])
            yield
            bank = PB[h]
            ptr = pR = pG = pT = ps1 = ps2 = pw = psv = pso = pss = bank
            ptb = bank[:].bitcast(BF16)
            c.op("pe", lambda e: e.transpose(out=ptb[:, 0:128], in_=kT[:], identity=idB[:]), reads=[kT, idB], writes=[ptr])
            c.op("pe", lambda e: e.transpose(out=ptb[:, 128:256], in_=vT[:], identity=idB[:]), reads=[vT, idB], writes=[ptr])
            c.op("pe", lambda e: e.matmul(pR[:, 128:256], lhsT=gbc[:], rhs=U[:], start=True, stop=True), reads=[gbc, U], writes=[pR])
            c.op("pe", lambda e: e.matmul(pG[:, 256:384], lhsT=kT[:], rhs=kT[:], start=True, stop=True), reads=[kT], writes=[pG])
            c.op("pe", lambda e: e.matmul(pG[:, 384:512], lhsT=kT[:], rhs=qT[:], start=True, stop=True), reads=[kT, qT], writes=[pG])
            yield
            vb, kbe, kd = r_vb(), r_kbe(), r_kd()
            c.op("dve", lambda e: e.tensor_scalar(out=kbe[:], in0=ptb[:, 0:128], scalar1=BK[:, tt, col:col + 1], scalar2=None, op0=ALU.mult), reads=[ptr, BK], writes=[kbe])
            c.op("dve", lambda e: e.tensor_scalar(out=kd[:], in0=ptb[:, 0:128], scalar1=KDS[:, tt, col:col + 1], scalar2=None, op0=ALU.mult), reads=[ptr, KDS], writes=[kd])
            c.op("dve", lambda e: e.tensor_scalar(out=vb[:], in0=ptb[:, 128:256], scalar1=BT[:, tt, col:col + 1], scalar2=None, op0=ALU.mult), reads=[ptr, BT], writes=[vb])
            d1, d2 = r_t(), r_t()
            c.op("dve", lambda e: e.tensor_scalar(out=d1[:], in0=pR[:, 128:256], scalar1=gc_ap, scalar2=0.0, op0=ALU.subtract, op1=ALU.max), reads=[pR, GC], writes=[d1])
            c.op("dve", lambda e: e.tensor_scalar(out=d2[:], in0=pR[:, 128:256], scalar1=gc_ap, scalar2=0.0, op0=ALU.subtract, op1=ALU.min), reads=[pR, GC], writes=[d2])
            er = gbc
            c.op("act", lambda e: e.activation(out=er[:], in_=pR[:, 128:256], func=AF.Exp), reads=[pR], writes=[er])
            yield
            c.op("act", lambda e: e.activation(out=d1[:], in_=d1[:], func=AF.Exp, scale=-1.0), reads=[d1], writes=[d1])
            c.op("act", lambda e: e.activation(out=d2[:], in_=d2[:], func=AF.Exp), reads=[d2], writes=[d2])
            qd = r_qd()
            c.op("pool", lambda e: e.tensor_tensor(out=qd[:], in0=qT[:], in1=er[:], op=ALU.mult), reads=[qT, er], writes=[qd])
            yield
            pp = r_pp()
            c.op("dve", lambda e: e.scalar_tensor_tensor(out=d1[:], in0=pG[:, 256:384], scalar=NBT[:, tt, col:col + 1], in1=d1[:], op0=ALU.mult, op1=ALU.mult), reads=[pG, NBT, d1], writes=[d1])
            c.op("dve", lambda e: e.tensor_tensor(out=d2[:], in0=pG[:, 384:512], in1=d2[:], op=ALU.mult), reads=[pG, d2], writes=[d2])
            yield
            c.op("pool", lambda e: e.tensor_tensor(out=pp[:, 0:128], in0=d1[:], in1=MS[:], op=ALU.mult), reads=[d1, MS], writes=[pp])
            at = r_at()
            c.op("pool", lambda e: e.tensor_tensor(out=at[:], in0=d2[:], in1=MI[:], op=ALU.mult), reads=[d2, MI], writes=[at])
            yield
            c.op("pe", lambda e: e.transpose(out=pT[:, :128], in_=pp[:, 0:128], identity=idF[:]), reads=[pp, idF], writes=[pT])
            yield
            c.op("act", lambda e: e.copy(out=pp[:, 128:256], in_=pT[:, :128]), reads=[pT], writes=[pp])
            yield
            MM = mlu if d == 0 else mul
            am = r_am()
            c.op("pool", lambda e: e.tensor_tensor(out=am[:], in0=pp[:], in1=MM[:, 0, :], op=ALU.mult), reads=[pp, MM], writes=[am])
            yield
            tw = r_tw()
            c.op("dve", lambda e: e.tensor_tensor(out=tw[:], in0=am[:], in1=id2[:], op=ALU.add), reads=[am, id2], writes=[tw])
            am = r_am()
            c.op("pool", lambda e: e.tensor_tensor(out=am[:], in0=pp[:], in1=MM[:, 1, :], op=ALU.mult), reads=[pp, MM], writes=[am])
            yield
            for s in range(1, 7):
                c.op("pe", lambda e: e.matmul(ps1[:, 0:128], lhsT=am[:, 128:256], rhs=tw[:, 0:128], start=True, stop=True), reads=[am, tw], writes=[ps1])
                c.op("pe", lambda e: e.matmul(ps1[:, 128:256], lhsT=am[:, 0:128], rhs=tw[:, 128:256], start=True, stop=True), reads=[am, tw], writes=[ps1])
                if s < 6:
                    am = r_am()
                    c.op("pool", lambda e: e.tensor_tensor(out=am[:], in0=pp[:], in1=MM[:, s + 1, :], op=ALU.mult), reads=[pp, MM], writes=[am])
                yield
                p1 = r_tq()
                c.op("act", lambda e: e.copy(out=p1[:], in_=ps1[:, 0:256]), reads=[ps1], writes=[p1])
                yield
                c.op("pe", lambda e: e.matmul(ps2[:, 256:384], lhsT=tw[:, 128:256], rhs=p1[:, 0:128], start=True, stop=True), reads=[tw, p1], writes=[ps2])
                c.op("pe", lambda e: e.matmul(ps2[:, 384:512], lhsT=tw[:, 0:128], rhs=p1[:, 128:256], start=True, stop=True), reads=[tw, p1], writes=[ps2])
                yield
                ntw = r_tw()
                c.op("dve", lambda e: e.tensor_tensor(out=ntw[:], in0=ps2[:, 256:512], in1=tw[:], op=ALU.add), reads=[ps2, tw], writes=[ntw])
                tw = ntw
                yield
            c.op("pe", lambda e: e.matmul(pw[:, :128], lhsT=kbe[:], rhs=tw[:, 128:256], start=True, stop=True), reads=[kbe, tw], writes=[pw])
            yield
            nw = r_nw()
            c.op("act", lambda e: e.activation(out=nw[:], in_=pw[:, :128], func=AF.Copy, scale=-1.0), reads=[pw], writes=[nw])
            yield
            S_f, S_b = Sf[h], Sb[h]
            c.op("pe", lambda e: e.matmul(psv[:, 128:256], lhsT=tw[:, 128:256], rhs=vb[:], start=True, stop=False), reads=[tw, vb], writes=[psv])
            c.op("pe", lambda e: e.matmul(psv[:, 128:256], lhsT=nw[:], rhs=S_f[:], start=False, stop=True), reads=[nw, S_f], writes=[psv])
            yield
            vn = r_vn()
            vnb = r_vnb()
            c.op("act", lambda e: e.copy(out=vn[:], in_=psv[:, 128:256]), reads=[psv], writes=[vn])
            c.op("act", lambda e: e.copy(out=vnb[:], in_=vn[:]), reads=[vn], writes=[vnb])
            yield
            c.op("pe", lambda e: e.matmul(pso[:, 256:384], lhsT=S_b[:], rhs=qd[:], start=True, stop=False), reads=[S_b, qd], writes=[pso])
            c.op("pe", lambda e: e.matmul(pso[:, 256:384], lhsT=vnb[:], rhs=at[:], start=False, stop=True), reads=[vnb, at], writes=[pso])
            c.op("pe", lambda e: e.matmul(pss[:, 384:512], lhsT=kd[:], rhs=vn[:], start=True, stop=True), reads=[kd, vn], writes=[pss])
            yield
            oa = OACC[h]
            if d == 0:
                c.op("act", lambda e: e.copy(out=oa[:, t0:t0 + 128], in_=pso[:, 256:384]), reads=[pso], writes=[oa])
            else:
                c.op("dve", lambda e: e.tensor_tensor(out=oa[:, t0:t0 + 128], in0=pso[:, 256:384], in1=oa[:, t0:t0 + 128], op=ALU.add), reads=[pso, oa], writes=[oa])
            c.op("dve", lambda e: e.scalar_tensor_tensor(out=S_f[:], in0=S_f[:], scalar=EGL[:, tt, col:col + 1], in1=pss[:, 384:512], op0=ALU.mult, op1=ALU.add), reads=[S_f, EGL, pss], writes=[S_f])
            yield
            c.op("act", lambda e: e.copy(out=S_b[:], in_=S_f[:]), reads=[S_f], writes=[S_b])
            if last_seq is not None:
                so = r_so()
                c.op("pool", lambda e: e.tensor_copy(out=so[:], in_=S_f[:]), reads=[S_f], writes=[so])
                c.dma("sp", OSD[last_seq, l, d, h], so[:], reads=[so])

        def init_state(s, h, d):
            if s is None:
                c.dma("sp", Sf[h][:], I["sdn"][l, d, h], writes=[Sf[h]])
                c.op("act", lambda e: e.copy(out=Sb[h][:], in_=Sf[h][:]), reads=[Sf[h]], writes=[Sb[h]])
            else:
                c.op("pool", lambda e: e.memset(Sf[h][:], 0.0), writes=[Sf[h]])
                c.op("pool", lambda e: e.memset(Sb[h][:], 0.0), writes=[Sb[h]])

        def inst(tt, h, d, s, first, last):
            if first:
                init_state(s, h, d)
            yield from process(tt, h, d, s if (s is not None and last) else None)

        queue = []
        for d in range(2):
            seqs = [(s, [2 * s, 2 * s + 1]) for s in range(4)] + [(None, list(range(8, 16)))]
            for s, tiles in seqs:
                order = tiles if d == 0 else tiles[::-1]
                for i, tt in enumerate(order):
                    for h in range(8):
                        queue.append(inst(tt, h, d, s, i == 0, i == len(order) - 1))
        active = []
        qi = 0
        cyc = 0
        STAG = 5
        while qi < len(queue) or active:
            if qi < len(queue) and len(active) < NI and (qi >= NI or cyc >= qi * STAG):
                active.append(queue[qi])
                qi += 1
            alive = []
            for g in active:
                try:
                    next(g)
                    alive.append(g)
                except StopIteration:
                    pass
            active = alive
            cyc += 1
        c.end_stage()
        osq = c.sb([128, 512], BF16, "dosq")
        rr = c.sb([128, 512], F32, "drr")
        sdz = [c.sb([128, NT], BF16, "sdz") for _ in range(2)]
        orow = [c.sb([128, NT], BF16, "dorow") for _ in range(2)]
        for h in range(8):
            oa = OACC[h]
            z_ = sdz[h % 2]
            orw = orow[h % 2]
            c.dma("sp", z_[:], SDZ[h], writes=[z_])
            for (u0, nu) in TILES:
                c.op("act", lambda e: e.activation(out=osq[:, :nu], in_=oa[:, u0:u0 + nu], func=AF.Square), reads=[oa], writes=[osq])
                p2 = self.pb()
                c.op("pe", lambda e: e.matmul(p2[:, :nu], lhsT=onesB[:], rhs=osq[:, :nu], start=True, stop=True), reads=[onesB, osq], writes=[p2])
                c.op("act", lambda e: e.activation(out=rr[:, :nu], in_=p2[:, :nu], func=AF.Sqrt, scale=1.0 / 128, bias=eps[:, 0:1]), reads=[p2, eps], writes=[rr])
                c.op("dve", lambda e: e.reciprocal(out=rr[:, :nu], in_=rr[:, :nu]), reads=[rr], writes=[rr])
                c.op("dve", lambda e: e.scalar_tensor_tensor(out=rr[:, :nu], in0=oa[:, u0:u0 + nu], scalar=dnn[:, 0:1], in1=rr[:, :nu], op0=ALU.mult, op1=ALU.mult), reads=[oa, dnn, rr], writes=[rr])
                c.op("dve", lambda e: e.tensor_tensor(out=orw[:, u0:u0 + nu], in0=rr[:, :nu], in1=z_[:, u0:u0 + nu], op=ALU.mult), reads=[rr, z_], writes=[orw])
            c.dma("sp", MO[2][h], orw[:], reads=[orw])
        c.end_stage()


PI = math.pi


class KB5(KB4):
    def hy_tables(self, l, L):
        c = self.c
        I = self.I
        nch = L // 128
        TAB = self.scr("HTAB%d" % L, [2, 2 * nch + 1, 128, 1024], BF16)
        c.begin_stage()
        onesF, m0 = self.onesF, None
        m0 = c.sb([128, 2], F32, "m0")
        c.dma("sp", m0[:], I["m0"], writes=[m0])
        fT = c.sb([33, L], F32, "fT")
        c.dma("sp", fT[:], I["featsT%d" % L], writes=[fT])
        w1 = c.sb([33, 64], F32, "w1")
        w2 = c.sb([64, 64], F32, "w2")
        w3 = c.sb([64, 4096], F32, "w3")
        c.dma("sp", w1[:], I["hy_w1"][l], writes=[w1])
        c.dma("sp", w2[:], I["hy_w2"][l], writes=[w2])
        c.dma("sp", w3[:], I["hy_w3"][l], writes=[w3])
        vec = c.sb([64, 4], F32, "hvec")
        for i, nm in enumerate(("hy_b1", "hy_freq1", "hy_b2", "hy_freq2")):
            c.dma("sp", vec[:, i:i + 1], I[nm][l].rearrange("(p o) -> p o", o=1), writes=[vec])
        hid1 = c.sb([64, L], F32, "hid1")
        hid2 = c.sb([64, L], F32, "hid2")
        msk = c.sb([64, 512], F32, "msk")

        def sin_layer(dst, wT, K, src, bi, fi):
            for t0 in range(0, L, 512):
                n = min(512, L - t0)
                ps = self.pb()
                c.op("pe", lambda e: e.matmul(ps[:64, :n], lhsT=wT[:K, :], rhs=src[:K, t0:t0 + n], start=True, stop=True), reads=[wT, src], writes=[ps])
                d = dst
                c.op("dve", lambda e: e.tensor_scalar(out=d[:, t0:t0 + n], in0=ps[:64, :n], scalar1=vec[:, bi:bi + 1], scalar2=vec[:, fi:fi + 1], op0=ALU.add, op1=ALU.mult), reads=[ps, vec], writes=[d])
                for _ in range(2):
                    c.op("dve", lambda e: e.tensor_scalar(out=msk[:, :n], in0=d[:, t0:t0 + n], scalar1=PI, scalar2=-2 * PI, op0=ALU.is_gt, op1=ALU.mult), reads=[d], writes=[msk])
                    c.op("dve", lambda e: e.tensor_tensor(out=d[:, t0:t0 + n], in0=d[:, t0:t0 + n], in1=msk[:, :n], op=ALU.add), reads=[d, msk], writes=[d])
                    c.op("dve", lambda e: e.tensor_scalar(out=msk[:, :n], in0=d[:, t0:t0 + n], scalar1=-PI, scalar2=2 * PI, op0=ALU.is_lt, op1=ALU.mult), reads=[d], writes=[msk])
                    c.op("dve", lambda e: e.tensor_tensor(out=d[:, t0:t0 + n], in0=d[:, t0:t0 + n], in1=msk[:, :n], op=ALU.add), reads=[d, msk], writes=[d])
                c.op("act", lambda e: e.activation(out=d[:, t0:t0 + n], in_=d[:, t0:t0 + n], func=AF.Sin), reads=[d], writes=[d])

        sin_layer(hid1, w1, 33, fT, 0, 1)
        sin_layer(hid2, w2, 64, hid1, 2, 3)
        win = c.sb([128, nch, 1024], F32, "win")
        c.dma("sp", win[:], I["win%d" % L].rearrange("(t p) c -> p t c", p=128), writes=[win])
        CA = c.sb([128, nch, L], BF16, "CA")
        MB = c.sb([128, nch, L], BF16, "MB")
        c.dma("pool", CA[:], I["CA%d" % L].rearrange("(t p) k -> p t k", p=128), writes=[CA])
        c.dma("pool", MB[:], I["MB%d" % L].rearrange("(t p) k -> p t k", p=128), writes=[MB])
        hw = [c.sb([128, nch, 512], F32, "hw") for _ in range(2)]
        ab = c.sb([128, 512], F32, "hab")
        rinv = c.sb([128, 512], F32, "hrinv")
        hs = c.sb([128, nch, 512], BF16, "hs")
        hd = c.sb([128, nch, 512], BF16, "hd")
        to = [c.sb([128, 512], BF16, "hto") for _ in range(3)]
        tf = [c.sb([128, 512], F32, "htf") for _ in range(2)]
        nto = [0]
        for order in range(2):
            for half in range(2):
                for d in range(2):
                    col0 = d * 2048 + order * 1024 + half * 512
                    H = hw[d]
                    pn = self.pacc()
                    for tt in range(nch):
                        ps = self.pb()
                        c.op("pe", lambda e: e.matmul(ps[:, :512], lhsT=hid2[:, tt * 128:(tt + 1) * 128], rhs=w3[:, col0:col0 + 512], start=True, stop=True), reads=[hid2, w3], writes=[ps])
                        c.op("dve", lambda e: e.tensor_tensor(out=H[:, tt, :], in0=ps[:, :512], in1=win[:, tt, half * 512:(half + 1) * 512], op=ALU.mult), reads=[ps, win], writes=[H])
                        c.op("act", lambda e: e.activation(out=ab[:], in_=H[:, tt, :], func=AF.Abs), reads=[H], writes=[ab])
                        c.op("pe", lambda e: e.matmul(pn[:, :512], lhsT=onesF[:], rhs=ab[:], start=(tt == 0), stop=(tt == nch - 1)), reads=[onesF, ab], writes=[pn])
                    c.op("dve", lambda e: e.tensor_scalar(out=rinv[:], in0=pn[:, :512], scalar1=EPS, scalar2=None, op0=ALU.add), reads=[pn], writes=[rinv])
                    c.op("dve", lambda e: e.reciprocal(out=rinv[:], in_=rinv[:]), reads=[rinv], writes=[rinv])
                    for tt in range(nch):
                        c.op("dve", lambda e: e.tensor_tensor(out=H[:, tt, :], in0=H[:, tt, :], in1=rinv[:], op=ALU.mult), reads=[H, rinv], writes=[H])
                c.op("dve", lambda e: e.tensor_tensor(out=hs[:], in0=hw[0][:], in1=hw[1][:], op=ALU.add), reads=[hw[0], hw[1]], writes=[hs])
                c.op("pool", lambda e: e.tensor_tensor(out=hd[:], in0=hw[0][:], in1=hw[1][:], op=ALU.subtract), reads=[hw[0], hw[1]], writes=[hd])
                cs = slice(half * 512, (half + 1) * 512)
                for kc in range(nch):
                    pa = self.pb()
                    for tt in range(nch):
                        c.op("pe", lambda e: e.matmul(pa[:, :512], lhsT=CA[:, tt, kc * 128:(kc + 1) * 128], rhs=hs[:, tt, :], start=(tt == 0), stop=(tt == nch - 1)), reads=[CA, hs], writes=[pa])
                    pbd = self.pb()
                    for tt in range(nch):
                        c.op("pe", lambda e: e.matmul(pbd[:, :512], lhsT=MB[:, tt, kc * 128:(kc + 1) * 128], rhs=hd[:, tt, :], start=(tt == 0), stop=(tt == nch - 1)), reads=[MB, hd], writes=[pbd])
                    oa = to[nto[0] % 3]
                    nto[0] += 1
                    c.op("act", lambda e: e.copy(out=oa[:], in_=pa[:, :512]), reads=[pa], writes=[oa])
                    c.dma("sp", TAB[order, kc][:, cs], oa[:], reads=[oa])
                    ob = to[nto[0] % 3]
                    nto[0] += 1
                    if kc > 0:
                        c.op("act", lambda e: e.copy(out=ob[:], in_=pbd[:, :512]), reads=[pbd], writes=[ob])
                        c.dma("sp", TAB[order, nch + kc][:, cs], ob[:], reads=[ob])
                    else:
                        pbs = self.pb()
                        for tt in range(nch):
                            c.op("pe", lambda e: e.matmul(pbs[:, :512], lhsT=MB[:, tt, 0:128], rhs=hs[:, tt, :], start=(tt == 0), stop=(tt == nch - 1)), reads=[MB, hs], writes=[pbs])
                        c.op("dve", lambda e: e.tensor_scalar(out=ob[:], in0=pbd[:, :512], scalar1=m0[:, 0:1], scalar2=None, op0=ALU.mult), reads=[pbd, m0], writes=[ob])
                        c.dma("sp", TAB[order, nch][:, cs], ob[:], reads=[ob])
                        t1, t2 = tf[0], tf[1]
                        c.op("dve", lambda e: e.tensor_scalar(out=t1[:], in0=pa[:, :512], scalar1=m0[:, 0:1], scalar2=None, op0=ALU.mult), reads=[pa, m0], writes=[t1])
                        c.op("dve", lambda e: e.tensor_scalar(out=t2[:], in0=pbs[:, :512], scalar1=m0[:, 1:2], scalar2=None, op0=ALU.mult), reads=[pbs, m0], writes=[t2])
                        oc = to[nto[0] % 3]
                        nto[0] += 1
                        c.op("pool", lambda e: e.tensor_tensor(out=oc[:], in0=t1[:], in1=t2[:], op=ALU.add), reads=[t1, t2], writes=[oc])
                        c.dma("sp", TAB[order, 2 * nch][:, cs], oc[:], reads=[oc])
        c.end_stage()

    def hy_data(self, l, L, tokbase, B):
        c = self.c
        I = self.I
        nch = L // 128
        ncol = B * 1024
        TAB = self.S["HTAB%d" % L]
        HV, HX = self.S["HV"], self.S["HX"]
        MO = self.scr("MO", [3, 8, 128, NT], BF16)
        Z1 = self.scr("HZ1", [8, 128, NT], F32)
        idB = self.identB
        c.begin_stage()
        CA = c.sb([128, nch, L], BF16, "CA")
        MB = c.sb([128, nch, L], BF16, "MB")
        IA = c.sb([128, nch, L], BF16, "IA")
        IB = c.sb([128, nch, L], BF16, "IB")
        for t, nm in ((CA, "CA"), (MB, "MB"), (IA, "IA"), (IB, "IB")):
            c.dma("pool", t[:], I["%s%d" % (nm, L)].rearrange("(t p) k -> p t k", p=128), writes=[t])
        AH = c.sb([128, nch, 1024], BF16, "AH")
        HB1 = c.sb([128, nch, 1024], BF16, "HB1")
        HA2 = c.sb([128, 1024], BF16, "HA2")
        hb = c.sb([128, 16], F32, "hbias")
        self.load_cols(hb, hb[:], I["hy_bias"][l].rearrange("o (n p) -> (o n) p", p=128), 16)
        z = c.sb([128, nch, ncol], BF16, "z")
        Y = c.sb([128, 2 * nch, ncol], BF16, "Y")
        ntok = B * L
        rowv = [c.sb([128, ntok], BF16, "hrv") for _ in range(2)]
        rowx = [c.sb([128, ntok], F32, "hrx") for _ in range(2)]
        rowz = [c.sb([128, ntok], F32, "hrz") for _ in range(2)]
        rowo = [c.sb([128, ntok], BF16, "hro") for _ in range(2)]
        tm = [c.sb([128, 512], F32, "htm") for _ in range(4)]

        def to_tokmajor(src_rows_fn):
            for ch in range(8):
                row = src_rows_fn(ch)
                for b in range(B):
                    for tq in range(0, nch, 4):
                        ps = self.pb()
                        pb16 = ps[:].bitcast(BF16)
                        nq = min(4, nch - tq)
                        for j in range(nq):
                            tt = tq + j
                            t0 = b * L + tt * 128
                            c.op("pe", lambda e: e.transpose(out=pb16[:, j * 128:(j + 1) * 128], in_=row[:, t0:t0 + 128], identity=idB[:]), reads=[row, idB], writes=[ps])
                        for j in range(nq):
                            tt = tq + j
                            eng = "dve" if j % 2 == 0 else "act"
                            dst = z[:, tt, b * 1024 + ch * 128:b * 1024 + (ch + 1) * 128]
                            if eng == "dve":
                                c.op("dve", lambda e: e.tensor_copy(out=dst, in_=pb16[:, j * 128:(j + 1) * 128]), reads=[ps], writes=[z])
                            else:
                                c.op("act", lambda e: e.copy(out=dst, in_=pb16[:, j * 128:(j + 1) * 128]), reads=[ps], writes=[z])

        for order in range(2):
            c.dma("sp", AH[:], TAB[order, 0:nch].rearrange("k p c -> p k c"), writes=[AH])
            c.dma("sp", HB1[:], TAB[order, nch:2 * nch].rearrange("k p c -> p k c"), writes=[HB1])
            c.dma("sp", HA2[:], TAB[order, 2 * nch], writes=[HA2])
            if order == 0:
                def rows_v(ch):
                    r = rowv[ch % 2]
                    c.dma("sp", r[:], HV[ch][:, tokbase:tokbase + ntok], writes=[r])
                    return r
                to_tokmajor(rows_v)
            else:
                def rows_z(ch):
                    rz = rowz[ch % 2]
                    r = rowv[ch % 2]
                    c.dma("sp", rz[:], Z1[ch][:, tokbase:tokbase + ntok], writes=[rz])
                    c.op("act", lambda e: e.copy(out=r[:], in_=rz[:]), reads=[rz], writes=[r])
                    return r
                to_tokmajor(rows_z)
            for ct in range(ncol // 512):
                c0 = (ct * 512) % 1024
                for kc in range(nch):
                    pa = self.pb()
                    for tt in range(nch):
                        c.op("pe", lambda e: e.matmul(pa[:, :512], lhsT=CA[:, tt, kc * 128:(kc + 1) * 128], rhs=z[:, tt, ct * 512:(ct + 1) * 512], start=(tt == 0), stop=(tt == nch - 1)), reads=[CA, z], writes=[pa])
                    pq = self.pb()
                    for tt in range(nch):
                        c.op("pe", lambda e: e.matmul(pq[:, :512], lhsT=MB[:, tt, kc * 128:(kc + 1) * 128], rhs=z[:, tt, ct * 512:(ct + 1) * 512], start=(tt == 0), stop=(tt == nch - 1)), reads=[MB, z], writes=[pq])
                    t1, t2, t3, t4 = tm
                    ah = AH[:, kc, c0:c0 + 512]
                    h1 = HB1[:, kc, c0:c0 + 512]
                    a2 = HA2[:, c0:c0 + 512] if kc == 0 else ah
                    c.op("dve", lambda e: e.tensor_tensor(out=t1[:], in0=pa[:, :512], in1=ah, op=ALU.mult), reads=[pa, AH], writes=[t1])
                    c.op("dve", lambda e: e.tensor_tensor(out=t2[:], in0=pq[:, :512], in1=h1, op=ALU.mult), reads=[pq, HB1], writes=[t2])
                    c.op("pool", lambda e: e.tensor_tensor(out=Y[:, kc, ct * 512:(ct + 1) * 512], in0=t1[:], in1=t2[:], op=ALU.subtract), reads=[t1, t2], writes=[Y])
                    c.op("dve", lambda e: e.tensor_tensor(out=t3[:], in0=pa[:, :512], in1=h1, op=ALU.mult), reads=[pa, HB1], writes=[t3])
                    c.op("dve", lambda e: e.tensor_tensor(out=t4[:], in0=pq[:, :512], in1=a2, op=ALU.mult), reads=[pq, HA2, AH], writes=[t4])
                    c.op("pool", lambda e: e.tensor_tensor(out=Y[:, nch + kc, ct * 512:(ct + 1) * 512], in0=t3[:], in1=t4[:], op=ALU.add), reads=[t3, t4], writes=[Y])
            for ch in range(8):
                rx = rowx[ch % 2]
                c.dma("sp", rx[:], HX[order * 8 + ch][:, tokbase:tokbase + ntok], writes=[rx])
                if order == 0:
                    rb = rowv[ch % 2]
                    c.dma("sp", rb[:], HV[ch][:, tokbase:tokbase + ntok], writes=[rb])
                    ro = rowz[ch % 2]
                else:
                    rb = rowz[ch % 2]
                    c.dma("sp", rb[:], Z1[ch][:, tokbase:tokbase + ntok], writes=[rb])
                    ro = rowo[ch % 2]
                for b in range(B):
                    for r0 in range(0, L, 512):
                        n = min(512, L - r0)
                        ps = self.pacc()
                        for kk in range(2 * nch):
                            M = IA if kk < nch else IB
                            c.op("pe", lambda e: e.matmul(ps[:, :n], lhsT=Y[:, kk, b * 1024 + ch * 128:b * 1024 + (ch + 1) * 128], rhs=M[:, kk % nch, r0:r0 + n], start=(kk == 0), stop=(kk == 2 * nch - 1)),
                                 reads=[Y, M], writes=[ps])
                        g0 = b * L + r0
                        t1 = tm[0]
                        c.op("dve", lambda e: e.scalar_tensor_tensor(out=t1[:, :n], in0=rb[:, g0:g0 + n], scalar=hb[:, order * 8 + ch:order * 8 + ch + 1], in1=ps[:, :n], op0=ALU.mult, op1=ALU.add), reads=[rb, hb, ps], writes=[t1])
                        c.op("dve", lambda e: e.tensor_tensor(out=ro[:, g0:g0 + n], in0=t1[:, :n], in1=rx[:, g0:g0 + n], op=ALU.mult), reads=[t1, rx], writes=[ro])
                if order == 0:
                    c.dma("sp", Z1[ch][:, tokbase:tokbase + ntok], ro[:], reads=[ro])
                else:
                    c.dma("sp", MO[1][ch][:, tokbase:tokbase + ntok], ro[:], reads=[ro])
            c.barrier()
        c.end_stage()

    def stage_hy(self, l):
        self.hy_tables(l, 256)
        self.hy_data(l, 256, 0, 4)
        self.hy_tables(l, 1024)
        self.hy_data(l, 1024, 1024, 1)

_CACHE = {}


def build_full():
    kb = KB5()
    c = kb.c
    kb.stage_input()
    c.mark('input')
    kb.stage_mod()
    c.mark('mod')
    XA = kb.S["XT0"]
    XB = kb.scr("XT1", [16, 128, NT], F32)
    XC = kb.scr("XT2", [16, 128, NT], F32)
    cur = XA
    for l in range(2):
        kb.stage_norm(cur, l, 0)
        c.mark('norm1')
        kb.stage_proj(l)
        c.end_stage()
        c.mark('proj')
        kb.stage_ret(l)
        c.mark('ret')
        kb.stage_hy(l)
        c.mark('hy')
        kb.stage_dn(l)
        c.mark('dn')
        kb.stage_merge(l, cur, XB)
        c.mark('merge+wo')
        kb.stage_norm(XB, l, 1)
        c.mark('norm2')
        kb.stage_ffn_up(l)
        c.end_stage()
        c.mark('ffn_up')
        kb.stage_ffn_down(l, XB, XC)
        c.mark('ffn_down')
        cur, XB, XC = XC, cur, XB
    kb.stage_final(cur)
    c.mark('final')
    c.finish()
    return kb


def kernel(**inputs):
    z = {k: np.ascontiguousarray(np.asarray(v)) for k, v in inputs.items()}
    if "kb" not in _CACHE:
        _CACHE["kb"] = build_full()
    kb = _CACHE["kb"]
    in_maps = []
    for core in range(8):
        m = dict(kb.consts)
        for k in WSPEC:
            m[k] = z[k]
        xp = z["x_prompt"][4 * core:4 * core + 4].reshape(1024, 2048)
        xs = z["x_sample"][core // 2]
        m["xin"] = np.ascontiguousarray(np.concatenate([xp, xs], 0))
        m["sret"] = np.ascontiguousarray(z["state_ret"][core // 2])
        m["sdn"] = np.ascontiguousarray(z["state_dn"][core // 2])
        m["cvec"] = np.ascontiguousarray(np.stack([z["c_ctx"], z["c"][core // 2]]))
        in_maps.append(m)
    res = run_bass_kernel_spmd(kb.nc, in_maps, core_ids=list(range(8))).results
    y_prompt = np.concatenate([np.asarray(res[c]["y"])[:1024].reshape(4, 256, 2048) for c in range(8)], 0).astype(np.float32)
    y_sample = np.stack([np.asarray(res[2 * b]["y"])[1024:] for b in range(4)], 0).astype(np.float32)
    new_ret = np.concatenate([np.asarray(res[c]["osret"]) for c in range(8)], 0).astype(np.float32)
    new_dn = np.concatenate([np.asarray(res[c]["osdn"]) for c in range(8)], 0).astype(np.float32)
    return (y_prompt, y_sample, new_ret, new_dn)
```

```python
import numpy as np
from contextlib import ExitStack
import concourse.bass as bass
import concourse.mybir as mybir
from concourse.bass_utils import run_bass_kernel_spmd

F32 = mybir.dt.float32
BF16 = mybir.dt.bfloat16
I32 = mybir.dt.int32
AF = mybir.ActivationFunctionType
ALU = mybir.AluOpType
AX = mybir.AxisListType


class Buf:
    __slots__ = ("t", "name", "w", "r", "psum")

    def __init__(self, t, name, psum=False):
        self.t = t
        self.name = name
        self.psum = psum
        self.w = None
        self.r = {}

    def __getitem__(self, idx):
        return self.t[idx]


class Ctx:
    SEM_LIMIT = 30000
    NDMA = 24

    def __init__(self, nc):
        self.nc = nc
        self.es = ExitStack()
        self.eng = {"pe": nc.tensor, "act": nc.scalar, "dve": nc.vector, "pool": nc.gpsimd, "sp": nc.sync}
        self.cur = {}
        self.waited = {k: {} for k in self.eng}
        self.nsem = 0
        for k in self.eng:
            self._new_sem(k)
        self.dpool = {}
        for q in ("sp", "pool", "act"):
            self.dpool[q] = [[self._alloc_sem("d%s%d" % (q, i)), 0] for i in range(self.NDMA)]
        self.dnext = {q: 0 for q in self.dpool}
        self.stage_es = None
        self.uid = 0
        self.tot = {}
        self.marks = []

    def mark(self, name):
        self.marks.append((name, dict(self.tot)))

    def _alloc_sem(self, name):
        self.nsem += 1
        s = self.es.enter_context(self.nc.semaphore("%s_%d" % (name, self.nsem)))
        if not hasattr(self, "allsems"):
            self.allsems = []
        self.allsems.append(s)
        return s

    def _new_sem(self, k):
        self.cur[k] = [self._alloc_sem("e" + k), 0]

    def begin_stage(self):
        if not hasattr(self, "stack"):
            self.stack = []
        self.stack.append(ExitStack())
        self.stage_es = self.stack[-1]

    def end_stage(self):
        self.barrier()
        self.stack.pop().close()
        self.stage_es = self.stack[-1] if self.stack else None

    def sb(self, shape, dt, name="t", persist=False):
        self.uid += 1
        nm = "%s_%d" % (name, self.uid)
        es = self.es if persist else self.stage_es
        t = es.enter_context(self.nc.sbuf_tensor(nm, list(shape), dt))
        return Buf(t, nm)

    def ps(self, shape, dt, name="p"):
        self.uid += 1
        nm = "%s_%d" % (name, self.uid)
        t = self.es.enter_context(self.nc.psum_tensor(nm, list(shape), dt))
        return Buf(t, nm, psum=True)

    def _wait(self, k, tok):
        if tok is None:
            return
        sem, val = tok
        if k == "pe" and sem is self.cur["pe"][0]:
            return
        w = self.waited[k]
        key = id(sem)
        if w.get(key, (None, 0))[1] >= val:
            return
        w[key] = (sem, val)
        self.eng[k].wait_ge(sem, val)

    def _deps(self, k, reads, writes):
        for b in reads:
            self._wait(k, b.w)
            if b.psum:
                for tok in list(b.r.values()):
                    self._wait(k, tok)
        for b in writes:
            self._wait(k, b.w)
            for tok in list(b.r.values()):
                self._wait(k, tok)

    def _commit(self, tok, reads, writes):
        for b in reads:
            b.r[id(tok[0])] = tok
        for b in writes:
            b.w = tok
            b.r = {}

    def op(self, k, fn, reads=(), writes=()):
        self._deps(k, reads, writes)
        c = self.cur[k]
        if c[1] >= self.SEM_LIMIT:
            self._new_sem(k)
            c = self.cur[k]
        c[1] += 1
        self.tot[k] = self.tot.get(k, 0) + 1
        fn(self.eng[k]).then_inc(c[0], 1)
        tok = (c[0], c[1])
        self._commit(tok, reads, writes)
        return tok

    def dma(self, q, out, in_, reads=(), writes=(), **kw):
        pool = self.dpool[q]
        i = self.dnext[q]
        self.dnext[q] = (i + 1) % len(pool)
        slot = pool[i]
        if slot[1] > 0:
            self._wait(q, (slot[0], slot[1]))
        if slot[1] >= self.SEM_LIMIT:
            slot[0] = self._alloc_sem("d" + q)
            slot[1] = 0
        self._deps(q, reads, writes)
        slot[1] += 16
        self.eng[q].dma_start(out=out, in_=in_, **kw).then_inc(slot[0], 16)
        tok = (slot[0], slot[1])
        self._commit(tok, reads, writes)
        return tok

    def barrier(self):
        toks = [(c[0], c[1]) for c in self.cur.values() if c[1] > 0]
        for q in self.dpool:
            for slot in self.dpool[q]:
                if slot[1] > 0:
                    toks.append((slot[0], slot[1]))
        for k in self.eng:
            for tok in toks:
                self._wait(k, tok)

    def finish(self):
        self.barrier()
        self.es.close()

import math
import numpy as np


def make_consts():
    f = np.float32
    C = {}
    i = np.arange(128)
    C["ident"] = np.eye(128, dtype=f)
    C["ones"] = np.ones((128, 128), f)
    C["UF"] = (i[:, None] <= i[None, :]).astype(f)
    C["UB"] = (i[:, None] >= i[None, :]).astype(f)
    C["SU"] = (i[:, None] < i[None, :]).astype(f)
    C["SL"] = (i[:, None] > i[None, :]).astype(f)
    L = 1024
    pos = np.arange(L, dtype=np.float64)
    pos_r = np.floor(pos / 64)
    pos_c = pos % 64
    inv = 10000.0 ** (-np.arange(16, dtype=np.float64) / 16)
    cosT = np.zeros((128, L))
    sinT = np.zeros((128, L))
    for p in range(128):
        q = p % 64
        half = q // 32
        x2 = (q % 32) // 16
        fi = q % 16
        ang = (pos_r if half == 0 else pos_c) * inv[fi]
        cosT[p] = np.cos(ang)
        sinT[p] = np.sin(ang) * (1.0 if x2 else -1.0)
    C["ropeC"] = cosT.astype(f)
    C["ropeS"] = sinT.astype(f)
    for nm, LL in (("S", 1024), ("P", 256)):
        u = np.arange(2 * LL - 128)[None, :]
        p = np.arange(128)[:, None]
        d = u - p - (LL - 128)
        C["dpos" + nm] = np.maximum(d, 0).astype(f)
        C["dneg" + nm] = np.maximum(-d, 0).astype(f)
        C["dz" + nm] = (d == 0).astype(f)
    j = np.arange(128)[:, None] + 128 * np.arange(2)[None, :]
    C["expfb"] = np.concatenate([255 - j, j], axis=1).astype(f)
    ii = np.arange(1024)[None, :].repeat(128, 0)
    C["idx1"] = (ii + 1).astype(f)
    C["idx2"] = (1024 - ii).astype(f)
    for LL in (256, 1024):
        t = np.linspace(0.0, 1.0, LL, dtype=np.float32)[:, None].astype(np.float64)
        wpos = 2.0 * math.pi * np.arange(LL, dtype=np.float64)[:, None] / LL
        fr = np.linspace(1e-4, 15, 16, dtype=np.float32)[None, :].astype(np.float64)
        feats = np.concatenate([t, np.cos(fr * wpos), -np.sin(fr * wpos)], axis=-1)
        C["featsT%d" % LL] = np.ascontiguousarray(feats.T).astype(f)
        deltas = np.abs(np.linspace(math.log(1e-2) / 0.3, math.log(1e-2) / 1.5, 1024, dtype=np.float32)).astype(np.float64)
        C["win%d" % LL] = np.exp(-t * deltas[None, :]).astype(f)
        N = 2 * LL
        tt = np.arange(LL, dtype=np.float64)[:, None]
        kk = np.arange(LL, dtype=np.float64)[None, :]
        ang = 2.0 * math.pi * tt * kk / N
        CA = np.cos(ang)
        MB = -np.sin(ang)
        MB[:, 0] = (-1.0) ** np.arange(LL)
        IA = (2.0 / N) * np.cos(ang.T)
        IA[0, :] = 1.0 / N
        IB = -(2.0 / N) * np.sin(ang.T)
        IB[0, :] = ((-1.0) ** np.arange(LL)) / N
        C["CA%d" % LL] = CA.astype(f)
        C["MB%d" % LL] = MB.astype(f)
        C["IA%d" % LL] = IA.astype(f)
        C["IB%d" % LL] = IB.astype(f)
    ii, jj = np.meshgrid(np.arange(128), np.arange(128), indexing="ij")
    ML = np.stack([(((ii >> (s + 1)) == (jj >> (s + 1))) & ((ii >> s) != (jj >> s)) & (ii > jj)).astype(f) for s in range(7)])
    MU = np.ascontiguousarray(ML.transpose(0, 2, 1))
    C["MLU"] = np.ascontiguousarray(np.concatenate([ML, MU], axis=2).transpose(1, 0, 2))
    C["MUL"] = np.ascontiguousarray(np.concatenate([MU, ML], axis=2).transpose(1, 0, 2))
    m0 = np.ones((128, 2), f)
    m0[0, 0] = 0.0
    m0[:, 1] = 1.0 - m0[:, 0]
    C["m0"] = m0
    C["eps"] = np.full((128, 1), 1e-6, f)
    return C


import math

NT = 2048
TILES = [(0, 512), (512, 512), (1024, 512), (1536, 512)]
EPS = 1e-6

WSPEC = dict(
    w_ada=[2, 2048, 12288], b_ada=[2, 12288], norm1=[2, 2048], w_in=[2, 2048, 16416], ret_decay=[2, 2, 8],
    hy_short=[2, 3, 3072], hy_w1=[2, 33, 64], hy_b1=[2, 64], hy_freq1=[2, 64], hy_w2=[2, 64, 64], hy_b2=[2, 64],
    hy_freq2=[2, 64], hy_w3=[2, 64, 4096], hy_bias=[2, 2, 1024], dn_conv=[2, 3, 3072], dn_a_log=[2, 2, 8],
    dn_dt_bias=[2, 2, 8], dn_norm=[2, 128], p_ret=[2, 1024, 2048], p_hy=[2, 1024, 2048], p_dn=[2, 1024, 2048],
    w_o=[2, 2048, 2048], norm2=[2, 2048], w_up=[2, 2048, 11008], ffn_conv=[2, 3, 11008], w_down=[2, 5504, 2048],
    norm_f=[2048])


class KB:
    def __init__(self, stop_after=None, dbg=(), wspec=None, ext_in=()):
        self.stop_after = stop_after
        self.dbg = set(dbg)
        nc = self.nc = bass.Bass("TRN2", target_bir_lowering=False)
        self.c = Ctx(nc)
        self.I = {}
        self.consts = make_consts()
        for k, v in self.consts.items():
            self.I[k] = nc.dram_tensor(k, list(v.shape), F32, kind="ExternalInput").ap()
        self.ext_in = set(ext_in)
        for k, s in (wspec or WSPEC).items():
            self.I[k] = nc.dram_tensor(k, s, F32, kind="ExternalInput").ap()
        self.I["xin"] = nc.dram_tensor("xin", [NT, 2048], F32, kind="ExternalInput").ap()
        self.I["sret"] = nc.dram_tensor("sret", [2, 2, 8, 64, 128], F32, kind="ExternalInput").ap()
        self.I["sdn"] = nc.dram_tensor("sdn", [2, 2, 8, 128, 128], F32, kind="ExternalInput").ap()
        self.I["cvec"] = nc.dram_tensor("cvec", [2, 2048], F32, kind="ExternalInput").ap()
        self.O = {}
        self.O["y"] = nc.dram_tensor("y", [NT, 2048], F32, kind="ExternalOutput").ap()
        self.O["osret"] = nc.dram_tensor("osret", [4, 2, 2, 8, 64, 128], F32, kind="ExternalOutput").ap()
        self.O["osdn"] = nc.dram_tensor("osdn", [4, 2, 2, 8, 128, 128], F32, kind="ExternalOutput").ap()
        self.S = {}
        c = self.c
        self.PB = [c.ps([128, 512], F32, "pb") for _ in range(8)]
        self.pbi = 0
        self.identF = c.sb([128, 128], F32, "identF", persist=True)
        self.identB = c.sb([128, 128], BF16, "identB", persist=True)
        self.onesB = c.sb([128, 128], BF16, "onesB", persist=True)
        self.onesF = c.sb([128, 128], F32, "onesF", persist=True)
        self.epsT = c.sb([128, 1], F32, "epsT", persist=True)
        self.rows = c.sb([128, 128], F32, "rows", persist=True)
        c.dma("sp", self.identF[:], self.I["ident"], writes=[self.identF])
        c.dma("pool", self.identB[:], self.I["ident"], writes=[self.identB])
        c.dma("pool", self.onesB[:], self.I["ones"], writes=[self.onesB])
        c.dma("sp", self.onesF[:], self.I["ones"], writes=[self.onesF])
        c.dma("sp", self.epsT[:], self.I["eps"], writes=[self.epsT])
        self.HT = None
        self.modT = c.sb([128, 2, 96, 2], F32, "modT", persist=True)
        self.sca = c.sb([128, 2, 2, 16, 2], F32, "sca", persist=True)
        self.gbraw = c.sb([128, 16, 32], F32, "gbraw", persist=True)

    def scr(self, name, shape, dt):
        if name not in self.S:
            kind = "ExternalOutput" if name in self.dbg else ("ExternalInput" if name in self.ext_in else "Internal")
            self.S[name] = self.nc.dram_tensor(name, list(shape), dt, kind=kind).ap()
        return self.S[name]

    def pb(self):
        self.pbi = (self.pbi + 1) % 6
        return self.PB[self.pbi]

    def pacc(self):
        self.pai = (getattr(self, "pai", 0) + 1) % 2
        return self.PB[6 + self.pai]

    def load_cols(self, dstbuf, dst_ap, src_rows, n):
        c = self.c
        rows = self.rows
        c.dma("sp", rows[:n, :], src_rows, writes=[rows])
        ps = self.pb()
        idf = self.identF
        c.op("pe", lambda e: e.transpose(out=ps[:, :n], in_=rows[:n, :], identity=idf[:n, :n]), reads=[rows, idf], writes=[ps])
        c.op("dve", lambda e: e.tensor_copy(out=dst_ap, in_=ps[:, :n]), reads=[ps], writes=[dstbuf])

    def stage_input(self):
        c = self.c
        XT = self.scr("XT0", [16, 128, NT], F32)
        c.begin_stage()
        xs = [c.sb([128, 2048], F32, "xs") for _ in range(2)]
        xo = [c.sb([128, 16, 128], F32, "xo") for _ in range(2)]
        idf = self.identF
        for tt in range(16):
            a = xs[tt % 2]
            o = xo[tt % 2]
            c.dma("sp", a[:], self.I["xin"][tt * 128:(tt + 1) * 128, :], writes=[a])
            for g in range(4):
                ps = self.pb()
                for j in range(4):
                    ch = g * 4 + j
                    c.op("pe", lambda e: e.transpose(out=ps[:, j * 128:(j + 1) * 128], in_=a[:, ch * 128:(ch + 1) * 128], identity=idf[:]),
                         reads=[a, idf], writes=[ps])
                eng = "dve" if g % 2 == 0 else "act"
                if eng == "dve":
                    c.op("dve", lambda e: e.tensor_copy(out=o[:, g * 4:(g + 1) * 4, :], in_=ps[:].rearrange("p (j t) -> p j t", j=4)), reads=[ps], writes=[o])
                else:
                    c.op("act", lambda e: e.copy(out=o[:, g * 4:(g + 1) * 4, :], in_=ps[:].rearrange("p (j t) -> p j t", j=4)), reads=[ps], writes=[o])
            c.dma("sp", XT[:, :, tt * 128:(tt + 1) * 128].rearrange("c p t -> p c t"), o[:], reads=[o])
        c.end_stage()

    def stage_mod(self):
        c = self.c
        I = self.I
        c.begin_stage()
        scT = c.sb([128, 2, 16], F32, "scT")
        self.load_cols(scT, scT[:].rearrange("p b k -> p (b k)"), I["cvec"].rearrange("b (k p) -> (b k) p", p=128), 32)
        c.op("act", lambda e: e.activation(out=scT[:], in_=scT[:], func=AF.Silu), reads=[scT], writes=[scT])
        bada = c.sb([128, 96], F32, "bada")
        nw = c.sb([128, 16], F32, "nw")
        wa = [c.sb([128, 16, 512], F32, "wa") for _ in range(3)]
        mrow = [c.sb([2, 512], F32, "mrow") for _ in range(2)]
        idf = self.identF
        for l in range(2):
            self.load_cols(bada, bada[:], I["b_ada"][l].rearrange("(n p) -> n p", p=128), 96)
            ps = self.pacc()
            for blk in range(24):
                w = wa[blk % 3]
                c.dma("sp" if blk % 2 == 0 else "act", w[:], I["w_ada"][l][:, blk * 512:(blk + 1) * 512].rearrange("(k p) n -> p k n", p=128), writes=[w])
                pr = self.pb()
                for k in range(16):
                    c.op("pe", lambda e: e.matmul(pr[:2, :512], lhsT=scT[:, :, k], rhs=w[:, k, :], start=(k == 0), stop=(k == 15)), reads=[w, scT], writes=[pr])
                mr = mrow[blk % 2]
                c.op("act", lambda e: e.copy(out=mr[:], in_=pr[:2, :512]), reads=[pr], writes=[mr])
                for j in range(4):
                    ch = blk * 4 + j
                    c.op("pe", lambda e: e.transpose(out=ps[:, ch * 2:ch * 2 + 2], in_=mr[:2, j * 128:(j + 1) * 128], identity=idf[:2, :2]), reads=[mr, idf], writes=[ps])
            mt = self.modT
            for b in range(2):
                c.op("dve", lambda e: e.tensor_tensor(out=mt[:, l, :, b], in0=ps[:, 0:192].rearrange("p (c b) -> p c b", b=2)[:, :, b], in1=bada[:], op=ALU.add),
                     reads=[ps, bada], writes=[mt])
            for which, (nm, scb) in enumerate((("norm1", 16), ("norm2", 64))):
                self.load_cols(nw, nw[:], I[nm][l].rearrange("(n p) -> n p", p=128), 16)
                sc = self.sca
                for b in range(2):
                    c.op("dve", lambda e: e.scalar_tensor_tensor(out=sc[:, l, which, :, b], in0=mt[:, l, scb:scb + 16, b], scalar=1.0, in1=nw[:], op0=ALU.add, op1=ALU.mult),
                         reads=[mt, nw], writes=[sc])
        c.end_stage()

    def stage_norm(self, XT, l, which):
        c = self.c
        c.begin_stage()
        self.HT = c.sb([128, 16, NT], BF16, "HT")
        c.begin_stage()
        shb = 0 if which == 0 else 48
        xs = [c.sb([128, 16, 256], F32, "nx") for _ in range(2)]
        sq = c.sb([128, 16, 256], BF16, "nsq")
        tmps = [c.sb([128, 256], F32, "ntmp") for _ in range(6)]
        r = c.sb([128, 256], F32, "nr")
        HT, sc, mt, ones, eps = self.HT, self.sca, self.modT, self.onesB, self.epsT
        for ti in range(8):
            t0 = ti * 256
            b = 0 if t0 < 1024 else 1
            x = xs[ti % 2]
            c.dma("sp", x[:], XT[:, :, t0:t0 + 256].rearrange("c p t -> p c t"), writes=[x])
            c.op("act", lambda e: e.activation(out=sq[:], in_=x[:], func=AF.Square), reads=[x], writes=[sq])
            ps = self.pb()
            for ch in range(16):
                c.op("pe", lambda e: e.matmul(ps[:, :256], lhsT=ones[:], rhs=sq[:, ch, :], start=(ch == 0), stop=(ch == 15)), reads=[ones, sq], writes=[ps])
            c.op("act", lambda e: e.activation(out=r[:], in_=ps[:, :256], func=AF.Sqrt, scale=1.0 / 2048, bias=eps[:, 0:1]), reads=[ps, eps], writes=[r])
            c.op("dve", lambda e: e.reciprocal(out=r[:], in_=r[:]), reads=[r], writes=[r])
            for ch in range(16):
                tmp = tmps[ch % 6]
                c.op("dve", lambda e: e.tensor_tensor(out=tmp[:], in0=x[:, ch, :], in1=r[:], op=ALU.mult), reads=[x, r], writes=[tmp])
                c.op("act", lambda e: e.activation(out=HT[:, ch, t0:t0 + 256], in_=tmp[:], func=AF.Identity,
                                                   scale=sc[:, l, which, ch, b:b + 1], bias=mt[:, l, shb + ch, b:b + 1]), reads=[tmp, sc, mt], writes=[HT])
        c.end_stage()


class KB2(KB):
    def wbufs(self, KC, n=3, width=256):
        return [self.c.sb([128, KC, width], BF16, "wb") for _ in range(n)]

    def lin_fm(self, W, blocks, IN, KC, epi, tiles=TILES, wb=None, prep=None):
        c = self.c
        if wb is None:
            wb = self.wbufs(KC)
        for bi, (c0, ncol) in enumerate(blocks):
            self.wbi = getattr(self, "wbi", 0) + 1
            w = wb[self.wbi % len(wb)]
            c.dma("pool", w[:, :KC, :ncol], W[:, c0:c0 + ncol].rearrange("(k p) n -> p k n", p=128), writes=[w])
            aux = prep(w, c0, ncol) if prep else None
            for off in range(0, ncol, 128):
                n = min(128, ncol - off)
                for ti, (t0, nt) in enumerate(tiles):
                    ps = self.pb()
                    for k in range(KC):
                        c.op("pe", lambda e: e.matmul(ps[:n, :nt], lhsT=w[:, k, off:off + n], rhs=IN[:, k, t0:t0 + nt], start=(k == 0), stop=(k == KC - 1)),
                             reads=[w, IN], writes=[ps])
                    epi(c0 + off, n, ti, t0, nt, ps, (aux, off))

    def lin_tm(self, W, blocks, IN, KC, epi, ttiles, wb=None):
        c = self.c
        if wb is None:
            wb = self.wbufs(KC)
        for bi, (c0, ncol) in enumerate(blocks):
            self.wbi = getattr(self, "wbi", 0) + 1
            w = wb[self.wbi % len(wb)]
            c.dma("pool", w[:, :KC, :ncol], W[:, c0:c0 + ncol].rearrange("(k p) n -> p k n", p=128), writes=[w])
            for tt in ttiles:
                ps = self.pb()
                for k in range(KC):
                    c.op("pe", lambda e: e.matmul(ps[:, :ncol], lhsT=IN[:, k, tt * 128:(tt + 1) * 128], rhs=w[:, k, :ncol], start=(k == 0), stop=(k == KC - 1)),
                         reads=[w, IN], writes=[ps])
                epi(c0, ncol, tt, ps)

    @staticmethod
    def blocks(a, b, step=256):
        return [(x, min(step, b - x)) for x in range(a, b, step)]

    def conv_row(self, src, dst, wt, ci, ntap_chunks):
        c = self.c
        w0 = wt[:, 0 * ntap_chunks + ci:0 * ntap_chunks + ci + 1]
        w1 = wt[:, 1 * ntap_chunks + ci:1 * ntap_chunks + ci + 1]
        w2 = wt[:, 2 * ntap_chunks + ci:2 * ntap_chunks + ci + 1]
        c.op("act", lambda e: e.activation(out=dst[:], in_=src[:], func=AF.Copy, scale=w1), reads=[src, wt], writes=[dst])
        sp = src[:, 0:1024].rearrange("p (s t) -> p s t", t=256)
        dp = dst[:, 0:1024].rearrange("p (s t) -> p s t", t=256)
        c.op("dve", lambda e: e.scalar_tensor_tensor(out=dp[:, :, 1:256], in0=sp[:, :, 0:255], scalar=w0, in1=dp[:, :, 1:256], op0=ALU.mult, op1=ALU.add), reads=[src, wt, dst], writes=[dst])
        c.op("dve", lambda e: e.scalar_tensor_tensor(out=dp[:, :, 0:255], in0=sp[:, :, 1:256], scalar=w2, in1=dp[:, :, 0:255], op0=ALU.mult, op1=ALU.add), reads=[src, wt, dst], writes=[dst])
        c.op("dve", lambda e: e.scalar_tensor_tensor(out=dst[:, 1025:2048], in0=src[:, 1024:2047], scalar=w0, in1=dst[:, 1025:2048], op0=ALU.mult, op1=ALU.add), reads=[src, wt, dst], writes=[dst])
        c.op("dve", lambda e: e.scalar_tensor_tensor(out=dst[:, 1024:2047], in0=src[:, 1025:2048], scalar=w2, in1=dst[:, 1024:2047], op0=ALU.mult, op1=ALU.add), reads=[src, wt, dst], writes=[dst])

    def stage_proj(self, l):
        c = self.c
        I = self.I
        W = I["w_in"][l]
        HT = self.HT
        c.begin_stage()
        wb = self.wbufs(16, 3, 512)
        QK = self.scr("QK", [8, 128, NT], BF16)
        VTM = self.scr("VTM", [16, 128, 1024], BF16)
        KTM = self.scr("KTM", [8, 128, 512], BF16)
        SRG = self.scr("SRG", [8, 128, NT], BF16)
        HV = self.scr("HV", [8, 128, NT], BF16)
        HX = self.scr("HX", [16, 128, NT], F32)
        DQK = self.scr("DQK", [16, 128, NT], BF16)
        DVT = self.scr("DVT", [8, 128, NT], BF16)
        SDZ = self.scr("SDZ", [8, 128, NT], BF16)
        GATE = self.scr("GATE", [48, 128, NT], BF16)
        rowb = [c.sb([128, NT], BF16, "rowb") for _ in range(2)]
        rowf = [c.sb([128, NT], F32, "rowf") for _ in range(2)]
        rowg = [c.sb([128, NT], F32, "rowg") for _ in range(2)]
        cnt = [0]

        ropeC = c.sb([128, 1024], F32, "ropeC")
        ropeS = c.sb([128, 1024], F32, "ropeS")
        c.dma("sp", ropeC[:], I["ropeC"], writes=[ropeC])
        c.dma("sp", ropeS[:], I["ropeS"], writes=[ropeS])
        wperm = [c.sb([128, 16, 256], BF16, "wperm") for _ in range(2)]
        t1 = c.sb([128, 512], F32, "t1")
        t2 = c.sb([128, 512], F32, "t2")
        pc = [0]

        def prep_qk(w, c0, ncol):
            wp = wperm[pc[0] % 2]
            pc[0] += 1
            src = w[:, :, 0:256].rearrange("p k (a two s) -> p k a two s", two=2, s=16)
            dst = wp[:].rearrange("p k (a two s) -> p k a two s", two=2, s=16)
            c.op("dve", lambda e: e.tensor_copy(out=dst[:, :, :, 0, :], in_=src[:, :, :, 1, :]), reads=[w], writes=[wp])
            c.op("act", lambda e: e.copy(out=dst[:, :, :, 1, :], in_=src[:, :, :, 0, :]), reads=[w], writes=[wp])
            return wp

        def epi_qk(col0, n, ti, t0, nt, ps, auxoff):
            wp, off = auxoff
            ci = col0 // 128
            row = rowb[ci % 2]
            scale = 1.0 if ci < 4 else 0.125
            if ti < 2:
                c.op("act", lambda e: e.activation(out=row[:, t0:t0 + nt], in_=ps[:, :nt], func=AF.Copy, scale=scale), reads=[ps], writes=[row])
            else:
                ps2 = self.pb()
                for k in range(16):
                    c.op("pe", lambda e: e.matmul(ps2[:, :nt], lhsT=wp[:, k, off:off + 128], rhs=HT[:, k, t0:t0 + nt], start=(k == 0), stop=(k == 15)), reads=[wp, HT], writes=[ps2])
                s0 = t0 - 1024
                c.op("dve", lambda e: e.tensor_tensor(out=t1[:, :nt], in0=ps[:, :nt], in1=ropeC[:, s0:s0 + nt], op=ALU.mult), reads=[ps, ropeC], writes=[t1])
                c.op("dve", lambda e: e.tensor_tensor(out=t2[:, :nt], in0=ps2[:, :nt], in1=ropeS[:, s0:s0 + nt], op=ALU.mult), reads=[ps2, ropeS], writes=[t2])
                c.op("dve", lambda e: e.tensor_tensor(out=t1[:, :nt], in0=t1[:, :nt], in1=t2[:, :nt], op=ALU.add), reads=[t1, t2], writes=[t1])
                c.op("act", lambda e: e.activation(out=row[:, t0:t0 + nt], in_=t1[:, :nt], func=AF.Copy, scale=scale), reads=[t1], writes=[row])
            if ti == 3:
                c.dma("sp", QK[ci], row[:], reads=[row])

        self.lin_fm(W, self.blocks(0, 1024), HT, 16, epi_qk, wb=wb, prep=prep_qk)

        stv = [c.sb([128, 512], BF16, "stv") for _ in range(4)]

        def epi_v(col0, ncol, tt, ps):
            s = stv[cnt[0] % 4]
            cnt[0] += 1
            if cnt[0] % 2:
                c.op("dve", lambda e: e.tensor_copy(out=s[:, :ncol], in_=ps[:, :ncol]), reads=[ps], writes=[s])
            else:
                c.op("act", lambda e: e.copy(out=s[:, :ncol], in_=ps[:, :ncol]), reads=[ps], writes=[s])
            c.dma("sp", VTM[tt][:, col0 - 1024:col0 - 1024 + ncol], s[:, :ncol], reads=[s])

        def epi_ktm(col0, ncol, tt, ps):
            s = stv[cnt[0] % 4]
            cnt[0] += 1
            c.op("act", lambda e: e.activation(out=s[:, :ncol], in_=ps[:, :ncol], func=AF.Copy, scale=0.125), reads=[ps], writes=[s])
            c.dma("sp", KTM[tt][:, col0 - 512:col0 - 512 + ncol], s[:, :ncol], reads=[s])

        self.lin_tm(W, self.blocks(1024, 2048, 512), HT, 16, epi_v, range(16), wb=wb)
        self.lin_tm(W, self.blocks(512, 1024, 512), HT, 16, epi_ktm, range(8), wb=wb)

        def mk_epi_act(base, dst, func):
            def epi(col0, n, ti, t0, nt, ps, aux):
                ci = (col0 - base) // 128
                row = rowb[ci % 2]
                c.op("act", lambda e: e.activation(out=row[:, t0:t0 + nt], in_=ps[:, :nt], func=func), reads=[ps], writes=[row])
                if ti == 3:
                    c.dma("sp", dst[ci], row[:], reads=[row])
            return epi

        self.lin_fm(W, self.blocks(2048, 3072, 512), HT, 16, mk_epi_act(2048, SRG, AF.Silu), wb=wb)
        self.lin_fm(W, self.blocks(9216, 10240, 512), HT, 16, mk_epi_act(9216, SDZ, AF.Silu), wb=wb)
        self.lin_fm(W, self.blocks(10272, 16416, 512), HT, 16, mk_epi_act(10272, GATE, AF.Sigmoid), wb=wb)

        hyw = c.sb([128, 72], F32, "hyw")
        self.load_cols(hyw, hyw[:], I["hy_short"][l].rearrange("k (n p) -> (k n) p", p=128), 72)
        dnw = c.sb([128, 72], F32, "dnw")
        self.load_cols(dnw, dnw[:], I["dn_conv"][l].rearrange("k (n p) -> (k n) p", p=128), 72)

        def epi_hy(col0, n, ti, t0, nt, ps, aux):
            ci = (col0 - 3072) // 128
            raw = rowf[ci % 2]
            c.op("act", lambda e: e.copy(out=raw[:, t0:t0 + nt], in_=ps[:, :nt]), reads=[ps], writes=[raw])
            if ti == 3:
                cv = rowg[ci % 2]
                self.conv_row(raw, cv, hyw, ci, 24)
                if ci < 8:
                    row = rowb[ci % 2]
                    c.op("act", lambda e: e.copy(out=row[:], in_=cv[:]), reads=[cv], writes=[row])
                    c.dma("sp", HV[ci], row[:], reads=[row])
                else:
                    c.dma("sp", HX[ci - 8], cv[:], reads=[cv])

        self.lin_fm(W, self.blocks(3072, 6144, 512), HT, 16, epi_hy, wb=wb)

        sqb = c.sb([128, NT], BF16, "sqb")
        rn = c.sb([128, 512], F32, "rn")
        ones, eps = self.onesB, self.epsT

        dn_pending = []

        def epi_dn(col0, n, ti, t0, nt, ps, aux):
            ci = (col0 - 6144) // 128
            raw = rowf[ci % 2]
            c.op("act", lambda e: e.copy(out=raw[:, t0:t0 + nt], in_=ps[:, :nt]), reads=[ps], writes=[raw])
            if ti == 3:
                while dn_pending:
                    dn_pending.pop(0)()
                dn_pending.append(lambda ci=ci, raw=raw: dn_post(ci, raw))

        def dn_post(ci, raw):
            cv = rowg[ci % 2]
            self.conv_row(raw, cv, dnw, ci, 24)
            c.op("act", lambda e: e.activation(out=cv[:], in_=cv[:], func=AF.Silu), reads=[cv], writes=[cv])
            row = rowb[ci % 2]
            if ci < 16:
                c.op("act", lambda e: e.activation(out=sqb[:], in_=cv[:], func=AF.Square), reads=[cv], writes=[sqb])
                for tj, (u0, nu) in enumerate(TILES):
                    p2 = self.pb()
                    c.op("pe", lambda e: e.matmul(p2[:, :nu], lhsT=ones[:], rhs=sqb[:, u0:u0 + nu], start=True, stop=True), reads=[ones, sqb], writes=[p2])
                    c.op("act", lambda e: e.activation(out=rn[:, :nu], in_=p2[:, :nu], func=AF.Sqrt, bias=eps[:, 0:1]), reads=[p2, eps], writes=[rn])
                    c.op("dve", lambda e: e.reciprocal(out=rn[:, :nu], in_=rn[:, :nu]), reads=[rn], writes=[rn])
                    sc_ = (128 ** -0.5) if ci < 8 else 1.0
                    c.op("dve", lambda e: e.scalar_tensor_tensor(out=row[:, u0:u0 + nu], in0=cv[:, u0:u0 + nu], scalar=sc_, in1=rn[:, :nu], op0=ALU.mult, op1=ALU.mult),
                         reads=[cv, rn], writes=[row])
                c.dma("sp", DQK[ci], row[:], reads=[row])
            else:
                c.op("act", lambda e: e.copy(out=row[:], in_=cv[:]), reads=[cv], writes=[row])
                c.dma("sp", DVT[ci - 16], row[:], reads=[row])

        self.lin_fm(W, self.blocks(6144, 9216, 512), HT, 16, epi_dn, wb=wb)
        while dn_pending:
            dn_pending.pop(0)()

        gbraw = self.gbraw

        def epi_gb(col0, ncol, tt, ps):
            c.op("dve", lambda e: e.tensor_copy(out=gbraw[:, tt, :], in_=ps[:, :32]), reads=[ps], writes=[gbraw])

        self.lin_tm(W, [(10240, 32)], HT, 16, epi_gb, range(16), wb=wb)
        c.end_stage()

    def stage_merge(self, l, XTin, XTout):
        c = self.c
        I = self.I
        c.begin_stage()
        MO = self.scr("MO", [3, 8, 128, NT], BF16)
        GATE = self.S["GATE"]
        MIX = self.scr("MIX", [16, 128, NT], BF16)
        mo = [c.sb([128, 8, NT], BF16, "mo") for _ in range(3)]
        for b in range(3):
            c.dma("sp", mo[b][:], MO[b].rearrange("c p t -> p c t"), writes=[mo[b]])
        wp = [c.sb([128, 8, 512], BF16, "wp") for _ in range(3)]
        gr = [c.sb([128, NT], BF16, "gr") for _ in range(4)]
        acc = [c.sb([128, NT], F32, "acc") for _ in range(4)]
        mixb = [c.sb([128, NT], BF16, "mixb") for _ in range(2)]
        tmp = c.sb([128, 512], F32, "mtmp")
        PW = [I["p_ret"][l], I["p_hy"][l], I["p_dn"][l]]
        n = 0
        nw_ = 0
        for jg in range(4):
            for b in range(3):
                w = wp[nw_ % 3]
                nw_ += 1
                c.dma("pool", w[:], PW[b][:, jg * 512:(jg + 1) * 512].rearrange("(k p) n -> p k n", p=128), writes=[w])
                for jj in range(4):
                    j = jg * 4 + jj
                    a = acc[jj]
                    g = gr[n % 4]
                    n += 1
                    c.dma("sp", g[:], GATE[b * 16 + j], writes=[g])
                    for ti, (t0, nt) in enumerate(TILES):
                        ps = self.pb()
                        for k_ in range(8):
                            c.op("pe", lambda e: e.matmul(ps[:, :nt], lhsT=w[:, k_, jj * 128:(jj + 1) * 128], rhs=mo[b][:, k_, t0:t0 + nt], start=(k_ == 0), stop=(k_ == 7)), reads=[w, mo[b]], writes=[ps])
                        if b == 0:
                            c.op("dve", lambda e: e.tensor_tensor(out=a[:, t0:t0 + nt], in0=ps[:, :nt], in1=g[:, t0:t0 + nt], op=ALU.mult), reads=[ps, g], writes=[a])
                        else:
                            c.op("dve", lambda e: e.tensor_tensor(out=tmp[:, :nt], in0=ps[:, :nt], in1=g[:, t0:t0 + nt], op=ALU.mult), reads=[ps, g], writes=[tmp])
                            c.op("dve", lambda e: e.tensor_tensor(out=a[:, t0:t0 + nt], in0=a[:, t0:t0 + nt], in1=tmp[:, :nt], op=ALU.add), reads=[a, tmp], writes=[a])
            for jj in range(4):
                j = jg * 4 + jj
                m = mixb[j % 2]
                a = acc[jj]
                c.op("act", lambda e: e.copy(out=m[:], in_=a[:]), reads=[a], writes=[m])
                c.dma("sp", MIX[j], m[:], reads=[m])
        c.end_stage()
        c.begin_stage()
        mix = c.sb([128, 16, NT], BF16, "mix")
        c.dma("sp", mix[:], MIX.rearrange("c p t -> p c t"), writes=[mix])
        self.resid_epilogue(I["w_o"][l], 2048, mix, 16, l, 32, XTin, XTout)
        c.end_stage()

    def resid_epilogue(self, W, ncols, IN, KC, l, gbase, XTin, XTout, tiles=TILES, wb=None):
        c = self.c
        mt = self.modT
        xr = [c.sb([128, NT], F32, "xr") for _ in range(2)]
        lo = tiles[0][0]
        hi = tiles[-1][0] + tiles[-1][1]

        def epi(col0, n, ti, t0, nt, ps, aux):
            j = col0 // 128
            x = xr[j % 2]
            if ti == 0:
                c.dma("sp", x[:, lo:hi], XTin[j][:, lo:hi], writes=[x])
            b = 0 if t0 < 1024 else 1
            c.op("dve", lambda e: e.scalar_tensor_tensor(out=x[:, t0:t0 + nt], in0=ps[:, :nt], scalar=mt[:, l, gbase + j, b:b + 1], in1=x[:, t0:t0 + nt], op0=ALU.mult, op1=ALU.add),
                 reads=[ps, mt, x], writes=[x])
            if ti == len(tiles) - 1:
                c.dma("sp", XTout[j][:, lo:hi], x[:, lo:hi], reads=[x])

        if wb is None:
            wb = self.wbufs(KC, 3, 512)
        self.lin_fm(W, self.blocks(0, ncols, 512), IN, KC, epi, tiles=tiles, wb=wb)

    def stage_ffn_up(self, l):
        c = self.c
        I = self.I
        ACTT = self.scr("ACTT", [43, 128, NT], BF16)
        c.begin_stage()
        fw = c.sb([128, 3 * 86], F32, "fw")
        for k in range(3):
            self.load_cols(fw, fw[:, k * 86:(k + 1) * 86], I["ffn_conv"][l][k].rearrange("(n p) -> n p", p=128), 86)
        rowf = [c.sb([128, NT], F32, "frow") for _ in range(3)]
        ga = [c.sb([128, NT], F32, "ga") for _ in range(2)]
        gb = [c.sb([128, NT], F32, "gb") for _ in range(2)]
        ab = [c.sb([128, NT], BF16, "ab") for _ in range(2)]
        wb = self.wbufs(16, 4, 512)
        HT = self.HT
        W = I["w_up"][l]
        cnt = [0]
        nb = 0
        for blk in range(11):
            c0 = blk * 512
            ncol = min(512, 5504 - c0)
            wa_, wg_ = wb[nb % 4], wb[(nb + 1) % 4]
            nb += 2
            c.dma("pool", wa_[:, :, :ncol], W[:, c0:c0 + ncol].rearrange("(k p) n -> p k n", p=128), writes=[wa_])
            c.dma("pool", wg_[:, :, :ncol], W[:, 5504 + c0:5504 + c0 + ncol].rearrange("(k p) n -> p k n", p=128), writes=[wg_])
            for off in range(0, ncol, 128):
                ci = (c0 + off) // 128
                for which, w in ((0, wa_), (1, wg_)):
                    raw = rowf[cnt[0] % 3]
                    cnt[0] += 1
                    for ti, (t0, nt) in enumerate(TILES):
                        ps = self.pb()
                        for kk in range(16):
                            c.op("pe", lambda e: e.matmul(ps[:, :nt], lhsT=w[:, kk, off:off + 128], rhs=HT[:, kk, t0:t0 + nt], start=(kk == 0), stop=(kk == 15)), reads=[w, HT], writes=[ps])
                        if ti % 2 == 0:
                            c.op("act", lambda e: e.copy(out=raw[:, t0:t0 + nt], in_=ps[:, :nt]), reads=[ps], writes=[raw])
                        else:
                            c.op("dve", lambda e: e.tensor_copy(out=raw[:, t0:t0 + nt], in_=ps[:, :nt]), reads=[ps], writes=[raw])
                    if which == 0:
                        g = ga[ci % 2]
                        self.conv_row(raw, g, fw, ci, 86)
                        c.op("act", lambda e: e.activation(out=g[:], in_=g[:], func=AF.Silu), reads=[g], writes=[g])
                    else:
                        g = ga[ci % 2]
                        g2 = gb[ci % 2]
                        self.conv_row(raw, g2, fw, 43 + ci, 86)
                        a_ = ab[ci % 2]
                        c.op("dve", lambda e: e.tensor_tensor(out=a_[:], in0=g[:], in1=g2[:], op=ALU.mult), reads=[g, g2], writes=[a_])
                        c.dma("sp", ACTT[ci], a_[:], reads=[a_])
        c.end_stage()

    def stage_ffn_down(self, l, XTin, XTout):
        c = self.c
        I = self.I
        ACTT = self.S["ACTT"]
        for half in range(2):
            c.begin_stage()
            a = c.sb([128, 43, 1024], BF16, "actin")
            c.dma("sp", a[:], ACTT[:, :, half * 1024:(half + 1) * 1024].rearrange("c p t -> p c t"), writes=[a])

            class Shift:
                def __init__(s, buf, sh):
                    s.buf, s.sh = buf, sh
            wb = self.wbufs(43, 2, 512)
            tiles = [(half * 1024, 512), (half * 1024 + 512, 512)]
            self._resid_shift(I["w_down"][l], a, 43, l, 80, XTin, XTout, tiles, half * 1024, wb)
            c.end_stage()

    def _resid_shift(self, W, IN, KC, l, gbase, XTin, XTout, tiles, tshift, wb):
        c = self.c
        mt = self.modT
        xr = [c.sb([128, 1024], F32, "xr2") for _ in range(2)]
        blocks = self.blocks(0, 2048, 512)
        for bi, (c0, ncol) in enumerate(blocks):
            w = wb[bi % len(wb)]
            c.dma("pool", w[:, :KC, :ncol], W[:, c0:c0 + ncol].rearrange("(k p) n -> p k n", p=128), writes=[w])
            for off in range(0, ncol, 128):
                j = (c0 + off) // 128
                x = xr[j % 2]
                c.dma("sp", x[:], XTin[j][:, tshift:tshift + 1024], writes=[x])
                for ti, (t0, nt) in enumerate(tiles):
                    ps = self.pb()
                    for k in range(KC):
                        c.op("pe", lambda e: e.matmul(ps[:, :nt], lhsT=w[:, k, off:off + 128], rhs=IN[:, k, t0 - tshift:t0 - tshift + nt], start=(k == 0), stop=(k == KC - 1)),
                             reads=[w, IN], writes=[ps])
                    b = 0 if t0 < 1024 else 1
                    c.op("dve", lambda e: e.scalar_tensor_tensor(out=x[:, t0 - tshift:t0 - tshift + nt], in0=ps[:, :nt], scalar=mt[:, l, gbase + j, b:b + 1],
                                                                 in1=x[:, t0 - tshift:t0 - tshift + nt], op0=ALU.mult, op1=ALU.add), reads=[ps, mt, x], writes=[x])
                c.dma("sp", XTout[j][:, tshift:tshift + 1024], x[:], reads=[x])

    def stage_final(self, XT):
        c = self.c
        I = self.I
        c.begin_stage()
        nf = c.sb([128, 16], F32, "nf")
        self.load_cols(nf, nf[:], I["norm_f"].rearrange("(n p) -> n p", p=128), 16)
        xs = [c.sb([128, 16, 128], F32, "fx") for _ in range(2)]
        sq = c.sb([128, 16, 128], BF16, "fsq")
        r = c.sb([128, 128], F32, "fr")
        yo = [c.sb([128, 2048], F32, "yo") for _ in range(2)]
        ones, eps, idf = self.onesB, self.epsT, self.identF
        for tt in range(16):
            t0 = tt * 128
            x = xs[tt % 2]
            y = yo[tt % 2]
            c.dma("sp", x[:], XT[:, :, t0:t0 + 128].rearrange("c p t -> p c t"), writes=[x])
            c.op("act", lambda e: e.activation(out=sq[:], in_=x[:], func=AF.Square), reads=[x], writes=[sq])
            ps = self.pb()
            for ch in range(16):
                c.op("pe", lambda e: e.matmul(ps[:, :128], lhsT=ones[:], rhs=sq[:, ch, :], start=(ch == 0), stop=(ch == 15)), reads=[ones, sq], writes=[ps])
            c.op("act", lambda e: e.activation(out=r[:], in_=ps[:, :128], func=AF.Sqrt, scale=1.0 / 2048, bias=eps[:, 0:1]), reads=[ps, eps], writes=[r])
            c.op("dve", lambda e: e.reciprocal(out=r[:], in_=r[:]), reads=[r], writes=[r])
            for ch in range(16):
                c.op("dve", lambda e: e.scalar_tensor_tensor(out=x[:, ch, :], in0=x[:, ch, :], scalar=nf[:, ch:ch + 1], in1=r[:], op0=ALU.mult, op1=ALU.mult), reads=[x, nf, r], writes=[x])
            for g in range(4):
                p2 = self.pb()
                for j in range(4):
                    ch = g * 4 + j
                    c.op("pe", lambda e: e.transpose(out=p2[:, j * 128:(j + 1) * 128], in_=x[:, ch, :], identity=idf[:]), reads=[x, idf], writes=[p2])
                if g % 2 == 0:
                    c.op("dve", lambda e: e.tensor_copy(out=y[:, g * 512:(g + 1) * 512], in_=p2[:]), reads=[p2], writes=[y])
                else:
                    c.op("act", lambda e: e.copy(out=y[:, g * 512:(g + 1) * 512], in_=p2[:]), reads=[p2], writes=[y])
            c.dma("sp", self.O["y"][t0:t0 + 128, :], y[:], reads=[y])
        c.end_stage()


class KB3(KB2):
    def stage_ret(self, l):
        c = self.c
        I = self.I
        c.begin_stage()
        QK, VTM, KTM, SRG = self.S["QK"], self.S["VTM"], self.S["KTM"], self.S["SRG"]
        MO = self.scr("MO", [3, 8, 128, NT], BF16)
        OSR = self.O["osret"]
        ones, eps = self.onesB, self.epsT
        V = c.sb([128, 16, 1024], BF16, "V")
        c.dma("sp", V[:], VTM.rearrange("t p n -> p t n"), writes=[V])
        Kt = c.sb([128, 8, 512], BF16, "Kt")
        c.dma("sp", Kt[:], KTM.rearrange("t p n -> p t n"), writes=[Kt])
        lg = c.sb([128, 16], F32, "lg")
        c.dma("sp", lg[:], I["ret_decay"][l].rearrange("d h -> (d h)").partition_broadcast(128), writes=[lg])
        c.op("act", lambda e: e.activation(out=lg[:], in_=lg[:], func=AF.Exp, scale=-1.0), reads=[lg], writes=[lg])
        c.op("act", lambda e: e.activation(out=lg[:], in_=lg[:], func=AF.Ln, bias=self.onesF[:, 0:1]), reads=[lg], writes=[lg])
        c.op("dve", lambda e: e.tensor_scalar(out=lg[:], in0=lg[:], scalar1=-1.0, scalar2=None, op0=ALU.mult), reads=[lg], writes=[lg])
        tabs = {}
        for nm, w in (("S", 1920), ("P", 384)):
            for k in ("dpos", "dneg", "dz"):
                t = c.sb([128, w], F32, k + nm)
                c.dma("sp", t[:], I[k + nm], writes=[t])
                tabs[k + nm] = t
        expfb = c.sb([128, 4], F32, "expfb")
        c.dma("sp", expfb[:], I["expfb"], writes=[expfb])
        idx1 = c.sb([128, 1024], F32, "idx1")
        idx2 = c.sb([128, 1024], F32, "idx2")
        c.dma("sp", idx1[:], I["idx1"], writes=[idx1])
        c.dma("sp", idx2[:], I["idx2"], writes=[idx2])
        KD = c.sb([128, 2, 8, 2], F32, "KD")
        for d in range(2):
            for h in range(8):
                c.op("act", lambda e: e.activation(out=KD[:, d, h, :], in_=expfb[:, d * 2:d * 2 + 2], func=AF.Exp, scale=lg[:, d * 8 + h:d * 8 + h + 1]), reads=[expfb, lg], writes=[KD])
        TabS2 = [c.sb([128, 1920], F32, "TabS") for _ in range(2)]
        TabP2 = [c.sb([128, 384], F32, "TabP") for _ in range(2)]
        ttmp = c.sb([128, 1920], F32, "ttmp")
        qrow = c.sb([128, NT], BF16, "qrow")
        krow = c.sb([128, NT], BF16, "krow")
        lgsel = c.sb([128, 2], F32, "lgsel")
        dec = c.sb([128, 1024], F32, "dec")
        qdf = c.sb([128, 1024], BF16, "qdf")
        qdb = c.sb([128, 1024], BF16, "qdb")
        S0 = c.sb([128, 2, 128], BF16, "S0")
        srg = c.sb([128, NT], BF16, "srg")
        sm = [c.sb([128, 512], BF16, "sm") for _ in range(3)]
        osb = c.sb([128, 512], F32, "osb")
        osq = c.sb([128, 512], BF16, "osq")
        rr = c.sb([128, 512], F32, "rr")
        orow = c.sb([128, NT], BF16, "orow")
        kf = [c.sb([128, 64], BF16, "kf") for _ in range(4)]
        sst = [c.sb([64, 128], F32, "sst") for _ in range(4)]
        n_sm = [0]
        n_kf = [0]

        def build_tab(Tab, nm, w, h):
            dp, dn, dz = tabs["dpos" + nm], tabs["dneg" + nm], tabs["dz" + nm]
            c.op("dve", lambda e: e.tensor_scalar(out=ttmp[:, :w], in0=dp[:], scalar1=lg[:, h:h + 1], scalar2=None, op0=ALU.mult), reads=[dp, lg], writes=[ttmp])
            c.op("dve", lambda e: e.scalar_tensor_tensor(out=ttmp[:, :w], in0=dn[:], scalar=lg[:, 8 + h:9 + h], in1=ttmp[:, :w], op0=ALU.mult, op1=ALU.add), reads=[dn, lg, ttmp], writes=[ttmp])
            c.op("act", lambda e: e.activation(out=ttmp[:, :w], in_=ttmp[:, :w], func=AF.Exp), reads=[ttmp], writes=[ttmp])
            c.op("dve", lambda e: e.tensor_tensor(out=Tab[:, :w], in0=ttmp[:, :w], in1=dz[:], op=ALU.add), reads=[ttmp, dz], writes=[Tab])

        def finish_o(pso, n, h, tok0):
            c.op("act", lambda e: e.copy(out=osb[:, :n], in_=pso[:, :n]), reads=[pso], writes=[osb])
            c.op("act", lambda e: e.activation(out=osq[:, :n], in_=osb[:, :n], func=AF.Square), reads=[osb], writes=[osq])
            p2 = self.pb()
            c.op("pe", lambda e: e.matmul(p2[:, :n], lhsT=ones[:], rhs=osq[:, :n], start=True, stop=True), reads=[ones, osq], writes=[p2])
            c.op("act", lambda e: e.activation(out=rr[:, :n], in_=p2[:, :n], func=AF.Sqrt, scale=1.0 / 128, bias=eps[:, 0:1]), reads=[p2, eps], writes=[rr])
            c.op("dve", lambda e: e.reciprocal(out=rr[:, :n], in_=rr[:, :n]), reads=[rr], writes=[rr])
            c.op("dve", lambda e: e.tensor_tensor(out=osb[:, :n], in0=osb[:, :n], in1=rr[:, :n], op=ALU.mult), reads=[osb, rr], writes=[osb])
            c.op("dve", lambda e: e.tensor_tensor(out=orow[:, tok0:tok0 + n], in0=osb[:, :n], in1=srg[:, tok0:tok0 + n], op=ALU.mult), reads=[osb, srg], writes=[orow])

        srg2 = [srg, c.sb([128, NT], BF16, "srg2")]
        orow2 = [orow, c.sb([128, NT], BF16, "orow2")]
        sm.append(c.sb([128, 512], BF16, "sm"))
        NS = len(sm)
        pending = []

        def flush(keep):
            while len(pending) > keep:
                pending.pop(0)()

        def scores_loop(n_j, mk_score, mk_acc):
            nxt = mk_score(0)
            for jc in range(n_j):
                cur = nxt
                nxt = mk_score(jc + 1) if jc + 1 < n_j else None
                mk_acc(jc, cur)

        for h in range(8):
            hp, po = h // 2, (h % 2) * 64
            srg_h, orow_h = srg2[h % 2], orow2[h % 2]
            if h % 2 == 0:
                c.dma("sp", qrow[:], QK[hp], writes=[qrow])
                c.dma("sp", krow[:], QK[4 + hp], writes=[krow])
                for d in range(2):
                    c.op("dve", lambda e: e.tensor_copy(out=lgsel[0:64, d:d + 1], in_=lg[0:64, d * 8 + h:d * 8 + h + 1]), reads=[lg], writes=[lgsel])
                    c.op("dve", lambda e: e.tensor_copy(out=lgsel[64:128, d:d + 1], in_=lg[64:128, d * 8 + h + 1:d * 8 + h + 2]), reads=[lg], writes=[lgsel])
                c.op("act", lambda e: e.activation(out=dec[:], in_=idx1[:], func=AF.Exp, scale=lgsel[:, 0:1]), reads=[idx1, lgsel], writes=[dec])
                c.op("dve", lambda e: e.tensor_tensor(out=qdf[:], in0=qrow[:, 1024:2048], in1=dec[:], op=ALU.mult), reads=[qrow, dec], writes=[qdf])
                c.op("act", lambda e: e.activation(out=dec[:], in_=idx2[:], func=AF.Exp, scale=lgsel[:, 1:2]), reads=[idx2, lgsel], writes=[dec])
                c.op("dve", lambda e: e.tensor_tensor(out=qdb[:], in0=qrow[:, 1024:2048], in1=dec[:], op=ALU.mult), reads=[qrow, dec], writes=[qdb])
            c.dma("sp", srg_h[:], SRG[h], writes=[srg_h])
            for d in range(2):
                c.dma("pool", S0[po:po + 64, d, :], I["sret"][l, d, h], writes=[S0])
            if h == 0:
                build_tab(TabS2[0], "S", 1920, 0)
                build_tab(TabP2[0], "P", 384, 0)
            TabS, TabP = TabS2[h % 2], TabP2[h % 2]
            if h + 1 < 8:
                build_tab(TabS2[(h + 1) % 2], "S", 1920, h + 1)
                build_tab(TabP2[(h + 1) % 2], "P", 384, h + 1)

            def fin(pso, n, tok0, h=h, srg_h=srg_h, orow_h=orow_h, store=False):
                def f():
                    c.op("act", lambda e: e.copy(out=osb[:, :n], in_=pso[:, :n]), reads=[pso], writes=[osb])
                    c.op("act", lambda e: e.activation(out=osq[:, :n], in_=osb[:, :n], func=AF.Square), reads=[osb], writes=[osq])
                    p2 = self.pb()
                    c.op("pe", lambda e: e.matmul(p2[:, :n], lhsT=ones[:], rhs=osq[:, :n], start=True, stop=True), reads=[ones, osq], writes=[p2])
                    c.op("act", lambda e: e.activation(out=rr[:, :n], in_=p2[:, :n], func=AF.Sqrt, scale=1.0 / 128, bias=eps[:, 0:1]), reads=[p2, eps], writes=[rr])
                    c.op("dve", lambda e: e.reciprocal(out=rr[:, :n], in_=rr[:, :n]), reads=[rr], writes=[rr])
                    c.op("dve", lambda e: e.tensor_tensor(out=osb[:, :n], in0=osb[:, :n], in1=rr[:, :n], op=ALU.mult), reads=[osb, rr], writes=[osb])
                    c.op("dve", lambda e: e.tensor_tensor(out=orow_h[:, tok0:tok0 + n], in0=osb[:, :n], in1=srg_h[:, tok0:tok0 + n], op=ALU.mult), reads=[osb, srg_h], writes=[orow_h])
                    if store:
                        c.dma("sp", MO[0][h], orow_h[:], reads=[orow_h])
                return f

            for i0 in (0, 512):
                pso = self.pacc()

                def mk_score(jc, i0=i0):
                    pss = self.pb()
                    c.op("pe", lambda e: e.matmul(pss[:, :512], lhsT=krow[po:po + 64, 1024 + jc * 128:1024 + (jc + 1) * 128], rhs=qrow[po:po + 64, 1024 + i0:1024 + i0 + 512], start=True, stop=True),
                         reads=[krow, qrow], writes=[pss])
                    return pss

                def mk_acc(jc, pss, i0=i0, pso=pso):
                    s = sm[n_sm[0] % NS]
                    n_sm[0] += 1
                    u0 = i0 - 128 * jc + 896
                    c.op("dve", lambda e: e.tensor_tensor(out=s[:], in0=pss[:, :512], in1=TabS[:, u0:u0 + 512], op=ALU.mult), reads=[pss, TabS], writes=[s])
                    c.op("pe", lambda e: e.matmul(pso[:, :512], lhsT=V[:, 8 + jc, h * 128:(h + 1) * 128], rhs=s[:], start=(jc == 0), stop=False), reads=[V, s], writes=[pso])

                scores_loop(8, mk_score, mk_acc)
                c.op("pe", lambda e: e.matmul(pso[:, :512], lhsT=S0[po:po + 64, 0, :], rhs=qdf[po:po + 64, i0:i0 + 512], start=False, stop=False), reads=[S0, qdf], writes=[pso])
                c.op("pe", lambda e: e.matmul(pso[:, :512], lhsT=S0[po:po + 64, 1, :], rhs=qdb[po:po + 64, i0:i0 + 512], start=False, stop=True), reads=[S0, qdb], writes=[pso])
                pending.append(fin(pso, 512, 1024 + i0))
                flush(1)
            for s_ in range(4):
                b0 = s_ * 256
                pso = self.pacc()

                def mk_score(jc, b0=b0):
                    pss = self.pb()
                    c.op("pe", lambda e: e.matmul(pss[:, :256], lhsT=krow[po:po + 64, b0 + jc * 128:b0 + (jc + 1) * 128], rhs=qrow[po:po + 64, b0:b0 + 256], start=True, stop=True),
                         reads=[krow, qrow], writes=[pss])
                    return pss

                def mk_acc(jc, pss, pso=pso, s_=s_):
                    s = sm[n_sm[0] % NS]
                    n_sm[0] += 1
                    u0 = 128 - 128 * jc
                    c.op("dve", lambda e: e.tensor_tensor(out=s[:, :256], in0=pss[:, :256], in1=TabP[:, u0:u0 + 256], op=ALU.mult), reads=[pss, TabP], writes=[s])
                    c.op("pe", lambda e: e.matmul(pso[:, :256], lhsT=V[:, s_ * 2 + jc, h * 128:(h + 1) * 128], rhs=s[:, :256], start=(jc == 0), stop=(jc == 1)), reads=[V, s], writes=[pso])

                scores_loop(2, mk_score, mk_acc)
                pending.append(fin(pso, 256, b0, store=(s_ == 3)))
                flush(1)
                for d in range(2):
                    pst = self.pb()
                    for tt in range(2):
                        k_ = kf[n_kf[0] % 4]
                        n_kf[0] += 1
                        c.op("dve", lambda e: e.tensor_scalar(out=k_[:], in0=Kt[:, s_ * 2 + tt, h * 64:(h + 1) * 64], scalar1=KD[:, d, h, tt:tt + 1], scalar2=None, op0=ALU.mult), reads=[Kt, KD], writes=[k_])
                        c.op("pe", lambda e: e.matmul(pst[:64, :128], lhsT=k_[:], rhs=V[:, s_ * 2 + tt, h * 128:(h + 1) * 128], start=(tt == 0), stop=(tt == 1)), reads=[k_, V], writes=[pst])
                    st = sst[(s_ * 2 + d) % 4]
                    c.op("act", lambda e: e.copy(out=st[:], in_=pst[:64, :128]), reads=[pst], writes=[st])
                    c.dma("sp", OSR[s_, l, d, h], st[:], reads=[st])
        flush(0)
        c.end_stage()


import os


class KB4(KB3):
    def stage_dn(self, l):
        c = self.c
        I = self.I
        c.begin_stage()
        DQK, DVT, SDZ = self.S["DQK"], self.S["DVT"], self.S["SDZ"]
        MO = self.scr("MO", [3, 8, 128, NT], BF16)
        OSD = self.O["osdn"]
        onesF, onesB, eps, idF, idB = self.onesF, self.onesB, self.epsT, self.identF, self.identB
        gbraw = self.gbraw
        cm = {}
        for nm in ("UF", "UB", "SU", "SL"):
            t = c.sb([128, 128], F32, nm)
            c.dma("sp", t[:], I[nm], writes=[t])
            cm[nm] = t
        alog = c.sb([128, 16], F32, "alog")
        dtb = c.sb([128, 16], F32, "dtb")
        c.dma("sp", alog[:], I["dn_a_log"][l].rearrange("d h -> (d h)").partition_broadcast(128), writes=[alog])
        c.dma("sp", dtb[:], I["dn_dt_bias"][l].rearrange("d h -> (d h)").partition_broadcast(128), writes=[dtb])
        dnn = c.sb([128, 1], F32, "dnn")
        c.dma("sp", dnn[:], I["dn_norm"][l].rearrange("(p o) -> p o", o=1), writes=[dnn])
        c.op("act", lambda e: e.activation(out=alog[:], in_=alog[:], func=AF.Exp), reads=[alog], writes=[alog])
        c.op("dve", lambda e: e.tensor_scalar(out=alog[:], in0=alog[:], scalar1=-1.0, scalar2=None, op0=ALU.mult), reads=[alog], writes=[alog])
        G = c.sb([128, 16, 16], F32, "G")
        BT = c.sb([128, 16, 16], F32, "BT")
        NBT = c.sb([128, 16, 16], F32, "NBT")
        for tt in range(16):
            c.op("dve", lambda e: e.tensor_tensor(out=G[:, tt, :], in0=gbraw[:, tt, 0:16], in1=dtb[:], op=ALU.add), reads=[gbraw, dtb], writes=[G])
        c.op("act", lambda e: e.activation(out=G[:], in_=G[:], func=AF.Exp), reads=[G], writes=[G])
        c.op("act", lambda e: e.activation(out=G[:], in_=G[:], func=AF.Ln, bias=onesF[:, 0:1]), reads=[G, onesF], writes=[G])
        for tt in range(16):
            c.op("dve", lambda e: e.tensor_tensor(out=G[:, tt, :], in0=G[:, tt, :], in1=alog[:], op=ALU.mult), reads=[G, alog], writes=[G])
        c.op("act", lambda e: e.activation(out=BT[:], in_=gbraw[:, :, 16:32], func=AF.Sigmoid), reads=[gbraw], writes=[BT])
        c.op("dve", lambda e: e.tensor_scalar(out=NBT[:], in0=BT[:], scalar1=-1.0, scalar2=None, op0=ALU.mult), reads=[BT], writes=[NBT])
        GC = c.sb([128, 16, 16], F32, "GC")
        GL = c.sb([128, 16, 16], F32, "GL")
        for tt in range(16):
            for d in range(2):
                U = cm["UF"] if d == 0 else cm["UB"]
                ps = self.pb()
                c.op("pe", lambda e: e.matmul(ps[:, 0:8], lhsT=U[:], rhs=G[:, tt, d * 8:d * 8 + 8], start=True, stop=True), reads=[U, G], writes=[ps])
                c.op("pe", lambda e: e.matmul(ps[:, 8:16], lhsT=onesF[:], rhs=G[:, tt, d * 8:d * 8 + 8], start=True, stop=True), reads=[onesF, G], writes=[ps])
                c.op("dve", lambda e: e.tensor_copy(out=GC[:, tt, d * 8:d * 8 + 8], in_=ps[:, 0:8]), reads=[ps], writes=[GC])
                c.op("dve", lambda e: e.tensor_copy(out=GL[:, tt, d * 8:d * 8 + 8], in_=ps[:, 8:16]), reads=[ps], writes=[GL])
        BK = c.sb([128, 16, 16], F32, "BK")
        KDS = c.sb([128, 16, 16], F32, "KDS")
        EGL = c.sb([128, 16, 16], F32, "EGL")
        c.op("act", lambda e: e.activation(out=BK[:], in_=GC[:], func=AF.Exp), reads=[GC], writes=[BK])
        c.op("dve", lambda e: e.tensor_tensor(out=BK[:], in0=BK[:], in1=BT[:], op=ALU.mult), reads=[BK, BT], writes=[BK])
        c.op("dve", lambda e: e.tensor_tensor(out=KDS[:], in0=GL[:], in1=GC[:], op=ALU.subtract), reads=[GL, GC], writes=[KDS])
        c.op("act", lambda e: e.activation(out=KDS[:], in_=KDS[:], func=AF.Exp), reads=[KDS], writes=[KDS])
        c.op("act", lambda e: e.activation(out=EGL[:], in_=GL[:], func=AF.Exp), reads=[GL], writes=[EGL])

        if "GDBG" in self.dbg:
            gd = self.scr("GDBG", [4, 128, 16, 16], F32)
            for i_, t_ in enumerate((G, BT, GC, GL)):
                c.dma("sp", gd[i_], t_[:], reads=[t_])
        OACC = [c.sb([128, NT], F32, "oacc") for _ in range(8)]
        c.begin_stage()
        Sf = [c.sb([128, 128], F32, "Sf") for _ in range(8)]
        Sb = [c.sb([128, 128], BF16, "Sb") for _ in range(8)]
        NI = 8

        def ring(shape, dt, nm, n=NI):
            bufs = [c.sb(shape, dt, nm) for _ in range(n)]
            st = [0]

            def nxt():
                st[0] += 1
                return bufs[st[0] % n]
            return nxt
        r_q = ring([128, 128], BF16, "rq")
        r_k = ring([128, 128], BF16, "rk")
        r_v = ring([128, 128], BF16, "rv")
        r_gbc = ring([128, 128], F32, "rgbc")
        r_t = ring([128, 128], F32, "rt", 2 * NI)
        r_vb = ring([128, 128], F32, "rvb")
        r_kbe = ring([128, 128], F32, "rkbe")
        r_kd = ring([128, 128], F32, "rkd")
        r_nw = ring([128, 128], F32, "rnw")
        r_at = ring([128, 128], BF16, "rat")
        r_qd = ring([128, 128], BF16, "rqd")
        r_vn = ring([128, 128], F32, "rvn")
        r_vnb = ring([128, 128], BF16, "rvnb")
        r_so = ring([128, 128], F32, "rso", 3)
        r_tw = ring([128, 256], F32, "rtw", 2 * NI)
        r_tq = ring([128, 256], F32, "rtq", NI)
        r_am = ring([128, 256], F32, "ram", 2 * NI)
        r_pp = ring([128, 256], F32, "rpp", NI)
        mlu = c.sb([128, 7, 256], F32, "mlu")
        mul = c.sb([128, 7, 256], F32, "mul")
        c.dma("sp", mlu[:], I["MLU"], writes=[mlu])
        c.dma("sp", mul[:], I["MUL"], writes=[mul])
        id2 = c.sb([128, 256], F32, "id2")
        c.dma("sp", id2[:, 0:128], I["ident"], writes=[id2])
        c.dma("sp", id2[:, 128:256], I["ident"], writes=[id2])
        PB = self.PB
        pbn = [0]

        def pbank():
            pbn[0] += 1
            return PB[pbn[0] % 8]

        def process(tt, h, d, last_seq):
            t0 = tt * 128
            col = d * 8 + h
            U = cm["UF"] if d == 0 else cm["UB"]
            MS = cm["SL"] if d == 0 else cm["SU"]
            MI = cm["UF"] if d == 0 else cm["UB"]
            gc_ap = GC[:, tt, col:col + 1]
            qT, kT, vT = r_q(), r_k(), r_v()
            c.dma("sp", qT[:], DQK[h][:, t0:t0 + 128], writes=[qT])
            c.dma("sp", kT[:], DQK[8 + h][:, t0:t0 + 128], writes=[kT])
            c.dma("sp", vT[:], DVT[h][:, t0:t0 + 128], writes=[vT])
            gbc = r_gbc()
            c.op("act", lambda e: e.activation(out=gbc[:], in_=onesF[:], func=AF.Copy, scale=G[:, tt, col:col + 1]), reads=[onesF, G], writes=[gbc])
            yield
            bank = PB[h]
            ptr = pR = pG = pT = ps1 = ps2 = pw = psv = pso = pss = bank
            ptb = bank[:].bitcast(BF16)
            c.op("pe", lambda e: e.transpose(out=ptb[:, 0:128], in_=kT[:], identity=idB[:]), reads=[kT, idB], writes=[ptr])
            c.op("pe", lambda e: e.transpose(out=ptb[:, 128:256], in_=vT[:], identity=idB[:]), reads=[vT, idB], writes=[ptr])
            c.op("pe", lambda e: e.matmul(pR[:, 128:256], lhsT=gbc[:], rhs=U[:], start=True, stop=True), reads=[gbc, U], writes=[pR])
            c.op("pe", lambda e: e.matmul(pG[:, 256:384], lhsT=kT[:], rhs=kT[:], start=True, stop=True), reads=[kT], writes=[pG])
            c.op("pe", lambda e: e.matmul(pG[:, 384:512], lhsT=kT[:], rhs=qT[:], start=True, stop=True), reads=[kT, qT], writes=[pG])
            yield
            vb, kbe, kd = r_vb(), r_kbe(), r_kd()
            c.op("dve", lambda e: e.tensor_scalar(out=kbe[:], in0=ptb[:, 0:128], scalar1=BK[:, tt, col:col + 1], scalar2=None, op0=ALU.mult), reads=[ptr, BK], writes=[kbe])
            c.op("dve", lambda e: e.tensor_scalar(out=kd[:], in0=ptb[:, 0:128], scalar1=KDS[:, tt, col:col + 1], scalar2=None, op0=ALU.mult), reads=[ptr, KDS], writes=[kd])
            c.op("dve", lambda e: e.tensor_scalar(out=vb[:], in0=ptb[:, 128:256], scalar1=BT[:, tt, col:col + 1], scalar2=None, op0=ALU.mult), reads=[ptr, BT], writes=[vb])
            d1, d2 = r_t(), r_t()
            c.op("dve", lambda e: e.tensor_scalar(out=d1[:], in0=pR[:, 128:256], scalar1=gc_ap, scalar2=0.0, op0=ALU.subtract, op1=ALU.max), reads=[pR, GC], writes=[d1])
            c.op("dve", lambda e: e.tensor_scalar(out=d2[:], in0=pR[:, 128:256], scalar1=gc_ap, scalar2=0.0, op0=ALU.subtract, op1=ALU.min), reads=[pR, GC], writes=[d2])
            er = gbc
            c.op("act", lambda e: e.activation(out=er[:], in_=pR[:, 128:256], func=AF.Exp), reads=[pR], writes=[er])
            yield
            c.op("act", lambda e: e.activation(out=d1[:], in_=d1[:], func=AF.Exp, scale=-1.0), reads=[d1], writes=[d1])
            c.op("act", lambda e: e.activation(out=d2[:], in_=d2[:], func=AF.Exp), reads=[d2], writes=[d2])
            qd = r_qd()
            c.op("pool", lambda e: e.tensor_tensor(out=qd[:], in0=qT[:], in1=er[:], op=ALU.mult), reads=[qT, er], writes=[qd])
            yield
            pp = r_pp()
            c.op("dve", lambda e: e.scalar_tensor_tensor(out=d1[:], in0=pG[:, 256:384], scalar=NBT[:, tt, col:col + 1], in1=d1[:], op0=ALU.mult, op1=ALU.mult), reads=[pG, NBT, d1], writes=[d1])
            c.op("dve", lambda e: e.tensor_tensor(out=d2[:], in0=pG[:, 384:512], in1=d2[:], op=ALU.mult), reads=[pG, d2], writes=[d2])
            yield
            c.op("pool", lambda e: e.tensor_tensor(out=pp[:, 0:128], in0=d1[:], in1=MS[:], op=ALU.mult), reads=[d1, MS], writes=[pp])
            at = r_at()
            c.op("pool", lambda e: e.tensor_tensor(out=at[:], in0=d2[:], in1=MI[:], op=ALU.mult), reads=[d2, MI], writes=[at])
            yield
            c.op("pe", lambda e: e.transpose(out=pT[:, :128], in_=pp[:, 0:128], identity=idF[:]), reads=[pp, idF], writes=[pT])
            yield
            c.op("act", lambda e: e.copy(out=pp[:, 128:256], in_=pT[:, :128]), reads=[pT], writes=[pp])
            yield
            MM = mlu if d == 0 else mul
            am = r_am()
            c.op("pool", lambda e: e.tensor_tensor(out=am[:], in0=pp[:], in1=MM[:, 0, :], op=ALU.mult), reads=[pp, MM], writes=[am])
            yield
            tw = r_tw()
            c.op("dve", lambda e: e.tensor_tensor(out=tw[:], in0=am[:], in1=id2[:], op=ALU.add), reads=[am, id2], writes=[tw])
            am = r_am()
            c.op("pool", lambda e: e.tensor_tensor(out=am[:], in0=pp[:], in1=MM[:, 1, :], op=ALU.mult), reads=[pp, MM], writes=[am])
            yield
            for s in range(1, 7):
                c.op("pe", lambda e: e.matmul(ps1[:, 0:128], lhsT=am[:, 128:256], rhs=tw[:, 0:128], start=True, stop=True), reads=[am, tw], writes=[ps1])
                c.op("pe", lambda e: e.matmul(ps1[:, 128:256], lhsT=am[:, 0:128], rhs=tw[:, 128:256], start=True, stop=True), reads=[am, tw], writes=[ps1])
                if s < 6:
                    am = r_am()
                    c.op("pool", lambda e: e.tensor_tensor(out=am[:], in0=pp[:], in1=MM[:, s + 1, :], op=ALU.mult), reads=[pp, MM], writes=[am])
                yield
                p1 = r_tq()
                c.op("act", lambda e: e.copy(out=p1[:], in_=ps1[:, 0:256]), reads=[ps1], writes=[p1])
                yield
                c.op("pe", lambda e: e.matmul(ps2[:, 256:384], lhsT=tw[:, 128:256], rhs=p1[:, 0:128], start=True, stop=True), reads=[tw, p1], writes=[ps2])
                c.op("pe", lambda e: e.matmul(ps2[:, 384:512], lhsT=tw[:, 0:128], rhs=p1[:, 128:256], start=True, stop=True), reads=[tw, p1], writes=[ps2])
                yield
                ntw = r_tw()
                c.op("dve", lambda e: e.tensor_tensor(out=ntw[:], in0=ps2[:, 256:512], in1=tw[:], op=ALU.add), reads=[ps2, tw], writes=[ntw])
                tw = ntw
                yield
            c.op("pe", lambda e: e.matmul(pw[:, :128], lhsT=kbe[:], rhs=tw[:, 128:256], start=True, stop=True), reads=[kbe, tw], writes=[pw])
            yield
            nw = r_nw()
            c.op("act", lambda e: e.activation(out=nw[:], in_=pw[:, :128], func=AF.Copy, scale=-1.0), reads=[pw], writes=[nw])
            yield
            S_f, S_b = Sf[h], Sb[h]
            c.op("pe", lambda e: e.matmul(psv[:, 128:256], lhsT=tw[:, 128:256], rhs=vb[:], start=True, stop=False), reads=[tw, vb], writes=[psv])
            c.op("pe", lambda e: e.matmul(psv[:, 128:256], lhsT=nw[:], rhs=S_f[:], start=False, stop=True), reads=[nw, S_f], writes=[psv])
            yield
            vn = r_vn()
            vnb = r_vnb()
            c.op("act", lambda e: e.copy(out=vn[:], in_=psv[:, 128:256]), reads=[psv], writes=[vn])
            c.op("act", lambda e: e.copy(out=vnb[:], in_=vn[:]), reads=[vn], writes=[vnb])
            yield
            c.op("pe", lambda e: e.matmul(pso[:, 256:384], lhsT=S_b[:], rhs=qd[:], start=True, stop=False), reads=[S_b, qd], writes=[pso])
            c.op("pe", lambda e: e.matmul(pso[:, 256:384], lhsT=vnb[:], rhs=at[:], start=False, stop=True), reads=[vnb, at], writes=[pso])
            c.op("pe", lambda e: e.matmul(pss[:, 384:512], lhsT=kd[:], rhs=vn[:], start=True, stop=True), reads=[kd, vn], writes=[pss])
            yield
            oa = OACC[h]
            if d == 0:
                c.op("act", lambda e: e.copy(out=oa[:, t0:t0 + 128], in_=pso[:, 256:384]), reads=[pso], writes=[oa])
            else:
                c.op("dve", lambda e: e.tensor_tensor(out=oa[:, t0:t0 + 128], in0=pso[:, 256:384], in1=oa[:, t0:t0 + 128], op=ALU.add), reads=[pso, oa], writes=[oa])
            c.op("dve", lambda e: e.scalar_tensor_tensor(out=S_f[:], in0=S_f[:], scalar=EGL[:, tt, col:col + 1], in1=pss[:, 384:512], op0=ALU.mult, op1=ALU.add), reads=[S_f, EGL, pss], writes=[S_f])
            yield
            c.op("act", lambda e: e.copy(out=S_b[:], in_=S_f[:]), reads=[S_f], writes=[S_b])
            if last_seq is not None:
                so = r_so()
                c.op("pool", lambda e: e.tensor_copy(out=so[:], in_=S_f[:]), reads=[S_f], writes=[so])
                c.dma("sp", OSD[last_seq, l, d, h], so[:], reads=[so])

        def init_state(s, h, d):
            if s is None:
                c.dma("sp", Sf[h][:], I["sdn"][l, d, h], writes=[Sf[h]])
                c.op("act", lambda e: e.copy(out=Sb[h][:], in_=Sf[h][:]), reads=[Sf[h]], writes=[Sb[h]])
            else:
                c.op("pool", lambda e: e.memset(Sf[h][:], 0.0), writes=[Sf[h]])
                c.op("pool", lambda e: e.memset(Sb[h][:], 0.0), writes=[Sb[h]])

        def inst(tt, h, d, s, first, last):
            if first:
                init_state(s, h, d)
            yield from process(tt, h, d, s if (s is not None and last) else None)

        queue = []
        for d in range(2):
            seqs = [(s, [2 * s, 2 * s + 1]) for s in range(4)] + [(None, list(range(8, 16)))]
            for s, tiles in seqs:
                order = tiles if d == 0 else tiles[::-1]
                for i, tt in enumerate(order):
                    for h in range(8):
                        queue.append(inst(tt, h, d, s, i == 0, i == len(order) - 1))
        active = []
        qi = 0
        cyc = 0
        STAG = 5
        while qi < len(queue) or active:
            if qi < len(queue) and len(active) < NI and (qi >= NI or cyc >= qi * STAG):
                active.append(queue[qi])
                qi += 1
            alive = []
            for g in active:
                try:
                    next(g)
                    alive.append(g)
                except StopIteration:
                    pass
            active = alive
            cyc += 1
        c.end_stage()
        osq = c.sb([128, 512], BF16, "dosq")
        rr = c.sb([128, 512], F32, "drr")
        sdz = [c.sb([128, NT], BF16, "sdz") for _ in range(2)]
        orow = [c.sb([128, NT], BF16, "dorow") for _ in range(2)]
        for h in range(8):
            oa = OACC[h]
            z_ = sdz[h % 2]
            orw = orow[h % 2]
            c.dma("sp", z_[:], SDZ[h], writes=[z_])
            for (u0, nu) in TILES:
                c.op("act", lambda e: e.activation(out=osq[:, :nu], in_=oa[:, u0:u0 + nu], func=AF.Square), reads=[oa], writes=[osq])
                p2 = self.pb()
                c.op("pe", lambda e: e.matmul(p2[:, :nu], lhsT=onesB[:], rhs=osq[:, :nu], start=True, stop=True), reads=[onesB, osq], writes=[p2])
                c.op("act", lambda e: e.activation(out=rr[:, :nu], in_=p2[:, :nu], func=AF.Sqrt, scale=1.0 / 128, bias=eps[:, 0:1]), reads=[p2, eps], writes=[rr])
                c.op("dve", lambda e: e.reciprocal(out=rr[:, :nu], in_=rr[:, :nu]), reads=[rr], writes=[rr])
                c.op("dve", lambda e: e.scalar_tensor_tensor(out=rr[:, :nu], in0=oa[:, u0:u0 + nu], scalar=dnn[:, 0:1], in1=rr[:, :nu], op0=ALU.mult, op1=ALU.mult), reads=[oa, dnn, rr], writes=[rr])
                c.op("dve", lambda e: e.tensor_tensor(out=orw[:, u0:u0 + nu], in0=rr[:, :nu], in1=z_[:, u0:u0 + nu], op=ALU.mult), reads=[rr, z_], writes=[orw])
            c.dma("sp", MO[2][h], orw[:], reads=[orw])
        c.end_stage()


PI = math.pi


class KB5(KB4):
    def hy_tables(self, l, L):
        c = self.c
        I = self.I
        nch = L // 128
        TAB = self.scr("HTAB%d" % L, [2, 2 * nch + 1, 128, 1024], BF16)
        c.begin_stage()
        onesF, m0 = self.onesF, None
        m0 = c.sb([128, 2], F32, "m0")
        c.dma("sp", m0[:], I["m0"], writes=[m0])
        fT = c.sb([33, L], F32, "fT")
        c.dma("sp", fT[:], I["featsT%d" % L], writes=[fT])
        w1 = c.sb([33, 64], F32, "w1")
        w2 = c.sb([64, 64], F32, "w2")
        w3 = c.sb([64, 4096], F32, "w3")
        c.dma("sp", w1[:], I["hy_w1"][l], writes=[w1])
        c.dma("sp", w2[:], I["hy_w2"][l], writes=[w2])
        c.dma("sp", w3[:], I["hy_w3"][l], writes=[w3])
        vec = c.sb([64, 4], F32, "hvec")
        for i, nm in enumerate(("hy_b1", "hy_freq1", "hy_b2", "hy_freq2")):
            c.dma("sp", vec[:, i:i + 1], I[nm][l].rearrange("(p o) -> p o", o=1), writes=[vec])
        hid1 = c.sb([64, L], F32, "hid1")
        hid2 = c.sb([64, L], F32, "hid2")
        msk = c.sb([64, 512], F32, "msk")

        def sin_layer(dst, wT, K, src, bi, fi):
            for t0 in range(0, L, 512):
                n = min(512, L - t0)
                ps = self.pb()
                c.op("pe", lambda e: e.matmul(ps[:64, :n], lhsT=wT[:K, :], rhs=src[:K, t0:t0 + n], start=True, stop=True), reads=[wT, src], writes=[ps])
                d = dst
                c.op("dve", lambda e: e.tensor_scalar(out=d[:, t0:t0 + n], in0=ps[:64, :n], scalar1=vec[:, bi:bi + 1], scalar2=vec[:, fi:fi + 1], op0=ALU.add, op1=ALU.mult), reads=[ps, vec], writes=[d])
                for _ in range(2):
                    c.op("dve", lambda e: e.tensor_scalar(out=msk[:, :n], in0=d[:, t0:t0 + n], scalar1=PI, scalar2=-2 * PI, op0=ALU.is_gt, op1=ALU.mult), reads=[d], writes=[msk])
                    c.op("dve", lambda e: e.tensor_tensor(out=d[:, t0:t0 + n], in0=d[:, t0:t0 + n], in1=msk[:, :n], op=ALU.add), reads=[d, msk], writes=[d])
                    c.op("dve", lambda e: e.tensor_scalar(out=msk[:, :n], in0=d[:, t0:t0 + n], scalar1=-PI, scalar2=2 * PI, op0=ALU.is_lt, op1=ALU.mult), reads=[d], writes=[msk])
                    c.op("dve", lambda e: e.tensor_tensor(out=d[:, t0:t0 + n], in0=d[:, t0:t0 + n], in1=msk[:, :n], op=ALU.add), reads=[d, msk], writes=[d])
                c.op("act", lambda e: e.activation(out=d[:, t0:t0 + n], in_=d[:, t0:t0 + n], func=AF.Sin), reads=[d], writes=[d])

        sin_layer(hid1, w1, 33, fT, 0, 1)
        sin_layer(hid2, w2, 64, hid1, 2, 3)
        win = c.sb([128, nch, 1024], F32, "win")
        c.dma("sp", win[:], I["win%d" % L].rearrange("(t p) c -> p t c", p=128), writes=[win])
        CA = c.sb([128, nch, L], BF16, "CA")
        MB = c.sb([128, nch, L], BF16, "MB")
        c.dma("pool", CA[:], I["CA%d" % L].rearrange("(t p) k -> p t k", p=128), writes=[CA])
        c.dma("pool", MB[:], I["MB%d" % L].rearrange("(t p) k -> p t k", p=128), writes=[MB])
        hw = [c.sb([128, nch, 512], F32, "hw") for _ in range(2)]
        abs_ = [c.sb([128, 512], F32, "hab") for _ in range(2)]
        rinv = c.sb([128, 512], F32, "hrinv")
        hs = c.sb([128, nch, 512], BF16, "hs")
        hd = c.sb([128, nch, 512], BF16, "hd")
        to = [c.sb([128, 512], BF16, "hto") for _ in range(3)]
        tf = [c.sb([128, 512], F32, "htf") for _ in range(2)]
        nto = [0]
        for order in range(2):
            for half in range(2):
                for d in range(2):
                    col0 = d * 2048 + order * 1024 + half * 512
                    H = hw[d]
                    pn = self.pacc()

                    def gen_h(tt, col0=col0):
                        ps = self.pb()
                        c.op("pe", lambda e: e.matmul(ps[:, :512], lhsT=hid2[:, tt * 128:(tt + 1) * 128], rhs=w3[:, col0:col0 + 512], start=True, stop=True), reads=[hid2, w3], writes=[ps])
                        return ps
                    nxt = gen_h(0)
                    for tt in range(nch):
                        ps = nxt
                        nxt = gen_h(tt + 1) if tt + 1 < nch else None
                        a_ = abs_[tt % 2]
                        c.op("dve", lambda e: e.tensor_tensor(out=H[:, tt, :], in0=ps[:, :512], in1=win[:, tt, half * 512:(half + 1) * 512], op=ALU.mult), reads=[ps, win], writes=[H])
                        c.op("act", lambda e: e.activation(out=a_[:], in_=H[:, tt, :], func=AF.Abs), reads=[H], writes=[a_])
                        c.op("pe", lambda e: e.matmul(pn[:, :512], lhsT=onesF[:], rhs=a_[:], start=(tt == 0), stop=(tt == nch - 1)), reads=[onesF, a_], writes=[pn])
                    c.op("dve", lambda e: e.tensor_scalar(out=rinv[:], in0=pn[:, :512], scalar1=EPS, scalar2=None, op0=ALU.add), reads=[pn], writes=[rinv])
                    c.op("dve", lambda e: e.reciprocal(out=rinv[:], in_=rinv[:]), reads=[rinv], writes=[rinv])
                    for tt in range(nch):
                        c.op("dve", lambda e: e.tensor_tensor(out=H[:, tt, :], in0=H[:, tt, :], in1=rinv[:], op=ALU.mult), reads=[H, rinv], writes=[H])
                c.op("dve", lambda e: e.tensor_tensor(out=hs[:], in0=hw[0][:], in1=hw[1][:], op=ALU.add), reads=[hw[0], hw[1]], writes=[hs])
                c.op("pool", lambda e: e.tensor_tensor(out=hd[:], in0=hw[0][:], in1=hw[1][:], op=ALU.subtract), reads=[hw[0], hw[1]], writes=[hd])
                cs = slice(half * 512, (half + 1) * 512)
                for kc in range(nch):
                    pa = self.pb()
                    for tt in range(nch):
                        c.op("pe", lambda e: e.matmul(pa[:, :512], lhsT=CA[:, tt, kc * 128:(kc + 1) * 128], rhs=hs[:, tt, :], start=(tt == 0), stop=(tt == nch - 1)), reads=[CA, hs], writes=[pa])
                    pbd = self.pb()
                    for tt in range(nch):
                        c.op("pe", lambda e: e.matmul(pbd[:, :512], lhsT=MB[:, tt, kc * 128:(kc + 1) * 128], rhs=hd[:, tt, :], start=(tt == 0), stop=(tt == nch - 1)), reads=[MB, hd], writes=[pbd])
                    oa = to[nto[0] % 3]
                    nto[0] += 1
                    c.op("act", lambda e: e.copy(out=oa[:], in_=pa[:, :512]), reads=[pa], writes=[oa])
                    c.dma("sp", TAB[order, kc][:, cs], oa[:], reads=[oa])
                    ob = to[nto[0] % 3]
                    nto[0] += 1
                    if kc > 0:
                        c.op("act", lambda e: e.copy(out=ob[:], in_=pbd[:, :512]), reads=[pbd], writes=[ob])
                        c.dma("sp", TAB[order, nch + kc][:, cs], ob[:], reads=[ob])
                    else:
                        pbs = self.pb()
                        for tt in range(nch):
                            c.op("pe", lambda e: e.matmul(pbs[:, :512], lhsT=MB[:, tt, 0:128], rhs=hs[:, tt, :], start=(tt == 0), stop=(tt == nch - 1)), reads=[MB, hs], writes=[pbs])
                        c.op("dve", lambda e: e.tensor_scalar(out=ob[:], in0=pbd[:, :512], scalar1=m0[:, 0:1], scalar2=None, op0=ALU.mult), reads=[pbd, m0], writes=[ob])
                        c.dma("sp", TAB[order, nch][:, cs], ob[:], reads=[ob])
                        t1, t2 = tf[0], tf[1]
                        c.op("dve", lambda e: e.tensor_scalar(out=t1[:], in0=pa[:, :512], scalar1=m0[:, 0:1], scalar2=None, op0=ALU.mult), reads=[pa, m0], writes=[t1])
                        c.op("dve", lambda e: e.tensor_scalar(out=t2[:], in0=pbs[:, :512], scalar1=m0[:, 1:2], scalar2=None, op0=ALU.mult), reads=[pbs, m0], writes=[t2])
                        oc = to[nto[0] % 3]
                        nto[0] += 1
                        c.op("pool", lambda e: e.tensor_tensor(out=oc[:], in0=t1[:], in1=t2[:], op=ALU.add), reads=[t1, t2], writes=[oc])
                        c.dma("sp", TAB[order, 2 * nch][:, cs], oc[:], reads=[oc])
        c.end_stage()

    def hy_data(self, l, L, tokbase, B):
        c = self.c
        I = self.I
        nch = L // 128
        ncol = B * 1024
        TAB = self.S["HTAB%d" % L]
        HV, HX = self.S["HV"], self.S["HX"]
        MO = self.scr("MO", [3, 8, 128, NT], BF16)
        Z1 = self.scr("HZ1", [8, 128, NT], F32)
        idB = self.identB
        c.begin_stage()
        CA = c.sb([128, nch, L], BF16, "CA")
        MB = c.sb([128, nch, L], BF16, "MB")
        IA = c.sb([128, nch, L], BF16, "IA")
        IB = c.sb([128, nch, L], BF16, "IB")
        for t, nm in ((CA, "CA"), (MB, "MB"), (IA, "IA"), (IB, "IB")):
            c.dma("pool", t[:], I["%s%d" % (nm, L)].rearrange("(t p) k -> p t k", p=128), writes=[t])
        AH = c.sb([128, nch, 1024], BF16, "AH")
        HB1 = c.sb([128, nch, 1024], BF16, "HB1")
        HA2 = c.sb([128, 1024], BF16, "HA2")
        hb = c.sb([128, 16], F32, "hbias")
        self.load_cols(hb, hb[:], I["hy_bias"][l].rearrange("o (n p) -> (o n) p", p=128), 16)
        z = c.sb([128, nch, ncol], BF16, "z")
        Y = c.sb([128, 2 * nch, ncol], BF16, "Y")
        ntok = B * L
        rowv = [c.sb([128, ntok], BF16, "hrv") for _ in range(2)]
        rowx = [c.sb([128, ntok], F32, "hrx") for _ in range(2)]
        rowz = [c.sb([128, ntok], F32, "hrz") for _ in range(2)]
        rowo = [c.sb([128, ntok], BF16, "hro") for _ in range(2)]
        tm = [c.sb([128, 512], F32, "htm") for _ in range(4)]

        def to_tokmajor(src_rows_fn):
            for ch in range(8):
                row = src_rows_fn(ch)
                for b in range(B):
                    for tq in range(0, nch, 4):
                        ps = self.pb()
                        pb16 = ps[:].bitcast(BF16)
                        nq = min(4, nch - tq)
                        for j in range(nq):
                            tt = tq + j
                            t0 = b * L + tt * 128
                            c.op("pe", lambda e: e.transpose(out=pb16[:, j * 128:(j + 1) * 128], in_=row[:, t0:t0 + 128], identity=idB[:]), reads=[row, idB], writes=[ps])
                        for j in range(nq):
                            tt = tq + j
                            eng = "dve" if j % 2 == 0 else "act"
                            dst = z[:, tt, b * 1024 + ch * 128:b * 1024 + (ch + 1) * 128]
                            if eng == "dve":
                                c.op("dve", lambda e: e.tensor_copy(out=dst, in_=pb16[:, j * 128:(j + 1) * 128]), reads=[ps], writes=[z])
                            else:
                                c.op("act", lambda e: e.copy(out=dst, in_=pb16[:, j * 128:(j + 1) * 128]), reads=[ps], writes=[z])

        for order in range(2):
            c.dma("sp", AH[:], TAB[order, 0:nch].rearrange("k p c -> p k c"), writes=[AH])
            c.dma("sp", HB1[:], TAB[order, nch:2 * nch].rearrange("k p c -> p k c"), writes=[HB1])
            c.dma("sp", HA2[:], TAB[order, 2 * nch], writes=[HA2])
            if order == 0:
                def rows_v(ch):
                    r = rowv[ch % 2]
                    c.dma("sp", r[:], HV[ch][:, tokbase:tokbase + ntok], writes=[r])
                    return r
                to_tokmajor(rows_v)
            else:
                def rows_z(ch):
                    rz = rowz[ch % 2]
                    r = rowv[ch % 2]
                    c.dma("sp", rz[:], Z1[ch][:, tokbase:tokbase + ntok], writes=[rz])
                    c.op("act", lambda e: e.copy(out=r[:], in_=rz[:]), reads=[rz], writes=[r])
                    return r
                to_tokmajor(rows_z)
            for ct in range(ncol // 512):
                c0 = (ct * 512) % 1024
                for kc in range(nch):
                    pa = self.pb()
                    for tt in range(nch):
                        c.op("pe", lambda e: e.matmul(pa[:, :512], lhsT=CA[:, tt, kc * 128:(kc + 1) * 128], rhs=z[:, tt, ct * 512:(ct + 1) * 512], start=(tt == 0), stop=(tt == nch - 1)), reads=[CA, z], writes=[pa])
                    pq = self.pb()
                    for tt in range(nch):
                        c.op("pe", lambda e: e.matmul(pq[:, :512], lhsT=MB[:, tt, kc * 128:(kc + 1) * 128], rhs=z[:, tt, ct * 512:(ct + 1) * 512], start=(tt == 0), stop=(tt == nch - 1)), reads=[MB, z], writes=[pq])
                    t1, t2, t3, t4 = tm
                    ah = AH[:, kc, c0:c0 + 512]
                    h1 = HB1[:, kc, c0:c0 + 512]
                    a2 = HA2[:, c0:c0 + 512] if kc == 0 else ah
                    c.op("dve", lambda e: e.tensor_tensor(out=t1[:], in0=pa[:, :512], in1=ah, op=ALU.mult), reads=[pa, AH], writes=[t1])
                    c.op("dve", lambda e: e.tensor_tensor(out=t2[:], in0=pq[:, :512], in1=h1, op=ALU.mult), reads=[pq, HB1], writes=[t2])
                    c.op("pool", lambda e: e.tensor_tensor(out=Y[:, kc, ct * 512:(ct + 1) * 512], in0=t1[:], in1=t2[:], op=ALU.subtract), reads=[t1, t2], writes=[Y])
                    c.op("dve", lambda e: e.tensor_tensor(out=t3[:], in0=pa[:, :512], in1=h1, op=ALU.mult), reads=[pa, HB1], writes=[t3])
                    c.op("dve", lambda e: e.tensor_tensor(out=t4[:], in0=pq[:, :512], in1=a2, op=ALU.mult), reads=[pq, HA2, AH], writes=[t4])
                    c.op("pool", lambda e: e.tensor_tensor(out=Y[:, nch + kc, ct * 512:(ct + 1) * 512], in0=t3[:], in1=t4[:], op=ALU.add), reads=[t3, t4], writes=[Y])
            for ch in range(8):
                rx = rowx[ch % 2]
                c.dma("sp", rx[:], HX[order * 8 + ch][:, tokbase:tokbase + ntok], writes=[rx])
                if order == 0:
                    rb = rowv[ch % 2]
                    c.dma("sp", rb[:], HV[ch][:, tokbase:tokbase + ntok], writes=[rb])
                    ro = rowz[ch % 2]
                else:
                    rb = rowz[ch % 2]
                    c.dma("sp", rb[:], Z1[ch][:, tokbase:tokbase + ntok], writes=[rb])
                    ro = rowo[ch % 2]
                for b in range(B):
                    for r0 in range(0, L, 512):
                        n = min(512, L - r0)
                        ps = self.pacc()
                        for kk in range(2 * nch):
                            M = IA if kk < nch else IB
                            c.op("pe", lambda e: e.matmul(ps[:, :n], lhsT=Y[:, kk, b * 1024 + ch * 128:b * 1024 + (ch + 1) * 128], rhs=M[:, kk % nch, r0:r0 + n], start=(kk == 0), stop=(kk == 2 * nch - 1)),
                                 reads=[Y, M], writes=[ps])
                        g0 = b * L + r0
                        t1 = tm[0]
                        c.op("dve", lambda e: e.scalar_tensor_tensor(out=t1[:, :n], in0=rb[:, g0:g0 + n], scalar=hb[:, order * 8 + ch:order * 8 + ch + 1], in1=ps[:, :n], op0=ALU.mult, op1=ALU.add), reads=[rb, hb, ps], writes=[t1])
                        c.op("dve", lambda e: e.tensor_tensor(out=ro[:, g0:g0 + n], in0=t1[:, :n], in1=rx[:, g0:g0 + n], op=ALU.mult), reads=[t1, rx], writes=[ro])
                if order == 0:
                    c.dma("sp", Z1[ch][:, tokbase:tokbase + ntok], ro[:], reads=[ro])
                else:
                    c.dma("sp", MO[1][ch][:, tokbase:tokbase + ntok], ro[:], reads=[ro])
            c.barrier()
        c.end_stage()

    def stage_hy(self, l):
        self.hy_tables(l, 256)
        self.hy_data(l, 256, 0, 4)
        self.hy_tables(l, 1024)
        self.hy_data(l, 1024, 1024, 1)

_CACHE = {}


def build_full():
    kb = KB5()
    c = kb.c
    kb.stage_input()
    c.mark('input')
    kb.stage_mod()
    c.mark('mod')
    XA = kb.S["XT0"]
    XB = kb.scr("XT1", [16, 128, NT], F32)
    XC = kb.scr("XT2", [16, 128, NT], F32)
    cur = XA
    for l in range(2):
        kb.stage_norm(cur, l, 0)
        c.mark('norm1')
        kb.stage_proj(l)
        c.end_stage()
        c.mark('proj')
        kb.stage_ret(l)
        c.mark('ret')
        kb.stage_hy(l)
        c.mark('hy')
        kb.stage_dn(l)
        c.mark('dn')
        kb.stage_merge(l, cur, XB)
        c.mark('merge+wo')
        kb.stage_norm(XB, l, 1)
        c.mark('norm2')
        kb.stage_ffn_up(l)
        c.end_stage()
        c.mark('ffn_up')
        kb.stage_ffn_down(l, XB, XC)
        c.mark('ffn_down')
        cur, XB, XC = XC, cur, XB
    kb.stage_final(cur)
    c.mark('final')
    c.finish()
    return kb


def kernel(**inputs):
    z = {k: np.ascontiguousarray(np.asarray(v)) for k, v in inputs.items()}
    if "kb" not in _CACHE:
        _CACHE["kb"] = build_full()
    kb = _CACHE["kb"]
    in_maps = []
    for core in range(8):
        m = dict(kb.consts)
        for k in WSPEC:
            m[k] = z[k]
        xp = z["x_prompt"][4 * core:4 * core + 4].reshape(1024, 2048)
        xs = z["x_sample"][core // 2]
        m["xin"] = np.ascontiguousarray(np.concatenate([xp, xs], 0))
        m["sret"] = np.ascontiguousarray(z["state_ret"][core // 2])
        m["sdn"] = np.ascontiguousarray(z["state_dn"][core // 2])
        m["cvec"] = np.ascontiguousarray(np.stack([z["c_ctx"], z["c"][core // 2]]))
        in_maps.append(m)
    res = run_bass_kernel_spmd(kb.nc, in_maps, core_ids=list(range(8))).results
    y_prompt = np.concatenate([np.asarray(res[c]["y"])[:1024].reshape(4, 256, 2048) for c in range(8)], 0).astype(np.float32)
    y_sample = np.stack([np.asarray(res[2 * b]["y"])[1024:] for b in range(4)], 0).astype(np.float32)
    new_ret = np.concatenate([np.asarray(res[c]["osret"]) for c in range(8)], 0).astype(np.float32)
    new_dn = np.concatenate([np.asarray(res[c]["osdn"]) for c in range(8)], 0).astype(np.float32)
    return (y_prompt, y_sample, new_ret, new_dn)
```

```python
import numpy as np
from contextlib import ExitStack
import concourse.bass as bass
import concourse.mybir as mybir
from concourse.bass_utils import run_bass_kernel_spmd

F32 = mybir.dt.float32
BF16 = mybir.dt.bfloat16
I32 = mybir.dt.int32
AF = mybir.ActivationFunctionType
ALU = mybir.AluOpType
AX = mybir.AxisListType


class Buf:
    __slots__ = ("t", "name", "w", "r", "psum")

    def __init__(self, t, name, psum=False):
        self.t = t
        self.name = name
        self.psum = psum
        self.w = None
        self.r = {}

    def __getitem__(self, idx):
        return self.t[idx]


class Ctx:
    SEM_LIMIT = 30000
    NDMA = 24

    def __init__(self, nc):
        self.nc = nc
        self.es = ExitStack()
        self.eng = {"pe": nc.tensor, "act": nc.scalar, "dve": nc.vector, "pool": nc.gpsimd, "sp": nc.sync}
        self.cur = {}
        self.waited = {k: {} for k in self.eng}
        self.nsem = 0
        for k in self.eng:
            self._new_sem(k)
        self.dpool = {}
        for q in ("sp", "pool", "act"):
            self.dpool[q] = [[self._alloc_sem("d%s%d" % (q, i)), 0] for i in range(self.NDMA)]
        self.dnext = {q: 0 for q in self.dpool}
        self.stage_es = None
        self.uid = 0
        self.tot = {}
        self.marks = []

    def mark(self, name):
        self.marks.append((name, dict(self.tot)))

    def _alloc_sem(self, name):
        self.nsem += 1
        s = self.es.enter_context(self.nc.semaphore("%s_%d" % (name, self.nsem)))
        if not hasattr(self, "allsems"):
            self.allsems = []
        self.allsems.append(s)
        return s

    def _new_sem(self, k):
        self.cur[k] = [self._alloc_sem("e" + k), 0]

    def begin_stage(self):
        if not hasattr(self, "stack"):
            self.stack = []
        self.stack.append(ExitStack())
        self.stage_es = self.stack[-1]

    def end_stage(self):
        self.barrier()
        self.stack.pop().close()
        self.stage_es = self.stack[-1] if self.stack else None

    def sb(self, shape, dt, name="t", persist=False):
        self.uid += 1
        nm = "%s_%d" % (name, self.uid)
        es = self.es if persist else self.stage_es
        t = es.enter_context(self.nc.sbuf_tensor(nm, list(shape), dt))
        return Buf(t, nm)

    def ps(self, shape, dt, name="p"):
        self.uid += 1
        nm = "%s_%d" % (name, self.uid)
        t = self.es.enter_context(self.nc.psum_tensor(nm, list(shape), dt))
        return Buf(t, nm, psum=True)

    def _wait(self, k, tok):
        if tok is None:
            return
        sem, val = tok
        if k == "pe" and sem is self.cur["pe"][0]:
            return
        w = self.waited[k]
        key = id(sem)
        if w.get(key, (None, 0))[1] >= val:
            return
        w[key] = (sem, val)
        self.eng[k].wait_ge(sem, val)

    def _deps(self, k, reads, writes):
        for b in reads:
            self._wait(k, b.w)
            if b.psum:
                for tok in list(b.r.values()):
                    self._wait(k, tok)
        for b in writes:
            self._wait(k, b.w)
            for tok in list(b.r.values()):
                self._wait(k, tok)

    def _commit(self, tok, reads, writes):
        for b in reads:
            b.r[id(tok[0])] = tok
        for b in writes:
            b.w = tok
            b.r = {}

    def op(self, k, fn, reads=(), writes=()):
        self._deps(k, reads, writes)
        c = self.cur[k]
        if c[1] >= self.SEM_LIMIT:
            self._new_sem(k)
            c = self.cur[k]
        c[1] += 1
        self.tot[k] = self.tot.get(k, 0) + 1
        fn(self.eng[k]).then_inc(c[0], 1)
        tok = (c[0], c[1])
        self._commit(tok, reads, writes)
        return tok

    def dma(self, q, out, in_, reads=(), writes=(), **kw):
        pool = self.dpool[q]
        i = self.dnext[q]
        self.dnext[q] = (i + 1) % len(pool)
        slot = pool[i]
        if slot[1] > 0:
            self._wait(q, (slot[0], slot[1]))
        if slot[1] >= self.SEM_LIMIT:
            slot[0] = self._alloc_sem("d" + q)
            slot[1] = 0
        self._deps(q, reads, writes)
        slot[1] += 16
        self.eng[q].dma_start(out=out, in_=in_, **kw).then_inc(slot[0], 16)
        tok = (slot[0], slot[1])
        self._commit(tok, reads, writes)
        return tok

    def barrier(self):
        toks = [(c[0], c[1]) for c in self.cur.values() if c[1] > 0]
        for q in self.dpool:
            for slot in self.dpool[q]:
                if slot[1] > 0:
                    toks.append((slot[0], slot[1]))
        for k in self.eng:
            for tok in toks:
                self._wait(k, tok)

    def finish(self):
        self.barrier()
        self.es.close()

import math
import numpy as np


def make_consts():
    f = np.float32
    C = {}
    i = np.arange(128)
    C["ident"] = np.eye(128, dtype=f)
    C["ones"] = np.ones((128, 128), f)
    C["UF"] = (i[:, None] <= i[None, :]).astype(f)
    C["UB"] = (i[:, None] >= i[None, :]).astype(f)
    C["SU"] = (i[:, None] < i[None, :]).astype(f)
    C["SL"] = (i[:, None] > i[None, :]).astype(f)
    L = 1024
    pos = np.arange(L, dtype=np.float64)
    pos_r = np.floor(pos / 64)
    pos_c = pos % 64
    inv = 10000.0 ** (-np.arange(16, dtype=np.float64) / 16)
    cosT = np.zeros((128, L))
    sinT = np.zeros((128, L))
    for p in range(128):
        q = p % 64
        half = q // 32
        x2 = (q % 32) // 16
        fi = q % 16
        ang = (pos_r if half == 0 else pos_c) * inv[fi]
        cosT[p] = np.cos(ang)
        sinT[p] = np.sin(ang) * (1.0 if x2 else -1.0)
    C["ropeC"] = cosT.astype(f)
    C["ropeS"] = sinT.astype(f)
    for nm, LL in (("S", 1024), ("P", 256)):
        u = np.arange(2 * LL - 128)[None, :]
        p = np.arange(128)[:, None]
        d = u - p - (LL - 128)
        C["dpos" + nm] = np.maximum(d, 0).astype(f)
        C["dneg" + nm] = np.maximum(-d, 0).astype(f)
        C["dz" + nm] = (d == 0).astype(f)
    j = np.arange(128)[:, None] + 128 * np.arange(2)[None, :]
    C["expfb"] = np.concatenate([255 - j, j], axis=1).astype(f)
    ii = np.arange(1024)[None, :].repeat(128, 0)
    C["idx1"] = (ii + 1).astype(f)
    C["idx2"] = (1024 - ii).astype(f)
    for LL in (256, 1024):
        t = np.linspace(0.0, 1.0, LL, dtype=np.float32)[:, None].astype(np.float64)
        wpos = 2.0 * math.pi * np.arange(LL, dtype=np.float64)[:, None] / LL
        fr = np.linspace(1e-4, 15, 16, dtype=np.float32)[None, :].astype(np.float64)
        feats = np.concatenate([t, np.cos(fr * wpos), -np.sin(fr * wpos)], axis=-1)
        C["featsT%d" % LL] = np.ascontiguousarray(feats.T).astype(f)
        deltas = np.abs(np.linspace(math.log(1e-2) / 0.3, math.log(1e-2) / 1.5, 1024, dtype=np.float32)).astype(np.float64)
        C["win%d" % LL] = np.exp(-t * deltas[None, :]).astype(f)
        N = 2 * LL
        tt = np.arange(LL, dtype=np.float64)[:, None]
        kk = np.arange(LL, dtype=np.float64)[None, :]
        ang = 2.0 * math.pi * tt * kk / N
        CA = np.cos(ang)
        MB = -np.sin(ang)
        MB[:, 0] = (-1.0) ** np.arange(LL)
        IA = (2.0 / N) * np.cos(ang.T)
        IA[0, :] = 1.0 / N
        IB = -(2.0 / N) * np.sin(ang.T)
        IB[0, :] = ((-1.0) ** np.arange(LL)) / N
        C["CA%d" % LL] = CA.astype(f)
        C["MB%d" % LL] = MB.astype(f)
        C["IA%d" % LL] = IA.astype(f)
        C["IB%d" % LL] = IB.astype(f)
    ii, jj = np.meshgrid(np.arange(128), np.arange(128), indexing="ij")
    ML = np.stack([(((ii >> (s + 1)) == (jj >> (s + 1))) & ((ii >> s) != (jj >> s)) & (ii > jj)).astype(f) for s in range(7)])
    MU = np.ascontiguousarray(ML.transpose(0, 2, 1))
    C["MLU"] = np.ascontiguousarray(np.concatenate([ML, MU], axis=2).transpose(1, 0, 2))
    C["MUL"] = np.ascontiguousarray(np.concatenate([MU, ML], axis=2).transpose(1, 0, 2))
    m0 = np.ones((128, 2), f)
    m0[0, 0] = 0.0
    m0[:, 1] = 1.0 - m0[:, 0]
    C["m0"] = m0
    C["eps"] = np.full((128, 1), 1e-6, f)
    return C


import math

NT = 2048
TILES = [(0, 512), (512, 512), (1024, 512), (1536, 512)]
EPS = 1e-6

WSPEC = dict(
    w_ada=[2, 2048, 12288], b_ada=[2, 12288], norm1=[2, 2048], w_in=[2, 2048, 16416], ret_decay=[2, 2, 8],
    hy_short=[2, 3, 3072], hy_w1=[2, 33, 64], hy_b1=[2, 64], hy_freq1=[2, 64], hy_w2=[2, 64, 64], hy_b2=[2, 64],
    hy_freq2=[2, 64], hy_w3=[2, 64, 4096], hy_bias=[2, 2, 1024], dn_conv=[2, 3, 3072], dn_a_log=[2, 2, 8],
    dn_dt_bias=[2, 2, 8], dn_norm=[2, 128], p_ret=[2, 1024, 2048], p_hy=[2, 1024, 2048], p_dn=[2, 1024, 2048],
    w_o=[2, 2048, 2048], norm2=[2, 2048], w_up=[2, 2048, 11008], ffn_conv=[2, 3, 11008], w_down=[2, 5504, 2048],
    norm_f=[2048])


class KB:
    def __init__(self, stop_after=None, dbg=(), wspec=None, ext_in=()):
        self.stop_after = stop_after
        self.dbg = set(dbg)
        nc = self.nc = bass.Bass("TRN2", target_bir_lowering=False)
        self.c = Ctx(nc)
        self.I = {}
        self.consts = make_consts()
        for k, v in self.consts.items():
            self.I[k] = nc.dram_tensor(k, list(v.shape), F32, kind="ExternalInput").ap()
        self.ext_in = set(ext_in)
        for k, s in (wspec or WSPEC).items():
            self.I[k] = nc.dram_tensor(k, s, F32, kind="ExternalInput").ap()
        self.I["xin"] = nc.dram_tensor("xin", [NT, 2048], F32, kind="ExternalInput").ap()
        self.I["sret"] = nc.dram_tensor("sret", [2, 2, 8, 64, 128], F32, kind="ExternalInput").ap()
        self.I["sdn"] = nc.dram_tensor("sdn", [2, 2, 8, 128, 128], F32, kind="ExternalInput").ap()
        self.I["cvec"] = nc.dram_tensor("cvec", [2, 2048], F32, kind="ExternalInput").ap()
        self.O = {}
        self.O["y"] = nc.dram_tensor("y", [NT, 2048], F32, kind="ExternalOutput").ap()
        self.O["osret"] = nc.dram_tensor("osret", [4, 2, 2, 8, 64, 128], F32, kind="ExternalOutput").ap()
        self.O["osdn"] = nc.dram_tensor("osdn", [4, 2, 2, 8, 128, 128], F32, kind="ExternalOutput").ap()
        self.S = {}
        c = self.c
        self.PB = [c.ps([128, 512], F32, "pb") for _ in range(8)]
        self.pbi = 0
        self.identF = c.sb([128, 128], F32, "identF", persist=True)
        self.identB = c.sb([128, 128], BF16, "identB", persist=True)
        self.onesB = c.sb([128, 128], BF16, "onesB", persist=True)
        self.onesF = c.sb([128, 128], F32, "onesF", persist=True)
        self.epsT = c.sb([128, 1], F32, "epsT", persist=True)
        self.rows = c.sb([128, 128], F32, "rows", persist=True)
        c.dma("sp", self.identF[:], self.I["ident"], writes=[self.identF])
        c.dma("pool", self.identB[:], self.I["ident"], writes=[self.identB])
        c.dma("pool", self.onesB[:], self.I["ones"], writes=[self.onesB])
        c.dma("sp", self.onesF[:], self.I["ones"], writes=[self.onesF])
        c.dma("sp", self.epsT[:], self.I["eps"], writes=[self.epsT])
        self.HT = None
        self.modT = c.sb([128, 2, 96, 2], F32, "modT", persist=True)
        self.sca = c.sb([128, 2, 2, 16, 2], F32, "sca", persist=True)
        self.gbraw = c.sb([128, 16, 32], F32, "gbraw", persist=True)

    def scr(self, name, shape, dt):
        if name not in self.S:
            kind = "ExternalOutput" if name in self.dbg else ("ExternalInput" if name in self.ext_in else "Internal")
            self.S[name] = self.nc.dram_tensor(name, list(shape), dt, kind=kind).ap()
        return self.S[name]

    def pb(self):
        self.pbi = (self.pbi + 1) % 6
        return self.PB[self.pbi]

    def pacc(self):
        self.pai = (getattr(self, "pai", 0) + 1) % 2
        return self.PB[6 + self.pai]

    def load_cols(self, dstbuf, dst_ap, src_rows, n):
        c = self.c
        rows = self.rows
        c.dma("sp", rows[:n, :], src_rows, writes=[rows])
        ps = self.pb()
        idf = self.identF
        c.op("pe", lambda e: e.transpose(out=ps[:, :n], in_=rows[:n, :], identity=idf[:n, :n]), reads=[rows, idf], writes=[ps])
        c.op("dve", lambda e: e.tensor_copy(out=dst_ap, in_=ps[:, :n]), reads=[ps], writes=[dstbuf])

    def stage_input(self):
        c = self.c
        XT = self.scr("XT0", [16, 128, NT], F32)
        c.begin_stage()
        xs = [c.sb([128, 2048], F32, "xs") for _ in range(2)]
        xo = [c.sb([128, 16, 128], F32, "xo") for _ in range(2)]
        idf = self.identF
        for tt in range(16):
            a = xs[tt % 2]
            o = xo[tt % 2]
            c.dma("sp", a[:], self.I["xin"][tt * 128:(tt + 1) * 128, :], writes=[a])
            for g in range(4):
                ps = self.pb()
                for j in range(4):
                    ch = g * 4 + j
                    c.op("pe", lambda e: e.transpose(out=ps[:, j * 128:(j + 1) * 128], in_=a[:, ch * 128:(ch + 1) * 128], identity=idf[:]),
                         reads=[a, idf], writes=[ps])
                eng = "dve" if g % 2 == 0 else "act"
                if eng == "dve":
                    c.op("dve", lambda e: e.tensor_copy(out=o[:, g * 4:(g + 1) * 4, :], in_=ps[:].rearrange("p (j t) -> p j t", j=4)), reads=[ps], writes=[o])
                else:
                    c.op("act", lambda e: e.copy(out=o[:, g * 4:(g + 1) * 4, :], in_=ps[:].rearrange("p (j t) -> p j t", j=4)), reads=[ps], writes=[o])
            c.dma("sp", XT[:, :, tt * 128:(tt + 1) * 128].rearrange("c p t -> p c t"), o[:], reads=[o])
        c.end_stage()

    def stage_mod(self):
        c = self.c
        I = self.I
        c.begin_stage()
        scT = c.sb([128, 2, 16], F32, "scT")
        self.load_cols(scT, scT[:].rearrange("p b k -> p (b k)"), I["cvec"].rearrange("b (k p) -> (b k) p", p=128), 32)
        c.op("act", lambda e: e.activation(out=scT[:], in_=scT[:], func=AF.Silu), reads=[scT], writes=[scT])
        bada = c.sb([128, 96], F32, "bada")
        nw = c.sb([128, 16], F32, "nw")
        wa = [c.sb([128, 16, 512], F32, "wa") for _ in range(4)]
        mrow = [c.sb([2, 512], F32, "mrow") for _ in range(2)]
        idf = self.identF
        for l in range(2):
            self.load_cols(bada, bada[:], I["b_ada"][l].rearrange("(n p) -> n p", p=128), 96)
            ps = self.pacc()
            for blk in range(24):
                w = wa[blk % 4]
                c.dma(("sp", "act", "pool")[blk % 3], w[:], I["w_ada"][l][:, blk * 512:(blk + 1) * 512].rearrange("(k p) n -> p k n", p=128), writes=[w])
                pr = self.pb()
                for k in range(16):
                    c.op("pe", lambda e: e.matmul(pr[:2, :512], lhsT=scT[:, :, k], rhs=w[:, k, :], start=(k == 0), stop=(k == 15)), reads=[w, scT], writes=[pr])
                mr = mrow[blk % 2]
                c.op("act", lambda e: e.copy(out=mr[:], in_=pr[:2, :512]), reads=[pr], writes=[mr])
                for j in range(4):
                    ch = blk * 4 + j
                    c.op("pe", lambda e: e.transpose(out=ps[:, ch * 2:ch * 2 + 2], in_=mr[:2, j * 128:(j + 1) * 128], identity=idf[:2, :2]), reads=[mr, idf], writes=[ps])
            mt = self.modT
            for b in range(2):
                c.op("dve", lambda e: e.tensor_tensor(out=mt[:, l, :, b], in0=ps[:, 0:192].rearrange("p (c b) -> p c b", b=2)[:, :, b], in1=bada[:], op=ALU.add),
                     reads=[ps, bada], writes=[mt])
            for which, (nm, scb) in enumerate((("norm1", 16), ("norm2", 64))):
                self.load_cols(nw, nw[:], I[nm][l].rearrange("(n p) -> n p", p=128), 16)
                sc = self.sca
                for b in range(2):
                    c.op("dve", lambda e: e.scalar_tensor_tensor(out=sc[:, l, which, :, b], in0=mt[:, l, scb:scb + 16, b], scalar=1.0, in1=nw[:], op0=ALU.add, op1=ALU.mult),
                         reads=[mt, nw], writes=[sc])
        c.end_stage()

    def stage_norm(self, XT, l, which):
        c = self.c
        c.begin_stage()
        self.HT = c.sb([128, 16, NT], BF16, "HT")
        c.begin_stage()
        shb = 0 if which == 0 else 48
        xs = [c.sb([128, 16, 256], F32, "nx") for _ in range(2)]
        sq = c.sb([128, 16, 256], BF16, "nsq")
        tmps = [c.sb([128, 256], F32, "ntmp") for _ in range(6)]
        r = c.sb([128, 256], F32, "nr")
        HT, sc, mt, ones, eps = self.HT, self.sca, self.modT, self.onesB, self.epsT
        for ti in range(8):
            t0 = ti * 256
            b = 0 if t0 < 1024 else 1
            x = xs[ti % 2]
            c.dma("sp", x[:], XT[:, :, t0:t0 + 256].rearrange("c p t -> p c t"), writes=[x])
            c.op("act", lambda e: e.activation(out=sq[:], in_=x[:], func=AF.Square), reads=[x], writes=[sq])
            ps = self.pb()
            for ch in range(16):
                c.op("pe", lambda e: e.matmul(ps[:, :256], lhsT=ones[:], rhs=sq[:, ch, :], start=(ch == 0), stop=(ch == 15)), reads=[ones, sq], writes=[ps])
            c.op("act", lambda e: e.activation(out=r[:], in_=ps[:, :256], func=AF.Sqrt, scale=1.0 / 2048, bias=eps[:, 0:1]), reads=[ps, eps], writes=[r])
            c.op("dve", lambda e: e.reciprocal(out=r[:], in_=r[:]), reads=[r], writes=[r])
            for ch in range(16):
                tmp = tmps[ch % 6]
                c.op("dve", lambda e: e.tensor_tensor(out=tmp[:], in0=x[:, ch, :], in1=r[:], op=ALU.mult), reads=[x, r], writes=[tmp])
                c.op("act", lambda e: e.activation(out=HT[:, ch, t0:t0 + 256], in_=tmp[:], func=AF.Identity,
                                                   scale=sc[:, l, which, ch, b:b + 1], bias=mt[:, l, shb + ch, b:b + 1]), reads=[tmp, sc, mt], writes=[HT])
        c.end_stage()


class KB2(KB):
    def wbufs(self, KC, n=3, width=256):
        return [self.c.sb([128, KC, width], BF16, "wb") for _ in range(n)]

    def lin_fm(self, W, blocks, IN, KC, epi, tiles=TILES, wb=None, prep=None):
        c = self.c
        if wb is None:
            wb = self.wbufs(KC)
        for bi, (c0, ncol) in enumerate(blocks):
            self.wbi = getattr(self, "wbi", 0) + 1
            w = wb[self.wbi % len(wb)]
            c.dma("pool", w[:, :KC, :ncol], W[:, c0:c0 + ncol].rearrange("(k p) n -> p k n", p=128), writes=[w])
            aux = prep(w, c0, ncol) if prep else None
            for off in range(0, ncol, 128):
                n = min(128, ncol - off)
                for ti, (t0, nt) in enumerate(tiles):
                    ps = self.pb()
                    for k in range(KC):
                        c.op("pe", lambda e: e.matmul(ps[:n, :nt], lhsT=w[:, k, off:off + n], rhs=IN[:, k, t0:t0 + nt], start=(k == 0), stop=(k == KC - 1)),
                             reads=[w, IN], writes=[ps])
                    epi(c0 + off, n, ti, t0, nt, ps, (aux, off))

    def lin_tm(self, W, blocks, IN, KC, epi, ttiles, wb=None):
        c = self.c
        if wb is None:
            wb = self.wbufs(KC)
        for bi, (c0, ncol) in enumerate(blocks):
            self.wbi = getattr(self, "wbi", 0) + 1
            w = wb[self.wbi % len(wb)]
            c.dma("pool", w[:, :KC, :ncol], W[:, c0:c0 + ncol].rearrange("(k p) n -> p k n", p=128), writes=[w])
            for tt in ttiles:
                ps = self.pb()
                for k in range(KC):
                    c.op("pe", lambda e: e.matmul(ps[:, :ncol], lhsT=IN[:, k, tt * 128:(tt + 1) * 128], rhs=w[:, k, :ncol], start=(k == 0), stop=(k == KC - 1)),
                         reads=[w, IN], writes=[ps])
                epi(c0, ncol, tt, ps)

    @staticmethod
    def blocks(a, b, step=256):
        return [(x, min(step, b - x)) for x in range(a, b, step)]

    def conv_row(self, src, dst, wt, ci, ntap_chunks):
        c = self.c
        w0 = wt[:, 0 * ntap_chunks + ci:0 * ntap_chunks + ci + 1]
        w1 = wt[:, 1 * ntap_chunks + ci:1 * ntap_chunks + ci + 1]
        w2 = wt[:, 2 * ntap_chunks + ci:2 * ntap_chunks + ci + 1]
        c.op("act", lambda e: e.activation(out=dst[:], in_=src[:], func=AF.Copy, scale=w1), reads=[src, wt], writes=[dst])
        sp = src[:, 0:1024].rearrange("p (s t) -> p s t", t=256)
        dp = dst[:, 0:1024].rearrange("p (s t) -> p s t", t=256)
        c.op("dve", lambda e: e.scalar_tensor_tensor(out=dp[:, :, 1:256], in0=sp[:, :, 0:255], scalar=w0, in1=dp[:, :, 1:256], op0=ALU.mult, op1=ALU.add), reads=[src, wt, dst], writes=[dst])
        c.op("dve", lambda e: e.scalar_tensor_tensor(out=dp[:, :, 0:255], in0=sp[:, :, 1:256], scalar=w2, in1=dp[:, :, 0:255], op0=ALU.mult, op1=ALU.add), reads=[src, wt, dst], writes=[dst])
        c.op("dve", lambda e: e.scalar_tensor_tensor(out=dst[:, 1025:2048], in0=src[:, 1024:2047], scalar=w0, in1=dst[:, 1025:2048], op0=ALU.mult, op1=ALU.add), reads=[src, wt, dst], writes=[dst])
        c.op("dve", lambda e: e.scalar_tensor_tensor(out=dst[:, 1024:2047], in0=src[:, 1025:2048], scalar=w2, in1=dst[:, 1024:2047], op0=ALU.mult, op1=ALU.add), reads=[src, wt, dst], writes=[dst])

    def stage_proj(self, l):
        c = self.c
        I = self.I
        W = I["w_in"][l]
        HT = self.HT
        c.begin_stage()
        wb = self.wbufs(16, 3, 512)
        QK = self.scr("QK", [8, 128, NT], BF16)
        VTM = self.scr("VTM", [16, 128, 1024], BF16)
        KTM = self.scr("KTM", [8, 128, 512], BF16)
        SRG = self.scr("SRG", [8, 128, NT], BF16)
        HV = self.scr("HV", [8, 128, NT], BF16)
        HX = self.scr("HX", [16, 128, NT], F32)
        DQK = self.scr("DQK", [16, 128, NT], BF16)
        DVT = self.scr("DVT", [8, 128, NT], BF16)
        SDZ = self.scr("SDZ", [8, 128, NT], BF16)
        GATE = self.scr("GATE", [48, 128, NT], BF16)
        rowb = [c.sb([128, NT], BF16, "rowb") for _ in range(2)]
        rowf = [c.sb([128, NT], F32, "rowf") for _ in range(2)]
        rowg = [c.sb([128, NT], F32, "rowg") for _ in range(2)]
        cnt = [0]

        ropeC = c.sb([128, 1024], F32, "ropeC")
        ropeS = c.sb([128, 1024], F32, "ropeS")
        c.dma("sp", ropeC[:], I["ropeC"], writes=[ropeC])
        c.dma("sp", ropeS[:], I["ropeS"], writes=[ropeS])
        wperm = [c.sb([128, 16, 256], BF16, "wperm") for _ in range(2)]
        t1 = c.sb([128, 512], F32, "t1")
        t2 = c.sb([128, 512], F32, "t2")
        pc = [0]

        def prep_qk(w, c0, ncol):
            wp = wperm[pc[0] % 2]
            pc[0] += 1
            src = w[:, :, 0:256].rearrange("p k (a two s) -> p k a two s", two=2, s=16)
            dst = wp[:].rearrange("p k (a two s) -> p k a two s", two=2, s=16)
            c.op("dve", lambda e: e.tensor_copy(out=dst[:, :, :, 0, :], in_=src[:, :, :, 1, :]), reads=[w], writes=[wp])
            c.op("act", lambda e: e.copy(out=dst[:, :, :, 1, :], in_=src[:, :, :, 0, :]), reads=[w], writes=[wp])
            return wp

        def epi_qk(col0, n, ti, t0, nt, ps, auxoff):
            wp, off = auxoff
            ci = col0 // 128
            row = rowb[ci % 2]
            scale = 1.0 if ci < 4 else 0.125
            if ti < 2:
                c.op("act", lambda e: e.activation(out=row[:, t0:t0 + nt], in_=ps[:, :nt], func=AF.Copy, scale=scale), reads=[ps], writes=[row])
            else:
                ps2 = self.pb()
                for k in range(16):
                    c.op("pe", lambda e: e.matmul(ps2[:, :nt], lhsT=wp[:, k, off:off + 128], rhs=HT[:, k, t0:t0 + nt], start=(k == 0), stop=(k == 15)), reads=[wp, HT], writes=[ps2])
                s0 = t0 - 1024
                c.op("dve", lambda e: e.tensor_tensor(out=t1[:, :nt], in0=ps[:, :nt], in1=ropeC[:, s0:s0 + nt], op=ALU.mult), reads=[ps, ropeC], writes=[t1])
                c.op("dve", lambda e: e.tensor_tensor(out=t2[:, :nt], in0=ps2[:, :nt], in1=ropeS[:, s0:s0 + nt], op=ALU.mult), reads=[ps2, ropeS], writes=[t2])
                c.op("dve", lambda e: e.tensor_tensor(out=t1[:, :nt], in0=t1[:, :nt], in1=t2[:, :nt], op=ALU.add), reads=[t1, t2], writes=[t1])
                c.op("act", lambda e: e.activation(out=row[:, t0:t0 + nt], in_=t1[:, :nt], func=AF.Copy, scale=scale), reads=[t1], writes=[row])
            if ti == 3:
                c.dma("sp", QK[ci], row[:], reads=[row])

        self.lin_fm(W, self.blocks(0, 1024), HT, 16, epi_qk, wb=wb, prep=prep_qk)

        stv = [c.sb([128, 512], BF16, "stv") for _ in range(4)]

        def epi_v(col0, ncol, tt, ps):
            s = stv[cnt[0] % 4]
            cnt[0] += 1
            if cnt[0] % 2:
                c.op("dve", lambda e: e.tensor_copy(out=s[:, :ncol], in_=ps[:, :ncol]), reads=[ps], writes=[s])
            else:
                c.op("act", lambda e: e.copy(out=s[:, :ncol], in_=ps[:, :ncol]), reads=[ps], writes=[s])
            c.dma("sp", VTM[tt][:, col0 - 1024:col0 - 1024 + ncol], s[:, :ncol], reads=[s])

        def epi_ktm(col0, ncol, tt, ps):
            s = stv[cnt[0] % 4]
            cnt[0] += 1
            c.op("act", lambda e: e.activation(out=s[:, :ncol], in_=ps[:, :ncol], func=AF.Copy, scale=0.125), reads=[ps], writes=[s])
            c.dma("sp", KTM[tt][:, col0 - 512:col0 - 512 + ncol], s[:, :ncol], reads=[s])

        self.lin_tm(W, self.blocks(1024, 2048, 512), HT, 16, epi_v, range(16), wb=wb)
        self.lin_tm(W, self.blocks(512, 1024, 512), HT, 16, epi_ktm, range(8), wb=wb)

        def mk_epi_act(base, dst, func):
            def epi(col0, n, ti, t0, nt, ps, aux):
                ci = (col0 - base) // 128
                row = rowb[ci % 2]
                c.op("act", lambda e: e.activation(out=row[:, t0:t0 + nt], in_=ps[:, :nt], func=func), reads=[ps], writes=[row])
                if ti == 3:
                    c.dma("sp", dst[ci], row[:], reads=[row])
            return epi

        self.lin_fm(W, self.blocks(2048, 3072, 512), HT, 16, mk_epi_act(2048, SRG, AF.Silu), wb=wb)
        self.lin_fm(W, self.blocks(9216, 10240, 512), HT, 16, mk_epi_act(9216, SDZ, AF.Silu), wb=wb)
        self.lin_fm(W, self.blocks(10272, 16416, 512), HT, 16, mk_epi_act(10272, GATE, AF.Sigmoid), wb=wb)

        hyw = c.sb([128, 72], F32, "hyw")
        self.load_cols(hyw, hyw[:], I["hy_short"][l].rearrange("k (n p) -> (k n) p", p=128), 72)
        dnw = c.sb([128, 72], F32, "dnw")
        self.load_cols(dnw, dnw[:], I["dn_conv"][l].rearrange("k (n p) -> (k n) p", p=128), 72)

        def epi_hy(col0, n, ti, t0, nt, ps, aux):
            ci = (col0 - 3072) // 128
            raw = rowf[ci % 2]
            c.op("act", lambda e: e.copy(out=raw[:, t0:t0 + nt], in_=ps[:, :nt]), reads=[ps], writes=[raw])
            if ti == 3:
                cv = rowg[ci % 2]
                self.conv_row(raw, cv, hyw, ci, 24)
                if ci < 8:
                    row = rowb[ci % 2]
                    c.op("act", lambda e: e.copy(out=row[:], in_=cv[:]), reads=[cv], writes=[row])
                    c.dma("sp", HV[ci], row[:], reads=[row])
                else:
                    c.dma("sp", HX[ci - 8], cv[:], reads=[cv])

        self.lin_fm(W, self.blocks(3072, 6144, 512), HT, 16, epi_hy, wb=wb)

        sqb = c.sb([128, NT], BF16, "sqb")
        rns = [c.sb([128, 512], F32, "rn") for _ in range(3)]
        ones, eps = self.onesB, self.epsT

        dn_pending = []

        def epi_dn(col0, n, ti, t0, nt, ps, aux):
            ci = (col0 - 6144) // 128
            raw = rowf[ci % 2]
            c.op("act", lambda e: e.copy(out=raw[:, t0:t0 + nt], in_=ps[:, :nt]), reads=[ps], writes=[raw])
            if ti == 3:
                while dn_pending:
                    dn_pending.pop(0)()
                dn_pending.append(lambda ci=ci, raw=raw: dn_post(ci, raw))

        def dn_post(ci, raw):
            cv = rowg[ci % 2]
            self.conv_row(raw, cv, dnw, ci, 24)
            c.op("act", lambda e: e.activation(out=cv[:], in_=cv[:], func=AF.Silu), reads=[cv], writes=[cv])
            row = rowb[ci % 2]
            if ci < 16:
                c.op("act", lambda e: e.activation(out=sqb[:], in_=cv[:], func=AF.Square), reads=[cv], writes=[sqb])
                for tj, (u0, nu) in enumerate(TILES):
                    p2 = self.pb()
                    c.op("pe", lambda e: e.matmul(p2[:, :nu], lhsT=ones[:], rhs=sqb[:, u0:u0 + nu], start=True, stop=True), reads=[ones, sqb], writes=[p2])
                    rn = rns[tj % 3]
                    c.op("act", lambda e: e.activation(out=rn[:, :nu], in_=p2[:, :nu], func=AF.Ln, bias=eps[:, 0:1]), reads=[p2, eps], writes=[rn])
                    c.op("act", lambda e: e.activation(out=rn[:, :nu], in_=rn[:, :nu], func=AF.Exp, scale=-0.5), reads=[rn], writes=[rn])
                    sc_ = (128 ** -0.5) if ci < 8 else 1.0
                    c.op("dve", lambda e: e.scalar_tensor_tensor(out=row[:, u0:u0 + nu], in0=cv[:, u0:u0 + nu], scalar=sc_, in1=rn[:, :nu], op0=ALU.mult, op1=ALU.mult),
                         reads=[cv, rn], writes=[row])
                c.dma("sp", DQK[ci], row[:], reads=[row])
            else:
                c.op("act", lambda e: e.copy(out=row[:], in_=cv[:]), reads=[cv], writes=[row])
                c.dma("sp", DVT[ci - 16], row[:], reads=[row])

        self.lin_fm(W, self.blocks(6144, 9216, 512), HT, 16, epi_dn, wb=wb)
        while dn_pending:
            dn_pending.pop(0)()

        gbraw = self.gbraw

        def epi_gb(col0, ncol, tt, ps):
            c.op("dve", lambda e: e.tensor_copy(out=gbraw[:, tt, :], in_=ps[:, :32]), reads=[ps], writes=[gbraw])

        self.lin_tm(W, [(10240, 32)], HT, 16, epi_gb, range(16), wb=wb)
        c.end_stage()

    def stage_merge(self, l, XTin, XTout):
        c = self.c
        I = self.I
        c.begin_stage()
        MO = self.scr("MO", [3, 8, 128, NT], BF16)
        GATE = self.S["GATE"]
        MIX = self.scr("MIX", [16, 128, NT], BF16)
        mo = [c.sb([128, 8, NT], BF16, "mo") for _ in range(3)]
        for b in range(3):
            c.dma("sp", mo[b][:], MO[b].rearrange("c p t -> p c t"), writes=[mo[b]])
        wp = [c.sb([128, 8, 512], BF16, "wp") for _ in range(3)]
        gr = [c.sb([128, NT], BF16, "gr") for _ in range(4)]
        acc = [c.sb([128, NT], F32, "acc") for _ in range(4)]
        mixb = [c.sb([128, NT], BF16, "mixb") for _ in range(2)]
        tmp = c.sb([128, 512], F32, "mtmp")
        PW = [I["p_ret"][l], I["p_hy"][l], I["p_dn"][l]]
        n = 0
        nw_ = 0
        for jg in range(4):
            for b in range(3):
                w = wp[nw_ % 3]
                nw_ += 1
                c.dma("pool", w[:], PW[b][:, jg * 512:(jg + 1) * 512].rearrange("(k p) n -> p k n", p=128), writes=[w])
                for jj in range(4):
                    j = jg * 4 + jj
                    a = acc[jj]
                    g = gr[n % 4]
                    n += 1
                    c.dma("sp", g[:], GATE[b * 16 + j], writes=[g])
                    for ti, (t0, nt) in enumerate(TILES):
                        ps = self.pb()
                        for k_ in range(8):
                            c.op("pe", lambda e: e.matmul(ps[:, :nt], lhsT=w[:, k_, jj * 128:(jj + 1) * 128], rhs=mo[b][:, k_, t0:t0 + nt], start=(k_ == 0), stop=(k_ == 7)), reads=[w, mo[b]], writes=[ps])
                        if b == 0:
                            c.op("dve", lambda e: e.tensor_tensor(out=a[:, t0:t0 + nt], in0=ps[:, :nt], in1=g[:, t0:t0 + nt], op=ALU.mult), reads=[ps, g], writes=[a])
                        else:
                            c.op("dve", lambda e: e.tensor_tensor(out=tmp[:, :nt], in0=ps[:, :nt], in1=g[:, t0:t0 + nt], op=ALU.mult), reads=[ps, g], writes=[tmp])
                            c.op("dve", lambda e: e.tensor_tensor(out=a[:, t0:t0 + nt], in0=a[:, t0:t0 + nt], in1=tmp[:, :nt], op=ALU.add), reads=[a, tmp], writes=[a])
            for jj in range(4):
                j = jg * 4 + jj
                m = mixb[j % 2]
                a = acc[jj]
                c.op("act", lambda e: e.copy(out=m[:], in_=a[:]), reads=[a], writes=[m])
                c.dma("sp", MIX[j], m[:], reads=[m])
        c.end_stage()
        c.begin_stage()
        mix = c.sb([128, 16, NT], BF16, "mix")
        c.dma("sp", mix[:], MIX.rearrange("c p t -> p c t"), writes=[mix])
        self.resid_epilogue(I["w_o"][l], 2048, mix, 16, l, 32, XTin, XTout)
        c.end_stage()

    def resid_epilogue(self, W, ncols, IN, KC, l, gbase, XTin, XTout, tiles=TILES, wb=None):
        c = self.c
        mt = self.modT
        xr = [c.sb([128, NT], F32, "xr") for _ in range(2)]
        lo = tiles[0][0]
        hi = tiles[-1][0] + tiles[-1][1]

        def epi(col0, n, ti, t0, nt, ps, aux):
            j = col0 // 128
            x = xr[j % 2]
            if ti == 0:
                c.dma("sp", x[:, lo:hi], XTin[j][:, lo:hi], writes=[x])
            b = 0 if t0 < 1024 else 1
            c.op("dve", lambda e: e.scalar_tensor_tensor(out=x[:, t0:t0 + nt], in0=ps[:, :nt], scalar=mt[:, l, gbase + j, b:b + 1], in1=x[:, t0:t0 + nt], op0=ALU.mult, op1=ALU.add),
                 reads=[ps, mt, x], writes=[x])
            if ti == len(tiles) - 1:
                c.dma("sp", XTout[j][:, lo:hi], x[:, lo:hi], reads=[x])

        if wb is None:
            wb = self.wbufs(KC, 3, 512)
        self.lin_fm(W, self.blocks(0, ncols, 512), IN, KC, epi, tiles=tiles, wb=wb)

    def stage_ffn_up(self, l):
        c = self.c
        I = self.I
        ACTT = self.scr("ACTT", [43, 128, NT], BF16)
        c.begin_stage()
        fw = c.sb([128, 3 * 86], F32, "fw")
        for k in range(3):
            self.load_cols(fw, fw[:, k * 86:(k + 1) * 86], I["ffn_conv"][l][k].rearrange("(n p) -> n p", p=128), 86)
        rowf = [c.sb([128, NT], F32, "frow") for _ in range(3)]
        ga = [c.sb([128, NT], F32, "ga") for _ in range(2)]
        gb = [c.sb([128, NT], F32, "gb") for _ in range(2)]
        ab = [c.sb([128, NT], BF16, "ab") for _ in range(2)]
        wb = self.wbufs(16, 4, 512)
        HT = self.HT
        W = I["w_up"][l]
        cnt = [0]
        nb = 0
        for blk in range(11):
            c0 = blk * 512
            ncol = min(512, 5504 - c0)
            wa_, wg_ = wb[nb % 4], wb[(nb + 1) % 4]
            nb += 2
            c.dma("pool", wa_[:, :, :ncol], W[:, c0:c0 + ncol].rearrange("(k p) n -> p k n", p=128), writes=[wa_])
            c.dma("pool", wg_[:, :, :ncol], W[:, 5504 + c0:5504 + c0 + ncol].rearrange("(k p) n -> p k n", p=128), writes=[wg_])
            for off in range(0, ncol, 128):
                ci = (c0 + off) // 128
                for which, w in ((0, wa_), (1, wg_)):
                    raw = rowf[cnt[0] % 3]
                    cnt[0] += 1
                    for ti, (t0, nt) in enumerate(TILES):
                        ps = self.pb()
                        for kk in range(16):
                            c.op("pe", lambda e: e.matmul(ps[:, :nt], lhsT=w[:, kk, off:off + 128], rhs=HT[:, kk, t0:t0 + nt], start=(kk == 0), stop=(kk == 15)), reads=[w, HT], writes=[ps])
                        if ti % 2 == 0:
                            c.op("act", lambda e: e.copy(out=raw[:, t0:t0 + nt], in_=ps[:, :nt]), reads=[ps], writes=[raw])
                        else:
                            c.op("dve", lambda e: e.tensor_copy(out=raw[:, t0:t0 + nt], in_=ps[:, :nt]), reads=[ps], writes=[raw])
                    if which == 0:
                        g = ga[ci % 2]
                        self.conv_row(raw, g, fw, ci, 86)
                        c.op("act", lambda e: e.activation(out=g[:], in_=g[:], func=AF.Silu), reads=[g], writes=[g])
                    else:
                        g = ga[ci % 2]
                        g2 = gb[ci % 2]
                        self.conv_row(raw, g2, fw, 43 + ci, 86)
                        a_ = ab[ci % 2]
                        c.op("dve", lambda e: e.tensor_tensor(out=a_[:], in0=g[:], in1=g2[:], op=ALU.mult), reads=[g, g2], writes=[a_])
                        c.dma("sp", ACTT[ci], a_[:], reads=[a_])
        c.end_stage()

    def stage_ffn_down(self, l, XTin, XTout):
        c = self.c
        I = self.I
        ACTT = self.S["ACTT"]
        for half in range(2):
            c.begin_stage()
            a = c.sb([128, 43, 1024], BF16, "actin")
            c.dma("sp", a[:], ACTT[:, :, half * 1024:(half + 1) * 1024].rearrange("c p t -> p c t"), writes=[a])

            class Shift:
                def __init__(s, buf, sh):
                    s.buf, s.sh = buf, sh
            wb = self.wbufs(43, 2, 512)
            tiles = [(half * 1024, 512), (half * 1024 + 512, 512)]
            self._resid_shift(I["w_down"][l], a, 43, l, 80, XTin, XTout, tiles, half * 1024, wb)
            c.end_stage()

    def _resid_shift(self, W, IN, KC, l, gbase, XTin, XTout, tiles, tshift, wb):
        c = self.c
        mt = self.modT
        xr = [c.sb([128, 1024], F32, "xr2") for _ in range(2)]
        blocks = self.blocks(0, 2048, 512)
        for bi, (c0, ncol) in enumerate(blocks):
            w = wb[bi % len(wb)]
            c.dma("pool", w[:, :KC, :ncol], W[:, c0:c0 + ncol].rearrange("(k p) n -> p k n", p=128), writes=[w])
            for off in range(0, ncol, 128):
                j = (c0 + off) // 128
                x = xr[j % 2]
                c.dma("sp", x[:], XTin[j][:, tshift:tshift + 1024], writes=[x])
                for ti, (t0, nt) in enumerate(tiles):
                    ps = self.pb()
                    for k in range(KC):
                        c.op("pe", lambda e: e.matmul(ps[:, :nt], lhsT=w[:, k, off:off + 128], rhs=IN[:, k, t0 - tshift:t0 - tshift + nt], start=(k == 0), stop=(k == KC - 1)),
                             reads=[w, IN], writes=[ps])
                    b = 0 if t0 < 1024 else 1
                    c.op("dve", lambda e: e.scalar_tensor_tensor(out=x[:, t0 - tshift:t0 - tshift + nt], in0=ps[:, :nt], scalar=mt[:, l, gbase + j, b:b + 1],
                                                                 in1=x[:, t0 - tshift:t0 - tshift + nt], op0=ALU.mult, op1=ALU.add), reads=[ps, mt, x], writes=[x])
                c.dma("sp", XTout[j][:, tshift:tshift + 1024], x[:], reads=[x])

    def stage_final(self, XT):
        c = self.c
        I = self.I
        c.begin_stage()
        nf = c.sb([128, 16], F32, "nf")
        self.load_cols(nf, nf[:], I["norm_f"].rearrange("(n p) -> n p", p=128), 16)
        xs = [c.sb([128, 16, 128], F32, "fx") for _ in range(2)]
        sq = c.sb([128, 16, 128], BF16, "fsq")
        r = c.sb([128, 128], F32, "fr")
        yo = [c.sb([128, 2048], F32, "yo") for _ in range(2)]
        ones, eps, idf = self.onesB, self.epsT, self.identF
        for tt in range(16):
            t0 = tt * 128
            x = xs[tt % 2]
            y = yo[tt % 2]
            c.dma("sp", x[:], XT[:, :, t0:t0 + 128].rearrange("c p t -> p c t"), writes=[x])
            c.op("act", lambda e: e.activation(out=sq[:], in_=x[:], func=AF.Square), reads=[x], writes=[sq])
            ps = self.pb()
            for ch in range(16):
                c.op("pe", lambda e: e.matmul(ps[:, :128], lhsT=ones[:], rhs=sq[:, ch, :], start=(ch == 0), stop=(ch == 15)), reads=[ones, sq], writes=[ps])
            c.op("act", lambda e: e.activation(out=r[:], in_=ps[:, :128], func=AF.Sqrt, scale=1.0 / 2048, bias=eps[:, 0:1]), reads=[ps, eps], writes=[r])
            c.op("dve", lambda e: e.reciprocal(out=r[:], in_=r[:]), reads=[r], writes=[r])
            for ch in range(16):
                c.op("dve", lambda e: e.scalar_tensor_tensor(out=x[:, ch, :], in0=x[:, ch, :], scalar=nf[:, ch:ch + 1], in1=r[:], op0=ALU.mult, op1=ALU.mult), reads=[x, nf, r], writes=[x])
            for g in range(4):
                p2 = self.pb()
                for j in range(4):
                    ch = g * 4 + j
                    c.op("pe", lambda e: e.transpose(out=p2[:, j * 128:(j + 1) * 128], in_=x[:, ch, :], identity=idf[:]), reads=[x, idf], writes=[p2])
                if g % 2 == 0:
                    c.op("dve", lambda e: e.tensor_copy(out=y[:, g * 512:(g + 1) * 512], in_=p2[:]), reads=[p2], writes=[y])
                else:
                    c.op("act", lambda e: e.copy(out=y[:, g * 512:(g + 1) * 512], in_=p2[:]), reads=[p2], writes=[y])
            c.dma("sp", self.O["y"][t0:t0 + 128, :], y[:], reads=[y])
        c.end_stage()


class KB3(KB2):
    def stage_ret(self, l):
        c = self.c
        I = self.I
        c.begin_stage()
        QK, VTM, KTM, SRG = self.S["QK"], self.S["VTM"], self.S["KTM"], self.S["SRG"]
        MO = self.scr("MO", [3, 8, 128, NT], BF16)
        OSR = self.O["osret"]
        ones, eps = self.onesB, self.epsT
        V = c.sb([128, 16, 1024], BF16, "V")
        c.dma("sp", V[:], VTM.rearrange("t p n -> p t n"), writes=[V])
        Kt = c.sb([128, 8, 512], BF16, "Kt")
        c.dma("sp", Kt[:], KTM.rearrange("t p n -> p t n"), writes=[Kt])
        lg = c.sb([128, 16], F32, "lg")
        c.dma("sp", lg[:], I["ret_decay"][l].rearrange("d h -> (d h)").partition_broadcast(128), writes=[lg])
        c.op("act", lambda e: e.activation(out=lg[:], in_=lg[:], func=AF.Exp, scale=-1.0), reads=[lg], writes=[lg])
        c.op("act", lambda e: e.activation(out=lg[:], in_=lg[:], func=AF.Ln, bias=self.onesF[:, 0:1]), reads=[lg], writes=[lg])
        c.op("dve", lambda e: e.tensor_scalar(out=lg[:], in0=lg[:], scalar1=-1.0, scalar2=None, op0=ALU.mult), reads=[lg], writes=[lg])
        tabs = {}
        for nm, w in (("S", 1920), ("P", 384)):
            for k in ("dpos", "dneg", "dz"):
                t = c.sb([128, w], F32, k + nm)
                c.dma("sp", t[:], I[k + nm], writes=[t])
                tabs[k + nm] = t
        expfb = c.sb([128, 4], F32, "expfb")
        c.dma("sp", expfb[:], I["expfb"], writes=[expfb])
        idx1 = c.sb([128, 1024], F32, "idx1")
        idx2 = c.sb([128, 1024], F32, "idx2")
        c.dma("sp", idx1[:], I["idx1"], writes=[idx1])
        c.dma("sp", idx2[:], I["idx2"], writes=[idx2])
        KD = c.sb([128, 2, 8, 2], F32, "KD")
        for d in range(2):
            for h in range(8):
                c.op("act", lambda e: e.activation(out=KD[:, d, h, :], in_=expfb[:, d * 2:d * 2 + 2], func=AF.Exp, scale=lg[:, d * 8 + h:d * 8 + h + 1]), reads=[expfb, lg], writes=[KD])
        TabS2 = [c.sb([128, 1920], F32, "TabS") for _ in range(2)]
        TabP2 = [c.sb([128, 384], F32, "TabP") for _ in range(2)]
        ttmp = c.sb([128, 1920], F32, "ttmp")
        qrow = c.sb([128, NT], BF16, "qrow")
        krow = c.sb([128, NT], BF16, "krow")
        lgsel = c.sb([128, 2], F32, "lgsel")
        dec = c.sb([128, 1024], F32, "dec")
        qdf = c.sb([128, 1024], BF16, "qdf")
        qdb = c.sb([128, 1024], BF16, "qdb")
        S0 = c.sb([128, 2, 128], BF16, "S0")
        srg = c.sb([128, NT], BF16, "srg")
        sm = [c.sb([128, 512], BF16, "sm") for _ in range(3)]
        osb = c.sb([128, 512], F32, "osb")
        osq = c.sb([128, 512], BF16, "osq")
        rr = c.sb([128, 512], F32, "rr")
        orow = c.sb([128, NT], BF16, "orow")
        kf = [c.sb([128, 64], BF16, "kf") for _ in range(4)]
        sst = [c.sb([64, 128], F32, "sst") for _ in range(4)]
        n_sm = [0]
        n_kf = [0]

        def build_tab(Tab, nm, w, h):
            dp, dn, dz = tabs["dpos" + nm], tabs["dneg" + nm], tabs["dz" + nm]
            c.op("dve", lambda e: e.tensor_scalar(out=ttmp[:, :w], in0=dp[:], scalar1=lg[:, h:h + 1], scalar2=None, op0=ALU.mult), reads=[dp, lg], writes=[ttmp])
            c.op("dve", lambda e: e.scalar_tensor_tensor(out=ttmp[:, :w], in0=dn[:], scalar=lg[:, 8 + h:9 + h], in1=ttmp[:, :w], op0=ALU.mult, op1=ALU.add), reads=[dn, lg, ttmp], writes=[ttmp])
            c.op("act", lambda e: e.activation(out=ttmp[:, :w], in_=ttmp[:, :w], func=AF.Exp), reads=[ttmp], writes=[ttmp])
            c.op("dve", lambda e: e.tensor_tensor(out=Tab[:, :w], in0=ttmp[:, :w], in1=dz[:], op=ALU.add), reads=[ttmp, dz], writes=[Tab])

        def finish_o(pso, n, h, tok0):
            c.op("act", lambda e: e.copy(out=osb[:, :n], in_=pso[:, :n]), reads=[pso], writes=[osb])
            c.op("act", lambda e: e.activation(out=osq[:, :n], in_=osb[:, :n], func=AF.Square), reads=[osb], writes=[osq])
            p2 = self.pb()
            c.op("pe", lambda e: e.matmul(p2[:, :n], lhsT=ones[:], rhs=osq[:, :n], start=True, stop=True), reads=[ones, osq], writes=[p2])
            c.op("act", lambda e: e.activation(out=rr[:, :n], in_=p2[:, :n], func=AF.Sqrt, scale=1.0 / 128, bias=eps[:, 0:1]), reads=[p2, eps], writes=[rr])
            c.op("dve", lambda e: e.reciprocal(out=rr[:, :n], in_=rr[:, :n]), reads=[rr], writes=[rr])
            c.op("dve", lambda e: e.tensor_tensor(out=osb[:, :n], in0=osb[:, :n], in1=rr[:, :n], op=ALU.mult), reads=[osb, rr], writes=[osb])
            c.op("dve", lambda e: e.tensor_tensor(out=orow[:, tok0:tok0 + n], in0=osb[:, :n], in1=srg[:, tok0:tok0 + n], op=ALU.mult), reads=[osb, srg], writes=[orow])

        srg2 = [srg, c.sb([128, NT], BF16, "srg2")]
        orow2 = [orow, c.sb([128, NT], BF16, "orow2")]
        sm.append(c.sb([128, 512], BF16, "sm"))
        NS = len(sm)
        pending = []

        def flush(keep):
            while len(pending) > keep:
                pending.pop(0)()

        def scores_loop(n_j, mk_score, mk_acc):
            nxt = mk_score(0)
            for jc in range(n_j):
                cur = nxt
                nxt = mk_score(jc + 1) if jc + 1 < n_j else None
                mk_acc(jc, cur)

        for h in range(8):
            hp, po = h // 2, (h % 2) * 64
            srg_h, orow_h = srg2[h % 2], orow2[h % 2]
            if h % 2 == 0:
                c.dma("sp", qrow[:], QK[hp], writes=[qrow])
                c.dma("sp", krow[:], QK[4 + hp], writes=[krow])
                for d in range(2):
                    c.op("dve", lambda e: e.tensor_copy(out=lgsel[0:64, d:d + 1], in_=lg[0:64, d * 8 + h:d * 8 + h + 1]), reads=[lg], writes=[lgsel])
                    c.op("dve", lambda e: e.tensor_copy(out=lgsel[64:128, d:d + 1], in_=lg[64:128, d * 8 + h + 1:d * 8 + h + 2]), reads=[lg], writes=[lgsel])
                c.op("act", lambda e: e.activation(out=dec[:], in_=idx1[:], func=AF.Exp, scale=lgsel[:, 0:1]), reads=[idx1, lgsel], writes=[dec])
                c.op("dve", lambda e: e.tensor_tensor(out=qdf[:], in0=qrow[:, 1024:2048], in1=dec[:], op=ALU.mult), reads=[qrow, dec], writes=[qdf])
                c.op("act", lambda e: e.activation(out=dec[:], in_=idx2[:], func=AF.Exp, scale=lgsel[:, 1:2]), reads=[idx2, lgsel], writes=[dec])
                c.op("dve", lambda e: e.tensor_tensor(out=qdb[:], in0=qrow[:, 1024:2048], in1=dec[:], op=ALU.mult), reads=[qrow, dec], writes=[qdb])
            c.dma("sp", srg_h[:], SRG[h], writes=[srg_h])
            for d in range(2):
                c.dma("pool", S0[po:po + 64, d, :], I["sret"][l, d, h], writes=[S0])
            if h == 0:
                build_tab(TabS2[0], "S", 1920, 0)
                build_tab(TabP2[0], "P", 384, 0)
            TabS, TabP = TabS2[h % 2], TabP2[h % 2]
            if h + 1 < 8:
                build_tab(TabS2[(h + 1) % 2], "S", 1920, h + 1)
                build_tab(TabP2[(h + 1) % 2], "P", 384, h + 1)

            def fin(pso, n, tok0, h=h, srg_h=srg_h, orow_h=orow_h, store=False):
                def f():
                    c.op("act", lambda e: e.copy(out=osb[:, :n], in_=pso[:, :n]), reads=[pso], writes=[osb])
                    c.op("act", lambda e: e.activation(out=osq[:, :n], in_=osb[:, :n], func=AF.Square), reads=[osb], writes=[osq])
                    p2 = self.pb()
                    c.op("pe", lambda e: e.matmul(p2[:, :n], lhsT=ones[:], rhs=osq[:, :n], start=True, stop=True), reads=[ones, osq], writes=[p2])
                    c.op("act", lambda e: e.activation(out=rr[:, :n], in_=p2[:, :n], func=AF.Ln, scale=1.0 / 128, bias=eps[:, 0:1]), reads=[p2, eps], writes=[rr])
                    c.op("act", lambda e: e.activation(out=rr[:, :n], in_=rr[:, :n], func=AF.Exp, scale=-0.5), reads=[rr], writes=[rr])
                    c.op("dve", lambda e: e.tensor_tensor(out=osb[:, :n], in0=osb[:, :n], in1=rr[:, :n], op=ALU.mult), reads=[osb, rr], writes=[osb])
                    c.op("dve", lambda e: e.tensor_tensor(out=orow_h[:, tok0:tok0 + n], in0=osb[:, :n], in1=srg_h[:, tok0:tok0 + n], op=ALU.mult), reads=[osb, srg_h], writes=[orow_h])
                    if store:
                        c.dma("sp", MO[0][h], orow_h[:], reads=[orow_h])
                return f

            for i0 in (0, 512):
                pso = self.pacc()

                def mk_score(jc, i0=i0):
                    pss = self.pb()
                    c.op("pe", lambda e: e.matmul(pss[:, :512], lhsT=krow[po:po + 64, 1024 + jc * 128:1024 + (jc + 1) * 128], rhs=qrow[po:po + 64, 1024 + i0:1024 + i0 + 512], start=True, stop=True),
                         reads=[krow, qrow], writes=[pss])
                    return pss

                def mk_acc(jc, pss, i0=i0, pso=pso):
                    s = sm[n_sm[0] % NS]
                    n_sm[0] += 1
                    u0 = i0 - 128 * jc + 896
                    c.op("dve", lambda e: e.tensor_tensor(out=s[:], in0=pss[:, :512], in1=TabS[:, u0:u0 + 512], op=ALU.mult), reads=[pss, TabS], writes=[s])
                    c.op("pe", lambda e: e.matmul(pso[:, :512], lhsT=V[:, 8 + jc, h * 128:(h + 1) * 128], rhs=s[:], start=(jc == 0), stop=False), reads=[V, s], writes=[pso])

                scores_loop(8, mk_score, mk_acc)
                c.op("pe", lambda e: e.matmul(pso[:, :512], lhsT=S0[po:po + 64, 0, :], rhs=qdf[po:po + 64, i0:i0 + 512], start=False, stop=False), reads=[S0, qdf], writes=[pso])
                c.op("pe", lambda e: e.matmul(pso[:, :512], lhsT=S0[po:po + 64, 1, :], rhs=qdb[po:po + 64, i0:i0 + 512], start=False, stop=True), reads=[S0, qdb], writes=[pso])
                pending.append(fin(pso, 512, 1024 + i0))
                flush(1)
            for s_ in range(4):
                b0 = s_ * 256
                pso = self.pacc()

                def mk_score(jc, b0=b0):
                    pss = self.pb()
                    c.op("pe", lambda e: e.matmul(pss[:, :256], lhsT=krow[po:po + 64, b0 + jc * 128:b0 + (jc + 1) * 128], rhs=qrow[po:po + 64, b0:b0 + 256], start=True, stop=True),
                         reads=[krow, qrow], writes=[pss])
                    return pss

                def mk_acc(jc, pss, pso=pso, s_=s_):
                    s = sm[n_sm[0] % NS]
                    n_sm[0] += 1
                    u0 = 128 - 128 * jc
                    c.op("dve", lambda e: e.tensor_tensor(out=s[:, :256], in0=pss[:, :256], in1=TabP[:, u0:u0 + 256], op=ALU.mult), reads=[pss, TabP], writes=[s])
                    c.op("pe", lambda e: e.matmul(pso[:, :256], lhsT=V[:, s_ * 2 + jc, h * 128:(h + 1) * 128], rhs=s[:, :256], start=(jc == 0), stop=(jc == 1)), reads=[V, s], writes=[pso])

                scores_loop(2, mk_score, mk_acc)
                pending.append(fin(pso, 256, b0, store=(s_ == 3)))
                flush(1)
                for d in range(2):
                    pst = self.pb()
                    for tt in range(2):
                        k_ = kf[n_kf[0] % 4]
                        n_kf[0] += 1
                        c.op("dve", lambda e: e.tensor_scalar(out=k_[:], in0=Kt[:, s_ * 2 + tt, h * 64:(h + 1) * 64], scalar1=KD[:, d, h, tt:tt + 1], scalar2=None, op0=ALU.mult), reads=[Kt, KD], writes=[k_])
                        c.op("pe", lambda e: e.matmul(pst[:64, :128], lhsT=k_[:], rhs=V[:, s_ * 2 + tt, h * 128:(h + 1) * 128], start=(tt == 0), stop=(tt == 1)), reads=[k_, V], writes=[pst])
                    st = sst[(s_ * 2 + d) % 4]
                    c.op("act", lambda e: e.copy(out=st[:], in_=pst[:64, :128]), reads=[pst], writes=[st])
                    c.dma("sp", OSR[s_, l, d, h], st[:], reads=[st])
        flush(0)
        c.end_stage()


import os


class KB4(KB3):
    def stage_dn(self, l):
        c = self.c
        I = self.I
        c.begin_stage()
        DQK, DVT, SDZ = self.S["DQK"], self.S["DVT"], self.S["SDZ"]
        MO = self.scr("MO", [3, 8, 128, NT], BF16)
        OSD = self.O["osdn"]
        onesF, onesB, eps, idF, idB = self.onesF, self.onesB, self.epsT, self.identF, self.identB
        gbraw = self.gbraw
        cm = {}
        for nm in ("UF", "UB", "SU", "SL"):
            t = c.sb([128, 128], F32, nm)
            c.dma("sp", t[:], I[nm], writes=[t])
            cm[nm] = t
        alog = c.sb([128, 16], F32, "alog")
        dtb = c.sb([128, 16], F32, "dtb")
        c.dma("sp", alog[:], I["dn_a_log"][l].rearrange("d h -> (d h)").partition_broadcast(128), writes=[alog])
        c.dma("sp", dtb[:], I["dn_dt_bias"][l].rearrange("d h -> (d h)").partition_broadcast(128), writes=[dtb])
        dnn = c.sb([128, 1], F32, "dnn")
        c.dma("sp", dnn[:], I["dn_norm"][l].rearrange("(p o) -> p o", o=1), writes=[dnn])
        c.op("act", lambda e: e.activation(out=alog[:], in_=alog[:], func=AF.Exp), reads=[alog], writes=[alog])
        c.op("dve", lambda e: e.tensor_scalar(out=alog[:], in0=alog[:], scalar1=-1.0, scalar2=None, op0=ALU.mult), reads=[alog], writes=[alog])
        G = c.sb([128, 16, 16], F32, "G")
        BT = c.sb([128, 16, 16], F32, "BT")
        NBT = c.sb([128, 16, 16], F32, "NBT")
        for tt in range(16):
            c.op("dve", lambda e: e.tensor_tensor(out=G[:, tt, :], in0=gbraw[:, tt, 0:16], in1=dtb[:], op=ALU.add), reads=[gbraw, dtb], writes=[G])
        c.op("act", lambda e: e.activation(out=G[:], in_=G[:], func=AF.Exp), reads=[G], writes=[G])
        c.op("act", lambda e: e.activation(out=G[:], in_=G[:], func=AF.Ln, bias=onesF[:, 0:1]), reads=[G, onesF], writes=[G])
        for tt in range(16):
            c.op("dve", lambda e: e.tensor_tensor(out=G[:, tt, :], in0=G[:, tt, :], in1=alog[:], op=ALU.mult), reads=[G, alog], writes=[G])
        c.op("act", lambda e: e.activation(out=BT[:], in_=gbraw[:, :, 16:32], func=AF.Sigmoid), reads=[gbraw], writes=[BT])
        c.op("dve", lambda e: e.tensor_scalar(out=NBT[:], in0=BT[:], scalar1=-1.0, scalar2=None, op0=ALU.mult), reads=[BT], writes=[NBT])
        GC = c.sb([128, 16, 16], F32, "GC")
        GL = c.sb([128, 16, 16], F32, "GL")
        for tt in range(16):
            for d in range(2):
                U = cm["UF"] if d == 0 else cm["UB"]
                ps = self.pb()
                c.op("pe", lambda e: e.matmul(ps[:, 0:8], lhsT=U[:], rhs=G[:, tt, d * 8:d * 8 + 8], start=True, stop=True), reads=[U, G], writes=[ps])
                c.op("pe", lambda e: e.matmul(ps[:, 8:16], lhsT=onesF[:], rhs=G[:, tt, d * 8:d * 8 + 8], start=True, stop=True), reads=[onesF, G], writes=[ps])
                c.op("dve", lambda e: e.tensor_copy(out=GC[:, tt, d * 8:d * 8 + 8], in_=ps[:, 0:8]), reads=[ps], writes=[GC])
                c.op("dve", lambda e: e.tensor_copy(out=GL[:, tt, d * 8:d * 8 + 8], in_=ps[:, 8:16]), reads=[ps], writes=[GL])
        BK = c.sb([128, 16, 16], F32, "BK")
        KDS = c.sb([128, 16, 16], F32, "KDS")
        EGL = c.sb([128, 16, 16], F32, "EGL")
        c.op("act", lambda e: e.activation(out=BK[:], in_=GC[:], func=AF.Exp), reads=[GC], writes=[BK])
        c.op("dve", lambda e: e.tensor_tensor(out=BK[:], in0=BK[:], in1=BT[:], op=ALU.mult), reads=[BK, BT], writes=[BK])
        c.op("dve", lambda e: e.tensor_tensor(out=KDS[:], in0=GL[:], in1=GC[:], op=ALU.subtract), reads=[GL, GC], writes=[KDS])
        c.op("act", lambda e: e.activation(out=KDS[:], in_=KDS[:], func=AF.Exp), reads=[KDS], writes=[KDS])
        c.op("act", lambda e: e.activation(out=EGL[:], in_=GL[:], func=AF.Exp), reads=[GL], writes=[EGL])

        if "GDBG" in self.dbg:
            gd = self.scr("GDBG", [4, 128, 16, 16], F32)
            for i_, t_ in enumerate((G, BT, GC, GL)):
                c.dma("sp", gd[i_], t_[:], reads=[t_])
        OACC = [c.sb([128, NT], F32, "oacc") for _ in range(8)]
        c.begin_stage()
        Sf = [c.sb([128, 128], F32, "Sf") for _ in range(8)]
        Sb = [c.sb([128, 128], BF16, "Sb") for _ in range(8)]
        NI = 8

        def ring(shape, dt, nm, n=NI):
            bufs = [c.sb(shape, dt, nm) for _ in range(n)]
            st = [0]

            def nxt():
                st[0] += 1
                return bufs[st[0] % n]
            return nxt
        r_q = ring([128, 128], BF16, "rq")
        r_k = ring([128, 128], BF16, "rk")
        r_v = ring([128, 128], BF16, "rv")
        r_gbc = ring([128, 128], F32, "rgbc")
        r_t = ring([128, 128], F32, "rt", 2 * NI)
        r_vb = ring([128, 128], F32, "rvb")
        r_kbe = ring([128, 128], F32, "rkbe")
        r_kd = ring([128, 128], F32, "rkd")
        r_nw = ring([128, 128], F32, "rnw")
        r_at = ring([128, 128], BF16, "rat")
        r_qd = ring([128, 128], BF16, "rqd")
        r_vn = ring([128, 128], F32, "rvn")
        r_vnb = ring([128, 128], BF16, "rvnb")
        r_so = ring([128, 128], F32, "rso", 3)
        r_tw = ring([128, 256], F32, "rtw", 2 * NI)
        r_tq = ring([128, 256], F32, "rtq", NI)
        r_am = ring([128, 256], F32, "ram", 2 * NI)
        r_pp = ring([128, 256], F32, "rpp", NI)
        mlu = c.sb([128, 7, 256], F32, "mlu")
        mul = c.sb([128, 7, 256], F32, "mul")
        c.dma("sp", mlu[:], I["MLU"], writes=[mlu])
        c.dma("sp", mul[:], I["MUL"], writes=[mul])
        id2 = c.sb([128, 256], F32, "id2")
        c.dma("sp", id2[:, 0:128], I["ident"], writes=[id2])
        c.dma("sp", id2[:, 128:256], I["ident"], writes=[id2])
        PB = self.PB
        pbn = [0]

        def pbank():
            pbn[0] += 1
            return PB[pbn[0] % 8]

        def process(tt, h, d, last_seq):
            t0 = tt * 128
            col = d * 8 + h
            U = cm["UF"] if d == 0 else cm["UB"]
            MS = cm["SL"] if d == 0 else cm["SU"]
            MI = cm["UF"] if d == 0 else cm["UB"]
            gc_ap = GC[:, tt, col:col + 1]
            qT, kT, vT = r_q(), r_k(), r_v()
            c.dma("sp", qT[:], DQK[h][:, t0:t0 + 128], writes=[qT])
            c.dma("sp", kT[:], DQK[8 + h][:, t0:t0 + 128], writes=[kT])
            c.dma("sp", vT[:], DVT[h][:, t0:t0 + 128], writes=[vT])
            gbc = r_gbc()
            c.op("act", lambda e: e.activation(out=gbc[:], in_=onesF[:], func=AF.Copy, scale=G[:, tt, col:col + 1]), reads=[onesF, G], writes=[gbc])
            yield
            bank = PB[h]
            ptr = pR = pG = pT = ps1 = ps2 = pw = psv = pso = pss = bank
            ptb = bank[:].bitcast(BF16)
            c.op("pe", lambda e: e.transpose(out=ptb[:, 0:128], in_=kT[:], identity=idB[:]), reads=[kT, idB], writes=[ptr])
            c.op("pe", lambda e: e.transpose(out=ptb[:, 128:256], in_=vT[:], identity=idB[:]), reads=[vT, idB], writes=[ptr])
            c.op("pe", lambda e: e.matmul(pR[:, 128:256], lhsT=gbc[:], rhs=U[:], start=True, stop=True), reads=[gbc, U], writes=[pR])
            c.op("pe", lambda e: e.matmul(pG[:, 256:384], lhsT=kT[:], rhs=kT[:], start=True, stop=True), reads=[kT], writes=[pG])
            c.op("pe", lambda e: e.matmul(pG[:, 384:512], lhsT=kT[:], rhs=qT[:], start=True, stop=True), reads=[kT, qT], writes=[pG])
            yield
            vb, kbe, kd = r_vb(), r_kbe(), r_kd()
            c.op("dve", lambda e: e.tensor_scalar(out=kbe[:], in0=ptb[:, 0:128], scalar1=BK[:, tt, col:col + 1], scalar2=None, op0=ALU.mult), reads=[ptr, BK], writes=[kbe])
            c.op("dve", lambda e: e.tensor_scalar(out=kd[:], in0=ptb[:, 0:128], scalar1=KDS[:, tt, col:col + 1], scalar2=None, op0=ALU.mult), reads=[ptr, KDS], writes=[kd])
            c.op("dve", lambda e: e.tensor_scalar(out=vb[:], in0=ptb[:, 128:256], scalar1=BT[:, tt, col:col + 1], scalar2=None, op0=ALU.mult), reads=[ptr, BT], writes=[vb])
            d1, d2 = r_t(), r_t()
            c.op("dve", lambda e: e.tensor_scalar(out=d1[:], in0=pR[:, 128:256], scalar1=gc_ap, scalar2=0.0, op0=ALU.subtract, op1=ALU.max), reads=[pR, GC], writes=[d1])
            c.op("dve", lambda e: e.tensor_scalar(out=d2[:], in0=pR[:, 128:256], scalar1=gc_ap, scalar2=0.0, op0=ALU.subtract, op1=ALU.min), reads=[pR, GC], writes=[d2])
            er = gbc
            c.op("act", lambda e: e.activation(out=er[:], in_=pR[:, 128:256], func=AF.Exp), reads=[pR], writes=[er])
            yield
            c.op("act", lambda e: e.activation(out=d1[:], in_=d1[:], func=AF.Exp, scale=-1.0), reads=[d1], writes=[d1])
            c.op("act", lambda e: e.activation(out=d2[:], in_=d2[:], func=AF.Exp), reads=[d2], writes=[d2])
            qd = r_qd()
            c.op("pool", lambda e: e.tensor_tensor(out=qd[:], in0=qT[:], in1=er[:], op=ALU.mult), reads=[qT, er], writes=[qd])
            yield
            pp = r_pp()
            c.op("dve", lambda e: e.scalar_tensor_tensor(out=d1[:], in0=pG[:, 256:384], scalar=NBT[:, tt, col:col + 1], in1=d1[:], op0=ALU.mult, op1=ALU.mult), reads=[pG, NBT, d1], writes=[d1])
            c.op("dve", lambda e: e.tensor_tensor(out=d2[:], in0=pG[:, 384:512], in1=d2[:], op=ALU.mult), reads=[pG, d2], writes=[d2])
            yield
            c.op("pool", lambda e: e.tensor_tensor(out=pp[:, 0:128], in0=d1[:], in1=MS[:], op=ALU.mult), reads=[d1, MS], writes=[pp])
            at = r_at()
            c.op("pool", lambda e: e.tensor_tensor(out=at[:], in0=d2[:], in1=MI[:], op=ALU.mult), reads=[d2, MI], writes=[at])
            yield
            c.op("pe", lambda e: e.transpose(out=pT[:, :128], in_=pp[:, 0:128], identity=idF[:]), reads=[pp, idF], writes=[pT])
            yield
            c.op("act", lambda e: e.copy(out=pp[:, 128:256], in_=pT[:, :128]), reads=[pT], writes=[pp])
            yield
            MM = mlu if d == 0 else mul
            am = r_am()
            c.op("pool", lambda e: e.tensor_tensor(out=am[:], in0=pp[:], in1=MM[:, 0, :], op=ALU.mult), reads=[pp, MM], writes=[am])
            yield
            tw = r_tw()
            c.op("dve", lambda e: e.tensor_tensor(out=tw[:], in0=am[:], in1=id2[:], op=ALU.add), reads=[am, id2], writes=[tw])
            am = r_am()
            c.op("pool", lambda e: e.tensor_tensor(out=am[:], in0=pp[:], in1=MM[:, 1, :], op=ALU.mult), reads=[pp, MM], writes=[am])
            yield
            for s in range(1, 7):
                c.op("pe", lambda e: e.matmul(ps1[:, 0:128], lhsT=am[:, 128:256], rhs=tw[:, 0:128], start=True, stop=True), reads=[am, tw], writes=[ps1])
                c.op("pe", lambda e: e.matmul(ps1[:, 128:256], lhsT=am[:, 0:128], rhs=tw[:, 128:256], start=True, stop=True), reads=[am, tw], writes=[ps1])
                if s < 6:
                    am = r_am()
                    c.op("pool", lambda e: e.tensor_tensor(out=am[:], in0=pp[:], in1=MM[:, s + 1, :], op=ALU.mult), reads=[pp, MM], writes=[am])
                yield
                p1 = r_tq()
                c.op("act", lambda e: e.copy(out=p1[:], in_=ps1[:, 0:256]), reads=[ps1], writes=[p1])
                yield
                c.op("pe", lambda e: e.matmul(ps2[:, 256:384], lhsT=tw[:, 128:256], rhs=p1[:, 0:128], start=True, stop=True), reads=[tw, p1], writes=[ps2])
                c.op("pe", lambda e: e.matmul(ps2[:, 384:512], lhsT=tw[:, 0:128], rhs=p1[:, 128:256], start=True, stop=True), reads=[tw, p1], writes=[ps2])
                yield
                ntw = r_tw()
                c.op("dve", lambda e: e.tensor_tensor(out=ntw[:], in0=ps2[:, 256:512], in1=tw[:], op=ALU.add), reads=[ps2, tw], writes=[ntw])
                tw = ntw
                yield
            c.op("pe", lambda e: e.matmul(pw[:, :128], lhsT=kbe[:], rhs=tw[:, 128:256], start=True, stop=True), reads=[kbe, tw], writes=[pw])
            yield
            nw = r_nw()
            c.op("act", lambda e: e.activation(out=nw[:], in_=pw[:, :128], func=AF.Copy, scale=-1.0), reads=[pw], writes=[nw])
            yield
            S_f, S_b = Sf[h], Sb[h]
            c.op("pe", lambda e: e.matmul(psv[:, 128:256], lhsT=tw[:, 128:256], rhs=vb[:], start=True, stop=False), reads=[tw, vb], writes=[psv])
            c.op("pe", lambda e: e.matmul(psv[:, 128:256], lhsT=nw[:], rhs=S_f[:], start=False, stop=True), reads=[nw, S_f], writes=[psv])
            yield
            vn = r_vn()
            vnb = r_vnb()
            c.op("act", lambda e: e.copy(out=vn[:], in_=psv[:, 128:256]), reads=[psv], writes=[vn])
            c.op("act", lambda e: e.copy(out=vnb[:], in_=vn[:]), reads=[vn], writes=[vnb])
            yield
            c.op("pe", lambda e: e.matmul(pso[:, 256:384], lhsT=S_b[:], rhs=qd[:], start=True, stop=False), reads=[S_b, qd], writes=[pso])
            c.op("pe", lambda e: e.matmul(pso[:, 256:384], lhsT=vnb[:], rhs=at[:], start=False, stop=True), reads=[vnb, at], writes=[pso])
            c.op("pe", lambda e: e.matmul(pss[:, 384:512], lhsT=kd[:], rhs=vn[:], start=True, stop=True), reads=[kd, vn], writes=[pss])
            yield
            oa = OACC[h]
            if d == 0:
                c.op("act", lambda e: e.copy(out=oa[:, t0:t0 + 128], in_=pso[:, 256:384]), reads=[pso], writes=[oa])
            else:
                c.op("dve", lambda e: e.tensor_tensor(out=oa[:, t0:t0 + 128], in0=pso[:, 256:384], in1=oa[:, t0:t0 + 128], op=ALU.add), reads=[pso, oa], writes=[oa])
            c.op("dve", lambda e: e.scalar_tensor_tensor(out=S_f[:], in0=S_f[:], scalar=EGL[:, tt, col:col + 1], in1=pss[:, 384:512], op0=ALU.mult, op1=ALU.add), reads=[S_f, EGL, pss], writes=[S_f])
            yield
            c.op("act", lambda e: e.copy(out=S_b[:], in_=S_f[:]), reads=[S_f], writes=[S_b])
            if last_seq is not None:
                so = r_so()
                c.op("pool", lambda e: e.tensor_copy(out=so[:], in_=S_f[:]), reads=[S_f], writes=[so])
                c.dma("sp", OSD[last_seq, l, d, h], so[:], reads=[so])

        def init_state(s, h, d):
            if s is None:
                c.dma("sp", Sf[h][:], I["sdn"][l, d, h], writes=[Sf[h]])
                c.op("act", lambda e: e.copy(out=Sb[h][:], in_=Sf[h][:]), reads=[Sf[h]], writes=[Sb[h]])
            else:
                c.op("pool", lambda e: e.memset(Sf[h][:], 0.0), writes=[Sf[h]])
                c.op("pool", lambda e: e.memset(Sb[h][:], 0.0), writes=[Sb[h]])

        def inst(tt, h, d, s, first, last):
            if first:
                init_state(s, h, d)
            yield from process(tt, h, d, s if (s is not None and last) else None)

        queue = []
        for d in range(2):
            seqs = [(s, [2 * s, 2 * s + 1]) for s in range(4)] + [(None, list(range(8, 16)))]
            for s, tiles in seqs:
                order = tiles if d == 0 else tiles[::-1]
                for i, tt in enumerate(order):
                    for h in range(8):
                        queue.append(inst(tt, h, d, s, i == 0, i == len(order) - 1))
        active = []
        qi = 0
        cyc = 0
        STAG = 5
        while qi < len(queue) or active:
            if qi < len(queue) and len(active) < NI and (qi >= NI or cyc >= qi * STAG):
                active.append(queue[qi])
                qi += 1
            alive = []
            for g in active:
                try:
                    next(g)
                    alive.append(g)
                except StopIteration:
                    pass
            active = alive
            cyc += 1
        c.end_stage()
        osq = c.sb([128, 512], BF16, "dosq")
        rr = c.sb([128, 512], F32, "drr")
        sdz = [c.sb([128, NT], BF16, "sdz") for _ in range(2)]
        orow = [c.sb([128, NT], BF16, "dorow") for _ in range(2)]
        for h in range(8):
            oa = OACC[h]
            z_ = sdz[h % 2]
            orw = orow[h % 2]
            c.dma("sp", z_[:], SDZ[h], writes=[z_])
            for (u0, nu) in TILES:
                c.op("act", lambda e: e.activation(out=osq[:, :nu], in_=oa[:, u0:u0 + nu], func=AF.Square), reads=[oa], writes=[osq])
                p2 = self.pb()
                c.op("pe", lambda e: e.matmul(p2[:, :nu], lhsT=onesB[:], rhs=osq[:, :nu], start=True, stop=True), reads=[onesB, osq], writes=[p2])
                c.op("act", lambda e: e.activation(out=rr[:, :nu], in_=p2[:, :nu], func=AF.Ln, scale=1.0 / 128, bias=eps[:, 0:1]), reads=[p2, eps], writes=[rr])
                c.op("act", lambda e: e.activation(out=rr[:, :nu], in_=rr[:, :nu], func=AF.Exp, scale=-0.5), reads=[rr], writes=[rr])
                c.op("dve", lambda e: e.scalar_tensor_tensor(out=rr[:, :nu], in0=oa[:, u0:u0 + nu], scalar=dnn[:, 0:1], in1=rr[:, :nu], op0=ALU.mult, op1=ALU.mult), reads=[oa, dnn, rr], writes=[rr])
                c.op("dve", lambda e: e.tensor_tensor(out=orw[:, u0:u0 + nu], in0=rr[:, :nu], in1=z_[:, u0:u0 + nu], op=ALU.mult), reads=[rr, z_], writes=[orw])
            c.dma("sp", MO[2][h], orw[:], reads=[orw])
        c.end_stage()


PI = math.pi


class KB5(KB4):
    def hy_tables(self, l, L):
        c = self.c
        I = self.I
        nch = L // 128
        TAB = self.scr("HTAB%d" % L, [2, 2 * nch + 1, 128, 1024], BF16)
        c.begin_stage()
        onesF, m0 = self.onesF, None
        m0 = c.sb([128, 2], F32, "m0")
        c.dma("sp", m0[:], I["m0"], writes=[m0])
        fT = c.sb([33, L], F32, "fT")
        c.dma("sp", fT[:], I["featsT%d" % L], writes=[fT])
        w1 = c.sb([33, 64], F32, "w1")
        w2 = c.sb([64, 64], F32, "w2")
        w3 = c.sb([64, 4096], F32, "w3")
        c.dma("sp", w1[:], I["hy_w1"][l], writes=[w1])
        c.dma("sp", w2[:], I["hy_w2"][l], writes=[w2])
        c.dma("sp", w3[:], I["hy_w3"][l], writes=[w3])
        vec = c.sb([64, 4], F32, "hvec")
        for i, nm in enumerate(("hy_b1", "hy_freq1", "hy_b2", "hy_freq2")):
            c.dma("sp", vec[:, i:i + 1], I[nm][l].rearrange("(p o) -> p o", o=1), writes=[vec])
        hid1 = c.sb([64, L], F32, "hid1")
        hid2 = c.sb([64, L], F32, "hid2")
        msk = c.sb([64, 512], F32, "msk")

        def sin_layer(dst, wT, K, src, bi, fi):
            for t0 in range(0, L, 512):
                n = min(512, L - t0)
                ps = self.pb()
                c.op("pe", lambda e: e.matmul(ps[:64, :n], lhsT=wT[:K, :], rhs=src[:K, t0:t0 + n], start=True, stop=True), reads=[wT, src], writes=[ps])
                d = dst
                c.op("dve", lambda e: e.tensor_scalar(out=d[:, t0:t0 + n], in0=ps[:64, :n], scalar1=vec[:, bi:bi + 1], scalar2=vec[:, fi:fi + 1], op0=ALU.add, op1=ALU.mult), reads=[ps, vec], writes=[d])
                for _ in range(2):
                    c.op("dve", lambda e: e.tensor_scalar(out=msk[:, :n], in0=d[:, t0:t0 + n], scalar1=PI, scalar2=-2 * PI, op0=ALU.is_gt, op1=ALU.mult), reads=[d], writes=[msk])
                    c.op("dve", lambda e: e.tensor_tensor(out=d[:, t0:t0 + n], in0=d[:, t0:t0 + n], in1=msk[:, :n], op=ALU.add), reads=[d, msk], writes=[d])
                    c.op("dve", lambda e: e.tensor_scalar(out=msk[:, :n], in0=d[:, t0:t0 + n], scalar1=-PI, scalar2=2 * PI, op0=ALU.is_lt, op1=ALU.mult), reads=[d], writes=[msk])
                    c.op("dve", lambda e: e.tensor_tensor(out=d[:, t0:t0 + n], in0=d[:, t0:t0 + n], in1=msk[:, :n], op=ALU.add), reads=[d, msk], writes=[d])
                c.op("act", lambda e: e.activation(out=d[:, t0:t0 + n], in_=d[:, t0:t0 + n], func=AF.Sin), reads=[d], writes=[d])

        sin_layer(hid1, w1, 33, fT, 0, 1)
        sin_layer(hid2, w2, 64, hid1, 2, 3)
        win = c.sb([128, nch, 1024], F32, "win")
        c.dma("sp", win[:], I["win%d" % L].rearrange("(t p) c -> p t c", p=128), writes=[win])
        CA = c.sb([128, nch, L], BF16, "CA")
        MB = c.sb([128, nch, L], BF16, "MB")
        c.dma("pool", CA[:], I["CA%d" % L].rearrange("(t p) k -> p t k", p=128), writes=[CA])
        c.dma("pool", MB[:], I["MB%d" % L].rearrange("(t p) k -> p t k", p=128), writes=[MB])
        hw = [c.sb([128, nch, 512], F32, "hw") for _ in range(2)]
        abs_ = [c.sb([128, 512], F32, "hab") for _ in range(2)]
        rinv = c.sb([128, 512], F32, "hrinv")
        hs = c.sb([128, nch, 512], BF16, "hs")
        hd = c.sb([128, nch, 512], BF16, "hd")
        to = [c.sb([128, 512], BF16, "hto") for _ in range(3)]
        tf = [c.sb([128, 512], F32, "htf") for _ in range(2)]
        nto = [0]
        for order in range(2):
            for half in range(2):
                for d in range(2):
                    col0 = d * 2048 + order * 1024 + half * 512
                    H = hw[d]
                    pn = self.pacc()

                    def gen_h(tt, col0=col0):
                        ps = self.pb()
                        c.op("pe", lambda e: e.matmul(ps[:, :512], lhsT=hid2[:, tt * 128:(tt + 1) * 128], rhs=w3[:, col0:col0 + 512], start=True, stop=True), reads=[hid2, w3], writes=[ps])
                        return ps
                    nxt = gen_h(0)
                    for tt in range(nch):
                        ps = nxt
                        nxt = gen_h(tt + 1) if tt + 1 < nch else None
                        a_ = abs_[tt % 2]
                        c.op("dve", lambda e: e.tensor_tensor(out=H[:, tt, :], in0=ps[:, :512], in1=win[:, tt, half * 512:(half + 1) * 512], op=ALU.mult), reads=[ps, win], writes=[H])
                        c.op("act", lambda e: e.activation(out=a_[:], in_=H[:, tt, :], func=AF.Abs), reads=[H], writes=[a_])
                        c.op("pe", lambda e: e.matmul(pn[:, :512], lhsT=onesF[:], rhs=a_[:], start=(tt == 0), stop=(tt == nch - 1)), reads=[onesF, a_], writes=[pn])
                    c.op("dve", lambda e: e.tensor_scalar(out=rinv[:], in0=pn[:, :512], scalar1=EPS, scalar2=None, op0=ALU.add), reads=[pn], writes=[rinv])
                    c.op("dve", lambda e: e.reciprocal(out=rinv[:], in_=rinv[:]), reads=[rinv], writes=[rinv])
                    for tt in range(nch):
                        c.op("dve", lambda e: e.tensor_tensor(out=H[:, tt, :], in0=H[:, tt, :], in1=rinv[:], op=ALU.mult), reads=[H, rinv], writes=[H])
                c.op("dve", lambda e: e.tensor_tensor(out=hs[:], in0=hw[0][:], in1=hw[1][:], op=ALU.add), reads=[hw[0], hw[1]], writes=[hs])
                c.op("pool", lambda e: e.tensor_tensor(out=hd[:], in0=hw[0][:], in1=hw[1][:], op=ALU.subtract), reads=[hw[0], hw[1]], writes=[hd])
                cs = slice(half * 512, (half + 1) * 512)
                for kc in range(nch):
                    pa = self.pb()
                    for tt in range(nch):
                        c.op("pe", lambda e: e.matmul(pa[:, :512], lhsT=CA[:, tt, kc * 128:(kc + 1) * 128], rhs=hs[:, tt, :], start=(tt == 0), stop=(tt == nch - 1)), reads=[CA, hs], writes=[pa])
                    pbd = self.pb()
                    for tt in range(nch):
                        c.op("pe", lambda e: e.matmul(pbd[:, :512], lhsT=MB[:, tt, kc * 128:(kc + 1) * 128], rhs=hd[:, tt, :], start=(tt == 0), stop=(tt == nch - 1)), reads=[MB, hd], writes=[pbd])
                    oa = to[nto[0] % 3]
                    nto[0] += 1
                    c.op("act", lambda e: e.copy(out=oa[:], in_=pa[:, :512]), reads=[pa], writes=[oa])
                    c.dma("sp", TAB[order, kc][:, cs], oa[:], reads=[oa])
                    ob = to[nto[0] % 3]
                    nto[0] += 1
                    if kc > 0:
                        c.op("act", lambda e: e.copy(out=ob[:], in_=pbd[:, :512]), reads=[pbd], writes=[ob])
                        c.dma("sp", TAB[order, nch + kc][:, cs], ob[:], reads=[ob])
                    else:
                        pbs = self.pb()
                        for tt in range(nch):
                            c.op("pe", lambda e: e.matmul(pbs[:, :512], lhsT=MB[:, tt, 0:128], rhs=hs[:, tt, :], start=(tt == 0), stop=(tt == nch - 1)), reads=[MB, hs], writes=[pbs])
                        c.op("dve", lambda e: e.tensor_scalar(out=ob[:], in0=pbd[:, :512], scalar1=m0[:, 0:1], scalar2=None, op0=ALU.mult), reads=[pbd, m0], writes=[ob])
                        c.dma("sp", TAB[order, nch][:, cs], ob[:], reads=[ob])
                        t1, t2 = tf[0], tf[1]
                        c.op("dve", lambda e: e.tensor_scalar(out=t1[:], in0=pa[:, :512], scalar1=m0[:, 0:1], scalar2=None, op0=ALU.mult), reads=[pa, m0], writes=[t1])
                        c.op("dve", lambda e: e.tensor_scalar(out=t2[:], in0=pbs[:, :512], scalar1=m0[:, 1:2], scalar2=None, op0=ALU.mult), reads=[pbs, m0], writes=[t2])
                        oc = to[nto[0] % 3]
                        nto[0] += 1
                        c.op("pool", lambda e: e.tensor_tensor(out=oc[:], in0=t1[:], in1=t2[:], op=ALU.add), reads=[t1, t2], writes=[oc])
                        c.dma("sp", TAB[order, 2 * nch][:, cs], oc[:], reads=[oc])
        c.end_stage()

    def hy_data(self, l, L, tokbase, B):
        c = self.c
        I = self.I
        nch = L // 128
        ncol = B * 1024
        TAB = self.S["HTAB%d" % L]
        HV, HX = self.S["HV"], self.S["HX"]
        MO = self.scr("MO", [3, 8, 128, NT], BF16)
        Z1 = self.scr("HZ1", [8, 128, NT], F32)
        idB = self.identB
        c.begin_stage()
        CA = c.sb([128, nch, L], BF16, "CA")
        MB = c.sb([128, nch, L], BF16, "MB")
        IA = c.sb([128, nch, L], BF16, "IA")
        IB = c.sb([128, nch, L], BF16, "IB")
        for t, nm in ((CA, "CA"), (MB, "MB"), (IA, "IA"), (IB, "IB")):
            c.dma("pool", t[:], I["%s%d" % (nm, L)].rearrange("(t p) k -> p t k", p=128), writes=[t])
        AH = c.sb([128, nch, 1024], BF16, "AH")
        HB1 = c.sb([128, nch, 1024], BF16, "HB1")
        HA2 = c.sb([128, 1024], BF16, "HA2")
        hb = c.sb([128, 16], F32, "hbias")
        self.load_cols(hb, hb[:], I["hy_bias"][l].rearrange("o (n p) -> (o n) p", p=128), 16)
        z = c.sb([128, nch, ncol], BF16, "z")
        Y = c.sb([128, 2 * nch, ncol], BF16, "Y")
        ntok = B * L
        rowv = [c.sb([128, ntok], BF16, "hrv") for _ in range(2)]
        rowx = [c.sb([128, ntok], F32, "hrx") for _ in range(2)]
        rowz = [c.sb([128, ntok], F32, "hrz") for _ in range(2)]
        rowo = [c.sb([128, ntok], BF16, "hro") for _ in range(2)]
        tm = [c.sb([128, 512], F32, "htm") for _ in range(4)]

        def to_tokmajor(src_rows_fn):
            for ch in range(8):
                row = src_rows_fn(ch)
                for b in range(B):
                    for tq in range(0, nch, 4):
                        ps = self.pb()
                        pb16 = ps[:].bitcast(BF16)
                        nq = min(4, nch - tq)
                        for j in range(nq):
                            tt = tq + j
                            t0 = b * L + tt * 128
                            c.op("pe", lambda e: e.transpose(out=pb16[:, j * 128:(j + 1) * 128], in_=row[:, t0:t0 + 128], identity=idB[:]), reads=[row, idB], writes=[ps])
                        for j in range(nq):
                            tt = tq + j
                            eng = "dve" if j % 2 == 0 else "act"
                            dst = z[:, tt, b * 1024 + ch * 128:b * 1024 + (ch + 1) * 128]
                            if eng == "dve":
                                c.op("dve", lambda e: e.tensor_copy(out=dst, in_=pb16[:, j * 128:(j + 1) * 128]), reads=[ps], writes=[z])
                            else:
                                c.op("act", lambda e: e.copy(out=dst, in_=pb16[:, j * 128:(j + 1) * 128]), reads=[ps], writes=[z])

        for order in range(2):
            c.dma("sp", AH[:], TAB[order, 0:nch].rearrange("k p c -> p k c"), writes=[AH])
            c.dma("sp", HB1[:], TAB[order, nch:2 * nch].rearrange("k p c -> p k c"), writes=[HB1])
            c.dma("sp", HA2[:], TAB[order, 2 * nch], writes=[HA2])
            if order == 0:
                def rows_v(ch):
                    r = rowv[ch % 2]
                    c.dma("sp", r[:], HV[ch][:, tokbase:tokbase + ntok], writes=[r])
                    return r
                to_tokmajor(rows_v)
            else:
                def rows_z(ch):
                    rz = rowz[ch % 2]
                    r = rowv[ch % 2]
                    c.dma("sp", rz[:], Z1[ch][:, tokbase:tokbase + ntok], writes=[rz])
                    c.op("act", lambda e: e.copy(out=r[:], in_=rz[:]), reads=[rz], writes=[r])
                    return r
                to_tokmajor(rows_z)
            for ct in range(ncol // 512):
                c0 = (ct * 512) % 1024
                for kc in range(nch):
                    pa = self.pb()
                    for tt in range(nch):
                        c.op("pe", lambda e: e.matmul(pa[:, :512], lhsT=CA[:, tt, kc * 128:(kc + 1) * 128], rhs=z[:, tt, ct * 512:(ct + 1) * 512], start=(tt == 0), stop=(tt == nch - 1)), reads=[CA, z], writes=[pa])
                    pq = self.pb()
                    for tt in range(nch):
                        c.op("pe", lambda e: e.matmul(pq[:, :512], lhsT=MB[:, tt, kc * 128:(kc + 1) * 128], rhs=z[:, tt, ct * 512:(ct + 1) * 512], start=(tt == 0), stop=(tt == nch - 1)), reads=[MB, z], writes=[pq])
                    t1, t2, t3, t4 = tm
                    ah = AH[:, kc, c0:c0 + 512]
                    h1 = HB1[:, kc, c0:c0 + 512]
                    a2 = HA2[:, c0:c0 + 512] if kc == 0 else ah
                    c.op("dve", lambda e: e.tensor_tensor(out=t1[:], in0=pa[:, :512], in1=ah, op=ALU.mult), reads=[pa, AH], writes=[t1])
                    c.op("dve", lambda e: e.tensor_tensor(out=t2[:], in0=pq[:, :512], in1=h1, op=ALU.mult), reads=[pq, HB1], writes=[t2])
                    c.op("pool", lambda e: e.tensor_tensor(out=Y[:, kc, ct * 512:(ct + 1) * 512], in0=t1[:], in1=t2[:], op=ALU.subtract), reads=[t1, t2], writes=[Y])
                    c.op("dve", lambda e: e.tensor_tensor(out=t3[:], in0=pa[:, :512], in1=h1, op=ALU.mult), reads=[pa, HB1], writes=[t3])
                    c.op("dve", lambda e: e.tensor_tensor(out=t4[:], in0=pq[:, :512], in1=a2, op=ALU.mult), reads=[pq, HA2, AH], writes=[t4])
                    c.op("pool", lambda e: e.tensor_tensor(out=Y[:, nch + kc, ct * 512:(ct + 1) * 512], in0=t3[:], in1=t4[:], op=ALU.add), reads=[t3, t4], writes=[Y])
            for ch in range(8):
                rx = rowx[ch % 2]
                c.dma("sp", rx[:], HX[order * 8 + ch][:, tokbase:tokbase + ntok], writes=[rx])
                if order == 0:
                    rb = rowv[ch % 2]
                    c.dma("sp", rb[:], HV[ch][:, tokbase:tokbase + ntok], writes=[rb])
                    ro = rowz[ch % 2]
                else:
                    rb = rowz[ch % 2]
                    c.dma("sp", rb[:], Z1[ch][:, tokbase:tokbase + ntok], writes=[rb])
                    ro = rowo[ch % 2]
                for b in range(B):
                    for r0 in range(0, L, 512):
                        n = min(512, L - r0)
                        ps = self.pacc()
                        for kk in range(2 * nch):
                            M = IA if kk < nch else IB
                            c.op("pe", lambda e: e.matmul(ps[:, :n], lhsT=Y[:, kk, b * 1024 + ch * 128:b * 1024 + (ch + 1) * 128], rhs=M[:, kk % nch, r0:r0 + n], start=(kk == 0), stop=(kk == 2 * nch - 1)),
                                 reads=[Y, M], writes=[ps])
                        g0 = b * L + r0
                        t1 = tm[0]
                        c.op("dve", lambda e: e.scalar_tensor_tensor(out=t1[:, :n], in0=rb[:, g0:g0 + n], scalar=hb[:, order * 8 + ch:order * 8 + ch + 1], in1=ps[:, :n], op0=ALU.mult, op1=ALU.add), reads=[rb, hb, ps], writes=[t1])
                        c.op("dve", lambda e: e.tensor_tensor(out=ro[:, g0:g0 + n], in0=t1[:, :n], in1=rx[:, g0:g0 + n], op=ALU.mult), reads=[t1, rx], writes=[ro])
                if order == 0:
                    c.dma("sp", Z1[ch][:, tokbase:tokbase + ntok], ro[:], reads=[ro])
                else:
                    c.dma("sp", MO[1][ch][:, tokbase:tokbase + ntok], ro[:], reads=[ro])
            c.barrier()
        c.end_stage()

    def stage_hy(self, l):
        self.hy_tables(l, 256)
        self.hy_data(l, 256, 0, 4)
        self.hy_tables(l, 1024)
        self.hy_data(l, 1024, 1024, 1)

_CACHE = {}


def build_full():
    kb = KB5()
    c = kb.c
    kb.stage_input()
    c.mark('input')
    kb.stage_mod()
    c.mark('mod')
    XA = kb.S["XT0"]
    XB = kb.scr("XT1", [16, 128, NT], F32)
    XC = kb.scr("XT2", [16, 128, NT], F32)
    cur = XA
    for l in range(2):
        kb.stage_norm(cur, l, 0)
        c.mark('norm1')
        kb.stage_proj(l)
        c.end_stage()
        c.mark('proj')
        kb.stage_ret(l)
        c.mark('ret')
        kb.stage_hy(l)
        c.mark('hy')
        kb.stage_dn(l)
        c.mark('dn')
        kb.stage_merge(l, cur, XB)
        c.mark('merge+wo')
        kb.stage_norm(XB, l, 1)
        c.mark('norm2')
        kb.stage_ffn_up(l)
        c.end_stage()
        c.mark('ffn_up')
        kb.stage_ffn_down(l, XB, XC)
        c.mark('ffn_down')
        cur, XB, XC = XC, cur, XB
    kb.stage_final(cur)
    c.mark('final')
    c.finish()
    return kb


def kernel(**inputs):
    z = {k: np.ascontiguousarray(np.asarray(v)) for k, v in inputs.items()}
    if "kb" not in _CACHE:
        _CACHE["kb"] = build_full()
    kb = _CACHE["kb"]
    in_maps = []
    for core in range(8):
        m = dict(kb.consts)
        for k in WSPEC:
            m[k] = z[k]
        xp = z["x_prompt"][4 * core:4 * core + 4].reshape(1024, 2048)
        xs = z["x_sample"][core // 2]
        m["xin"] = np.ascontiguousarray(np.concatenate([xp, xs], 0))
        m["sret"] = np.ascontiguousarray(z["state_ret"][core // 2])
        m["sdn"] = np.ascontiguousarray(z["state_dn"][core // 2])
        m["cvec"] = np.ascontiguousarray(np.stack([z["c_ctx"], z["c"][core // 2]]))
        in_maps.append(m)
    res = run_bass_kernel_spmd(kb.nc, in_maps, core_ids=list(range(8))).results
    y_prompt = np.concatenate([np.asarray(res[c]["y"])[:1024].reshape(4, 256, 2048) for c in range(8)], 0).astype(np.float32)
    y_sample = np.stack([np.asarray(res[2 * b]["y"])[1024:] for b in range(4)], 0).astype(np.float32)
    new_ret = np.concatenate([np.asarray(res[c]["osret"]) for c in range(8)], 0).astype(np.float32)
    new_dn = np.concatenate([np.asarray(res[c]["osdn"]) for c in range(8)], 0).astype(np.float32)
    return (y_prompt, y_sample, new_ret, new_dn)
```

```python
import numpy as np
from contextlib import ExitStack
import concourse.bass as bass
import concourse.mybir as mybir
from concourse.bass_utils import run_bass_kernel_spmd

F32 = mybir.dt.float32
BF16 = mybir.dt.bfloat16
I32 = mybir.dt.int32
AF = mybir.ActivationFunctionType
ALU = mybir.AluOpType
AX = mybir.AxisListType


class Buf:
    __slots__ = ("t", "name", "w", "r", "psum")

    def __init__(self, t, name, psum=False):
        self.t = t
        self.name = name
        self.psum = psum
        self.w = None
        self.r = {}

    def __getitem__(self, idx):
        return self.t[idx]


class Ctx:
    SEM_LIMIT = 30000
    NDMA = 24

    def __init__(self, nc):
        self.nc = nc
        self.es = ExitStack()
        self.eng = {"pe": nc.tensor, "act": nc.scalar, "dve": nc.vector, "pool": nc.gpsimd, "sp": nc.sync}
        self.cur = {}
        self.waited = {k: {} for k in self.eng}
        self.nsem = 0
        for k in self.eng:
            self._new_sem(k)
        self.dpool = {}
        for q in ("sp", "pool", "act"):
            self.dpool[q] = [[self._alloc_sem("d%s%d" % (q, i)), 0] for i in range(self.NDMA)]
        self.dnext = {q: 0 for q in self.dpool}
        self.stage_es = None
        self.uid = 0
        self.tot = {}
        self.marks = []

    def mark(self, name):
        self.marks.append((name, dict(self.tot)))

    def _alloc_sem(self, name):
        self.nsem += 1
        s = self.es.enter_context(self.nc.semaphore("%s_%d" % (name, self.nsem)))
        if not hasattr(self, "allsems"):
            self.allsems = []
        self.allsems.append(s)
        return s

    def _new_sem(self, k):
        self.cur[k] = [self._alloc_sem("e" + k), 0]

    def begin_stage(self):
        if not hasattr(self, "stack"):
            self.stack = []
        self.stack.append(ExitStack())
        self.stage_es = self.stack[-1]

    def end_stage(self):
        self.barrier()
        self.stack.pop().close()
        self.stage_es = self.stack[-1] if self.stack else None

    def sb(self, shape, dt, name="t", persist=False):
        self.uid += 1
        nm = "%s_%d" % (name, self.uid)
        es = self.es if persist else self.stage_es
        t = es.enter_context(self.nc.sbuf_tensor(nm, list(shape), dt))
        return Buf(t, nm)

    def ps(self, shape, dt, name="p"):
        self.uid += 1
        nm = "%s_%d" % (name, self.uid)
        t = self.es.enter_context(self.nc.psum_tensor(nm, list(shape), dt))
        return Buf(t, nm, psum=True)

    def _wait(self, k, tok):
        if tok is None:
            return
        sem, val = tok
        if k == "pe" and sem is self.cur["pe"][0]:
            return
        w = self.waited[k]
        key = id(sem)
        if w.get(key, (None, 0))[1] >= val:
            return
        w[key] = (sem, val)
        self.eng[k].wait_ge(sem, val)

    def _deps(self, k, reads, writes):
        for b in reads:
            self._wait(k, b.w)
            if b.psum:
                for tok in list(b.r.values()):
                    self._wait(k, tok)
        for b in writes:
            self._wait(k, b.w)
            for tok in list(b.r.values()):
                self._wait(k, tok)

    def _commit(self, tok, reads, writes):
        for b in reads:
            b.r[id(tok[0])] = tok
        for b in writes:
            b.w = tok
            b.r = {}

    def op(self, k, fn, reads=(), writes=()):
        self._deps(k, reads, writes)
        c = self.cur[k]
        if c[1] >= self.SEM_LIMIT:
            self._new_sem(k)
            c = self.cur[k]
        c[1] += 1
        self.tot[k] = self.tot.get(k, 0) + 1
        fn(self.eng[k]).then_inc(c[0], 1)
        tok = (c[0], c[1])
        self._commit(tok, reads, writes)
        return tok

    def dma(self, q, out, in_, reads=(), writes=(), **kw):
        pool = self.dpool[q]
        i = self.dnext[q]
        self.dnext[q] = (i + 1) % len(pool)
        slot = pool[i]
        if slot[1] > 0:
            self._wait(q, (slot[0], slot[1]))
        if slot[1] >= self.SEM_LIMIT:
            slot[0] = self._alloc_sem("d" + q)
            slot[1] = 0
        self._deps(q, reads, writes)
        slot[1] += 16
        self.eng[q].dma_start(out=out, in_=in_, **kw).then_inc(slot[0], 16)
        tok = (slot[0], slot[1])
        self._commit(tok, reads, writes)
        return tok

    def barrier(self):
        toks = [(c[0], c[1]) for c in self.cur.values() if c[1] > 0]
        for q in self.dpool:
            for slot in self.dpool[q]:
                if slot[1] > 0:
                    toks.append((slot[0], slot[1]))
        for k in self.eng:
            for tok in toks:
                self._wait(k, tok)

    def finish(self):
        self.barrier()
        self.es.close()

import math
import numpy as np


def make_consts():
    f = np.float32
    C = {}
    i = np.arange(128)
    C["ident"] = np.eye(128, dtype=f)
    C["ones"] = np.ones((128, 128), f)
    C["UF"] = (i[:, None] <= i[None, :]).astype(f)
    C["UB"] = (i[:, None] >= i[None, :]).astype(f)
    C["SU"] = (i[:, None] < i[None, :]).astype(f)
    C["SL"] = (i[:, None] > i[None, :]).astype(f)
    L = 1024
    pos = np.arange(L, dtype=np.float64)
    pos_r = np.floor(pos / 64)
    pos_c = pos % 64
    inv = 10000.0 ** (-np.arange(16, dtype=np.float64) / 16)
    cosT = np.zeros((128, L))
    sinT = np.zeros((128, L))
    for p in range(128):
        q = p % 64
        half = q // 32
        x2 = (q % 32) // 16
        fi = q % 16
        ang = (pos_r if half == 0 else pos_c) * inv[fi]
        cosT[p] = np.cos(ang)
        sinT[p] = np.sin(ang) * (1.0 if x2 else -1.0)
    C["ropeC"] = cosT.astype(f)
    C["ropeS"] = sinT.astype(f)
    for nm, LL in (("S", 1024), ("P", 256)):
        u = np.arange(2 * LL - 128)[None, :]
        p = np.arange(128)[:, None]
        d = u - p - (LL - 128)
        C["dpos" + nm] = np.maximum(d, 0).astype(f)
        C["dneg" + nm] = np.maximum(-d, 0).astype(f)
        C["dz" + nm] = (d == 0).astype(f)
    j = np.arange(128)[:, None] + 128 * np.arange(2)[None, :]
    C["expfb"] = np.concatenate([255 - j, j], axis=1).astype(f)
    ii = np.arange(1024)[None, :].repeat(128, 0)
    C["idx1"] = (ii + 1).astype(f)
    C["idx2"] = (1024 - ii).astype(f)
    for LL in (256, 1024):
        t = np.linspace(0.0, 1.0, LL, dtype=np.float32)[:, None].astype(np.float64)
        wpos = 2.0 * math.pi * np.arange(LL, dtype=np.float64)[:, None] / LL
        fr = np.linspace(1e-4, 15, 16, dtype=np.float32)[None, :].astype(np.float64)
        feats = np.concatenate([t, np.cos(fr * wpos), -np.sin(fr * wpos)], axis=-1)
        C["featsT%d" % LL] = np.ascontiguousarray(feats.T).astype(f)
        deltas = np.abs(np.linspace(math.log(1e-2) / 0.3, math.log(1e-2) / 1.5, 1024, dtype=np.float32)).astype(np.float64)
        C["win%d" % LL] = np.exp(-t * deltas[None, :]).astype(f)
        N = 2 * LL
        tt = np.arange(LL, dtype=np.float64)[:, None]
        kk = np.arange(LL, dtype=np.float64)[None, :]
        ang = 2.0 * math.pi * tt * kk / N
        CA = np.cos(ang)
        MB = -np.sin(ang)
        MB[:, 0] = (-1.0) ** np.arange(LL)
        IA = (2.0 / N) * np.cos(ang.T)
        IA[0, :] = 1.0 / N
        IB = -(2.0 / N) * np.sin(ang.T)
        IB[0, :] = ((-1.0) ** np.arange(LL)) / N
        C["CA%d" % LL] = CA.astype(f)
        C["MB%d" % LL] = MB.astype(f)
        C["IA%d" % LL] = IA.astype(f)
        C["IB%d" % LL] = IB.astype(f)
    ii, jj = np.meshgrid(np.arange(128), np.arange(128), indexing="ij")
    ML = np.stack([(((ii >> (s + 1)) == (jj >> (s + 1))) & ((ii >> s) != (jj >> s)) & (ii > jj)).astype(f) for s in range(7)])
    MU = np.ascontiguousarray(ML.transpose(0, 2, 1))
    C["MLU"] = np.ascontiguousarray(np.concatenate([ML, MU], axis=2).transpose(1, 0, 2))
    C["MUL"] = np.ascontiguousarray(np.concatenate([MU, ML], axis=2).transpose(1, 0, 2))
    m0 = np.ones((128, 2), f)
    m0[0, 0] = 0.0
    m0[:, 1] = 1.0 - m0[:, 0]
    C["m0"] = m0
    C["eps"] = np.full((128, 1), 1e-6, f)
    return C


import math

NT = 2048
TILES = [(0, 512), (512, 512), (1024, 512), (1536, 512)]
EPS = 1e-6

WSPEC = dict(
    w_ada=[2, 2048, 12288], b_ada=[2, 12288], norm1=[2, 2048], w_in=[2, 2048, 16416], ret_decay=[2, 2, 8],
    hy_short=[2, 3, 3072], hy_w1=[2, 33, 64], hy_b1=[2, 64], hy_freq1=[2, 64], hy_w2=[2, 64, 64], hy_b2=[2, 64],
    hy_freq2=[2, 64], hy_w3=[2, 64, 4096], hy_bias=[2, 2, 1024], dn_conv=[2, 3, 3072], dn_a_log=[2, 2, 8],
    dn_dt_bias=[2, 2, 8], dn_norm=[2, 128], p_ret=[2, 1024, 2048], p_hy=[2, 1024, 2048], p_dn=[2, 1024, 2048],
    w_o=[2, 2048, 2048], norm2=[2, 2048], w_up=[2, 2048, 11008], ffn_conv=[2, 3, 11008], w_down=[2, 5504, 2048],
    norm_f=[2048])


class KB:
    def __init__(self, stop_after=None, dbg=(), wspec=None, ext_in=()):
        self.stop_after = stop_after
        self.dbg = set(dbg)
        nc = self.nc = bass.Bass("TRN2", target_bir_lowering=False)
        self.c = Ctx(nc)
        self.I = {}
        self.consts = make_consts()
        for k, v in self.consts.items():
            self.I[k] = nc.dram_tensor(k, list(v.shape), F32, kind="ExternalInput").ap()
        self.ext_in = set(ext_in)
        for k, s in (wspec or WSPEC).items():
            self.I[k] = nc.dram_tensor(k, s, F32, kind="ExternalInput").ap()
        self.I["xin"] = nc.dram_tensor("xin", [NT, 2048], F32, kind="ExternalInput").ap()
        self.I["sret"] = nc.dram_tensor("sret", [2, 2, 8, 64, 128], F32, kind="ExternalInput").ap()
        self.I["sdn"] = nc.dram_tensor("sdn", [2, 2, 8, 128, 128], F32, kind="ExternalInput").ap()
        self.I["cvec"] = nc.dram_tensor("cvec", [2, 2048], F32, kind="ExternalInput").ap()
        self.O = {}
        self.O["y"] = nc.dram_tensor("y", [NT, 2048], F32, kind="ExternalOutput").ap()
        self.O["osret"] = nc.dram_tensor("osret", [4, 2, 2, 8, 64, 128], F32, kind="ExternalOutput").ap()
        self.O["osdn"] = nc.dram_tensor("osdn", [4, 2, 2, 8, 128, 128], F32, kind="ExternalOutput").ap()
        self.S = {}
        c = self.c
        self.PB = [c.ps([128, 512], F32, "pb") for _ in range(8)]
        self.pbi = 0
        self.identF = c.sb([128, 128], F32, "identF", persist=True)
        self.identB = c.sb([128, 128], BF16, "identB", persist=True)
        self.onesB = c.sb([128, 128], BF16, "onesB", persist=True)
        self.onesF = c.sb([128, 128], F32, "onesF", persist=True)
        self.epsT = c.sb([128, 1], F32, "epsT", persist=True)
        self.rows = c.sb([128, 128], F32, "rows", persist=True)
        c.dma("sp", self.identF[:], self.I["ident"], writes=[self.identF])
        c.dma("pool", self.identB[:], self.I["ident"], writes=[self.identB])
        c.dma("pool", self.onesB[:], self.I["ones"], writes=[self.onesB])
        c.dma("sp", self.onesF[:], self.I["ones"], writes=[self.onesF])
        c.dma("sp", self.epsT[:], self.I["eps"], writes=[self.epsT])
        self.HT = None
        self.modT = c.sb([128, 2, 96, 2], F32, "modT", persist=True)
        self.sca = c.sb([128, 2, 2, 16, 2], F32, "sca", persist=True)
        self.gbraw = c.sb([128, 16, 32], F32, "gbraw", persist=True)

    def scr(self, name, shape, dt):
        if name not in self.S:
            kind = "ExternalOutput" if name in self.dbg else ("ExternalInput" if name in self.ext_in else "Internal")
            self.S[name] = self.nc.dram_tensor(name, list(shape), dt, kind=kind).ap()
        return self.S[name]

    def pb(self):
        self.pbi = (self.pbi + 1) % 6
        return self.PB[self.pbi]

    def pacc(self):
        self.pai = (getattr(self, "pai", 0) + 1) % 2
        return self.PB[6 + self.pai]

    def load_cols(self, dstbuf, dst_ap, src_rows, n):
        c = self.c
        rows = self.rows
        c.dma("sp", rows[:n, :], src_rows, writes=[rows])
        ps = self.pb()
        idf = self.identF
        c.op("pe", lambda e: e.transpose(out=ps[:, :n], in_=rows[:n, :], identity=idf[:n, :n]), reads=[rows, idf], writes=[ps])
        c.op("dve", lambda e: e.tensor_copy(out=dst_ap, in_=ps[:, :n]), reads=[ps], writes=[dstbuf])

    def stage_input(self, own=True):
        c = self.c
        XT = self.scr("XT0", [16, 128, NT], F32)
        if own:
            c.begin_stage()
        xs = [c.sb([128, 2048], F32, "xs") for _ in range(2)]
        xo = [c.sb([128, 16, 128], F32, "xo") for _ in range(2)]
        idf = self.identF
        for tt in range(16):
            a = xs[tt % 2]
            o = xo[tt % 2]
            c.dma("sp", a[:], self.I["xin"][tt * 128:(tt + 1) * 128, :], writes=[a])
            for g in range(4):
                ps = self.pb()
                for j in range(4):
                    ch = g * 4 + j
                    c.op("pe", lambda e: e.transpose(out=ps[:, j * 128:(j + 1) * 128], in_=a[:, ch * 128:(ch + 1) * 128], identity=idf[:]),
                         reads=[a, idf], writes=[ps])
                eng = "dve" if g % 2 == 0 else "act"
                if eng == "dve":
                    c.op("dve", lambda e: e.tensor_copy(out=o[:, g * 4:(g + 1) * 4, :], in_=ps[:].rearrange("p (j t) -> p j t", j=4)), reads=[ps], writes=[o])
                else:
                    c.op("act", lambda e: e.copy(out=o[:, g * 4:(g + 1) * 4, :], in_=ps[:].rearrange("p (j t) -> p j t", j=4)), reads=[ps], writes=[o])
            c.dma("sp", XT[:, :, tt * 128:(tt + 1) * 128].rearrange("c p t -> p c t"), o[:], reads=[o])
        if own:
            c.end_stage()

    def stage_mod(self):
        c = self.c
        I = self.I
        c.begin_stage()
        scT = c.sb([128, 2, 16], F32, "scT")
        self.load_cols(scT, scT[:].rearrange("p b k -> p (b k)"), I["cvec"].rearrange("b (k p) -> (b k) p", p=128), 32)
        c.op("act", lambda e: e.activation(out=scT[:], in_=scT[:], func=AF.Silu), reads=[scT], writes=[scT])
        bada = c.sb([128, 96], F32, "bada")
        nw = c.sb([128, 16], F32, "nw")
        wa = [c.sb([128, 16, 512], F32, "wa") for _ in range(4)]
        mrow = [c.sb([2, 512], F32, "mrow") for _ in range(2)]
        idf = self.identF
        for l in range(2):
            self.load_cols(bada, bada[:], I["b_ada"][l].rearrange("(n p) -> n p", p=128), 96)
            ps = self.pacc()
            for blk in range(24):
                w = wa[blk % 4]
                c.dma(("sp", "act", "pool")[blk % 3], w[:], I["w_ada"][l][:, blk * 512:(blk + 1) * 512].rearrange("(k p) n -> p k n", p=128), writes=[w])
                pr = self.pb()
                for k in range(16):
                    c.op("pe", lambda e: e.matmul(pr[:2, :512], lhsT=scT[:, :, k], rhs=w[:, k, :], start=(k == 0), stop=(k == 15)), reads=[w, scT], writes=[pr])
                mr = mrow[blk % 2]
                c.op("act", lambda e: e.copy(out=mr[:], in_=pr[:2, :512]), reads=[pr], writes=[mr])
                for j in range(4):
                    ch = blk * 4 + j
                    c.op("pe", lambda e: e.transpose(out=ps[:, ch * 2:ch * 2 + 2], in_=mr[:2, j * 128:(j + 1) * 128], identity=idf[:2, :2]), reads=[mr, idf], writes=[ps])
            mt = self.modT
            for b in range(2):
                c.op("dve", lambda e: e.tensor_tensor(out=mt[:, l, :, b], in0=ps[:, 0:192].rearrange("p (c b) -> p c b", b=2)[:, :, b], in1=bada[:], op=ALU.add),
                     reads=[ps, bada], writes=[mt])
            for which, (nm, scb) in enumerate((("norm1", 16), ("norm2", 64))):
                self.load_cols(nw, nw[:], I[nm][l].rearrange("(n p) -> n p", p=128), 16)
                sc = self.sca
                for b in range(2):
                    c.op("dve", lambda e: e.scalar_tensor_tensor(out=sc[:, l, which, :, b], in0=mt[:, l, scb:scb + 16, b], scalar=1.0, in1=nw[:], op0=ALU.add, op1=ALU.mult),
                         reads=[mt, nw], writes=[sc])
        c.end_stage()

    def stage_norm(self, XT, l, which):
        c = self.c
        c.begin_stage()
        self.HT = c.sb([128, 16, NT], BF16, "HT")
        c.begin_stage()
        shb = 0 if which == 0 else 48
        xs = [c.sb([128, 16, 256], F32, "nx") for _ in range(2)]
        sq = c.sb([128, 16, 256], BF16, "nsq")
        tmps = [c.sb([128, 256], F32, "ntmp") for _ in range(6)]
        r = c.sb([128, 256], F32, "nr")
        HT, sc, mt, ones, eps = self.HT, self.sca, self.modT, self.onesB, self.epsT
        for ti in range(8):
            t0 = ti * 256
            b = 0 if t0 < 1024 else 1
            x = xs[ti % 2]
            c.dma("sp", x[:], XT[:, :, t0:t0 + 256].rearrange("c p t -> p c t"), writes=[x])
            c.op("act", lambda e: e.activation(out=sq[:], in_=x[:], func=AF.Square), reads=[x], writes=[sq])
            ps = self.pb()
            for ch in range(16):
                c.op("pe", lambda e: e.matmul(ps[:, :256], lhsT=ones[:], rhs=sq[:, ch, :], start=(ch == 0), stop=(ch == 15)), reads=[ones, sq], writes=[ps])
            c.op("act", lambda e: e.activation(out=r[:], in_=ps[:, :256], func=AF.Sqrt, scale=1.0 / 2048, bias=eps[:, 0:1]), reads=[ps, eps], writes=[r])
            c.op("dve", lambda e: e.reciprocal(out=r[:], in_=r[:]), reads=[r], writes=[r])
            for ch in range(16):
                tmp = tmps[ch % 6]
                c.op("dve", lambda e: e.tensor_tensor(out=tmp[:], in0=x[:, ch, :], in1=r[:], op=ALU.mult), reads=[x, r], writes=[tmp])
                c.op("act", lambda e: e.activation(out=HT[:, ch, t0:t0 + 256], in_=tmp[:], func=AF.Identity,
                                                   scale=sc[:, l, which, ch, b:b + 1], bias=mt[:, l, shb + ch, b:b + 1]), reads=[tmp, sc, mt], writes=[HT])
        c.end_stage()


class KB2(KB):
    def wbufs(self, KC, n=3, width=256):
        return [self.c.sb([128, KC, width], BF16, "wb") for _ in range(n)]

    def lin_fm(self, W, blocks, IN, KC, epi, tiles=TILES, wb=None, prep=None):
        c = self.c
        if wb is None:
            wb = self.wbufs(KC)
        for bi, (c0, ncol) in enumerate(blocks):
            self.wbi = getattr(self, "wbi", 0) + 1
            w = wb[self.wbi % len(wb)]
            c.dma("pool", w[:, :KC, :ncol], W[:, c0:c0 + ncol].rearrange("(k p) n -> p k n", p=128), writes=[w])
            aux = prep(w, c0, ncol) if prep else None
            for off in range(0, ncol, 128):
                n = min(128, ncol - off)
                for ti, (t0, nt) in enumerate(tiles):
                    ps = self.pb()
                    for k in range(KC):
                        c.op("pe", lambda e: e.matmul(ps[:n, :nt], lhsT=w[:, k, off:off + n], rhs=IN[:, k, t0:t0 + nt], start=(k == 0), stop=(k == KC - 1)),
                             reads=[w, IN], writes=[ps])
                    epi(c0 + off, n, ti, t0, nt, ps, (aux, off))

    def lin_tm(self, W, blocks, IN, KC, epi, ttiles, wb=None):
        c = self.c
        if wb is None:
            wb = self.wbufs(KC)
        for bi, (c0, ncol) in enumerate(blocks):
            self.wbi = getattr(self, "wbi", 0) + 1
            w = wb[self.wbi % len(wb)]
            c.dma("pool", w[:, :KC, :ncol], W[:, c0:c0 + ncol].rearrange("(k p) n -> p k n", p=128), writes=[w])
            for tt in ttiles:
                ps = self.pb()
                for k in range(KC):
                    c.op("pe", lambda e: e.matmul(ps[:, :ncol], lhsT=IN[:, k, tt * 128:(tt + 1) * 128], rhs=w[:, k, :ncol], start=(k == 0), stop=(k == KC - 1)),
                         reads=[w, IN], writes=[ps])
                epi(c0, ncol, tt, ps)

    @staticmethod
    def blocks(a, b, step=256):
        return [(x, min(step, b - x)) for x in range(a, b, step)]

    def conv_row(self, src, dst, wt, ci, ntap_chunks):
        c = self.c
        w0 = wt[:, 0 * ntap_chunks + ci:0 * ntap_chunks + ci + 1]
        w1 = wt[:, 1 * ntap_chunks + ci:1 * ntap_chunks + ci + 1]
        w2 = wt[:, 2 * ntap_chunks + ci:2 * ntap_chunks + ci + 1]
        c.op("act", lambda e: e.activation(out=dst[:], in_=src[:], func=AF.Copy, scale=w1), reads=[src, wt], writes=[dst])
        sp = src[:, 0:1024].rearrange("p (s t) -> p s t", t=256)
        dp = dst[:, 0:1024].rearrange("p (s t) -> p s t", t=256)
        c.op("dve", lambda e: e.scalar_tensor_tensor(out=dp[:, :, 1:256], in0=sp[:, :, 0:255], scalar=w0, in1=dp[:, :, 1:256], op0=ALU.mult, op1=ALU.add), reads=[src, wt, dst], writes=[dst])
        c.op("dve", lambda e: e.scalar_tensor_tensor(out=dp[:, :, 0:255], in0=sp[:, :, 1:256], scalar=w2, in1=dp[:, :, 0:255], op0=ALU.mult, op1=ALU.add), reads=[src, wt, dst], writes=[dst])
        c.op("dve", lambda e: e.scalar_tensor_tensor(out=dst[:, 1025:2048], in0=src[:, 1024:2047], scalar=w0, in1=dst[:, 1025:2048], op0=ALU.mult, op1=ALU.add), reads=[src, wt, dst], writes=[dst])
        c.op("dve", lambda e: e.scalar_tensor_tensor(out=dst[:, 1024:2047], in0=src[:, 1025:2048], scalar=w2, in1=dst[:, 1024:2047], op0=ALU.mult, op1=ALU.add), reads=[src, wt, dst], writes=[dst])

    def stage_proj(self, l):
        c = self.c
        I = self.I
        W = I["w_in"][l]
        HT = self.HT
        c.begin_stage()
        wb = self.wbufs(16, 3, 512)
        QK = self.scr("QK", [8, 128, NT], BF16)
        VTM = self.scr("VTM", [16, 128, 1024], BF16)
        KTM = self.scr("KTM", [8, 128, 512], BF16)
        SRG = self.scr("SRG", [8, 128, NT], BF16)
        HV = self.scr("HV", [8, 128, NT], BF16)
        HX = self.scr("HX", [16, 128, NT], F32)
        DQK = self.scr("DQK", [16, 128, NT], BF16)
        DVT = self.scr("DVT", [8, 128, NT], BF16)
        SDZ = self.scr("SDZ", [8, 128, NT], BF16)
        GATE = self.scr("GATE", [48, 128, NT], BF16)
        rowb = [c.sb([128, NT], BF16, "rowb") for _ in range(2)]
        rowf = [c.sb([128, NT], F32, "rowf") for _ in range(2)]
        rowg = [c.sb([128, NT], F32, "rowg") for _ in range(2)]
        cnt = [0]

        ropeC = c.sb([128, 1024], F32, "ropeC")
        ropeS = c.sb([128, 1024], F32, "ropeS")
        c.dma("sp", ropeC[:], I["ropeC"], writes=[ropeC])
        c.dma("sp", ropeS[:], I["ropeS"], writes=[ropeS])
        wperm = [c.sb([128, 16, 256], BF16, "wperm") for _ in range(2)]
        t1 = c.sb([128, 512], F32, "t1")
        t2 = c.sb([128, 512], F32, "t2")
        pc = [0]

        def prep_qk(w, c0, ncol):
            wp = wperm[pc[0] % 2]
            pc[0] += 1
            src = w[:, :, 0:256].rearrange("p k (a two s) -> p k a two s", two=2, s=16)
            dst = wp[:].rearrange("p k (a two s) -> p k a two s", two=2, s=16)
            c.op("dve", lambda e: e.tensor_copy(out=dst[:, :, :, 0, :], in_=src[:, :, :, 1, :]), reads=[w], writes=[wp])
            c.op("act", lambda e: e.copy(out=dst[:, :, :, 1, :], in_=src[:, :, :, 0, :]), reads=[w], writes=[wp])
            return wp

        def epi_qk(col0, n, ti, t0, nt, ps, auxoff):
            wp, off = auxoff
            ci = col0 // 128
            row = rowb[ci % 2]
            scale = 1.0 if ci < 4 else 0.125
            if ti < 2:
                c.op("act", lambda e: e.activation(out=row[:, t0:t0 + nt], in_=ps[:, :nt], func=AF.Copy, scale=scale), reads=[ps], writes=[row])
            else:
                ps2 = self.pb()
                for k in range(16):
                    c.op("pe", lambda e: e.matmul(ps2[:, :nt], lhsT=wp[:, k, off:off + 128], rhs=HT[:, k, t0:t0 + nt], start=(k == 0), stop=(k == 15)), reads=[wp, HT], writes=[ps2])
                s0 = t0 - 1024
                c.op("dve", lambda e: e.tensor_tensor(out=t1[:, :nt], in0=ps[:, :nt], in1=ropeC[:, s0:s0 + nt], op=ALU.mult), reads=[ps, ropeC], writes=[t1])
                c.op("dve", lambda e: e.tensor_tensor(out=t2[:, :nt], in0=ps2[:, :nt], in1=ropeS[:, s0:s0 + nt], op=ALU.mult), reads=[ps2, ropeS], writes=[t2])
                c.op("dve", lambda e: e.tensor_tensor(out=t1[:, :nt], in0=t1[:, :nt], in1=t2[:, :nt], op=ALU.add), reads=[t1, t2], writes=[t1])
                c.op("act", lambda e: e.activation(out=row[:, t0:t0 + nt], in_=t1[:, :nt], func=AF.Copy, scale=scale), reads=[t1], writes=[row])
            if ti == 3:
                c.dma("sp", QK[ci], row[:], reads=[row])

        self.lin_fm(W, self.blocks(0, 1024), HT, 16, epi_qk, wb=wb, prep=prep_qk)

        stv = [c.sb([128, 512], BF16, "stv") for _ in range(4)]

        def epi_v(col0, ncol, tt, ps):
            s = stv[cnt[0] % 4]
            cnt[0] += 1
            if cnt[0] % 2:
                c.op("dve", lambda e: e.tensor_copy(out=s[:, :ncol], in_=ps[:, :ncol]), reads=[ps], writes=[s])
            else:
                c.op("act", lambda e: e.copy(out=s[:, :ncol], in_=ps[:, :ncol]), reads=[ps], writes=[s])
            c.dma("sp", VTM[tt][:, col0 - 1024:col0 - 1024 + ncol], s[:, :ncol], reads=[s])

        def epi_ktm(col0, ncol, tt, ps):
            s = stv[cnt[0] % 4]
            cnt[0] += 1
            c.op("act", lambda e: e.activation(out=s[:, :ncol], in_=ps[:, :ncol], func=AF.Copy, scale=0.125), reads=[ps], writes=[s])
            c.dma("sp", KTM[tt][:, col0 - 512:col0 - 512 + ncol], s[:, :ncol], reads=[s])

        self.lin_tm(W, self.blocks(1024, 2048, 512), HT, 16, epi_v, range(16), wb=wb)
        self.lin_tm(W, self.blocks(512, 1024, 512), HT, 16, epi_ktm, range(8), wb=wb)

        def mk_epi_act(base, dst, func):
            def epi(col0, n, ti, t0, nt, ps, aux):
                ci = (col0 - base) // 128
                row = rowb[ci % 2]
                c.op("act", lambda e: e.activation(out=row[:, t0:t0 + nt], in_=ps[:, :nt], func=func), reads=[ps], writes=[row])
                if ti == 3:
                    c.dma("sp", dst[ci], row[:], reads=[row])
            return epi

        self.lin_fm(W, self.blocks(2048, 3072, 512), HT, 16, mk_epi_act(2048, SRG, AF.Silu), wb=wb)
        self.lin_fm(W, self.blocks(9216, 10240, 512), HT, 16, mk_epi_act(9216, SDZ, AF.Silu), wb=wb)
        self.lin_fm(W, self.blocks(10272, 16416, 512), HT, 16, mk_epi_act(10272, GATE, AF.Sigmoid), wb=wb)

        hyw = c.sb([128, 72], F32, "hyw")
        self.load_cols(hyw, hyw[:], I["hy_short"][l].rearrange("k (n p) -> (k n) p", p=128), 72)
        dnw = c.sb([128, 72], F32, "dnw")
        self.load_cols(dnw, dnw[:], I["dn_conv"][l].rearrange("k (n p) -> (k n) p", p=128), 72)

        def epi_hy(col0, n, ti, t0, nt, ps, aux):
            ci = (col0 - 3072) // 128
            raw = rowf[ci % 2]
            c.op("act", lambda e: e.copy(out=raw[:, t0:t0 + nt], in_=ps[:, :nt]), reads=[ps], writes=[raw])
            if ti == 3:
                cv = rowg[ci % 2]
                self.conv_row(raw, cv, hyw, ci, 24)
                if ci < 8:
                    row = rowb[ci % 2]
                    c.op("act", lambda e: e.copy(out=row[:], in_=cv[:]), reads=[cv], writes=[row])
                    c.dma("sp", HV[ci], row[:], reads=[row])
                else:
                    c.dma("sp", HX[ci - 8], cv[:], reads=[cv])

        self.lin_fm(W, self.blocks(3072, 6144, 512), HT, 16, epi_hy, wb=wb)

        sqb = c.sb([128, NT], BF16, "sqb")
        rns = [c.sb([128, 512], F32, "rn") for _ in range(3)]
        ones, eps = self.onesB, self.epsT

        dn_pending = []

        def epi_dn(col0, n, ti, t0, nt, ps, aux):
            ci = (col0 - 6144) // 128
            raw = rowf[ci % 2]
            c.op("act", lambda e: e.copy(out=raw[:, t0:t0 + nt], in_=ps[:, :nt]), reads=[ps], writes=[raw])
            if ti == 3:
                while dn_pending:
                    dn_pending.pop(0)()
                dn_pending.append(lambda ci=ci, raw=raw: dn_post(ci, raw))

        def dn_post(ci, raw):
            cv = rowg[ci % 2]
            self.conv_row(raw, cv, dnw, ci, 24)
            c.op("act", lambda e: e.activation(out=cv[:], in_=cv[:], func=AF.Silu), reads=[cv], writes=[cv])
            row = rowb[ci % 2]
            if ci < 16:
                c.op("act", lambda e: e.activation(out=sqb[:], in_=cv[:], func=AF.Square), reads=[cv], writes=[sqb])
                for tj, (u0, nu) in enumerate(TILES):
                    p2 = self.pb()
                    c.op("pe", lambda e: e.matmul(p2[:, :nu], lhsT=ones[:], rhs=sqb[:, u0:u0 + nu], start=True, stop=True), reads=[ones, sqb], writes=[p2])
                    rn = rns[tj % 3]
                    c.op("act", lambda e: e.activation(out=rn[:, :nu], in_=p2[:, :nu], func=AF.Ln, bias=eps[:, 0:1]), reads=[p2, eps], writes=[rn])
                    c.op("act", lambda e: e.activation(out=rn[:, :nu], in_=rn[:, :nu], func=AF.Exp, scale=-0.5), reads=[rn], writes=[rn])
                    sc_ = (128 ** -0.5) if ci < 8 else 1.0
                    c.op("dve", lambda e: e.scalar_tensor_tensor(out=row[:, u0:u0 + nu], in0=cv[:, u0:u0 + nu], scalar=sc_, in1=rn[:, :nu], op0=ALU.mult, op1=ALU.mult),
                         reads=[cv, rn], writes=[row])
                c.dma("sp", DQK[ci], row[:], reads=[row])
            else:
                c.op("act", lambda e: e.copy(out=row[:], in_=cv[:]), reads=[cv], writes=[row])
                c.dma("sp", DVT[ci - 16], row[:], reads=[row])

        self.lin_fm(W, self.blocks(6144, 9216, 512), HT, 16, epi_dn, wb=wb)
        while dn_pending:
            dn_pending.pop(0)()

        gbraw = self.gbraw

        def epi_gb(col0, ncol, tt, ps):
            c.op("dve", lambda e: e.tensor_copy(out=gbraw[:, tt, :], in_=ps[:, :32]), reads=[ps], writes=[gbraw])

        self.lin_tm(W, [(10240, 32)], HT, 16, epi_gb, range(16), wb=wb)
        c.end_stage()

    def stage_merge(self, l, XTin, XTout):
        c = self.c
        I = self.I
        c.begin_stage()
        MO = self.scr("MO", [3, 8, 128, NT], BF16)
        GATE = self.S["GATE"]
        MIX = self.scr("MIX", [16, 128, NT], BF16)
        mo = [c.sb([128, 8, NT], BF16, "mo") for _ in range(3)]
        for b in range(3):
            c.dma("sp", mo[b][:], MO[b].rearrange("c p t -> p c t"), writes=[mo[b]])
        wp = [c.sb([128, 8, 512], BF16, "wp") for _ in range(3)]
        gr = [c.sb([128, NT], BF16, "gr") for _ in range(4)]
        acc = [c.sb([128, NT], F32, "acc") for _ in range(4)]
        mixb = [c.sb([128, NT], BF16, "mixb") for _ in range(2)]
        tmp = c.sb([128, 512], F32, "mtmp")
        PW = [I["p_ret"][l], I["p_hy"][l], I["p_dn"][l]]
        n = 0
        nw_ = 0
        for jg in range(4):
            for b in range(3):
                w = wp[nw_ % 3]
                nw_ += 1
                c.dma("pool", w[:], PW[b][:, jg * 512:(jg + 1) * 512].rearrange("(k p) n -> p k n", p=128), writes=[w])
                for jj in range(4):
                    j = jg * 4 + jj
                    a = acc[jj]
                    g = gr[n % 4]
                    n += 1
                    c.dma("sp", g[:], GATE[b * 16 + j], writes=[g])
                    for ti, (t0, nt) in enumerate(TILES):
                        ps = self.pb()
                        for k_ in range(8):
                            c.op("pe", lambda e: e.matmul(ps[:, :nt], lhsT=w[:, k_, jj * 128:(jj + 1) * 128], rhs=mo[b][:, k_, t0:t0 + nt], start=(k_ == 0), stop=(k_ == 7)), reads=[w, mo[b]], writes=[ps])
                        if b == 0:
                            c.op("dve", lambda e: e.tensor_tensor(out=a[:, t0:t0 + nt], in0=ps[:, :nt], in1=g[:, t0:t0 + nt], op=ALU.mult), reads=[ps, g], writes=[a])
                        else:
                            c.op("dve", lambda e: e.tensor_tensor(out=tmp[:, :nt], in0=ps[:, :nt], in1=g[:, t0:t0 + nt], op=ALU.mult), reads=[ps, g], writes=[tmp])
                            c.op("dve", lambda e: e.tensor_tensor(out=a[:, t0:t0 + nt], in0=a[:, t0:t0 + nt], in1=tmp[:, :nt], op=ALU.add), reads=[a, tmp], writes=[a])
            for jj in range(4):
                j = jg * 4 + jj
                m = mixb[j % 2]
                a = acc[jj]
                c.op("act", lambda e: e.copy(out=m[:], in_=a[:]), reads=[a], writes=[m])
                c.dma("sp", MIX[j], m[:], reads=[m])
        c.end_stage()
        c.begin_stage()
        mix = c.sb([128, 16, NT], BF16, "mix")
        c.dma("sp", mix[:], MIX.rearrange("c p t -> p c t"), writes=[mix])
        self.resid_epilogue(I["w_o"][l], 2048, mix, 16, l, 32, XTin, XTout)
        c.end_stage()

    def resid_epilogue(self, W, ncols, IN, KC, l, gbase, XTin, XTout, tiles=TILES, wb=None):
        c = self.c
        mt = self.modT
        xr = [c.sb([128, NT], F32, "xr") for _ in range(2)]
        lo = tiles[0][0]
        hi = tiles[-1][0] + tiles[-1][1]

        def epi(col0, n, ti, t0, nt, ps, aux):
            j = col0 // 128
            x = xr[j % 2]
            if ti == 0:
                c.dma("sp", x[:, lo:hi], XTin[j][:, lo:hi], writes=[x])
            b = 0 if t0 < 1024 else 1
            c.op("dve", lambda e: e.scalar_tensor_tensor(out=x[:, t0:t0 + nt], in0=ps[:, :nt], scalar=mt[:, l, gbase + j, b:b + 1], in1=x[:, t0:t0 + nt], op0=ALU.mult, op1=ALU.add),
                 reads=[ps, mt, x], writes=[x])
            if ti == len(tiles) - 1:
                c.dma("sp", XTout[j][:, lo:hi], x[:, lo:hi], reads=[x])

        if wb is None:
            wb = self.wbufs(KC, 3, 512)
        self.lin_fm(W, self.blocks(0, ncols, 512), IN, KC, epi, tiles=tiles, wb=wb)

    def stage_ffn_up(self, l):
        c = self.c
        I = self.I
        ACTT = self.scr("ACTT", [43, 128, NT], BF16)
        c.begin_stage()
        fw = c.sb([128, 3 * 86], F32, "fw")
        for k in range(3):
            self.load_cols(fw, fw[:, k * 86:(k + 1) * 86], I["ffn_conv"][l][k].rearrange("(n p) -> n p", p=128), 86)
        rowf = [c.sb([128, NT], F32, "frow") for _ in range(3)]
        ga = [c.sb([128, NT], F32, "ga") for _ in range(2)]
        gb = [c.sb([128, NT], F32, "gb") for _ in range(2)]
        ab = [c.sb([128, NT], BF16, "ab") for _ in range(2)]
        wb = self.wbufs(16, 4, 512)
        HT = self.HT
        W = I["w_up"][l]
        cnt = [0]
        nb = 0
        for blk in range(11):
            c0 = blk * 512
            ncol = min(512, 5504 - c0)
            wa_, wg_ = wb[nb % 4], wb[(nb + 1) % 4]
            nb += 2
            c.dma("pool", wa_[:, :, :ncol], W[:, c0:c0 + ncol].rearrange("(k p) n -> p k n", p=128), writes=[wa_])
            c.dma("pool", wg_[:, :, :ncol], W[:, 5504 + c0:5504 + c0 + ncol].rearrange("(k p) n -> p k n", p=128), writes=[wg_])
            for off in range(0, ncol, 128):
                ci = (c0 + off) // 128
                for which, w in ((0, wa_), (1, wg_)):
                    raw = rowf[cnt[0] % 3]
                    cnt[0] += 1
                    for ti, (t0, nt) in enumerate(TILES):
                        ps = self.pb()
                        for kk in range(16):
                            c.op("pe", lambda e: e.matmul(ps[:, :nt], lhsT=w[:, kk, off:off + 128], rhs=HT[:, kk, t0:t0 + nt], start=(kk == 0), stop=(kk == 15)), reads=[w, HT], writes=[ps])
                        if ti % 2 == 0:
                            c.op("act", lambda e: e.copy(out=raw[:, t0:t0 + nt], in_=ps[:, :nt]), reads=[ps], writes=[raw])
                        else:
                            c.op("dve", lambda e: e.tensor_copy(out=raw[:, t0:t0 + nt], in_=ps[:, :nt]), reads=[ps], writes=[raw])
                    if which == 0:
                        g = ga[ci % 2]
                        self.conv_row(raw, g, fw, ci, 86)
                        c.op("act", lambda e: e.activation(out=g[:], in_=g[:], func=AF.Silu), reads=[g], writes=[g])
                    else:
                        g = ga[ci % 2]
                        g2 = gb[ci % 2]
                        self.conv_row(raw, g2, fw, 43 + ci, 86)
                        a_ = ab[ci % 2]
                        c.op("dve", lambda e: e.tensor_tensor(out=a_[:], in0=g[:], in1=g2[:], op=ALU.mult), reads=[g, g2], writes=[a_])
                        c.dma("sp", ACTT[ci], a_[:], reads=[a_])
        c.end_stage()

    def stage_ffn_down(self, l, XTin, XTout):
        c = self.c
        I = self.I
        ACTT = self.S["ACTT"]
        for half in range(2):
            c.begin_stage()
            a = c.sb([128, 43, 1024], BF16, "actin")
            c.dma("sp", a[:], ACTT[:, :, half * 1024:(half + 1) * 1024].rearrange("c p t -> p c t"), writes=[a])

            class Shift:
                def __init__(s, buf, sh):
                    s.buf, s.sh = buf, sh
            wb = self.wbufs(43, 2, 512)
            tiles = [(half * 1024, 512), (half * 1024 + 512, 512)]
            self._resid_shift(I["w_down"][l], a, 43, l, 80, XTin, XTout, tiles, half * 1024, wb)
            c.end_stage()

    def _resid_shift(self, W, IN, KC, l, gbase, XTin, XTout, tiles, tshift, wb):
        c = self.c
        mt = self.modT
        xr = [c.sb([128, 1024], F32, "xr2") for _ in range(2)]
        blocks = self.blocks(0, 2048, 512)
        for bi, (c0, ncol) in enumerate(blocks):
            w = wb[bi % len(wb)]
            c.dma("pool", w[:, :KC, :ncol], W[:, c0:c0 + ncol].rearrange("(k p) n -> p k n", p=128), writes=[w])
            for off in range(0, ncol, 128):
                j = (c0 + off) // 128
                x = xr[j % 2]
                c.dma("sp", x[:], XTin[j][:, tshift:tshift + 1024], writes=[x])
                for ti, (t0, nt) in enumerate(tiles):
                    ps = self.pb()
                    for k in range(KC):
                        c.op("pe", lambda e: e.matmul(ps[:, :nt], lhsT=w[:, k, off:off + 128], rhs=IN[:, k, t0 - tshift:t0 - tshift + nt], start=(k == 0), stop=(k == KC - 1)),
                             reads=[w, IN], writes=[ps])
                    b = 0 if t0 < 1024 else 1
                    c.op("dve", lambda e: e.scalar_tensor_tensor(out=x[:, t0 - tshift:t0 - tshift + nt], in0=ps[:, :nt], scalar=mt[:, l, gbase + j, b:b + 1],
                                                                 in1=x[:, t0 - tshift:t0 - tshift + nt], op0=ALU.mult, op1=ALU.add), reads=[ps, mt, x], writes=[x])
                c.dma("sp", XTout[j][:, tshift:tshift + 1024], x[:], reads=[x])

    def stage_final(self, XT):
        c = self.c
        I = self.I
        c.begin_stage()
        nf = c.sb([128, 16], F32, "nf")
        self.load_cols(nf, nf[:], I["norm_f"].rearrange("(n p) -> n p", p=128), 16)
        xs = [c.sb([128, 16, 128], F32, "fx") for _ in range(2)]
        sq = c.sb([128, 16, 128], BF16, "fsq")
        r = c.sb([128, 128], F32, "fr")
        yo = [c.sb([128, 2048], F32, "yo") for _ in range(2)]
        ones, eps, idf = self.onesB, self.epsT, self.identF
        for tt in range(16):
            t0 = tt * 128
            x = xs[tt % 2]
            y = yo[tt % 2]
            c.dma("sp", x[:], XT[:, :, t0:t0 + 128].rearrange("c p t -> p c t"), writes=[x])
            c.op("act", lambda e: e.activation(out=sq[:], in_=x[:], func=AF.Square), reads=[x], writes=[sq])
            ps = self.pb()
            for ch in range(16):
                c.op("pe", lambda e: e.matmul(ps[:, :128], lhsT=ones[:], rhs=sq[:, ch, :], start=(ch == 0), stop=(ch == 15)), reads=[ones, sq], writes=[ps])
            c.op("act", lambda e: e.activation(out=r[:], in_=ps[:, :128], func=AF.Sqrt, scale=1.0 / 2048, bias=eps[:, 0:1]), reads=[ps, eps], writes=[r])
            c.op("dve", lambda e: e.reciprocal(out=r[:], in_=r[:]), reads=[r], writes=[r])
            for ch in range(16):
                c.op("dve", lambda e: e.scalar_tensor_tensor(out=x[:, ch, :], in0=x[:, ch, :], scalar=nf[:, ch:ch + 1], in1=r[:], op0=ALU.mult, op1=ALU.mult), reads=[x, nf, r], writes=[x])
            for g in range(4):
                p2 = self.pb()
                for j in range(4):
                    ch = g * 4 + j
                    c.op("pe", lambda e: e.transpose(out=p2[:, j * 128:(j + 1) * 128], in_=x[:, ch, :], identity=idf[:]), reads=[x, idf], writes=[p2])
                if g % 2 == 0:
                    c.op("dve", lambda e: e.tensor_copy(out=y[:, g * 512:(g + 1) * 512], in_=p2[:]), reads=[p2], writes=[y])
                else:
                    c.op("act", lambda e: e.copy(out=y[:, g * 512:(g + 1) * 512], in_=p2[:]), reads=[p2], writes=[y])
            c.dma("sp", self.O["y"][t0:t0 + 128, :], y[:], reads=[y])
        c.end_stage()


class KB3(KB2):
    def stage_ret(self, l):
        c = self.c
        I = self.I
        c.begin_stage()
        QK, VTM, KTM, SRG = self.S["QK"], self.S["VTM"], self.S["KTM"], self.S["SRG"]
        MO = self.scr("MO", [3, 8, 128, NT], BF16)
        OSR = self.O["osret"]
        ones, eps = self.onesB, self.epsT
        V = c.sb([128, 16, 1024], BF16, "V")
        c.dma("sp", V[:], VTM.rearrange("t p n -> p t n"), writes=[V])
        Kt = c.sb([128, 8, 512], BF16, "Kt")
        c.dma("sp", Kt[:], KTM.rearrange("t p n -> p t n"), writes=[Kt])
        lg = c.sb([128, 16], F32, "lg")
        c.dma("sp", lg[:], I["ret_decay"][l].rearrange("d h -> (d h)").partition_broadcast(128), writes=[lg])
        c.op("act", lambda e: e.activation(out=lg[:], in_=lg[:], func=AF.Exp, scale=-1.0), reads=[lg], writes=[lg])
        c.op("act", lambda e: e.activation(out=lg[:], in_=lg[:], func=AF.Ln, bias=self.onesF[:, 0:1]), reads=[lg], writes=[lg])
        c.op("dve", lambda e: e.tensor_scalar(out=lg[:], in0=lg[:], scalar1=-1.0, scalar2=None, op0=ALU.mult), reads=[lg], writes=[lg])
        tabs = {}
        for nm, w in (("S", 1920), ("P", 384)):
            for k in ("dpos", "dneg", "dz"):
                t = c.sb([128, w], F32, k + nm)
                c.dma("sp", t[:], I[k + nm], writes=[t])
                tabs[k + nm] = t
        expfb = c.sb([128, 4], F32, "expfb")
        c.dma("sp", expfb[:], I["expfb"], writes=[expfb])
        idx1 = c.sb([128, 1024], F32, "idx1")
        idx2 = c.sb([128, 1024], F32, "idx2")
        c.dma("sp", idx1[:], I["idx1"], writes=[idx1])
        c.dma("sp", idx2[:], I["idx2"], writes=[idx2])
        KD = c.sb([128, 2, 8, 2], F32, "KD")
        for d in range(2):
            for h in range(8):
                c.op("act", lambda e: e.activation(out=KD[:, d, h, :], in_=expfb[:, d * 2:d * 2 + 2], func=AF.Exp, scale=lg[:, d * 8 + h:d * 8 + h + 1]), reads=[expfb, lg], writes=[KD])
        TabS2 = [c.sb([128, 1920], F32, "TabS") for _ in range(2)]
        TabP2 = [c.sb([128, 384], F32, "TabP") for _ in range(2)]
        ttmp = c.sb([128, 1920], F32, "ttmp")
        qrow = c.sb([128, NT], BF16, "qrow")
        krow = c.sb([128, NT], BF16, "krow")
        lgsel = c.sb([128, 2], F32, "lgsel")
        dec = c.sb([128, 1024], F32, "dec")
        qdf = c.sb([128, 1024], BF16, "qdf")
        qdb = c.sb([128, 1024], BF16, "qdb")
        S0 = c.sb([128, 2, 128], BF16, "S0")
        srg = c.sb([128, NT], BF16, "srg")
        sm = [c.sb([128, 512], BF16, "sm") for _ in range(3)]
        osb = c.sb([128, 512], F32, "osb")
        osq = c.sb([128, 512], BF16, "osq")
        rr = c.sb([128, 512], F32, "rr")
        orow = c.sb([128, NT], BF16, "orow")
        kf = [c.sb([128, 64], BF16, "kf") for _ in range(4)]
        sst = [c.sb([64, 128], F32, "sst") for _ in range(4)]
        n_sm = [0]
        n_kf = [0]

        def build_tab(Tab, nm, w, h):
            dp, dn, dz = tabs["dpos" + nm], tabs["dneg" + nm], tabs["dz" + nm]
            c.op("dve", lambda e: e.tensor_scalar(out=ttmp[:, :w], in0=dp[:], scalar1=lg[:, h:h + 1], scalar2=None, op0=ALU.mult), reads=[dp, lg], writes=[ttmp])
            c.op("dve", lambda e: e.scalar_tensor_tensor(out=ttmp[:, :w], in0=dn[:], scalar=lg[:, 8 + h:9 + h], in1=ttmp[:, :w], op0=ALU.mult, op1=ALU.add), reads=[dn, lg, ttmp], writes=[ttmp])
            c.op("act", lambda e: e.activation(out=ttmp[:, :w], in_=ttmp[:, :w], func=AF.Exp), reads=[ttmp], writes=[ttmp])
            c.op("dve", lambda e: e.tensor_tensor(out=Tab[:, :w], in0=ttmp[:, :w], in1=dz[:], op=ALU.add), reads=[ttmp, dz], writes=[Tab])

        def finish_o(pso, n, h, tok0):
            c.op("act", lambda e: e.copy(out=osb[:, :n], in_=pso[:, :n]), reads=[pso], writes=[osb])
            c.op("act", lambda e: e.activation(out=osq[:, :n], in_=osb[:, :n], func=AF.Square), reads=[osb], writes=[osq])
            p2 = self.pb()
            c.op("pe", lambda e: e.matmul(p2[:, :n], lhsT=ones[:], rhs=osq[:, :n], start=True, stop=True), reads=[ones, osq], writes=[p2])
            c.op("act", lambda e: e.activation(out=rr[:, :n], in_=p2[:, :n], func=AF.Sqrt, scale=1.0 / 128, bias=eps[:, 0:1]), reads=[p2, eps], writes=[rr])
            c.op("dve", lambda e: e.reciprocal(out=rr[:, :n], in_=rr[:, :n]), reads=[rr], writes=[rr])
            c.op("dve", lambda e: e.tensor_tensor(out=osb[:, :n], in0=osb[:, :n], in1=rr[:, :n], op=ALU.mult), reads=[osb, rr], writes=[osb])
            c.op("dve", lambda e: e.tensor_tensor(out=orow[:, tok0:tok0 + n], in0=osb[:, :n], in1=srg[:, tok0:tok0 + n], op=ALU.mult), reads=[osb, srg], writes=[orow])

        srg2 = [srg, c.sb([128, NT], BF16, "srg2")]
        orow2 = [orow, c.sb([128, NT], BF16, "orow2")]
        sm.append(c.sb([128, 512], BF16, "sm"))
        NS = len(sm)
        pending = []

        def flush(keep):
            while len(pending) > keep:
                pending.pop(0)()

        def scores_loop(n_j, mk_score, mk_acc):
            nxt = mk_score(0)
            for jc in range(n_j):
                cur = nxt
                nxt = mk_score(jc + 1) if jc + 1 < n_j else None
                mk_acc(jc, cur)

        for h in range(8):
            hp, po = h // 2, (h % 2) * 64
            srg_h, orow_h = srg2[h % 2], orow2[h % 2]
            if h % 2 == 0:
                c.dma("sp", qrow[:], QK[hp], writes=[qrow])
                c.dma("sp", krow[:], QK[4 + hp], writes=[krow])
                for d in range(2):
                    c.op("dve", lambda e: e.tensor_copy(out=lgsel[0:64, d:d + 1], in_=lg[0:64, d * 8 + h:d * 8 + h + 1]), reads=[lg], writes=[lgsel])
                    c.op("dve", lambda e: e.tensor_copy(out=lgsel[64:128, d:d + 1], in_=lg[64:128, d * 8 + h + 1:d * 8 + h + 2]), reads=[lg], writes=[lgsel])
                c.op("act", lambda e: e.activation(out=dec[:], in_=idx1[:], func=AF.Exp, scale=lgsel[:, 0:1]), reads=[idx1, lgsel], writes=[dec])
                c.op("dve", lambda e: e.tensor_tensor(out=qdf[:], in0=qrow[:, 1024:2048], in1=dec[:], op=ALU.mult), reads=[qrow, dec], writes=[qdf])
                c.op("act", lambda e: e.activation(out=dec[:], in_=idx2[:], func=AF.Exp, scale=lgsel[:, 1:2]), reads=[idx2, lgsel], writes=[dec])
                c.op("dve", lambda e: e.tensor_tensor(out=qdb[:], in0=qrow[:, 1024:2048], in1=dec[:], op=ALU.mult), reads=[qrow, dec], writes=[qdb])
            c.dma("sp", srg_h[:], SRG[h], writes=[srg_h])
            for d in range(2):
                c.dma("pool", S0[po:po + 64, d, :], I["sret"][l, d, h], writes=[S0])
            if h == 0:
                build_tab(TabS2[0], "S", 1920, 0)
                build_tab(TabP2[0], "P", 384, 0)
            TabS, TabP = TabS2[h % 2], TabP2[h % 2]
            if h + 1 < 8:
                build_tab(TabS2[(h + 1) % 2], "S", 1920, h + 1)
                build_tab(TabP2[(h + 1) % 2], "P", 384, h + 1)

            def fin(pso, n, tok0, h=h, srg_h=srg_h, orow_h=orow_h, store=False):
                def f():
                    c.op("act", lambda e: e.copy(out=osb[:, :n], in_=pso[:, :n]), reads=[pso], writes=[osb])
                    c.op("act", lambda e: e.activation(out=osq[:, :n], in_=osb[:, :n], func=AF.Square), reads=[osb], writes=[osq])
                    p2 = self.pb()
                    c.op("pe", lambda e: e.matmul(p2[:, :n], lhsT=ones[:], rhs=osq[:, :n], start=True, stop=True), reads=[ones, osq], writes=[p2])
                    c.op("act", lambda e: e.activation(out=rr[:, :n], in_=p2[:, :n], func=AF.Ln, scale=1.0 / 128, bias=eps[:, 0:1]), reads=[p2, eps], writes=[rr])
                    c.op("act", lambda e: e.activation(out=rr[:, :n], in_=rr[:, :n], func=AF.Exp, scale=-0.5), reads=[rr], writes=[rr])
                    c.op("dve", lambda e: e.tensor_tensor(out=osb[:, :n], in0=osb[:, :n], in1=rr[:, :n], op=ALU.mult), reads=[osb, rr], writes=[osb])
                    c.op("dve", lambda e: e.tensor_tensor(out=orow_h[:, tok0:tok0 + n], in0=osb[:, :n], in1=srg_h[:, tok0:tok0 + n], op=ALU.mult), reads=[osb, srg_h], writes=[orow_h])
                    if store:
                        c.dma("sp", MO[0][h], orow_h[:], reads=[orow_h])
                return f

            for i0 in (0, 512):
                pso = self.pacc()

                def mk_score(jc, i0=i0):
                    pss = self.pb()
                    c.op("pe", lambda e: e.matmul(pss[:, :512], lhsT=krow[po:po + 64, 1024 + jc * 128:1024 + (jc + 1) * 128], rhs=qrow[po:po + 64, 1024 + i0:1024 + i0 + 512], start=True, stop=True),
                         reads=[krow, qrow], writes=[pss])
                    return pss

                def mk_acc(jc, pss, i0=i0, pso=pso):
                    s = sm[n_sm[0] % NS]
                    n_sm[0] += 1
                    u0 = i0 - 128 * jc + 896
                    c.op("dve", lambda e: e.tensor_tensor(out=s[:], in0=pss[:, :512], in1=TabS[:, u0:u0 + 512], op=ALU.mult), reads=[pss, TabS], writes=[s])
                    c.op("pe", lambda e: e.matmul(pso[:, :512], lhsT=V[:, 8 + jc, h * 128:(h + 1) * 128], rhs=s[:], start=(jc == 0), stop=False), reads=[V, s], writes=[pso])

                scores_loop(8, mk_score, mk_acc)
                c.op("pe", lambda e: e.matmul(pso[:, :512], lhsT=S0[po:po + 64, 0, :], rhs=qdf[po:po + 64, i0:i0 + 512], start=False, stop=False), reads=[S0, qdf], writes=[pso])
                c.op("pe", lambda e: e.matmul(pso[:, :512], lhsT=S0[po:po + 64, 1, :], rhs=qdb[po:po + 64, i0:i0 + 512], start=False, stop=True), reads=[S0, qdb], writes=[pso])
                pending.append(fin(pso, 512, 1024 + i0))
                flush(1)
            for s_ in range(4):
                b0 = s_ * 256
                pso = self.pacc()

                def mk_score(jc, b0=b0):
                    pss = self.pb()
                    c.op("pe", lambda e: e.matmul(pss[:, :256], lhsT=krow[po:po + 64, b0 + jc * 128:b0 + (jc + 1) * 128], rhs=qrow[po:po + 64, b0:b0 + 256], start=True, stop=True),
                         reads=[krow, qrow], writes=[pss])
                    return pss

                def mk_acc(jc, pss, pso=pso, s_=s_):
                    s = sm[n_sm[0] % NS]
                    n_sm[0] += 1
                    u0 = 128 - 128 * jc
                    c.op("dve", lambda e: e.tensor_tensor(out=s[:, :256], in0=pss[:, :256], in1=TabP[:, u0:u0 + 256], op=ALU.mult), reads=[pss, TabP], writes=[s])
                    c.op("pe", lambda e: e.matmul(pso[:, :256], lhsT=V[:, s_ * 2 + jc, h * 128:(h + 1) * 128], rhs=s[:, :256], start=(jc == 0), stop=(jc == 1)), reads=[V, s], writes=[pso])

                scores_loop(2, mk_score, mk_acc)
                pending.append(fin(pso, 256, b0, store=(s_ == 3)))
                flush(1)
                for d in range(2):
                    pst = self.pb()
                    for tt in range(2):
                        k_ = kf[n_kf[0] % 4]
                        n_kf[0] += 1
                        c.op("dve", lambda e: e.tensor_scalar(out=k_[:], in0=Kt[:, s_ * 2 + tt, h * 64:(h + 1) * 64], scalar1=KD[:, d, h, tt:tt + 1], scalar2=None, op0=ALU.mult), reads=[Kt, KD], writes=[k_])
                        c.op("pe", lambda e: e.matmul(pst[:64, :128], lhsT=k_[:], rhs=V[:, s_ * 2 + tt, h * 128:(h + 1) * 128], start=(tt == 0), stop=(tt == 1)), reads=[k_, V], writes=[pst])
                    st = sst[(s_ * 2 + d) % 4]
                    c.op("act", lambda e: e.copy(out=st[:], in_=pst[:64, :128]), reads=[pst], writes=[st])
                    c.dma("sp", OSR[s_, l, d, h], st[:], reads=[st])
        flush(0)
        c.end_stage()


import os


class KB4(KB3):
    def stage_dn(self, l):
        c = self.c
        I = self.I
        c.begin_stage()
        DQK, DVT, SDZ = self.S["DQK"], self.S["DVT"], self.S["SDZ"]
        MO = self.scr("MO", [3, 8, 128, NT], BF16)
        OSD = self.O["osdn"]
        onesF, onesB, eps, idF, idB = self.onesF, self.onesB, self.epsT, self.identF, self.identB
        gbraw = self.gbraw
        cm = {}
        for nm in ("UF", "UB", "SU", "SL"):
            t = c.sb([128, 128], F32, nm)
            c.dma("sp", t[:], I[nm], writes=[t])
            cm[nm] = t
        alog = c.sb([128, 16], F32, "alog")
        dtb = c.sb([128, 16], F32, "dtb")
        c.dma("sp", alog[:], I["dn_a_log"][l].rearrange("d h -> (d h)").partition_broadcast(128), writes=[alog])
        c.dma("sp", dtb[:], I["dn_dt_bias"][l].rearrange("d h -> (d h)").partition_broadcast(128), writes=[dtb])
        dnn = c.sb([128, 1], F32, "dnn")
        c.dma("sp", dnn[:], I["dn_norm"][l].rearrange("(p o) -> p o", o=1), writes=[dnn])
        c.op("act", lambda e: e.activation(out=alog[:], in_=alog[:], func=AF.Exp), reads=[alog], writes=[alog])
        c.op("dve", lambda e: e.tensor_scalar(out=alog[:], in0=alog[:], scalar1=-1.0, scalar2=None, op0=ALU.mult), reads=[alog], writes=[alog])
        G = c.sb([128, 16, 16], F32, "G")
        BT = c.sb([128, 16, 16], F32, "BT")
        NBT = c.sb([128, 16, 16], F32, "NBT")
        for tt in range(16):
            c.op("dve", lambda e: e.tensor_tensor(out=G[:, tt, :], in0=gbraw[:, tt, 0:16], in1=dtb[:], op=ALU.add), reads=[gbraw, dtb], writes=[G])
        c.op("act", lambda e: e.activation(out=G[:], in_=G[:], func=AF.Exp), reads=[G], writes=[G])
        c.op("act", lambda e: e.activation(out=G[:], in_=G[:], func=AF.Ln, bias=onesF[:, 0:1]), reads=[G, onesF], writes=[G])
        for tt in range(16):
            c.op("dve", lambda e: e.tensor_tensor(out=G[:, tt, :], in0=G[:, tt, :], in1=alog[:], op=ALU.mult), reads=[G, alog], writes=[G])
        c.op("act", lambda e: e.activation(out=BT[:], in_=gbraw[:, :, 16:32], func=AF.Sigmoid), reads=[gbraw], writes=[BT])
        c.op("dve", lambda e: e.tensor_scalar(out=NBT[:], in0=BT[:], scalar1=-1.0, scalar2=None, op0=ALU.mult), reads=[BT], writes=[NBT])
        GC = c.sb([128, 16, 16], F32, "GC")
        GL = c.sb([128, 16, 16], F32, "GL")
        for tt in range(16):
            for d in range(2):
                U = cm["UF"] if d == 0 else cm["UB"]
                ps = self.pb()
                c.op("pe", lambda e: e.matmul(ps[:, 0:8], lhsT=U[:], rhs=G[:, tt, d * 8:d * 8 + 8], start=True, stop=True), reads=[U, G], writes=[ps])
                c.op("pe", lambda e: e.matmul(ps[:, 8:16], lhsT=onesF[:], rhs=G[:, tt, d * 8:d * 8 + 8], start=True, stop=True), reads=[onesF, G], writes=[ps])
                c.op("dve", lambda e: e.tensor_copy(out=GC[:, tt, d * 8:d * 8 + 8], in_=ps[:, 0:8]), reads=[ps], writes=[GC])
                c.op("dve", lambda e: e.tensor_copy(out=GL[:, tt, d * 8:d * 8 + 8], in_=ps[:, 8:16]), reads=[ps], writes=[GL])
        BK = c.sb([128, 16, 16], F32, "BK")
        KDS = c.sb([128, 16, 16], F32, "KDS")
        EGL = c.sb([128, 16, 16], F32, "EGL")
        c.op("act", lambda e: e.activation(out=BK[:], in_=GC[:], func=AF.Exp), reads=[GC], writes=[BK])
        c.op("dve", lambda e: e.tensor_tensor(out=BK[:], in0=BK[:], in1=BT[:], op=ALU.mult), reads=[BK, BT], writes=[BK])
        c.op("dve", lambda e: e.tensor_tensor(out=KDS[:], in0=GL[:], in1=GC[:], op=ALU.subtract), reads=[GL, GC], writes=[KDS])
        c.op("act", lambda e: e.activation(out=KDS[:], in_=KDS[:], func=AF.Exp), reads=[KDS], writes=[KDS])
        c.op("act", lambda e: e.activation(out=EGL[:], in_=GL[:], func=AF.Exp), reads=[GL], writes=[EGL])

        if "GDBG" in self.dbg:
            gd = self.scr("GDBG", [4, 128, 16, 16], F32)
            for i_, t_ in enumerate((G, BT, GC, GL)):
                c.dma("sp", gd[i_], t_[:], reads=[t_])
        OACC = [c.sb([128, NT], F32, "oacc") for _ in range(8)]
        c.begin_stage()
        Sf = [c.sb([128, 128], F32, "Sf") for _ in range(8)]
        Sb = [c.sb([128, 128], BF16, "Sb") for _ in range(8)]
        NI = 8

        def ring(shape, dt, nm, n=NI):
            bufs = [c.sb(shape, dt, nm) for _ in range(n)]
            st = [0]

            def nxt():
                st[0] += 1
                return bufs[st[0] % n]
            return nxt
        r_q = ring([128, 128], BF16, "rq")
        r_k = ring([128, 128], BF16, "rk")
        r_v = ring([128, 128], BF16, "rv")
        r_gbc = ring([128, 128], F32, "rgbc")
        r_t = ring([128, 128], F32, "rt", 2 * NI)
        r_vb = ring([128, 128], F32, "rvb")
        r_kbe = ring([128, 128], F32, "rkbe")
        r_kd = ring([128, 128], F32, "rkd")
        r_nw = ring([128, 128], F32, "rnw")
        r_at = ring([128, 128], BF16, "rat")
        r_qd = ring([128, 128], BF16, "rqd")
        r_vn = ring([128, 128], F32, "rvn")
        r_vnb = ring([128, 128], BF16, "rvnb")
        r_so = ring([128, 128], F32, "rso", 3)
        r_tw = ring([128, 256], F32, "rtw", 2 * NI)
        r_tq = ring([128, 256], F32, "rtq", NI)
        r_am = ring([128, 256], F32, "ram", 2 * NI)
        r_pp = ring([128, 256], F32, "rpp", NI)
        mlu = c.sb([128, 7, 256], F32, "mlu")
        mul = c.sb([128, 7, 256], F32, "mul")
        c.dma("sp", mlu[:], I["MLU"], writes=[mlu])
        c.dma("sp", mul[:], I["MUL"], writes=[mul])
        id2 = c.sb([128, 256], F32, "id2")
        c.dma("sp", id2[:, 0:128], I["ident"], writes=[id2])
        c.dma("sp", id2[:, 128:256], I["ident"], writes=[id2])
        PB = self.PB
        pbn = [0]

        def pbank():
            pbn[0] += 1
            return PB[pbn[0] % 8]

        def process(tt, h, d, last_seq):
            t0 = tt * 128
            col = d * 8 + h
            U = cm["UF"] if d == 0 else cm["UB"]
            MS = cm["SL"] if d == 0 else cm["SU"]
            MI = cm["UF"] if d == 0 else cm["UB"]
            gc_ap = GC[:, tt, col:col + 1]
            qT, kT, vT = r_q(), r_k(), r_v()
            c.dma("sp", qT[:], DQK[h][:, t0:t0 + 128], writes=[qT])
            c.dma("sp", kT[:], DQK[8 + h][:, t0:t0 + 128], writes=[kT])
            c.dma("sp", vT[:], DVT[h][:, t0:t0 + 128], writes=[vT])
            gbc = r_gbc()
            c.op("act", lambda e: e.activation(out=gbc[:], in_=onesF[:], func=AF.Copy, scale=G[:, tt, col:col + 1]), reads=[onesF, G], writes=[gbc])
            yield
            bank = PB[h]
            ptr = pR = pG = pT = ps1 = ps2 = pw = psv = pso = pss = bank
            ptb = bank[:].bitcast(BF16)
            c.op("pe", lambda e: e.transpose(out=ptb[:, 0:128], in_=kT[:], identity=idB[:]), reads=[kT, idB], writes=[ptr])
            c.op("pe", lambda e: e.transpose(out=ptb[:, 128:256], in_=vT[:], identity=idB[:]), reads=[vT, idB], writes=[ptr])
            c.op("pe", lambda e: e.matmul(pR[:, 128:256], lhsT=gbc[:], rhs=U[:], start=True, stop=True), reads=[gbc, U], writes=[pR])
            c.op("pe", lambda e: e.matmul(pG[:, 256:384], lhsT=kT[:], rhs=kT[:], start=True, stop=True), reads=[kT], writes=[pG])
            c.op("pe", lambda e: e.matmul(pG[:, 384:512], lhsT=kT[:], rhs=qT[:], start=True, stop=True), reads=[kT, qT], writes=[pG])
            yield
            vb, kbe, kd = r_vb(), r_kbe(), r_kd()
            c.op("dve", lambda e: e.tensor_scalar(out=kbe[:], in0=ptb[:, 0:128], scalar1=BK[:, tt, col:col + 1], scalar2=None, op0=ALU.mult), reads=[ptr, BK], writes=[kbe])
            c.op("dve", lambda e: e.tensor_scalar(out=kd[:], in0=ptb[:, 0:128], scalar1=KDS[:, tt, col:col + 1], scalar2=None, op0=ALU.mult), reads=[ptr, KDS], writes=[kd])
            c.op("dve", lambda e: e.tensor_scalar(out=vb[:], in0=ptb[:, 128:256], scalar1=BT[:, tt, col:col + 1], scalar2=None, op0=ALU.mult), reads=[ptr, BT], writes=[vb])
            d1, d2 = r_t(), r_t()
            c.op("dve", lambda e: e.tensor_scalar(out=d1[:], in0=pR[:, 128:256], scalar1=gc_ap, scalar2=0.0, op0=ALU.subtract, op1=ALU.max), reads=[pR, GC], writes=[d1])
            c.op("dve", lambda e: e.tensor_scalar(out=d2[:], in0=pR[:, 128:256], scalar1=gc_ap, scalar2=0.0, op0=ALU.subtract, op1=ALU.min), reads=[pR, GC], writes=[d2])
            er = gbc
            c.op("act", lambda e: e.activation(out=er[:], in_=pR[:, 128:256], func=AF.Exp), reads=[pR], writes=[er])
            yield
            c.op("act", lambda e: e.activation(out=d1[:], in_=d1[:], func=AF.Exp, scale=-1.0), reads=[d1], writes=[d1])
            c.op("act", lambda e: e.activation(out=d2[:], in_=d2[:], func=AF.Exp), reads=[d2], writes=[d2])
            qd = r_qd()
            c.op("pool", lambda e: e.tensor_tensor(out=qd[:], in0=qT[:], in1=er[:], op=ALU.mult), reads=[qT, er], writes=[qd])
            yield
            pp = r_pp()
            c.op("dve", lambda e: e.scalar_tensor_tensor(out=d1[:], in0=pG[:, 256:384], scalar=NBT[:, tt, col:col + 1], in1=d1[:], op0=ALU.mult, op1=ALU.mult), reads=[pG, NBT, d1], writes=[d1])
            c.op("dve", lambda e: e.tensor_tensor(out=d2[:], in0=pG[:, 384:512], in1=d2[:], op=ALU.mult), reads=[pG, d2], writes=[d2])
            yield
            c.op("pool", lambda e: e.tensor_tensor(out=pp[:, 0:128], in0=d1[:], in1=MS[:], op=ALU.mult), reads=[d1, MS], writes=[pp])
            at = r_at()
            c.op("pool", lambda e: e.tensor_tensor(out=at[:], in0=d2[:], in1=MI[:], op=ALU.mult), reads=[d2, MI], writes=[at])
            yield
            c.op("pe", lambda e: e.transpose(out=pT[:, :128], in_=pp[:, 0:128], identity=idF[:]), reads=[pp, idF], writes=[pT])
            yield
            c.op("act", lambda e: e.copy(out=pp[:, 128:256], in_=pT[:, :128]), reads=[pT], writes=[pp])
            yield
            MM = mlu if d == 0 else mul
            am = r_am()
            c.op("pool", lambda e: e.tensor_tensor(out=am[:], in0=pp[:], in1=MM[:, 0, :], op=ALU.mult), reads=[pp, MM], writes=[am])
            yield
            tw = r_tw()
            c.op("dve", lambda e: e.tensor_tensor(out=tw[:], in0=am[:], in1=id2[:], op=ALU.add), reads=[am, id2], writes=[tw])
            am = r_am()
            c.op("pool", lambda e: e.tensor_tensor(out=am[:], in0=pp[:], in1=MM[:, 1, :], op=ALU.mult), reads=[pp, MM], writes=[am])
            yield
            for s in range(1, 7):
                c.op("pe", lambda e: e.matmul(ps1[:, 0:128], lhsT=am[:, 128:256], rhs=tw[:, 0:128], start=True, stop=True), reads=[am, tw], writes=[ps1])
                c.op("pe", lambda e: e.matmul(ps1[:, 128:256], lhsT=am[:, 0:128], rhs=tw[:, 128:256], start=True, stop=True), reads=[am, tw], writes=[ps1])
                if s < 6:
                    am = r_am()
                    c.op("pool", lambda e: e.tensor_tensor(out=am[:], in0=pp[:], in1=MM[:, s + 1, :], op=ALU.mult), reads=[pp, MM], writes=[am])
                yield
                p1 = r_tq()
                c.op("act", lambda e: e.copy(out=p1[:], in_=ps1[:, 0:256]), reads=[ps1], writes=[p1])
                yield
                c.op("pe", lambda e: e.matmul(ps2[:, 256:384], lhsT=tw[:, 128:256], rhs=p1[:, 0:128], start=True, stop=True), reads=[tw, p1], writes=[ps2])
                c.op("pe", lambda e: e.matmul(ps2[:, 384:512], lhsT=tw[:, 0:128], rhs=p1[:, 128:256], start=True, stop=True), reads=[tw, p1], writes=[ps2])
                yield
                ntw = r_tw()
                c.op("dve", lambda e: e.tensor_tensor(out=ntw[:], in0=ps2[:, 256:512], in1=tw[:], op=ALU.add), reads=[ps2, tw], writes=[ntw])
                tw = ntw
                yield
            c.op("pe", lambda e: e.matmul(pw[:, :128], lhsT=kbe[:], rhs=tw[:, 128:256], start=True, stop=True), reads=[kbe, tw], writes=[pw])
            yield
            nw = r_nw()
            c.op("act", lambda e: e.activation(out=nw[:], in_=pw[:, :128], func=AF.Copy, scale=-1.0), reads=[pw], writes=[nw])
            yield
            S_f, S_b = Sf[h], Sb[h]
            c.op("pe", lambda e: e.matmul(psv[:, 128:256], lhsT=tw[:, 128:256], rhs=vb[:], start=True, stop=False), reads=[tw, vb], writes=[psv])
            c.op("pe", lambda e: e.matmul(psv[:, 128:256], lhsT=nw[:], rhs=S_f[:], start=False, stop=True), reads=[nw, S_f], writes=[psv])
            yield
            vn = r_vn()
            vnb = r_vnb()
            c.op("act", lambda e: e.copy(out=vn[:], in_=psv[:, 128:256]), reads=[psv], writes=[vn])
            c.op("act", lambda e: e.copy(out=vnb[:], in_=vn[:]), reads=[vn], writes=[vnb])
            yield
            c.op("pe", lambda e: e.matmul(pso[:, 256:384], lhsT=S_b[:], rhs=qd[:], start=True, stop=False), reads=[S_b, qd], writes=[pso])
            c.op("pe", lambda e: e.matmul(pso[:, 256:384], lhsT=vnb[:], rhs=at[:], start=False, stop=True), reads=[vnb, at], writes=[pso])
            c.op("pe", lambda e: e.matmul(pss[:, 384:512], lhsT=kd[:], rhs=vn[:], start=True, stop=True), reads=[kd, vn], writes=[pss])
            yield
            oa = OACC[h]
            if d == 0:
                c.op("act", lambda e: e.copy(out=oa[:, t0:t0 + 128], in_=pso[:, 256:384]), reads=[pso], writes=[oa])
            else:
                c.op("dve", lambda e: e.tensor_tensor(out=oa[:, t0:t0 + 128], in0=pso[:, 256:384], in1=oa[:, t0:t0 + 128], op=ALU.add), reads=[pso, oa], writes=[oa])
            c.op("dve", lambda e: e.scalar_tensor_tensor(out=S_f[:], in0=S_f[:], scalar=EGL[:, tt, col:col + 1], in1=pss[:, 384:512], op0=ALU.mult, op1=ALU.add), reads=[S_f, EGL, pss], writes=[S_f])
            yield
            c.op("act", lambda e: e.copy(out=S_b[:], in_=S_f[:]), reads=[S_f], writes=[S_b])
            if last_seq is not None:
                so = r_so()
                c.op("pool", lambda e: e.tensor_copy(out=so[:], in_=S_f[:]), reads=[S_f], writes=[so])
                c.dma("sp", OSD[last_seq, l, d, h], so[:], reads=[so])

        def init_state(s, h, d):
            if s is None:
                c.dma("sp", Sf[h][:], I["sdn"][l, d, h], writes=[Sf[h]])
                c.op("act", lambda e: e.copy(out=Sb[h][:], in_=Sf[h][:]), reads=[Sf[h]], writes=[Sb[h]])
            else:
                c.op("pool", lambda e: e.memset(Sf[h][:], 0.0), writes=[Sf[h]])
                c.op("pool", lambda e: e.memset(Sb[h][:], 0.0), writes=[Sb[h]])

        def inst(tt, h, d, s, first, last):
            if first:
                init_state(s, h, d)
            yield from process(tt, h, d, s if (s is not None and last) else None)

        queue = []
        for d in range(2):
            seqs = [(s, [2 * s, 2 * s + 1]) for s in range(4)] + [(None, list(range(8, 16)))]
            for s, tiles in seqs:
                order = tiles if d == 0 else tiles[::-1]
                for i, tt in enumerate(order):
                    for h in range(8):
                        queue.append(inst(tt, h, d, s, i == 0, i == len(order) - 1))
        active = []
        qi = 0
        cyc = 0
        STAG = 5
        while qi < len(queue) or active:
            if qi < len(queue) and len(active) < NI and (qi >= NI or cyc >= qi * STAG):
                active.append(queue[qi])
                qi += 1
            alive = []
            for g in active:
                try:
                    next(g)
                    alive.append(g)
                except StopIteration:
                    pass
            active = alive
            cyc += 1
        c.end_stage()
        osq = c.sb([128, 512], BF16, "dosq")
        rr = c.sb([128, 512], F32, "drr")
        sdz = [c.sb([128, NT], BF16, "sdz") for _ in range(2)]
        orow = [c.sb([128, NT], BF16, "dorow") for _ in range(2)]
        for h in range(8):
            oa = OACC[h]
            z_ = sdz[h % 2]
            orw = orow[h % 2]
            c.dma("sp", z_[:], SDZ[h], writes=[z_])
            for (u0, nu) in TILES:
                c.op("act", lambda e: e.activation(out=osq[:, :nu], in_=oa[:, u0:u0 + nu], func=AF.Square), reads=[oa], writes=[osq])
                p2 = self.pb()
                c.op("pe", lambda e: e.matmul(p2[:, :nu], lhsT=onesB[:], rhs=osq[:, :nu], start=True, stop=True), reads=[onesB, osq], writes=[p2])
                c.op("act", lambda e: e.activation(out=rr[:, :nu], in_=p2[:, :nu], func=AF.Ln, scale=1.0 / 128, bias=eps[:, 0:1]), reads=[p2, eps], writes=[rr])
                c.op("act", lambda e: e.activation(out=rr[:, :nu], in_=rr[:, :nu], func=AF.Exp, scale=-0.5), reads=[rr], writes=[rr])
                c.op("dve", lambda e: e.scalar_tensor_tensor(out=rr[:, :nu], in0=oa[:, u0:u0 + nu], scalar=dnn[:, 0:1], in1=rr[:, :nu], op0=ALU.mult, op1=ALU.mult), reads=[oa, dnn, rr], writes=[rr])
                c.op("dve", lambda e: e.tensor_tensor(out=orw[:, u0:u0 + nu], in0=rr[:, :nu], in1=z_[:, u0:u0 + nu], op=ALU.mult), reads=[rr, z_], writes=[orw])
            c.dma("sp", MO[2][h], orw[:], reads=[orw])
        c.end_stage()


PI = math.pi


class KB5(KB4):
    def hy_tables(self, l, L):
        c = self.c
        I = self.I
        nch = L // 128
        TAB = self.scr("HTAB%d" % L, [2, 2 * nch + 1, 128, 1024], BF16)
        c.begin_stage()
        onesF, m0 = self.onesF, None
        m0 = c.sb([128, 2], F32, "m0")
        c.dma("sp", m0[:], I["m0"], writes=[m0])
        fT = c.sb([33, L], F32, "fT")
        c.dma("sp", fT[:], I["featsT%d" % L], writes=[fT])
        w1 = c.sb([33, 64], F32, "w1")
        w2 = c.sb([64, 64], F32, "w2")
        w3 = c.sb([64, 4096], F32, "w3")
        c.dma("sp", w1[:], I["hy_w1"][l], writes=[w1])
        c.dma("sp", w2[:], I["hy_w2"][l], writes=[w2])
        c.dma("sp", w3[:], I["hy_w3"][l], writes=[w3])
        vec = c.sb([64, 4], F32, "hvec")
        for i, nm in enumerate(("hy_b1", "hy_freq1", "hy_b2", "hy_freq2")):
            c.dma("sp", vec[:, i:i + 1], I[nm][l].rearrange("(p o) -> p o", o=1), writes=[vec])
        hid1 = c.sb([64, L], F32, "hid1")
        hid2 = c.sb([64, L], F32, "hid2")
        msk = c.sb([64, 512], F32, "msk")

        def sin_layer(dst, wT, K, src, bi, fi):
            for t0 in range(0, L, 512):
                n = min(512, L - t0)
                ps = self.pb()
                c.op("pe", lambda e: e.matmul(ps[:64, :n], lhsT=wT[:K, :], rhs=src[:K, t0:t0 + n], start=True, stop=True), reads=[wT, src], writes=[ps])
                d = dst
                c.op("dve", lambda e: e.tensor_scalar(out=d[:, t0:t0 + n], in0=ps[:64, :n], scalar1=vec[:, bi:bi + 1], scalar2=vec[:, fi:fi + 1], op0=ALU.add, op1=ALU.mult), reads=[ps, vec], writes=[d])
                for _ in range(2):
                    c.op("dve", lambda e: e.tensor_scalar(out=msk[:, :n], in0=d[:, t0:t0 + n], scalar1=PI, scalar2=-2 * PI, op0=ALU.is_gt, op1=ALU.mult), reads=[d], writes=[msk])
                    c.op("dve", lambda e: e.tensor_tensor(out=d[:, t0:t0 + n], in0=d[:, t0:t0 + n], in1=msk[:, :n], op=ALU.add), reads=[d, msk], writes=[d])
                    c.op("dve", lambda e: e.tensor_scalar(out=msk[:, :n], in0=d[:, t0:t0 + n], scalar1=-PI, scalar2=2 * PI, op0=ALU.is_lt, op1=ALU.mult), reads=[d], writes=[msk])
                    c.op("dve", lambda e: e.tensor_tensor(out=d[:, t0:t0 + n], in0=d[:, t0:t0 + n], in1=msk[:, :n], op=ALU.add), reads=[d, msk], writes=[d])
                c.op("act", lambda e: e.activation(out=d[:, t0:t0 + n], in_=d[:, t0:t0 + n], func=AF.Sin), reads=[d], writes=[d])

        sin_layer(hid1, w1, 33, fT, 0, 1)
        sin_layer(hid2, w2, 64, hid1, 2, 3)
        win = c.sb([128, nch, 1024], F32, "win")
        c.dma("sp", win[:], I["win%d" % L].rearrange("(t p) c -> p t c", p=128), writes=[win])
        CA = c.sb([128, nch, L], BF16, "CA")
        MB = c.sb([128, nch, L], BF16, "MB")
        c.dma("pool", CA[:], I["CA%d" % L].rearrange("(t p) k -> p t k", p=128), writes=[CA])
        c.dma("pool", MB[:], I["MB%d" % L].rearrange("(t p) k -> p t k", p=128), writes=[MB])
        hw = [c.sb([128, nch, 512], F32, "hw") for _ in range(2)]
        abs_ = [c.sb([128, 512], F32, "hab") for _ in range(2)]
        rinv = c.sb([128, 512], F32, "hrinv")
        hs = c.sb([128, nch, 512], BF16, "hs")
        hd = c.sb([128, nch, 512], BF16, "hd")
        to = [c.sb([128, 512], BF16, "hto") for _ in range(3)]
        tf = [c.sb([128, 512], F32, "htf") for _ in range(2)]
        nto = [0]
        for order in range(2):
            for half in range(2):
                for d in range(2):
                    col0 = d * 2048 + order * 1024 + half * 512
                    H = hw[d]
                    pn = self.pacc()

                    def gen_h(tt, col0=col0):
                        ps = self.pb()
                        c.op("pe", lambda e: e.matmul(ps[:, :512], lhsT=hid2[:, tt * 128:(tt + 1) * 128], rhs=w3[:, col0:col0 + 512], start=True, stop=True), reads=[hid2, w3], writes=[ps])
                        return ps
                    nxt = gen_h(0)
                    for tt in range(nch):
                        ps = nxt
                        nxt = gen_h(tt + 1) if tt + 1 < nch else None
                        a_ = abs_[tt % 2]
                        c.op("dve", lambda e: e.tensor_tensor(out=H[:, tt, :], in0=ps[:, :512], in1=win[:, tt, half * 512:(half + 1) * 512], op=ALU.mult), reads=[ps, win], writes=[H])
                        c.op("act", lambda e: e.activation(out=a_[:], in_=H[:, tt, :], func=AF.Abs), reads=[H], writes=[a_])
                        c.op("pe", lambda e: e.matmul(pn[:, :512], lhsT=onesF[:], rhs=a_[:], start=(tt == 0), stop=(tt == nch - 1)), reads=[onesF, a_], writes=[pn])
                    c.op("dve", lambda e: e.tensor_scalar(out=rinv[:], in0=pn[:, :512], scalar1=EPS, scalar2=None, op0=ALU.add), reads=[pn], writes=[rinv])
                    c.op("dve", lambda e: e.reciprocal(out=rinv[:], in_=rinv[:]), reads=[rinv], writes=[rinv])
                    for tt in range(nch):
                        c.op("dve", lambda e: e.tensor_tensor(out=H[:, tt, :], in0=H[:, tt, :], in1=rinv[:], op=ALU.mult), reads=[H, rinv], writes=[H])
                c.op("dve", lambda e: e.tensor_tensor(out=hs[:], in0=hw[0][:], in1=hw[1][:], op=ALU.add), reads=[hw[0], hw[1]], writes=[hs])
                c.op("pool", lambda e: e.tensor_tensor(out=hd[:], in0=hw[0][:], in1=hw[1][:], op=ALU.subtract), reads=[hw[0], hw[1]], writes=[hd])
                cs = slice(half * 512, (half + 1) * 512)
                for kc in range(nch):
                    pa = self.pb()
                    for tt in range(nch):
                        c.op("pe", lambda e: e.matmul(pa[:, :512], lhsT=CA[:, tt, kc * 128:(kc + 1) * 128], rhs=hs[:, tt, :], start=(tt == 0), stop=(tt == nch - 1)), reads=[CA, hs], writes=[pa])
                    pbd = self.pb()
                    for tt in range(nch):
                        c.op("pe", lambda e: e.matmul(pbd[:, :512], lhsT=MB[:, tt, kc * 128:(kc + 1) * 128], rhs=hd[:, tt, :], start=(tt == 0), stop=(tt == nch - 1)), reads=[MB, hd], writes=[pbd])
                    oa = to[nto[0] % 3]
                    nto[0] += 1
                    c.op("act", lambda e: e.copy(out=oa[:], in_=pa[:, :512]), reads=[pa], writes=[oa])
                    c.dma("sp", TAB[order, kc][:, cs], oa[:], reads=[oa])
                    ob = to[nto[0] % 3]
                    nto[0] += 1
                    if kc > 0:
                        c.op("act", lambda e: e.copy(out=ob[:], in_=pbd[:, :512]), reads=[pbd], writes=[ob])
                        c.dma("sp", TAB[order, nch + kc][:, cs], ob[:], reads=[ob])
                    else:
                        pbs = self.pb()
                        for tt in range(nch):
                            c.op("pe", lambda e: e.matmul(pbs[:, :512], lhsT=MB[:, tt, 0:128], rhs=hs[:, tt, :], start=(tt == 0), stop=(tt == nch - 1)), reads=[MB, hs], writes=[pbs])
                        c.op("dve", lambda e: e.tensor_scalar(out=ob[:], in0=pbd[:, :512], scalar1=m0[:, 0:1], scalar2=None, op0=ALU.mult), reads=[pbd, m0], writes=[ob])
                        c.dma("sp", TAB[order, nch][:, cs], ob[:], reads=[ob])
                        t1, t2 = tf[0], tf[1]
                        c.op("dve", lambda e: e.tensor_scalar(out=t1[:], in0=pa[:, :512], scalar1=m0[:, 0:1], scalar2=None, op0=ALU.mult), reads=[pa, m0], writes=[t1])
                        c.op("dve", lambda e: e.tensor_scalar(out=t2[:], in0=pbs[:, :512], scalar1=m0[:, 1:2], scalar2=None, op0=ALU.mult), reads=[pbs, m0], writes=[t2])
                        oc = to[nto[0] % 3]
                        nto[0] += 1
                        c.op("pool", lambda e: e.tensor_tensor(out=oc[:], in0=t1[:], in1=t2[:], op=ALU.add), reads=[t1, t2], writes=[oc])
                        c.dma("sp", TAB[order, 2 * nch][:, cs], oc[:], reads=[oc])
        c.end_stage()

    def hy_data(self, l, L, tokbase, B):
        c = self.c
        I = self.I
        nch = L // 128
        ncol = B * 1024
        TAB = self.S["HTAB%d" % L]
        HV, HX = self.S["HV"], self.S["HX"]
        MO = self.scr("MO", [3, 8, 128, NT], BF16)
        Z1 = self.scr("HZ1", [8, 128, NT], F32)
        idB = self.identB
        c.begin_stage()
        CA = c.sb([128, nch, L], BF16, "CA")
        MB = c.sb([128, nch, L], BF16, "MB")
        IA = c.sb([128, nch, L], BF16, "IA")
        IB = c.sb([128, nch, L], BF16, "IB")
        for t, nm in ((CA, "CA"), (MB, "MB"), (IA, "IA"), (IB, "IB")):
            c.dma("pool", t[:], I["%s%d" % (nm, L)].rearrange("(t p) k -> p t k", p=128), writes=[t])
        AH = c.sb([128, nch, 1024], BF16, "AH")
        HB1 = c.sb([128, nch, 1024], BF16, "HB1")
        HA2 = c.sb([128, 1024], BF16, "HA2")
        hb = c.sb([128, 16], F32, "hbias")
        self.load_cols(hb, hb[:], I["hy_bias"][l].rearrange("o (n p) -> (o n) p", p=128), 16)
        z = c.sb([128, nch, ncol], BF16, "z")
        Y = c.sb([128, 2 * nch, ncol], BF16, "Y")
        ntok = B * L
        rowv = [c.sb([128, ntok], BF16, "hrv") for _ in range(2)]
        rowx = [c.sb([128, ntok], F32, "hrx") for _ in range(2)]
        rowz = [c.sb([128, ntok], F32, "hrz") for _ in range(2)]
        rowo = [c.sb([128, ntok], BF16, "hro") for _ in range(2)]
        tm = [c.sb([128, 512], F32, "htm") for _ in range(8)]
        tmi = [0]

        def to_tokmajor(src_rows_fn):
            for ch in range(8):
                row = src_rows_fn(ch)
                for b in range(B):
                    for tq in range(0, nch, 4):
                        ps = self.pb()
                        pb16 = ps[:].bitcast(BF16)
                        nq = min(4, nch - tq)
                        for j in range(nq):
                            tt = tq + j
                            t0 = b * L + tt * 128
                            c.op("pe", lambda e: e.transpose(out=pb16[:, j * 128:(j + 1) * 128], in_=row[:, t0:t0 + 128], identity=idB[:]), reads=[row, idB], writes=[ps])
                        for j in range(nq):
                            tt = tq + j
                            eng = "dve" if j % 2 == 0 else "act"
                            dst = z[:, tt, b * 1024 + ch * 128:b * 1024 + (ch + 1) * 128]
                            if eng == "dve":
                                c.op("dve", lambda e: e.tensor_copy(out=dst, in_=pb16[:, j * 128:(j + 1) * 128]), reads=[ps], writes=[z])
                            else:
                                c.op("act", lambda e: e.copy(out=dst, in_=pb16[:, j * 128:(j + 1) * 128]), reads=[ps], writes=[z])

        for order in range(2):
            c.dma("sp", AH[:], TAB[order, 0:nch].rearrange("k p c -> p k c"), writes=[AH])
            c.dma("sp", HB1[:], TAB[order, nch:2 * nch].rearrange("k p c -> p k c"), writes=[HB1])
            c.dma("sp", HA2[:], TAB[order, 2 * nch], writes=[HA2])
            if order == 0:
                def rows_v(ch):
                    r = rowv[ch % 2]
                    c.dma("sp", r[:], HV[ch][:, tokbase:tokbase + ntok], writes=[r])
                    return r
                to_tokmajor(rows_v)
            else:
                def rows_z(ch):
                    rz = rowz[ch % 2]
                    r = rowv[ch % 2]
                    c.dma("sp", rz[:], Z1[ch][:, tokbase:tokbase + ntok], writes=[rz])
                    c.op("act", lambda e: e.copy(out=r[:], in_=rz[:]), reads=[rz], writes=[r])
                    return r
                to_tokmajor(rows_z)
            for ct in range(ncol // 512):
                c0 = (ct * 512) % 1024
                for kc in range(nch):
                    pa = self.pb()
                    for tt in range(nch):
                        c.op("pe", lambda e: e.matmul(pa[:, :512], lhsT=CA[:, tt, kc * 128:(kc + 1) * 128], rhs=z[:, tt, ct * 512:(ct + 1) * 512], start=(tt == 0), stop=(tt == nch - 1)), reads=[CA, z], writes=[pa])
                    pq = self.pb()
                    for tt in range(nch):
                        c.op("pe", lambda e: e.matmul(pq[:, :512], lhsT=MB[:, tt, kc * 128:(kc + 1) * 128], rhs=z[:, tt, ct * 512:(ct + 1) * 512], start=(tt == 0), stop=(tt == nch - 1)), reads=[MB, z], writes=[pq])
                    tmi[0] += 1
                    t1, t2, t3, t4 = tm[(tmi[0] % 2) * 4:(tmi[0] % 2) * 4 + 4]
                    ah = AH[:, kc, c0:c0 + 512]
                    h1 = HB1[:, kc, c0:c0 + 512]
                    a2 = HA2[:, c0:c0 + 512] if kc == 0 else ah
                    c.op("dve", lambda e: e.tensor_tensor(out=t1[:], in0=pa[:, :512], in1=ah, op=ALU.mult), reads=[pa, AH], writes=[t1])
                    c.op("dve", lambda e: e.tensor_tensor(out=t2[:], in0=pq[:, :512], in1=h1, op=ALU.mult), reads=[pq, HB1], writes=[t2])
                    c.op("pool", lambda e: e.tensor_tensor(out=Y[:, kc, ct * 512:(ct + 1) * 512], in0=t1[:], in1=t2[:], op=ALU.subtract), reads=[t1, t2], writes=[Y])
                    c.op("dve", lambda e: e.tensor_tensor(out=t3[:], in0=pa[:, :512], in1=h1, op=ALU.mult), reads=[pa, HB1], writes=[t3])
                    c.op("dve", lambda e: e.tensor_tensor(out=t4[:], in0=pq[:, :512], in1=a2, op=ALU.mult), reads=[pq, HA2, AH], writes=[t4])
                    c.op("pool", lambda e: e.tensor_tensor(out=Y[:, nch + kc, ct * 512:(ct + 1) * 512], in0=t3[:], in1=t4[:], op=ALU.add), reads=[t3, t4], writes=[Y])
            for ch in range(8):
                rx = rowx[ch % 2]
                c.dma("sp", rx[:], HX[order * 8 + ch][:, tokbase:tokbase + ntok], writes=[rx])
                if order == 0:
                    rb = rowv[ch % 2]
                    c.dma("sp", rb[:], HV[ch][:, tokbase:tokbase + ntok], writes=[rb])
                    ro = rowz[ch % 2]
                else:
                    rb = rowz[ch % 2]
                    c.dma("sp", rb[:], Z1[ch][:, tokbase:tokbase + ntok], writes=[rb])
                    ro = rowo[ch % 2]
                for b in range(B):
                    for r0 in range(0, L, 512):
                        n = min(512, L - r0)
                        ps = self.pacc()
                        for kk in range(2 * nch):
                            M = IA if kk < nch else IB
                            c.op("pe", lambda e: e.matmul(ps[:, :n], lhsT=Y[:, kk, b * 1024 + ch * 128:b * 1024 + (ch + 1) * 128], rhs=M[:, kk % nch, r0:r0 + n], start=(kk == 0), stop=(kk == 2 * nch - 1)),
                                 reads=[Y, M], writes=[ps])
                        g0 = b * L + r0
                        tmi[0] += 1
                        t1 = tm[tmi[0] % 8]
                        c.op("dve", lambda e: e.scalar_tensor_tensor(out=t1[:, :n], in0=rb[:, g0:g0 + n], scalar=hb[:, order * 8 + ch:order * 8 + ch + 1], in1=ps[:, :n], op0=ALU.mult, op1=ALU.add), reads=[rb, hb, ps], writes=[t1])
                        c.op("dve", lambda e: e.tensor_tensor(out=ro[:, g0:g0 + n], in0=t1[:, :n], in1=rx[:, g0:g0 + n], op=ALU.mult), reads=[t1, rx], writes=[ro])
                if order == 0:
                    c.dma("sp", Z1[ch][:, tokbase:tokbase + ntok], ro[:], reads=[ro])
                else:
                    c.dma("sp", MO[1][ch][:, tokbase:tokbase + ntok], ro[:], reads=[ro])
            c.barrier()
        c.end_stage()

    def stage_hy(self, l):
        self.hy_tables(l, 256)
        self.hy_data(l, 256, 0, 4)
        self.hy_tables(l, 1024)
        self.hy_data(l, 1024, 1024, 1)

_CACHE = {}


def build_full():
    kb = KB5()
    c = kb.c
    c.begin_stage()
    kb.stage_input(own=False)
    c.mark('input')
    kb.stage_mod()
    c.end_stage()
    c.mark('mod')
    XA = kb.S["XT0"]
    XB = kb.scr("XT1", [16, 128, NT], F32)
    XC = kb.scr("XT2", [16, 128, NT], F32)
    cur = XA
    for l in range(2):
        kb.stage_norm(cur, l, 0)
        c.mark('norm1')
        kb.stage_proj(l)
        c.end_stage()
        c.mark('proj')
        kb.stage_ret(l)
        c.mark('ret')
        kb.stage_hy(l)
        c.mark('hy')
        kb.stage_dn(l)
        c.mark('dn')
        kb.stage_merge(l, cur, XB)
        c.mark('merge+wo')
        kb.stage_norm(XB, l, 1)
        c.mark('norm2')
        kb.stage_ffn_up(l)
        c.end_stage()
        c.mark('ffn_up')
        kb.stage_ffn_down(l, XB, XC)
        c.mark('ffn_down')
        cur, XB, XC = XC, cur, XB
    kb.stage_final(cur)
    c.mark('final')
    c.finish()
    return kb


def kernel(**inputs):
    z = {k: np.ascontiguousarray(np.asarray(v)) for k, v in inputs.items()}
    if "kb" not in _CACHE:
        _CACHE["kb"] = build_full()
    kb = _CACHE["kb"]
    in_maps = []
    for core in range(8):
        m = dict(kb.consts)
        for k in WSPEC:
            m[k] = z[k]
        xp = z["x_prompt"][4 * core:4 * core + 4].reshape(1024, 2048)
        xs = z["x_sample"][core // 2]
        m["xin"] = np.ascontiguousarray(np.concatenate([xp, xs], 0))
        m["sret"] = np.ascontiguousarray(z["state_ret"][core // 2])
        m["sdn"] = np.ascontiguousarray(z["state_dn"][core // 2])
        m["cvec"] = np.ascontiguousarray(np.stack([z["c_ctx"], z["c"][core // 2]]))
        in_maps.append(m)
    res = run_bass_kernel_spmd(kb.nc, in_maps, core_ids=list(range(8))).results
    y_prompt = np.concatenate([np.asarray(res[c]["y"])[:1024].reshape(4, 256, 2048) for c in range(8)], 0).astype(np.float32)
    y_sample = np.stack([np.asarray(res[2 * b]["y"])[1024:] for b in range(4)], 0).astype(np.float32)
    new_ret = np.concatenate([np.asarray(res[c]["osret"]) for c in range(8)], 0).astype(np.float32)
    new_dn = np.concatenate([np.asarray(res[c]["osdn"]) for c in range(8)], 0).astype(np.float32)
    return (y_prompt, y_sample, new_ret, new_dn)
```

```python
import numpy as np
from contextlib import ExitStack
import concourse.bass as bass
import concourse.mybir as mybir
from concourse.bass_utils import run_bass_kernel_spmd

F32 = mybir.dt.float32
BF16 = mybir.dt.bfloat16
I32 = mybir.dt.int32
AF = mybir.ActivationFunctionType
ALU = mybir.AluOpType
AX = mybir.AxisListType


class Buf:
    __slots__ = ("t", "name", "w", "r", "psum")

    def __init__(self, t, name, psum=False):
        self.t = t
        self.name = name
        self.psum = psum
        self.w = None
        self.r = {}

    def __getitem__(self, idx):
        return self.t[idx]


class Ctx:
    SEM_LIMIT = 30000
    NDMA = 24

    def __init__(self, nc):
        self.nc = nc
        self.es = ExitStack()
        self.eng = {"pe": nc.tensor, "act": nc.scalar, "dve": nc.vector, "pool": nc.gpsimd, "sp": nc.sync}
        self.cur = {}
        self.waited = {k: {} for k in self.eng}
        self.nsem = 0
        for k in self.eng:
            self._new_sem(k)
        self.dpool = {}
        for q in ("sp", "pool", "act"):
            self.dpool[q] = [[self._alloc_sem("d%s%d" % (q, i)), 0] for i in range(self.NDMA)]
        self.dnext = {q: 0 for q in self.dpool}
        self.stage_es = None
        self.uid = 0
        self.tot = {}
        self.marks = []

    def mark(self, name):
        self.marks.append((name, dict(self.tot)))

    def _alloc_sem(self, name):
        self.nsem += 1
        s = self.es.enter_context(self.nc.semaphore("%s_%d" % (name, self.nsem)))
        if not hasattr(self, "allsems"):
            self.allsems = []
        self.allsems.append(s)
        return s

    def _new_sem(self, k):
        self.cur[k] = [self._alloc_sem("e" + k), 0]

    def begin_stage(self):
        if not hasattr(self, "stack"):
            self.stack = []
        self.stack.append(ExitStack())
        self.stage_es = self.stack[-1]

    def end_stage(self):
        self.barrier()
        self.stack.pop().close()
        self.stage_es = self.stack[-1] if self.stack else None

    def sb(self, shape, dt, name="t", persist=False):
        self.uid += 1
        nm = "%s_%d" % (name, self.uid)
        es = self.es if persist else self.stage_es
        t = es.enter_context(self.nc.sbuf_tensor(nm, list(shape), dt))
        return Buf(t, nm)

    def ps(self, shape, dt, name="p"):
        self.uid += 1
        nm = "%s_%d" % (name, self.uid)
        t = self.es.enter_context(self.nc.psum_tensor(nm, list(shape), dt))
        return Buf(t, nm, psum=True)

    def _wait(self, k, tok):
        if tok is None:
            return
        sem, val = tok
        if k == "pe" and sem is self.cur["pe"][0]:
            return
        w = self.waited[k]
        key = id(sem)
        if w.get(key, (None, 0))[1] >= val:
            return
        w[key] = (sem, val)
        self.eng[k].wait_ge(sem, val)

    def _deps(self, k, reads, writes):
        for b in reads:
            self._wait(k, b.w)
            if b.psum:
                for tok in list(b.r.values()):
                    self._wait(k, tok)
        for b in writes:
            self._wait(k, b.w)
            for tok in list(b.r.values()):
                self._wait(k, tok)

    def _commit(self, tok, reads, writes):
        for b in reads:
            b.r[id(tok[0])] = tok
        for b in writes:
            b.w = tok
            b.r = {}

    def op(self, k, fn, reads=(), writes=()):
        self._deps(k, reads, writes)
        c = self.cur[k]
        if c[1] >= self.SEM_LIMIT:
            self._new_sem(k)
            c = self.cur[k]
        c[1] += 1
        self.tot[k] = self.tot.get(k, 0) + 1
        fn(self.eng[k]).then_inc(c[0], 1)
        tok = (c[0], c[1])
        self._commit(tok, reads, writes)
        return tok

    def dma(self, q, out, in_, reads=(), writes=(), **kw):
        pool = self.dpool[q]
        i = self.dnext[q]
        self.dnext[q] = (i + 1) % len(pool)
        slot = pool[i]
        if slot[1] > 0:
            self._wait(q, (slot[0], slot[1]))
        if slot[1] >= self.SEM_LIMIT:
            slot[0] = self._alloc_sem("d" + q)
            slot[1] = 0
        self._deps(q, reads, writes)
        slot[1] += 16
        self.eng[q].dma_start(out=out, in_=in_, **kw).then_inc(slot[0], 16)
        tok = (slot[0], slot[1])
        self._commit(tok, reads, writes)
        return tok

    def barrier(self):
        toks = [(c[0], c[1]) for c in self.cur.values() if c[1] > 0]
        for q in self.dpool:
            for slot in self.dpool[q]:
                if slot[1] > 0:
                    toks.append((slot[0], slot[1]))
        for k in self.eng:
            for tok in toks:
                self._wait(k, tok)

    def finish(self):
        self.barrier()
        self.es.close()

import math
import numpy as np


def make_consts():
    f = np.float32
    C = {}
    i = np.arange(128)
    C["ident"] = np.eye(128, dtype=f)
    C["ones"] = np.ones((128, 128), f)
    C["UF"] = (i[:, None] <= i[None, :]).astype(f)
    C["UB"] = (i[:, None] >= i[None, :]).astype(f)
    C["SU"] = (i[:, None] < i[None, :]).astype(f)
    C["SL"] = (i[:, None] > i[None, :]).astype(f)
    L = 1024
    pos = np.arange(L, dtype=np.float64)
    pos_r = np.floor(pos / 64)
    pos_c = pos % 64
    inv = 10000.0 ** (-np.arange(16, dtype=np.float64) / 16)
    cosT = np.zeros((128, L))
    sinT = np.zeros((128, L))
    for p in range(128):
        q = p % 64
        half = q // 32
        x2 = (q % 32) // 16
        fi = q % 16
        ang = (pos_r if half == 0 else pos_c) * inv[fi]
        cosT[p] = np.cos(ang)
        sinT[p] = np.sin(ang) * (1.0 if x2 else -1.0)
    C["ropeC"] = cosT.astype(f)
    C["ropeS"] = sinT.astype(f)
    for nm, LL in (("S", 1024), ("P", 256)):
        u = np.arange(2 * LL - 128)[None, :]
        p = np.arange(128)[:, None]
        d = u - p - (LL - 128)
        C["dpos" + nm] = np.maximum(d, 0).astype(f)
        C["dneg" + nm] = np.maximum(-d, 0).astype(f)
        C["dz" + nm] = (d == 0).astype(f)
    j = np.arange(128)[:, None] + 128 * np.arange(2)[None, :]
    C["expfb"] = np.concatenate([255 - j, j], axis=1).astype(f)
    ii = np.arange(1024)[None, :].repeat(128, 0)
    C["idx1"] = (ii + 1).astype(f)
    C["idx2"] = (1024 - ii).astype(f)
    for LL in (256, 1024):
        t = np.linspace(0.0, 1.0, LL, dtype=np.float32)[:, None].astype(np.float64)
        wpos = 2.0 * math.pi * np.arange(LL, dtype=np.float64)[:, None] / LL
        fr = np.linspace(1e-4, 15, 16, dtype=np.float32)[None, :].astype(np.float64)
        feats = np.concatenate([t, np.cos(fr * wpos), -np.sin(fr * wpos)], axis=-1)
        C["featsT%d" % LL] = np.ascontiguousarray(feats.T).astype(f)
        deltas = np.abs(np.linspace(math.log(1e-2) / 0.3, math.log(1e-2) / 1.5, 1024, dtype=np.float32)).astype(np.float64)
        C["win%d" % LL] = np.exp(-t * deltas[None, :]).astype(f)
        N = 2 * LL
        tt = np.arange(LL, dtype=np.float64)[:, None]
        kk = np.arange(LL, dtype=np.float64)[None, :]
        ang = 2.0 * math.pi * tt * kk / N
        CA = np.cos(ang)
        MB = -np.sin(ang)
        MB[:, 0] = (-1.0) ** np.arange(LL)
        IA = (2.0 / N) * np.cos(ang.T)
        IA[0, :] = 1.0 / N
        IB = -(2.0 / N) * np.sin(ang.T)
        IB[0, :] = ((-1.0) ** np.arange(LL)) / N
        C["CA%d" % LL] = CA.astype(f)
        C["MB%d" % LL] = MB.astype(f)
        C["IA%d" % LL] = IA.astype(f)
        C["IB%d" % LL] = IB.astype(f)
    ii, jj = np.meshgrid(np.arange(128), np.arange(128), indexing="ij")
    ML = np.stack([(((ii >> (s + 1)) == (jj >> (s + 1))) & ((ii >> s) != (jj >> s)) & (ii > jj)).astype(f) for s in range(7)])
    MU = np.ascontiguousarray(ML.transpose(0, 2, 1))
    C["MLU"] = np.ascontiguousarray(np.concatenate([ML, MU], axis=2).transpose(1, 0, 2))
    C["MUL"] = np.ascontiguousarray(np.concatenate([MU, ML], axis=2).transpose(1, 0, 2))
    m0 = np.ones((128, 2), f)
    m0[0, 0] = 0.0
    m0[:, 1] = 1.0 - m0[:, 0]
    C["m0"] = m0
    C["eps"] = np.full((128, 1), 1e-6, f)
    return C


import math

NT = 2048
TILES = [(0, 512), (512, 512), (1024, 512), (1536, 512)]
EPS = 1e-6

WSPEC = dict(
    w_ada=[2, 2048, 12288], b_ada=[2, 12288], norm1=[2, 2048], w_in=[2, 2048, 16416], ret_decay=[2, 2, 8],
    hy_short=[2, 3, 3072], hy_w1=[2, 33, 64], hy_b1=[2, 64], hy_freq1=[2, 64], hy_w2=[2, 64, 64], hy_b2=[2, 64],
    hy_freq2=[2, 64], hy_w3=[2, 64, 4096], hy_bias=[2, 2, 1024], dn_conv=[2, 3, 3072], dn_a_log=[2, 2, 8],
    dn_dt_bias=[2, 2, 8], dn_norm=[2, 128], p_ret=[2, 1024, 2048], p_hy=[2, 1024, 2048], p_dn=[2, 1024, 2048],
    w_o=[2, 2048, 2048], norm2=[2, 2048], w_up=[2, 2048, 11008], ffn_conv=[2, 3, 11008], w_down=[2, 5504, 2048],
    norm_f=[2048])


class KB:
    def __init__(self, stop_after=None, dbg=(), wspec=None, ext_in=()):
        self.stop_after = stop_after
        self.dbg = set(dbg)
        nc = self.nc = bass.Bass("TRN2", target_bir_lowering=False)
        self.c = Ctx(nc)
        self.I = {}
        self.consts = make_consts()
        for k, v in self.consts.items():
            self.I[k] = nc.dram_tensor(k, list(v.shape), F32, kind="ExternalInput").ap()
        self.ext_in = set(ext_in)
        for k, s in (wspec or WSPEC).items():
            self.I[k] = nc.dram_tensor(k, s, F32, kind="ExternalInput").ap()
        self.I["xin"] = nc.dram_tensor("xin", [NT, 2048], F32, kind="ExternalInput").ap()
        self.I["sret"] = nc.dram_tensor("sret", [2, 2, 8, 64, 128], F32, kind="ExternalInput").ap()
        self.I["sdn"] = nc.dram_tensor("sdn", [2, 2, 8, 128, 128], F32, kind="ExternalInput").ap()
        self.I["cvec"] = nc.dram_tensor("cvec", [2, 2048], F32, kind="ExternalInput").ap()
        self.O = {}
        self.O["y"] = nc.dram_tensor("y", [NT, 2048], F32, kind="ExternalOutput").ap()
        self.O["osret"] = nc.dram_tensor("osret", [4, 2, 2, 8, 64, 128], F32, kind="ExternalOutput").ap()
        self.O["osdn"] = nc.dram_tensor("osdn", [4, 2, 2, 8, 128, 128], F32, kind="ExternalOutput").ap()
        self.S = {}
        c = self.c
        self.PB = [c.ps([128, 512], F32, "pb") for _ in range(8)]
        self.pbi = 0
        self.identF = c.sb([128, 128], F32, "identF", persist=True)
        self.identB = c.sb([128, 128], BF16, "identB", persist=True)
        self.onesB = c.sb([128, 128], BF16, "onesB", persist=True)
        self.onesF = c.sb([128, 128], F32, "onesF", persist=True)
        self.epsT = c.sb([128, 1], F32, "epsT", persist=True)
        self.rows = c.sb([128, 128], F32, "rows", persist=True)
        c.dma("sp", self.identF[:], self.I["ident"], writes=[self.identF])
        c.dma("pool", self.identB[:], self.I["ident"], writes=[self.identB])
        c.dma("pool", self.onesB[:], self.I["ones"], writes=[self.onesB])
        c.dma("sp", self.onesF[:], self.I["ones"], writes=[self.onesF])
        c.dma("sp", self.epsT[:], self.I["eps"], writes=[self.epsT])
        self.HT = None
        self.modT = c.sb([128, 2, 96, 2], F32, "modT", persist=True)
        self.sca = c.sb([128, 2, 2, 16, 2], F32, "sca", persist=True)
        self.gbraw = c.sb([128, 16, 32], F32, "gbraw", persist=True)

    def scr(self, name, shape, dt):
        if name not in self.S:
            kind = "ExternalOutput" if name in self.dbg else ("ExternalInput" if name in self.ext_in else "Internal")
            self.S[name] = self.nc.dram_tensor(name, list(shape), dt, kind=kind).ap()
        return self.S[name]

    def pb(self):
        self.pbi = (self.pbi + 1) % 6
        return self.PB[self.pbi]

    def pacc(self):
        self.pai = (getattr(self, "pai", 0) + 1) % 2
        return self.PB[6 + self.pai]

    def load_cols(self, dstbuf, dst_ap, src_rows, n):
        c = self.c
        rows = self.rows
        c.dma("sp", rows[:n, :], src_rows, writes=[rows])
        ps = self.pb()
        idf = self.identF
        c.op("pe", lambda e: e.transpose(out=ps[:, :n], in_=rows[:n, :], identity=idf[:n, :n]), reads=[rows, idf], writes=[ps])
        c.op("dve", lambda e: e.tensor_copy(out=dst_ap, in_=ps[:, :n]), reads=[ps], writes=[dstbuf])

    def stage_input(self, own=True):
        c = self.c
        XT = self.scr("XT0", [16, 128, NT], F32)
        if own:
            c.begin_stage()
        xs = [c.sb([128, 2048], F32, "xs") for _ in range(2)]
        xo = [c.sb([128, 16, 128], F32, "xo") for _ in range(2)]
        idf = self.identF
        for tt in range(16):
            a = xs[tt % 2]
            o = xo[tt % 2]
            c.dma("sp", a[:], self.I["xin"][tt * 128:(tt + 1) * 128, :], writes=[a])
            for g in range(4):
                ps = self.pb()
                for j in range(4):
                    ch = g * 4 + j
                    c.op("pe", lambda e: e.transpose(out=ps[:, j * 128:(j + 1) * 128], in_=a[:, ch * 128:(ch + 1) * 128], identity=idf[:]),
                         reads=[a, idf], writes=[ps])
                eng = "dve" if g % 2 == 0 else "act"
                if eng == "dve":
                    c.op("dve", lambda e: e.tensor_copy(out=o[:, g * 4:(g + 1) * 4, :], in_=ps[:].rearrange("p (j t) -> p j t", j=4)), reads=[ps], writes=[o])
                else:
                    c.op("act", lambda e: e.copy(out=o[:, g * 4:(g + 1) * 4, :], in_=ps[:].rearrange("p (j t) -> p j t", j=4)), reads=[ps], writes=[o])
            c.dma("sp", XT[:, :, tt * 128:(tt + 1) * 128].rearrange("c p t -> p c t"), o[:], reads=[o])
        if own:
            c.end_stage()

    def stage_mod(self):
        c = self.c
        I = self.I
        c.begin_stage()
        scT = c.sb([128, 2, 16], F32, "scT")
        self.load_cols(scT, scT[:].rearrange("p b k -> p (b k)"), I["cvec"].rearrange("b (k p) -> (b k) p", p=128), 32)
        c.op("act", lambda e: e.activation(out=scT[:], in_=scT[:], func=AF.Silu), reads=[scT], writes=[scT])
        bada = c.sb([128, 96], F32, "bada")
        nw = c.sb([128, 16], F32, "nw")
        wa = [c.sb([128, 16, 512], F32, "wa") for _ in range(4)]
        mrow = [c.sb([2, 512], F32, "mrow") for _ in range(2)]
        idf = self.identF
        for l in range(2):
            self.load_cols(bada, bada[:], I["b_ada"][l].rearrange("(n p) -> n p", p=128), 96)
            ps = self.pacc()
            for blk in range(24):
                w = wa[blk % 4]
                c.dma(("sp", "act", "pool")[blk % 3], w[:], I["w_ada"][l][:, blk * 512:(blk + 1) * 512].rearrange("(k p) n -> p k n", p=128), writes=[w])
                pr = self.pb()
                for k in range(16):
                    c.op("pe", lambda e: e.matmul(pr[:2, :512], lhsT=scT[:, :, k], rhs=w[:, k, :], start=(k == 0), stop=(k == 15)), reads=[w, scT], writes=[pr])
                mr = mrow[blk % 2]
                c.op("act", lambda e: e.copy(out=mr[:], in_=pr[:2, :512]), reads=[pr], writes=[mr])
                for j in range(4):
                    ch = blk * 4 + j
                    c.op("pe", lambda e: e.transpose(out=ps[:, ch * 2:ch * 2 + 2], in_=mr[:2, j * 128:(j + 1) * 128], identity=idf[:2, :2]), reads=[mr, idf], writes=[ps])
            mt = self.modT
            for b in range(2):
                c.op("dve", lambda e: e.tensor_tensor(out=mt[:, l, :, b], in0=ps[:, 0:192].rearrange("p (c b) -> p c b", b=2)[:, :, b], in1=bada[:], op=ALU.add),
                     reads=[ps, bada], writes=[mt])
            for which, (nm, scb) in enumerate((("norm1", 16), ("norm2", 64))):
                self.load_cols(nw, nw[:], I[nm][l].rearrange("(n p) -> n p", p=128), 16)
                sc = self.sca
                for b in range(2):
                    c.op("dve", lambda e: e.scalar_tensor_tensor(out=sc[:, l, which, :, b], in0=mt[:, l, scb:scb + 16, b], scalar=1.0, in1=nw[:], op0=ALU.add, op1=ALU.mult),
                         reads=[mt, nw], writes=[sc])
        c.end_stage()

    def stage_norm(self, XT, l, which):
        c = self.c
        c.begin_stage()
        self.HT = c.sb([128, 16, NT], BF16, "HT")
        c.begin_stage()
        shb = 0 if which == 0 else 48
        xs = [c.sb([128, 16, 256], F32, "nx") for _ in range(2)]
        sq = c.sb([128, 16, 256], BF16, "nsq")
        tmps = [c.sb([128, 256], F32, "ntmp") for _ in range(6)]
        r = c.sb([128, 256], F32, "nr")
        HT, sc, mt, ones, eps = self.HT, self.sca, self.modT, self.onesB, self.epsT
        for ti in range(8):
            t0 = ti * 256
            b = 0 if t0 < 1024 else 1
            x = xs[ti % 2]
            c.dma("sp", x[:], XT[:, :, t0:t0 + 256].rearrange("c p t -> p c t"), writes=[x])
            c.op("act", lambda e: e.activation(out=sq[:], in_=x[:], func=AF.Square), reads=[x], writes=[sq])
            ps = self.pb()
            for ch in range(16):
                c.op("pe", lambda e: e.matmul(ps[:, :256], lhsT=ones[:], rhs=sq[:, ch, :], start=(ch == 0), stop=(ch == 15)), reads=[ones, sq], writes=[ps])
            c.op("act", lambda e: e.activation(out=r[:], in_=ps[:, :256], func=AF.Ln, scale=1.0 / 2048, bias=eps[:, 0:1]), reads=[ps, eps], writes=[r])
            c.op("act", lambda e: e.activation(out=r[:], in_=r[:], func=AF.Exp, scale=-0.5), reads=[r], writes=[r])
            for ch in range(16):
                tmp = tmps[ch % 6]
                c.op("dve", lambda e: e.tensor_tensor(out=tmp[:], in0=x[:, ch, :], in1=r[:], op=ALU.mult), reads=[x, r], writes=[tmp])
                c.op("act", lambda e: e.activation(out=HT[:, ch, t0:t0 + 256], in_=tmp[:], func=AF.Identity,
                                                   scale=sc[:, l, which, ch, b:b + 1], bias=mt[:, l, shb + ch, b:b + 1]), reads=[tmp, sc, mt], writes=[HT])
        c.end_stage()


class KB2(KB):
    def wbufs(self, KC, n=3, width=256):
        return [self.c.sb([128, KC, width], BF16, "wb") for _ in range(n)]

    def lin_fm(self, W, blocks, IN, KC, epi, tiles=TILES, wb=None, prep=None):
        c = self.c
        if wb is None:
            wb = self.wbufs(KC)
        for bi, (c0, ncol) in enumerate(blocks):
            self.wbi = getattr(self, "wbi", 0) + 1
            w = wb[self.wbi % len(wb)]
            c.dma("pool", w[:, :KC, :ncol], W[:, c0:c0 + ncol].rearrange("(k p) n -> p k n", p=128), writes=[w])
            aux = prep(w, c0, ncol) if prep else None
            for off in range(0, ncol, 128):
                n = min(128, ncol - off)
                for ti, (t0, nt) in enumerate(tiles):
                    ps = self.pb()
                    for k in range(KC):
                        c.op("pe", lambda e: e.matmul(ps[:n, :nt], lhsT=w[:, k, off:off + n], rhs=IN[:, k, t0:t0 + nt], start=(k == 0), stop=(k == KC - 1)),
                             reads=[w, IN], writes=[ps])
                    epi(c0 + off, n, ti, t0, nt, ps, (aux, off))

    def lin_tm(self, W, blocks, IN, KC, epi, ttiles, wb=None):
        c = self.c
        if wb is None:
            wb = self.wbufs(KC)
        for bi, (c0, ncol) in enumerate(blocks):
            self.wbi = getattr(self, "wbi", 0) + 1
            w = wb[self.wbi % len(wb)]
            c.dma("pool", w[:, :KC, :ncol], W[:, c0:c0 + ncol].rearrange("(k p) n -> p k n", p=128), writes=[w])
            for tt in ttiles:
                ps = self.pb()
                for k in range(KC):
                    c.op("pe", lambda e: e.matmul(ps[:, :ncol], lhsT=IN[:, k, tt * 128:(tt + 1) * 128], rhs=w[:, k, :ncol], start=(k == 0), stop=(k == KC - 1)),
                         reads=[w, IN], writes=[ps])
                epi(c0, ncol, tt, ps)

    @staticmethod
    def blocks(a, b, step=256):
        return [(x, min(step, b - x)) for x in range(a, b, step)]

    def conv_row(self, src, dst, wt, ci, ntap_chunks):
        c = self.c
        w0 = wt[:, 0 * ntap_chunks + ci:0 * ntap_chunks + ci + 1]
        w1 = wt[:, 1 * ntap_chunks + ci:1 * ntap_chunks + ci + 1]
        w2 = wt[:, 2 * ntap_chunks + ci:2 * ntap_chunks + ci + 1]
        c.op("act", lambda e: e.activation(out=dst[:], in_=src[:], func=AF.Copy, scale=w1), reads=[src, wt], writes=[dst])
        sp = src[:, 0:1024].rearrange("p (s t) -> p s t", t=256)
        dp = dst[:, 0:1024].rearrange("p (s t) -> p s t", t=256)
        c.op("dve", lambda e: e.scalar_tensor_tensor(out=dp[:, :, 1:256], in0=sp[:, :, 0:255], scalar=w0, in1=dp[:, :, 1:256], op0=ALU.mult, op1=ALU.add), reads=[src, wt, dst], writes=[dst])
        c.op("dve", lambda e: e.scalar_tensor_tensor(out=dp[:, :, 0:255], in0=sp[:, :, 1:256], scalar=w2, in1=dp[:, :, 0:255], op0=ALU.mult, op1=ALU.add), reads=[src, wt, dst], writes=[dst])
        c.op("dve", lambda e: e.scalar_tensor_tensor(out=dst[:, 1025:2048], in0=src[:, 1024:2047], scalar=w0, in1=dst[:, 1025:2048], op0=ALU.mult, op1=ALU.add), reads=[src, wt, dst], writes=[dst])
        c.op("dve", lambda e: e.scalar_tensor_tensor(out=dst[:, 1024:2047], in0=src[:, 1025:2048], scalar=w2, in1=dst[:, 1024:2047], op0=ALU.mult, op1=ALU.add), reads=[src, wt, dst], writes=[dst])

    def stage_proj(self, l):
        c = self.c
        I = self.I
        W = I["w_in"][l]
        HT = self.HT
        c.begin_stage()
        wb = self.wbufs(16, 3, 512)
        QK = self.scr("QK", [8, 128, NT], BF16)
        VTM = self.scr("VTM", [16, 128, 1024], BF16)
        KTM = self.scr("KTM", [8, 128, 512], BF16)
        SRG = self.scr("SRG", [8, 128, NT], BF16)
        HV = self.scr("HV", [8, 128, NT], BF16)
        HX = self.scr("HX", [16, 128, NT], F32)
        DQK = self.scr("DQK", [16, 128, NT], BF16)
        DVT = self.scr("DVT", [8, 128, NT], BF16)
        SDZ = self.scr("SDZ", [8, 128, NT], BF16)
        GATE = self.scr("GATE", [48, 128, NT], BF16)
        rowb = [c.sb([128, NT], BF16, "rowb") for _ in range(2)]
        rowf = [c.sb([128, NT], F32, "rowf") for _ in range(2)]
        rowg = [c.sb([128, NT], F32, "rowg") for _ in range(2)]
        cnt = [0]

        ropeC = c.sb([128, 1024], F32, "ropeC")
        ropeS = c.sb([128, 1024], F32, "ropeS")
        c.dma("sp", ropeC[:], I["ropeC"], writes=[ropeC])
        c.dma("sp", ropeS[:], I["ropeS"], writes=[ropeS])
        wperm = [c.sb([128, 16, 256], BF16, "wperm") for _ in range(2)]
        t1 = c.sb([128, 512], F32, "t1")
        t2 = c.sb([128, 512], F32, "t2")
        pc = [0]

        def prep_qk(w, c0, ncol):
            wp = wperm[pc[0] % 2]
            pc[0] += 1
            src = w[:, :, 0:256].rearrange("p k (a two s) -> p k a two s", two=2, s=16)
            dst = wp[:].rearrange("p k (a two s) -> p k a two s", two=2, s=16)
            c.op("dve", lambda e: e.tensor_copy(out=dst[:, :, :, 0, :], in_=src[:, :, :, 1, :]), reads=[w], writes=[wp])
            c.op("act", lambda e: e.copy(out=dst[:, :, :, 1, :], in_=src[:, :, :, 0, :]), reads=[w], writes=[wp])
            return wp

        def epi_qk(col0, n, ti, t0, nt, ps, auxoff):
            wp, off = auxoff
            ci = col0 // 128
            row = rowb[ci % 2]
            scale = 1.0 if ci < 4 else 0.125
            if ti < 2:
                c.op("act", lambda e: e.activation(out=row[:, t0:t0 + nt], in_=ps[:, :nt], func=AF.Copy, scale=scale), reads=[ps], writes=[row])
            else:
                ps2 = self.pb()
                for k in range(16):
                    c.op("pe", lambda e: e.matmul(ps2[:, :nt], lhsT=wp[:, k, off:off + 128], rhs=HT[:, k, t0:t0 + nt], start=(k == 0), stop=(k == 15)), reads=[wp, HT], writes=[ps2])
                s0 = t0 - 1024
                c.op("dve", lambda e: e.tensor_tensor(out=t1[:, :nt], in0=ps[:, :nt], in1=ropeC[:, s0:s0 + nt], op=ALU.mult), reads=[ps, ropeC], writes=[t1])
                c.op("dve", lambda e: e.tensor_tensor(out=t2[:, :nt], in0=ps2[:, :nt], in1=ropeS[:, s0:s0 + nt], op=ALU.mult), reads=[ps2, ropeS], writes=[t2])
                c.op("dve", lambda e: e.tensor_tensor(out=t1[:, :nt], in0=t1[:, :nt], in1=t2[:, :nt], op=ALU.add), reads=[t1, t2], writes=[t1])
                c.op("act", lambda e: e.activation(out=row[:, t0:t0 + nt], in_=t1[:, :nt], func=AF.Copy, scale=scale), reads=[t1], writes=[row])
            if ti == 3:
                c.dma("sp", QK[ci], row[:], reads=[row])

        self.lin_fm(W, self.blocks(0, 1024), HT, 16, epi_qk, wb=wb, prep=prep_qk)

        stv = [c.sb([128, 512], BF16, "stv") for _ in range(4)]

        def epi_v(col0, ncol, tt, ps):
            s = stv[cnt[0] % 4]
            cnt[0] += 1
            if cnt[0] % 2:
                c.op("dve", lambda e: e.tensor_copy(out=s[:, :ncol], in_=ps[:, :ncol]), reads=[ps], writes=[s])
            else:
                c.op("act", lambda e: e.copy(out=s[:, :ncol], in_=ps[:, :ncol]), reads=[ps], writes=[s])
            c.dma("sp", VTM[tt][:, col0 - 1024:col0 - 1024 + ncol], s[:, :ncol], reads=[s])

        def epi_ktm(col0, ncol, tt, ps):
            s = stv[cnt[0] % 4]
            cnt[0] += 1
            c.op("act", lambda e: e.activation(out=s[:, :ncol], in_=ps[:, :ncol], func=AF.Copy, scale=0.125), reads=[ps], writes=[s])
            c.dma("sp", KTM[tt][:, col0 - 512:col0 - 512 + ncol], s[:, :ncol], reads=[s])

        self.lin_tm(W, self.blocks(1024, 2048, 512), HT, 16, epi_v, range(16), wb=wb)
        self.lin_tm(W, self.blocks(512, 1024, 512), HT, 16, epi_ktm, range(8), wb=wb)

        def mk_epi_act(base, dst, func):
            def epi(col0, n, ti, t0, nt, ps, aux):
                ci = (col0 - base) // 128
                row = rowb[ci % 2]
                c.op("act", lambda e: e.activation(out=row[:, t0:t0 + nt], in_=ps[:, :nt], func=func), reads=[ps], writes=[row])
                if ti == 3:
                    c.dma("sp", dst[ci], row[:], reads=[row])
            return epi

        self.lin_fm(W, self.blocks(2048, 3072, 512), HT, 16, mk_epi_act(2048, SRG, AF.Silu), wb=wb)
        self.lin_fm(W, self.blocks(9216, 10240, 512), HT, 16, mk_epi_act(9216, SDZ, AF.Silu), wb=wb)
        self.lin_fm(W, self.blocks(10272, 16416, 512), HT, 16, mk_epi_act(10272, GATE, AF.Sigmoid), wb=wb)

        hyw = c.sb([128, 72], F32, "hyw")
        self.load_cols(hyw, hyw[:], I["hy_short"][l].rearrange("k (n p) -> (k n) p", p=128), 72)
        dnw = c.sb([128, 72], F32, "dnw")
        self.load_cols(dnw, dnw[:], I["dn_conv"][l].rearrange("k (n p) -> (k n) p", p=128), 72)

        def epi_hy(col0, n, ti, t0, nt, ps, aux):
            ci = (col0 - 3072) // 128
            raw = rowf[ci % 2]
            c.op("act", lambda e: e.copy(out=raw[:, t0:t0 + nt], in_=ps[:, :nt]), reads=[ps], writes=[raw])
            if ti == 3:
                cv = rowg[ci % 2]
                self.conv_row(raw, cv, hyw, ci, 24)
                if ci < 8:
                    row = rowb[ci % 2]
                    c.op("act", lambda e: e.copy(out=row[:], in_=cv[:]), reads=[cv], writes=[row])
                    c.dma("sp", HV[ci], row[:], reads=[row])
                else:
                    c.dma("sp", HX[ci - 8], cv[:], reads=[cv])

        self.lin_fm(W, self.blocks(3072, 6144, 512), HT, 16, epi_hy, wb=wb)

        sqb = c.sb([128, NT], BF16, "sqb")
        rns = [c.sb([128, 512], F32, "rn") for _ in range(3)]
        ones, eps = self.onesB, self.epsT

        dn_pending = []

        def epi_dn(col0, n, ti, t0, nt, ps, aux):
            ci = (col0 - 6144) // 128
            raw = rowf[ci % 2]
            c.op("act", lambda e: e.copy(out=raw[:, t0:t0 + nt], in_=ps[:, :nt]), reads=[ps], writes=[raw])
            if ti == 3:
                while dn_pending:
                    dn_pending.pop(0)()
                dn_pending.append(lambda ci=ci, raw=raw: dn_post(ci, raw))

        def dn_post(ci, raw):
            cv = rowg[ci % 2]
            self.conv_row(raw, cv, dnw, ci, 24)
            c.op("act", lambda e: e.activation(out=cv[:], in_=cv[:], func=AF.Silu), reads=[cv], writes=[cv])
            row = rowb[ci % 2]
            if ci < 16:
                c.op("act", lambda e: e.activation(out=sqb[:], in_=cv[:], func=AF.Square), reads=[cv], writes=[sqb])
                for tj, (u0, nu) in enumerate(TILES):
                    p2 = self.pb()
                    c.op("pe", lambda e: e.matmul(p2[:, :nu], lhsT=ones[:], rhs=sqb[:, u0:u0 + nu], start=True, stop=True), reads=[ones, sqb], writes=[p2])
                    rn = rns[tj % 3]
                    c.op("act", lambda e: e.activation(out=rn[:, :nu], in_=p2[:, :nu], func=AF.Ln, bias=eps[:, 0:1]), reads=[p2, eps], writes=[rn])
                    c.op("act", lambda e: e.activation(out=rn[:, :nu], in_=rn[:, :nu], func=AF.Exp, scale=-0.5), reads=[rn], writes=[rn])
                    sc_ = (128 ** -0.5) if ci < 8 else 1.0
                    c.op("dve", lambda e: e.scalar_tensor_tensor(out=row[:, u0:u0 + nu], in0=cv[:, u0:u0 + nu], scalar=sc_, in1=rn[:, :nu], op0=ALU.mult, op1=ALU.mult),
                         reads=[cv, rn], writes=[row])
                c.dma("sp", DQK[ci], row[:], reads=[row])
            else:
                c.op("act", lambda e: e.copy(out=row[:], in_=cv[:]), reads=[cv], writes=[row])
                c.dma("sp", DVT[ci - 16], row[:], reads=[row])

        self.lin_fm(W, self.blocks(6144, 9216, 512), HT, 16, epi_dn, wb=wb)
        while dn_pending:
            dn_pending.pop(0)()

        gbraw = self.gbraw

        def epi_gb(col0, ncol, tt, ps):
            c.op("dve", lambda e: e.tensor_copy(out=gbraw[:, tt, :], in_=ps[:, :32]), reads=[ps], writes=[gbraw])

        self.lin_tm(W, [(10240, 32)], HT, 16, epi_gb, range(16), wb=wb)
        c.end_stage()

    def stage_merge(self, l, XTin, XTout):
        c = self.c
        I = self.I
        c.begin_stage()
        MO = self.scr("MO", [3, 8, 128, NT], BF16)
        GATE = self.S["GATE"]
        MIX = self.scr("MIX", [16, 128, NT], BF16)
        mo = [c.sb([128, 8, NT], BF16, "mo") for _ in range(3)]
        for b in range(3):
            c.dma("sp", mo[b][:], MO[b].rearrange("c p t -> p c t"), writes=[mo[b]])
        wp = [c.sb([128, 8, 512], BF16, "wp") for _ in range(3)]
        gr = [c.sb([128, NT], BF16, "gr") for _ in range(4)]
        acc = [c.sb([128, NT], F32, "acc") for _ in range(4)]
        mixb = [c.sb([128, NT], BF16, "mixb") for _ in range(2)]
        tmp = c.sb([128, 512], F32, "mtmp")
        PW = [I["p_ret"][l], I["p_hy"][l], I["p_dn"][l]]
        n = 0
        nw_ = 0
        for jg in range(4):
            for b in range(3):
                w = wp[nw_ % 3]
                nw_ += 1
                c.dma("pool", w[:], PW[b][:, jg * 512:(jg + 1) * 512].rearrange("(k p) n -> p k n", p=128), writes=[w])
                for jj in range(4):
                    j = jg * 4 + jj
                    a = acc[jj]
                    g = gr[n % 4]
                    n += 1
                    c.dma("sp", g[:], GATE[b * 16 + j], writes=[g])
                    for ti, (t0, nt) in enumerate(TILES):
                        ps = self.pb()
                        for k_ in range(8):
                            c.op("pe", lambda e: e.matmul(ps[:, :nt], lhsT=w[:, k_, jj * 128:(jj + 1) * 128], rhs=mo[b][:, k_, t0:t0 + nt], start=(k_ == 0), stop=(k_ == 7)), reads=[w, mo[b]], writes=[ps])
                        if b == 0:
                            c.op("dve", lambda e: e.tensor_tensor(out=a[:, t0:t0 + nt], in0=ps[:, :nt], in1=g[:, t0:t0 + nt], op=ALU.mult), reads=[ps, g], writes=[a])
                        else:
                            c.op("dve", lambda e: e.tensor_tensor(out=tmp[:, :nt], in0=ps[:, :nt], in1=g[:, t0:t0 + nt], op=ALU.mult), reads=[ps, g], writes=[tmp])
                            c.op("dve", lambda e: e.tensor_tensor(out=a[:, t0:t0 + nt], in0=a[:, t0:t0 + nt], in1=tmp[:, :nt], op=ALU.add), reads=[a, tmp], writes=[a])
            for jj in range(4):
                j = jg * 4 + jj
                m = mixb[j % 2]
                a = acc[jj]
                c.op("act", lambda e: e.copy(out=m[:], in_=a[:]), reads=[a], writes=[m])
                c.dma("sp", MIX[j], m[:], reads=[m])
        c.end_stage()
        c.begin_stage()
        mix = c.sb([128, 16, NT], BF16, "mix")
        c.dma("sp", mix[:], MIX.rearrange("c p t -> p c t"), writes=[mix])
        self.resid_epilogue(I["w_o"][l], 2048, mix, 16, l, 32, XTin, XTout)
        c.end_stage()

    def resid_epilogue(self, W, ncols, IN, KC, l, gbase, XTin, XTout, tiles=TILES, wb=None):
        c = self.c
        mt = self.modT
        xr = [c.sb([128, NT], F32, "xr") for _ in range(2)]
        lo = tiles[0][0]
        hi = tiles[-1][0] + tiles[-1][1]

        def epi(col0, n, ti, t0, nt, ps, aux):
            j = col0 // 128
            x = xr[j % 2]
            if ti == 0:
                c.dma("sp", x[:, lo:hi], XTin[j][:, lo:hi], writes=[x])
            b = 0 if t0 < 1024 else 1
            c.op("dve", lambda e: e.scalar_tensor_tensor(out=x[:, t0:t0 + nt], in0=ps[:, :nt], scalar=mt[:, l, gbase + j, b:b + 1], in1=x[:, t0:t0 + nt], op0=ALU.mult, op1=ALU.add),
                 reads=[ps, mt, x], writes=[x])
            if ti == len(tiles) - 1:
                c.dma("sp", XTout[j][:, lo:hi], x[:, lo:hi], reads=[x])

        if wb is None:
            wb = self.wbufs(KC, 3, 512)
        self.lin_fm(W, self.blocks(0, ncols, 512), IN, KC, epi, tiles=tiles, wb=wb)

    def stage_ffn_up(self, l):
        c = self.c
        I = self.I
        ACTT = self.scr("ACTT", [43, 128, NT], BF16)
        c.begin_stage()
        fw = c.sb([128, 3 * 86], F32, "fw")
        for k in range(3):
            self.load_cols(fw, fw[:, k * 86:(k + 1) * 86], I["ffn_conv"][l][k].rearrange("(n p) -> n p", p=128), 86)
        rowf = [c.sb([128, NT], F32, "frow") for _ in range(3)]
        ga = [c.sb([128, NT], F32, "ga") for _ in range(2)]
        gb = [c.sb([128, NT], F32, "gb") for _ in range(2)]
        ab = [c.sb([128, NT], BF16, "ab") for _ in range(2)]
        wb = self.wbufs(16, 4, 512)
        HT = self.HT
        W = I["w_up"][l]
        cnt = [0]
        nb = 0
        for blk in range(11):
            c0 = blk * 512
            ncol = min(512, 5504 - c0)
            wa_, wg_ = wb[nb % 4], wb[(nb + 1) % 4]
            nb += 2
            c.dma("pool", wa_[:, :, :ncol], W[:, c0:c0 + ncol].rearrange("(k p) n -> p k n", p=128), writes=[wa_])
            c.dma("pool", wg_[:, :, :ncol], W[:, 5504 + c0:5504 + c0 + ncol].rearrange("(k p) n -> p k n", p=128), writes=[wg_])
            for off in range(0, ncol, 128):
                ci = (c0 + off) // 128
                for which, w in ((0, wa_), (1, wg_)):
                    raw = rowf[cnt[0] % 3]
                    cnt[0] += 1
                    for ti, (t0, nt) in enumerate(TILES):
                        ps = self.pb()
                        for kk in range(16):
                            c.op("pe", lambda e: e.matmul(ps[:, :nt], lhsT=w[:, kk, off:off + 128], rhs=HT[:, kk, t0:t0 + nt], start=(kk == 0), stop=(kk == 15)), reads=[w, HT], writes=[ps])
                        if ti % 2 == 0:
                            c.op("act", lambda e: e.copy(out=raw[:, t0:t0 + nt], in_=ps[:, :nt]), reads=[ps], writes=[raw])
                        else:
                            c.op("dve", lambda e: e.tensor_copy(out=raw[:, t0:t0 + nt], in_=ps[:, :nt]), reads=[ps], writes=[raw])
                    if which == 0:
                        g = ga[ci % 2]
                        self.conv_row(raw, g, fw, ci, 86)
                        c.op("act", lambda e: e.activation(out=g[:], in_=g[:], func=AF.Silu), reads=[g], writes=[g])
                    else:
                        g = ga[ci % 2]
                        g2 = gb[ci % 2]
                        self.conv_row(raw, g2, fw, 43 + ci, 86)
                        a_ = ab[ci % 2]
                        c.op("dve", lambda e: e.tensor_tensor(out=a_[:], in0=g[:], in1=g2[:], op=ALU.mult), reads=[g, g2], writes=[a_])
                        c.dma("sp", ACTT[ci], a_[:], reads=[a_])
        c.end_stage()

    def stage_ffn_down(self, l, XTin, XTout):
        c = self.c
        I = self.I
        ACTT = self.S["ACTT"]
        for half in range(2):
            c.begin_stage()
            a = []
            for g0 in range(0, 43, 11):
                ng = min(11, 43 - g0)
                ap_ = c.sb([128, ng, 1024], BF16, "actin")
                c.dma("sp" if (g0 // 11) % 2 == 0 else "act", ap_[:], ACTT[g0:g0 + ng, :, half * 1024:(half + 1) * 1024].rearrange("c p t -> p c t"), writes=[ap_])
                a.append(ap_)

            class Shift:
                def __init__(s, buf, sh):
                    s.buf, s.sh = buf, sh
            wb = self.wbufs(43, 2, 512)
            tiles = [(half * 1024, 512), (half * 1024 + 512, 512)]
            self._resid_shift(I["w_down"][l], a, 43, l, 80, XTin, XTout, tiles, half * 1024, wb)
            c.end_stage()

    def _resid_shift(self, W, IN, KC, l, gbase, XTin, XTout, tiles, tshift, wb):
        c = self.c
        mt = self.modT
        xr = [c.sb([128, 1024], F32, "xr2") for _ in range(2)]
        blocks = self.blocks(0, 2048, 512)
        for bi, (c0, ncol) in enumerate(blocks):
            w = wb[bi % len(wb)]
            c.dma("pool", w[:, :KC, :ncol], W[:, c0:c0 + ncol].rearrange("(k p) n -> p k n", p=128), writes=[w])
            for off in range(0, ncol, 128):
                j = (c0 + off) // 128
                x = xr[j % 2]
                c.dma("sp", x[:], XTin[j][:, tshift:tshift + 1024], writes=[x])
                for ti, (t0, nt) in enumerate(tiles):
                    ps = self.pb()
                    for k in range(KC):
                        INp = IN[k // 11]
                        c.op("pe", lambda e: e.matmul(ps[:, :nt], lhsT=w[:, k, off:off + 128], rhs=INp[:, k % 11, t0 - tshift:t0 - tshift + nt], start=(k == 0), stop=(k == KC - 1)),
                             reads=[w, INp], writes=[ps])
                    b = 0 if t0 < 1024 else 1
                    c.op("dve", lambda e: e.scalar_tensor_tensor(out=x[:, t0 - tshift:t0 - tshift + nt], in0=ps[:, :nt], scalar=mt[:, l, gbase + j, b:b + 1],
                                                                 in1=x[:, t0 - tshift:t0 - tshift + nt], op0=ALU.mult, op1=ALU.add), reads=[ps, mt, x], writes=[x])
                c.dma("sp", XTout[j][:, tshift:tshift + 1024], x[:], reads=[x])

    def stage_final(self, XT):
        c = self.c
        I = self.I
        c.begin_stage()
        nf = c.sb([128, 16], F32, "nf")
        self.load_cols(nf, nf[:], I["norm_f"].rearrange("(n p) -> n p", p=128), 16)
        xs = [c.sb([128, 16, 128], F32, "fx") for _ in range(2)]
        sq = c.sb([128, 16, 128], BF16, "fsq")
        r = c.sb([128, 128], F32, "fr")
        yo = [c.sb([128, 2048], F32, "yo") for _ in range(2)]
        ones, eps, idf = self.onesB, self.epsT, self.identF
        for tt in range(16):
            t0 = tt * 128
            x = xs[tt % 2]
            y = yo[tt % 2]
            c.dma("sp", x[:], XT[:, :, t0:t0 + 128].rearrange("c p t -> p c t"), writes=[x])
            c.op("act", lambda e: e.activation(out=sq[:], in_=x[:], func=AF.Square), reads=[x], writes=[sq])
            ps = self.pb()
            for ch in range(16):
                c.op("pe", lambda e: e.matmul(ps[:, :128], lhsT=ones[:], rhs=sq[:, ch, :], start=(ch == 0), stop=(ch == 15)), reads=[ones, sq], writes=[ps])
            c.op("act", lambda e: e.activation(out=r[:], in_=ps[:, :128], func=AF.Sqrt, scale=1.0 / 2048, bias=eps[:, 0:1]), reads=[ps, eps], writes=[r])
            c.op("dve", lambda e: e.reciprocal(out=r[:], in_=r[:]), reads=[r], writes=[r])
            for ch in range(16):
                c.op("dve", lambda e: e.scalar_tensor_tensor(out=x[:, ch, :], in0=x[:, ch, :], scalar=nf[:, ch:ch + 1], in1=r[:], op0=ALU.mult, op1=ALU.mult), reads=[x, nf, r], writes=[x])
            for g in range(4):
                p2 = self.pb()
                for j in range(4):
                    ch = g * 4 + j
                    c.op("pe", lambda e: e.transpose(out=p2[:, j * 128:(j + 1) * 128], in_=x[:, ch, :], identity=idf[:]), reads=[x, idf], writes=[p2])
                if g % 2 == 0:
                    c.op("dve", lambda e: e.tensor_copy(out=y[:, g * 512:(g + 1) * 512], in_=p2[:]), reads=[p2], writes=[y])
                else:
                    c.op("act", lambda e: e.copy(out=y[:, g * 512:(g + 1) * 512], in_=p2[:]), reads=[p2], writes=[y])
            c.dma("sp", self.O["y"][t0:t0 + 128, :], y[:], reads=[y])
        c.end_stage()


class KB3(KB2):
    def stage_ret(self, l):
        c = self.c
        I = self.I
        c.begin_stage()
        QK, VTM, KTM, SRG = self.S["QK"], self.S["VTM"], self.S["KTM"], self.S["SRG"]
        MO = self.scr("MO", [3, 8, 128, NT], BF16)
        OSR = self.O["osret"]
        ones, eps = self.onesB, self.epsT
        V = c.sb([128, 16, 1024], BF16, "V")
        c.dma("sp", V[:], VTM.rearrange("t p n -> p t n"), writes=[V])
        Kt = c.sb([128, 8, 512], BF16, "Kt")
        c.dma("sp", Kt[:], KTM.rearrange("t p n -> p t n"), writes=[Kt])
        lg = c.sb([128, 16], F32, "lg")
        c.dma("sp", lg[:], I["ret_decay"][l].rearrange("d h -> (d h)").partition_broadcast(128), writes=[lg])
        c.op("act", lambda e: e.activation(out=lg[:], in_=lg[:], func=AF.Exp, scale=-1.0), reads=[lg], writes=[lg])
        c.op("act", lambda e: e.activation(out=lg[:], in_=lg[:], func=AF.Ln, bias=self.onesF[:, 0:1]), reads=[lg], writes=[lg])
        c.op("dve", lambda e: e.tensor_scalar(out=lg[:], in0=lg[:], scalar1=-1.0, scalar2=None, op0=ALU.mult), reads=[lg], writes=[lg])
        tabs = {}
        for nm, w in (("S", 1920), ("P", 384)):
            for k in ("dpos", "dneg", "dz"):
                t = c.sb([128, w], F32, k + nm)
                c.dma("sp", t[:], I[k + nm], writes=[t])
                tabs[k + nm] = t
        expfb = c.sb([128, 4], F32, "expfb")
        c.dma("sp", expfb[:], I["expfb"], writes=[expfb])
        idx1 = c.sb([128, 1024], F32, "idx1")
        idx2 = c.sb([128, 1024], F32, "idx2")
        c.dma("sp", idx1[:], I["idx1"], writes=[idx1])
        c.dma("sp", idx2[:], I["idx2"], writes=[idx2])
        KD = c.sb([128, 2, 8, 2], F32, "KD")
        for d in range(2):
            for h in range(8):
                c.op("act", lambda e: e.activation(out=KD[:, d, h, :], in_=expfb[:, d * 2:d * 2 + 2], func=AF.Exp, scale=lg[:, d * 8 + h:d * 8 + h + 1]), reads=[expfb, lg], writes=[KD])
        TabS2 = [c.sb([128, 1920], F32, "TabS") for _ in range(2)]
        TabP2 = [c.sb([128, 384], F32, "TabP") for _ in range(2)]
        ttmp = c.sb([128, 1920], F32, "ttmp")
        qrow = c.sb([128, NT], BF16, "qrow")
        krow = c.sb([128, NT], BF16, "krow")
        lgsel = c.sb([128, 2], F32, "lgsel")
        dec = c.sb([128, 1024], F32, "dec")
        qdf = c.sb([128, 1024], BF16, "qdf")
        qdb = c.sb([128, 1024], BF16, "qdb")
        S0 = c.sb([128, 2, 128], BF16, "S0")
        srg = c.sb([128, NT], BF16, "srg")
        sm = [c.sb([128, 512], BF16, "sm") for _ in range(3)]
        osb = c.sb([128, 512], F32, "osb")
        osq = c.sb([128, 512], BF16, "osq")
        rr = c.sb([128, 512], F32, "rr")
        orow = c.sb([128, NT], BF16, "orow")
        kf = [c.sb([128, 64], BF16, "kf") for _ in range(4)]
        sst = [c.sb([64, 128], F32, "sst") for _ in range(4)]
        n_sm = [0]
        n_kf = [0]

        def build_tab(Tab, nm, w, h):
            dp, dn, dz = tabs["dpos" + nm], tabs["dneg" + nm], tabs["dz" + nm]
            c.op("dve", lambda e: e.tensor_scalar(out=ttmp[:, :w], in0=dp[:], scalar1=lg[:, h:h + 1], scalar2=None, op0=ALU.mult), reads=[dp, lg], writes=[ttmp])
            c.op("dve", lambda e: e.scalar_tensor_tensor(out=ttmp[:, :w], in0=dn[:], scalar=lg[:, 8 + h:9 + h], in1=ttmp[:, :w], op0=ALU.mult, op1=ALU.add), reads=[dn, lg, ttmp], writes=[ttmp])
            c.op("act", lambda e: e.activation(out=ttmp[:, :w], in_=ttmp[:, :w], func=AF.Exp), reads=[ttmp], writes=[ttmp])
            c.op("dve", lambda e: e.tensor_tensor(out=Tab[:, :w], in0=ttmp[:, :w], in1=dz[:], op=ALU.add), reads=[ttmp, dz], writes=[Tab])

        def finish_o(pso, n, h, tok0):
            c.op("act", lambda e: e.copy(out=osb[:, :n], in_=pso[:, :n]), reads=[pso], writes=[osb])
            c.op("act", lambda e: e.activation(out=osq[:, :n], in_=osb[:, :n], func=AF.Square), reads=[osb], writes=[osq])
            p2 = self.pb()
            c.op("pe", lambda e: e.matmul(p2[:, :n], lhsT=ones[:], rhs=osq[:, :n], start=True, stop=True), reads=[ones, osq], writes=[p2])
            c.op("act", lambda e: e.activation(out=rr[:, :n], in_=p2[:, :n], func=AF.Sqrt, scale=1.0 / 128, bias=eps[:, 0:1]), reads=[p2, eps], writes=[rr])
            c.op("dve", lambda e: e.reciprocal(out=rr[:, :n], in_=rr[:, :n]), reads=[rr], writes=[rr])
            c.op("dve", lambda e: e.tensor_tensor(out=osb[:, :n], in0=osb[:, :n], in1=rr[:, :n], op=ALU.mult), reads=[osb, rr], writes=[osb])
            c.op("dve", lambda e: e.tensor_tensor(out=orow[:, tok0:tok0 + n], in0=osb[:, :n], in1=srg[:, tok0:tok0 + n], op=ALU.mult), reads=[osb, srg], writes=[orow])

        srg2 = [srg, c.sb([128, NT], BF16, "srg2")]
        orow2 = [orow, c.sb([128, NT], BF16, "orow2")]
        sm.append(c.sb([128, 512], BF16, "sm"))
        NS = len(sm)
        pending = []

        def flush(keep):
            while len(pending) > keep:
                pending.pop(0)()

        def scores_loop(n_j, mk_score, mk_acc):
            nxt = mk_score(0)
            for jc in range(n_j):
                cur = nxt
                nxt = mk_score(jc + 1) if jc + 1 < n_j else None
                mk_acc(jc, cur)

        for h in range(8):
            hp, po = h // 2, (h % 2) * 64
            srg_h, orow_h = srg2[h % 2], orow2[h % 2]
            if h % 2 == 0:
                c.dma("sp", qrow[:], QK[hp], writes=[qrow])
                c.dma("sp", krow[:], QK[4 + hp], writes=[krow])
                for d in range(2):
                    c.op("dve", lambda e: e.tensor_copy(out=lgsel[0:64, d:d + 1], in_=lg[0:64, d * 8 + h:d * 8 + h + 1]), reads=[lg], writes=[lgsel])
                    c.op("dve", lambda e: e.tensor_copy(out=lgsel[64:128, d:d + 1], in_=lg[64:128, d * 8 + h + 1:d * 8 + h + 2]), reads=[lg], writes=[lgsel])
                c.op("act", lambda e: e.activation(out=dec[:], in_=idx1[:], func=AF.Exp, scale=lgsel[:, 0:1]), reads=[idx1, lgsel], writes=[dec])
                c.op("dve", lambda e: e.tensor_tensor(out=qdf[:], in0=qrow[:, 1024:2048], in1=dec[:], op=ALU.mult), reads=[qrow, dec], writes=[qdf])
                c.op("act", lambda e: e.activation(out=dec[:], in_=idx2[:], func=AF.Exp, scale=lgsel[:, 1:2]), reads=[idx2, lgsel], writes=[dec])
                c.op("dve", lambda e: e.tensor_tensor(out=qdb[:], in0=qrow[:, 1024:2048], in1=dec[:], op=ALU.mult), reads=[qrow, dec], writes=[qdb])
            c.dma("sp", srg_h[:], SRG[h], writes=[srg_h])
            for d in range(2):
                c.dma("pool", S0[po:po + 64, d, :], I["sret"][l, d, h], writes=[S0])
            if h == 0:
                build_tab(TabS2[0], "S", 1920, 0)
                build_tab(TabP2[0], "P", 384, 0)
            TabS, TabP = TabS2[h % 2], TabP2[h % 2]
            if h + 1 < 8:
                build_tab(TabS2[(h + 1) % 2], "S", 1920, h + 1)
                build_tab(TabP2[(h + 1) % 2], "P", 384, h + 1)

            def fin(pso, n, tok0, h=h, srg_h=srg_h, orow_h=orow_h, store=False):
                def f():
                    c.op("act", lambda e: e.copy(out=osb[:, :n], in_=pso[:, :n]), reads=[pso], writes=[osb])
                    c.op("act", lambda e: e.activation(out=osq[:, :n], in_=osb[:, :n], func=AF.Square), reads=[osb], writes=[osq])
                    p2 = self.pb()
                    c.op("pe", lambda e: e.matmul(p2[:, :n], lhsT=ones[:], rhs=osq[:, :n], start=True, stop=True), reads=[ones, osq], writes=[p2])
                    c.op("act", lambda e: e.activation(out=rr[:, :n], in_=p2[:, :n], func=AF.Ln, scale=1.0 / 128, bias=eps[:, 0:1]), reads=[p2, eps], writes=[rr])
                    c.op("act", lambda e: e.activation(out=rr[:, :n], in_=rr[:, :n], func=AF.Exp, scale=-0.5), reads=[rr], writes=[rr])
                    c.op("dve", lambda e: e.tensor_tensor(out=osb[:, :n], in0=osb[:, :n], in1=rr[:, :n], op=ALU.mult), reads=[osb, rr], writes=[osb])
                    c.op("dve", lambda e: e.tensor_tensor(out=orow_h[:, tok0:tok0 + n], in0=osb[:, :n], in1=srg_h[:, tok0:tok0 + n], op=ALU.mult), reads=[osb, srg_h], writes=[orow_h])
                    if store:
                        c.dma("sp", MO[0][h], orow_h[:], reads=[orow_h])
                return f

            for i0 in (0, 512):
                pso = self.pacc()

                def mk_score(jc, i0=i0):
                    pss = self.pb()
                    c.op("pe", lambda e: e.matmul(pss[:, :512], lhsT=krow[po:po + 64, 1024 + jc * 128:1024 + (jc + 1) * 128], rhs=qrow[po:po + 64, 1024 + i0:1024 + i0 + 512], start=True, stop=True),
                         reads=[krow, qrow], writes=[pss])
                    return pss

                def mk_acc(jc, pss, i0=i0, pso=pso):
                    s = sm[n_sm[0] % NS]
                    n_sm[0] += 1
                    u0 = i0 - 128 * jc + 896
                    c.op("dve", lambda e: e.tensor_tensor(out=s[:], in0=pss[:, :512], in1=TabS[:, u0:u0 + 512], op=ALU.mult), reads=[pss, TabS], writes=[s])
                    c.op("pe", lambda e: e.matmul(pso[:, :512], lhsT=V[:, 8 + jc, h * 128:(h + 1) * 128], rhs=s[:], start=(jc == 0), stop=False), reads=[V, s], writes=[pso])

                scores_loop(8, mk_score, mk_acc)
                c.op("pe", lambda e: e.matmul(pso[:, :512], lhsT=S0[po:po + 64, 0, :], rhs=qdf[po:po + 64, i0:i0 + 512], start=False, stop=False), reads=[S0, qdf], writes=[pso])
                c.op("pe", lambda e: e.matmul(pso[:, :512], lhsT=S0[po:po + 64, 1, :], rhs=qdb[po:po + 64, i0:i0 + 512], start=False, stop=True), reads=[S0, qdb], writes=[pso])
                pending.append(fin(pso, 512, 1024 + i0))
                flush(1)
            for s_ in range(4):
                b0 = s_ * 256
                pso = self.pacc()

                def mk_score(jc, b0=b0):
                    pss = self.pb()
                    c.op("pe", lambda e: e.matmul(pss[:, :256], lhsT=krow[po:po + 64, b0 + jc * 128:b0 + (jc + 1) * 128], rhs=qrow[po:po + 64, b0:b0 + 256], start=True, stop=True),
                         reads=[krow, qrow], writes=[pss])
                    return pss

                def mk_acc(jc, pss, pso=pso, s_=s_):
                    s = sm[n_sm[0] % NS]
                    n_sm[0] += 1
                    u0 = 128 - 128 * jc
                    c.op("dve", lambda e: e.tensor_tensor(out=s[:, :256], in0=pss[:, :256], in1=TabP[:, u0:u0 + 256], op=ALU.mult), reads=[pss, TabP], writes=[s])
                    c.op("pe", lambda e: e.matmul(pso[:, :256], lhsT=V[:, s_ * 2 + jc, h * 128:(h + 1) * 128], rhs=s[:, :256], start=(jc == 0), stop=(jc == 1)), reads=[V, s], writes=[pso])

                scores_loop(2, mk_score, mk_acc)
                pending.append(fin(pso, 256, b0, store=(s_ == 3)))
                flush(1)
                for d in range(2):
                    pst = self.pb()
                    for tt in range(2):
                        k_ = kf[n_kf[0] % 4]
                        n_kf[0] += 1
                        c.op("dve", lambda e: e.tensor_scalar(out=k_[:], in0=Kt[:, s_ * 2 + tt, h * 64:(h + 1) * 64], scalar1=KD[:, d, h, tt:tt + 1], scalar2=None, op0=ALU.mult), reads=[Kt, KD], writes=[k_])
                        c.op("pe", lambda e: e.matmul(pst[:64, :128], lhsT=k_[:], rhs=V[:, s_ * 2 + tt, h * 128:(h + 1) * 128], start=(tt == 0), stop=(tt == 1)), reads=[k_, V], writes=[pst])
                    st = sst[(s_ * 2 + d) % 4]
                    c.op("act", lambda e: e.copy(out=st[:], in_=pst[:64, :128]), reads=[pst], writes=[st])
                    c.dma("sp", OSR[s_, l, d, h], st[:], reads=[st])
        flush(0)
        c.end_stage()


import os


class KB4(KB3):
    def stage_dn(self, l):
        c = self.c
        I = self.I
        c.begin_stage()
        DQK, DVT, SDZ = self.S["DQK"], self.S["DVT"], self.S["SDZ"]
        MO = self.scr("MO", [3, 8, 128, NT], BF16)
        OSD = self.O["osdn"]
        onesF, onesB, eps, idF, idB = self.onesF, self.onesB, self.epsT, self.identF, self.identB
        gbraw = self.gbraw
        cm = {}
        for nm in ("UF", "UB", "SU", "SL"):
            t = c.sb([128, 128], F32, nm)
            c.dma("sp", t[:], I[nm], writes=[t])
            cm[nm] = t
        alog = c.sb([128, 16], F32, "alog")
        dtb = c.sb([128, 16], F32, "dtb")
        c.dma("sp", alog[:], I["dn_a_log"][l].rearrange("d h -> (d h)").partition_broadcast(128), writes=[alog])
        c.dma("sp", dtb[:], I["dn_dt_bias"][l].rearrange("d h -> (d h)").partition_broadcast(128), writes=[dtb])
        dnn = c.sb([128, 1], F32, "dnn")
        c.dma("sp", dnn[:], I["dn_norm"][l].rearrange("(p o) -> p o", o=1), writes=[dnn])
        c.op("act", lambda e: e.activation(out=alog[:], in_=alog[:], func=AF.Exp), reads=[alog], writes=[alog])
        c.op("dve", lambda e: e.tensor_scalar(out=alog[:], in0=alog[:], scalar1=-1.0, scalar2=None, op0=ALU.mult), reads=[alog], writes=[alog])
        G = c.sb([128, 16, 16], F32, "G")
        BT = c.sb([128, 16, 16], F32, "BT")
        NBT = c.sb([128, 16, 16], F32, "NBT")
        for tt in range(16):
            c.op("dve", lambda e: e.tensor_tensor(out=G[:, tt, :], in0=gbraw[:, tt, 0:16], in1=dtb[:], op=ALU.add), reads=[gbraw, dtb], writes=[G])
        c.op("act", lambda e: e.activation(out=G[:], in_=G[:], func=AF.Exp), reads=[G], writes=[G])
        c.op("act", lambda e: e.activation(out=G[:], in_=G[:], func=AF.Ln, bias=onesF[:, 0:1]), reads=[G, onesF], writes=[G])
        for tt in range(16):
            c.op("dve", lambda e: e.tensor_tensor(out=G[:, tt, :], in0=G[:, tt, :], in1=alog[:], op=ALU.mult), reads=[G, alog], writes=[G])
        c.op("act", lambda e: e.activation(out=BT[:], in_=gbraw[:, :, 16:32], func=AF.Sigmoid), reads=[gbraw], writes=[BT])
        c.op("dve", lambda e: e.tensor_scalar(out=NBT[:], in0=BT[:], scalar1=-1.0, scalar2=None, op0=ALU.mult), reads=[BT], writes=[NBT])
        GC = c.sb([128, 16, 16], F32, "GC")
        GL = c.sb([128, 16, 16], F32, "GL")
        for tt in range(16):
            for d in range(2):
                U = cm["UF"] if d == 0 else cm["UB"]
                ps = self.pb()
                c.op("pe", lambda e: e.matmul(ps[:, 0:8], lhsT=U[:], rhs=G[:, tt, d * 8:d * 8 + 8], start=True, stop=True), reads=[U, G], writes=[ps])
                c.op("pe", lambda e: e.matmul(ps[:, 8:16], lhsT=onesF[:], rhs=G[:, tt, d * 8:d * 8 + 8], start=True, stop=True), reads=[onesF, G], writes=[ps])
                c.op("dve", lambda e: e.tensor_copy(out=GC[:, tt, d * 8:d * 8 + 8], in_=ps[:, 0:8]), reads=[ps], writes=[GC])
                c.op("dve", lambda e: e.tensor_copy(out=GL[:, tt, d * 8:d * 8 + 8], in_=ps[:, 8:16]), reads=[ps], writes=[GL])
        BK = c.sb([128, 16, 16], F32, "BK")
        KDS = c.sb([128, 16, 16], F32, "KDS")
        EGL = c.sb([128, 16, 16], F32, "EGL")
        c.op("act", lambda e: e.activation(out=BK[:], in_=GC[:], func=AF.Exp), reads=[GC], writes=[BK])
        c.op("dve", lambda e: e.tensor_tensor(out=BK[:], in0=BK[:], in1=BT[:], op=ALU.mult), reads=[BK, BT], writes=[BK])
        c.op("dve", lambda e: e.tensor_tensor(out=KDS[:], in0=GL[:], in1=GC[:], op=ALU.subtract), reads=[GL, GC], writes=[KDS])
        c.op("act", lambda e: e.activation(out=KDS[:], in_=KDS[:], func=AF.Exp), reads=[KDS], writes=[KDS])
        c.op("act", lambda e: e.activation(out=EGL[:], in_=GL[:], func=AF.Exp), reads=[GL], writes=[EGL])

        if "GDBG" in self.dbg:
            gd = self.scr("GDBG", [4, 128, 16, 16], F32)
            for i_, t_ in enumerate((G, BT, GC, GL)):
                c.dma("sp", gd[i_], t_[:], reads=[t_])
        OACC = [c.sb([128, NT], F32, "oacc") for _ in range(8)]
        c.begin_stage()
        Sf = [c.sb([128, 128], F32, "Sf") for _ in range(8)]
        Sb = [c.sb([128, 128], BF16, "Sb") for _ in range(8)]
        NI = 8

        def ring(shape, dt, nm, n=NI):
            bufs = [c.sb(shape, dt, nm) for _ in range(n)]
            st = [0]

            def nxt():
                st[0] += 1
                return bufs[st[0] % n]
            return nxt
        r_q = ring([128, 128], BF16, "rq")
        r_k = ring([128, 128], BF16, "rk")
        r_v = ring([128, 128], BF16, "rv")
        r_gbc = ring([128, 128], F32, "rgbc")
        r_t = ring([128, 128], F32, "rt", 2 * NI)
        r_vb = ring([128, 128], F32, "rvb")
        r_kbe = ring([128, 128], F32, "rkbe")
        r_kd = ring([128, 128], F32, "rkd")
        r_nw = ring([128, 128], F32, "rnw")
        r_at = ring([128, 128], BF16, "rat")
        r_qd = ring([128, 128], BF16, "rqd")
        r_vn = ring([128, 128], F32, "rvn")
        r_vnb = ring([128, 128], BF16, "rvnb")
        r_so = ring([128, 128], F32, "rso", 3)
        r_tw = ring([128, 256], F32, "rtw", 2 * NI)
        r_tq = ring([128, 256], F32, "rtq", NI)
        r_am = ring([128, 256], F32, "ram", 2 * NI)
        r_pp = ring([128, 256], F32, "rpp", NI)
        mlu = c.sb([128, 7, 256], F32, "mlu")
        mul = c.sb([128, 7, 256], F32, "mul")
        c.dma("sp", mlu[:], I["MLU"], writes=[mlu])
        c.dma("sp", mul[:], I["MUL"], writes=[mul])
        id2 = c.sb([128, 256], F32, "id2")
        c.dma("sp", id2[:, 0:128], I["ident"], writes=[id2])
        c.dma("sp", id2[:, 128:256], I["ident"], writes=[id2])
        PB = self.PB
        pbn = [0]

        def pbank():
            pbn[0] += 1
            return PB[pbn[0] % 8]

        def process(tt, h, d, last_seq):
            t0 = tt * 128
            col = d * 8 + h
            U = cm["UF"] if d == 0 else cm["UB"]
            MS = cm["SL"] if d == 0 else cm["SU"]
            MI = cm["UF"] if d == 0 else cm["UB"]
            gc_ap = GC[:, tt, col:col + 1]
            qT, kT, vT = r_q(), r_k(), r_v()
            c.dma("sp", qT[:], DQK[h][:, t0:t0 + 128], writes=[qT])
            c.dma("sp", kT[:], DQK[8 + h][:, t0:t0 + 128], writes=[kT])
            c.dma("sp", vT[:], DVT[h][:, t0:t0 + 128], writes=[vT])
            gbc = r_gbc()
            c.op("act", lambda e: e.activation(out=gbc[:], in_=onesF[:], func=AF.Copy, scale=G[:, tt, col:col + 1]), reads=[onesF, G], writes=[gbc])
            yield
            bank = PB[h]
            ptr = pR = pG = pT = ps1 = ps2 = pw = psv = pso = pss = bank
            ptb = bank[:].bitcast(BF16)
            c.op("pe", lambda e: e.transpose(out=ptb[:, 0:128], in_=kT[:], identity=idB[:]), reads=[kT, idB], writes=[ptr])
            c.op("pe", lambda e: e.transpose(out=ptb[:, 128:256], in_=vT[:], identity=idB[:]), reads=[vT, idB], writes=[ptr])
            c.op("pe", lambda e: e.matmul(pR[:, 128:256], lhsT=gbc[:], rhs=U[:], start=True, stop=True), reads=[gbc, U], writes=[pR])
            c.op("pe", lambda e: e.matmul(pG[:, 256:384], lhsT=kT[:], rhs=kT[:], start=True, stop=True), reads=[kT], writes=[pG])
            c.op("pe", lambda e: e.matmul(pG[:, 384:512], lhsT=kT[:], rhs=qT[:], start=True, stop=True), reads=[kT, qT], writes=[pG])
            yield
            vb, kbe, kd = r_vb(), r_kbe(), r_kd()
            c.op("dve", lambda e: e.tensor_scalar(out=kbe[:], in0=ptb[:, 0:128], scalar1=BK[:, tt, col:col + 1], scalar2=None, op0=ALU.mult), reads=[ptr, BK], writes=[kbe])
            c.op("dve", lambda e: e.tensor_scalar(out=kd[:], in0=ptb[:, 0:128], scalar1=KDS[:, tt, col:col + 1], scalar2=None, op0=ALU.mult), reads=[ptr, KDS], writes=[kd])
            c.op("dve", lambda e: e.tensor_scalar(out=vb[:], in0=ptb[:, 128:256], scalar1=BT[:, tt, col:col + 1], scalar2=None, op0=ALU.mult), reads=[ptr, BT], writes=[vb])
            d1, d2 = r_t(), r_t()
            c.op("dve", lambda e: e.tensor_scalar(out=d1[:], in0=pR[:, 128:256], scalar1=gc_ap, scalar2=0.0, op0=ALU.subtract, op1=ALU.max), reads=[pR, GC], writes=[d1])
            c.op("dve", lambda e: e.tensor_scalar(out=d2[:], in0=pR[:, 128:256], scalar1=gc_ap, scalar2=0.0, op0=ALU.subtract, op1=ALU.min), reads=[pR, GC], writes=[d2])
            er = gbc
            c.op("act", lambda e: e.activation(out=er[:], in_=pR[:, 128:256], func=AF.Exp), reads=[pR], writes=[er])
            yield
            c.op("act", lambda e: e.activation(out=d1[:], in_=d1[:], func=AF.Exp, scale=-1.0), reads=[d1], writes=[d1])
            c.op("act", lambda e: e.activation(out=d2[:], in_=d2[:], func=AF.Exp), reads=[d2], writes=[d2])
            qd = r_qd()
            c.op("pool", lambda e: e.tensor_tensor(out=qd[:], in0=qT[:], in1=er[:], op=ALU.mult), reads=[qT, er], writes=[qd])
            yield
            pp = r_pp()
            c.op("dve", lambda e: e.scalar_tensor_tensor(out=d1[:], in0=pG[:, 256:384], scalar=NBT[:, tt, col:col + 1], in1=d1[:], op0=ALU.mult, op1=ALU.mult), reads=[pG, NBT, d1], writes=[d1])
            c.op("dve", lambda e: e.tensor_tensor(out=d2[:], in0=pG[:, 384:512], in1=d2[:], op=ALU.mult), reads=[pG, d2], writes=[d2])
            yield
            c.op("pool", lambda e: e.tensor_tensor(out=pp[:, 0:128], in0=d1[:], in1=MS[:], op=ALU.mult), reads=[d1, MS], writes=[pp])
            at = r_at()
            c.op("pool", lambda e: e.tensor_tensor(out=at[:], in0=d2[:], in1=MI[:], op=ALU.mult), reads=[d2, MI], writes=[at])
            yield
            c.op("pe", lambda e: e.transpose(out=pT[:, :128], in_=pp[:, 0:128], identity=idF[:]), reads=[pp, idF], writes=[pT])
            yield
            c.op("act", lambda e: e.copy(out=pp[:, 128:256], in_=pT[:, :128]), reads=[pT], writes=[pp])
            yield
            MM = mlu if d == 0 else mul
            am = r_am()
            c.op("pool", lambda e: e.tensor_tensor(out=am[:], in0=pp[:], in1=MM[:, 0, :], op=ALU.mult), reads=[pp, MM], writes=[am])
            yield
            tw = r_tw()
            c.op("dve", lambda e: e.tensor_tensor(out=tw[:], in0=am[:], in1=id2[:], op=ALU.add), reads=[am, id2], writes=[tw])
            am = r_am()
            c.op("pool", lambda e: e.tensor_tensor(out=am[:], in0=pp[:], in1=MM[:, 1, :], op=ALU.mult), reads=[pp, MM], writes=[am])
            yield
            for s in range(1, 7):
                c.op("pe", lambda e: e.matmul(ps1[:, 0:128], lhsT=am[:, 128:256], rhs=tw[:, 0:128], start=True, stop=True), reads=[am, tw], writes=[ps1])
                c.op("pe", lambda e: e.matmul(ps1[:, 128:256], lhsT=am[:, 0:128], rhs=tw[:, 128:256], start=True, stop=True), reads=[am, tw], writes=[ps1])
                if s < 6:
                    am = r_am()
                    c.op("pool", lambda e: e.tensor_tensor(out=am[:], in0=pp[:], in1=MM[:, s + 1, :], op=ALU.mult), reads=[pp, MM], writes=[am])
                yield
                p1 = r_tq()
                c.op("act", lambda e: e.copy(out=p1[:], in_=ps1[:, 0:256]), reads=[ps1], writes=[p1])
                yield
                c.op("pe", lambda e: e.matmul(ps2[:, 256:384], lhsT=tw[:, 128:256], rhs=p1[:, 0:128], start=True, stop=True), reads=[tw, p1], writes=[ps2])
                c.op("pe", lambda e: e.matmul(ps2[:, 384:512], lhsT=tw[:, 0:128], rhs=p1[:, 128:256], start=True, stop=True), reads=[tw, p1], writes=[ps2])
                yield
                ntw = r_tw()
                c.op("dve", lambda e: e.tensor_tensor(out=ntw[:], in0=ps2[:, 256:512], in1=tw[:], op=ALU.add), reads=[ps2, tw], writes=[ntw])
                tw = ntw
                yield
            c.op("pe", lambda e: e.matmul(pw[:, :128], lhsT=kbe[:], rhs=tw[:, 128:256], start=True, stop=True), reads=[kbe, tw], writes=[pw])
            yield
            nw = r_nw()
            c.op("act", lambda e: e.activation(out=nw[:], in_=pw[:, :128], func=AF.Copy, scale=-1.0), reads=[pw], writes=[nw])
            yield
            S_f, S_b = Sf[h], Sb[h]
            c.op("pe", lambda e: e.matmul(psv[:, 128:256], lhsT=tw[:, 128:256], rhs=vb[:], start=True, stop=False), reads=[tw, vb], writes=[psv])
            c.op("pe", lambda e: e.matmul(psv[:, 128:256], lhsT=nw[:], rhs=S_f[:], start=False, stop=True), reads=[nw, S_f], writes=[psv])
            yield
            vn = r_vn()
            vnb = r_vnb()
            c.op("act", lambda e: e.copy(out=vn[:], in_=psv[:, 128:256]), reads=[psv], writes=[vn])
            c.op("act", lambda e: e.copy(out=vnb[:], in_=vn[:]), reads=[vn], writes=[vnb])
            yield
            c.op("pe", lambda e: e.matmul(pso[:, 256:384], lhsT=S_b[:], rhs=qd[:], start=True, stop=False), reads=[S_b, qd], writes=[pso])
            c.op("pe", lambda e: e.matmul(pso[:, 256:384], lhsT=vnb[:], rhs=at[:], start=False, stop=True), reads=[vnb, at], writes=[pso])
            c.op("pe", lambda e: e.matmul(pss[:, 384:512], lhsT=kd[:], rhs=vn[:], start=True, stop=True), reads=[kd, vn], writes=[pss])
            yield
            oa = OACC[h]
            if d == 0:
                c.op("act", lambda e: e.copy(out=oa[:, t0:t0 + 128], in_=pso[:, 256:384]), reads=[pso], writes=[oa])
            else:
                c.op("dve", lambda e: e.tensor_tensor(out=oa[:, t0:t0 + 128], in0=pso[:, 256:384], in1=oa[:, t0:t0 + 128], op=ALU.add), reads=[pso, oa], writes=[oa])
            c.op("dve", lambda e: e.scalar_tensor_tensor(out=S_f[:], in0=S_f[:], scalar=EGL[:, tt, col:col + 1], in1=pss[:, 384:512], op0=ALU.mult, op1=ALU.add), reads=[S_f, EGL, pss], writes=[S_f])
            yield
            c.op("act", lambda e: e.copy(out=S_b[:], in_=S_f[:]), reads=[S_f], writes=[S_b])
            if last_seq is not None:
                so = r_so()
                c.op("pool", lambda e: e.tensor_copy(out=so[:], in_=S_f[:]), reads=[S_f], writes=[so])
                c.dma("sp", OSD[last_seq, l, d, h], so[:], reads=[so])

        def init_state(s, h, d):
            if s is None:
                c.dma("sp", Sf[h][:], I["sdn"][l, d, h], writes=[Sf[h]])
                c.op("act", lambda e: e.copy(out=Sb[h][:], in_=Sf[h][:]), reads=[Sf[h]], writes=[Sb[h]])
            else:
                c.op("pool", lambda e: e.memset(Sf[h][:], 0.0), writes=[Sf[h]])
                c.op("pool", lambda e: e.memset(Sb[h][:], 0.0), writes=[Sb[h]])

        def inst(tt, h, d, s, first, last):
            if first:
                init_state(s, h, d)
            yield from process(tt, h, d, s if (s is not None and last) else None)

        queue = []
        for d in range(2):
            seqs = [(s, [2 * s, 2 * s + 1]) for s in range(4)] + [(None, list(range(8, 16)))]
            for s, tiles in seqs:
                order = tiles if d == 0 else tiles[::-1]
                for i, tt in enumerate(order):
                    for h in range(8):
                        queue.append(inst(tt, h, d, s, i == 0, i == len(order) - 1))
        active = []
        qi = 0
        cyc = 0
        STAG = 5
        while qi < len(queue) or active:
            if qi < len(queue) and len(active) < NI and (qi >= NI or cyc >= qi * STAG):
                active.append(queue[qi])
                qi += 1
            alive = []
            for g in active:
                try:
                    next(g)
                    alive.append(g)
                except StopIteration:
                    pass
            active = alive
            cyc += 1
        c.end_stage()
        osq = c.sb([128, 512], BF16, "dosq")
        rr = c.sb([128, 512], F32, "drr")
        sdz = [c.sb([128, NT], BF16, "sdz") for _ in range(2)]
        orow = [c.sb([128, NT], BF16, "dorow") for _ in range(2)]
        for h in range(8):
            oa = OACC[h]
            z_ = sdz[h % 2]
            orw = orow[h % 2]
            c.dma("sp", z_[:], SDZ[h], writes=[z_])
            for (u0, nu) in TILES:
                c.op("act", lambda e: e.activation(out=osq[:, :nu], in_=oa[:, u0:u0 + nu], func=AF.Square), reads=[oa], writes=[osq])
                p2 = self.pb()
                c.op("pe", lambda e: e.matmul(p2[:, :nu], lhsT=onesB[:], rhs=osq[:, :nu], start=True, stop=True), reads=[onesB, osq], writes=[p2])
                c.op("act", lambda e: e.activation(out=rr[:, :nu], in_=p2[:, :nu], func=AF.Ln, scale=1.0 / 128, bias=eps[:, 0:1]), reads=[p2, eps], writes=[rr])
                c.op("act", lambda e: e.activation(out=rr[:, :nu], in_=rr[:, :nu], func=AF.Exp, scale=-0.5), reads=[rr], writes=[rr])
                c.op("dve", lambda e: e.scalar_tensor_tensor(out=rr[:, :nu], in0=oa[:, u0:u0 + nu], scalar=dnn[:, 0:1], in1=rr[:, :nu], op0=ALU.mult, op1=ALU.mult), reads=[oa, dnn, rr], writes=[rr])
                c.op("dve", lambda e: e.tensor_tensor(out=orw[:, u0:u0 + nu], in0=rr[:, :nu], in1=z_[:, u0:u0 + nu], op=ALU.mult), reads=[rr, z_], writes=[orw])
            c.dma("sp", MO[2][h], orw[:], reads=[orw])
        c.end_stage()


PI = math.pi


class KB5(KB4):
    def hy_tables(self, l, L):
        c = self.c
        I = self.I
        nch = L // 128
        TAB = self.scr("HTAB%d" % L, [2, 2 * nch + 1, 128, 1024], BF16)
        c.begin_stage()
        onesF, m0 = self.onesF, None
        m0 = c.sb([128, 2], F32, "m0")
        c.dma("sp", m0[:], I["m0"], writes=[m0])
        fT = c.sb([33, L], F32, "fT")
        c.dma("sp", fT[:], I["featsT%d" % L], writes=[fT])
        w1 = c.sb([33, 64], F32, "w1")
        w2 = c.sb([64, 64], F32, "w2")
        w3 = c.sb([64, 4096], F32, "w3")
        c.dma("sp", w1[:], I["hy_w1"][l], writes=[w1])
        c.dma("sp", w2[:], I["hy_w2"][l], writes=[w2])
        c.dma("sp", w3[:], I["hy_w3"][l], writes=[w3])
        vec = c.sb([64, 4], F32, "hvec")
        for i, nm in enumerate(("hy_b1", "hy_freq1", "hy_b2", "hy_freq2")):
            c.dma("sp", vec[:, i:i + 1], I[nm][l].rearrange("(p o) -> p o", o=1), writes=[vec])
        hid1 = c.sb([64, L], F32, "hid1")
        hid2 = c.sb([64, L], F32, "hid2")
        msk = c.sb([64, 512], F32, "msk")

        def sin_layer(dst, wT, K, src, bi, fi):
            for t0 in range(0, L, 512):
                n = min(512, L - t0)
                ps = self.pb()
                c.op("pe", lambda e: e.matmul(ps[:64, :n], lhsT=wT[:K, :], rhs=src[:K, t0:t0 + n], start=True, stop=True), reads=[wT, src], writes=[ps])
                d = dst
                c.op("dve", lambda e: e.tensor_scalar(out=d[:, t0:t0 + n], in0=ps[:64, :n], scalar1=vec[:, bi:bi + 1], scalar2=vec[:, fi:fi + 1], op0=ALU.add, op1=ALU.mult), reads=[ps, vec], writes=[d])
                for _ in range(2):
                    c.op("dve", lambda e: e.tensor_scalar(out=msk[:, :n], in0=d[:, t0:t0 + n], scalar1=PI, scalar2=-2 * PI, op0=ALU.is_gt, op1=ALU.mult), reads=[d], writes=[msk])
                    c.op("dve", lambda e: e.tensor_tensor(out=d[:, t0:t0 + n], in0=d[:, t0:t0 + n], in1=msk[:, :n], op=ALU.add), reads=[d, msk], writes=[d])
                    c.op("dve", lambda e: e.tensor_scalar(out=msk[:, :n], in0=d[:, t0:t0 + n], scalar1=-PI, scalar2=2 * PI, op0=ALU.is_lt, op1=ALU.mult), reads=[d], writes=[msk])
                    c.op("dve", lambda e: e.tensor_tensor(out=d[:, t0:t0 + n], in0=d[:, t0:t0 + n], in1=msk[:, :n], op=ALU.add), reads=[d, msk], writes=[d])
                c.op("act", lambda e: e.activation(out=d[:, t0:t0 + n], in_=d[:, t0:t0 + n], func=AF.Sin), reads=[d], writes=[d])

        sin_layer(hid1, w1, 33, fT, 0, 1)
        sin_layer(hid2, w2, 64, hid1, 2, 3)
        win = c.sb([128, nch, 1024], F32, "win")
        c.dma("sp", win[:], I["win%d" % L].rearrange("(t p) c -> p t c", p=128), writes=[win])
        CA = c.sb([128, nch, L], BF16, "CA")
        MB = c.sb([128, nch, L], BF16, "MB")
        c.dma("pool", CA[:], I["CA%d" % L].rearrange("(t p) k -> p t k", p=128), writes=[CA])
        c.dma("pool", MB[:], I["MB%d" % L].rearrange("(t p) k -> p t k", p=128), writes=[MB])
        hw = [c.sb([128, nch, 512], F32, "hw") for _ in range(2)]
        abs_ = [c.sb([128, 512], F32, "hab") for _ in range(2)]
        rinv = c.sb([128, 512], F32, "hrinv")
        hs = c.sb([128, nch, 512], BF16, "hs")
        hd = c.sb([128, nch, 512], BF16, "hd")
        to = [c.sb([128, 512], BF16, "hto") for _ in range(3)]
        tf = [c.sb([128, 512], F32, "htf") for _ in range(2)]
        nto = [0]
        for order in range(2):
            for half in range(2):
                for d in range(2):
                    col0 = d * 2048 + order * 1024 + half * 512
                    H = hw[d]
                    pn = self.pacc()

                    def gen_h(tt, col0=col0):
                        ps = self.pb()
                        c.op("pe", lambda e: e.matmul(ps[:, :512], lhsT=hid2[:, tt * 128:(tt + 1) * 128], rhs=w3[:, col0:col0 + 512], start=True, stop=True), reads=[hid2, w3], writes=[ps])
                        return ps
                    nxt = gen_h(0)
                    for tt in range(nch):
                        ps = nxt
                        nxt = gen_h(tt + 1) if tt + 1 < nch else None
                        a_ = abs_[tt % 2]
                        c.op("dve", lambda e: e.tensor_tensor(out=H[:, tt, :], in0=ps[:, :512], in1=win[:, tt, half * 512:(half + 1) * 512], op=ALU.mult), reads=[ps, win], writes=[H])
                        c.op("act", lambda e: e.activation(out=a_[:], in_=H[:, tt, :], func=AF.Abs), reads=[H], writes=[a_])
                        c.op("pe", lambda e: e.matmul(pn[:, :512], lhsT=onesF[:], rhs=a_[:], start=(tt == 0), stop=(tt == nch - 1)), reads=[onesF, a_], writes=[pn])
                    c.op("dve", lambda e: e.tensor_scalar(out=rinv[:], in0=pn[:, :512], scalar1=EPS, scalar2=None, op0=ALU.add), reads=[pn], writes=[rinv])
                    c.op("dve", lambda e: e.reciprocal(out=rinv[:], in_=rinv[:]), reads=[rinv], writes=[rinv])
                    for tt in range(nch):
                        c.op("dve", lambda e: e.tensor_tensor(out=H[:, tt, :], in0=H[:, tt, :], in1=rinv[:], op=ALU.mult), reads=[H, rinv], writes=[H])
                c.op("dve", lambda e: e.tensor_tensor(out=hs[:], in0=hw[0][:], in1=hw[1][:], op=ALU.add), reads=[hw[0], hw[1]], writes=[hs])
                c.op("pool", lambda e: e.tensor_tensor(out=hd[:], in0=hw[0][:], in1=hw[1][:], op=ALU.subtract), reads=[hw[0], hw[1]], writes=[hd])
                cs = slice(half * 512, (half + 1) * 512)
                for kc in range(nch):
                    pa = self.pb()
                    for tt in range(nch):
                        c.op("pe", lambda e: e.matmul(pa[:, :512], lhsT=CA[:, tt, kc * 128:(kc + 1) * 128], rhs=hs[:, tt, :], start=(tt == 0), stop=(tt == nch - 1)), reads=[CA, hs], writes=[pa])
                    pbd = self.pb()
                    for tt in range(nch):
                        c.op("pe", lambda e: e.matmul(pbd[:, :512], lhsT=MB[:, tt, kc * 128:(kc + 1) * 128], rhs=hd[:, tt, :], start=(tt == 0), stop=(tt == nch - 1)), reads=[MB, hd], writes=[pbd])
                    oa = to[nto[0] % 3]
                    nto[0] += 1
                    c.op("act", lambda e: e.copy(out=oa[:], in_=pa[:, :512]), reads=[pa], writes=[oa])
                    c.dma("sp", TAB[order, kc][:, cs], oa[:], reads=[oa])
                    ob = to[nto[0] % 3]
                    nto[0] += 1
                    if kc > 0:
                        c.op("act", lambda e: e.copy(out=ob[:], in_=pbd[:, :512]), reads=[pbd], writes=[ob])
                        c.dma("sp", TAB[order, nch + kc][:, cs], ob[:], reads=[ob])
                    else:
                        pbs = self.pb()
                        for tt in range(nch):
                            c.op("pe", lambda e: e.matmul(pbs[:, :512], lhsT=MB[:, tt, 0:128], rhs=hs[:, tt, :], start=(tt == 0), stop=(tt == nch - 1)), reads=[MB, hs], writes=[pbs])
                        c.op("dve", lambda e: e.tensor_scalar(out=ob[:], in0=pbd[:, :512], scalar1=m0[:, 0:1], scalar2=None, op0=ALU.mult), reads=[pbd, m0], writes=[ob])
                        c.dma("sp", TAB[order, nch][:, cs], ob[:], reads=[ob])
                        t1, t2 = tf[0], tf[1]
                        c.op("dve", lambda e: e.tensor_scalar(out=t1[:], in0=pa[:, :512], scalar1=m0[:, 0:1], scalar2=None, op0=ALU.mult), reads=[pa, m0], writes=[t1])
                        c.op("dve", lambda e: e.tensor_scalar(out=t2[:], in0=pbs[:, :512], scalar1=m0[:, 1:2], scalar2=None, op0=ALU.mult), reads=[pbs, m0], writes=[t2])
                        oc = to[nto[0] % 3]
                        nto[0] += 1
                        c.op("pool", lambda e: e.tensor_tensor(out=oc[:], in0=t1[:], in1=t2[:], op=ALU.add), reads=[t1, t2], writes=[oc])
                        c.dma("sp", TAB[order, 2 * nch][:, cs], oc[:], reads=[oc])
        c.end_stage()

    def hy_data(self, l, L, tokbase, B):
        c = self.c
        I = self.I
        nch = L // 128
        ncol = B * 1024
        TAB = self.S["HTAB%d" % L]
        HV, HX = self.S["HV"], self.S["HX"]
        MO = self.scr("MO", [3, 8, 128, NT], BF16)
        Z1 = self.scr("HZ1", [8, 128, NT], F32)
        idB = self.identB
        c.begin_stage()
        CA = c.sb([128, nch, L], BF16, "CA")
        MB = c.sb([128, nch, L], BF16, "MB")
        IA = c.sb([128, nch, L], BF16, "IA")
        IB = c.sb([128, nch, L], BF16, "IB")
        for t, nm in ((CA, "CA"), (MB, "MB"), (IA, "IA"), (IB, "IB")):
            c.dma("pool", t[:], I["%s%d" % (nm, L)].rearrange("(t p) k -> p t k", p=128), writes=[t])
        AH = c.sb([128, nch, 1024], BF16, "AH")
        HB1 = c.sb([128, nch, 1024], BF16, "HB1")
        HA2 = c.sb([128, 1024], BF16, "HA2")
        hb = c.sb([128, 16], F32, "hbias")
        self.load_cols(hb, hb[:], I["hy_bias"][l].rearrange("o (n p) -> (o n) p", p=128), 16)
        z = c.sb([128, nch, ncol], BF16, "z")
        Y = c.sb([128, 2 * nch, ncol], BF16, "Y")
        ntok = B * L
        rowv = [c.sb([128, ntok], BF16, "hrv") for _ in range(2)]
        rowx = [c.sb([128, ntok], F32, "hrx") for _ in range(2)]
        rowz = [c.sb([128, ntok], F32, "hrz") for _ in range(2)]
        rowo = [c.sb([128, ntok], BF16, "hro") for _ in range(2)]
        tm = [c.sb([128, 512], F32, "htm") for _ in range(8)]
        tmi = [0]

        def to_tokmajor(src_rows_fn):
            for ch in range(8):
                row = src_rows_fn(ch)
                for b in range(B):
                    for tq in range(0, nch, 4):
                        ps = self.pb()
                        pb16 = ps[:].bitcast(BF16)
                        nq = min(4, nch - tq)
                        for j in range(nq):
                            tt = tq + j
                            t0 = b * L + tt * 128
                            c.op("pe", lambda e: e.transpose(out=pb16[:, j * 128:(j + 1) * 128], in_=row[:, t0:t0 + 128], identity=idB[:]), reads=[row, idB], writes=[ps])
                        for j in range(nq):
                            tt = tq + j
                            eng = "dve" if j % 2 == 0 else "act"
                            dst = z[:, tt, b * 1024 + ch * 128:b * 1024 + (ch + 1) * 128]
                            if eng == "dve":
                                c.op("dve", lambda e: e.tensor_copy(out=dst, in_=pb16[:, j * 128:(j + 1) * 128]), reads=[ps], writes=[z])
                            else:
                                c.op("act", lambda e: e.copy(out=dst, in_=pb16[:, j * 128:(j + 1) * 128]), reads=[ps], writes=[z])

        for order in range(2):
            c.dma("sp", AH[:], TAB[order, 0:nch].rearrange("k p c -> p k c"), writes=[AH])
            c.dma("sp", HB1[:], TAB[order, nch:2 * nch].rearrange("k p c -> p k c"), writes=[HB1])
            c.dma("sp", HA2[:], TAB[order, 2 * nch], writes=[HA2])
            if order == 0:
                def rows_v(ch):
                    r = rowv[ch % 2]
                    c.dma("sp", r[:], HV[ch][:, tokbase:tokbase + ntok], writes=[r])
                    return r
                to_tokmajor(rows_v)
            else:
                def rows_z(ch):
                    rz = rowz[ch % 2]
                    r = rowv[ch % 2]
                    c.dma("sp", rz[:], Z1[ch][:, tokbase:tokbase + ntok], writes=[rz])
                    c.op("act", lambda e: e.copy(out=r[:], in_=rz[:]), reads=[rz], writes=[r])
                    return r
                to_tokmajor(rows_z)
            for ct in range(ncol // 512):
                c0 = (ct * 512) % 1024
                for kc in range(nch):
                    pa = self.pb()
                    for tt in range(nch):
                        c.op("pe", lambda e: e.matmul(pa[:, :512], lhsT=CA[:, tt, kc * 128:(kc + 1) * 128], rhs=z[:, tt, ct * 512:(ct + 1) * 512], start=(tt == 0), stop=(tt == nch - 1)), reads=[CA, z], writes=[pa])
                    pq = self.pb()
                    for tt in range(nch):
                        c.op("pe", lambda e: e.matmul(pq[:, :512], lhsT=MB[:, tt, kc * 128:(kc + 1) * 128], rhs=z[:, tt, ct * 512:(ct + 1) * 512], start=(tt == 0), stop=(tt == nch - 1)), reads=[MB, z], writes=[pq])
                    tmi[0] += 1
                    t1, t2, t3, t4 = tm[(tmi[0] % 2) * 4:(tmi[0] % 2) * 4 + 4]
                    ah = AH[:, kc, c0:c0 + 512]
                    h1 = HB1[:, kc, c0:c0 + 512]
                    a2 = HA2[:, c0:c0 + 512] if kc == 0 else ah
                    c.op("dve", lambda e: e.tensor_tensor(out=t1[:], in0=pa[:, :512], in1=ah, op=ALU.mult), reads=[pa, AH], writes=[t1])
                    c.op("dve", lambda e: e.tensor_tensor(out=t2[:], in0=pq[:, :512], in1=h1, op=ALU.mult), reads=[pq, HB1], writes=[t2])
                    c.op("pool", lambda e: e.tensor_tensor(out=Y[:, kc, ct * 512:(ct + 1) * 512], in0=t1[:], in1=t2[:], op=ALU.subtract), reads=[t1, t2], writes=[Y])
                    c.op("dve", lambda e: e.tensor_tensor(out=t3[:], in0=pa[:, :512], in1=h1, op=ALU.mult), reads=[pa, HB1], writes=[t3])
                    c.op("dve", lambda e: e.tensor_tensor(out=t4[:], in0=pq[:, :512], in1=a2, op=ALU.mult), reads=[pq, HA2, AH], writes=[t4])
                    c.op("pool", lambda e: e.tensor_tensor(out=Y[:, nch + kc, ct * 512:(ct + 1) * 512], in0=t3[:], in1=t4[:], op=ALU.add), reads=[t3, t4], writes=[Y])
            for ch in range(8):
                rx = rowx[ch % 2]
                c.dma("sp", rx[:], HX[order * 8 + ch][:, tokbase:tokbase + ntok], writes=[rx])
                if order == 0:
                    rb = rowv[ch % 2]
                    c.dma("sp", rb[:], HV[ch][:, tokbase:tokbase + ntok], writes=[rb])
                    ro = rowz[ch % 2]
                else:
                    rb = rowz[ch % 2]
                    c.dma("sp", rb[:], Z1[ch][:, tokbase:tokbase + ntok], writes=[rb])
                    ro = rowo[ch % 2]
                for b in range(B):
                    for r0 in range(0, L, 512):
                        n = min(512, L - r0)
                        ps = self.pacc()
                        for kk in range(2 * nch):
                            M = IA if kk < nch else IB
                            c.op("pe", lambda e: e.matmul(ps[:, :n], lhsT=Y[:, kk, b * 1024 + ch * 128:b * 1024 + (ch + 1) * 128], rhs=M[:, kk % nch, r0:r0 + n], start=(kk == 0), stop=(kk == 2 * nch - 1)),
                                 reads=[Y, M], writes=[ps])
                        g0 = b * L + r0
                        tmi[0] += 1
                        t1 = tm[tmi[0] % 8]
                        c.op("dve", lambda e: e.scalar_tensor_tensor(out=t1[:, :n], in0=rb[:, g0:g0 + n], scalar=hb[:, order * 8 + ch:order * 8 + ch + 1], in1=ps[:, :n], op0=ALU.mult, op1=ALU.add), reads=[rb, hb, ps], writes=[t1])
                        c.op("dve", lambda e: e.tensor_tensor(out=ro[:, g0:g0 + n], in0=t1[:, :n], in1=rx[:, g0:g0 + n], op=ALU.mult), reads=[t1, rx], writes=[ro])
                if order == 0:
                    c.dma("sp", Z1[ch][:, tokbase:tokbase + ntok], ro[:], reads=[ro])
                else:
                    c.dma("sp", MO[1][ch][:, tokbase:tokbase + ntok], ro[:], reads=[ro])
            c.barrier()
        c.end_stage()

    def stage_hy(self, l):
        self.hy_tables(l, 256)
        self.hy_data(l, 256, 0, 4)
        self.hy_tables(l, 1024)
        self.hy_data(l, 1024, 1024, 1)

_CACHE = {}


def build_full():
    kb = KB5()
    c = kb.c
    c.begin_stage()
    kb.stage_input(own=False)
    c.mark('input')
    kb.stage_mod()
    c.end_stage()
    c.mark('mod')
    XA = kb.S["XT0"]
    XB = kb.scr("XT1", [16, 128, NT], F32)
    XC = kb.scr("XT2", [16, 128, NT], F32)
    cur = XA
    for l in range(2):
        kb.stage_norm(cur, l, 0)
        c.mark('norm1')
        kb.stage_proj(l)
        c.end_stage()
        c.mark('proj')
        kb.stage_ret(l)
        c.mark('ret')
        kb.stage_hy(l)
        c.mark('hy')
        kb.stage_dn(l)
        c.mark('dn')
        kb.stage_merge(l, cur, XB)
        c.mark('merge+wo')
        kb.stage_norm(XB, l, 1)
        c.mark('norm2')
        kb.stage_ffn_up(l)
        c.end_stage()
        c.mark('ffn_up')
        kb.stage_ffn_down(l, XB, XC)
        c.mark('ffn_down')
        cur, XB, XC = XC, cur, XB
    kb.stage_final(cur)
    c.mark('final')
    c.finish()
    return kb


def kernel(**inputs):
    z = {k: np.ascontiguousarray(np.asarray(v)) for k, v in inputs.items()}
    if "kb" not in _CACHE:
        _CACHE["kb"] = build_full()
    kb = _CACHE["kb"]
    in_maps = []
    for core in range(8):
        m = dict(kb.consts)
        for k in WSPEC:
            m[k] = z[k]
        xp = z["x_prompt"][4 * core:4 * core + 4].reshape(1024, 2048)
        xs = z["x_sample"][core // 2]
        m["xin"] = np.ascontiguousarray(np.concatenate([xp, xs], 0))
        m["sret"] = np.ascontiguousarray(z["state_ret"][core // 2])
        m["sdn"] = np.ascontiguousarray(z["state_dn"][core // 2])
        m["cvec"] = np.ascontiguousarray(np.stack([z["c_ctx"], z["c"][core // 2]]))
        in_maps.append(m)
    res = run_bass_kernel_spmd(kb.nc, in_maps, core_ids=list(range(8))).results
    y_prompt = np.concatenate([np.asarray(res[c]["y"])[:1024].reshape(4, 256, 2048) for c in range(8)], 0).astype(np.float32)
    y_sample = np.stack([np.asarray(res[2 * b]["y"])[1024:] for b in range(4)], 0).astype(np.float32)
    new_ret = np.concatenate([np.asarray(res[c]["osret"]) for c in range(8)], 0).astype(np.float32)
    new_dn = np.concatenate([np.asarray(res[c]["osdn"]) for c in range(8)], 0).astype(np.float32)
    return (y_prompt, y_sample, new_ret, new_dn)
```
